# Optimizing a Trainium2 kernel written in Bass

```python
import math
import jax, jax.numpy as jnp
from jax import lax
import numpy as np

D_MODEL = 1024
BATCH = 8
SEQ = 8192
DEPTH = 2
DEC_BATCH = 2
DEC_SEQ = 8192
PAST_LEN = 128

BRANCH_W = D_MODEL // 2
N_BRANCH = 4
CONV_W = 4
NORM_EPS = 1e-6

LRU_W = BRANCH_W
LRU_BLOCKS = 4
LRU_BLOCK_W = LRU_W // LRU_BLOCKS
LRU_C = 8.0

GLA_HEADS = 4
GLA_DK = BRANCH_W // 2 // GLA_HEADS
GLA_DV = BRANCH_W // GLA_HEADS
GLA_RANK = 16
GLA_TAU = 16.0
GLA_CHUNK = 64

DN_HEADS = 4
DN_DK = BRANCH_W // DN_HEADS
DN_DV = BRANCH_W // DN_HEADS
DN_CHUNK = 64

S5_GROUP_W = 16
S5_GROUPS = BRANCH_W // S5_GROUP_W
S5_STATE = 64

IN_WIDTHS = (
    LRU_W, LRU_W,
    GLA_HEADS * GLA_DK, GLA_HEADS * GLA_DK, GLA_HEADS * GLA_DV, BRANCH_W, 2 * GLA_RANK,
    3 * BRANCH_W, BRANCH_W, 4 * DN_HEADS,
    BRANCH_W, BRANCH_W,
)
D_IN = sum(IN_WIDTHS)

kernel_name = 'hybrid_bidir_rglru_gla_gdn_s5'


def rms_norm(x, g):
    xf = x.astype(jnp.float32)
    y = xf * lax.rsqrt(jnp.mean(xf * xf, axis=-1, keepdims=True) + NORM_EPS)
    return (y * g.astype(jnp.float32)).astype(x.dtype)


def l2norm(x):
    return x * lax.rsqrt(jnp.sum(x * x, axis=-1, keepdims=True) + NORM_EPS)


def _flip(t):
    return jnp.flip(t, axis=1)


def centred_dwconv(x, w):
    left = CONV_W // 2
    return lax.conv_general_dilated(
        x, w[:, None, :], window_strides=(1,), padding=[(left, CONV_W - 1 - left)],
        dimension_numbers=('NWC', 'WIO', 'NWC'), feature_group_count=x.shape[-1])


def _linear_combine(left, right):
    a1, b1 = left
    a2, b2 = right
    return a1 * a2, a2 * b1 + b2


def _complex_combine(left, right):
    ar1, ai1, br1, bi1 = left
    ar2, ai2, br2, bi2 = right
    return (ar2 * ar1 - ai2 * ai1, ar2 * ai1 + ai2 * ar1,
            ar2 * br1 - ai2 * bi1 + br2, ar2 * bi1 + ai2 * br1 + bi2)


def rglru_branch(u, gate, conv_w, conv_b, w_a, b_a, w_x, b_x, lam):
    f32 = jnp.float32
    bsz, L, _ = u.shape
    xc = centred_dwconv(u.astype(f32), conv_w.astype(f32)) + conv_b.astype(f32)
    xb = xc.reshape(bsz, L, LRU_BLOCKS, LRU_BLOCK_W)
    hs = []
    for d, rev in enumerate((False, True)):
        r = jax.nn.sigmoid(jnp.einsum('blhi,hij->blhj', xb, w_a[d].astype(f32)).reshape(bsz, L, LRU_W) + b_a[d].astype(f32))
        i = jax.nn.sigmoid(jnp.einsum('blhi,hij->blhj', xb, w_x[d].astype(f32)).reshape(bsz, L, LRU_W) + b_x[d].astype(f32))
        log_a = -LRU_C * jax.nn.softplus(-lam[d].astype(f32)) * r
        drive = jnp.sqrt(-jnp.expm1(2.0 * log_a)) * (i * xc)
        _, h = lax.associative_scan(_linear_combine, (jnp.exp(log_a), drive), axis=1, reverse=rev)
        hs.append(h)
    return ((hs[0] + hs[1]) * jax.nn.silu(gate.astype(f32))).astype(u.dtype)


def gla_chunked(q, k, v, log_a):
    bsz, L, H, DK = q.shape
    DV = v.shape[-1]
    C = GLA_CHUNK
    N = L // C
    rs = lambda t: t.reshape(bsz, N, C, H, t.shape[-1])
    q, k, v, log_a = rs(q), rs(k), rs(v), rs(log_a)
    b = jnp.cumsum(log_a, axis=2)
    b_end = b[:, :, -1:]
    q_dec = q * jnp.exp(b)
    k_inv = k * jnp.exp(-b)
    k_end = k * jnp.exp(b_end - b)
    causal = jnp.tril(jnp.ones((C, C), bool))
    scores = jnp.where(causal, jnp.einsum('bnthk,bnshk->bnhts', q_dec, k_inv), 0.0)
    o_intra = jnp.einsum('bnhts,bnshv->bnthv', scores, v)

    def step(S, inp):
        qd, ke, vv, de = inp
        o = jnp.einsum('bthk,bhkv->bthv', qd, S)
        S = de[..., None] * S + jnp.einsum('bshk,bshv->bhkv', ke, vv)
        return S, o

    s0 = jnp.zeros((bsz, H, DK, DV), jnp.float32)
    xs = (jnp.moveaxis(q_dec, 1, 0), jnp.moveaxis(k_end, 1, 0), jnp.moveaxis(v, 1, 0),
          jnp.moveaxis(jnp.exp(b_end[:, :, 0]), 1, 0))
    _, o_inter = lax.scan(step, s0, xs)
    return (o_intra + jnp.moveaxis(o_inter, 0, 1)).reshape(bsz, L, H, DV)


def gla_branch(q, k, v, gate, lr, w_up, b_up, norm_g):
    f32 = jnp.float32
    bsz, L, _ = q.shape
    q = q.astype(f32).reshape(bsz, L, GLA_HEADS, GLA_DK) * GLA_DK ** -0.5
    k = k.astype(f32).reshape(bsz, L, GLA_HEADS, GLA_DK)
    v = v.astype(f32).reshape(bsz, L, GLA_HEADS, GLA_DV)
    lr = lr.astype(f32).reshape(bsz, L, 2, GLA_RANK)
    log_a = [
        (jax.nn.log_sigmoid(jnp.einsum('blr,rk->blk', lr[:, :, d], w_up[d].astype(f32)) + b_up[d].astype(f32))
         / GLA_TAU).reshape(bsz, L, GLA_HEADS, GLA_DK)
        for d in range(2)]
    o = gla_chunked(q, k, v, log_a[0]) + _flip(gla_chunked(_flip(q), _flip(k), _flip(v), _flip(log_a[1])))
    o = rms_norm(o, norm_g).reshape(bsz, L, BRANCH_W)
    return (o * jax.nn.silu(gate.astype(f32))).astype(gate.dtype)


def gated_delta_chunked(q, k, v, beta, g):
    bsz, L, H, DK = q.shape
    DV = v.shape[-1]
    C = DN_CHUNK
    N = L // C
    chunk4 = lambda t: t.reshape(bsz, N, C, H, t.shape[-1]).transpose(0, 1, 3, 2, 4)
    chunk3 = lambda t: t.reshape(bsz, N, C, H).transpose(0, 1, 3, 2)
    q, k, v = chunk4(q), chunk4(k), chunk4(v)
    beta = chunk3(beta)
    G = jnp.cumsum(chunk3(g), axis=-1)
    causal = jnp.tril(jnp.ones((C, C), bool))
    strict = jnp.tril(jnp.ones((C, C), bool), -1)
    diff = G[..., :, None] - G[..., None, :]
    gamma = jnp.where(causal, jnp.exp(jnp.where(causal, diff, 0.0)), 0.0)
    k_beta = k * beta[..., None]
    a_mat = jnp.where(strict, jnp.einsum('bnhtk,bnhsk->bnhts', k_beta, k) * gamma, 0.0)
    eye = jnp.eye(C, dtype=jnp.float32)
    t_mat = lax.linalg.triangular_solve(a_mat + eye, jnp.broadcast_to(eye, a_mat.shape),
                                        left_side=True, lower=True, unit_diagonal=True)
    w = jnp.einsum('bnhts,bnhsk->bnhtk', t_mat, k_beta * jnp.exp(G)[..., None])
    u = jnp.einsum('bnhts,bnhsv->bnhtv', t_mat, v * beta[..., None])
    attn = jnp.where(causal, jnp.einsum('bnhtk,bnhsk->bnhts', q, k) * gamma, 0.0)
    q_dec = q * jnp.exp(G)[..., None]
    k_end = k * jnp.exp(G[..., -1:] - G)[..., None]
    dec_end = jnp.exp(G[..., -1])

    def step(S, inp):
        qd, ke, ww, uu, at, de = inp
        v_new = uu - jnp.einsum('bhtk,bhkv->bhtv', ww, S)
        o = jnp.einsum('bhtk,bhkv->bhtv', qd, S) + jnp.einsum('bhts,bhsv->bhtv', at, v_new)
        S = de[..., None, None] * S + jnp.einsum('bhtk,bhtv->bhkv', ke, v_new)
        return S, o

    s0 = jnp.zeros((bsz, H, DK, DV), jnp.float32)
    xs = tuple(jnp.moveaxis(t, 1, 0) for t in (q_dec, k_end, w, u, attn, dec_end))
    _, o = lax.scan(step, s0, xs)
    return o.transpose(1, 0, 3, 2, 4).reshape(bsz, L, H, DV)


def deltanet_branch(qkv, gate, ba, conv_w, a_log, dt_bias, norm_g):
    f32 = jnp.float32
    bsz, L, _ = qkv.shape
    qkv = jax.nn.silu(centred_dwconv(qkv.astype(f32), conv_w.astype(f32)))
    q, k, v = jnp.split(qkv, 3, axis=-1)
    q = l2norm(q.reshape(bsz, L, DN_HEADS, DN_DK)) * DN_DK ** -0.5
    k = l2norm(k.reshape(bsz, L, DN_HEADS, DN_DK))
    v = v.reshape(bsz, L, DN_HEADS, DN_DV)
    ba = ba.astype(f32).reshape(bsz, L, 2, 2, DN_HEADS)
    beta = jax.nn.sigmoid(ba[:, :, :, 0])
    g = -jnp.exp(a_log.astype(f32)) * jax.nn.softplus(ba[:, :, :, 1] + dt_bias.astype(f32))
    o = (gated_delta_chunked(q, k, v, beta[:, :, 0], g[:, :, 0])
         + _flip(gated_delta_chunked(_flip(q), _flip(k), _flip(v), _flip(beta[:, :, 1]), _flip(g[:, :, 1]))))
    o = rms_norm(o, norm_g).reshape(bsz, L, BRANCH_W)
    return (o * jax.nn.silu(gate.astype(f32))).astype(gate.dtype)


def _s5_discretise(lam_re, lam_im, log_dt, b_re, b_im):
    dt = jnp.exp(log_dt)[:, None]
    mag = jnp.exp(lam_re * dt)
    a_re = mag * jnp.cos(lam_im * dt)
    a_im = mag * jnp.sin(lam_im * dt)
    den = lam_re * lam_re + lam_im * lam_im
    n_re = a_re - 1.0
    f_re = (n_re * lam_re + a_im * lam_im) / den
    f_im = (a_im * lam_re - n_re * lam_im) / den
    bb_re = f_re[..., None] * b_re - f_im[..., None] * b_im
    bb_im = f_re[..., None] * b_im + f_im[..., None] * b_re
    return a_re, a_im, bb_re, bb_im


def s5_branch(u, gate, lam_re, lam_im, log_dt, b_re, b_im, c_re, c_im, d_skip, w_glu, b_glu):
    f32 = jnp.float32
    bsz, L, _ = u.shape
    uf = u.astype(f32)
    disc = [_s5_discretise(lam_re[d].astype(f32), lam_im[d].astype(f32), log_dt[d].astype(f32),
                           b_re[d].astype(f32), b_im[d].astype(f32)) for d in range(2)]
    cs = [(c_re[d].astype(f32), c_im[d].astype(f32)) for d in range(2)]

    def one_sequence(us):
        ys = []
        for d, rev in enumerate((False, True)):
            a_re, a_im, bb_re, bb_im = disc[d]
            bu_re = jnp.einsum('lgi,gpi->lgp', us, bb_re)
            bu_im = jnp.einsum('lgi,gpi->lgp', us, bb_im)
            _, _, s_re, s_im = lax.associative_scan(
                _complex_combine,
                (jnp.broadcast_to(a_re, bu_re.shape), jnp.broadcast_to(a_im, bu_re.shape), bu_re, bu_im),
                axis=0, reverse=rev)
            ys.append(jnp.einsum('lgp,gip->lgi', s_re, cs[d][0]) - jnp.einsum('lgp,gip->lgi', s_im, cs[d][1]))
        return ys[0] + ys[1]

    y = lax.map(one_sequence, uf.reshape(bsz, L, S5_GROUPS, S5_GROUP_W)).reshape(bsz, L, BRANCH_W)
    y = y + d_skip.astype(f32) * uf
    z = jax.nn.gelu(y)
    z = z * jax.nn.sigmoid(z @ w_glu.astype(f32) + b_glu.astype(f32))
    return (z * jax.nn.silu(gate.astype(f32))).astype(u.dtype)


def trunk_layer(x, norm_g, w_in, lru_conv_w, lru_conv_b, lru_w_a, lru_b_a, lru_w_x, lru_b_x, lru_lambda,
                gla_w_up, gla_b_up, gla_norm_g, dn_conv_w, dn_a_log, dn_dt_bias, dn_norm_g,
                s5_lambda_re, s5_lambda_im, s5_log_dt, s5_b_re, s5_b_im, s5_c_re, s5_c_im, s5_d,
                s5_w_glu, s5_b_glu, w_branch, w_merge_gate, b_merge_gate, w_out):
    xn = rms_norm(x, norm_g)
    proj = jnp.einsum('bld,de->ble', xn, w_in)
    split_at = np.cumsum(IN_WIDTHS)[:-1].tolist()
    (lru_x, lru_gate, gla_q, gla_k, gla_v, gla_gate, gla_lr,
     dn_qkv, dn_gate, dn_ba, s5_u, s5_gate) = jnp.split(proj, split_at, axis=-1)
    branches = (
        rglru_branch(lru_x, lru_gate, lru_conv_w, lru_conv_b, lru_w_a, lru_b_a, lru_w_x, lru_b_x, lru_lambda),
        gla_branch(gla_q, gla_k, gla_v, gla_gate, gla_lr, gla_w_up, gla_b_up, gla_norm_g),
        deltanet_branch(dn_qkv, dn_gate, dn_ba, dn_conv_w, dn_a_log, dn_dt_bias, dn_norm_g),
        s5_branch(s5_u, s5_gate, s5_lambda_re, s5_lambda_im, s5_log_dt, s5_b_re, s5_b_im,
                  s5_c_re, s5_c_im, s5_d, s5_w_glu, s5_b_glu),
    )
    merged = None
    for n, y in enumerate(branches):
        gate = jax.nn.sigmoid(xn @ w_merge_gate[n] + b_merge_gate[n])
        term = gate * (y @ w_branch[n])
        merged = term if merged is None else merged + term
    return x + merged @ w_out


def setup_inputs(seed: int = 0) -> dict:
    key = jax.random.key(seed)
    ks = iter(jax.random.split(key, 48))
    f32 = jnp.float32

    def nrm(shape, scale):
        return scale * jax.random.normal(next(ks), shape, f32)

    def unif(shape, lo, hi):
        return jax.random.uniform(next(ks), shape, f32, lo, hi)

    L = DEPTH
    x_prompt = nrm((BATCH, SEQ, D_MODEL), 1.0)
    x_sample = nrm((DEC_BATCH, DEC_SEQ, D_MODEL), 1.0)
    norm_g = 1.0 + nrm((L, D_MODEL), 0.02)
    w_in = nrm((L, D_MODEL, D_IN), D_MODEL ** -0.5)
    lru_conv_w = nrm((L, CONV_W, LRU_W), CONV_W ** -0.5)
    lru_conv_b = nrm((L, LRU_W), 0.01)
    lru_w_a = nrm((L, 2, LRU_BLOCKS, LRU_BLOCK_W, LRU_BLOCK_W), LRU_BLOCK_W ** -0.5)
    lru_b_a = nrm((L, 2, LRU_W), 0.01)
    lru_w_x = nrm((L, 2, LRU_BLOCKS, LRU_BLOCK_W, LRU_BLOCK_W), LRU_BLOCK_W ** -0.5)
    lru_b_x = nrm((L, 2, LRU_W), 0.01)
    a0 = unif((L, 2, LRU_W), 0.9, 0.999)
    s = a0 ** (1.0 / LRU_C)
    lru_lambda = jnp.log(s) - jnp.log1p(-s)
    gla_w_up = nrm((L, 2, GLA_RANK, GLA_HEADS * GLA_DK), GLA_RANK ** -0.5)
    gla_b_up = nrm((L, 2, GLA_HEADS * GLA_DK), 0.01)
    gla_norm_g = 1.0 + nrm((L, GLA_DV), 0.02)
    dn_conv_w = nrm((L, CONV_W, 3 * BRANCH_W), CONV_W ** -0.5)
    dn_a_log = jnp.log(unif((L, 2, DN_HEADS), 1.0, 16.0))
    dt = jnp.exp(unif((L, 2, DN_HEADS), math.log(1e-3), math.log(1e-1)))
    dn_dt_bias = dt + jnp.log(-jnp.expm1(-dt))
    dn_norm_g = 1.0 + nrm((L, DN_DV), 0.02)
    s5_shape = (L, 2, S5_GROUPS, S5_STATE)
    s5_lambda_re = -0.5 + nrm(s5_shape, 0.01)
    s5_lambda_im = jnp.pi * jnp.arange(S5_STATE, dtype=f32) + nrm(s5_shape, 0.01)
    s5_log_dt = unif((L, 2, S5_GROUPS), math.log(1e-3), math.log(1e-1))
    b_scale = (2.0 * S5_GROUP_W) ** -0.5
    s5_b_re = nrm((L, 2, S5_GROUPS, S5_STATE, S5_GROUP_W), b_scale)
    s5_b_im = nrm((L, 2, S5_GROUPS, S5_STATE, S5_GROUP_W), b_scale)
    c_scale = S5_STATE ** -0.5
    s5_c_re = nrm((L, 2, S5_GROUPS, S5_GROUP_W, S5_STATE), c_scale)
    s5_c_im = nrm((L, 2, S5_GROUPS, S5_GROUP_W, S5_STATE), c_scale)
    s5_d = nrm((L, BRANCH_W), 1.0)
    s5_w_glu = nrm((L, BRANCH_W, BRANCH_W), BRANCH_W ** -0.5)
    s5_b_glu = nrm((L, BRANCH_W), 0.01)
    w_branch = nrm((L, N_BRANCH, BRANCH_W, D_MODEL), BRANCH_W ** -0.5)
    w_merge_gate = nrm((L, N_BRANCH, D_MODEL, D_MODEL), D_MODEL ** -0.5)
    b_merge_gate = nrm((L, N_BRANCH, D_MODEL), 0.01)
    w_out = nrm((L, D_MODEL, D_MODEL), D_MODEL ** -0.5)
    final_norm_g = 1.0 + nrm((D_MODEL,), 0.02)
    return {
        'x_prompt': x_prompt, 'x_sample': x_sample,
        'norm_g': norm_g, 'w_in': w_in,
        'lru_conv_w': lru_conv_w, 'lru_conv_b': lru_conv_b, 'lru_w_a': lru_w_a, 'lru_b_a': lru_b_a,
        'lru_w_x': lru_w_x, 'lru_b_x': lru_b_x, 'lru_lambda': lru_lambda,
        'gla_w_up': gla_w_up, 'gla_b_up': gla_b_up, 'gla_norm_g': gla_norm_g,
        'dn_conv_w': dn_conv_w, 'dn_a_log': dn_a_log, 'dn_dt_bias': dn_dt_bias, 'dn_norm_g': dn_norm_g,
        's5_lambda_re': s5_lambda_re, 's5_lambda_im': s5_lambda_im, 's5_log_dt': s5_log_dt,
        's5_b_re': s5_b_re, 's5_b_im': s5_b_im, 's5_c_re': s5_c_re, 's5_c_im': s5_c_im,
        's5_d': s5_d, 's5_w_glu': s5_w_glu, 's5_b_glu': s5_b_glu,
        'w_branch': w_branch, 'w_merge_gate': w_merge_gate, 'b_merge_gate': b_merge_gate,
        'w_out': w_out, 'final_norm_g': final_norm_g,
    }


def reference(x_prompt, x_sample, norm_g, w_in, lru_conv_w, lru_conv_b, lru_w_a, lru_b_a, lru_w_x,
              lru_b_x, lru_lambda, gla_w_up, gla_b_up, gla_norm_g, dn_conv_w, dn_a_log, dn_dt_bias,
              dn_norm_g, s5_lambda_re, s5_lambda_im, s5_log_dt, s5_b_re, s5_b_im, s5_c_re, s5_c_im,
              s5_d, s5_w_glu, s5_b_glu, w_branch, w_merge_gate, b_merge_gate, w_out, final_norm_g):
    layer_params = (norm_g, w_in, lru_conv_w, lru_conv_b, lru_w_a, lru_b_a, lru_w_x, lru_b_x, lru_lambda,
                    gla_w_up, gla_b_up, gla_norm_g, dn_conv_w, dn_a_log, dn_dt_bias, dn_norm_g,
                    s5_lambda_re, s5_lambda_im, s5_log_dt, s5_b_re, s5_b_im, s5_c_re, s5_c_im, s5_d,
                    s5_w_glu, s5_b_glu, w_branch, w_merge_gate, b_merge_gate, w_out)

    def trunk(x):
        for l in range(DEPTH):
            x = trunk_layer(x, *[p[l] for p in layer_params])
        return rms_norm(x, final_norm_g)

    y_prompt = trunk(x_prompt)
    y_sample = trunk(x_sample)
    return (y_prompt, y_sample)
```

```python
import numpy as np
import ml_dtypes
from contextlib import ExitStack
import concourse.bass as bass
import concourse.mybir as mybir
from concourse.bass_utils import run_bass_kernel_spmd

F32 = mybir.dt.float32
BF16 = mybir.dt.bfloat16
ALU = mybir.AluOpType
AF = mybir.ActivationFunctionType

D = 1024
BW = 512
D_IN = 5680
EPS = 1e-6
O_LRU_X, O_LRU_G = 0, 512
O_GLA_Q, O_GLA_K, O_GLA_V, O_GLA_G, O_GLA_LR = 1024, 1280, 1536, 2048, 2560
O_DN_QKV, O_DN_G, O_DN_BA = 2592, 4128, 4640
O_S5_U, O_S5_G = 4656, 5168


class Buf:
    __slots__ = ("ap", "w", "r", "name")

    def __init__(self, ap, name=""):
        self.ap = ap
        self.w = []
        self.r = []
        self.name = name

    def __getitem__(self, k):
        return self.ap[k]


class Prog:
    def __init__(self, nc, n_dma_sems=40):
        self.nc = nc
        self.eng = {"pe": nc.tensor, "act": nc.scalar, "dve": nc.vector, "pool": nc.gpsimd, "sp": nc.sync}
        self.sem = {k: nc.alloc_semaphore("s_" + k) for k in self.eng}
        self.cnt = {k: 0 for k in self.eng}
        self.seen = {k: {} for k in self.eng}
        self.dsem = [nc.alloc_semaphore("d%d" % i) for i in range(n_dma_sems)]
        self.dcnt = [0] * n_dma_sems
        self.dnext = 0
        self.ninst = 0

    def buf(self, ap, name=""):
        return Buf(ap, name)

    def uname(self, name):
        self.uid = getattr(self, "uid", 0) + 1
        return "%s_%d" % (name, self.uid)

    def _wait(self, e, dep):
        if dep[0] == "dma":
            key = ("dma", dep[1]); val = dep[2]
            if self.seen[e].get(key, 0) >= val:
                return
            self.eng[e].wait_ge(self.dsem[dep[1]], val)
        else:
            f, val = dep
            if f == e and e == "pe":
                return
            if f == e and e == "sp":
                return
            key = f
            if self.seen[e].get(key, 0) >= val:
                return
            self.eng[e].wait_ge(self.sem[f], val)
        self.seen[e][key] = val
        self.ninst += 1

    def _deps(self, e, reads, writes):
        for b in reads:
            for d in b.w:
                self._wait(e, d)
        for b in writes:
            for d in b.w:
                self._wait(e, d)
            for d in b.r:
                self._wait(e, d)

    def op(self, e, inst_fn, reads=(), writes=()):
        self._deps(e, reads, writes)
        inst = inst_fn(self.eng[e])
        inst.then_inc(self.sem[e], 1)
        self.cnt[e] += 1
        me = (e, self.cnt[e])
        for b in reads:
            b.r.append(me)
            if len(b.r) > 24:
                b.r = b.r[-24:] if False else self._compress(b.r)
        for b in writes:
            b.w = [me]
            b.r = []
        self.ninst += 1
        return inst

    @staticmethod
    def _compress(lst):
        best = {}
        for d in lst:
            k = ("dma", d[1]) if d[0] == "dma" else d[0]
            v = d[2] if d[0] == "dma" else d[1]
            if k not in best or v > best[k][0]:
                best[k] = (v, d)
        return [x[1] for x in best.values()]

    def dma(self, out, in_, reads=(), writes=(), q="sp", **kw):
        self._deps(q, reads, writes)
        j = self.dnext
        self.dnext = (self.dnext + 1) % len(self.dsem)
        if self.dcnt[j] > 0:
            self._wait(q, ("dma", j, self.dcnt[j]))
        self.dcnt[j] += 16
        self.eng[q].dma_start(out=out, in_=in_, **kw).then_inc(self.dsem[j], 16)
        me = ("dma", j, self.dcnt[j])
        for b in reads:
            b.r.append(me)
            if len(b.r) > 24:
                b.r = self._compress(b.r)
        for b in writes:
            b.w = [me]
            b.r = []
        self.ninst += 1

    def barrier(self):
        for e in self.eng:
            for f in self.eng:
                if f != e and self.cnt[f] > 0:
                    self._wait(e, (f, self.cnt[f]))
            for j, c in enumerate(self.dcnt):
                if c > 0:
                    self._wait(e, ("dma", j, c))


class Ctx:
    pass


def mk_psum(pg, nc):
    banks = []
    for i in range(6):
        banks.append(pg.buf(nc.alloc_psum_tensor("psf%d" % i, [128, 512], F32).ap(), "psf%d" % i))
    bb = []
    for i in range(2):
        bb.append(pg.buf(nc.alloc_psum_tensor("psb%d" % i, [128, 1024], BF16).ap(), "psb%d" % i))
    return banks, bb


class Rot:
    def __init__(self, items):
        self.items = items
        self.i = 0

    def get(self):
        x = self.items[self.i]
        self.i = (self.i + 1) % len(self.items)
        return x


PF_LRU_X, PF_LRU_G, PF_GLA_Q, PF_GLA_K, PF_DN_QKV, PF_S5_U, PF_S5_G, PF_GLA_LR = 0, 512, 1024, 1280, 1536, 3072, 3584, 4096
PF_ROWS = 4128
PF_CHUNKS = ([(O_LRU_X + 128 * i, 128) for i in range(4)] + [(O_LRU_G + 128 * i, 128) for i in range(4)]
             + [(O_GLA_Q + 128 * i, 128) for i in range(2)] + [(O_GLA_K + 128 * i, 128) for i in range(2)]
             + [(O_DN_QKV + 128 * i, 128) for i in range(12)] + [(O_S5_U + 128 * i, 128) for i in range(4)]
             + [(O_S5_G + 128 * i, 128) for i in range(4)] + [(O_GLA_LR, 32)])
PT_GLA_K, PT_GLA_V, PT_GLA_G, PT_DN_G, PT_DN_BA = 0, 256, 768, 1280, 1792
PT_COLS = 1808
PT_GROUPS = [(1280, 512, 0), (1792, 512, 512), (2304, 256, 1024), (4128, 512, 1280), (4640, 16, 1792)]


def load_cast_bf16(pg, nc, es, dst, src_ap, rows, cols, name, chunk=2048):
    st = [pg.buf(es.enter_context(nc.sbuf_tensor(name + "_st%d" % i, [128, chunk], F32)).ap()) for i in range(2)]
    i = 0
    for c0 in range(0, cols, chunk):
        cw = min(chunk, cols - c0)
        s = st[i % 2]
        pg.dma(s.ap[:rows, :cw], src_ap[:, c0:c0 + cw], writes=[s])
        if i % 2 == 0:
            pg.op("act", lambda e: e.copy(dst[0][:rows, c0:c0 + cw], s.ap[:rows, :cw]), reads=[s], writes=[dst[1]])
        else:
            pg.op("dve", lambda e: e.tensor_copy(out=dst[0][:rows, c0:c0 + cw], in_=s.ap[:rows, :cw]), reads=[s], writes=[dst[1]])
        i += 1


def phase_P(pg, cx, es, L, x_ap, l):
    nc = cx.nc
    TT = 512
    sb = lambda name, shape, dt: pg.buf(es.enter_context(nc.sbuf_tensor(pg.uname(name), shape, dt)).ap(), name)
    wbf = sb("P_w", [128, 8, D_IN], BF16)
    w_src = cx.w["w_in"][l].rearrange("(k p) c -> p k c", p=128)
    WC = D_IN // 4
    st = [sb("P_wst%d" % i, [128, WC], F32) for i in range(2)]
    for k in range(8):
        for q in range(4):
            s = st[q % 2]
            pg.dma(s.ap, w_src[:, k, q * WC:(q + 1) * WC], writes=[s])
            if q % 2 == 0:
                pg.op("act", lambda e: e.copy(wbf.ap[:, k, q * WC:(q + 1) * WC], s.ap), reads=[s], writes=[wbf])
            else:
                pg.op("dve", lambda e: e.tensor_copy(out=wbf.ap[:, k, q * WC:(q + 1) * WC], in_=s.ap), reads=[s], writes=[wbf])
    gk = sb("P_g", [128, 8], F32)
    load_T(pg, cx, gk, gk.ap, cx.w["norm_g"][l].rearrange("(k p) -> k p", p=128), 8)
    xt = [sb("P_x%d" % i, [128, 4, D], F32) for i in range(2)]
    xs = sb("P_xs", [128, D], BF16)
    junk = sb("P_junk", [128, D], BF16)
    ss = sb("P_ss", [128, 4], F32)
    xnT = [sb("P_xnT%d" % i, [128, 8, TT], BF16) for i in range(2)]
    stf = [sb("P_stf%d" % i, [128, 4, TT], F32) for i in range(2)]
    stt = [sb("P_stt%d" % i, [128, PT_COLS], F32) for i in range(1)]
    XNTv = cx.XNT.rearrange("(k p) t -> p k t", p=128)
    xv = x_ap.rearrange("(n j p) d -> n p j d", p=128, j=4)
    gb = gk.ap.unsqueeze(2).to_broadcast([128, 8, 128])
    nt = L // TT
    evac_i = 0
    for it in range(nt):
        x_b = xt[it % 2]
        pg.dma(x_b.ap, xv[it], writes=[x_b])
        xn = xnT[it % 2]
        for j in range(4):
            pg.op("act", lambda e: e.activation(out=junk.ap, in_=x_b.ap[:, j, :], func=AF.Square, accum_out=ss.ap[:, j:j + 1]),
                  reads=[x_b], writes=[junk, ss])
            pg.op("act", lambda e: e.activation(out=ss.ap[:, j:j + 1], in_=ss.ap[:, j:j + 1], func=AF.Sqrt, scale=1.0 / D, bias=cx.eps.ap[:, 0:1]),
                  reads=[ss, cx.eps], writes=[ss])
            pg.op("dve", lambda e: e.reciprocal(out=ss.ap[:, j:j + 1], in_=ss.ap[:, j:j + 1]), reads=[ss], writes=[ss])
            pg.op("dve", lambda e: e.tensor_scalar(out=xs.ap, in0=x_b.ap[:, j, :], scalar1=ss.ap[:, j:j + 1], scalar2=None, op0=ALU.mult),
                  reads=[x_b, ss], writes=[xs])
            pb = cx.psb.get()
            for k in range(8):
                pg.op("pe", lambda e: e.transpose(out=pb.ap[:, k * 128:(k + 1) * 128], in_=xs.ap[:, k * 128:(k + 1) * 128], identity=cx.identb.ap),
                      reads=[xs, cx.identb], writes=[pb])
            pg.op("dve", lambda e: e.tensor_tensor(out=xn.ap[:, :, j * 128:(j + 1) * 128], in0=pb.ap.rearrange("p (k t) -> p k t", k=8), in1=gb, op=ALU.mult),
                  reads=[pb, gk], writes=[xn])
        pg.dma(XNTv[:, :, it * TT:(it + 1) * TT], xn.ap, reads=[xn])
        for ci, (c0, cw) in enumerate(PF_CHUNKS):
            ps = cx.psf.get()
            for k in range(8):
                pg.op("pe", lambda e: e.matmul(ps.ap[:cw, :], lhsT=wbf.ap[:, k, c0:c0 + cw], rhs=xn.ap[:, k, :], start=(k == 0), stop=(k == 7)),
                      reads=[wbf, xn], writes=[ps])
            sbuf = stf[(ci // 4) % 2]
            evac_i += 1
            if evac_i % 2 == 0:
                pg.op("act", lambda e: e.copy(sbuf.ap[:cw, ci % 4, :], ps.ap[:cw, :]), reads=[ps], writes=[sbuf])
            else:
                pg.op("dve", lambda e: e.tensor_copy(out=sbuf.ap[:cw, ci % 4, :], in_=ps.ap[:cw, :]), reads=[ps], writes=[sbuf])
            if ci % 4 == 3:
                cb = ci // 4
                pg.dma(PFv_slice(cx, cb * 4, 4, it * TT, TT), sbuf.ap, reads=[sbuf])
            elif ci == len(PF_CHUNKS) - 1:
                pg.dma(cx.PF[4096:4128, it * TT:(it + 1) * TT], sbuf.ap[:32, 0, :], reads=[sbuf])
        for j in range(4):
            sbuf = stt[0]
            for (c0, cw, o0) in PT_GROUPS:
                ps = cx.psf.get()
                for k in range(8):
                    pg.op("pe", lambda e: e.matmul(ps.ap[:, :cw], lhsT=xn.ap[:, k, j * 128:(j + 1) * 128], rhs=wbf.ap[:, k, c0:c0 + cw], start=(k == 0), stop=(k == 7)),
                          reads=[wbf, xn], writes=[ps])
                evac_i += 1
                if evac_i % 2 == 0:
                    pg.op("act", lambda e: e.copy(sbuf.ap[:, o0:o0 + cw], ps.ap[:, :cw]), reads=[ps], writes=[sbuf])
                else:
                    pg.op("dve", lambda e: e.tensor_copy(out=sbuf.ap[:, o0:o0 + cw], in_=ps.ap[:, :cw]), reads=[ps], writes=[sbuf])
            t0 = it * TT + j * 128
            pg.dma(cx.PT[t0:t0 + 128, :], sbuf.ap, reads=[sbuf])


def PFv_slice(cx, c0, nch, t0, tw):
    return cx.PF[c0 * 128:(c0 + nch) * 128, t0:t0 + tw].rearrange("(c p) t -> p c t", p=128)


def load_T(pg, cx, dst, dst_ap, src_ap, n, st_view=None, wd=128, **kw):
    st = cx.ldst
    pg.dma(st.ap[:n, :wd] if st_view is None else st_view(st.ap[:n, :wd]), src_ap, writes=[st], **kw)
    ps = cx.psf.get()
    pg.op("pe", lambda e: e.transpose(out=ps.ap[:wd, :n], in_=st.ap[:n, :wd], identity=cx.identf.ap[:n, :n]), reads=[st, cx.identf], writes=[ps])
    pg.op("dve", lambda e: e.tensor_copy(out=dst_ap, in_=ps.ap[:wd, :n]), reads=[ps], writes=[dst])


def phase_LRU(pg, cx, es, L, l):
    nc = cx.nc
    sb = lambda name, shape, dt: pg.buf(es.enter_context(nc.sbuf_tensor(pg.uname(name), shape, dt)).ap(), name)
    w = cx.w
    TL = min(2048, L)
    ntile = L // TL
    cw = sb("L_cw", [128, 4, 4], F32)
    load_T(pg, cx, cw, cw.ap.rearrange("p j c -> p (j c)"), w["lru_conv_w"][l].rearrange("j (c p) -> (j c) p", p=128), 16)
    cb = sb("L_cb", [128, 4], F32)
    load_T(pg, cx, cb, cb.ap, w["lru_conv_b"][l].rearrange("(c p) -> c p", p=128), 4)
    bias = sb("L_bias", [128, 2, 2, 4], F32)
    load_T(pg, cx, bias, bias.ap[:, 0].rearrange("p d c -> p (d c)"), w["lru_b_a"][l].rearrange("d (c p) -> (d c) p", p=128), 8)
    load_T(pg, cx, bias, bias.ap[:, 1].rearrange("p d c -> p (d c)"), w["lru_b_x"][l].rearrange("d (c p) -> (d c) p", p=128), 8)
    lam = sb("L_lam", [128, 2, 4], F32)
    load_T(pg, cx, lam, lam.ap.rearrange("p d c -> p (d c)"), w["lru_lambda"][l].rearrange("d (c p) -> (d c) p", p=128), 8)
    coef = sb("L_coef", [128, 2, 4], F32)
    coef2 = sb("L_coef2", [128, 2, 4], F32)
    pg.op("act", lambda e: e.activation(out=coef.ap, in_=lam.ap, func=AF.Exp, scale=-1.0), reads=[lam], writes=[coef])
    pg.op("act", lambda e: e.activation(out=coef.ap, in_=coef.ap, func=AF.Ln, bias=cx.one.ap[:, 0:1]), reads=[coef, cx.one], writes=[coef])
    pg.op("dve", lambda e: e.tensor_scalar(out=coef2.ap, in0=coef.ap, scalar1=-16.0, scalar2=None, op0=ALU.mult), reads=[coef], writes=[coef2])
    pg.op("dve", lambda e: e.tensor_scalar(out=coef.ap, in0=coef.ap, scalar1=-8.0, scalar2=None, op0=ALU.mult), reads=[coef], writes=[coef])
    wg = sb("L_wg", [128, 2, 2, 4, 128], BF16)
    wst = sb("L_wst", [128, 2, 4, 128], F32)
    for ai, nm in enumerate(("lru_w_a", "lru_w_x")):
        pg.dma(wst.ap, w[nm][l].rearrange("d h i j -> i d h j"), writes=[wst])
        pg.op("dve", lambda e: e.tensor_copy(out=wg.ap[:, ai], in_=wst.ap), reads=[wst], writes=[wg])
    XC = sb("L_XC", [128, L], F32)
    XCB = sb("L_XCB", [128, L], BF16)
    HF = sb("L_HF", [128, L], F32)
    xin = sb("L_xin", [128, TL + 3], F32)
    rt = sb("L_r", [128, TL], F32)
    itl = sb("L_i", [128, TL], F32)
    at = sb("L_a", [128, TL], F32)
    t2 = sb("L_t2", [128, TL], F32)
    gt = sb("L_g", [128, TL], F32)
    yb = sb("L_y", [128, TL], BF16)
    carry = sb("L_carry", [128, 1], F32)
    for c in range(4):
        prow = PF_LRU_X + c * 128
        for it in range(ntile):
            t0 = it * TL
            lo = max(t0 - 2, 0)
            hi = min(t0 + TL + 1, L)
            if it == 0 or it == ntile - 1:
                pg.op("pool", lambda e: e.memset(xin.ap, 0.0), writes=[xin])
            pg.dma(xin.ap[:, lo - (t0 - 2):hi - (t0 - 2)], cx.PF[prow:prow + 128, lo:hi], writes=[xin])
            xo = XC.ap[:, t0:t0 + TL]
            pg.op("dve", lambda e: e.tensor_scalar(out=xo, in0=xin.ap[:, 0:TL], scalar1=cw.ap[:, 0, c:c + 1], scalar2=cb.ap[:, c:c + 1], op0=ALU.mult, op1=ALU.add),
                  reads=[xin, cw, cb], writes=[XC])
            for j in range(1, 4):
                pg.op("dve", lambda e: e.scalar_tensor_tensor(out=xo, in0=xin.ap[:, j:j + TL], scalar=cw.ap[:, j, c:c + 1], in1=xo, op0=ALU.mult, op1=ALU.add),
                      reads=[xin, cw, XC], writes=[XC])
            pg.op("act", lambda e: e.copy(XCB.ap[:, t0:t0 + TL], xo), reads=[XC], writes=[XCB])
        for d in range(2):
            order = range(ntile) if d == 0 else range(ntile - 1, -1, -1)
            for n_i, it in enumerate(order):
                t0 = it * TL
                for s0 in range(0, TL, 512):
                    for ai, dst in ((0, rt), (1, itl)):
                        ps = cx.psf.get()
                        pg.op("pe", lambda e: e.matmul(ps.ap, lhsT=wg.ap[:, ai, d, c, :], rhs=XCB.ap[:, t0 + s0:t0 + s0 + 512], start=True, stop=True),
                              reads=[wg, XCB], writes=[ps])
                        pg.op("act", lambda e: e.activation(out=dst.ap[:, s0:s0 + 512], in_=ps.ap, func=AF.Sigmoid, bias=bias.ap[:, ai, d, c:c + 1]),
                              reads=[ps, bias], writes=[dst])
                pg.op("act", lambda e: e.activation(out=at.ap, in_=rt.ap, func=AF.Exp, scale=coef.ap[:, d, c:c + 1]), reads=[rt, coef], writes=[at])
                pg.op("act", lambda e: e.activation(out=t2.ap, in_=rt.ap, func=AF.Exp, scale=coef2.ap[:, d, c:c + 1]), reads=[rt, coef2], writes=[t2])
                pg.op("act", lambda e: e.activation(out=t2.ap, in_=t2.ap, func=AF.Sqrt, scale=-1.0, bias=cx.one.ap[:, 0:1]), reads=[t2, cx.one], writes=[t2])
                pg.op("pool", lambda e: e.tensor_tensor(out=itl.ap, in0=itl.ap, in1=XC.ap[:, t0:t0 + TL], op=ALU.mult), reads=[itl, XC], writes=[itl])
                pg.op("dve", lambda e: e.tensor_tensor(out=t2.ap, in0=t2.ap, in1=itl.ap, op=ALU.mult), reads=[t2, itl], writes=[t2])
                init = 0.0 if n_i == 0 else carry.ap[:, 0:1]
                rds = [at, t2] + ([] if n_i == 0 else [carry])
                if d == 0:
                    ho = HF.ap[:, t0:t0 + TL]
                    pg.op("dve", lambda e: e.tensor_tensor_scan(out=ho, data0=at.ap, data1=t2.ap, initial=init, op0=ALU.mult, op1=ALU.add),
                          reads=rds, writes=[HF])
                    pg.op("dve", lambda e: e.tensor_copy(out=carry.ap, in_=HF.ap[:, t0 + TL - 1:t0 + TL]), reads=[HF], writes=[carry])
                else:
                    rv = lambda ap: bass.AP(ap.tensor, ap.offset + TL - 1, [list(ap.ap[0]), [-1, TL]])
                    pg.op("dve", lambda e: e.tensor_tensor_scan(out=rv(rt.ap), data0=rv(at.ap), data1=rv(t2.ap), initial=init, op0=ALU.mult, op1=ALU.add),
                          reads=rds, writes=[rt])
                    pg.op("dve", lambda e: e.tensor_copy(out=carry.ap, in_=rt.ap[:, 0:1]), reads=[rt], writes=[carry])
                    grow = PF_LRU_G + c * 128
                    pg.dma(gt.ap, cx.PF[grow:grow + 128, t0:t0 + TL], writes=[gt])
                    pg.op("act", lambda e: e.activation(out=gt.ap, in_=gt.ap, func=AF.Silu), reads=[gt], writes=[gt])
                    pg.op("pool", lambda e: e.tensor_tensor(out=rt.ap, in0=rt.ap, in1=HF.ap[:, t0:t0 + TL], op=ALU.add), reads=[rt, HF], writes=[rt])
                    pg.op("dve", lambda e: e.tensor_tensor(out=yb.ap, in0=rt.ap, in1=gt.ap, op=ALU.mult), reads=[rt, gt], writes=[yb])
                    pg.dma(cx.BT[c * 128:(c + 1) * 128, t0:t0 + TL], yb.ap, reads=[yb])


def phase_M(pg, cx, es, L, x_ap, xout_ap, l, last):
    nc = cx.nc
    TT = 512
    sb = lambda name, shape, dt: pg.buf(es.enter_context(nc.sbuf_tensor(pg.uname(name), shape, dt)).ap(), name)
    w = cx.w
    wmg = sb("M_wmg", [128, 4, 8, D], BF16)
    wbr = sb("M_wbr", [128, 4, 4, D], BF16)
    wout = sb("M_wout", [128, 8, D], BF16)
    st = [sb("M_st%d" % i, [128, D], F32) for i in range(2)]
    jobs = []
    for n in range(4):
        for k in range(8):
            jobs.append((w["w_merge_gate"][l, n, k * 128:(k + 1) * 128, :], wmg, wmg.ap[:, n, k, :]))
        for k in range(4):
            jobs.append((w["w_branch"][l, n, k * 128:(k + 1) * 128, :], wbr, wbr.ap[:, n, k, :]))
    for k in range(8):
        jobs.append((w["w_out"][l, k * 128:(k + 1) * 128, :], wout, wout.ap[:, k, :]))
    for i, (src, dbuf, dap) in enumerate(jobs):
        s = st[i % 2]
        pg.dma(s.ap, src, writes=[s])
        if i % 2 == 0:
            pg.op("act", lambda e: e.copy(dap, s.ap), reads=[s], writes=[dbuf])
        else:
            pg.op("dve", lambda e: e.tensor_copy(out=dap, in_=s.ap), reads=[s], writes=[dbuf])
    bmg = sb("M_bmg", [128, 4, 8], F32)
    load_T(pg, cx, bmg, bmg.ap.rearrange("p n c -> p (n c)"), w["b_merge_gate"][l].rearrange("n (c p) -> (n c) p", p=128), 32)
    if last:
        fg = sb("M_fg", [128, D], F32)
        fsrc = w["final_norm_g"]
        pg.dma(fg.ap, bass.AP(fsrc.tensor, fsrc.offset, [[0, 128], [1, D]]), writes=[fg])
        ss = sb("M_ss", [128, 4], F32)
        junk = sb("M_junk", [128, D], BF16)
    xn = sb("M_xn", [128, 8, TT], BF16)
    bt = sb("M_bt", [128, 16, TT], BF16)
    xt = sb("M_x", [128, 4, D], F32)
    mg = sb("M_mg", [128, 8, TT], BF16)
    gsb = sb("M_g", [128, TT], F32)
    tmp = sb("M_tmp", [128, TT], F32)
    acc = sb("M_acc", [128, TT], F32)
    XNTv = cx.XNT.rearrange("(k p) t -> p k t", p=128)
    BTv = cx.BT.rearrange("(k p) t -> p k t", p=128)
    xv = x_ap.rearrange("(n j p) d -> n p j d", p=128, j=4)
    ov = xout_ap.rearrange("(n j p) d -> n p j d", p=128, j=4)
    for it in range(L // TT):
        ts = slice(it * TT, (it + 1) * TT)
        pg.dma(xn.ap, XNTv[:, :, ts], writes=[xn])
        pg.dma(bt.ap, BTv[:, :, ts], writes=[bt])
        pg.dma(xt.ap, xv[it], writes=[xt])
        for oc in range(8):
            ocs = slice(oc * 128, (oc + 1) * 128)
            for n in range(4):
                pg_ = cx.psf.get()
                for k in range(8):
                    pg.op("pe", lambda e: e.matmul(pg_.ap, lhsT=wmg.ap[:, n, k, ocs], rhs=xn.ap[:, k, :], start=(k == 0), stop=(k == 7)),
                          reads=[wmg, xn], writes=[pg_])
                pg.op("act", lambda e: e.activation(out=gsb.ap, in_=pg_.ap, func=AF.Sigmoid, bias=bmg.ap[:, n, oc:oc + 1]), reads=[pg_, bmg], writes=[gsb])
                pb = cx.psf.get()
                for k in range(4):
                    pg.op("pe", lambda e: e.matmul(pb.ap, lhsT=wbr.ap[:, n, k, ocs], rhs=bt.ap[:, n * 4 + k, :], start=(k == 0), stop=(k == 3)),
                          reads=[wbr, bt], writes=[pb])
                if n == 0:
                    pg.op("dve", lambda e: e.tensor_tensor(out=acc.ap, in0=pb.ap, in1=gsb.ap, op=ALU.mult), reads=[pb, gsb], writes=[acc])
                else:
                    pg.op("dve", lambda e: e.tensor_tensor(out=tmp.ap, in0=pb.ap, in1=gsb.ap, op=ALU.mult), reads=[pb, gsb], writes=[tmp])
                    if n < 3:
                        pg.op("pool", lambda e: e.tensor_tensor(out=acc.ap, in0=acc.ap, in1=tmp.ap, op=ALU.add), reads=[acc, tmp], writes=[acc])
                    else:
                        pg.op("pool", lambda e: e.tensor_tensor(out=mg.ap[:, oc, :], in0=acc.ap, in1=tmp.ap, op=ALU.add), reads=[acc, tmp], writes=[mg])
        for j in range(4):
            for hf in range(2):
                hs = slice(hf * 512, (hf + 1) * 512)
                ps = cx.psf.get()
                for k in range(8):
                    pg.op("pe", lambda e: e.matmul(ps.ap, lhsT=mg.ap[:, k, j * 128:(j + 1) * 128], rhs=wout.ap[:, k, hs], start=(k == 0), stop=(k == 7)),
                          reads=[mg, wout], writes=[ps])
                pg.op("dve", lambda e: e.tensor_tensor(out=xt.ap[:, j, hs], in0=ps.ap, in1=xt.ap[:, j, hs], op=ALU.add), reads=[ps, xt], writes=[xt])
            if last:
                pg.op("act", lambda e: e.activation(out=junk.ap, in_=xt.ap[:, j, :], func=AF.Square, accum_out=ss.ap[:, j:j + 1]), reads=[xt], writes=[junk, ss])
                pg.op("act", lambda e: e.activation(out=ss.ap[:, j:j + 1], in_=ss.ap[:, j:j + 1], func=AF.Sqrt, scale=1.0 / D, bias=cx.eps.ap[:, 0:1]),
                      reads=[ss, cx.eps], writes=[ss])
                pg.op("dve", lambda e: e.reciprocal(out=ss.ap[:, j:j + 1], in_=ss.ap[:, j:j + 1]), reads=[ss], writes=[ss])
                pg.op("dve", lambda e: e.scalar_tensor_tensor(out=xt.ap[:, j, :], in0=xt.ap[:, j, :], scalar=ss.ap[:, j:j + 1], in1=fg.ap, op0=ALU.mult, op1=ALU.mult),
                      reads=[xt, ss, fg], writes=[xt])
        pg.dma(ov[it], xt.ap, reads=[xt])


def phase_GLA(pg, cx, es, L, l):
    nc = cx.nc
    sb = lambda name, shape, dt: pg.buf(es.enter_context(nc.sbuf_tensor(pg.uname(name), shape, dt)).ap(), name)
    w = cx.w
    NB = L // 128
    wup = sb("G_wup", [32, 2, 256], F32)
    for d in range(2):
        pg.dma(wup.ap[0:16, d, :], w["gla_w_up"][l, d], writes=[wup])
        pg.dma(wup.ap[16:17, d, :], w["gla_b_up"][l, d:d + 1, :], writes=[wup])
    gn = sb("G_gn", [128, 128], F32)
    gsrc = w["gla_norm_g"][l]
    pg.dma(gn.ap, bass.AP(gsrc.tensor, gsrc.offset, [[0, 128], [1, 128]]), writes=[gn])
    lrT = [sb("G_lrT%d" % i, [32, 128], F32) for i in range(2)]
    for b in lrT:
        pg.op("dve", lambda e: e.memset(b.ap, 1.0), writes=[b])
    qk = [sb("G_qk%d" % i, [128, 4, 128], F32) for i in range(2)]
    tk = [sb("G_tk%d" % i, [128, 1280], F32) for i in range(2)]
    obt = [sb("G_ob%d" % i, [128, 512], F32) for i in range(2)]
    la = sb("G_la", [128, 256], F32)
    e1 = sb("G_e1", [128, 256], F32)
    eb = sb("G_eb", [128, 2, 128], F32)
    enb = sb("G_enb", [128, 2, 128], F32)
    qd = sb("G_qd", [128, 2, 128], BF16)
    ki = sb("G_ki", [128, 2, 128], BF16)
    ed = sb("G_ed", [128, 256], F32)
    kend = sb("G_kend", [128, 256], BF16)
    vb = sb("G_vb", [128, 512], BF16)
    sm = [sb("G_sm%d" % i, [128, 128], BF16) for i in range(2)]
    S32 = [sb("G_S32_%d" % h, [128, 128], F32) for h in range(4)]
    Sb = [sb("G_Sb_%d" % h, [128, 128], BF16) for h in range(4)]
    osb = sb("G_osb", [128, 512], F32)
    ssq = sb("G_ssq", [128, 4], F32)
    junk = sb("G_junk", [128, 128], BF16)
    ysb = sb("G_ysb", [128, 512], BF16)
    yT = sb("G_yT", [128, 4, 128], BF16)
    PFq = cx.PF[PF_GLA_Q:PF_GLA_Q + 512, :].rearrange("(c p) t -> p c t", p=128)
    for d in (1, 0):
        pg.barrier()
        for h in range(4):
            pg.op("dve", lambda e: e.memset(S32[h].ap, 0.0), writes=[S32[h]])
            pg.op("pool", lambda e: e.memset(Sb[h].ap, 0.0), writes=[Sb[h]])
        order = range(NB) if d == 0 else range(NB - 1, -1, -1)
        for bi, blk in enumerate(order):
            t0 = blk * 128
            ts = slice(t0, t0 + 128)
            qkb = qk[bi % 2]; tkb = tk[bi % 2]; lrb = lrT[bi % 2]; ob = obt[bi % 2]
            pg.dma(qkb.ap, PFq[:, :, ts], writes=[qkb])
            pg.dma(tkb.ap, cx.PT[ts, 0:1280], writes=[tkb])
            pg.dma(lrb.ap[0:16, :], cx.PF[PF_GLA_LR + 16 * d:PF_GLA_LR + 16 * d + 16, ts], writes=[lrb])
            if d == 0:
                pg.dma(ob.ap, cx.OB[ts, 0:512], writes=[ob])
            zp = cx.psf.get()
            pg.op("pe", lambda e: e.matmul(zp.ap[:, :256], lhsT=lrb.ap[0:17, :], rhs=wup.ap[0:17, d, :], start=True, stop=True), reads=[lrb, wup], writes=[zp])
            pg.op("act", lambda e: e.activation(out=e1.ap, in_=zp.ap[:, :256], func=AF.Exp, scale=-1.0), reads=[zp], writes=[e1])
            pg.op("act", lambda e: e.activation(out=e1.ap, in_=e1.ap, func=AF.Ln, bias=cx.one.ap[:, 0:1]), reads=[e1, cx.one], writes=[e1])
            pg.op("dve", lambda e: e.tensor_scalar(out=la.ap, in0=e1.ap, scalar1=-1.0 / 16.0, scalar2=None, op0=ALU.mult), reads=[e1], writes=[la])
            bp = cx.psf.get()
            for h2 in range(2):
                pg.op("pe", lambda e: e.matmul(bp.ap[:, h2 * 128:(h2 + 1) * 128], lhsT=la.ap[:, h2 * 128:(h2 + 1) * 128], rhs=cx.m_incl.ap[:, d, :], start=True, stop=True),
                      reads=[la, cx.m_incl], writes=[bp])
            bp3 = bp.ap[:, 0:256].rearrange("p (c t) -> p c t", c=2)
            pg.op("act", lambda e: e.activation(out=eb.ap, in_=bp3, func=AF.Exp), reads=[bp], writes=[eb])
            pg.op("act", lambda e: e.activation(out=enb.ap, in_=bp3, func=AF.Exp, scale=-1.0), reads=[bp], writes=[enb])
            pg.op("dve", lambda e: e.scalar_tensor_tensor(out=qd.ap, in0=qkb.ap[:, 0:2, :], scalar=0.125, in1=eb.ap, op0=ALU.mult, op1=ALU.mult), reads=[qkb, eb], writes=[qd])
            pg.op("pool", lambda e: e.tensor_tensor(out=ki.ap, in0=qkb.ap[:, 2:4, :], in1=enb.ap, op=ALU.mult), reads=[qkb, enb], writes=[ki])
            dp = cx.psf.get()
            pg.op("pe", lambda e: e.matmul(dp.ap[:, :256], lhsT=cx.m_sa.ap[:, d, :], rhs=la.ap, start=True, stop=True), reads=[la, cx.m_sa], writes=[dp])
            pg.op("act", lambda e: e.activation(out=ed.ap, in_=dp.ap[:, :256], func=AF.Exp), reads=[dp], writes=[ed])
            pg.op("dve", lambda e: e.tensor_tensor(out=kend.ap, in0=tkb.ap[:, 0:256], in1=ed.ap, op=ALU.mult), reads=[tkb, ed], writes=[kend])
            pg.op("pool", lambda e: e.tensor_copy(out=vb.ap, in_=tkb.ap[:, 256:768]), reads=[tkb], writes=[vb])
            op_ = cx.pso
            chunks = (0, 1) if d == 0 else (1, 0)
            for h in range(4):
                h2, hp = h // 2, (h % 2) * 64
                hc = slice(h * 128, (h + 1) * 128)
                sp_ = cx.psf.get()
                pg.op("pe", lambda e: e.matmul(sp_.ap[:, :128], lhsT=ki.ap[hp:hp + 64, h2, :], rhs=qd.ap[hp:hp + 64, h2, :], start=True, stop=True), reads=[ki, qd], writes=[sp_])
                smb = sm[h % 2]
                pg.op("dve", lambda e: e.tensor_tensor(out=smb.ap, in0=sp_.ap[:, :128], in1=cx.m_incl.ap[:, d, :], op=ALU.mult), reads=[sp_, cx.m_incl], writes=[smb])
                pg.op("pe", lambda e: e.matmul(op_.ap[:, hc], lhsT=smb.ap, rhs=vb.ap[:, hc], start=True, stop=False), reads=[smb, vb], writes=[op_])
                for ci, c in enumerate(chunks):
                    r0 = c * 64
                    pg.op("pe", lambda e: e.matmul(op_.ap[r0:r0 + 64, hc], lhsT=qd.ap[hp:hp + 64, h2, r0:r0 + 64], rhs=Sb[h].ap[hp:hp + 64, :], start=False, stop=(ci == 1)),
                          reads=[qd, Sb[h]], writes=[op_])
                    kvp = cx.psf.get()
                    pg.op("pe", lambda e: e.matmul(kvp.ap[hp:hp + 64, :128], lhsT=kend.ap[r0:r0 + 64, h * 64:(h + 1) * 64], rhs=vb.ap[r0:r0 + 64, hc], start=True, stop=True),
                          reads=[kend, vb], writes=[kvp])
                    col = r0 + 63 if d == 0 else r0
                    pg.op("dve", lambda e: e.scalar_tensor_tensor(out=S32[h].ap[hp:hp + 64, :], in0=S32[h].ap[hp:hp + 64, :], scalar=eb.ap[hp:hp + 64, h2, col:col + 1],
                                                                  in1=kvp.ap[hp:hp + 64, :128], op0=ALU.mult, op1=ALU.add), reads=[S32[h], eb, kvp], writes=[S32[h]])
                    pg.op("act", lambda e: e.copy(Sb[h].ap[hp:hp + 64, :], S32[h].ap[hp:hp + 64, :]), reads=[S32[h]], writes=[Sb[h]])
            if d == 1:
                pg.op("act", lambda e: e.copy(osb.ap, op_.ap), reads=[op_], writes=[osb])
                pg.dma(cx.OB[ts, 0:512], osb.ap, reads=[osb])
            else:
                pg.op("dve", lambda e: e.tensor_tensor(out=osb.ap, in0=op_.ap, in1=ob.ap, op=ALU.add), reads=[op_, ob], writes=[osb])
                head_norm_gate_store(pg, cx, osb, ssq, junk, gn, tkb, 768, ysb, yT, 512, ts)


def head_norm_gate_store(pg, cx, osb, ssq, junk, gn, tkb, gcol, ysb, yT, bt_row0, ts):
    for h in range(4):
        hc = slice(h * 128, (h + 1) * 128)
        pg.op("act", lambda e: e.activation(out=junk.ap, in_=osb.ap[:, hc], func=AF.Square, accum_out=ssq.ap[:, h:h + 1]), reads=[osb], writes=[junk, ssq])
    pg.op("act", lambda e: e.activation(out=ssq.ap, in_=ssq.ap, func=AF.Sqrt, scale=1.0 / 128.0, bias=cx.eps.ap[:, 0:1]), reads=[ssq, cx.eps], writes=[ssq])
    pg.op("dve", lambda e: e.reciprocal(out=ssq.ap, in_=ssq.ap), reads=[ssq], writes=[ssq])
    for h in range(4):
        hc = slice(h * 128, (h + 1) * 128)
        pg.op("dve", lambda e: e.scalar_tensor_tensor(out=osb.ap[:, hc], in0=osb.ap[:, hc], scalar=ssq.ap[:, h:h + 1], in1=gn.ap, op0=ALU.mult, op1=ALU.mult),
              reads=[osb, ssq, gn], writes=[osb])
    pg.op("act", lambda e: e.activation(out=tkb.ap[:, gcol:gcol + 512], in_=tkb.ap[:, gcol:gcol + 512], func=AF.Silu), reads=[tkb], writes=[tkb])
    pg.op("dve", lambda e: e.tensor_tensor(out=ysb.ap, in0=osb.ap, in1=tkb.ap[:, gcol:gcol + 512], op=ALU.mult), reads=[osb, tkb], writes=[ysb])
    pb = cx.psb.get()
    for h in range(4):
        pg.op("pe", lambda e: e.transpose(out=pb.ap[:, h * 128:(h + 1) * 128], in_=ysb.ap[:, h * 128:(h + 1) * 128], identity=cx.identb.ap), reads=[ysb, cx.identb], writes=[pb])
    pg.op("act", lambda e: e.copy(yT.ap, pb.ap[:, 0:512].rearrange("p (c t) -> p c t", c=4)), reads=[pb], writes=[yT])
    pg.dma(cx.BT[bt_row0:bt_row0 + 512, ts].rearrange("(c p) t -> p c t", p=128), yT.ap, reads=[yT])


def phase_DN(pg, cx, es, L, l):
    nc = cx.nc
    w = cx.w
    NB = L // 128
    with ExitStack() as es0:
        sb = lambda name, shape, dt: pg.buf(es0.enter_context(nc.sbuf_tensor(pg.uname(name), shape, dt)).ap(), name)
        TL = 512
        cwD = sb("D0_cw", [128, 4, 12], F32)
        load_T(pg, cx, cwD, cwD.ap.rearrange("p j c -> p (j c)"), w["dn_conv_w"][l].rearrange("j (c p) -> (j c) p", p=128), 48)
        xin = [sb("D0_xin%d" % i, [128, TL + 3], F32) for i in range(2)]
        xc = sb("D0_xc", [128, TL], F32)
        sq = sb("D0_sq", [128, TL], F32)
        rs = sb("D0_rs", [128, TL], F32)
        fm = sb("D0_fm", [128, 12, TL], BF16)
        tm = sb("D0_tm", [128, 4, 1024], BF16)
        nt = L // TL
        for it in range(nt):
            t0 = it * TL
            lo, hi = max(t0 - 2, 0), min(t0 + TL + 1, L)
            for c in range(12):
                xb = xin[c % 2]
                if it == 0 or it == nt - 1:
                    pg.op("pool", lambda e: e.memset(xb.ap, 0.0), writes=[xb])
                prow = PF_DN_QKV + c * 128
                pg.dma(xb.ap[:, lo - (t0 - 2):hi - (t0 - 2)], cx.PF[prow:prow + 128, lo:hi], writes=[xb])
                pg.op("dve", lambda e: e.tensor_scalar(out=xc.ap, in0=xb.ap[:, 0:TL], scalar1=cwD.ap[:, 0, c:c + 1], scalar2=None, op0=ALU.mult), reads=[xb, cwD], writes=[xc])
                for j in range(1, 4):
                    pg.op("dve", lambda e: e.scalar_tensor_tensor(out=xc.ap, in0=xb.ap[:, j:j + TL], scalar=cwD.ap[:, j, c:c + 1], in1=xc.ap, op0=ALU.mult, op1=ALU.add),
                          reads=[xb, cwD, xc], writes=[xc])
                if c >= 8:
                    pg.op("act", lambda e: e.activation(out=fm.ap[:, c, :], in_=xc.ap, func=AF.Silu), reads=[xc], writes=[fm])
                else:
                    pg.op("act", lambda e: e.activation(out=xc.ap, in_=xc.ap, func=AF.Silu), reads=[xc], writes=[xc])
                    pg.op("pool", lambda e: e.tensor_tensor(out=sq.ap, in0=xc.ap, in1=xc.ap, op=ALU.mult), reads=[xc], writes=[sq])
                    ps = cx.psf.get()
                    pg.op("pe", lambda e: e.matmul(ps.ap, lhsT=cx.onesf.ap, rhs=sq.ap, start=True, stop=True), reads=[cx.onesf, sq], writes=[ps])
                    pg.op("act", lambda e: e.activation(out=rs.ap, in_=ps.ap, func=AF.Sqrt, bias=cx.eps.ap[:, 0:1]), reads=[ps, cx.eps], writes=[rs])
                    pg.op("dve", lambda e: e.reciprocal(out=rs.ap, in_=rs.ap), reads=[rs], writes=[rs])
                    sc = (128.0 ** -0.5) if c < 4 else 1.0
                    pg.op("dve", lambda e: e.scalar_tensor_tensor(out=fm.ap[:, c, :], in0=xc.ap, scalar=sc, in1=rs.ap, op0=ALU.mult, op1=ALU.mult), reads=[xc, rs], writes=[fm])
            pg.dma(cx.QKT[:, t0:t0 + TL].rearrange("(c p) t -> p c t", p=128), fm.ap[:, 0:8, :], reads=[fm])
            for j in range(4):
                pb = cx.psb.get()
                for c in range(8):
                    pg.op("pe", lambda e: e.transpose(out=pb.ap[:, c * 128:(c + 1) * 128], in_=fm.ap[:, 4 + c, j * 128:(j + 1) * 128], identity=cx.identb.ap),
                          reads=[fm, cx.identb], writes=[pb])
                pg.op("act", lambda e: e.copy(tm.ap[:, j, :], pb.ap), reads=[pb], writes=[tm])
            pg.dma(cx.KVT[t0:t0 + TL, :].rearrange("(j p) c -> p j c", p=128), tm.ap, reads=[tm])
        pg.barrier()
    sb = lambda name, shape, dt: pg.buf(es.enter_context(nc.sbuf_tensor(pg.uname(name), shape, dt)).ap(), name)
    ba = sb("D_ba", [128, NB, 16], F32)
    pg.dma(ba.ap, cx.PT[:, PT_DN_BA:PT_DN_BA + 16].rearrange("(n p) c -> p n c", p=128), writes=[ba])
    ba4 = ba.ap.rearrange("p n (d j h) -> p n d j h", d=2, j=2)
    dtb = sb("D_dtb", [128, 8], F32)
    nea = sb("D_nea", [128, 8], F32)
    s1 = w["dn_dt_bias"][l]
    pg.dma(dtb.ap, bass.AP(s1.tensor, s1.offset, [[0, 128], [1, 8]]), writes=[dtb])
    s2 = w["dn_a_log"][l]
    pg.dma(nea.ap, bass.AP(s2.tensor, s2.offset, [[0, 128], [1, 8]]), writes=[nea])
    pg.op("act", lambda e: e.activation(out=nea.ap, in_=nea.ap, func=AF.Exp), reads=[nea], writes=[nea])
    pg.op("dve", lambda e: e.tensor_scalar(out=nea.ap, in0=nea.ap, scalar1=-1.0, scalar2=None, op0=ALU.mult), reads=[nea], writes=[nea])
    beta = sb("D_beta", [128, NB, 2, 4], F32)
    nbeta = sb("D_nbeta", [128, NB, 2, 4], F32)
    g = sb("D_g", [128, NB, 2, 4], F32)
    pg.op("act", lambda e: e.activation(out=beta.ap, in_=ba4[:, :, :, 0, :], func=AF.Sigmoid), reads=[ba], writes=[beta])
    pg.op("dve", lambda e: e.tensor_scalar(out=nbeta.ap, in0=beta.ap, scalar1=-1.0, scalar2=None, op0=ALU.mult), reads=[beta], writes=[nbeta])
    dtb_b = dtb.ap.rearrange("p (d h) -> p d h", d=2).unsqueeze(1).to_broadcast([128, NB, 2, 4])
    nea_b = nea.ap.rearrange("p (d h) -> p d h", d=2).unsqueeze(1).to_broadcast([128, NB, 2, 4])
    pg.op("dve", lambda e: e.tensor_tensor(out=g.ap, in0=ba4[:, :, :, 1, :], in1=dtb_b, op=ALU.add), reads=[ba, dtb], writes=[g])
    pg.op("act", lambda e: e.activation(out=g.ap, in_=g.ap, func=AF.Exp), reads=[g], writes=[g])
    pg.op("act", lambda e: e.activation(out=g.ap, in_=g.ap, func=AF.Ln, bias=cx.one.ap[:, 0:1]), reads=[g, cx.one], writes=[g])
    pg.op("dve", lambda e: e.tensor_tensor(out=g.ap, in0=g.ap, in1=nea_b, op=ALU.mult), reads=[g, nea], writes=[g])
    eG = sb("D_eG", [128, NB, 2, 4], F32)
    eD = sb("D_eD", [128, NB, 2, 4], F32)
    bg = sb("D_bg", [128, NB, 2, 4], F32)
    deB = sb("D_deB", [128, 2, NB, 2, 4], F32)
    NQ = 32
    for d in range(2):
        for n0 in range(0, NB, NQ):
            nn = min(NQ, NB - n0)
            for (msk, dst, fn) in ((cx.m_incl.ap[:, d, :], eG, 0), (cx.m_sa.ap[:, d, :], eD, 0), (cx.chunkind.ap[:, 0, :], deB, 1), (cx.chunkind.ap[:, 1, :], deB, 2)):
                ps = cx.psf.get()
                pv = ps.ap[:, :nn * 4].rearrange("p (n h) -> p n h", h=4)
                pg.op("pe", lambda e: e.matmul(pv, lhsT=msk, rhs=g.ap[:, n0:n0 + nn, d, :], start=True, stop=True), reads=[g, cx.m_incl, cx.m_sa, cx.chunkind], writes=[ps])
                o_ap = dst.ap[:, n0:n0 + nn, d, :] if fn == 0 else dst.ap[:, fn - 1, n0:n0 + nn, d, :]
                pg.op("act", lambda e: e.activation(out=o_ap, in_=pv, func=AF.Exp), reads=[ps], writes=[dst])
    pg.op("dve", lambda e: e.tensor_tensor(out=bg.ap, in0=beta.ap, in1=eG.ap, op=ALU.mult), reads=[beta, eG], writes=[bg])
    gn = sb("D_gn", [128, 128], F32)
    gsrc = w["dn_norm_g"][l]
    pg.dma(gn.ap, bass.AP(gsrc.tensor, gsrc.offset, [[0, 128], [1, 128]]), writes=[gn])
    qk = [sb("D_qk%d" % i, [128, 8, 128], BF16) for i in range(2)]
    kv = [sb("D_kv%d" % i, [128, 2, 4, 128], BF16) for i in range(2)]
    gt = [sb("D_gt%d" % i, [128, 1280 + 512], F32) for i in range(1)]
    obt = [sb("D_ob%d" % i, [128, 512], F32) for i in range(2)]
    vb4 = sb("D_vb4", [128, 4, 128], BF16)
    kbg4 = sb("D_kbg4", [128, 4, 128], BF16)
    kend4 = sb("D_kend4", [128, 4, 128], BF16)
    gtri = [sb("D_gtri%d" % i, [128, 128], F32) for i in range(2)]
    gam = [sb("D_gam%d" % i, [128, 3, 128], F32) for i in range(2)]
    gamm = [sb("D_gamm%d" % i, [128, 2, 128], F32) for i in range(2)]
    qd = [sb("D_qd%d" % i, [128, 128], BF16) for i in range(2)]
    Cm = [sb("D_C%d" % i, [128, 128], BF16) for i in range(2)]
    attnT = [sb("D_at%d" % i, [128, 128], BF16) for i in range(2)]
    BC = [sb("D_BC%d" % i, [128, 2, 128], BF16) for i in range(4)]
    Pm = [sb("D_P%d" % i, [128, 128], BF16) for i in range(4)]
    Pm32 = [sb("D_P32_%d" % i, [128, 128], F32) for i in range(4)]
    usb = [sb("D_u%d" % i, [128, 128], F32) for i in range(2)]
    wT = [sb("D_wT%d" % i, [128, 128], BF16) for i in range(2)]
    vn = [sb("D_vn%d" % i, [128, 128], BF16) for i in range(2)]
    S32 = [sb("D_S32_%d" % h, [128, 128], F32) for h in range(4)]
    Sb = [sb("D_Sb_%d" % h, [128, 128], BF16) for h in range(4)]
    osb = sb("D_osb", [128, 512], F32)
    ssq = sb("D_ssq", [128, 4], F32)
    junk = sb("D_junk", [128, 128], BF16)
    ysb = sb("D_ysb", [128, 512], BF16)
    yT = sb("D_yT", [128, 4, 128], BF16)
    QKv = cx.QKT.rearrange("(c p) t -> p c t", p=128)
    rr = [0]
    for d in (1, 0):
        pg.barrier()
        for h in range(4):
            pg.op("dve", lambda e: e.memset(S32[h].ap, 0.0), writes=[S32[h]])
            pg.op("pool", lambda e: e.memset(Sb[h].ap, 0.0), writes=[Sb[h]])
        order = range(NB) if d == 0 else range(NB - 1, -1, -1)
        chunks = (0, 1) if d == 0 else (1, 0)
        for bi, blk in enumerate(order):
            t0 = blk * 128
            ts = slice(t0, t0 + 128)
            qkb = qk[bi % 2]; kvb = kv[bi % 2]; ob = obt[bi % 2]; gtb = gt[0]
            pg.dma(qkb.ap, QKv[:, :, ts], writes=[qkb])
            pg.dma(kvb.ap.rearrange("p a h c -> p (a h c)"), cx.KVT[ts, :], writes=[kvb])
            if d == 0:
                pg.dma(ob.ap, cx.OB[ts, 512:1024], writes=[ob])
                pg.dma(gtb.ap[:, 0:512], cx.PT[ts, PT_DN_G:PT_DN_G + 512], writes=[gtb])
            bcast = lambda t: t.ap[:, blk, d, :].unsqueeze(2).to_broadcast([128, 4, 128])
            pg.op("dve", lambda e: e.tensor_tensor(out=vb4.ap, in0=kvb.ap[:, 1], in1=bcast(beta), op=ALU.mult), reads=[kvb, beta], writes=[vb4])
            pg.op("pool", lambda e: e.tensor_tensor(out=kbg4.ap, in0=kvb.ap[:, 0], in1=bcast(bg), op=ALU.mult), reads=[kvb, bg], writes=[kbg4])
            pg.op("pool", lambda e: e.tensor_tensor(out=kend4.ap, in0=kvb.ap[:, 0], in1=bcast(eD), op=ALU.mult), reads=[kvb, eD], writes=[kend4])
            op_ = cx.pso
            for h in range(4):
                i2 = h % 2
                hc = slice(h * 128, (h + 1) * 128)
                gsc = g.ap[:, blk, d, h:h + 1]
                pg.op("dve", lambda e: e.tensor_scalar(out=gtri[i2].ap, in0=cx.m_incl.ap[:, d, :], scalar1=gsc, scalar2=None, op0=ALU.mult), reads=[cx.m_incl, g], writes=[gtri[i2]])
                dps = cx.psf.get()
                pg.op("pe", lambda e: e.matmul(dps.ap[:, 0:128], lhsT=gtri[i2].ap, rhs=cx.m_sa.ap[:, d, :], start=True, stop=True), reads=[gtri[i2], cx.m_sa], writes=[dps])
                pg.op("pe", lambda e: e.matmul(dps.ap[:, 128:256], lhsT=cx.m_sa.ap[:, d, :], rhs=gtri[i2].ap, start=True, stop=True), reads=[gtri[i2], cx.m_sa], writes=[dps])
                pg.op("pe", lambda e: e.matmul(dps.ap[:, 256:384], lhsT=cx.onesf.ap, rhs=gtri[i2].ap, start=True, stop=True), reads=[gtri[i2], cx.onesf], writes=[dps])
                pg.op("act", lambda e: e.activation(out=gam[i2].ap.rearrange("p a t -> p (a t)"), in_=dps.ap[:, 0:384], func=AF.Exp), reads=[dps], writes=[gam[i2]])
                pg.op("pool", lambda e: e.tensor_tensor(out=gamm[i2].ap, in0=gam[i2].ap[:, 0:2, :], in1=cx.m_dn.ap[:, d], op=ALU.mult), reads=[gam[i2], cx.m_dn], writes=[gamm[i2]])
                pg.op("pool", lambda e: e.tensor_tensor(out=qd[i2].ap, in0=qkb.ap[:, h, :], in1=gam[i2].ap[:, 2, :], op=ALU.mult), reads=[qkb, gam[i2]], writes=[qd[i2]])
                kps = cx.psf.get()
                pg.op("pe", lambda e: e.matmul(kps.ap[:, 0:128], lhsT=qkb.ap[:, 4 + h, :], rhs=qkb.ap[:, 4 + h, :], start=True, stop=True), reads=[qkb], writes=[kps])
                pg.op("pe", lambda e: e.matmul(kps.ap[:, 128:256], lhsT=qkb.ap[:, 4 + h, :], rhs=qkb.ap[:, h, :], start=True, stop=True), reads=[qkb], writes=[kps])
                pg.op("dve", lambda e: e.scalar_tensor_tensor(out=Cm[i2].ap, in0=kps.ap[:, 0:128], scalar=nbeta.ap[:, blk, d, h:h + 1], in1=gamm[i2].ap[:, 0, :], op0=ALU.mult, op1=ALU.mult),
                      reads=[kps, nbeta, gamm[i2]], writes=[Cm[i2]])
                pg.op("dve", lambda e: e.tensor_tensor(out=attnT[i2].ap, in0=kps.ap[:, 128:256], in1=gamm[i2].ap[:, 1, :], op=ALU.mult), reads=[kps, gamm[i2]], writes=[attnT[i2]])
                tb = cx.psb.get()
                pg.op("pe", lambda e: e.transpose(out=tb.ap[:, 0:128], in_=Cm[i2].ap, identity=cx.identb.ap), reads=[Cm[i2], cx.identb], writes=[tb])
                bc0 = BC[rr[0] % 4]; rr[0] += 1
                pg.op("act", lambda e: e.copy(bc0.ap[:, 0, :], tb.ap[:, 0:128]), reads=[tb], writes=[bc0])
                pg.op("pool", lambda e: e.tensor_copy(out=bc0.ap[:, 1, :], in_=Cm[i2].ap), reads=[Cm[i2]], writes=[bc0])
                p0 = Pm[rr[0] % 4]; p032 = Pm32[rr[0] % 4]; rr[0] += 1
                pg.op("pool", lambda e: e.tensor_tensor(out=p0.ap, in0=bc0.ap[:, 0, :], in1=cx.identb.ap, op=ALU.add), reads=[bc0, cx.identb], writes=[p0])
                pg.op("pool", lambda e: e.tensor_tensor(out=p032.ap, in0=bc0.ap[:, 0, :], in1=cx.identf.ap, op=ALU.add), reads=[bc0, cx.identf], writes=[p032])
                bcp, pp, pp32 = bc0, p0, p032
                for k in range(1, 6):
                    sq_ = cx.psf.get()
                    if k < 5:
                        pg.op("pe", lambda e: e.matmul(sq_.ap[:, 0:128], lhsT=bcp.ap[:, 1, :], rhs=bcp.ap[:, 0, :], start=True, stop=True), reads=[bcp], writes=[sq_])
                    pg.op("pe", lambda e: e.matmul(sq_.ap[:, 128:256], lhsT=bcp.ap[:, 0, :], rhs=bcp.ap[:, 1, :], start=True, stop=True), reads=[bcp], writes=[sq_])
                    bcn = BC[rr[0] % 4]; rr[0] += 1
                    if k < 5:
                        pg.op("act", lambda e: e.copy(bcn.ap.rearrange("p a t -> p (a t)"), sq_.ap[:, 0:256]), reads=[sq_], writes=[bcn])
                    else:
                        pg.op("act", lambda e: e.copy(bcn.ap[:, 1, :], sq_.ap[:, 128:256]), reads=[sq_], writes=[bcn])
                    pps = cx.psf.get()
                    pg.op("pe", lambda e: e.matmul(pps.ap[:, 0:128], lhsT=bcn.ap[:, 1, :], rhs=pp.ap, start=True, stop=True), reads=[bcn, pp], writes=[pps])
                    pn = Pm[rr[0] % 4]; pn32 = Pm32[rr[0] % 4]; rr[0] += 1
                    pg.op("dve", lambda e: e.tensor_tensor(out=pn32.ap, in0=pps.ap[:, 0:128], in1=pp32.ap, op=ALU.add), reads=[pps, pp32], writes=[pn32])
                    pg.op("pool", lambda e: e.tensor_copy(out=pn.ap, in_=pn32.ap), reads=[pn32], writes=[pn])
                    bcp, pp, pp32 = bcn, pn, pn32
                ups = cx.psf.get()
                pg.op("pe", lambda e: e.matmul(ups.ap[:, 0:128], lhsT=pp.ap, rhs=vb4.ap[:, h, :], start=True, stop=True), reads=[pp, vb4], writes=[ups])
                pg.op("pe", lambda e: e.matmul(ups.ap[:, 128:256], lhsT=kbg4.ap[:, h, :], rhs=pp.ap, start=True, stop=True), reads=[pp, kbg4], writes=[ups])
                pg.op("act", lambda e: e.copy(usb[i2].ap, ups.ap[:, 0:128]), reads=[ups], writes=[usb[i2]])
                pg.op("act", lambda e: e.copy(wT[i2].ap, ups.ap[:, 128:256]), reads=[ups], writes=[wT[i2]])
                for ci, c in enumerate(chunks):
                    r0 = c * 64
                    rs_ = slice(r0, r0 + 64)
                    wps = cx.psf.get()
                    pg.op("pe", lambda e: e.matmul(wps.ap[rs_, 0:128], lhsT=wT[i2].ap[:, rs_], rhs=Sb[h].ap, start=True, stop=True), reads=[wT[i2], Sb[h]], writes=[wps])
                    pg.op("dve", lambda e: e.scalar_tensor_tensor(out=vn[i2].ap[rs_, :], in0=wps.ap[rs_, 0:128], scalar=-1.0, in1=usb[i2].ap[rs_, :], op0=ALU.mult, op1=ALU.add), reads=[usb[i2], wps], writes=[vn[i2]])
                    pg.op("pe", lambda e: e.matmul(op_.ap[rs_, hc], lhsT=qd[i2].ap[:, rs_], rhs=Sb[h].ap, start=True, stop=False), reads=[qd[i2], Sb[h]], writes=[op_])
                    pg.op("pe", lambda e: e.matmul(op_.ap[rs_, hc], lhsT=attnT[i2].ap[rs_, rs_], rhs=vn[i2].ap[rs_, :], start=False, stop=True), reads=[attnT[i2], vn[i2]], writes=[op_])
                    kvp = cx.psf.get()
                    pg.op("pe", lambda e: e.matmul(kvp.ap[:, 0:128], lhsT=kend4.ap[rs_, h, :], rhs=vn[i2].ap[rs_, :], start=True, stop=True), reads=[kend4, vn[i2]], writes=[kvp])
                    pg.op("dve", lambda e: e.scalar_tensor_tensor(out=S32[h].ap, in0=S32[h].ap, scalar=deB.ap[:, c, blk, d, h:h + 1], in1=kvp.ap[:, 0:128], op0=ALU.mult, op1=ALU.add),
                          reads=[S32[h], deB, kvp], writes=[S32[h]])
                    pg.op("act", lambda e: e.copy(Sb[h].ap, S32[h].ap), reads=[S32[h]], writes=[Sb[h]])
            if d == 1:
                pg.op("act", lambda e: e.copy(osb.ap, op_.ap), reads=[op_], writes=[osb])
                pg.dma(cx.OB[ts, 512:1024], osb.ap, reads=[osb])
            else:
                pg.op("dve", lambda e: e.tensor_tensor(out=osb.ap, in0=op_.ap, in1=ob.ap, op=ALU.add), reads=[op_, ob], writes=[osb])
                head_norm_gate_store(pg, cx, osb, ssq, junk, gn, gtb, 0, ysb, yT, 1024, ts)


def phase_S5(pg, cx, es, L, l):
    nc = cx.nc
    w = cx.w
    sb = lambda name, shape, dt: pg.buf(es.enter_context(nc.sbuf_tensor(pg.uname(name), shape, dt)).ap(), name)
    NS = int(np.ceil(np.log2(L)))
    dve = lambda fn, r, wr: pg.op("dve", fn, reads=r, writes=wr)
    A = lambda nm: sb("S_" + nm, [128, 32], F32)
    lre, lim, dt_, ar, ai, m_, sn, cs, Are, Aim, t1, t2, t3, fre, fim, den = [A(n) for n in
        ("lre", "lim", "dt", "ar", "ai", "m", "sn", "cs", "Are", "Aim", "t1", "t2", "t3", "fre", "fim", "den")]
    load_T(pg, cx, lre, lre.ap, w["s5_lambda_re"][l].rearrange("d (gh gl) p -> (d gh) (gl p)", gl=2), 32)
    load_T(pg, cx, lim, lim.ap, w["s5_lambda_im"][l].rearrange("d (gh gl) p -> (d gh) (gl p)", gl=2), 32)
    ld2 = sb("S_ld2", [32, 2], F32)
    pg.dma(ld2.ap, w["s5_log_dt"][l].rearrange("d (gh gl) -> (d gh) gl", gl=2), writes=[ld2])
    stl = sb("S_stl", [32, 128], F32)
    for gl in range(2):
        dve(lambda e: e.tensor_copy(out=stl.ap[:, 64 * gl:64 * gl + 64], in_=ld2.ap[:, gl:gl + 1].to_broadcast([32, 64])), [ld2], [stl])
    psl = cx.psf.get()
    pg.op("pe", lambda e: e.transpose(out=psl.ap[:, :32], in_=stl.ap, identity=cx.identf.ap[:32, :32]), reads=[stl, cx.identf], writes=[psl])
    dve(lambda e: e.tensor_copy(out=dt_.ap, in_=psl.ap[:, :32]), [psl], [dt_])
    pg.op("act", lambda e: e.activation(out=dt_.ap, in_=dt_.ap, func=AF.Exp), reads=[dt_], writes=[dt_])
    dve(lambda e: e.tensor_tensor(out=ar.ap, in0=lre.ap, in1=dt_.ap, op=ALU.mult), [lre, dt_], [ar])
    dve(lambda e: e.tensor_tensor(out=ai.ap, in0=lim.ap, in1=dt_.ap, op=ALU.mult), [lim, dt_], [ai])
    pg.op("act", lambda e: e.activation(out=m_.ap, in_=ar.ap, func=AF.Exp, scale=1.0 / 16), reads=[ar], writes=[m_])
    pg.op("act", lambda e: e.activation(out=sn.ap, in_=ai.ap, func=AF.Sin, scale=1.0 / 16), reads=[ai], writes=[sn])
    pg.op("act", lambda e: e.activation(out=cs.ap, in_=ai.ap, func=AF.Sin, scale=1.0 / 16, bias=cx.halfpi.ap[:, 0:1]), reads=[ai, cx.halfpi], writes=[cs])
    dve(lambda e: e.tensor_tensor(out=Are.ap, in0=m_.ap, in1=cs.ap, op=ALU.mult), [m_, cs], [Are])
    dve(lambda e: e.tensor_tensor(out=Aim.ap, in0=m_.ap, in1=sn.ap, op=ALU.mult), [m_, sn], [Aim])

    def csquare(re, im):
        dve(lambda e: e.tensor_tensor(out=t1.ap, in0=re, in1=re, op=ALU.mult), [Are, PW], [t1])
        dve(lambda e: e.tensor_tensor(out=t2.ap, in0=im, in1=im, op=ALU.mult), [Aim, PW], [t2])
        dve(lambda e: e.tensor_tensor(out=t3.ap, in0=re, in1=im, op=ALU.mult), [Are, Aim, PW], [t3])

    PW = sb("S_PW", [128, 32, NS, 3], F32)
    for _ in range(4):
        csquare(Are.ap, Aim.ap)
        dve(lambda e: e.tensor_tensor(out=Are.ap, in0=t1.ap, in1=t2.ap, op=ALU.subtract), [t1, t2], [Are])
        dve(lambda e: e.tensor_scalar(out=Aim.ap, in0=t3.ap, scalar1=2.0, scalar2=None, op0=ALU.mult), [t3], [Aim])
    dve(lambda e: e.tensor_tensor(out=den.ap, in0=lre.ap, in1=lre.ap, op=ALU.mult), [lre], [den])
    dve(lambda e: e.tensor_tensor(out=t1.ap, in0=lim.ap, in1=lim.ap, op=ALU.mult), [lim], [t1])
    dve(lambda e: e.tensor_tensor(out=den.ap, in0=den.ap, in1=t1.ap, op=ALU.add), [den, t1], [den])
    dve(lambda e: e.reciprocal(out=den.ap, in_=den.ap), [den], [den])
    dve(lambda e: e.tensor_scalar(out=t3.ap, in0=Are.ap, scalar1=-1.0, scalar2=None, op0=ALU.add), [Are], [t3])
    dve(lambda e: e.tensor_tensor(out=t1.ap, in0=t3.ap, in1=lre.ap, op=ALU.mult), [t3, lre], [t1])
    dve(lambda e: e.tensor_tensor(out=t2.ap, in0=Aim.ap, in1=lim.ap, op=ALU.mult), [Aim, lim], [t2])
    dve(lambda e: e.tensor_tensor(out=t1.ap, in0=t1.ap, in1=t2.ap, op=ALU.add), [t1, t2], [t1])
    dve(lambda e: e.tensor_tensor(out=fre.ap, in0=t1.ap, in1=den.ap, op=ALU.mult), [t1, den], [fre])
    dve(lambda e: e.tensor_tensor(out=t1.ap, in0=Aim.ap, in1=lre.ap, op=ALU.mult), [Aim, lre], [t1])
    dve(lambda e: e.tensor_tensor(out=t2.ap, in0=t3.ap, in1=lim.ap, op=ALU.mult), [t3, lim], [t2])
    dve(lambda e: e.tensor_tensor(out=t1.ap, in0=t1.ap, in1=t2.ap, op=ALU.subtract), [t1, t2], [t1])
    dve(lambda e: e.tensor_tensor(out=fim.ap, in0=t1.ap, in1=den.ap, op=ALU.mult), [t1, den], [fim])
    for k in range(NS):
        if k == 0:
            dve(lambda e: e.tensor_copy(out=PW.ap[:, :, 0, 0], in_=Are.ap), [Are], [PW])
            dve(lambda e: e.tensor_copy(out=PW.ap[:, :, 0, 1], in_=Aim.ap), [Aim], [PW])
        else:
            csquare(PW.ap[:, :, k - 1, 0], PW.ap[:, :, k - 1, 1])
            dve(lambda e: e.tensor_tensor(out=PW.ap[:, :, k, 0], in0=t1.ap, in1=t2.ap, op=ALU.subtract), [t1, t2], [PW])
            dve(lambda e: e.tensor_scalar(out=PW.ap[:, :, k, 1], in0=t3.ap, scalar1=2.0, scalar2=None, op0=ALU.mult), [t3], [PW])
        dve(lambda e: e.tensor_scalar(out=PW.ap[:, :, k, 2], in0=PW.ap[:, :, k, 1], scalar1=-1.0, scalar2=None, op0=ALU.mult), [PW], [PW])
    BL = sb("S_BL", [32, 64, 128], BF16)
    CL = sb("S_CL", [128, 32, 2, 32], F32)
    with ExitStack() as es1:
        sb1 = lambda name, shape, dt: pg.buf(es1.enter_context(nc.sbuf_tensor(pg.uname(name), shape, dt)).ap(), name)
        Bt = [sb1("S_Bt%d" % i, [128, 32, 16], F32) for i in range(2)]
        Bb = [sb1("S_Bb%d" % i, [128, 32, 16], F32) for i in range(2)]
        tmpb = sb1("S_tmpb", [128, 32, 16], F32)
        Wb = sb1("S_Wb", [128, 32, 2, 32], F32)
        for i, nm in enumerate(("s5_b_re", "s5_b_im")):
            base = w[nm][l]
            pg.dma(Bt[i].ap, bass.AP(base.tensor, base.offset, [[16, 128], [2048, 32], [1, 16]]), writes=[Bt[i]])
        fb = lambda t: t.ap.unsqueeze(2).to_broadcast([128, 32, 16])
        dve(lambda e: e.tensor_tensor(out=Bb[0].ap, in0=Bt[0].ap, in1=fb(fre), op=ALU.mult), [Bt[0], fre], [Bb[0]])
        dve(lambda e: e.tensor_tensor(out=tmpb.ap, in0=Bt[1].ap, in1=fb(fim), op=ALU.mult), [Bt[1], fim], [tmpb])
        dve(lambda e: e.tensor_tensor(out=Bb[0].ap, in0=Bb[0].ap, in1=tmpb.ap, op=ALU.subtract), [Bb[0], tmpb], [Bb[0]])
        dve(lambda e: e.tensor_tensor(out=Bb[1].ap, in0=Bt[1].ap, in1=fb(fre), op=ALU.mult), [Bt[1], fre], [Bb[1]])
        dve(lambda e: e.tensor_tensor(out=tmpb.ap, in0=Bt[0].ap, in1=fb(fim), op=ALU.mult), [Bt[0], fim], [tmpb])
        dve(lambda e: e.tensor_tensor(out=Bb[1].ap, in0=Bb[1].ap, in1=tmpb.ap, op=ALU.add), [Bb[1], tmpb], [Bb[1]])
        pg.op("pool", lambda e: e.memset(Wb.ap, 0.0), writes=[Wb])
        for c in range(2):
            dve(lambda e: e.tensor_copy(out=Wb.ap[0:64, :, c, 0:16], in_=Bb[c].ap[0:64]), [Bb[c]], [Wb])
            dve(lambda e: e.tensor_copy(out=Wb.ap[64:128, :, c, 16:32], in_=Bb[c].ap[64:128]), [Bb[c]], [Wb])
        for q0 in range(0, 64, 4):
            ps = cx.psf.get()
            for q in range(4):
                dg, c = (q0 + q) // 2, (q0 + q) % 2
                pg.op("pe", lambda e: e.transpose(out=ps.ap[:32, q * 128:(q + 1) * 128], in_=Wb.ap[:, dg, c, :], identity=cx.identf.ap), reads=[Wb, cx.identf], writes=[ps])
            pg.op("act", lambda e: e.copy(BL.ap[:, q0:q0 + 4, :], ps.ap[:32, :].rearrange("p (q m) -> p q m", q=4)), reads=[ps], writes=[BL])
        St0 = sb1("S_St0", [128, 64], F32)
        St = sb1("S_St", [128, 128], F32)
        for d in range(2):
            for c, nm in enumerate(("s5_c_re", "s5_c_im")):
                for blk in range(4):
                    pg.dma(St0.ap, w[nm][l, d, 8 * blk:8 * blk + 8].rearrange("g i p -> (g i) p"), writes=[St0])
                    sgn = 1.0 if c == 0 else -1.0
                    for hh in range(2):
                        dve(lambda e: e.tensor_scalar(out=St.ap[:, 64 * hh:64 * hh + 64], in0=St0.ap, scalar1=cx.pm.ap[:, hh:hh + 1], scalar2=sgn, op0=ALU.mult, op1=ALU.mult),
                            [St0, cx.pm], [St])
                    ps = cx.psf.get()
                    pg.op("pe", lambda e: e.transpose(out=ps.ap[:, 0:128], in_=St.ap, identity=cx.identf.ap), reads=[St, cx.identf], writes=[ps])
                    dg0 = d * 16 + blk * 4
                    pg.op("act", lambda e: e.copy(CL.ap[:, dg0:dg0 + 4, c, :], ps.ap[:, 0:128].rearrange("p (q m) -> p q m", q=4)), reads=[ps], writes=[CL])
    dsk = sb("S_dsk", [32, 16], F32)
    load_T(pg, cx, dsk, dsk.ap, w["s5_d"][l].rearrange("(g q) -> g q", q=32), 16, wd=32)
    bgl = sb("S_bgl", [128, 4], F32)
    load_T(pg, cx, bgl, bgl.ap, w["s5_b_glu"][l].rearrange("(c p) -> c p", p=128), 4)
    es2 = ExitStack()
    sb2 = lambda name, shape, dt: pg.buf(es2.enter_context(nc.sbuf_tensor(pg.uname(name), shape, dt)).ap(), name)
    X = [sb2("S_X%d" % i, [128, L], F32) for i in range(3)]
    Yc = sb2("S_Yc", [32, L], F32)
    ut = [sb2("S_ut%d" % i, [32, 512], F32) for i in range(2)]
    ub = [sb2("S_ub%d" % i, [32, 512], BF16) for i in range(2)]
    NT = L // 512
    ev = [0]
    for gh in range(16):
        q = gh % 4
        c4 = gh // 4
        urow = PF_S5_U + 32 * gh
        for d in range(2):
            dg = d * 16 + gh
            re, im, T = X[0], X[1], X[2]
            for it in range(NT):
                tsl = slice(it * 512, (it + 1) * 512)
                u_t = ut[it % 2]; u_b = ub[it % 2]
                pg.dma(u_t.ap, cx.PF[urow:urow + 32, tsl], writes=[u_t])
                pg.op("pool", lambda e: e.tensor_copy(out=u_b.ap, in_=u_t.ap), reads=[u_t], writes=[u_b])
                for c, dstb in ((0, re), (1, im)):
                    ps = cx.psf.get()
                    pg.op("pe", lambda e: e.matmul(ps.ap, lhsT=BL.ap[:, dg * 2 + c, :], rhs=u_b.ap, start=True, stop=True), reads=[BL, u_b], writes=[ps])
                    ev[0] += 1
                    if ev[0] % 2 == 0:
                        pg.op("act", lambda e: e.copy(dstb.ap[:, tsl], ps.ap), reads=[ps], writes=[dstb])
                    else:
                        pg.op("dve", lambda e: e.tensor_copy(out=dstb.ap[:, tsl], in_=ps.ap), reads=[ps], writes=[dstb])
            for k in range(NS):
                s = 1 << k
                if s >= L:
                    break
                cre, cim, ncim = PW.ap[:, dg, k, 0:1], PW.ap[:, dg, k, 1:2], PW.ap[:, dg, k, 2:3]
                if d == 0:
                    dst, src, keep = slice(s, L), slice(0, L - s), slice(0, s)
                else:
                    dst, src, keep = slice(0, L - s), slice(s, L), slice(L - s, L)
                n = L - s
                e1, e2 = "dve", "dve"
                pg.op(e1, lambda e: e.scalar_tensor_tensor(out=T.ap[:, dst], in0=re.ap[:, src], scalar=cre, in1=re.ap[:, dst], op0=ALU.mult, op1=ALU.add), reads=[re, PW], writes=[T])
                pg.op(e1, lambda e: e.scalar_tensor_tensor(out=T.ap[:, dst], in0=im.ap[:, src], scalar=ncim, in1=T.ap[:, dst], op0=ALU.mult, op1=ALU.add), reads=[im, T, PW], writes=[T])
                pg.op("pool", lambda e: e.tensor_copy(out=T.ap[:, keep], in_=re.ap[:, keep]), reads=[re], writes=[T])
                if d == 0:
                    rv = lambda ap, sl: bass.AP(ap.tensor, ap[:, sl].offset + (sl.stop - sl.start) - 1, [list(ap.ap[0]), [-1, sl.stop - sl.start]])
                    pg.op(e2, lambda e: e.scalar_tensor_tensor(out=rv(im.ap, dst), in0=rv(im.ap, src), scalar=cre, in1=rv(im.ap, dst), op0=ALU.mult, op1=ALU.add), reads=[im, PW], writes=[im])
                else:
                    pg.op(e2, lambda e: e.scalar_tensor_tensor(out=im.ap[:, dst], in0=im.ap[:, src], scalar=cre, in1=im.ap[:, dst], op0=ALU.mult, op1=ALU.add), reads=[im, PW], writes=[im])
                pg.op(e2, lambda e: e.scalar_tensor_tensor(out=im.ap[:, dst], in0=re.ap[:, src], scalar=cim, in1=im.ap[:, dst], op0=ALU.mult, op1=ALU.add), reads=[re, im, PW], writes=[im])
                re, T = T, re
            for it in range(NT):
                tsl = slice(it * 512, (it + 1) * 512)
                ps = cx.psf.get()
                po = ps.ap[0:32, :]
                pg.op("pe", lambda e: e.matmul(po, lhsT=CL.ap[:, dg, 0, :], rhs=re.ap[:, tsl], start=True, stop=False), reads=[CL, re], writes=[ps])
                pg.op("pe", lambda e: e.matmul(po, lhsT=CL.ap[:, dg, 1, :], rhs=im.ap[:, tsl], start=False, stop=True), reads=[CL, im], writes=[ps])
                yo = Yc.ap[:, tsl]
                if d == 0:
                    pg.op("act", lambda e: e.copy(yo, po), reads=[ps], writes=[Yc])
                else:
                    pg.op("dve", lambda e: e.tensor_tensor(out=yo, in0=po, in1=yo, op=ALU.add), reads=[ps, Yc], writes=[Yc])
            X[0], X[1], X[2] = X[0], X[1], X[2]
        TW = min(2048, L)
        for t0 in range(0, L, TW):
            tsl = slice(t0, t0 + TW)
            ua = X[2].ap[0:32, 0:TW]
            x2 = X[1].ap[0:32, 0:TW]
            zo = X[0].ap[0:32, 0:TW].bitcast(BF16)[:, 0:TW]
            pg.dma(ua, cx.PF[urow:urow + 32, tsl], writes=[X[2]])
            yv = Yc.ap[:, tsl]
            dve(lambda e: e.scalar_tensor_tensor(out=yv, in0=ua, scalar=dsk.ap[:, gh:gh + 1], in1=yv, op0=ALU.mult, op1=ALU.add), [X[2], dsk, Yc], [Yc])
            pg.op("pool", lambda e: e.tensor_tensor(out=x2, in0=yv, in1=yv, op=ALU.mult), reads=[Yc], writes=[X[1]])
            dve(lambda e: e.tensor_scalar(out=x2, in0=x2, scalar1=0.044715, scalar2=1.0, op0=ALU.mult, op1=ALU.add), [X[1]], [X[1]])
            pg.op("pool", lambda e: e.tensor_tensor(out=x2, in0=x2, in1=yv, op=ALU.mult), reads=[Yc, X[1]], writes=[X[1]])
            pg.op("act", lambda e: e.activation(out=x2, in_=x2, func=AF.Sigmoid, scale=1.5957691216), reads=[X[1]], writes=[X[1]])
            dve(lambda e: e.tensor_tensor(out=zo, in0=x2, in1=yv, op=ALU.mult), [X[1], Yc], [X[0]])
            pg.dma(cx.ZT[urow - PF_S5_U:urow - PF_S5_U + 32, tsl], zo, reads=[X[0]])
    pg.barrier()
    es2.close()
    wg = sb("S_wg", [128, 4, 512], BF16)
    wst = sb("S_wst", [128, 2048], F32)
    pg.dma(wst.ap[:, 0:2048].rearrange("p (k c) -> p k c", k=4), w["s5_w_glu"][l].rearrange("(k p) c -> p k c", p=128), writes=[wst])
    dve(lambda e: e.tensor_copy(out=wg.ap, in_=wst.ap[:, 0:2048].rearrange("p (k c) -> p k c", k=4)), [wst], [wg])
    zt = [sb("S_zt%d" % i, [128, 4, 512], BF16) for i in range(2)]
    gt = [sb("S_gt%d" % i, [128, 4, 512], F32) for i in range(2)]
    sg = sb("S_sg", [128, 512], F32)
    yo = [sb("S_yo%d" % i, [128, 4, 512], BF16) for i in range(2)]
    ZTv = cx.ZT.rearrange("(k p) t -> p k t", p=128)
    for it in range(NT):
        tsl = slice(it * 512, (it + 1) * 512)
        z_ = zt[it % 2]; g_ = gt[it % 2]; y_ = yo[it % 2]
        pg.dma(z_.ap, ZTv[:, :, tsl], writes=[z_])
        pg.dma(g_.ap, cx.PF[PF_S5_G:PF_S5_G + 512, tsl].rearrange("(k p) t -> p k t", p=128), writes=[g_])
        pg.op("act", lambda e: e.activation(out=g_.ap, in_=g_.ap, func=AF.Silu), reads=[g_], writes=[g_])
        pg.op("pool", lambda e: e.tensor_tensor(out=g_.ap, in0=g_.ap, in1=z_.ap, op=ALU.mult), reads=[g_, z_], writes=[g_])
        for oc in range(4):
            ps = cx.psf.get()
            for k in range(4):
                pg.op("pe", lambda e: e.matmul(ps.ap, lhsT=wg.ap[:, k, oc * 128:(oc + 1) * 128], rhs=z_.ap[:, k, :], start=(k == 0), stop=(k == 3)), reads=[wg, z_], writes=[ps])
            pg.op("act", lambda e: e.activation(out=sg.ap, in_=ps.ap, func=AF.Sigmoid, bias=bgl.ap[:, oc:oc + 1]), reads=[ps, bgl], writes=[sg])
            dve(lambda e: e.tensor_tensor(out=y_.ap[:, oc, :], in0=sg.ap, in1=g_.ap[:, oc, :], op=ALU.mult), [sg, g_], [y_])
        pg.dma(cx.BT[1536:2048, tsl].rearrange("(k p) t -> p k t", p=128), y_.ap, reads=[y_])


W_NAMES = ["norm_g", "w_in", "lru_conv_w", "lru_conv_b", "lru_w_a", "lru_b_a", "lru_w_x", "lru_b_x", "lru_lambda",
           "gla_w_up", "gla_b_up", "gla_norm_g", "dn_conv_w", "dn_a_log", "dn_dt_bias", "dn_norm_g",
           "s5_lambda_re", "s5_lambda_im", "s5_log_dt", "s5_b_re", "s5_b_im", "s5_c_re", "s5_c_im", "s5_d",
           "s5_w_glu", "s5_b_glu", "w_branch", "w_merge_gate", "b_merge_gate", "w_out", "final_norm_g"]


def host_consts():
    c = {}
    c["identb"] = np.eye(128, dtype=np.float32).astype(ml_dtypes.bfloat16)
    c["identf"] = np.eye(128, dtype=np.float32)
    idx = np.arange(128)
    same = (idx[:, None] // 64) == (idx[None, :] // 64)
    le = idx[:, None] <= idx[None, :]
    lt = idx[:, None] < idx[None, :]
    c["m_incl"] = np.stack([(same & le), (same & le.T)]).astype(np.float32)
    c["m_strict_after"] = np.stack([(same & lt.T), (same & lt)]).astype(np.float32)
    c["m_dn"] = np.stack([c["m_strict_after"], c["m_incl"]], axis=1)
    c["chunkind"] = np.stack([np.repeat((idx // 64 == cc)[:, None], 128, axis=1) for cc in range(2)]).astype(np.float32)
    c["onesf"] = np.ones((128, 128), np.float32)
    ev = ((idx // 16) % 2 == 0).astype(np.float32)
    c["pm"] = np.stack([ev, 1.0 - ev], axis=1).astype(np.float32)
    return c


def build(L, shapes, nslot=2, depth=2, debug=False, branches=("lru", "gla", "dn", "s5")):
    from contextlib import ExitStack
    nc = bass.Bass("TRN2", target_bir_lowering=False)
    pg = Prog(nc)
    cx = Ctx()
    cx.nc = nc
    cx.pg = pg
    cx.w = {}
    for nm in W_NAMES:
        cx.w[nm] = nc.dram_tensor(nm, list(shapes[nm]), F32, kind="ExternalInput").ap()
    hc = host_consts()
    cx.cd = {}
    for nm, arr in hc.items():
        cx.cd[nm] = nc.dram_tensor("c_" + nm, list(arr.shape), BF16 if arr.dtype == ml_dtypes.bfloat16 else F32, kind="ExternalInput").ap()
    xs = [nc.dram_tensor("x%d" % s, [L, D], F32, kind="ExternalInput").ap() for s in range(nslot)]
    ys = [nc.dram_tensor("y%d" % s, [L, D], F32, kind="ExternalOutput").ap() for s in range(nslot)]
    sk = "ExternalOutput" if debug else "Internal"
    cx.PF = nc.dram_tensor("PF", [PF_ROWS, L], F32, kind=sk).ap()
    cx.PT = nc.dram_tensor("PT", [L, PT_COLS], F32, kind=sk).ap()
    cx.XNT = nc.dram_tensor("XNT", [D, L], BF16, kind=sk).ap()
    cx.BT = nc.dram_tensor("BT", [2048, L], BF16, kind=sk).ap()
    cx.OB = nc.dram_tensor("OB", [L, 1024], F32, kind=sk).ap()
    XS = [nc.dram_tensor("XS%d" % s, [L, D], F32, kind=sk).ap() for s in range(nslot)]
    cx.QKT = nc.dram_tensor("QKT", [1024, L], BF16, kind=sk).ap()
    cx.KVT = nc.dram_tensor("KVT", [L, 1024], BF16, kind=sk).ap()
    cx.ZT = nc.dram_tensor("ZT", [512, L], BF16, kind=sk).ap()
    psf, psb = mk_psum(pg, nc)
    cx.psf = Rot(psf[:5])
    cx.pso = psf[5]
    cx.psb = Rot(psb)
    gsb = lambda name, shape, dt: pg.buf(nc.alloc_sbuf_tensor(name, shape, dt).ap(), name)
    cx.identb = gsb("identb", [128, 128], BF16)
    pg.dma(cx.identb.ap, cx.cd["identb"], writes=[cx.identb])
    cx.identf = gsb("identf", [128, 128], F32)
    pg.dma(cx.identf.ap, cx.cd["identf"], writes=[cx.identf])
    cx.eps = gsb("eps", [128, 1], F32)
    pg.op("dve", lambda e: e.memset(cx.eps.ap, EPS), writes=[cx.eps])
    cx.one = gsb("one", [128, 1], F32)
    pg.op("dve", lambda e: e.memset(cx.one.ap, 1.0), writes=[cx.one])
    cx.ldst = gsb("ldst", [128, 128], F32)
    cx.m_incl = gsb("m_incl", [128, 2, 128], F32)
    pg.dma(cx.m_incl.ap, cx.cd["m_incl"].rearrange("d s t -> s d t"), writes=[cx.m_incl])
    cx.m_sa = gsb("m_sa", [128, 2, 128], F32)
    pg.dma(cx.m_sa.ap, cx.cd["m_strict_after"].rearrange("d s t -> s d t"), writes=[cx.m_sa])
    cx.m_dn = gsb("m_dn", [128, 2, 2, 128], F32)
    pg.dma(cx.m_dn.ap[:, 0], cx.cd["m_dn"][0].rearrange("j s t -> s j t"), writes=[cx.m_dn])
    pg.dma(cx.m_dn.ap[:, 1], cx.cd["m_dn"][1].rearrange("j s t -> s j t"), writes=[cx.m_dn])
    cx.chunkind = gsb("chunkind", [128, 2, 128], F32)
    pg.dma(cx.chunkind.ap, cx.cd["chunkind"].rearrange("c s m -> s c m"), writes=[cx.chunkind])
    cx.onesf = gsb("onesf", [128, 128], F32)
    pg.dma(cx.onesf.ap, cx.cd["onesf"], writes=[cx.onesf])
    cx.pm = gsb("pm", [128, 2], F32)
    pg.dma(cx.pm.ap, cx.cd["pm"], writes=[cx.pm])
    cx.halfpi = gsb("halfpi", [128, 1], F32)
    pg.op("dve", lambda e: e.memset(cx.halfpi.ap, float(np.pi / 2)), writes=[cx.halfpi])
    cx.zb = gsb("zb", [128, 2048], BF16)
    pg.op("pool", lambda e: e.memset(cx.zb.ap, 0.0), writes=[cx.zb])
    bidx = {"lru": 0, "gla": 1, "dn": 2, "s5": 3}
    for l in range(depth):
        for s in range(nslot):
            xin = xs[s] if l == 0 else XS[s]
            last = (l == depth - 1)
            xout = ys[s] if last else XS[s]
            pg.barrier()
            with ExitStack() as es:
                phase_P(pg, cx, es, L, xin, l)
                pg.barrier()
            for bn in ("lru", "gla", "dn", "s5"):
                if bn not in branches:
                    b = bidx[bn]
                    for t0 in range(0, L, 2048):
                        tw = min(2048, L - t0)
                        for c in range(4):
                            pg.dma(cx.BT[b * 512 + c * 128:b * 512 + (c + 1) * 128, t0:t0 + tw], cx.zb.ap[:, :tw], reads=[cx.zb])
            if "lru" in branches:
                with ExitStack() as es:
                    phase_LRU(pg, cx, es, L, l)
                    pg.barrier()
            if "s5" in branches:
                with ExitStack() as es:
                    phase_S5(pg, cx, es, L, l)
                    pg.barrier()
            if "gla" in branches:
                with ExitStack() as es:
                    phase_GLA(pg, cx, es, L, l)
                    pg.barrier()
            if "dn" in branches:
                with ExitStack() as es:
                    phase_DN(pg, cx, es, L, l)
                    pg.barrier()
            pg.barrier()
            with ExitStack() as es:
                phase_M(pg, cx, es, L, xin, xout, l, last)
                pg.barrier()
    pg.barrier()
    return nc, pg, hc


_CACHE = {}


def kernel(**inputs):
    L = inputs["x_prompt"].shape[1]
    shapes = {nm: inputs[nm].shape for nm in W_NAMES}
    nc, pg, hc = build(L, shapes)
    xp = np.ascontiguousarray(inputs["x_prompt"], dtype=np.float32)
    xsm = np.ascontiguousarray(inputs["x_sample"], dtype=np.float32)
    wmap = {nm: np.ascontiguousarray(inputs[nm], dtype=np.float32) for nm in W_NAMES}
    in_maps = []
    for c in range(8):
        m = dict(wmap)
        for nm, arr in hc.items():
            m["c_" + nm] = arr
        m["x0"] = xp[c]
        m["x1"] = xsm[c % 2]
        in_maps.append(m)
    res = run_bass_kernel_spmd(nc, in_maps, core_ids=list(range(8)))
    y_prompt = np.stack([np.asarray(res.results[c]["y0"], dtype=np.float32) for c in range(8)], axis=0)
    y_sample = np.stack([np.asarray(res.results[c]["y1"], dtype=np.float32) for c in range(2)], axis=0)
    return (y_prompt, y_sample)
```

```python
import numpy as np
import ml_dtypes
from contextlib import ExitStack
import concourse.bass as bass
import concourse.mybir as mybir
from concourse.bass_utils import run_bass_kernel_spmd

F32 = mybir.dt.float32
BF16 = mybir.dt.bfloat16
ALU = mybir.AluOpType
AF = mybir.ActivationFunctionType

D = 1024
BW = 512
D_IN = 5680
EPS = 1e-6
O_LRU_X, O_LRU_G = 0, 512
O_GLA_Q, O_GLA_K, O_GLA_V, O_GLA_G, O_GLA_LR = 1024, 1280, 1536, 2048, 2560
O_DN_QKV, O_DN_G, O_DN_BA = 2592, 4128, 4640
O_S5_U, O_S5_G = 4656, 5168


SAME_ENGINE_SYNC = True


class Buf:
    __slots__ = ("ap", "w", "r", "name")

    def __init__(self, ap, name=""):
        self.ap = ap
        self.w = []
        self.r = []
        self.name = name

    def __getitem__(self, k):
        return self.ap[k]


class Prog:
    def __init__(self, nc, n_dma_sems=40):
        self.nc = nc
        self.eng = {"pe": nc.tensor, "act": nc.scalar, "dve": nc.vector, "pool": nc.gpsimd, "sp": nc.sync}
        self.sem = {k: nc.alloc_semaphore("s_" + k) for k in self.eng}
        self.cnt = {k: 0 for k in self.eng}
        self.seen = {k: {} for k in self.eng}
        self.dsem = [nc.alloc_semaphore("d%d" % i) for i in range(n_dma_sems)]
        self.dcnt = [0] * n_dma_sems
        self.dnext = 0
        self.ninst = 0

    def buf(self, ap, name=""):
        return Buf(ap, name)

    def uname(self, name):
        self.uid = getattr(self, "uid", 0) + 1
        return "%s_%d" % (name, self.uid)

    def _wait(self, e, dep):
        if dep[0] == "dma":
            key = ("dma", dep[1]); val = dep[2]
            if self.seen[e].get(key, 0) >= val:
                return
            self.eng[e].wait_ge(self.dsem[dep[1]], val)
        else:
            f, val = dep
            if f == e and (e in ("pe", "sp") or not SAME_ENGINE_SYNC):
                return
            key = f
            if self.seen[e].get(key, 0) >= val:
                return
            self.eng[e].wait_ge(self.sem[f], val)
        self.seen[e][key] = val
        self.ninst += 1

    def _deps(self, e, reads, writes):
        for b in reads:
            for d in b.w:
                self._wait(e, d)
        for b in writes:
            for d in b.w:
                self._wait(e, d)
            for d in b.r:
                self._wait(e, d)

    def op(self, e, inst_fn, reads=(), writes=()):
        self._deps(e, reads, writes)
        inst = inst_fn(self.eng[e])
        inst.then_inc(self.sem[e], 1)
        self.cnt[e] += 1
        me = (e, self.cnt[e])
        for b in reads:
            b.r.append(me)
            if len(b.r) > 24:
                b.r = b.r[-24:] if False else self._compress(b.r)
        for b in writes:
            b.w = [me]
            b.r = []
        self.ninst += 1
        return inst

    @staticmethod
    def _compress(lst):
        best = {}
        for d in lst:
            k = ("dma", d[1]) if d[0] == "dma" else d[0]
            v = d[2] if d[0] == "dma" else d[1]
            if k not in best or v > best[k][0]:
                best[k] = (v, d)
        return [x[1] for x in best.values()]

    def dma(self, out, in_, reads=(), writes=(), q="sp", **kw):
        self._deps(q, reads, writes)
        j = self.dnext
        self.dnext = (self.dnext + 1) % len(self.dsem)
        if self.dcnt[j] > 0:
            self._wait(q, ("dma", j, self.dcnt[j]))
        self.dcnt[j] += 16
        self.eng[q].dma_start(out=out, in_=in_, **kw).then_inc(self.dsem[j], 16)
        me = ("dma", j, self.dcnt[j])
        for b in reads:
            b.r.append(me)
            if len(b.r) > 24:
                b.r = self._compress(b.r)
        for b in writes:
            b.w = [me]
            b.r = []
        self.ninst += 1

    def barrier(self):
        for e in self.eng:
            for f in self.eng:
                if f != e and self.cnt[f] > 0:
                    self._wait(e, (f, self.cnt[f]))
            for j, c in enumerate(self.dcnt):
                if c > 0:
                    self._wait(e, ("dma", j, c))


class Ctx:
    pass


def mk_psum(pg, nc):
    banks = []
    for i in range(6):
        banks.append(pg.buf(nc.alloc_psum_tensor("psf%d" % i, [128, 512], F32).ap(), "psf%d" % i))
    bb = []
    for i in range(2):
        bb.append(pg.buf(nc.alloc_psum_tensor("psb%d" % i, [128, 1024], BF16).ap(), "psb%d" % i))
    return banks, bb


class Rot:
    def __init__(self, items):
        self.items = items
        self.i = 0

    def get(self):
        x = self.items[self.i]
        self.i = (self.i + 1) % len(self.items)
        return x


PF_LRU_X, PF_LRU_G, PF_GLA_Q, PF_GLA_K, PF_DN_QKV, PF_S5_U, PF_S5_G, PF_GLA_LR = 0, 512, 1024, 1280, 1536, 3072, 3584, 4096
PF_ROWS = 4128
PF_CHUNKS = ([(O_LRU_X + 128 * i, 128) for i in range(4)] + [(O_LRU_G + 128 * i, 128) for i in range(4)]
             + [(O_GLA_Q + 128 * i, 128) for i in range(2)] + [(O_GLA_K + 128 * i, 128) for i in range(2)]
             + [(O_DN_QKV + 128 * i, 128) for i in range(12)] + [(O_S5_U + 128 * i, 128) for i in range(4)]
             + [(O_S5_G + 128 * i, 128) for i in range(4)] + [(O_GLA_LR, 32)])
PT_GLA_K, PT_GLA_V, PT_GLA_G, PT_DN_G, PT_DN_BA = 0, 256, 768, 1280, 1792
PT_COLS = 1808
PT_GROUPS = [(1280, 512, 0), (1792, 512, 512), (2304, 256, 1024), (4128, 512, 1280), (4640, 16, 1792)]


def load_cast_bf16(pg, nc, es, dst, src_ap, rows, cols, name, chunk=2048):
    st = [pg.buf(es.enter_context(nc.sbuf_tensor(name + "_st%d" % i, [128, chunk], F32)).ap()) for i in range(2)]
    i = 0
    for c0 in range(0, cols, chunk):
        cw = min(chunk, cols - c0)
        s = st[i % 2]
        pg.dma(s.ap[:rows, :cw], src_ap[:, c0:c0 + cw], writes=[s])
        if i % 2 == 0:
            pg.op("act", lambda e: e.copy(dst[0][:rows, c0:c0 + cw], s.ap[:rows, :cw]), reads=[s], writes=[dst[1]])
        else:
            pg.op("dve", lambda e: e.tensor_copy(out=dst[0][:rows, c0:c0 + cw], in_=s.ap[:rows, :cw]), reads=[s], writes=[dst[1]])
        i += 1


def phase_P(pg, cx, es, L, x_ap, l):
    nc = cx.nc
    TT = 512
    sb = lambda name, shape, dt: pg.buf(es.enter_context(nc.sbuf_tensor(pg.uname(name), shape, dt)).ap(), name)
    wbf = sb("P_w", [128, 8, D_IN], BF16)
    w_src = cx.w["w_in"][l].rearrange("(k p) c -> p k c", p=128)
    WC = D_IN // 4
    st = [sb("P_wst%d" % i, [128, WC], F32) for i in range(2)]
    for k in range(8):
        for q in range(4):
            s = st[q % 2]
            pg.dma(s.ap, w_src[:, k, q * WC:(q + 1) * WC], writes=[s])
            if q % 2 == 0:
                pg.op("act", lambda e: e.copy(wbf.ap[:, k, q * WC:(q + 1) * WC], s.ap), reads=[s], writes=[wbf])
            else:
                pg.op("dve", lambda e: e.tensor_copy(out=wbf.ap[:, k, q * WC:(q + 1) * WC], in_=s.ap), reads=[s], writes=[wbf])
    gk = sb("P_g", [128, 8], F32)
    load_T(pg, cx, gk, gk.ap, cx.w["norm_g"][l].rearrange("(k p) -> k p", p=128), 8)
    xt = [sb("P_x%d" % i, [128, 4, D], F32) for i in range(2)]
    xs = sb("P_xs", [128, D], BF16)
    junk = sb("P_junk", [128, D], BF16)
    ss = sb("P_ss", [128, 4], F32)
    xnT = [sb("P_xnT%d" % i, [128, 8, TT], BF16) for i in range(2)]
    stf = [sb("P_stf%d" % i, [128, 4, TT], F32) for i in range(2)]
    stt = [sb("P_stt%d" % i, [128, PT_COLS], F32) for i in range(1)]
    XNTv = cx.XNT.rearrange("(k p) t -> p k t", p=128)
    xv = x_ap.rearrange("(n j p) d -> n p j d", p=128, j=4)
    gb = gk.ap.unsqueeze(2).to_broadcast([128, 8, 128])
    nt = L // TT
    evac_i = 0
    for it in range(nt):
        x_b = xt[it % 2]
        pg.dma(x_b.ap, xv[it], writes=[x_b])
        xn = xnT[it % 2]
        for j in range(4):
            pg.op("act", lambda e: e.activation(out=junk.ap, in_=x_b.ap[:, j, :], func=AF.Square, accum_out=ss.ap[:, j:j + 1]),
                  reads=[x_b], writes=[junk, ss])
            pg.op("act", lambda e: e.activation(out=ss.ap[:, j:j + 1], in_=ss.ap[:, j:j + 1], func=AF.Sqrt, scale=1.0 / D, bias=cx.eps.ap[:, 0:1]),
                  reads=[ss, cx.eps], writes=[ss])
            pg.op("dve", lambda e: e.reciprocal(out=ss.ap[:, j:j + 1], in_=ss.ap[:, j:j + 1]), reads=[ss], writes=[ss])
            pg.op("dve", lambda e: e.tensor_scalar(out=xs.ap, in0=x_b.ap[:, j, :], scalar1=ss.ap[:, j:j + 1], scalar2=None, op0=ALU.mult),
                  reads=[x_b, ss], writes=[xs])
            pb = cx.psb.get()
            for k in range(8):
                pg.op("pe", lambda e: e.transpose(out=pb.ap[:, k * 128:(k + 1) * 128], in_=xs.ap[:, k * 128:(k + 1) * 128], identity=cx.identb.ap),
                      reads=[xs, cx.identb], writes=[pb])
            pg.op("dve", lambda e: e.tensor_tensor(out=xn.ap[:, :, j * 128:(j + 1) * 128], in0=pb.ap.rearrange("p (k t) -> p k t", k=8), in1=gb, op=ALU.mult),
                  reads=[pb, gk], writes=[xn])
        pg.dma(XNTv[:, :, it * TT:(it + 1) * TT], xn.ap, reads=[xn])
        for ci, (c0, cw) in enumerate(PF_CHUNKS):
            ps = cx.psf.get()
            for k in range(8):
                pg.op("pe", lambda e: e.matmul(ps.ap[:cw, :], lhsT=wbf.ap[:, k, c0:c0 + cw], rhs=xn.ap[:, k, :], start=(k == 0), stop=(k == 7)),
                      reads=[wbf, xn], writes=[ps])
            sbuf = stf[(ci // 4) % 2]
            evac_i += 1
            if evac_i % 2 == 0:
                pg.op("act", lambda e: e.copy(sbuf.ap[:cw, ci % 4, :], ps.ap[:cw, :]), reads=[ps], writes=[sbuf])
            else:
                pg.op("dve", lambda e: e.tensor_copy(out=sbuf.ap[:cw, ci % 4, :], in_=ps.ap[:cw, :]), reads=[ps], writes=[sbuf])
            if ci % 4 == 3:
                cb = ci // 4
                pg.dma(PFv_slice(cx, cb * 4, 4, it * TT, TT), sbuf.ap, reads=[sbuf])
            elif ci == len(PF_CHUNKS) - 1:
                pg.dma(cx.PF[4096:4128, it * TT:(it + 1) * TT], sbuf.ap[:32, 0, :], reads=[sbuf])
        for j in range(4):
            sbuf = stt[0]
            for (c0, cw, o0) in PT_GROUPS:
                ps = cx.psf.get()
                for k in range(8):
                    pg.op("pe", lambda e: e.matmul(ps.ap[:, :cw], lhsT=xn.ap[:, k, j * 128:(j + 1) * 128], rhs=wbf.ap[:, k, c0:c0 + cw], start=(k == 0), stop=(k == 7)),
                          reads=[wbf, xn], writes=[ps])
                evac_i += 1
                if evac_i % 2 == 0:
                    pg.op("act", lambda e: e.copy(sbuf.ap[:, o0:o0 + cw], ps.ap[:, :cw]), reads=[ps], writes=[sbuf])
                else:
                    pg.op("dve", lambda e: e.tensor_copy(out=sbuf.ap[:, o0:o0 + cw], in_=ps.ap[:, :cw]), reads=[ps], writes=[sbuf])
            t0 = it * TT + j * 128
            pg.dma(cx.PT[t0:t0 + 128, :], sbuf.ap, reads=[sbuf])


def PFv_slice(cx, c0, nch, t0, tw):
    return cx.PF[c0 * 128:(c0 + nch) * 128, t0:t0 + tw].rearrange("(c p) t -> p c t", p=128)


def load_T(pg, cx, dst, dst_ap, src_ap, n, st_view=None, wd=128, **kw):
    st = cx.ldst
    pg.dma(st.ap[:n, :wd] if st_view is None else st_view(st.ap[:n, :wd]), src_ap, writes=[st], **kw)
    ps = cx.psf.get()
    pg.op("pe", lambda e: e.transpose(out=ps.ap[:wd, :n], in_=st.ap[:n, :wd], identity=cx.identf.ap[:n, :n]), reads=[st, cx.identf], writes=[ps])
    pg.op("dve", lambda e: e.tensor_copy(out=dst_ap, in_=ps.ap[:wd, :n]), reads=[ps], writes=[dst])


def phase_LRU(pg, cx, es, L, l):
    nc = cx.nc
    sb = lambda name, shape, dt: pg.buf(es.enter_context(nc.sbuf_tensor(pg.uname(name), shape, dt)).ap(), name)
    w = cx.w
    TL = min(2048, L)
    ntile = L // TL
    cw = sb("L_cw", [128, 4, 4], F32)
    load_T(pg, cx, cw, cw.ap.rearrange("p j c -> p (j c)"), w["lru_conv_w"][l].rearrange("j (c p) -> (j c) p", p=128), 16)
    cb = sb("L_cb", [128, 4], F32)
    load_T(pg, cx, cb, cb.ap, w["lru_conv_b"][l].rearrange("(c p) -> c p", p=128), 4)
    bias = sb("L_bias", [128, 2, 2, 4], F32)
    load_T(pg, cx, bias, bias.ap[:, 0].rearrange("p d c -> p (d c)"), w["lru_b_a"][l].rearrange("d (c p) -> (d c) p", p=128), 8)
    load_T(pg, cx, bias, bias.ap[:, 1].rearrange("p d c -> p (d c)"), w["lru_b_x"][l].rearrange("d (c p) -> (d c) p", p=128), 8)
    lam = sb("L_lam", [128, 2, 4], F32)
    load_T(pg, cx, lam, lam.ap.rearrange("p d c -> p (d c)"), w["lru_lambda"][l].rearrange("d (c p) -> (d c) p", p=128), 8)
    coef = sb("L_coef", [128, 2, 4], F32)
    coef2 = sb("L_coef2", [128, 2, 4], F32)
    pg.op("act", lambda e: e.activation(out=coef.ap, in_=lam.ap, func=AF.Exp, scale=-1.0), reads=[lam], writes=[coef])
    pg.op("act", lambda e: e.activation(out=coef.ap, in_=coef.ap, func=AF.Ln, bias=cx.one.ap[:, 0:1]), reads=[coef, cx.one], writes=[coef])
    pg.op("dve", lambda e: e.tensor_scalar(out=coef2.ap, in0=coef.ap, scalar1=-16.0, scalar2=None, op0=ALU.mult), reads=[coef], writes=[coef2])
    pg.op("dve", lambda e: e.tensor_scalar(out=coef.ap, in0=coef.ap, scalar1=-8.0, scalar2=None, op0=ALU.mult), reads=[coef], writes=[coef])
    wg = sb("L_wg", [128, 2, 2, 4, 128], BF16)
    wst = sb("L_wst", [128, 2, 4, 128], F32)
    for ai, nm in enumerate(("lru_w_a", "lru_w_x")):
        pg.dma(wst.ap, w[nm][l].rearrange("d h i j -> i d h j"), writes=[wst])
        pg.op("dve", lambda e: e.tensor_copy(out=wg.ap[:, ai], in_=wst.ap), reads=[wst], writes=[wg])
    XC = sb("L_XC", [128, L], F32)
    XCB = sb("L_XCB", [128, L], BF16)
    HF = sb("L_HF", [128, L], F32)
    xin = sb("L_xin", [128, TL + 3], F32)
    rt = sb("L_r", [128, TL], F32)
    itl = sb("L_i", [128, TL], F32)
    at = sb("L_a", [128, TL], F32)
    t2 = sb("L_t2", [128, TL], F32)
    gt = sb("L_g", [128, TL], F32)
    yb = sb("L_y", [128, TL], BF16)
    carry = sb("L_carry", [128, 1], F32)
    for c in range(4):
        prow = PF_LRU_X + c * 128
        for it in range(ntile):
            t0 = it * TL
            lo = max(t0 - 2, 0)
            hi = min(t0 + TL + 1, L)
            if it == 0 or it == ntile - 1:
                pg.op("pool", lambda e: e.memset(xin.ap, 0.0), writes=[xin])
            pg.dma(xin.ap[:, lo - (t0 - 2):hi - (t0 - 2)], cx.PF[prow:prow + 128, lo:hi], writes=[xin])
            xo = XC.ap[:, t0:t0 + TL]
            pg.op("dve", lambda e: e.tensor_scalar(out=xo, in0=xin.ap[:, 0:TL], scalar1=cw.ap[:, 0, c:c + 1], scalar2=cb.ap[:, c:c + 1], op0=ALU.mult, op1=ALU.add),
                  reads=[xin, cw, cb], writes=[XC])
            for j in range(1, 4):
                pg.op("dve", lambda e: e.scalar_tensor_tensor(out=xo, in0=xin.ap[:, j:j + TL], scalar=cw.ap[:, j, c:c + 1], in1=xo, op0=ALU.mult, op1=ALU.add),
                      reads=[xin, cw, XC], writes=[XC])
            pg.op("act", lambda e: e.copy(XCB.ap[:, t0:t0 + TL], xo), reads=[XC], writes=[XCB])
        for d in range(2):
            order = range(ntile) if d == 0 else range(ntile - 1, -1, -1)
            for n_i, it in enumerate(order):
                t0 = it * TL
                for s0 in range(0, TL, 512):
                    for ai, dst in ((0, rt), (1, itl)):
                        ps = cx.psf.get()
                        pg.op("pe", lambda e: e.matmul(ps.ap, lhsT=wg.ap[:, ai, d, c, :], rhs=XCB.ap[:, t0 + s0:t0 + s0 + 512], start=True, stop=True),
                              reads=[wg, XCB], writes=[ps])
                        pg.op("act", lambda e: e.activation(out=dst.ap[:, s0:s0 + 512], in_=ps.ap, func=AF.Sigmoid, bias=bias.ap[:, ai, d, c:c + 1]),
                              reads=[ps, bias], writes=[dst])
                pg.op("act", lambda e: e.activation(out=at.ap, in_=rt.ap, func=AF.Exp, scale=coef.ap[:, d, c:c + 1]), reads=[rt, coef], writes=[at])
                pg.op("act", lambda e: e.activation(out=t2.ap, in_=rt.ap, func=AF.Exp, scale=coef2.ap[:, d, c:c + 1]), reads=[rt, coef2], writes=[t2])
                pg.op("act", lambda e: e.activation(out=t2.ap, in_=t2.ap, func=AF.Sqrt, scale=-1.0, bias=cx.one.ap[:, 0:1]), reads=[t2, cx.one], writes=[t2])
                pg.op("pool", lambda e: e.tensor_tensor(out=itl.ap, in0=itl.ap, in1=XC.ap[:, t0:t0 + TL], op=ALU.mult), reads=[itl, XC], writes=[itl])
                pg.op("dve", lambda e: e.tensor_tensor(out=t2.ap, in0=t2.ap, in1=itl.ap, op=ALU.mult), reads=[t2, itl], writes=[t2])
                init = 0.0 if n_i == 0 else carry.ap[:, 0:1]
                rds = [at, t2] + ([] if n_i == 0 else [carry])
                if d == 0:
                    ho = HF.ap[:, t0:t0 + TL]
                    pg.op("dve", lambda e: e.tensor_tensor_scan(out=ho, data0=at.ap, data1=t2.ap, initial=init, op0=ALU.mult, op1=ALU.add),
                          reads=rds, writes=[HF])
                    pg.op("dve", lambda e: e.tensor_copy(out=carry.ap, in_=HF.ap[:, t0 + TL - 1:t0 + TL]), reads=[HF], writes=[carry])
                else:
                    rv = lambda ap: bass.AP(ap.tensor, ap.offset + TL - 1, [list(ap.ap[0]), [-1, TL]])
                    pg.op("dve", lambda e: e.tensor_tensor_scan(out=rv(rt.ap), data0=rv(at.ap), data1=rv(t2.ap), initial=init, op0=ALU.mult, op1=ALU.add),
                          reads=rds, writes=[rt])
                    pg.op("dve", lambda e: e.tensor_copy(out=carry.ap, in_=rt.ap[:, 0:1]), reads=[rt], writes=[carry])
                    grow = PF_LRU_G + c * 128
                    pg.dma(gt.ap, cx.PF[grow:grow + 128, t0:t0 + TL], writes=[gt])
                    pg.op("act", lambda e: e.activation(out=gt.ap, in_=gt.ap, func=AF.Silu), reads=[gt], writes=[gt])
                    pg.op("pool", lambda e: e.tensor_tensor(out=rt.ap, in0=rt.ap, in1=HF.ap[:, t0:t0 + TL], op=ALU.add), reads=[rt, HF], writes=[rt])
                    pg.op("dve", lambda e: e.tensor_tensor(out=yb.ap, in0=rt.ap, in1=gt.ap, op=ALU.mult), reads=[rt, gt], writes=[yb])
                    pg.dma(cx.BT[c * 128:(c + 1) * 128, t0:t0 + TL], yb.ap, reads=[yb])


def phase_M(pg, cx, es, L, x_ap, xout_ap, l, last):
    nc = cx.nc
    TT = 512
    sb = lambda name, shape, dt: pg.buf(es.enter_context(nc.sbuf_tensor(pg.uname(name), shape, dt)).ap(), name)
    w = cx.w
    wmg = sb("M_wmg", [128, 4, 8, D], BF16)
    wbr = sb("M_wbr", [128, 4, 4, D], BF16)
    wout = sb("M_wout", [128, 8, D], BF16)
    st = [sb("M_st%d" % i, [128, D], F32) for i in range(2)]
    jobs = []
    for n in range(4):
        for k in range(8):
            jobs.append((w["w_merge_gate"][l, n, k * 128:(k + 1) * 128, :], wmg, wmg.ap[:, n, k, :]))
        for k in range(4):
            jobs.append((w["w_branch"][l, n, k * 128:(k + 1) * 128, :], wbr, wbr.ap[:, n, k, :]))
    for k in range(8):
        jobs.append((w["w_out"][l, k * 128:(k + 1) * 128, :], wout, wout.ap[:, k, :]))
    for i, (src, dbuf, dap) in enumerate(jobs):
        s = st[i % 2]
        pg.dma(s.ap, src, writes=[s])
        if i % 2 == 0:
            pg.op("act", lambda e: e.copy(dap, s.ap), reads=[s], writes=[dbuf])
        else:
            pg.op("dve", lambda e: e.tensor_copy(out=dap, in_=s.ap), reads=[s], writes=[dbuf])
    bmg = sb("M_bmg", [128, 4, 8], F32)
    load_T(pg, cx, bmg, bmg.ap.rearrange("p n c -> p (n c)"), w["b_merge_gate"][l].rearrange("n (c p) -> (n c) p", p=128), 32)
    if last:
        fg = sb("M_fg", [128, D], F32)
        fsrc = w["final_norm_g"]
        pg.dma(fg.ap, bass.AP(fsrc.tensor, fsrc.offset, [[0, 128], [1, D]]), writes=[fg])
        ss = sb("M_ss", [128, 4], F32)
        junk = sb("M_junk", [128, D], BF16)
    xn = sb("M_xn", [128, 8, TT], BF16)
    bt = sb("M_bt", [128, 16, TT], BF16)
    xt = sb("M_x", [128, 4, D], F32)
    mg = sb("M_mg", [128, 8, TT], BF16)
    gsb = sb("M_g", [128, TT], F32)
    tmp = sb("M_tmp", [128, TT], F32)
    acc = sb("M_acc", [128, TT], F32)
    XNTv = cx.XNT.rearrange("(k p) t -> p k t", p=128)
    BTv = cx.BT.rearrange("(k p) t -> p k t", p=128)
    xv = x_ap.rearrange("(n j p) d -> n p j d", p=128, j=4)
    ov = xout_ap.rearrange("(n j p) d -> n p j d", p=128, j=4)
    for it in range(L // TT):
        ts = slice(it * TT, (it + 1) * TT)
        pg.dma(xn.ap, XNTv[:, :, ts], writes=[xn])
        pg.dma(bt.ap, BTv[:, :, ts], writes=[bt])
        pg.dma(xt.ap, xv[it], writes=[xt])
        for oc in range(8):
            ocs = slice(oc * 128, (oc + 1) * 128)
            for n in range(4):
                pg_ = cx.psf.get()
                for k in range(8):
                    pg.op("pe", lambda e: e.matmul(pg_.ap, lhsT=wmg.ap[:, n, k, ocs], rhs=xn.ap[:, k, :], start=(k == 0), stop=(k == 7)),
                          reads=[wmg, xn], writes=[pg_])
                pg.op("act", lambda e: e.activation(out=gsb.ap, in_=pg_.ap, func=AF.Sigmoid, bias=bmg.ap[:, n, oc:oc + 1]), reads=[pg_, bmg], writes=[gsb])
                pb = cx.psf.get()
                for k in range(4):
                    pg.op("pe", lambda e: e.matmul(pb.ap, lhsT=wbr.ap[:, n, k, ocs], rhs=bt.ap[:, n * 4 + k, :], start=(k == 0), stop=(k == 3)),
                          reads=[wbr, bt], writes=[pb])
                if n == 0:
                    pg.op("dve", lambda e: e.tensor_tensor(out=acc.ap, in0=pb.ap, in1=gsb.ap, op=ALU.mult), reads=[pb, gsb], writes=[acc])
                else:
                    pg.op("dve", lambda e: e.tensor_tensor(out=tmp.ap, in0=pb.ap, in1=gsb.ap, op=ALU.mult), reads=[pb, gsb], writes=[tmp])
                    if n < 3:
                        pg.op("pool", lambda e: e.tensor_tensor(out=acc.ap, in0=acc.ap, in1=tmp.ap, op=ALU.add), reads=[acc, tmp], writes=[acc])
                    else:
                        pg.op("pool", lambda e: e.tensor_tensor(out=mg.ap[:, oc, :], in0=acc.ap, in1=tmp.ap, op=ALU.add), reads=[acc, tmp], writes=[mg])
        for j in range(4):
            for hf in range(2):
                hs = slice(hf * 512, (hf + 1) * 512)
                ps = cx.psf.get()
                for k in range(8):
                    pg.op("pe", lambda e: e.matmul(ps.ap, lhsT=mg.ap[:, k, j * 128:(j + 1) * 128], rhs=wout.ap[:, k, hs], start=(k == 0), stop=(k == 7)),
                          reads=[mg, wout], writes=[ps])
                pg.op("dve", lambda e: e.tensor_tensor(out=xt.ap[:, j, hs], in0=ps.ap, in1=xt.ap[:, j, hs], op=ALU.add), reads=[ps, xt], writes=[xt])
            if last:
                pg.op("act", lambda e: e.activation(out=junk.ap, in_=xt.ap[:, j, :], func=AF.Square, accum_out=ss.ap[:, j:j + 1]), reads=[xt], writes=[junk, ss])
                pg.op("act", lambda e: e.activation(out=ss.ap[:, j:j + 1], in_=ss.ap[:, j:j + 1], func=AF.Sqrt, scale=1.0 / D, bias=cx.eps.ap[:, 0:1]),
                      reads=[ss, cx.eps], writes=[ss])
                pg.op("dve", lambda e: e.reciprocal(out=ss.ap[:, j:j + 1], in_=ss.ap[:, j:j + 1]), reads=[ss], writes=[ss])
                pg.op("dve", lambda e: e.scalar_tensor_tensor(out=xt.ap[:, j, :], in0=xt.ap[:, j, :], scalar=ss.ap[:, j:j + 1], in1=fg.ap, op0=ALU.mult, op1=ALU.mult),
                      reads=[xt, ss, fg], writes=[xt])
        pg.dma(ov[it], xt.ap, reads=[xt])


def phase_GLA(pg, cx, es, L, l):
    nc = cx.nc
    sb = lambda name, shape, dt: pg.buf(es.enter_context(nc.sbuf_tensor(pg.uname(name), shape, dt)).ap(), name)
    w = cx.w
    NB = L // 128
    wup = sb("G_wup", [32, 2, 256], F32)
    for d in range(2):
        pg.dma(wup.ap[0:16, d, :], w["gla_w_up"][l, d], writes=[wup])
        pg.dma(wup.ap[16:17, d, :], w["gla_b_up"][l, d:d + 1, :], writes=[wup])
    gn = sb("G_gn", [128, 128], F32)
    gsrc = w["gla_norm_g"][l]
    pg.dma(gn.ap, bass.AP(gsrc.tensor, gsrc.offset, [[0, 128], [1, 128]]), writes=[gn])
    lrT = [sb("G_lrT%d" % i, [32, 128], F32) for i in range(2)]
    for b in lrT:
        pg.op("dve", lambda e: e.memset(b.ap, 1.0), writes=[b])
    qk = [sb("G_qk%d" % i, [128, 4, 128], F32) for i in range(2)]
    tk = [sb("G_tk%d" % i, [128, 1280], F32) for i in range(2)]
    obt = [sb("G_ob%d" % i, [128, 512], F32) for i in range(2)]
    la = sb("G_la", [128, 256], F32)
    e1 = sb("G_e1", [128, 256], F32)
    eb = sb("G_eb", [128, 2, 128], F32)
    enb = sb("G_enb", [128, 2, 128], F32)
    qd = sb("G_qd", [128, 2, 128], BF16)
    ki = sb("G_ki", [128, 2, 128], BF16)
    ed = sb("G_ed", [128, 256], F32)
    kend = sb("G_kend", [128, 256], BF16)
    vb = sb("G_vb", [128, 512], BF16)
    sm = [sb("G_sm%d" % i, [128, 128], BF16) for i in range(2)]
    S32 = [sb("G_S32_%d" % h, [128, 128], F32) for h in range(4)]
    Sb = [sb("G_Sb_%d" % h, [128, 128], BF16) for h in range(4)]
    osb = sb("G_osb", [128, 512], F32)
    ssq = sb("G_ssq", [128, 4], F32)
    junk = sb("G_junk", [128, 128], BF16)
    ysb = sb("G_ysb", [128, 512], BF16)
    yT = sb("G_yT", [128, 4, 128], BF16)
    PFq = cx.PF[PF_GLA_Q:PF_GLA_Q + 512, :].rearrange("(c p) t -> p c t", p=128)
    for d in (1, 0):
        pg.barrier()
        for h in range(4):
            pg.op("dve", lambda e: e.memset(S32[h].ap, 0.0), writes=[S32[h]])
            pg.op("pool", lambda e: e.memset(Sb[h].ap, 0.0), writes=[Sb[h]])
        order = range(NB) if d == 0 else range(NB - 1, -1, -1)
        for bi, blk in enumerate(order):
            t0 = blk * 128
            ts = slice(t0, t0 + 128)
            qkb = qk[bi % 2]; tkb = tk[bi % 2]; lrb = lrT[bi % 2]; ob = obt[bi % 2]
            pg.dma(qkb.ap, PFq[:, :, ts], writes=[qkb])
            pg.dma(tkb.ap, cx.PT[ts, 0:1280], writes=[tkb])
            pg.dma(lrb.ap[0:16, :], cx.PF[PF_GLA_LR + 16 * d:PF_GLA_LR + 16 * d + 16, ts], writes=[lrb])
            if d == 0:
                pg.dma(ob.ap, cx.OB[ts, 0:512], writes=[ob])
            zp = cx.psf.get()
            pg.op("pe", lambda e: e.matmul(zp.ap[:, :256], lhsT=lrb.ap[0:17, :], rhs=wup.ap[0:17, d, :], start=True, stop=True), reads=[lrb, wup], writes=[zp])
            pg.op("act", lambda e: e.activation(out=e1.ap, in_=zp.ap[:, :256], func=AF.Exp, scale=-1.0), reads=[zp], writes=[e1])
            pg.op("act", lambda e: e.activation(out=e1.ap, in_=e1.ap, func=AF.Ln, bias=cx.one.ap[:, 0:1]), reads=[e1, cx.one], writes=[e1])
            pg.op("dve", lambda e: e.tensor_scalar(out=la.ap, in0=e1.ap, scalar1=-1.0 / 16.0, scalar2=None, op0=ALU.mult), reads=[e1], writes=[la])
            bp = cx.psf.get()
            for h2 in range(2):
                pg.op("pe", lambda e: e.matmul(bp.ap[:, h2 * 128:(h2 + 1) * 128], lhsT=la.ap[:, h2 * 128:(h2 + 1) * 128], rhs=cx.m_incl.ap[:, d, :], start=True, stop=True),
                      reads=[la, cx.m_incl], writes=[bp])
            bp3 = bp.ap[:, 0:256].rearrange("p (c t) -> p c t", c=2)
            pg.op("act", lambda e: e.activation(out=eb.ap, in_=bp3, func=AF.Exp), reads=[bp], writes=[eb])
            pg.op("act", lambda e: e.activation(out=enb.ap, in_=bp3, func=AF.Exp, scale=-1.0), reads=[bp], writes=[enb])
            pg.op("dve", lambda e: e.scalar_tensor_tensor(out=qd.ap, in0=qkb.ap[:, 0:2, :], scalar=0.125, in1=eb.ap, op0=ALU.mult, op1=ALU.mult), reads=[qkb, eb], writes=[qd])
            pg.op("pool", lambda e: e.tensor_tensor(out=ki.ap, in0=qkb.ap[:, 2:4, :], in1=enb.ap, op=ALU.mult), reads=[qkb, enb], writes=[ki])
            dp = cx.psf.get()
            pg.op("pe", lambda e: e.matmul(dp.ap[:, :256], lhsT=cx.m_sa.ap[:, d, :], rhs=la.ap, start=True, stop=True), reads=[la, cx.m_sa], writes=[dp])
            pg.op("act", lambda e: e.activation(out=ed.ap, in_=dp.ap[:, :256], func=AF.Exp), reads=[dp], writes=[ed])
            pg.op("dve", lambda e: e.tensor_tensor(out=kend.ap, in0=tkb.ap[:, 0:256], in1=ed.ap, op=ALU.mult), reads=[tkb, ed], writes=[kend])
            pg.op("pool", lambda e: e.tensor_copy(out=vb.ap, in_=tkb.ap[:, 256:768]), reads=[tkb], writes=[vb])
            op_ = cx.pso
            chunks = (0, 1) if d == 0 else (1, 0)
            for h in range(4):
                h2, hp = h // 2, (h % 2) * 64
                hc = slice(h * 128, (h + 1) * 128)
                sp_ = cx.psf.get()
                pg.op("pe", lambda e: e.matmul(sp_.ap[:, :128], lhsT=ki.ap[hp:hp + 64, h2, :], rhs=qd.ap[hp:hp + 64, h2, :], start=True, stop=True), reads=[ki, qd], writes=[sp_])
                smb = sm[h % 2]
                pg.op("dve", lambda e: e.tensor_tensor(out=smb.ap, in0=sp_.ap[:, :128], in1=cx.m_incl.ap[:, d, :], op=ALU.mult), reads=[sp_, cx.m_incl], writes=[smb])
                pg.op("pe", lambda e: e.matmul(op_.ap[:, hc], lhsT=smb.ap, rhs=vb.ap[:, hc], start=True, stop=False), reads=[smb, vb], writes=[op_])
                for ci, c in enumerate(chunks):
                    r0 = c * 64
                    pg.op("pe", lambda e: e.matmul(op_.ap[r0:r0 + 64, hc], lhsT=qd.ap[hp:hp + 64, h2, r0:r0 + 64], rhs=Sb[h].ap[hp:hp + 64, :], start=False, stop=(ci == 1)),
                          reads=[qd, Sb[h]], writes=[op_])
                    kvp = cx.psf.get()
                    pg.op("pe", lambda e: e.matmul(kvp.ap[hp:hp + 64, :128], lhsT=kend.ap[r0:r0 + 64, h * 64:(h + 1) * 64], rhs=vb.ap[r0:r0 + 64, hc], start=True, stop=True),
                          reads=[kend, vb], writes=[kvp])
                    col = r0 + 63 if d == 0 else r0
                    pg.op("dve", lambda e: e.scalar_tensor_tensor(out=S32[h].ap[hp:hp + 64, :], in0=S32[h].ap[hp:hp + 64, :], scalar=eb.ap[hp:hp + 64, h2, col:col + 1],
                                                                  in1=kvp.ap[hp:hp + 64, :128], op0=ALU.mult, op1=ALU.add), reads=[S32[h], eb, kvp], writes=[S32[h]])
                    pg.op("act", lambda e: e.copy(Sb[h].ap[hp:hp + 64, :], S32[h].ap[hp:hp + 64, :]), reads=[S32[h]], writes=[Sb[h]])
            if d == 1:
                pg.op("act", lambda e: e.copy(osb.ap, op_.ap), reads=[op_], writes=[osb])
                pg.dma(cx.OB[ts, 0:512], osb.ap, reads=[osb])
            else:
                pg.op("dve", lambda e: e.tensor_tensor(out=osb.ap, in0=op_.ap, in1=ob.ap, op=ALU.add), reads=[op_, ob], writes=[osb])
                head_norm_gate_store(pg, cx, osb, ssq, junk, gn, tkb, 768, ysb, yT, 512, ts)


def head_norm_gate_store(pg, cx, osb, ssq, junk, gn, tkb, gcol, ysb, yT, bt_row0, ts):
    for h in range(4):
        hc = slice(h * 128, (h + 1) * 128)
        pg.op("act", lambda e: e.activation(out=junk.ap, in_=osb.ap[:, hc], func=AF.Square, accum_out=ssq.ap[:, h:h + 1]), reads=[osb], writes=[junk, ssq])
    pg.op("act", lambda e: e.activation(out=ssq.ap, in_=ssq.ap, func=AF.Sqrt, scale=1.0 / 128.0, bias=cx.eps.ap[:, 0:1]), reads=[ssq, cx.eps], writes=[ssq])
    pg.op("dve", lambda e: e.reciprocal(out=ssq.ap, in_=ssq.ap), reads=[ssq], writes=[ssq])
    for h in range(4):
        hc = slice(h * 128, (h + 1) * 128)
        pg.op("dve", lambda e: e.scalar_tensor_tensor(out=osb.ap[:, hc], in0=osb.ap[:, hc], scalar=ssq.ap[:, h:h + 1], in1=gn.ap, op0=ALU.mult, op1=ALU.mult),
              reads=[osb, ssq, gn], writes=[osb])
    pg.op("act", lambda e: e.activation(out=tkb.ap[:, gcol:gcol + 512], in_=tkb.ap[:, gcol:gcol + 512], func=AF.Silu), reads=[tkb], writes=[tkb])
    pg.op("dve", lambda e: e.tensor_tensor(out=ysb.ap, in0=osb.ap, in1=tkb.ap[:, gcol:gcol + 512], op=ALU.mult), reads=[osb, tkb], writes=[ysb])
    pb = cx.psb.get()
    for h in range(4):
        pg.op("pe", lambda e: e.transpose(out=pb.ap[:, h * 128:(h + 1) * 128], in_=ysb.ap[:, h * 128:(h + 1) * 128], identity=cx.identb.ap), reads=[ysb, cx.identb], writes=[pb])
    pg.op("act", lambda e: e.copy(yT.ap, pb.ap[:, 0:512].rearrange("p (c t) -> p c t", c=4)), reads=[pb], writes=[yT])
    pg.dma(cx.BT[bt_row0:bt_row0 + 512, ts].rearrange("(c p) t -> p c t", p=128), yT.ap, reads=[yT])


def phase_DN(pg, cx, es, L, l):
    nc = cx.nc
    w = cx.w
    NB = L // 128
    with ExitStack() as es0:
        sb = lambda name, shape, dt: pg.buf(es0.enter_context(nc.sbuf_tensor(pg.uname(name), shape, dt)).ap(), name)
        TL = 512
        cwD = sb("D0_cw", [128, 4, 12], F32)
        load_T(pg, cx, cwD, cwD.ap.rearrange("p j c -> p (j c)"), w["dn_conv_w"][l].rearrange("j (c p) -> (j c) p", p=128), 48)
        xin = [sb("D0_xin%d" % i, [128, TL + 3], F32) for i in range(2)]
        xc = sb("D0_xc", [128, TL], F32)
        sq = sb("D0_sq", [128, TL], F32)
        rs = sb("D0_rs", [128, TL], F32)
        fm = sb("D0_fm", [128, 12, TL], BF16)
        tm = sb("D0_tm", [128, 4, 1024], BF16)
        nt = L // TL
        for it in range(nt):
            t0 = it * TL
            lo, hi = max(t0 - 2, 0), min(t0 + TL + 1, L)
            for c in range(12):
                xb = xin[c % 2]
                if it == 0 or it == nt - 1:
                    pg.op("pool", lambda e: e.memset(xb.ap, 0.0), writes=[xb])
                prow = PF_DN_QKV + c * 128
                pg.dma(xb.ap[:, lo - (t0 - 2):hi - (t0 - 2)], cx.PF[prow:prow + 128, lo:hi], writes=[xb])
                pg.op("dve", lambda e: e.tensor_scalar(out=xc.ap, in0=xb.ap[:, 0:TL], scalar1=cwD.ap[:, 0, c:c + 1], scalar2=None, op0=ALU.mult), reads=[xb, cwD], writes=[xc])
                for j in range(1, 4):
                    pg.op("dve", lambda e: e.scalar_tensor_tensor(out=xc.ap, in0=xb.ap[:, j:j + TL], scalar=cwD.ap[:, j, c:c + 1], in1=xc.ap, op0=ALU.mult, op1=ALU.add),
                          reads=[xb, cwD, xc], writes=[xc])
                if c >= 8:
                    pg.op("act", lambda e: e.activation(out=fm.ap[:, c, :], in_=xc.ap, func=AF.Silu), reads=[xc], writes=[fm])
                else:
                    pg.op("act", lambda e: e.activation(out=xc.ap, in_=xc.ap, func=AF.Silu), reads=[xc], writes=[xc])
                    pg.op("pool", lambda e: e.tensor_tensor(out=sq.ap, in0=xc.ap, in1=xc.ap, op=ALU.mult), reads=[xc], writes=[sq])
                    ps = cx.psf.get()
                    pg.op("pe", lambda e: e.matmul(ps.ap, lhsT=cx.onesf.ap, rhs=sq.ap, start=True, stop=True), reads=[cx.onesf, sq], writes=[ps])
                    pg.op("act", lambda e: e.activation(out=rs.ap, in_=ps.ap, func=AF.Sqrt, bias=cx.eps.ap[:, 0:1]), reads=[ps, cx.eps], writes=[rs])
                    pg.op("dve", lambda e: e.reciprocal(out=rs.ap, in_=rs.ap), reads=[rs], writes=[rs])
                    sc = (128.0 ** -0.5) if c < 4 else 1.0
                    pg.op("dve", lambda e: e.scalar_tensor_tensor(out=fm.ap[:, c, :], in0=xc.ap, scalar=sc, in1=rs.ap, op0=ALU.mult, op1=ALU.mult), reads=[xc, rs], writes=[fm])
            pg.dma(cx.QKT[:, t0:t0 + TL].rearrange("(c p) t -> p c t", p=128), fm.ap[:, 0:8, :], reads=[fm])
            for j in range(4):
                pb = cx.psb.get()
                for c in range(8):
                    pg.op("pe", lambda e: e.transpose(out=pb.ap[:, c * 128:(c + 1) * 128], in_=fm.ap[:, 4 + c, j * 128:(j + 1) * 128], identity=cx.identb.ap),
                          reads=[fm, cx.identb], writes=[pb])
                pg.op("act", lambda e: e.copy(tm.ap[:, j, :], pb.ap), reads=[pb], writes=[tm])
            pg.dma(cx.KVT[t0:t0 + TL, :].rearrange("(j p) c -> p j c", p=128), tm.ap, reads=[tm])
        pg.barrier()
    sb = lambda name, shape, dt: pg.buf(es.enter_context(nc.sbuf_tensor(pg.uname(name), shape, dt)).ap(), name)
    ba = sb("D_ba", [128, NB, 16], F32)
    pg.dma(ba.ap, cx.PT[:, PT_DN_BA:PT_DN_BA + 16].rearrange("(n p) c -> p n c", p=128), writes=[ba])
    ba4 = ba.ap.rearrange("p n (d j h) -> p n d j h", d=2, j=2)
    dtb = sb("D_dtb", [128, 8], F32)
    nea = sb("D_nea", [128, 8], F32)
    s1 = w["dn_dt_bias"][l]
    pg.dma(dtb.ap, bass.AP(s1.tensor, s1.offset, [[0, 128], [1, 8]]), writes=[dtb])
    s2 = w["dn_a_log"][l]
    pg.dma(nea.ap, bass.AP(s2.tensor, s2.offset, [[0, 128], [1, 8]]), writes=[nea])
    pg.op("act", lambda e: e.activation(out=nea.ap, in_=nea.ap, func=AF.Exp), reads=[nea], writes=[nea])
    pg.op("dve", lambda e: e.tensor_scalar(out=nea.ap, in0=nea.ap, scalar1=-1.0, scalar2=None, op0=ALU.mult), reads=[nea], writes=[nea])
    beta = sb("D_beta", [128, NB, 2, 4], F32)
    nbeta = sb("D_nbeta", [128, NB, 2, 4], F32)
    g = sb("D_g", [128, NB, 2, 4], F32)
    pg.op("act", lambda e: e.activation(out=beta.ap, in_=ba4[:, :, :, 0, :], func=AF.Sigmoid), reads=[ba], writes=[beta])
    pg.op("dve", lambda e: e.tensor_scalar(out=nbeta.ap, in0=beta.ap, scalar1=-1.0, scalar2=None, op0=ALU.mult), reads=[beta], writes=[nbeta])
    dtb_b = dtb.ap.rearrange("p (d h) -> p d h", d=2).unsqueeze(1).to_broadcast([128, NB, 2, 4])
    nea_b = nea.ap.rearrange("p (d h) -> p d h", d=2).unsqueeze(1).to_broadcast([128, NB, 2, 4])
    pg.op("dve", lambda e: e.tensor_tensor(out=g.ap, in0=ba4[:, :, :, 1, :], in1=dtb_b, op=ALU.add), reads=[ba, dtb], writes=[g])
    pg.op("act", lambda e: e.activation(out=g.ap, in_=g.ap, func=AF.Exp), reads=[g], writes=[g])
    pg.op("act", lambda e: e.activation(out=g.ap, in_=g.ap, func=AF.Ln, bias=cx.one.ap[:, 0:1]), reads=[g, cx.one], writes=[g])
    pg.op("dve", lambda e: e.tensor_tensor(out=g.ap, in0=g.ap, in1=nea_b, op=ALU.mult), reads=[g, nea], writes=[g])
    eG = sb("D_eG", [128, NB, 2, 4], F32)
    eD = sb("D_eD", [128, NB, 2, 4], F32)
    bg = sb("D_bg", [128, NB, 2, 4], F32)
    deB = sb("D_deB", [128, 2, NB, 2, 4], F32)
    NQ = 32
    for d in range(2):
        for n0 in range(0, NB, NQ):
            nn = min(NQ, NB - n0)
            for (msk, dst, fn) in ((cx.m_incl.ap[:, d, :], eG, 0), (cx.m_sa.ap[:, d, :], eD, 0), (cx.chunkind.ap[:, 0, :], deB, 1), (cx.chunkind.ap[:, 1, :], deB, 2)):
                ps = cx.psf.get()
                pv = ps.ap[:, :nn * 4].rearrange("p (n h) -> p n h", h=4)
                pg.op("pe", lambda e: e.matmul(pv, lhsT=msk, rhs=g.ap[:, n0:n0 + nn, d, :], start=True, stop=True), reads=[g, cx.m_incl, cx.m_sa, cx.chunkind], writes=[ps])
                o_ap = dst.ap[:, n0:n0 + nn, d, :] if fn == 0 else dst.ap[:, fn - 1, n0:n0 + nn, d, :]
                pg.op("act", lambda e: e.activation(out=o_ap, in_=pv, func=AF.Exp), reads=[ps], writes=[dst])
    pg.op("dve", lambda e: e.tensor_tensor(out=bg.ap, in0=beta.ap, in1=eG.ap, op=ALU.mult), reads=[beta, eG], writes=[bg])
    gn = sb("D_gn", [128, 128], F32)
    gsrc = w["dn_norm_g"][l]
    pg.dma(gn.ap, bass.AP(gsrc.tensor, gsrc.offset, [[0, 128], [1, 128]]), writes=[gn])
    qk = [sb("D_qk%d" % i, [128, 8, 128], BF16) for i in range(2)]
    kv = [sb("D_kv%d" % i, [128, 2, 4, 128], BF16) for i in range(2)]
    gt = [sb("D_gt%d" % i, [128, 1280 + 512], F32) for i in range(1)]
    obt = [sb("D_ob%d" % i, [128, 512], F32) for i in range(2)]
    vb4 = sb("D_vb4", [128, 4, 128], BF16)
    kbg4 = sb("D_kbg4", [128, 4, 128], BF16)
    kend4 = sb("D_kend4", [128, 4, 128], BF16)
    gtri = [sb("D_gtri%d" % i, [128, 128], F32) for i in range(4)]
    gam = [sb("D_gam%d" % i, [128, 3, 128], F32) for i in range(4)]
    gamm = [sb("D_gamm%d" % i, [128, 2, 128], F32) for i in range(4)]
    qd = [sb("D_qd%d" % i, [128, 128], BF16) for i in range(4)]
    Cm = [sb("D_C%d" % i, [128, 128], BF16) for i in range(4)]
    attnT = [sb("D_at%d" % i, [128, 128], BF16) for i in range(4)]
    BC = [[sb("D_BC%d_%d" % (h, i), [128, 2, 128], BF16) for i in range(2)] for h in range(4)]
    Pm = [[sb("D_P%d_%d" % (h, i), [128, 128], BF16) for i in range(2)] for h in range(4)]
    Pm32 = [[sb("D_P32_%d_%d" % (h, i), [128, 128], F32) for i in range(2)] for h in range(4)]
    usb = [sb("D_u%d" % i, [128, 128], F32) for i in range(4)]
    wT = [sb("D_wT%d" % i, [128, 128], BF16) for i in range(4)]
    vn = [sb("D_vn%d" % i, [128, 128], BF16) for i in range(4)]
    S32 = [sb("D_S32_%d" % h, [128, 128], F32) for h in range(4)]
    Sb = [sb("D_Sb_%d" % h, [128, 128], BF16) for h in range(4)]
    osb = sb("D_osb", [128, 512], F32)
    ssq = sb("D_ssq", [128, 4], F32)
    junk = sb("D_junk", [128, 128], BF16)
    ysb = sb("D_ysb", [128, 512], BF16)
    yT = sb("D_yT", [128, 4, 128], BF16)
    QKv = cx.QKT.rearrange("(c p) t -> p c t", p=128)
    rr = [0]
    for d in (1, 0):
        pg.barrier()
        for h in range(4):
            pg.op("dve", lambda e: e.memset(S32[h].ap, 0.0), writes=[S32[h]])
            pg.op("pool", lambda e: e.memset(Sb[h].ap, 0.0), writes=[Sb[h]])
        order = range(NB) if d == 0 else range(NB - 1, -1, -1)
        chunks = (0, 1) if d == 0 else (1, 0)
        for bi, blk in enumerate(order):
            t0 = blk * 128
            ts = slice(t0, t0 + 128)
            qkb = qk[bi % 2]; kvb = kv[bi % 2]; ob = obt[bi % 2]; gtb = gt[0]
            pg.dma(qkb.ap, QKv[:, :, ts], writes=[qkb])
            pg.dma(kvb.ap.rearrange("p a h c -> p (a h c)"), cx.KVT[ts, :], writes=[kvb])
            if d == 0:
                pg.dma(ob.ap, cx.OB[ts, 512:1024], writes=[ob])
                pg.dma(gtb.ap[:, 0:512], cx.PT[ts, PT_DN_G:PT_DN_G + 512], writes=[gtb])
            bcast = lambda t: t.ap[:, blk, d, :].unsqueeze(2).to_broadcast([128, 4, 128])
            pg.op("dve", lambda e: e.tensor_tensor(out=vb4.ap, in0=kvb.ap[:, 1], in1=bcast(beta), op=ALU.mult), reads=[kvb, beta], writes=[vb4])
            pg.op("pool", lambda e: e.tensor_tensor(out=kbg4.ap, in0=kvb.ap[:, 0], in1=bcast(bg), op=ALU.mult), reads=[kvb, bg], writes=[kbg4])
            pg.op("pool", lambda e: e.tensor_tensor(out=kend4.ap, in0=kvb.ap[:, 0], in1=bcast(eD), op=ALU.mult), reads=[kvb, eD], writes=[kend4])
            op_ = cx.pso
            def head_gen(h, blk=blk, d=d, qkb=qkb, chunks=chunks, op_=op_):
                i2 = h
                bank = cx.psf.items[h]
                hc = slice(h * 128, (h + 1) * 128)
                gsc = g.ap[:, blk, d, h:h + 1]
                pg.op("dve", lambda e: e.tensor_scalar(out=gtri[i2].ap, in0=cx.m_incl.ap[:, d, :], scalar1=gsc, scalar2=None, op0=ALU.mult), reads=[cx.m_incl, g], writes=[gtri[i2]])
                yield
                dps = bank
                pg.op("pe", lambda e: e.matmul(dps.ap[:, 0:128], lhsT=gtri[i2].ap, rhs=cx.m_sa.ap[:, d, :], start=True, stop=True), reads=[gtri[i2], cx.m_sa], writes=[dps])
                pg.op("pe", lambda e: e.matmul(dps.ap[:, 128:256], lhsT=cx.m_sa.ap[:, d, :], rhs=gtri[i2].ap, start=True, stop=True), reads=[gtri[i2], cx.m_sa], writes=[dps])
                pg.op("pe", lambda e: e.matmul(dps.ap[:, 256:384], lhsT=cx.onesf.ap, rhs=gtri[i2].ap, start=True, stop=True), reads=[gtri[i2], cx.onesf], writes=[dps])
                yield
                pg.op("act", lambda e: e.activation(out=gam[i2].ap.rearrange("p a t -> p (a t)"), in_=dps.ap[:, 0:384], func=AF.Exp), reads=[dps], writes=[gam[i2]])
                yield
                pg.op("pool", lambda e: e.tensor_tensor(out=gamm[i2].ap, in0=gam[i2].ap[:, 0:2, :], in1=cx.m_dn.ap[:, d], op=ALU.mult), reads=[gam[i2], cx.m_dn], writes=[gamm[i2]])
                pg.op("pool", lambda e: e.tensor_tensor(out=qd[i2].ap, in0=qkb.ap[:, h, :], in1=gam[i2].ap[:, 2, :], op=ALU.mult), reads=[qkb, gam[i2]], writes=[qd[i2]])
                kps = bank
                pg.op("pe", lambda e: e.matmul(kps.ap[:, 0:128], lhsT=qkb.ap[:, 4 + h, :], rhs=qkb.ap[:, 4 + h, :], start=True, stop=True), reads=[qkb], writes=[kps])
                pg.op("pe", lambda e: e.matmul(kps.ap[:, 128:256], lhsT=qkb.ap[:, 4 + h, :], rhs=qkb.ap[:, h, :], start=True, stop=True), reads=[qkb], writes=[kps])
                yield
                pg.op("dve", lambda e: e.scalar_tensor_tensor(out=Cm[i2].ap, in0=kps.ap[:, 0:128], scalar=nbeta.ap[:, blk, d, h:h + 1], in1=gamm[i2].ap[:, 0, :], op0=ALU.mult, op1=ALU.mult),
                      reads=[kps, nbeta, gamm[i2]], writes=[Cm[i2]])
                pg.op("dve", lambda e: e.tensor_tensor(out=attnT[i2].ap, in0=kps.ap[:, 128:256], in1=gamm[i2].ap[:, 1, :], op=ALU.mult), reads=[kps, gamm[i2]], writes=[attnT[i2]])
                yield
                tb = bank
                tbv = bank.ap.bitcast(BF16)
                pg.op("pe", lambda e: e.transpose(out=tbv[:, 0:128], in_=Cm[i2].ap, identity=cx.identb.ap), reads=[Cm[i2], cx.identb], writes=[tb])
                yield
                hr = [0]
                bc0 = BC[h][hr[0] % 2]
                pg.op("act", lambda e: e.copy(bc0.ap[:, 0, :], tbv[:, 0:128]), reads=[tb], writes=[bc0])
                pg.op("pool", lambda e: e.tensor_copy(out=bc0.ap[:, 1, :], in_=Cm[i2].ap), reads=[Cm[i2]], writes=[bc0])
                p0 = Pm[h][hr[0] % 2]; p032 = Pm32[h][hr[0] % 2]; hr[0] += 1
                pg.op("pool", lambda e: e.tensor_tensor(out=p0.ap, in0=bc0.ap[:, 0, :], in1=cx.identb.ap, op=ALU.add), reads=[bc0, cx.identb], writes=[p0])
                pg.op("pool", lambda e: e.tensor_tensor(out=p032.ap, in0=bc0.ap[:, 0, :], in1=cx.identf.ap, op=ALU.add), reads=[bc0, cx.identf], writes=[p032])
                yield
                bcp, pp, pp32 = bc0, p0, p032
                for k in range(1, 6):
                    sq_ = bank
                    if k < 5:
                        pg.op("pe", lambda e: e.matmul(sq_.ap[:, 0:128], lhsT=bcp.ap[:, 1, :], rhs=bcp.ap[:, 0, :], start=True, stop=True), reads=[bcp], writes=[sq_])
                    pg.op("pe", lambda e: e.matmul(sq_.ap[:, 128:256], lhsT=bcp.ap[:, 0, :], rhs=bcp.ap[:, 1, :], start=True, stop=True), reads=[bcp], writes=[sq_])
                    yield
                    bcn = BC[h][hr[0] % 2]
                    if k < 5:
                        pg.op("act", lambda e: e.copy(bcn.ap.rearrange("p a t -> p (a t)"), sq_.ap[:, 0:256]), reads=[sq_], writes=[bcn])
                    else:
                        pg.op("act", lambda e: e.copy(bcn.ap[:, 1, :], sq_.ap[:, 128:256]), reads=[sq_], writes=[bcn])
                        yield
                    pps = bank
                    pg.op("pe", lambda e: e.matmul(pps.ap[:, 0:128], lhsT=bcn.ap[:, 1, :], rhs=pp.ap, start=True, stop=True), reads=[bcn, pp], writes=[pps])
                    yield
                    pn = Pm[h][hr[0] % 2]; pn32 = Pm32[h][hr[0] % 2]; hr[0] += 1
                    pg.op("dve", lambda e: e.tensor_tensor(out=pn32.ap, in0=pps.ap[:, 0:128], in1=pp32.ap, op=ALU.add), reads=[pps, pp32], writes=[pn32])
                    pg.op("pool", lambda e: e.tensor_copy(out=pn.ap, in_=pn32.ap), reads=[pn32], writes=[pn])
                    yield
                    bcp, pp, pp32 = bcn, pn, pn32
                ups = bank
                pg.op("pe", lambda e: e.matmul(ups.ap[:, 0:128], lhsT=pp.ap, rhs=vb4.ap[:, h, :], start=True, stop=True), reads=[pp, vb4], writes=[ups])
                pg.op("pe", lambda e: e.matmul(ups.ap[:, 128:256], lhsT=kbg4.ap[:, h, :], rhs=pp.ap, start=True, stop=True), reads=[pp, kbg4], writes=[ups])
                yield
                pg.op("act", lambda e: e.copy(usb[i2].ap, ups.ap[:, 0:128]), reads=[ups], writes=[usb[i2]])
                pg.op("act", lambda e: e.copy(wT[i2].ap, ups.ap[:, 128:256]), reads=[ups], writes=[wT[i2]])
                yield
                for ci, c in enumerate(chunks):
                    r0 = c * 64
                    rs_ = slice(r0, r0 + 64)
                    wps = bank
                    pg.op("pe", lambda e: e.matmul(wps.ap[rs_, 0:128], lhsT=wT[i2].ap[:, rs_], rhs=Sb[h].ap, start=True, stop=True), reads=[wT[i2], Sb[h]], writes=[wps])
                    yield
                    pg.op("dve", lambda e: e.scalar_tensor_tensor(out=vn[i2].ap[rs_, :], in0=wps.ap[rs_, 0:128], scalar=-1.0, in1=usb[i2].ap[rs_, :], op0=ALU.mult, op1=ALU.add), reads=[usb[i2], wps], writes=[vn[i2]])
                    yield
                    pg.op("pe", lambda e: e.matmul(op_.ap[rs_, hc], lhsT=qd[i2].ap[:, rs_], rhs=Sb[h].ap, start=True, stop=False), reads=[qd[i2], Sb[h]], writes=[op_])
                    pg.op("pe", lambda e: e.matmul(op_.ap[rs_, hc], lhsT=attnT[i2].ap[rs_, rs_], rhs=vn[i2].ap[rs_, :], start=False, stop=True), reads=[attnT[i2], vn[i2]], writes=[op_])
                    kvp = bank
                    pg.op("pe", lambda e: e.matmul(kvp.ap[:, 0:128], lhsT=kend4.ap[rs_, h, :], rhs=vn[i2].ap[rs_, :], start=True, stop=True), reads=[kend4, vn[i2]], writes=[kvp])
                    yield
                    pg.op("dve", lambda e: e.scalar_tensor_tensor(out=S32[h].ap, in0=S32[h].ap, scalar=deB.ap[:, c, blk, d, h:h + 1], in1=kvp.ap[:, 0:128], op0=ALU.mult, op1=ALU.add),
                          reads=[S32[h], deB, kvp], writes=[S32[h]])
                    pg.op("act", lambda e: e.copy(Sb[h].ap, S32[h].ap), reads=[S32[h]], writes=[Sb[h]])
                    yield
            gens = [head_gen(h) for h in range(4)]
            while gens:
                for gnr in list(gens):
                    try:
                        next(gnr)
                    except StopIteration:
                        gens.remove(gnr)
            if d == 1:
                pg.op("act", lambda e: e.copy(osb.ap, op_.ap), reads=[op_], writes=[osb])
                pg.dma(cx.OB[ts, 512:1024], osb.ap, reads=[osb])
            else:
                pg.op("dve", lambda e: e.tensor_tensor(out=osb.ap, in0=op_.ap, in1=ob.ap, op=ALU.add), reads=[op_, ob], writes=[osb])
                head_norm_gate_store(pg, cx, osb, ssq, junk, gn, gtb, 0, ysb, yT, 1024, ts)


def phase_S5(pg, cx, es, L, l):
    nc = cx.nc
    w = cx.w
    sb = lambda name, shape, dt: pg.buf(es.enter_context(nc.sbuf_tensor(pg.uname(name), shape, dt)).ap(), name)
    NS = int(np.ceil(np.log2(L)))
    dve = lambda fn, r, wr: pg.op("dve", fn, reads=r, writes=wr)
    A = lambda nm: sb("S_" + nm, [128, 32], F32)
    lre, lim, dt_, ar, ai, m_, sn, cs, Are, Aim, t1, t2, t3, fre, fim, den = [A(n) for n in
        ("lre", "lim", "dt", "ar", "ai", "m", "sn", "cs", "Are", "Aim", "t1", "t2", "t3", "fre", "fim", "den")]
    load_T(pg, cx, lre, lre.ap, w["s5_lambda_re"][l].rearrange("d (gh gl) p -> (d gh) (gl p)", gl=2), 32)
    load_T(pg, cx, lim, lim.ap, w["s5_lambda_im"][l].rearrange("d (gh gl) p -> (d gh) (gl p)", gl=2), 32)
    ld2 = sb("S_ld2", [32, 2], F32)
    pg.dma(ld2.ap, w["s5_log_dt"][l].rearrange("d (gh gl) -> (d gh) gl", gl=2), writes=[ld2])
    stl = sb("S_stl", [32, 128], F32)
    for gl in range(2):
        dve(lambda e: e.tensor_copy(out=stl.ap[:, 64 * gl:64 * gl + 64], in_=ld2.ap[:, gl:gl + 1].to_broadcast([32, 64])), [ld2], [stl])
    psl = cx.psf.get()
    pg.op("pe", lambda e: e.transpose(out=psl.ap[:, :32], in_=stl.ap, identity=cx.identf.ap[:32, :32]), reads=[stl, cx.identf], writes=[psl])
    dve(lambda e: e.tensor_copy(out=dt_.ap, in_=psl.ap[:, :32]), [psl], [dt_])
    pg.op("act", lambda e: e.activation(out=dt_.ap, in_=dt_.ap, func=AF.Exp), reads=[dt_], writes=[dt_])
    dve(lambda e: e.tensor_tensor(out=ar.ap, in0=lre.ap, in1=dt_.ap, op=ALU.mult), [lre, dt_], [ar])
    dve(lambda e: e.tensor_tensor(out=ai.ap, in0=lim.ap, in1=dt_.ap, op=ALU.mult), [lim, dt_], [ai])
    pg.op("act", lambda e: e.activation(out=m_.ap, in_=ar.ap, func=AF.Exp, scale=1.0 / 16), reads=[ar], writes=[m_])
    pg.op("act", lambda e: e.activation(out=sn.ap, in_=ai.ap, func=AF.Sin, scale=1.0 / 16), reads=[ai], writes=[sn])
    pg.op("act", lambda e: e.activation(out=cs.ap, in_=ai.ap, func=AF.Sin, scale=1.0 / 16, bias=cx.halfpi.ap[:, 0:1]), reads=[ai, cx.halfpi], writes=[cs])
    dve(lambda e: e.tensor_tensor(out=Are.ap, in0=m_.ap, in1=cs.ap, op=ALU.mult), [m_, cs], [Are])
    dve(lambda e: e.tensor_tensor(out=Aim.ap, in0=m_.ap, in1=sn.ap, op=ALU.mult), [m_, sn], [Aim])

    def csquare(re, im):
        dve(lambda e: e.tensor_tensor(out=t1.ap, in0=re, in1=re, op=ALU.mult), [Are, PW], [t1])
        dve(lambda e: e.tensor_tensor(out=t2.ap, in0=im, in1=im, op=ALU.mult), [Aim, PW], [t2])
        dve(lambda e: e.tensor_tensor(out=t3.ap, in0=re, in1=im, op=ALU.mult), [Are, Aim, PW], [t3])

    PW = sb("S_PW", [128, 32, NS, 3], F32)
    for _ in range(4):
        csquare(Are.ap, Aim.ap)
        dve(lambda e: e.tensor_tensor(out=Are.ap, in0=t1.ap, in1=t2.ap, op=ALU.subtract), [t1, t2], [Are])
        dve(lambda e: e.tensor_scalar(out=Aim.ap, in0=t3.ap, scalar1=2.0, scalar2=None, op0=ALU.mult), [t3], [Aim])
    dve(lambda e: e.tensor_tensor(out=den.ap, in0=lre.ap, in1=lre.ap, op=ALU.mult), [lre], [den])
    dve(lambda e: e.tensor_tensor(out=t1.ap, in0=lim.ap, in1=lim.ap, op=ALU.mult), [lim], [t1])
    dve(lambda e: e.tensor_tensor(out=den.ap, in0=den.ap, in1=t1.ap, op=ALU.add), [den, t1], [den])
    dve(lambda e: e.reciprocal(out=den.ap, in_=den.ap), [den], [den])
    dve(lambda e: e.tensor_scalar(out=t3.ap, in0=Are.ap, scalar1=-1.0, scalar2=None, op0=ALU.add), [Are], [t3])
    dve(lambda e: e.tensor_tensor(out=t1.ap, in0=t3.ap, in1=lre.ap, op=ALU.mult), [t3, lre], [t1])
    dve(lambda e: e.tensor_tensor(out=t2.ap, in0=Aim.ap, in1=lim.ap, op=ALU.mult), [Aim, lim], [t2])
    dve(lambda e: e.tensor_tensor(out=t1.ap, in0=t1.ap, in1=t2.ap, op=ALU.add), [t1, t2], [t1])
    dve(lambda e: e.tensor_tensor(out=fre.ap, in0=t1.ap, in1=den.ap, op=ALU.mult), [t1, den], [fre])
    dve(lambda e: e.tensor_tensor(out=t1.ap, in0=Aim.ap, in1=lre.ap, op=ALU.mult), [Aim, lre], [t1])
    dve(lambda e: e.tensor_tensor(out=t2.ap, in0=t3.ap, in1=lim.ap, op=ALU.mult), [t3, lim], [t2])
    dve(lambda e: e.tensor_tensor(out=t1.ap, in0=t1.ap, in1=t2.ap, op=ALU.subtract), [t1, t2], [t1])
    dve(lambda e: e.tensor_tensor(out=fim.ap, in0=t1.ap, in1=den.ap, op=ALU.mult), [t1, den], [fim])
    for k in range(NS):
        if k == 0:
            dve(lambda e: e.tensor_copy(out=PW.ap[:, :, 0, 0], in_=Are.ap), [Are], [PW])
            dve(lambda e: e.tensor_copy(out=PW.ap[:, :, 0, 1], in_=Aim.ap), [Aim], [PW])
        else:
            csquare(PW.ap[:, :, k - 1, 0], PW.ap[:, :, k - 1, 1])
            dve(lambda e: e.tensor_tensor(out=PW.ap[:, :, k, 0], in0=t1.ap, in1=t2.ap, op=ALU.subtract), [t1, t2], [PW])
            dve(lambda e: e.tensor_scalar(out=PW.ap[:, :, k, 1], in0=t3.ap, scalar1=2.0, scalar2=None, op0=ALU.mult), [t3], [PW])
        dve(lambda e: e.tensor_scalar(out=PW.ap[:, :, k, 2], in0=PW.ap[:, :, k, 1], scalar1=-1.0, scalar2=None, op0=ALU.mult), [PW], [PW])
    CL = sb("S_CL", [128, 32, 2, 32], F32)
    Wb = sb("S_Wb", [128, 32, 2, 32], F32)
    PWs = sb("S_PWs", [128, 32, 9, 2], F32)
    dve(lambda e: e.memset(PWs.ap[:, :, 0, 0], 1.0), [], [PWs])
    dve(lambda e: e.memset(PWs.ap[:, :, 0, 1], 0.0), [], [PWs])
    for k in range(1, 9):
        pr, pi_ = PWs.ap[:, :, k - 1, 0], PWs.ap[:, :, k - 1, 1]
        dve(lambda e: e.tensor_tensor(out=t1.ap, in0=pr, in1=PW.ap[:, :, 0, 0], op=ALU.mult), [PWs, PW], [t1])
        dve(lambda e: e.tensor_tensor(out=t2.ap, in0=pi_, in1=PW.ap[:, :, 0, 1], op=ALU.mult), [PWs, PW], [t2])
        dve(lambda e: e.tensor_tensor(out=PWs.ap[:, :, k, 0], in0=t1.ap, in1=t2.ap, op=ALU.subtract), [t1, t2], [PWs])
        dve(lambda e: e.tensor_tensor(out=t1.ap, in0=pr, in1=PW.ap[:, :, 0, 1], op=ALU.mult), [PWs, PW], [t1])
        dve(lambda e: e.tensor_tensor(out=t2.ap, in0=pi_, in1=PW.ap[:, :, 0, 0], op=ALU.mult), [PWs, PW], [t2])
        dve(lambda e: e.tensor_tensor(out=PWs.ap[:, :, k, 1], in0=t1.ap, in1=t2.ap, op=ALU.add), [t1, t2], [PWs])
    with ExitStack() as es1:
        sb1 = lambda name, shape, dt: pg.buf(es1.enter_context(nc.sbuf_tensor(pg.uname(name), shape, dt)).ap(), name)
        Bt = [sb1("S_Bt%d" % i, [128, 32, 16], F32) for i in range(2)]
        Bb = [sb1("S_Bb%d" % i, [128, 32, 16], F32) for i in range(2)]
        tmpb = sb1("S_tmpb", [128, 32, 16], F32)
        for i, nm in enumerate(("s5_b_re", "s5_b_im")):
            base = w[nm][l]
            pg.dma(Bt[i].ap, bass.AP(base.tensor, base.offset, [[16, 128], [2048, 32], [1, 16]]), writes=[Bt[i]])
        fb = lambda t: t.ap.unsqueeze(2).to_broadcast([128, 32, 16])
        dve(lambda e: e.tensor_tensor(out=Bb[0].ap, in0=Bt[0].ap, in1=fb(fre), op=ALU.mult), [Bt[0], fre], [Bb[0]])
        dve(lambda e: e.tensor_tensor(out=tmpb.ap, in0=Bt[1].ap, in1=fb(fim), op=ALU.mult), [Bt[1], fim], [tmpb])
        dve(lambda e: e.tensor_tensor(out=Bb[0].ap, in0=Bb[0].ap, in1=tmpb.ap, op=ALU.subtract), [Bb[0], tmpb], [Bb[0]])
        dve(lambda e: e.tensor_tensor(out=Bb[1].ap, in0=Bt[1].ap, in1=fb(fre), op=ALU.mult), [Bt[1], fre], [Bb[1]])
        dve(lambda e: e.tensor_tensor(out=tmpb.ap, in0=Bt[0].ap, in1=fb(fim), op=ALU.mult), [Bt[0], fim], [tmpb])
        dve(lambda e: e.tensor_tensor(out=Bb[1].ap, in0=Bb[1].ap, in1=tmpb.ap, op=ALU.add), [Bb[1], tmpb], [Bb[1]])
        pg.op("pool", lambda e: e.memset(Wb.ap, 0.0), writes=[Wb])
        for c in range(2):
            dve(lambda e: e.tensor_copy(out=Wb.ap[0:64, :, c, 0:16], in_=Bb[c].ap[0:64]), [Bb[c]], [Wb])
            dve(lambda e: e.tensor_copy(out=Wb.ap[64:128, :, c, 16:32], in_=Bb[c].ap[64:128]), [Bb[c]], [Wb])
        St0 = sb1("S_St0", [128, 64], F32)
        St = sb1("S_St", [128, 128], F32)
        for d in range(2):
            for c, nm in enumerate(("s5_c_re", "s5_c_im")):
                for blk in range(4):
                    pg.dma(St0.ap, w[nm][l, d, 8 * blk:8 * blk + 8].rearrange("g i p -> (g i) p"), writes=[St0])
                    sgn = 1.0 if c == 0 else -1.0
                    for hh in range(2):
                        dve(lambda e: e.tensor_scalar(out=St.ap[:, 64 * hh:64 * hh + 64], in0=St0.ap, scalar1=cx.pm.ap[:, hh:hh + 1], scalar2=sgn, op0=ALU.mult, op1=ALU.mult),
                            [St0, cx.pm], [St])
                    ps = cx.psf.get()
                    pg.op("pe", lambda e: e.transpose(out=ps.ap[:, 0:128], in_=St.ap, identity=cx.identf.ap), reads=[St, cx.identf], writes=[ps])
                    dg0 = d * 16 + blk * 4
                    pg.op("act", lambda e: e.copy(CL.ap[:, dg0:dg0 + 4, c, :], ps.ap[:, 0:128].rearrange("p (q m) -> p q m", q=4)), reads=[ps], writes=[CL])
    dsk = sb("S_dsk", [32, 16], F32)
    load_T(pg, cx, dsk, dsk.ap, w["s5_d"][l].rearrange("(g q) -> g q", q=32), 16, wd=32)
    bgl = sb("S_bgl", [128, 4], F32)
    load_T(pg, cx, bgl, bgl.ap, w["s5_b_glu"][l].rearrange("(c p) -> c p", p=128), 4)
    es2 = ExitStack()
    sb2 = lambda name, shape, dt: pg.buf(es2.enter_context(nc.sbuf_tensor(pg.uname(name), shape, dt)).ap(), name)
    NCH = L // 8
    NSC = int(np.ceil(np.log2(NCH)))
    HW = min(512, NCH)
    NH = NCH // HW
    ub = sb2("S_ub", [32, L], BF16)
    UW = min(2048, L)
    ust = [sb2("S_ust%d" % i, [32, UW], F32) for i in range(2)]
    Yc = sb2("S_Yc", [32, L], F32)
    XS_ = [[sb2("S_X%d_%d" % (d, i), [128, NCH + 2], F32) for i in range(3)] for d in range(2)]
    Xb = [[sb2("S_Xb%d_%d" % (d, c), [128, NCH + 2], BF16) for c in range(2)] for d in range(2)]
    Wt = sb2("S_Wt", [128, 2, 8, 32], F32)
    tmpw = sb2("S_tmpw", [128, 8, 32], F32)
    WsT = [sb2("S_WsT%d" % d, [32, 2, 8, 128], BF16) for d in range(2)]
    CI = [sb2("S_CI%d" % d, [128, 2, 8, 32], BF16) for d in range(2)]
    CIf = sb2("S_CIf", [128, 2, 8, 32], F32)
    Kd = [sb2("S_Kd%d" % d, [32, 8, 32], BF16) for d in range(2)]
    ua_t = sb2("S_ua", [32, UW], F32)
    x2_t = sb2("S_x2", [32, UW], F32)
    zo_t = sb2("S_zo", [32, UW], BF16)

    def strided(ap2, start, n, step):
        b0 = ap2[:, start:start + 1]
        return bass.AP(b0.tensor, b0.offset, [list(ap2.ap[0]), [step * ap2.ap[1][0], n]])

    ev = [0]
    bcW = lambda a: a.unsqueeze(1).to_broadcast([128, 8, 32])
    bcP = lambda a: a.unsqueeze(2).to_broadcast([128, 8, 32])
    for gh in range(16):
        urow = PF_S5_U + 32 * gh
        for i, t0 in enumerate(range(0, L, UW)):
            st_ = ust[i % 2]
            pg.dma(st_.ap, cx.PF[urow:urow + 32, t0:t0 + UW], writes=[st_])
            pg.op("pool", lambda e: e.tensor_copy(out=ub.ap[:, t0:t0 + UW], in_=st_.ap), reads=[st_], writes=[ub])
        for d in range(2):
            dg = d * 16 + gh
            wbr, wbi = Wb.ap[:, dg, 0, :], Wb.ap[:, dg, 1, :]
            pre, pim = PWs.ap[:, dg, 0:8, 0], PWs.ap[:, dg, 0:8, 1]
            dve(lambda e: e.tensor_tensor(out=Wt.ap[:, 0], in0=bcW(wbr), in1=bcP(pre), op=ALU.mult), [Wb, PWs], [Wt])
            dve(lambda e: e.tensor_tensor(out=tmpw.ap, in0=bcW(wbi), in1=bcP(pim), op=ALU.mult), [Wb, PWs], [tmpw])
            dve(lambda e: e.tensor_tensor(out=Wt.ap[:, 0], in0=Wt.ap[:, 0], in1=tmpw.ap, op=ALU.subtract), [Wt, tmpw], [Wt])
            dve(lambda e: e.tensor_tensor(out=Wt.ap[:, 1], in0=bcW(wbr), in1=bcP(pim), op=ALU.mult), [Wb, PWs], [Wt])
            dve(lambda e: e.tensor_tensor(out=tmpw.ap, in0=bcW(wbi), in1=bcP(pre), op=ALU.mult), [Wb, PWs], [tmpw])
            dve(lambda e: e.tensor_tensor(out=Wt.ap[:, 1], in0=Wt.ap[:, 1], in1=tmpw.ap, op=ALU.add), [Wt, tmpw], [Wt])
            for c in range(2):
                for t4 in range(0, 8, 4):
                    ps = cx.psf.get()
                    for tq in range(4):
                        pg.op("pe", lambda e: e.transpose(out=ps.ap[:32, tq * 128:(tq + 1) * 128], in_=Wt.ap[:, c, t4 + tq, :], identity=cx.identf.ap), reads=[Wt, cx.identf], writes=[ps])
                    pg.op("act", lambda e: e.copy(WsT[d].ap[:, c, t4:t4 + 4, :], ps.ap[:32, :].rearrange("p (q m) -> p q m", q=4)), reads=[ps], writes=[WsT[d]])
            cl0, cl1 = CL.ap[:, dg, 0, :], CL.ap[:, dg, 1, :]
            pre1, pim1 = PWs.ap[:, dg, 1:9, 0], PWs.ap[:, dg, 1:9, 1]
            dve(lambda e: e.tensor_tensor(out=CIf.ap[:, 0], in0=bcW(cl0), in1=bcP(pre1), op=ALU.mult), [CL, PWs], [CIf])
            dve(lambda e: e.tensor_tensor(out=tmpw.ap, in0=bcW(cl1), in1=bcP(pim1), op=ALU.mult), [CL, PWs], [tmpw])
            dve(lambda e: e.tensor_tensor(out=CI[d].ap[:, 0], in0=CIf.ap[:, 0], in1=tmpw.ap, op=ALU.add), [CIf, tmpw], [CI[d]])
            dve(lambda e: e.tensor_tensor(out=CIf.ap[:, 1], in0=bcW(cl1), in1=bcP(pre1), op=ALU.mult), [CL, PWs], [CIf])
            dve(lambda e: e.tensor_tensor(out=tmpw.ap, in0=bcW(cl0), in1=bcP(pim1), op=ALU.mult), [CL, PWs], [tmpw])
            dve(lambda e: e.tensor_tensor(out=CI[d].ap[:, 1], in0=CIf.ap[:, 1], in1=tmpw.ap, op=ALU.subtract), [CIf, tmpw], [CI[d]])
            ps = cx.psf.get()
            for tau in range(8):
                po = ps.ap[0:32, tau * 32:(tau + 1) * 32]
                pg.op("pe", lambda e: e.matmul(po, lhsT=Wt.ap[:, 0, tau, :], rhs=cl0, start=True, stop=False), reads=[Wt, CL], writes=[ps])
                pg.op("pe", lambda e: e.matmul(po, lhsT=Wt.ap[:, 1, tau, :], rhs=cl1, start=False, stop=True), reads=[Wt, CL], writes=[ps])
            pg.op("act", lambda e: e.copy(Kd[d].ap, ps.ap[0:32, 0:256].rearrange("p (t m) -> p t m", t=8)), reads=[ps], writes=[Kd[d]])
            re, im, T = XS_[d]
            for b_ in (re, im, T):
                pg.op("pool", lambda e: e.memset(b_.ap, 0.0), writes=[b_])
            for c, dstb in ((0, re), (1, im)):
                for hf in range(NH):
                    ps = cx.psf.get()
                    for s_ in range(8):
                        tau = 7 - s_ if d == 0 else s_
                        pg.op("pe", lambda e: e.matmul(ps.ap[:, :HW], lhsT=WsT[d].ap[:, c, tau, :], rhs=strided(ub.ap, hf * HW * 8 + s_, HW, 8), start=(s_ == 0), stop=(s_ == 7)),
                              reads=[WsT[d], ub], writes=[ps])
                    ev[0] += 1
                    if ev[0] % 2 == 0:
                        pg.op("act", lambda e: e.copy(dstb.ap[:, 1 + hf * HW:1 + (hf + 1) * HW], ps.ap[:, :HW]), reads=[ps], writes=[dstb])
                    else:
                        pg.op("dve", lambda e: e.tensor_copy(out=dstb.ap[:, 1 + hf * HW:1 + (hf + 1) * HW], in_=ps.ap[:, :HW]), reads=[ps], writes=[dstb])
            for k in range(NSC):
                sft = 1 << k
                if sft >= NCH:
                    break
                kk = k + 3
                cre, cim, ncim = PW.ap[:, dg, kk, 0:1], PW.ap[:, dg, kk, 1:2], PW.ap[:, dg, kk, 2:3]
                if d == 0:
                    dst, src, keep = slice(1 + sft, 1 + NCH), slice(1, 1 + NCH - sft), slice(1, 1 + sft)
                else:
                    dst, src, keep = slice(1, 1 + NCH - sft), slice(1 + sft, 1 + NCH), slice(1 + NCH - sft, 1 + NCH)
                dve(lambda e: e.scalar_tensor_tensor(out=T.ap[:, dst], in0=re.ap[:, src], scalar=cre, in1=re.ap[:, dst], op0=ALU.mult, op1=ALU.add), [re, PW], [T])
                dve(lambda e: e.scalar_tensor_tensor(out=T.ap[:, dst], in0=im.ap[:, src], scalar=ncim, in1=T.ap[:, dst], op0=ALU.mult, op1=ALU.add), [im, T, PW], [T])
                pg.op("pool", lambda e: e.tensor_copy(out=T.ap[:, keep], in_=re.ap[:, keep]), reads=[re], writes=[T])
                if d == 0:
                    rv = lambda ap, sl: bass.AP(ap.tensor, ap[:, sl].offset + (sl.stop - sl.start) - 1, [list(ap.ap[0]), [-1, sl.stop - sl.start]])
                    dve(lambda e: e.scalar_tensor_tensor(out=rv(im.ap, dst), in0=rv(im.ap, src), scalar=cre, in1=rv(im.ap, dst), op0=ALU.mult, op1=ALU.add), [im, PW], [im])
                else:
                    dve(lambda e: e.scalar_tensor_tensor(out=im.ap[:, dst], in0=im.ap[:, src], scalar=cre, in1=im.ap[:, dst], op0=ALU.mult, op1=ALU.add), [im, PW], [im])
                dve(lambda e: e.scalar_tensor_tensor(out=im.ap[:, dst], in0=re.ap[:, src], scalar=cim, in1=im.ap[:, dst], op0=ALU.mult, op1=ALU.add), [re, im, PW], [im])
                re, T = T, re
            pg.op("pool", lambda e: e.tensor_copy(out=Xb[d][0].ap, in_=re.ap), reads=[re], writes=[Xb[d][0]])
            pg.op("pool", lambda e: e.tensor_copy(out=Xb[d][1].ap, in_=im.ap), reads=[im], writes=[Xb[d][1]])
        for hf in range(NH):
            for sp in range(8):
                ps = cx.psf.get()
                po = ps.ap[0:32, :HW]
                mm = []
                for c in range(2):
                    mm.append((CI[0].ap[:, c, sp, :], Xb[0][c].ap[:, hf * HW:hf * HW + HW], [CI[0], Xb[0][c]]))
                    mm.append((CI[1].ap[:, c, 7 - sp, :], Xb[1][c].ap[:, hf * HW + 2:hf * HW + 2 + HW], [CI[1], Xb[1][c]]))
                for s_ in range(0, sp + 1):
                    mm.append((Kd[0].ap[:, sp - s_, :], strided(ub.ap, hf * HW * 8 + s_, HW, 8), [Kd[0], ub]))
                for s_ in range(sp, 8):
                    mm.append((Kd[1].ap[:, s_ - sp, :], strided(ub.ap, hf * HW * 8 + s_, HW, 8), [Kd[1], ub]))
                for i, (lh, rh, rd) in enumerate(mm):
                    pg.op("pe", lambda e: e.matmul(po, lhsT=lh, rhs=rh, start=(i == 0), stop=(i == len(mm) - 1)), reads=rd, writes=[ps])
                pg.op("act", lambda e: e.copy(strided(Yc.ap, hf * HW * 8 + sp, HW, 8), po), reads=[ps], writes=[Yc])
        for t0 in range(0, L, UW):
            tsl = slice(t0, t0 + UW)
            ua, x2, zo = ua_t.ap, x2_t.ap, zo_t.ap
            pg.dma(ua, cx.PF[urow:urow + 32, tsl], writes=[ua_t])
            yv = Yc.ap[:, tsl]
            dve(lambda e: e.scalar_tensor_tensor(out=yv, in0=ua, scalar=dsk.ap[:, gh:gh + 1], in1=yv, op0=ALU.mult, op1=ALU.add), [ua_t, dsk, Yc], [Yc])
            pg.op("pool", lambda e: e.tensor_tensor(out=x2, in0=yv, in1=yv, op=ALU.mult), reads=[Yc], writes=[x2_t])
            dve(lambda e: e.tensor_scalar(out=x2, in0=x2, scalar1=0.044715, scalar2=1.0, op0=ALU.mult, op1=ALU.add), [x2_t], [x2_t])
            pg.op("pool", lambda e: e.tensor_tensor(out=x2, in0=x2, in1=yv, op=ALU.mult), reads=[Yc, x2_t], writes=[x2_t])
            pg.op("act", lambda e: e.activation(out=x2, in_=x2, func=AF.Sigmoid, scale=1.5957691216), reads=[x2_t], writes=[x2_t])
            dve(lambda e: e.tensor_tensor(out=zo, in0=x2, in1=yv, op=ALU.mult), [x2_t, Yc], [zo_t])
            pg.dma(cx.ZT[urow - PF_S5_U:urow - PF_S5_U + 32, tsl], zo, reads=[zo_t])
    pg.barrier()
    es2.close()
    wg = sb("S_wg", [128, 4, 512], BF16)
    wst = sb("S_wst", [128, 2048], F32)
    pg.dma(wst.ap[:, 0:2048].rearrange("p (k c) -> p k c", k=4), w["s5_w_glu"][l].rearrange("(k p) c -> p k c", p=128), writes=[wst])
    dve(lambda e: e.tensor_copy(out=wg.ap, in_=wst.ap[:, 0:2048].rearrange("p (k c) -> p k c", k=4)), [wst], [wg])
    zt = [sb("S_zt%d" % i, [128, 4, 512], BF16) for i in range(2)]
    gt = [sb("S_gt%d" % i, [128, 4, 512], F32) for i in range(2)]
    sg = sb("S_sg", [128, 512], F32)
    yo = [sb("S_yo%d" % i, [128, 4, 512], BF16) for i in range(2)]
    ZTv = cx.ZT.rearrange("(k p) t -> p k t", p=128)
    NT = L // 512
    for it in range(NT):
        tsl = slice(it * 512, (it + 1) * 512)
        z_ = zt[it % 2]; g_ = gt[it % 2]; y_ = yo[it % 2]
        pg.dma(z_.ap, ZTv[:, :, tsl], writes=[z_])
        pg.dma(g_.ap, cx.PF[PF_S5_G:PF_S5_G + 512, tsl].rearrange("(k p) t -> p k t", p=128), writes=[g_])
        pg.op("act", lambda e: e.activation(out=g_.ap, in_=g_.ap, func=AF.Silu), reads=[g_], writes=[g_])
        pg.op("pool", lambda e: e.tensor_tensor(out=g_.ap, in0=g_.ap, in1=z_.ap, op=ALU.mult), reads=[g_, z_], writes=[g_])
        for oc in range(4):
            ps = cx.psf.get()
            for k in range(4):
                pg.op("pe", lambda e: e.matmul(ps.ap, lhsT=wg.ap[:, k, oc * 128:(oc + 1) * 128], rhs=z_.ap[:, k, :], start=(k == 0), stop=(k == 3)), reads=[wg, z_], writes=[ps])
            pg.op("act", lambda e: e.activation(out=sg.ap, in_=ps.ap, func=AF.Sigmoid, bias=bgl.ap[:, oc:oc + 1]), reads=[ps, bgl], writes=[sg])
            dve(lambda e: e.tensor_tensor(out=y_.ap[:, oc, :], in0=sg.ap, in1=g_.ap[:, oc, :], op=ALU.mult), [sg, g_], [y_])
        pg.dma(cx.BT[1536:2048, tsl].rearrange("(k p) t -> p k t", p=128), y_.ap, reads=[y_])


W_NAMES = ["norm_g", "w_in", "lru_conv_w", "lru_conv_b", "lru_w_a", "lru_b_a", "lru_w_x", "lru_b_x", "lru_lambda",
           "gla_w_up", "gla_b_up", "gla_norm_g", "dn_conv_w", "dn_a_log", "dn_dt_bias", "dn_norm_g",
           "s5_lambda_re", "s5_lambda_im", "s5_log_dt", "s5_b_re", "s5_b_im", "s5_c_re", "s5_c_im", "s5_d",
           "s5_w_glu", "s5_b_glu", "w_branch", "w_merge_gate", "b_merge_gate", "w_out", "final_norm_g"]


def host_consts():
    c = {}
    c["identb"] = np.eye(128, dtype=np.float32).astype(ml_dtypes.bfloat16)
    c["identf"] = np.eye(128, dtype=np.float32)
    idx = np.arange(128)
    same = (idx[:, None] // 64) == (idx[None, :] // 64)
    le = idx[:, None] <= idx[None, :]
    lt = idx[:, None] < idx[None, :]
    c["m_incl"] = np.stack([(same & le), (same & le.T)]).astype(np.float32)
    c["m_strict_after"] = np.stack([(same & lt.T), (same & lt)]).astype(np.float32)
    c["m_dn"] = np.stack([c["m_strict_after"], c["m_incl"]], axis=1)
    c["chunkind"] = np.stack([np.repeat((idx // 64 == cc)[:, None], 128, axis=1) for cc in range(2)]).astype(np.float32)
    c["onesf"] = np.ones((128, 128), np.float32)
    ev = ((idx // 16) % 2 == 0).astype(np.float32)
    c["pm"] = np.stack([ev, 1.0 - ev], axis=1).astype(np.float32)
    return c


def build(L, shapes, nslot=2, depth=2, debug=False, branches=("lru", "gla", "dn", "s5")):
    from contextlib import ExitStack
    nc = bass.Bass("TRN2", target_bir_lowering=False)
    pg = Prog(nc)
    cx = Ctx()
    cx.nc = nc
    cx.pg = pg
    cx.w = {}
    for nm in W_NAMES:
        cx.w[nm] = nc.dram_tensor(nm, list(shapes[nm]), F32, kind="ExternalInput").ap()
    hc = host_consts()
    cx.cd = {}
    for nm, arr in hc.items():
        cx.cd[nm] = nc.dram_tensor("c_" + nm, list(arr.shape), BF16 if arr.dtype == ml_dtypes.bfloat16 else F32, kind="ExternalInput").ap()
    xs = [nc.dram_tensor("x%d" % s, [L, D], F32, kind="ExternalInput").ap() for s in range(nslot)]
    ys = [nc.dram_tensor("y%d" % s, [L, D], F32, kind="ExternalOutput").ap() for s in range(nslot)]
    sk = "ExternalOutput" if debug else "Internal"
    cx.PF = nc.dram_tensor("PF", [PF_ROWS, L], F32, kind=sk).ap()
    cx.PT = nc.dram_tensor("PT", [L, PT_COLS], F32, kind=sk).ap()
    cx.XNT = nc.dram_tensor("XNT", [D, L], BF16, kind=sk).ap()
    cx.BT = nc.dram_tensor("BT", [2048, L], BF16, kind=sk).ap()
    cx.OB = nc.dram_tensor("OB", [L, 1024], F32, kind=sk).ap()
    XS = [nc.dram_tensor("XS%d" % s, [L, D], F32, kind=sk).ap() for s in range(nslot)]
    cx.QKT = nc.dram_tensor("QKT", [1024, L], BF16, kind=sk).ap()
    cx.KVT = nc.dram_tensor("KVT", [L, 1024], BF16, kind=sk).ap()
    cx.ZT = nc.dram_tensor("ZT", [512, L], BF16, kind=sk).ap()
    psf, psb = mk_psum(pg, nc)
    cx.psf = Rot(psf[:5])
    cx.pso = psf[5]
    cx.psb = Rot(psb)
    gsb = lambda name, shape, dt: pg.buf(nc.alloc_sbuf_tensor(name, shape, dt).ap(), name)
    cx.identb = gsb("identb", [128, 128], BF16)
    pg.dma(cx.identb.ap, cx.cd["identb"], writes=[cx.identb])
    cx.identf = gsb("identf", [128, 128], F32)
    pg.dma(cx.identf.ap, cx.cd["identf"], writes=[cx.identf])
    cx.eps = gsb("eps", [128, 1], F32)
    pg.op("dve", lambda e: e.memset(cx.eps.ap, EPS), writes=[cx.eps])
    cx.one = gsb("one", [128, 1], F32)
    pg.op("dve", lambda e: e.memset(cx.one.ap, 1.0), writes=[cx.one])
    cx.ldst = gsb("ldst", [128, 128], F32)
    cx.m_incl = gsb("m_incl", [128, 2, 128], F32)
    pg.dma(cx.m_incl.ap, cx.cd["m_incl"].rearrange("d s t -> s d t"), writes=[cx.m_incl])
    cx.m_sa = gsb("m_sa", [128, 2, 128], F32)
    pg.dma(cx.m_sa.ap, cx.cd["m_strict_after"].rearrange("d s t -> s d t"), writes=[cx.m_sa])
    cx.m_dn = gsb("m_dn", [128, 2, 2, 128], F32)
    pg.dma(cx.m_dn.ap[:, 0], cx.cd["m_dn"][0].rearrange("j s t -> s j t"), writes=[cx.m_dn])
    pg.dma(cx.m_dn.ap[:, 1], cx.cd["m_dn"][1].rearrange("j s t -> s j t"), writes=[cx.m_dn])
    cx.chunkind = gsb("chunkind", [128, 2, 128], F32)
    pg.dma(cx.chunkind.ap, cx.cd["chunkind"].rearrange("c s m -> s c m"), writes=[cx.chunkind])
    cx.onesf = gsb("onesf", [128, 128], F32)
    pg.dma(cx.onesf.ap, cx.cd["onesf"], writes=[cx.onesf])
    cx.pm = gsb("pm", [128, 2], F32)
    pg.dma(cx.pm.ap, cx.cd["pm"], writes=[cx.pm])
    cx.halfpi = gsb("halfpi", [128, 1], F32)
    pg.op("dve", lambda e: e.memset(cx.halfpi.ap, float(np.pi / 2)), writes=[cx.halfpi])
    cx.zb = gsb("zb", [128, 2048], BF16)
    pg.op("pool", lambda e: e.memset(cx.zb.ap, 0.0), writes=[cx.zb])
    bidx = {"lru": 0, "gla": 1, "dn": 2, "s5": 3}
    for l in range(depth):
        for s in range(nslot):
            xin = xs[s] if l == 0 else XS[s]
            last = (l == depth - 1)
            xout = ys[s] if last else XS[s]
            pg.barrier()
            with ExitStack() as es:
                phase_P(pg, cx, es, L, xin, l)
                pg.barrier()
            for bn in ("lru", "gla", "dn", "s5"):
                if bn not in branches:
                    b = bidx[bn]
                    for t0 in range(0, L, 2048):
                        tw = min(2048, L - t0)
                        for c in range(4):
                            pg.dma(cx.BT[b * 512 + c * 128:b * 512 + (c + 1) * 128, t0:t0 + tw], cx.zb.ap[:, :tw], reads=[cx.zb])
            if "lru" in branches:
                with ExitStack() as es:
                    phase_LRU(pg, cx, es, L, l)
                    pg.barrier()
            if "s5" in branches:
                with ExitStack() as es:
                    phase_S5(pg, cx, es, L, l)
                    pg.barrier()
            if "gla" in branches:
                with ExitStack() as es:
                    phase_GLA(pg, cx, es, L, l)
                    pg.barrier()
            if "dn" in branches:
                with ExitStack() as es:
                    phase_DN(pg, cx, es, L, l)
                    pg.barrier()
            pg.barrier()
            with ExitStack() as es:
                phase_M(pg, cx, es, L, xin, xout, l, last)
                pg.barrier()
    pg.barrier()
    return nc, pg, hc


_CACHE = {}


def kernel(**inputs):
    L = inputs["x_prompt"].shape[1]
    shapes = {nm: inputs[nm].shape for nm in W_NAMES}
    nc, pg, hc = build(L, shapes)
    xp = np.ascontiguousarray(inputs["x_prompt"], dtype=np.float32)
    xsm = np.ascontiguousarray(inputs["x_sample"], dtype=np.float32)
    wmap = {nm: np.ascontiguousarray(inputs[nm], dtype=np.float32) for nm in W_NAMES}
    in_maps = []
    for c in range(8):
        m = dict(wmap)
        for nm, arr in hc.items():
            m["c_" + nm] = arr
        m["x0"] = xp[c]
        m["x1"] = xsm[c % 2]
        in_maps.append(m)
    res = run_bass_kernel_spmd(nc, in_maps, core_ids=list(range(8)))
    y_prompt = np.stack([np.asarray(res.results[c]["y0"], dtype=np.float32) for c in range(8)], axis=0)
    y_sample = np.stack([np.asarray(res.results[c]["y1"], dtype=np.float32) for c in range(2)], axis=0)
    return (y_prompt, y_sample)
```

```python
import numpy as np
import ml_dtypes
from contextlib import ExitStack
import concourse.bass as bass
import concourse.mybir as mybir
from concourse.bass_utils import run_bass_kernel_spmd

F32 = mybir.dt.float32
BF16 = mybir.dt.bfloat16
ALU = mybir.AluOpType
AF = mybir.ActivationFunctionType

D = 1024
BW = 512
D_IN = 5680
EPS = 1e-6
O_LRU_X, O_LRU_G = 0, 512
O_GLA_Q, O_GLA_K, O_GLA_V, O_GLA_G, O_GLA_LR = 1024, 1280, 1536, 2048, 2560
O_DN_QKV, O_DN_G, O_DN_BA = 2592, 4128, 4640
O_S5_U, O_S5_G = 4656, 5168


SAME_ENGINE_SYNC = True
STORES_ON_POOL = True


class Buf:
    __slots__ = ("ap", "w", "r", "name")

    def __init__(self, ap, name=""):
        self.ap = ap
        self.w = []
        self.r = []
        self.name = name

    def __getitem__(self, k):
        return self.ap[k]


class Prog:
    def __init__(self, nc, n_dma_sems=40):
        self.nc = nc
        self.eng = {"pe": nc.tensor, "act": nc.scalar, "dve": nc.vector, "pool": nc.gpsimd, "sp": nc.sync}
        self.sem = {k: nc.alloc_semaphore("s_" + k) for k in self.eng}
        self.cnt = {k: 0 for k in self.eng}
        self.seen = {k: {} for k in self.eng}
        self.dsem = [nc.alloc_semaphore("d%d" % i) for i in range(n_dma_sems)]
        self.dcnt = [0] * n_dma_sems
        self.dnext = 0
        self.ninst = 0

    def buf(self, ap, name=""):
        return Buf(ap, name)

    def uname(self, name):
        self.uid = getattr(self, "uid", 0) + 1
        return "%s_%d" % (name, self.uid)

    def _wait(self, e, dep):
        if dep[0] == "dma":
            key = ("dma", dep[1]); val = dep[2]
            if self.seen[e].get(key, 0) >= val:
                return
            self.eng[e].wait_ge(self.dsem[dep[1]], val)
        else:
            f, val = dep
            if f == e and (e in ("pe", "sp") or not SAME_ENGINE_SYNC):
                return
            key = f
            if self.seen[e].get(key, 0) >= val:
                return
            self.eng[e].wait_ge(self.sem[f], val)
        self.seen[e][key] = val
        self.ninst += 1

    def _deps(self, e, reads, writes):
        for b in reads:
            for d in b.w:
                self._wait(e, d)
        for b in writes:
            for d in b.w:
                self._wait(e, d)
            for d in b.r:
                self._wait(e, d)

    def op(self, e, inst_fn, reads=(), writes=()):
        self._deps(e, reads, writes)
        inst = inst_fn(self.eng[e])
        inst.then_inc(self.sem[e], 1)
        self.cnt[e] += 1
        me = (e, self.cnt[e])
        for b in reads:
            b.r.append(me)
            if len(b.r) > 24:
                b.r = b.r[-24:] if False else self._compress(b.r)
        for b in writes:
            b.w = [me]
            b.r = []
        self.ninst += 1
        return inst

    @staticmethod
    def _compress(lst):
        best = {}
        for d in lst:
            k = ("dma", d[1]) if d[0] == "dma" else d[0]
            v = d[2] if d[0] == "dma" else d[1]
            if k not in best or v > best[k][0]:
                best[k] = (v, d)
        return [x[1] for x in best.values()]

    def dma(self, out, in_, reads=(), writes=(), q=None, **kw):
        if q is None:
            q = "sp" if (len(writes) > 0 or not STORES_ON_POOL) else "pool"
        self._deps(q, reads, writes)
        j = self.dnext
        self.dnext = (self.dnext + 1) % len(self.dsem)
        if self.dcnt[j] > 0:
            self._wait(q, ("dma", j, self.dcnt[j]))
        self.dcnt[j] += 16
        self.eng[q].dma_start(out=out, in_=in_, **kw).then_inc(self.dsem[j], 16)
        me = ("dma", j, self.dcnt[j])
        for b in reads:
            b.r.append(me)
            if len(b.r) > 24:
                b.r = self._compress(b.r)
        for b in writes:
            b.w = [me]
            b.r = []
        self.ninst += 1

    def barrier(self):
        for e in self.eng:
            for f in self.eng:
                if f != e and self.cnt[f] > 0:
                    self._wait(e, (f, self.cnt[f]))
            for j, c in enumerate(self.dcnt):
                if c > 0:
                    self._wait(e, ("dma", j, c))


class Ctx:
    pass


def mk_psum(pg, nc):
    banks = []
    for i in range(6):
        banks.append(pg.buf(nc.alloc_psum_tensor("psf%d" % i, [128, 512], F32).ap(), "psf%d" % i))
    bb = []
    for i in range(2):
        bb.append(pg.buf(nc.alloc_psum_tensor("psb%d" % i, [128, 1024], BF16).ap(), "psb%d" % i))
    return banks, bb


class Rot:
    def __init__(self, items):
        self.items = items
        self.i = 0

    def get(self):
        x = self.items[self.i]
        self.i = (self.i + 1) % len(self.items)
        return x


PF_LRU_X, PF_LRU_G, PF_GLA_Q, PF_GLA_K, PF_DN_QKV, PF_S5_U, PF_S5_G, PF_GLA_LR = 0, 512, 1024, 1280, 1536, 3072, 3584, 4096
PF_ROWS = 4128
PF_CHUNKS = ([(O_LRU_X + 128 * i, 128) for i in range(4)] + [(O_LRU_G + 128 * i, 128) for i in range(4)]
             + [(O_GLA_Q + 128 * i, 128) for i in range(2)] + [(O_GLA_K + 128 * i, 128) for i in range(2)]
             + [(O_DN_QKV + 128 * i, 128) for i in range(12)] + [(O_S5_U + 128 * i, 128) for i in range(4)]
             + [(O_S5_G + 128 * i, 128) for i in range(4)] + [(O_GLA_LR, 32)])
PT_GLA_K, PT_GLA_V, PT_GLA_G, PT_DN_G, PT_DN_BA = 0, 256, 768, 1280, 1792
PT_COLS = 1808
PT_GROUPS = [(1280, 512, 0), (1792, 512, 512), (2304, 256, 1024), (4128, 512, 1280), (4640, 16, 1792)]


def load_cast_bf16(pg, nc, es, dst, src_ap, rows, cols, name, chunk=2048):
    st = [pg.buf(es.enter_context(nc.sbuf_tensor(name + "_st%d" % i, [128, chunk], F32)).ap()) for i in range(2)]
    i = 0
    for c0 in range(0, cols, chunk):
        cw = min(chunk, cols - c0)
        s = st[i % 2]
        pg.dma(s.ap[:rows, :cw], src_ap[:, c0:c0 + cw], writes=[s])
        if i % 2 == 0:
            pg.op("act", lambda e: e.copy(dst[0][:rows, c0:c0 + cw], s.ap[:rows, :cw]), reads=[s], writes=[dst[1]])
        else:
            pg.op("dve", lambda e: e.tensor_copy(out=dst[0][:rows, c0:c0 + cw], in_=s.ap[:rows, :cw]), reads=[s], writes=[dst[1]])
        i += 1


def phase_P(pg, cx, es, L, x_ap, l):
    nc = cx.nc
    TT = 512
    sb = lambda name, shape, dt: pg.buf(es.enter_context(nc.sbuf_tensor(pg.uname(name), shape, dt)).ap(), name)
    wbf = sb("P_w", [128, 8, D_IN], BF16)
    w_src = cx.w["w_in"][l].rearrange("(k p) c -> p k c", p=128)
    WC = D_IN // 4
    st = [sb("P_wst%d" % i, [128, WC], F32) for i in range(2)]
    for k in range(8):
        for q in range(4):
            s = st[q % 2]
            pg.dma(s.ap, w_src[:, k, q * WC:(q + 1) * WC], writes=[s])
            if q % 2 == 0:
                pg.op("act", lambda e: e.copy(wbf.ap[:, k, q * WC:(q + 1) * WC], s.ap), reads=[s], writes=[wbf])
            else:
                pg.op("dve", lambda e: e.tensor_copy(out=wbf.ap[:, k, q * WC:(q + 1) * WC], in_=s.ap), reads=[s], writes=[wbf])
    gk = sb("P_g", [128, 8], F32)
    load_T(pg, cx, gk, gk.ap, cx.w["norm_g"][l].rearrange("(k p) -> k p", p=128), 8)
    xt = [sb("P_x%d" % i, [128, 4, D], F32) for i in range(2)]
    xs = sb("P_xs", [128, D], BF16)
    junk = sb("P_junk", [128, D], BF16)
    ss = sb("P_ss", [128, 4], F32)
    xnT = [sb("P_xnT%d" % i, [128, 8, TT], BF16) for i in range(2)]
    stf = [sb("P_stf%d" % i, [128, 4, TT], F32) for i in range(2)]
    stt = [sb("P_stt%d" % i, [128, PT_COLS], F32) for i in range(1)]
    XNTv = cx.XNT.rearrange("(k p) t -> p k t", p=128)
    xv = x_ap.rearrange("(n j p) d -> n p j d", p=128, j=4)
    gb = gk.ap.unsqueeze(2).to_broadcast([128, 8, 128])
    nt = L // TT
    evac_i = 0
    for it in range(nt):
        x_b = xt[it % 2]
        pg.dma(x_b.ap, xv[it], writes=[x_b])
        xn = xnT[it % 2]
        for j in range(4):
            pg.op("act", lambda e: e.activation(out=junk.ap, in_=x_b.ap[:, j, :], func=AF.Square, accum_out=ss.ap[:, j:j + 1]),
                  reads=[x_b], writes=[junk, ss])
            pg.op("act", lambda e: e.activation(out=ss.ap[:, j:j + 1], in_=ss.ap[:, j:j + 1], func=AF.Sqrt, scale=1.0 / D, bias=cx.eps.ap[:, 0:1]),
                  reads=[ss, cx.eps], writes=[ss])
            pg.op("dve", lambda e: e.reciprocal(out=ss.ap[:, j:j + 1], in_=ss.ap[:, j:j + 1]), reads=[ss], writes=[ss])
            pg.op("dve", lambda e: e.tensor_scalar(out=xs.ap, in0=x_b.ap[:, j, :], scalar1=ss.ap[:, j:j + 1], scalar2=None, op0=ALU.mult),
                  reads=[x_b, ss], writes=[xs])
            pb = cx.psb.get()
            for k in range(8):
                pg.op("pe", lambda e: e.transpose(out=pb.ap[:, k * 128:(k + 1) * 128], in_=xs.ap[:, k * 128:(k + 1) * 128], identity=cx.identb.ap),
                      reads=[xs, cx.identb], writes=[pb])
            pg.op("dve", lambda e: e.tensor_tensor(out=xn.ap[:, :, j * 128:(j + 1) * 128], in0=pb.ap.rearrange("p (k t) -> p k t", k=8), in1=gb, op=ALU.mult),
                  reads=[pb, gk], writes=[xn])
        pg.dma(XNTv[:, :, it * TT:(it + 1) * TT], xn.ap, reads=[xn])
        for ci, (c0, cw) in enumerate(PF_CHUNKS):
            ps = cx.psf.get()
            for k in range(8):
                pg.op("pe", lambda e: e.matmul(ps.ap[:cw, :], lhsT=wbf.ap[:, k, c0:c0 + cw], rhs=xn.ap[:, k, :], start=(k == 0), stop=(k == 7)),
                      reads=[wbf, xn], writes=[ps])
            sbuf = stf[(ci // 4) % 2]
            evac_i += 1
            if evac_i % 2 == 0:
                pg.op("act", lambda e: e.copy(sbuf.ap[:cw, ci % 4, :], ps.ap[:cw, :]), reads=[ps], writes=[sbuf])
            else:
                pg.op("dve", lambda e: e.tensor_copy(out=sbuf.ap[:cw, ci % 4, :], in_=ps.ap[:cw, :]), reads=[ps], writes=[sbuf])
            if ci % 4 == 3:
                cb = ci // 4
                pg.dma(PFv_slice(cx, cb * 4, 4, it * TT, TT), sbuf.ap, reads=[sbuf])
            elif ci == len(PF_CHUNKS) - 1:
                pg.dma(cx.PF[4096:4128, it * TT:(it + 1) * TT], sbuf.ap[:32, 0, :], reads=[sbuf])
        for j in range(4):
            sbuf = stt[0]
            for (c0, cw, o0) in PT_GROUPS:
                ps = cx.psf.get()
                for k in range(8):
                    pg.op("pe", lambda e: e.matmul(ps.ap[:, :cw], lhsT=xn.ap[:, k, j * 128:(j + 1) * 128], rhs=wbf.ap[:, k, c0:c0 + cw], start=(k == 0), stop=(k == 7)),
                          reads=[wbf, xn], writes=[ps])
                evac_i += 1
                if evac_i % 2 == 0:
                    pg.op("act", lambda e: e.copy(sbuf.ap[:, o0:o0 + cw], ps.ap[:, :cw]), reads=[ps], writes=[sbuf])
                else:
                    pg.op("dve", lambda e: e.tensor_copy(out=sbuf.ap[:, o0:o0 + cw], in_=ps.ap[:, :cw]), reads=[ps], writes=[sbuf])
            t0 = it * TT + j * 128
            pg.dma(cx.PT[t0:t0 + 128, :], sbuf.ap, reads=[sbuf])


def PFv_slice(cx, c0, nch, t0, tw):
    return cx.PF[c0 * 128:(c0 + nch) * 128, t0:t0 + tw].rearrange("(c p) t -> p c t", p=128)


def load_T(pg, cx, dst, dst_ap, src_ap, n, st_view=None, wd=128, **kw):
    st = cx.ldst
    pg.dma(st.ap[:n, :wd] if st_view is None else st_view(st.ap[:n, :wd]), src_ap, writes=[st], **kw)
    ps = cx.psf.get()
    pg.op("pe", lambda e: e.transpose(out=ps.ap[:wd, :n], in_=st.ap[:n, :wd], identity=cx.identf.ap[:n, :n]), reads=[st, cx.identf], writes=[ps])
    pg.op("dve", lambda e: e.tensor_copy(out=dst_ap, in_=ps.ap[:wd, :n]), reads=[ps], writes=[dst])


def phase_LRU(pg, cx, es, L, l):
    nc = cx.nc
    sb = lambda name, shape, dt: pg.buf(es.enter_context(nc.sbuf_tensor(pg.uname(name), shape, dt)).ap(), name)
    w = cx.w
    TL = min(2048, L)
    ntile = L // TL
    cw = sb("L_cw", [128, 4, 4], F32)
    load_T(pg, cx, cw, cw.ap.rearrange("p j c -> p (j c)"), w["lru_conv_w"][l].rearrange("j (c p) -> (j c) p", p=128), 16)
    cb = sb("L_cb", [128, 4], F32)
    load_T(pg, cx, cb, cb.ap, w["lru_conv_b"][l].rearrange("(c p) -> c p", p=128), 4)
    bias = sb("L_bias", [128, 2, 2, 4], F32)
    load_T(pg, cx, bias, bias.ap[:, 0].rearrange("p d c -> p (d c)"), w["lru_b_a"][l].rearrange("d (c p) -> (d c) p", p=128), 8)
    load_T(pg, cx, bias, bias.ap[:, 1].rearrange("p d c -> p (d c)"), w["lru_b_x"][l].rearrange("d (c p) -> (d c) p", p=128), 8)
    lam = sb("L_lam", [128, 2, 4], F32)
    load_T(pg, cx, lam, lam.ap.rearrange("p d c -> p (d c)"), w["lru_lambda"][l].rearrange("d (c p) -> (d c) p", p=128), 8)
    coef = sb("L_coef", [128, 2, 4], F32)
    coef2 = sb("L_coef2", [128, 2, 4], F32)
    pg.op("act", lambda e: e.activation(out=coef.ap, in_=lam.ap, func=AF.Exp, scale=-1.0), reads=[lam], writes=[coef])
    pg.op("act", lambda e: e.activation(out=coef.ap, in_=coef.ap, func=AF.Ln, bias=cx.one.ap[:, 0:1]), reads=[coef, cx.one], writes=[coef])
    pg.op("dve", lambda e: e.tensor_scalar(out=coef2.ap, in0=coef.ap, scalar1=-16.0, scalar2=None, op0=ALU.mult), reads=[coef], writes=[coef2])
    pg.op("dve", lambda e: e.tensor_scalar(out=coef.ap, in0=coef.ap, scalar1=-8.0, scalar2=None, op0=ALU.mult), reads=[coef], writes=[coef])
    wg = sb("L_wg", [128, 2, 2, 4, 128], BF16)
    wst = sb("L_wst", [128, 2, 4, 128], F32)
    for ai, nm in enumerate(("lru_w_a", "lru_w_x")):
        pg.dma(wst.ap, w[nm][l].rearrange("d h i j -> i d h j"), writes=[wst])
        pg.op("dve", lambda e: e.tensor_copy(out=wg.ap[:, ai], in_=wst.ap), reads=[wst], writes=[wg])
    XC = sb("L_XC", [128, L], F32)
    XCB = sb("L_XCB", [128, L], BF16)
    HF = sb("L_HF", [128, L], F32)
    xin = sb("L_xin", [128, TL + 3], F32)
    rt = sb("L_r", [128, TL], F32)
    itl = sb("L_i", [128, TL], F32)
    at = sb("L_a", [128, TL], F32)
    t2 = sb("L_t2", [128, TL], F32)
    gt = sb("L_g", [128, TL], F32)
    yb = sb("L_y", [128, TL], BF16)
    carry = sb("L_carry", [128, 1], F32)
    for c in range(4):
        prow = PF_LRU_X + c * 128
        for it in range(ntile):
            t0 = it * TL
            lo = max(t0 - 2, 0)
            hi = min(t0 + TL + 1, L)
            if it == 0 or it == ntile - 1:
                pg.op("pool", lambda e: e.memset(xin.ap, 0.0), writes=[xin])
            pg.dma(xin.ap[:, lo - (t0 - 2):hi - (t0 - 2)], cx.PF[prow:prow + 128, lo:hi], writes=[xin])
            xo = XC.ap[:, t0:t0 + TL]
            pg.op("dve", lambda e: e.tensor_scalar(out=xo, in0=xin.ap[:, 0:TL], scalar1=cw.ap[:, 0, c:c + 1], scalar2=cb.ap[:, c:c + 1], op0=ALU.mult, op1=ALU.add),
                  reads=[xin, cw, cb], writes=[XC])
            for j in range(1, 4):
                pg.op("dve", lambda e: e.scalar_tensor_tensor(out=xo, in0=xin.ap[:, j:j + TL], scalar=cw.ap[:, j, c:c + 1], in1=xo, op0=ALU.mult, op1=ALU.add),
                      reads=[xin, cw, XC], writes=[XC])
            pg.op("act", lambda e: e.copy(XCB.ap[:, t0:t0 + TL], xo), reads=[XC], writes=[XCB])
        for d in range(2):
            order = range(ntile) if d == 0 else range(ntile - 1, -1, -1)
            for n_i, it in enumerate(order):
                t0 = it * TL
                for s0 in range(0, TL, 512):
                    for ai, dst in ((0, rt), (1, itl)):
                        ps = cx.psf.get()
                        pg.op("pe", lambda e: e.matmul(ps.ap, lhsT=wg.ap[:, ai, d, c, :], rhs=XCB.ap[:, t0 + s0:t0 + s0 + 512], start=True, stop=True),
                              reads=[wg, XCB], writes=[ps])
                        pg.op("act", lambda e: e.activation(out=dst.ap[:, s0:s0 + 512], in_=ps.ap, func=AF.Sigmoid, bias=bias.ap[:, ai, d, c:c + 1]),
                              reads=[ps, bias], writes=[dst])
                pg.op("act", lambda e: e.activation(out=at.ap, in_=rt.ap, func=AF.Exp, scale=coef.ap[:, d, c:c + 1]), reads=[rt, coef], writes=[at])
                pg.op("act", lambda e: e.activation(out=t2.ap, in_=rt.ap, func=AF.Exp, scale=coef2.ap[:, d, c:c + 1]), reads=[rt, coef2], writes=[t2])
                pg.op("act", lambda e: e.activation(out=t2.ap, in_=t2.ap, func=AF.Sqrt, scale=-1.0, bias=cx.one.ap[:, 0:1]), reads=[t2, cx.one], writes=[t2])
                pg.op("pool", lambda e: e.tensor_tensor(out=itl.ap, in0=itl.ap, in1=XC.ap[:, t0:t0 + TL], op=ALU.mult), reads=[itl, XC], writes=[itl])
                pg.op("dve", lambda e: e.tensor_tensor(out=t2.ap, in0=t2.ap, in1=itl.ap, op=ALU.mult), reads=[t2, itl], writes=[t2])
                init = 0.0 if n_i == 0 else carry.ap[:, 0:1]
                rds = [at, t2] + ([] if n_i == 0 else [carry])
                if d == 0:
                    ho = HF.ap[:, t0:t0 + TL]
                    pg.op("dve", lambda e: e.tensor_tensor_scan(out=ho, data0=at.ap, data1=t2.ap, initial=init, op0=ALU.mult, op1=ALU.add),
                          reads=rds, writes=[HF])
                    pg.op("dve", lambda e: e.tensor_copy(out=carry.ap, in_=HF.ap[:, t0 + TL - 1:t0 + TL]), reads=[HF], writes=[carry])
                else:
                    rv = lambda ap: bass.AP(ap.tensor, ap.offset + TL - 1, [list(ap.ap[0]), [-1, TL]])
                    pg.op("dve", lambda e: e.tensor_tensor_scan(out=rv(rt.ap), data0=rv(at.ap), data1=rv(t2.ap), initial=init, op0=ALU.mult, op1=ALU.add),
                          reads=rds, writes=[rt])
                    pg.op("dve", lambda e: e.tensor_copy(out=carry.ap, in_=rt.ap[:, 0:1]), reads=[rt], writes=[carry])
                    grow = PF_LRU_G + c * 128
                    pg.dma(gt.ap, cx.PF[grow:grow + 128, t0:t0 + TL], writes=[gt])
                    pg.op("act", lambda e: e.activation(out=gt.ap, in_=gt.ap, func=AF.Silu), reads=[gt], writes=[gt])
                    pg.op("pool", lambda e: e.tensor_tensor(out=rt.ap, in0=rt.ap, in1=HF.ap[:, t0:t0 + TL], op=ALU.add), reads=[rt, HF], writes=[rt])
                    pg.op("dve", lambda e: e.tensor_tensor(out=yb.ap, in0=rt.ap, in1=gt.ap, op=ALU.mult), reads=[rt, gt], writes=[yb])
                    pg.dma(cx.BT[c * 128:(c + 1) * 128, t0:t0 + TL], yb.ap, reads=[yb])


def phase_M(pg, cx, es, L, x_ap, xout_ap, l, last):
    nc = cx.nc
    TT = 512
    sb = lambda name, shape, dt: pg.buf(es.enter_context(nc.sbuf_tensor(pg.uname(name), shape, dt)).ap(), name)
    w = cx.w
    wmg = sb("M_wmg", [128, 4, 8, D], BF16)
    wbr = sb("M_wbr", [128, 4, 4, D], BF16)
    wout = sb("M_wout", [128, 8, D], BF16)
    st = [sb("M_st%d" % i, [128, D], F32) for i in range(2)]
    jobs = []
    for n in range(4):
        for k in range(8):
            jobs.append((w["w_merge_gate"][l, n, k * 128:(k + 1) * 128, :], wmg, wmg.ap[:, n, k, :]))
        for k in range(4):
            jobs.append((w["w_branch"][l, n, k * 128:(k + 1) * 128, :], wbr, wbr.ap[:, n, k, :]))
    for k in range(8):
        jobs.append((w["w_out"][l, k * 128:(k + 1) * 128, :], wout, wout.ap[:, k, :]))
    for i, (src, dbuf, dap) in enumerate(jobs):
        s = st[i % 2]
        pg.dma(s.ap, src, writes=[s])
        if i % 2 == 0:
            pg.op("act", lambda e: e.copy(dap, s.ap), reads=[s], writes=[dbuf])
        else:
            pg.op("dve", lambda e: e.tensor_copy(out=dap, in_=s.ap), reads=[s], writes=[dbuf])
    bmg = sb("M_bmg", [128, 4, 8], F32)
    load_T(pg, cx, bmg, bmg.ap.rearrange("p n c -> p (n c)"), w["b_merge_gate"][l].rearrange("n (c p) -> (n c) p", p=128), 32)
    if last:
        fg = sb("M_fg", [128, D], F32)
        fsrc = w["final_norm_g"]
        pg.dma(fg.ap, bass.AP(fsrc.tensor, fsrc.offset, [[0, 128], [1, D]]), writes=[fg])
        ss = sb("M_ss", [128, 4], F32)
        junk = sb("M_junk", [128, D], BF16)
    xn = sb("M_xn", [128, 8, TT], BF16)
    bt = sb("M_bt", [128, 16, TT], BF16)
    xt = sb("M_x", [128, 4, D], F32)
    mg = sb("M_mg", [128, 8, TT], BF16)
    gsb = sb("M_g", [128, TT], F32)
    tmp = sb("M_tmp", [128, TT], F32)
    acc = sb("M_acc", [128, TT], F32)
    XNTv = cx.XNT.rearrange("(k p) t -> p k t", p=128)
    BTv = cx.BT.rearrange("(k p) t -> p k t", p=128)
    xv = x_ap.rearrange("(n j p) d -> n p j d", p=128, j=4)
    ov = xout_ap.rearrange("(n j p) d -> n p j d", p=128, j=4)
    for it in range(L // TT):
        ts = slice(it * TT, (it + 1) * TT)
        pg.dma(xn.ap, XNTv[:, :, ts], writes=[xn])
        pg.dma(bt.ap, BTv[:, :, ts], writes=[bt])
        pg.dma(xt.ap, xv[it], writes=[xt])
        for oc in range(8):
            ocs = slice(oc * 128, (oc + 1) * 128)
            for n in range(4):
                pg_ = cx.psf.get()
                for k in range(8):
                    pg.op("pe", lambda e: e.matmul(pg_.ap, lhsT=wmg.ap[:, n, k, ocs], rhs=xn.ap[:, k, :], start=(k == 0), stop=(k == 7)),
                          reads=[wmg, xn], writes=[pg_])
                pg.op("act", lambda e: e.activation(out=gsb.ap, in_=pg_.ap, func=AF.Sigmoid, bias=bmg.ap[:, n, oc:oc + 1]), reads=[pg_, bmg], writes=[gsb])
                pb = cx.psf.get()
                for k in range(4):
                    pg.op("pe", lambda e: e.matmul(pb.ap, lhsT=wbr.ap[:, n, k, ocs], rhs=bt.ap[:, n * 4 + k, :], start=(k == 0), stop=(k == 3)),
                          reads=[wbr, bt], writes=[pb])
                if n == 0:
                    pg.op("dve", lambda e: e.tensor_tensor(out=acc.ap, in0=pb.ap, in1=gsb.ap, op=ALU.mult), reads=[pb, gsb], writes=[acc])
                else:
                    pg.op("dve", lambda e: e.tensor_tensor(out=tmp.ap, in0=pb.ap, in1=gsb.ap, op=ALU.mult), reads=[pb, gsb], writes=[tmp])
                    if n < 3:
                        pg.op("pool", lambda e: e.tensor_tensor(out=acc.ap, in0=acc.ap, in1=tmp.ap, op=ALU.add), reads=[acc, tmp], writes=[acc])
                    else:
                        pg.op("pool", lambda e: e.tensor_tensor(out=mg.ap[:, oc, :], in0=acc.ap, in1=tmp.ap, op=ALU.add), reads=[acc, tmp], writes=[mg])
        for j in range(4):
            for hf in range(2):
                hs = slice(hf * 512, (hf + 1) * 512)
                ps = cx.psf.get()
                for k in range(8):
                    pg.op("pe", lambda e: e.matmul(ps.ap, lhsT=mg.ap[:, k, j * 128:(j + 1) * 128], rhs=wout.ap[:, k, hs], start=(k == 0), stop=(k == 7)),
                          reads=[mg, wout], writes=[ps])
                pg.op("dve", lambda e: e.tensor_tensor(out=xt.ap[:, j, hs], in0=ps.ap, in1=xt.ap[:, j, hs], op=ALU.add), reads=[ps, xt], writes=[xt])
            if last:
                pg.op("act", lambda e: e.activation(out=junk.ap, in_=xt.ap[:, j, :], func=AF.Square, accum_out=ss.ap[:, j:j + 1]), reads=[xt], writes=[junk, ss])
                pg.op("act", lambda e: e.activation(out=ss.ap[:, j:j + 1], in_=ss.ap[:, j:j + 1], func=AF.Sqrt, scale=1.0 / D, bias=cx.eps.ap[:, 0:1]),
                      reads=[ss, cx.eps], writes=[ss])
                pg.op("dve", lambda e: e.reciprocal(out=ss.ap[:, j:j + 1], in_=ss.ap[:, j:j + 1]), reads=[ss], writes=[ss])
                pg.op("dve", lambda e: e.scalar_tensor_tensor(out=xt.ap[:, j, :], in0=xt.ap[:, j, :], scalar=ss.ap[:, j:j + 1], in1=fg.ap, op0=ALU.mult, op1=ALU.mult),
                      reads=[xt, ss, fg], writes=[xt])
        pg.dma(ov[it], xt.ap, reads=[xt])


def phase_GLA(pg, cx, es, L, l):
    nc = cx.nc
    sb = lambda name, shape, dt: pg.buf(es.enter_context(nc.sbuf_tensor(pg.uname(name), shape, dt)).ap(), name)
    w = cx.w
    NB = L // 128
    wup = sb("G_wup", [32, 2, 256], F32)
    for d in range(2):
        pg.dma(wup.ap[0:16, d, :], w["gla_w_up"][l, d], writes=[wup])
        pg.dma(wup.ap[16:17, d, :], w["gla_b_up"][l, d:d + 1, :], writes=[wup])
    gn = sb("G_gn", [128, 128], F32)
    gsrc = w["gla_norm_g"][l]
    pg.dma(gn.ap, bass.AP(gsrc.tensor, gsrc.offset, [[0, 128], [1, 128]]), writes=[gn])
    lrT = [sb("G_lrT%d" % i, [32, 128], F32) for i in range(2)]
    for b in lrT:
        pg.op("dve", lambda e: e.memset(b.ap, 1.0), writes=[b])
    qk = [sb("G_qk%d" % i, [128, 4, 128], F32) for i in range(2)]
    tk = [sb("G_tk%d" % i, [128, 1280], F32) for i in range(2)]
    obt = [sb("G_ob%d" % i, [128, 512], F32) for i in range(2)]
    la = sb("G_la", [128, 256], F32)
    e1 = sb("G_e1", [128, 256], F32)
    eb = sb("G_eb", [128, 2, 128], F32)
    enb = sb("G_enb", [128, 2, 128], F32)
    qd = sb("G_qd", [128, 2, 128], BF16)
    ki = sb("G_ki", [128, 2, 128], BF16)
    ed = sb("G_ed", [128, 256], F32)
    kend = sb("G_kend", [128, 256], BF16)
    vb = sb("G_vb", [128, 512], BF16)
    sm = [sb("G_sm%d" % i, [128, 128], BF16) for i in range(4)]
    pre_ps = Rot([cx.psf.items[4], cx.pso])
    S32 = [sb("G_S32_%d" % h, [128, 128], F32) for h in range(4)]
    Sb = [sb("G_Sb_%d" % h, [128, 128], BF16) for h in range(4)]
    osb = sb("G_osb", [128, 512], F32)
    ssq = sb("G_ssq", [128, 4], F32)
    junk = sb("G_junk", [128, 128], BF16)
    ysb = sb("G_ysb", [128, 512], BF16)
    yT = sb("G_yT", [128, 4, 128], BF16)
    PFq = cx.PF[PF_GLA_Q:PF_GLA_Q + 512, :].rearrange("(c p) t -> p c t", p=128)
    for d in (1, 0):
        pg.barrier()
        for h in range(4):
            pg.op("dve", lambda e: e.memset(S32[h].ap, 0.0), writes=[S32[h]])
            pg.op("pool", lambda e: e.memset(Sb[h].ap, 0.0), writes=[Sb[h]])
        order = range(NB) if d == 0 else range(NB - 1, -1, -1)
        for bi, blk in enumerate(order):
            t0 = blk * 128
            ts = slice(t0, t0 + 128)
            qkb = qk[bi % 2]; tkb = tk[bi % 2]; lrb = lrT[bi % 2]; ob = obt[bi % 2]
            pg.dma(qkb.ap, PFq[:, :, ts], writes=[qkb])
            pg.dma(tkb.ap, cx.PT[ts, 0:1280], writes=[tkb])
            pg.dma(lrb.ap[0:16, :], cx.PF[PF_GLA_LR + 16 * d:PF_GLA_LR + 16 * d + 16, ts], writes=[lrb])
            if d == 0:
                pg.dma(ob.ap, cx.OB[ts, 0:512], writes=[ob])
            zp = pre_ps.get()
            pg.op("pe", lambda e: e.matmul(zp.ap[:, :256], lhsT=lrb.ap[0:17, :], rhs=wup.ap[0:17, d, :], start=True, stop=True), reads=[lrb, wup], writes=[zp])
            pg.op("act", lambda e: e.activation(out=e1.ap, in_=zp.ap[:, :256], func=AF.Exp, scale=-1.0), reads=[zp], writes=[e1])
            pg.op("act", lambda e: e.activation(out=e1.ap, in_=e1.ap, func=AF.Ln, bias=cx.one.ap[:, 0:1]), reads=[e1, cx.one], writes=[e1])
            pg.op("dve", lambda e: e.tensor_scalar(out=la.ap, in0=e1.ap, scalar1=-1.0 / 16.0, scalar2=None, op0=ALU.mult), reads=[e1], writes=[la])
            bp = pre_ps.get()
            for h2 in range(2):
                pg.op("pe", lambda e: e.matmul(bp.ap[:, h2 * 128:(h2 + 1) * 128], lhsT=la.ap[:, h2 * 128:(h2 + 1) * 128], rhs=cx.m_incl.ap[:, d, :], start=True, stop=True),
                      reads=[la, cx.m_incl], writes=[bp])
            bp3 = bp.ap[:, 0:256].rearrange("p (c t) -> p c t", c=2)
            pg.op("act", lambda e: e.activation(out=eb.ap, in_=bp3, func=AF.Exp), reads=[bp], writes=[eb])
            pg.op("act", lambda e: e.activation(out=enb.ap, in_=bp3, func=AF.Exp, scale=-1.0), reads=[bp], writes=[enb])
            pg.op("dve", lambda e: e.scalar_tensor_tensor(out=qd.ap, in0=qkb.ap[:, 0:2, :], scalar=0.125, in1=eb.ap, op0=ALU.mult, op1=ALU.mult), reads=[qkb, eb], writes=[qd])
            pg.op("pool", lambda e: e.tensor_tensor(out=ki.ap, in0=qkb.ap[:, 2:4, :], in1=enb.ap, op=ALU.mult), reads=[qkb, enb], writes=[ki])
            dp = pre_ps.get()
            pg.op("pe", lambda e: e.matmul(dp.ap[:, :256], lhsT=cx.m_sa.ap[:, d, :], rhs=la.ap, start=True, stop=True), reads=[la, cx.m_sa], writes=[dp])
            pg.op("act", lambda e: e.activation(out=ed.ap, in_=dp.ap[:, :256], func=AF.Exp), reads=[dp], writes=[ed])
            pg.op("dve", lambda e: e.tensor_tensor(out=kend.ap, in0=tkb.ap[:, 0:256], in1=ed.ap, op=ALU.mult), reads=[tkb, ed], writes=[kend])
            pg.op("pool", lambda e: e.tensor_copy(out=vb.ap, in_=tkb.ap[:, 256:768]), reads=[tkb], writes=[vb])
            chunks = (0, 1) if d == 0 else (1, 0)

            def head_gen(h, d=d, chunks=chunks):
                h2, hp = h // 2, (h % 2) * 64
                hc = slice(h * 128, (h + 1) * 128)
                bank = cx.psf.items[h]
                o_ps = bank.ap[:, 384:512]
                pg.op("pe", lambda e: e.matmul(bank.ap[:, 0:128], lhsT=ki.ap[hp:hp + 64, h2, :], rhs=qd.ap[hp:hp + 64, h2, :], start=True, stop=True), reads=[ki, qd], writes=[bank])
                yield
                smb = sm[h]
                pg.op("dve", lambda e: e.tensor_tensor(out=smb.ap, in0=bank.ap[:, 0:128], in1=cx.m_incl.ap[:, d, :], op=ALU.mult), reads=[bank, cx.m_incl], writes=[smb])
                yield
                r0 = chunks[0] * 64
                pg.op("pe", lambda e: e.matmul(o_ps, lhsT=smb.ap, rhs=vb.ap[:, hc], start=True, stop=False), reads=[smb, vb], writes=[bank])
                pg.op("pe", lambda e: e.matmul(bank.ap[r0:r0 + 64, 384:512], lhsT=qd.ap[hp:hp + 64, h2, r0:r0 + 64], rhs=Sb[h].ap[hp:hp + 64, :], start=False, stop=True),
                      reads=[qd, Sb[h]], writes=[bank])
                for ci, c in enumerate(chunks):
                    r0 = c * 64
                    if ci == 1:
                        pg.op("pe", lambda e: e.matmul(bank.ap[r0:r0 + 64, 256:384], lhsT=qd.ap[hp:hp + 64, h2, r0:r0 + 64], rhs=Sb[h].ap[hp:hp + 64, :], start=True, stop=True),
                              reads=[qd, Sb[h]], writes=[bank])
                    pg.op("pe", lambda e: e.matmul(bank.ap[hp:hp + 64, 128:256], lhsT=kend.ap[r0:r0 + 64, h * 64:(h + 1) * 64], rhs=vb.ap[r0:r0 + 64, hc], start=True, stop=True),
                          reads=[kend, vb], writes=[bank])
                    yield
                    col = r0 + 63 if d == 0 else r0
                    pg.op("dve", lambda e: e.scalar_tensor_tensor(out=S32[h].ap[hp:hp + 64, :], in0=S32[h].ap[hp:hp + 64, :], scalar=eb.ap[hp:hp + 64, h2, col:col + 1],
                                                                  in1=bank.ap[hp:hp + 64, 128:256], op0=ALU.mult, op1=ALU.add), reads=[S32[h], eb, bank], writes=[S32[h]])
                    yield
                    pg.op("act", lambda e: e.copy(Sb[h].ap[hp:hp + 64, :], S32[h].ap[hp:hp + 64, :]), reads=[S32[h]], writes=[Sb[h]])
                    yield
                r1 = chunks[1] * 64
                pg.op("dve", lambda e: e.tensor_copy(out=osb.ap[:, hc], in_=o_ps), reads=[bank], writes=[osb])
                pg.op("dve", lambda e: e.tensor_tensor(out=osb.ap[r1:r1 + 64, hc], in0=bank.ap[r1:r1 + 64, 256:384], in1=osb.ap[r1:r1 + 64, hc], op=ALU.add), reads=[bank, osb], writes=[osb])

            gens = [head_gen(h) for h in range(4)]
            while gens:
                for gnr in list(gens):
                    try:
                        next(gnr)
                    except StopIteration:
                        gens.remove(gnr)
            if d == 1:
                pg.dma(cx.OB[ts, 0:512], osb.ap, reads=[osb])
            else:
                pg.op("pool", lambda e: e.tensor_tensor(out=osb.ap, in0=osb.ap, in1=ob.ap, op=ALU.add), reads=[osb, ob], writes=[osb])
                head_norm_gate_store(pg, cx, osb, ssq, junk, gn, tkb, 768, ysb, yT, 512, ts)


def head_norm_gate_store(pg, cx, osb, ssq, junk, gn, tkb, gcol, ysb, yT, bt_row0, ts):
    for h in range(4):
        hc = slice(h * 128, (h + 1) * 128)
        pg.op("act", lambda e: e.activation(out=junk.ap, in_=osb.ap[:, hc], func=AF.Square, accum_out=ssq.ap[:, h:h + 1]), reads=[osb], writes=[junk, ssq])
    pg.op("act", lambda e: e.activation(out=ssq.ap, in_=ssq.ap, func=AF.Sqrt, scale=1.0 / 128.0, bias=cx.eps.ap[:, 0:1]), reads=[ssq, cx.eps], writes=[ssq])
    pg.op("dve", lambda e: e.reciprocal(out=ssq.ap, in_=ssq.ap), reads=[ssq], writes=[ssq])
    for h in range(4):
        hc = slice(h * 128, (h + 1) * 128)
        pg.op("dve", lambda e: e.scalar_tensor_tensor(out=osb.ap[:, hc], in0=osb.ap[:, hc], scalar=ssq.ap[:, h:h + 1], in1=gn.ap, op0=ALU.mult, op1=ALU.mult),
              reads=[osb, ssq, gn], writes=[osb])
    pg.op("act", lambda e: e.activation(out=tkb.ap[:, gcol:gcol + 512], in_=tkb.ap[:, gcol:gcol + 512], func=AF.Silu), reads=[tkb], writes=[tkb])
    pg.op("dve", lambda e: e.tensor_tensor(out=ysb.ap, in0=osb.ap, in1=tkb.ap[:, gcol:gcol + 512], op=ALU.mult), reads=[osb, tkb], writes=[ysb])
    pb = cx.psb.get()
    for h in range(4):
        pg.op("pe", lambda e: e.transpose(out=pb.ap[:, h * 128:(h + 1) * 128], in_=ysb.ap[:, h * 128:(h + 1) * 128], identity=cx.identb.ap), reads=[ysb, cx.identb], writes=[pb])
    pg.op("act", lambda e: e.copy(yT.ap, pb.ap[:, 0:512].rearrange("p (c t) -> p c t", c=4)), reads=[pb], writes=[yT])
    pg.dma(cx.BT[bt_row0:bt_row0 + 512, ts].rearrange("(c p) t -> p c t", p=128), yT.ap, reads=[yT])


def phase_DN(pg, cx, es, L, l):
    nc = cx.nc
    w = cx.w
    NB = L // 128
    with ExitStack() as es0:
        sb = lambda name, shape, dt: pg.buf(es0.enter_context(nc.sbuf_tensor(pg.uname(name), shape, dt)).ap(), name)
        TL = 512
        cwD = sb("D0_cw", [128, 4, 12], F32)
        load_T(pg, cx, cwD, cwD.ap.rearrange("p j c -> p (j c)"), w["dn_conv_w"][l].rearrange("j (c p) -> (j c) p", p=128), 48)
        xin = [sb("D0_xin%d" % i, [128, TL + 3], F32) for i in range(2)]
        xc = sb("D0_xc", [128, TL], F32)
        sq = sb("D0_sq", [128, TL], F32)
        rs = sb("D0_rs", [128, TL], F32)
        fm = sb("D0_fm", [128, 12, TL], BF16)
        tm = sb("D0_tm", [128, 4, 1024], BF16)
        nt = L // TL
        for it in range(nt):
            t0 = it * TL
            lo, hi = max(t0 - 2, 0), min(t0 + TL + 1, L)
            for c in range(12):
                xb = xin[c % 2]
                if it == 0 or it == nt - 1:
                    pg.op("pool", lambda e: e.memset(xb.ap, 0.0), writes=[xb])
                prow = PF_DN_QKV + c * 128
                pg.dma(xb.ap[:, lo - (t0 - 2):hi - (t0 - 2)], cx.PF[prow:prow + 128, lo:hi], writes=[xb])
                pg.op("dve", lambda e: e.tensor_scalar(out=xc.ap, in0=xb.ap[:, 0:TL], scalar1=cwD.ap[:, 0, c:c + 1], scalar2=None, op0=ALU.mult), reads=[xb, cwD], writes=[xc])
                for j in range(1, 4):
                    pg.op("dve", lambda e: e.scalar_tensor_tensor(out=xc.ap, in0=xb.ap[:, j:j + TL], scalar=cwD.ap[:, j, c:c + 1], in1=xc.ap, op0=ALU.mult, op1=ALU.add),
                          reads=[xb, cwD, xc], writes=[xc])
                if c >= 8:
                    pg.op("act", lambda e: e.activation(out=fm.ap[:, c, :], in_=xc.ap, func=AF.Silu), reads=[xc], writes=[fm])
                else:
                    pg.op("act", lambda e: e.activation(out=xc.ap, in_=xc.ap, func=AF.Silu), reads=[xc], writes=[xc])
                    pg.op("pool", lambda e: e.tensor_tensor(out=sq.ap, in0=xc.ap, in1=xc.ap, op=ALU.mult), reads=[xc], writes=[sq])
                    ps = cx.psf.get()
                    pg.op("pe", lambda e: e.matmul(ps.ap, lhsT=cx.onesf.ap, rhs=sq.ap, start=True, stop=True), reads=[cx.onesf, sq], writes=[ps])
                    pg.op("act", lambda e: e.activation(out=rs.ap, in_=ps.ap, func=AF.Sqrt, bias=cx.eps.ap[:, 0:1]), reads=[ps, cx.eps], writes=[rs])
                    pg.op("dve", lambda e: e.reciprocal(out=rs.ap, in_=rs.ap), reads=[rs], writes=[rs])
                    sc = (128.0 ** -0.5) if c < 4 else 1.0
                    pg.op("dve", lambda e: e.scalar_tensor_tensor(out=fm.ap[:, c, :], in0=xc.ap, scalar=sc, in1=rs.ap, op0=ALU.mult, op1=ALU.mult), reads=[xc, rs], writes=[fm])
            pg.dma(cx.QKT[:, t0:t0 + TL].rearrange("(c p) t -> p c t", p=128), fm.ap[:, 0:8, :], reads=[fm])
            for j in range(4):
                pb = cx.psb.get()
                for c in range(8):
                    pg.op("pe", lambda e: e.transpose(out=pb.ap[:, c * 128:(c + 1) * 128], in_=fm.ap[:, 4 + c, j * 128:(j + 1) * 128], identity=cx.identb.ap),
                          reads=[fm, cx.identb], writes=[pb])
                pg.op("act", lambda e: e.copy(tm.ap[:, j, :], pb.ap), reads=[pb], writes=[tm])
            pg.dma(cx.KVT[t0:t0 + TL, :].rearrange("(j p) c -> p j c", p=128), tm.ap, reads=[tm])
        pg.barrier()
    sb = lambda name, shape, dt: pg.buf(es.enter_context(nc.sbuf_tensor(pg.uname(name), shape, dt)).ap(), name)
    ba = sb("D_ba", [128, NB, 16], F32)
    pg.dma(ba.ap, cx.PT[:, PT_DN_BA:PT_DN_BA + 16].rearrange("(n p) c -> p n c", p=128), writes=[ba])
    ba4 = ba.ap.rearrange("p n (d j h) -> p n d j h", d=2, j=2)
    dtb = sb("D_dtb", [128, 8], F32)
    nea = sb("D_nea", [128, 8], F32)
    s1 = w["dn_dt_bias"][l]
    pg.dma(dtb.ap, bass.AP(s1.tensor, s1.offset, [[0, 128], [1, 8]]), writes=[dtb])
    s2 = w["dn_a_log"][l]
    pg.dma(nea.ap, bass.AP(s2.tensor, s2.offset, [[0, 128], [1, 8]]), writes=[nea])
    pg.op("act", lambda e: e.activation(out=nea.ap, in_=nea.ap, func=AF.Exp), reads=[nea], writes=[nea])
    pg.op("dve", lambda e: e.tensor_scalar(out=nea.ap, in0=nea.ap, scalar1=-1.0, scalar2=None, op0=ALU.mult), reads=[nea], writes=[nea])
    beta = sb("D_beta", [128, NB, 2, 4], F32)
    nbeta = sb("D_nbeta", [128, NB, 2, 4], F32)
    g = sb("D_g", [128, NB, 2, 4], F32)
    pg.op("act", lambda e: e.activation(out=beta.ap, in_=ba4[:, :, :, 0, :], func=AF.Sigmoid), reads=[ba], writes=[beta])
    pg.op("dve", lambda e: e.tensor_scalar(out=nbeta.ap, in0=beta.ap, scalar1=-1.0, scalar2=None, op0=ALU.mult), reads=[beta], writes=[nbeta])
    dtb_b = dtb.ap.rearrange("p (d h) -> p d h", d=2).unsqueeze(1).to_broadcast([128, NB, 2, 4])
    nea_b = nea.ap.rearrange("p (d h) -> p d h", d=2).unsqueeze(1).to_broadcast([128, NB, 2, 4])
    pg.op("dve", lambda e: e.tensor_tensor(out=g.ap, in0=ba4[:, :, :, 1, :], in1=dtb_b, op=ALU.add), reads=[ba, dtb], writes=[g])
    pg.op("act", lambda e: e.activation(out=g.ap, in_=g.ap, func=AF.Exp), reads=[g], writes=[g])
    pg.op("act", lambda e: e.activation(out=g.ap, in_=g.ap, func=AF.Ln, bias=cx.one.ap[:, 0:1]), reads=[g, cx.one], writes=[g])
    pg.op("dve", lambda e: e.tensor_tensor(out=g.ap, in0=g.ap, in1=nea_b, op=ALU.mult), reads=[g, nea], writes=[g])
    eG = sb("D_eG", [128, NB, 2, 4], F32)
    eD = sb("D_eD", [128, NB, 2, 4], F32)
    bg = sb("D_bg", [128, NB, 2, 4], F32)
    deB = sb("D_deB", [128, 2, NB, 2, 4], F32)
    NQ = 32
    for d in range(2):
        for n0 in range(0, NB, NQ):
            nn = min(NQ, NB - n0)
            for (msk, dst, fn) in ((cx.m_incl.ap[:, d, :], eG, 0), (cx.m_sa.ap[:, d, :], eD, 0), (cx.chunkind.ap[:, 0, :], deB, 1), (cx.chunkind.ap[:, 1, :], deB, 2)):
                ps = cx.psf.get()
                pv = ps.ap[:, :nn * 4].rearrange("p (n h) -> p n h", h=4)
                pg.op("pe", lambda e: e.matmul(pv, lhsT=msk, rhs=g.ap[:, n0:n0 + nn, d, :], start=True, stop=True), reads=[g, cx.m_incl, cx.m_sa, cx.chunkind], writes=[ps])
                o_ap = dst.ap[:, n0:n0 + nn, d, :] if fn == 0 else dst.ap[:, fn - 1, n0:n0 + nn, d, :]
                pg.op("act", lambda e: e.activation(out=o_ap, in_=pv, func=AF.Exp), reads=[ps], writes=[dst])
    pg.op("dve", lambda e: e.tensor_tensor(out=bg.ap, in0=beta.ap, in1=eG.ap, op=ALU.mult), reads=[beta, eG], writes=[bg])
    gn = sb("D_gn", [128, 128], F32)
    gsrc = w["dn_norm_g"][l]
    pg.dma(gn.ap, bass.AP(gsrc.tensor, gsrc.offset, [[0, 128], [1, 128]]), writes=[gn])
    qk = [sb("D_qk%d" % i, [128, 8, 128], BF16) for i in range(2)]
    kv = [sb("D_kv%d" % i, [128, 2, 4, 128], BF16) for i in range(2)]
    gt = [sb("D_gt%d" % i, [128, 1280 + 512], F32) for i in range(1)]
    obt = [sb("D_ob%d" % i, [128, 512], F32) for i in range(2)]
    vb4 = sb("D_vb4", [128, 4, 128], BF16)
    kbg4 = sb("D_kbg4", [128, 4, 128], BF16)
    kend4 = sb("D_kend4", [128, 4, 128], BF16)
    gtri = [sb("D_gtri%d" % i, [128, 128], F32) for i in range(4)]
    gam = [sb("D_gam%d" % i, [128, 3, 128], F32) for i in range(4)]
    gamm = [sb("D_gamm%d" % i, [128, 2, 128], F32) for i in range(4)]
    qd = [sb("D_qd%d" % i, [128, 128], BF16) for i in range(4)]
    Cm = [sb("D_C%d" % i, [128, 128], BF16) for i in range(4)]
    attnT = [sb("D_at%d" % i, [128, 128], BF16) for i in range(4)]
    BC = [[sb("D_BC%d_%d" % (h, i), [128, 2, 128], BF16) for i in range(2)] for h in range(4)]
    Pm = [[sb("D_P%d_%d" % (h, i), [128, 128], BF16) for i in range(2)] for h in range(4)]
    Pm32 = [[sb("D_P32_%d_%d" % (h, i), [128, 128], F32) for i in range(2)] for h in range(4)]
    usb = [sb("D_u%d" % i, [128, 128], F32) for i in range(4)]
    wT = [sb("D_wT%d" % i, [128, 128], BF16) for i in range(4)]
    vn = [sb("D_vn%d" % i, [128, 128], BF16) for i in range(4)]
    S32 = [sb("D_S32_%d" % h, [128, 128], F32) for h in range(4)]
    Sb = [sb("D_Sb_%d" % h, [128, 128], BF16) for h in range(4)]
    osb = sb("D_osb", [128, 512], F32)
    ssq = sb("D_ssq", [128, 4], F32)
    junk = sb("D_junk", [128, 128], BF16)
    ysb = sb("D_ysb", [128, 512], BF16)
    yT = sb("D_yT", [128, 4, 128], BF16)
    QKv = cx.QKT.rearrange("(c p) t -> p c t", p=128)
    rr = [0]
    for d in (1, 0):
        pg.barrier()
        for h in range(4):
            pg.op("dve", lambda e: e.memset(S32[h].ap, 0.0), writes=[S32[h]])
            pg.op("pool", lambda e: e.memset(Sb[h].ap, 0.0), writes=[Sb[h]])
        order = range(NB) if d == 0 else range(NB - 1, -1, -1)
        chunks = (0, 1) if d == 0 else (1, 0)
        for bi, blk in enumerate(order):
            t0 = blk * 128
            ts = slice(t0, t0 + 128)
            qkb = qk[bi % 2]; kvb = kv[bi % 2]; ob = obt[bi % 2]; gtb = gt[0]
            pg.dma(qkb.ap, QKv[:, :, ts], writes=[qkb])
            pg.dma(kvb.ap.rearrange("p a h c -> p (a h c)"), cx.KVT[ts, :], writes=[kvb])
            if d == 0:
                pg.dma(ob.ap, cx.OB[ts, 512:1024], writes=[ob])
                pg.dma(gtb.ap[:, 0:512], cx.PT[ts, PT_DN_G:PT_DN_G + 512], writes=[gtb])
            bcast = lambda t: t.ap[:, blk, d, :].unsqueeze(2).to_broadcast([128, 4, 128])
            pg.op("dve", lambda e: e.tensor_tensor(out=vb4.ap, in0=kvb.ap[:, 1], in1=bcast(beta), op=ALU.mult), reads=[kvb, beta], writes=[vb4])
            pg.op("pool", lambda e: e.tensor_tensor(out=kbg4.ap, in0=kvb.ap[:, 0], in1=bcast(bg), op=ALU.mult), reads=[kvb, bg], writes=[kbg4])
            pg.op("pool", lambda e: e.tensor_tensor(out=kend4.ap, in0=kvb.ap[:, 0], in1=bcast(eD), op=ALU.mult), reads=[kvb, eD], writes=[kend4])
            op_ = cx.pso
            def head_gen(h, blk=blk, d=d, qkb=qkb, chunks=chunks, op_=op_):
                i2 = h
                bank = cx.psf.items[h]
                hc = slice(h * 128, (h + 1) * 128)
                gsc = g.ap[:, blk, d, h:h + 1]
                pg.op("dve", lambda e: e.tensor_scalar(out=gtri[i2].ap, in0=cx.m_incl.ap[:, d, :], scalar1=gsc, scalar2=None, op0=ALU.mult), reads=[cx.m_incl, g], writes=[gtri[i2]])
                yield
                dps = bank
                pg.op("pe", lambda e: e.matmul(dps.ap[:, 0:128], lhsT=gtri[i2].ap, rhs=cx.m_sa.ap[:, d, :], start=True, stop=True), reads=[gtri[i2], cx.m_sa], writes=[dps])
                pg.op("pe", lambda e: e.matmul(dps.ap[:, 128:256], lhsT=cx.m_sa.ap[:, d, :], rhs=gtri[i2].ap, start=True, stop=True), reads=[gtri[i2], cx.m_sa], writes=[dps])
                pg.op("pe", lambda e: e.matmul(dps.ap[:, 256:384], lhsT=cx.onesf.ap, rhs=gtri[i2].ap, start=True, stop=True), reads=[gtri[i2], cx.onesf], writes=[dps])
                yield
                pg.op("act", lambda e: e.activation(out=gam[i2].ap.rearrange("p a t -> p (a t)"), in_=dps.ap[:, 0:384], func=AF.Exp), reads=[dps], writes=[gam[i2]])
                yield
                pg.op("pool", lambda e: e.tensor_tensor(out=gamm[i2].ap, in0=gam[i2].ap[:, 0:2, :], in1=cx.m_dn.ap[:, d], op=ALU.mult), reads=[gam[i2], cx.m_dn], writes=[gamm[i2]])
                pg.op("pool", lambda e: e.tensor_tensor(out=qd[i2].ap, in0=qkb.ap[:, h, :], in1=gam[i2].ap[:, 2, :], op=ALU.mult), reads=[qkb, gam[i2]], writes=[qd[i2]])
                kps = bank
                pg.op("pe", lambda e: e.matmul(kps.ap[:, 0:128], lhsT=qkb.ap[:, 4 + h, :], rhs=qkb.ap[:, 4 + h, :], start=True, stop=True), reads=[qkb], writes=[kps])
                pg.op("pe", lambda e: e.matmul(kps.ap[:, 128:256], lhsT=qkb.ap[:, 4 + h, :], rhs=qkb.ap[:, h, :], start=True, stop=True), reads=[qkb], writes=[kps])
                yield
                pg.op("dve", lambda e: e.scalar_tensor_tensor(out=Cm[i2].ap, in0=kps.ap[:, 0:128], scalar=nbeta.ap[:, blk, d, h:h + 1], in1=gamm[i2].ap[:, 0, :], op0=ALU.mult, op1=ALU.mult),
                      reads=[kps, nbeta, gamm[i2]], writes=[Cm[i2]])
                pg.op("dve", lambda e: e.tensor_tensor(out=attnT[i2].ap, in0=kps.ap[:, 128:256], in1=gamm[i2].ap[:, 1, :], op=ALU.mult), reads=[kps, gamm[i2]], writes=[attnT[i2]])
                yield
                tb = bank
                tbv = bank.ap.bitcast(BF16)
                pg.op("pe", lambda e: e.transpose(out=tbv[:, 0:128], in_=Cm[i2].ap, identity=cx.identb.ap), reads=[Cm[i2], cx.identb], writes=[tb])
                yield
                hr = [0]
                bc0 = BC[h][hr[0] % 2]
                pg.op("act", lambda e: e.copy(bc0.ap[:, 0, :], tbv[:, 0:128]), reads=[tb], writes=[bc0])
                pg.op("pool", lambda e: e.tensor_copy(out=bc0.ap[:, 1, :], in_=Cm[i2].ap), reads=[Cm[i2]], writes=[bc0])
                p0 = Pm[h][hr[0] % 2]; p032 = Pm32[h][hr[0] % 2]; hr[0] += 1
                pg.op("pool", lambda e: e.tensor_tensor(out=p0.ap, in0=bc0.ap[:, 0, :], in1=cx.identb.ap, op=ALU.add), reads=[bc0, cx.identb], writes=[p0])
                pg.op("pool", lambda e: e.tensor_tensor(out=p032.ap, in0=bc0.ap[:, 0, :], in1=cx.identf.ap, op=ALU.add), reads=[bc0, cx.identf], writes=[p032])
                yield
                bcp, pp, pp32 = bc0, p0, p032
                for k in range(1, 6):
                    sq_ = bank
                    if k < 5:
                        pg.op("pe", lambda e: e.matmul(sq_.ap[:, 0:128], lhsT=bcp.ap[:, 1, :], rhs=bcp.ap[:, 0, :], start=True, stop=True), reads=[bcp], writes=[sq_])
                    pg.op("pe", lambda e: e.matmul(sq_.ap[:, 128:256], lhsT=bcp.ap[:, 0, :], rhs=bcp.ap[:, 1, :], start=True, stop=True), reads=[bcp], writes=[sq_])
                    yield
                    bcn = BC[h][hr[0] % 2]
                    if k < 5:
                        pg.op("act", lambda e: e.copy(bcn.ap.rearrange("p a t -> p (a t)"), sq_.ap[:, 0:256]), reads=[sq_], writes=[bcn])
                    else:
                        pg.op("act", lambda e: e.copy(bcn.ap[:, 1, :], sq_.ap[:, 128:256]), reads=[sq_], writes=[bcn])
                        yield
                    pps = bank
                    pg.op("pe", lambda e: e.matmul(pps.ap[:, 0:128], lhsT=bcn.ap[:, 1, :], rhs=pp.ap, start=True, stop=True), reads=[bcn, pp], writes=[pps])
                    yield
                    pn = Pm[h][hr[0] % 2]; pn32 = Pm32[h][hr[0] % 2]; hr[0] += 1
                    pg.op("dve", lambda e: e.tensor_tensor(out=pn32.ap, in0=pps.ap[:, 0:128], in1=pp32.ap, op=ALU.add), reads=[pps, pp32], writes=[pn32])
                    pg.op("pool", lambda e: e.tensor_copy(out=pn.ap, in_=pn32.ap), reads=[pn32], writes=[pn])
                    yield
                    bcp, pp, pp32 = bcn, pn, pn32
                ups = bank
                pg.op("pe", lambda e: e.matmul(ups.ap[:, 0:128], lhsT=pp.ap, rhs=vb4.ap[:, h, :], start=True, stop=True), reads=[pp, vb4], writes=[ups])
                pg.op("pe", lambda e: e.matmul(ups.ap[:, 128:256], lhsT=kbg4.ap[:, h, :], rhs=pp.ap, start=True, stop=True), reads=[pp, kbg4], writes=[ups])
                yield
                pg.op("act", lambda e: e.copy(usb[i2].ap, ups.ap[:, 0:128]), reads=[ups], writes=[usb[i2]])
                pg.op("act", lambda e: e.copy(wT[i2].ap, ups.ap[:, 128:256]), reads=[ups], writes=[wT[i2]])
                yield
                for ci, c in enumerate(chunks):
                    r0 = c * 64
                    rs_ = slice(r0, r0 + 64)
                    wps = bank
                    pg.op("pe", lambda e: e.matmul(wps.ap[rs_, 0:128], lhsT=wT[i2].ap[:, rs_], rhs=Sb[h].ap, start=True, stop=True), reads=[wT[i2], Sb[h]], writes=[wps])
                    yield
                    pg.op("dve", lambda e: e.scalar_tensor_tensor(out=vn[i2].ap[rs_, :], in0=wps.ap[rs_, 0:128], scalar=-1.0, in1=usb[i2].ap[rs_, :], op0=ALU.mult, op1=ALU.add), reads=[usb[i2], wps], writes=[vn[i2]])
                    yield
                    pg.op("pe", lambda e: e.matmul(op_.ap[rs_, hc], lhsT=qd[i2].ap[:, rs_], rhs=Sb[h].ap, start=True, stop=False), reads=[qd[i2], Sb[h]], writes=[op_])
                    pg.op("pe", lambda e: e.matmul(op_.ap[rs_, hc], lhsT=attnT[i2].ap[rs_, rs_], rhs=vn[i2].ap[rs_, :], start=False, stop=True), reads=[attnT[i2], vn[i2]], writes=[op_])
                    kvp = bank
                    pg.op("pe", lambda e: e.matmul(kvp.ap[:, 0:128], lhsT=kend4.ap[rs_, h, :], rhs=vn[i2].ap[rs_, :], start=True, stop=True), reads=[kend4, vn[i2]], writes=[kvp])
                    yield
                    pg.op("dve", lambda e: e.scalar_tensor_tensor(out=S32[h].ap, in0=S32[h].ap, scalar=deB.ap[:, c, blk, d, h:h + 1], in1=kvp.ap[:, 0:128], op0=ALU.mult, op1=ALU.add),
                          reads=[S32[h], deB, kvp], writes=[S32[h]])
                    pg.op("act", lambda e: e.copy(Sb[h].ap, S32[h].ap), reads=[S32[h]], writes=[Sb[h]])
                    yield
            gens = [head_gen(h) for h in range(4)]
            while gens:
                for gnr in list(gens):
                    try:
                        next(gnr)
                    except StopIteration:
                        gens.remove(gnr)
            if d == 1:
                pg.op("act", lambda e: e.copy(osb.ap, op_.ap), reads=[op_], writes=[osb])
                pg.dma(cx.OB[ts, 512:1024], osb.ap, reads=[osb])
            else:
                pg.op("dve", lambda e: e.tensor_tensor(out=osb.ap, in0=op_.ap, in1=ob.ap, op=ALU.add), reads=[op_, ob], writes=[osb])
                head_norm_gate_store(pg, cx, osb, ssq, junk, gn, gtb, 0, ysb, yT, 1024, ts)


def phase_S5(pg, cx, es, L, l):
    nc = cx.nc
    w = cx.w
    sb = lambda name, shape, dt: pg.buf(es.enter_context(nc.sbuf_tensor(pg.uname(name), shape, dt)).ap(), name)
    NS = int(np.ceil(np.log2(L)))
    dve = lambda fn, r, wr: pg.op("dve", fn, reads=r, writes=wr)
    A = lambda nm: sb("S_" + nm, [128, 32], F32)
    lre, lim, dt_, ar, ai, m_, sn, cs, Are, Aim, t1, t2, t3, fre, fim, den = [A(n) for n in
        ("lre", "lim", "dt", "ar", "ai", "m", "sn", "cs", "Are", "Aim", "t1", "t2", "t3", "fre", "fim", "den")]
    load_T(pg, cx, lre, lre.ap, w["s5_lambda_re"][l].rearrange("d (gh gl) p -> (d gh) (gl p)", gl=2), 32)
    load_T(pg, cx, lim, lim.ap, w["s5_lambda_im"][l].rearrange("d (gh gl) p -> (d gh) (gl p)", gl=2), 32)
    ld2 = sb("S_ld2", [32, 2], F32)
    pg.dma(ld2.ap, w["s5_log_dt"][l].rearrange("d (gh gl) -> (d gh) gl", gl=2), writes=[ld2])
    stl = sb("S_stl", [32, 128], F32)
    for gl in range(2):
        dve(lambda e: e.tensor_copy(out=stl.ap[:, 64 * gl:64 * gl + 64], in_=ld2.ap[:, gl:gl + 1].to_broadcast([32, 64])), [ld2], [stl])
    psl = cx.psf.get()
    pg.op("pe", lambda e: e.transpose(out=psl.ap[:, :32], in_=stl.ap, identity=cx.identf.ap[:32, :32]), reads=[stl, cx.identf], writes=[psl])
    dve(lambda e: e.tensor_copy(out=dt_.ap, in_=psl.ap[:, :32]), [psl], [dt_])
    pg.op("act", lambda e: e.activation(out=dt_.ap, in_=dt_.ap, func=AF.Exp), reads=[dt_], writes=[dt_])
    dve(lambda e: e.tensor_tensor(out=ar.ap, in0=lre.ap, in1=dt_.ap, op=ALU.mult), [lre, dt_], [ar])
    dve(lambda e: e.tensor_tensor(out=ai.ap, in0=lim.ap, in1=dt_.ap, op=ALU.mult), [lim, dt_], [ai])
    pg.op("act", lambda e: e.activation(out=m_.ap, in_=ar.ap, func=AF.Exp, scale=1.0 / 16), reads=[ar], writes=[m_])
    pg.op("act", lambda e: e.activation(out=sn.ap, in_=ai.ap, func=AF.Sin, scale=1.0 / 16), reads=[ai], writes=[sn])
    pg.op("act", lambda e: e.activation(out=cs.ap, in_=ai.ap, func=AF.Sin, scale=1.0 / 16, bias=cx.halfpi.ap[:, 0:1]), reads=[ai, cx.halfpi], writes=[cs])
    dve(lambda e: e.tensor_tensor(out=Are.ap, in0=m_.ap, in1=cs.ap, op=ALU.mult), [m_, cs], [Are])
    dve(lambda e: e.tensor_tensor(out=Aim.ap, in0=m_.ap, in1=sn.ap, op=ALU.mult), [m_, sn], [Aim])

    def csquare(re, im):
        dve(lambda e: e.tensor_tensor(out=t1.ap, in0=re, in1=re, op=ALU.mult), [Are, PW], [t1])
        dve(lambda e: e.tensor_tensor(out=t2.ap, in0=im, in1=im, op=ALU.mult), [Aim, PW], [t2])
        dve(lambda e: e.tensor_tensor(out=t3.ap, in0=re, in1=im, op=ALU.mult), [Are, Aim, PW], [t3])

    PW = sb("S_PW", [128, 32, NS, 3], F32)
    for _ in range(4):
        csquare(Are.ap, Aim.ap)
        dve(lambda e: e.tensor_tensor(out=Are.ap, in0=t1.ap, in1=t2.ap, op=ALU.subtract), [t1, t2], [Are])
        dve(lambda e: e.tensor_scalar(out=Aim.ap, in0=t3.ap, scalar1=2.0, scalar2=None, op0=ALU.mult), [t3], [Aim])
    dve(lambda e: e.tensor_tensor(out=den.ap, in0=lre.ap, in1=lre.ap, op=ALU.mult), [lre], [den])
    dve(lambda e: e.tensor_tensor(out=t1.ap, in0=lim.ap, in1=lim.ap, op=ALU.mult), [lim], [t1])
    dve(lambda e: e.tensor_tensor(out=den.ap, in0=den.ap, in1=t1.ap, op=ALU.add), [den, t1], [den])
    dve(lambda e: e.reciprocal(out=den.ap, in_=den.ap), [den], [den])
    dve(lambda e: e.tensor_scalar(out=t3.ap, in0=Are.ap, scalar1=-1.0, scalar2=None, op0=ALU.add), [Are], [t3])
    dve(lambda e: e.tensor_tensor(out=t1.ap, in0=t3.ap, in1=lre.ap, op=ALU.mult), [t3, lre], [t1])
    dve(lambda e: e.tensor_tensor(out=t2.ap, in0=Aim.ap, in1=lim.ap, op=ALU.mult), [Aim, lim], [t2])
    dve(lambda e: e.tensor_tensor(out=t1.ap, in0=t1.ap, in1=t2.ap, op=ALU.add), [t1, t2], [t1])
    dve(lambda e: e.tensor_tensor(out=fre.ap, in0=t1.ap, in1=den.ap, op=ALU.mult), [t1, den], [fre])
    dve(lambda e: e.tensor_tensor(out=t1.ap, in0=Aim.ap, in1=lre.ap, op=ALU.mult), [Aim, lre], [t1])
    dve(lambda e: e.tensor_tensor(out=t2.ap, in0=t3.ap, in1=lim.ap, op=ALU.mult), [t3, lim], [t2])
    dve(lambda e: e.tensor_tensor(out=t1.ap, in0=t1.ap, in1=t2.ap, op=ALU.subtract), [t1, t2], [t1])
    dve(lambda e: e.tensor_tensor(out=fim.ap, in0=t1.ap, in1=den.ap, op=ALU.mult), [t1, den], [fim])
    for k in range(NS):
        if k == 0:
            dve(lambda e: e.tensor_copy(out=PW.ap[:, :, 0, 0], in_=Are.ap), [Are], [PW])
            dve(lambda e: e.tensor_copy(out=PW.ap[:, :, 0, 1], in_=Aim.ap), [Aim], [PW])
        else:
            csquare(PW.ap[:, :, k - 1, 0], PW.ap[:, :, k - 1, 1])
            dve(lambda e: e.tensor_tensor(out=PW.ap[:, :, k, 0], in0=t1.ap, in1=t2.ap, op=ALU.subtract), [t1, t2], [PW])
            dve(lambda e: e.tensor_scalar(out=PW.ap[:, :, k, 1], in0=t3.ap, scalar1=2.0, scalar2=None, op0=ALU.mult), [t3], [PW])
        dve(lambda e: e.tensor_scalar(out=PW.ap[:, :, k, 2], in0=PW.ap[:, :, k, 1], scalar1=-1.0, scalar2=None, op0=ALU.mult), [PW], [PW])
    CL = sb("S_CL", [128, 32, 2, 32], F32)
    Wb = sb("S_Wb", [128, 32, 2, 32], F32)
    PWs = sb("S_PWs", [128, 32, 9, 2], F32)
    dve(lambda e: e.memset(PWs.ap[:, :, 0, 0], 1.0), [], [PWs])
    dve(lambda e: e.memset(PWs.ap[:, :, 0, 1], 0.0), [], [PWs])
    for k in range(1, 9):
        pr, pi_ = PWs.ap[:, :, k - 1, 0], PWs.ap[:, :, k - 1, 1]
        dve(lambda e: e.tensor_tensor(out=t1.ap, in0=pr, in1=PW.ap[:, :, 0, 0], op=ALU.mult), [PWs, PW], [t1])
        dve(lambda e: e.tensor_tensor(out=t2.ap, in0=pi_, in1=PW.ap[:, :, 0, 1], op=ALU.mult), [PWs, PW], [t2])
        dve(lambda e: e.tensor_tensor(out=PWs.ap[:, :, k, 0], in0=t1.ap, in1=t2.ap, op=ALU.subtract), [t1, t2], [PWs])
        dve(lambda e: e.tensor_tensor(out=t1.ap, in0=pr, in1=PW.ap[:, :, 0, 1], op=ALU.mult), [PWs, PW], [t1])
        dve(lambda e: e.tensor_tensor(out=t2.ap, in0=pi_, in1=PW.ap[:, :, 0, 0], op=ALU.mult), [PWs, PW], [t2])
        dve(lambda e: e.tensor_tensor(out=PWs.ap[:, :, k, 1], in0=t1.ap, in1=t2.ap, op=ALU.add), [t1, t2], [PWs])
    with ExitStack() as es1:
        sb1 = lambda name, shape, dt: pg.buf(es1.enter_context(nc.sbuf_tensor(pg.uname(name), shape, dt)).ap(), name)
        Bt = [sb1("S_Bt%d" % i, [128, 32, 16], F32) for i in range(2)]
        Bb = [sb1("S_Bb%d" % i, [128, 32, 16], F32) for i in range(2)]
        tmpb = sb1("S_tmpb", [128, 32, 16], F32)
        for i, nm in enumerate(("s5_b_re", "s5_b_im")):
            base = w[nm][l]
            pg.dma(Bt[i].ap, bass.AP(base.tensor, base.offset, [[16, 128], [2048, 32], [1, 16]]), writes=[Bt[i]])
        fb = lambda t: t.ap.unsqueeze(2).to_broadcast([128, 32, 16])
        dve(lambda e: e.tensor_tensor(out=Bb[0].ap, in0=Bt[0].ap, in1=fb(fre), op=ALU.mult), [Bt[0], fre], [Bb[0]])
        dve(lambda e: e.tensor_tensor(out=tmpb.ap, in0=Bt[1].ap, in1=fb(fim), op=ALU.mult), [Bt[1], fim], [tmpb])
        dve(lambda e: e.tensor_tensor(out=Bb[0].ap, in0=Bb[0].ap, in1=tmpb.ap, op=ALU.subtract), [Bb[0], tmpb], [Bb[0]])
        dve(lambda e: e.tensor_tensor(out=Bb[1].ap, in0=Bt[1].ap, in1=fb(fre), op=ALU.mult), [Bt[1], fre], [Bb[1]])
        dve(lambda e: e.tensor_tensor(out=tmpb.ap, in0=Bt[0].ap, in1=fb(fim), op=ALU.mult), [Bt[0], fim], [tmpb])
        dve(lambda e: e.tensor_tensor(out=Bb[1].ap, in0=Bb[1].ap, in1=tmpb.ap, op=ALU.add), [Bb[1], tmpb], [Bb[1]])
        pg.op("pool", lambda e: e.memset(Wb.ap, 0.0), writes=[Wb])
        for c in range(2):
            dve(lambda e: e.tensor_copy(out=Wb.ap[0:64, :, c, 0:16], in_=Bb[c].ap[0:64]), [Bb[c]], [Wb])
            dve(lambda e: e.tensor_copy(out=Wb.ap[64:128, :, c, 16:32], in_=Bb[c].ap[64:128]), [Bb[c]], [Wb])
        St0 = sb1("S_St0", [128, 64], F32)
        St = sb1("S_St", [128, 128], F32)
        for d in range(2):
            for c, nm in enumerate(("s5_c_re", "s5_c_im")):
                for blk in range(4):
                    pg.dma(St0.ap, w[nm][l, d, 8 * blk:8 * blk + 8].rearrange("g i p -> (g i) p"), writes=[St0])
                    sgn = 1.0 if c == 0 else -1.0
                    for hh in range(2):
                        dve(lambda e: e.tensor_scalar(out=St.ap[:, 64 * hh:64 * hh + 64], in0=St0.ap, scalar1=cx.pm.ap[:, hh:hh + 1], scalar2=sgn, op0=ALU.mult, op1=ALU.mult),
                            [St0, cx.pm], [St])
                    ps = cx.psf.get()
                    pg.op("pe", lambda e: e.transpose(out=ps.ap[:, 0:128], in_=St.ap, identity=cx.identf.ap), reads=[St, cx.identf], writes=[ps])
                    dg0 = d * 16 + blk * 4
                    pg.op("act", lambda e: e.copy(CL.ap[:, dg0:dg0 + 4, c, :], ps.ap[:, 0:128].rearrange("p (q m) -> p q m", q=4)), reads=[ps], writes=[CL])
    dsk = sb("S_dsk", [32, 16], F32)
    load_T(pg, cx, dsk, dsk.ap, w["s5_d"][l].rearrange("(g q) -> g q", q=32), 16, wd=32)
    bgl = sb("S_bgl", [128, 4], F32)
    load_T(pg, cx, bgl, bgl.ap, w["s5_b_glu"][l].rearrange("(c p) -> c p", p=128), 4)
    es2 = ExitStack()
    sb2 = lambda name, shape, dt: pg.buf(es2.enter_context(nc.sbuf_tensor(pg.uname(name), shape, dt)).ap(), name)
    NCH = L // 8
    NSC = int(np.ceil(np.log2(NCH)))
    HW = min(512, NCH)
    NH = NCH // HW
    ub = sb2("S_ub", [32, L], BF16)
    UW = min(2048, L)
    ust = [sb2("S_ust%d" % i, [32, UW], F32) for i in range(2)]
    Yc = sb2("S_Yc", [32, L], F32)
    XS_ = [[sb2("S_X%d_%d" % (d, i), [128, NCH + 2], F32) for i in range(3)] for d in range(2)]
    Xb = [[sb2("S_Xb%d_%d" % (d, c), [128, NCH + 2], BF16) for c in range(2)] for d in range(2)]
    Wt = sb2("S_Wt", [128, 2, 8, 32], F32)
    tmpw = sb2("S_tmpw", [128, 8, 32], F32)
    WsT = [sb2("S_WsT%d" % d, [32, 2, 8, 128], BF16) for d in range(2)]
    CI = [sb2("S_CI%d" % d, [128, 2, 8, 32], BF16) for d in range(2)]
    CIf = sb2("S_CIf", [128, 2, 8, 32], F32)
    Kd = [sb2("S_Kd%d" % d, [32, 8, 32], BF16) for d in range(2)]
    ua_t = sb2("S_ua", [32, UW], F32)
    x2_t = sb2("S_x2", [32, UW], F32)
    zo_t = sb2("S_zo", [32, UW], BF16)

    def strided(ap2, start, n, step):
        b0 = ap2[:, start:start + 1]
        return bass.AP(b0.tensor, b0.offset, [list(ap2.ap[0]), [step * ap2.ap[1][0], n]])

    ev = [0]
    bcW = lambda a: a.unsqueeze(1).to_broadcast([128, 8, 32])
    bcP = lambda a: a.unsqueeze(2).to_broadcast([128, 8, 32])
    for gh in range(16):
        urow = PF_S5_U + 32 * gh
        for i, t0 in enumerate(range(0, L, UW)):
            st_ = ust[i % 2]
            pg.dma(st_.ap, cx.PF[urow:urow + 32, t0:t0 + UW], writes=[st_])
            pg.op("pool", lambda e: e.tensor_copy(out=ub.ap[:, t0:t0 + UW], in_=st_.ap), reads=[st_], writes=[ub])
        for d in range(2):
            dg = d * 16 + gh
            wbr, wbi = Wb.ap[:, dg, 0, :], Wb.ap[:, dg, 1, :]
            pre, pim = PWs.ap[:, dg, 0:8, 0], PWs.ap[:, dg, 0:8, 1]
            dve(lambda e: e.tensor_tensor(out=Wt.ap[:, 0], in0=bcW(wbr), in1=bcP(pre), op=ALU.mult), [Wb, PWs], [Wt])
            dve(lambda e: e.tensor_tensor(out=tmpw.ap, in0=bcW(wbi), in1=bcP(pim), op=ALU.mult), [Wb, PWs], [tmpw])
            dve(lambda e: e.tensor_tensor(out=Wt.ap[:, 0], in0=Wt.ap[:, 0], in1=tmpw.ap, op=ALU.subtract), [Wt, tmpw], [Wt])
            dve(lambda e: e.tensor_tensor(out=Wt.ap[:, 1], in0=bcW(wbr), in1=bcP(pim), op=ALU.mult), [Wb, PWs], [Wt])
            dve(lambda e: e.tensor_tensor(out=tmpw.ap, in0=bcW(wbi), in1=bcP(pre), op=ALU.mult), [Wb, PWs], [tmpw])
            dve(lambda e: e.tensor_tensor(out=Wt.ap[:, 1], in0=Wt.ap[:, 1], in1=tmpw.ap, op=ALU.add), [Wt, tmpw], [Wt])
            for c in range(2):
                for t4 in range(0, 8, 4):
                    ps = cx.psf.get()
                    for tq in range(4):
                        pg.op("pe", lambda e: e.transpose(out=ps.ap[:32, tq * 128:(tq + 1) * 128], in_=Wt.ap[:, c, t4 + tq, :], identity=cx.identf.ap), reads=[Wt, cx.identf], writes=[ps])
                    pg.op("act", lambda e: e.copy(WsT[d].ap[:, c, t4:t4 + 4, :], ps.ap[:32, :].rearrange("p (q m) -> p q m", q=4)), reads=[ps], writes=[WsT[d]])
            cl0, cl1 = CL.ap[:, dg, 0, :], CL.ap[:, dg, 1, :]
            pre1, pim1 = PWs.ap[:, dg, 1:9, 0], PWs.ap[:, dg, 1:9, 1]
            dve(lambda e: e.tensor_tensor(out=CIf.ap[:, 0], in0=bcW(cl0), in1=bcP(pre1), op=ALU.mult), [CL, PWs], [CIf])
            dve(lambda e: e.tensor_tensor(out=tmpw.ap, in0=bcW(cl1), in1=bcP(pim1), op=ALU.mult), [CL, PWs], [tmpw])
            dve(lambda e: e.tensor_tensor(out=CI[d].ap[:, 0], in0=CIf.ap[:, 0], in1=tmpw.ap, op=ALU.add), [CIf, tmpw], [CI[d]])
            dve(lambda e: e.tensor_tensor(out=CIf.ap[:, 1], in0=bcW(cl1), in1=bcP(pre1), op=ALU.mult), [CL, PWs], [CIf])
            dve(lambda e: e.tensor_tensor(out=tmpw.ap, in0=bcW(cl0), in1=bcP(pim1), op=ALU.mult), [CL, PWs], [tmpw])
            dve(lambda e: e.tensor_tensor(out=CI[d].ap[:, 1], in0=CIf.ap[:, 1], in1=tmpw.ap, op=ALU.subtract), [CIf, tmpw], [CI[d]])
            ps = cx.psf.get()
            for tau in range(8):
                po = ps.ap[0:32, tau * 32:(tau + 1) * 32]
                pg.op("pe", lambda e: e.matmul(po, lhsT=Wt.ap[:, 0, tau, :], rhs=cl0, start=True, stop=False), reads=[Wt, CL], writes=[ps])
                pg.op("pe", lambda e: e.matmul(po, lhsT=Wt.ap[:, 1, tau, :], rhs=cl1, start=False, stop=True), reads=[Wt, CL], writes=[ps])
            pg.op("act", lambda e: e.copy(Kd[d].ap, ps.ap[0:32, 0:256].rearrange("p (t m) -> p t m", t=8)), reads=[ps], writes=[Kd[d]])
            re, im, T = XS_[d]
            for b_ in (re, im, T):
                pg.op("pool", lambda e: e.memset(b_.ap, 0.0), writes=[b_])
            for c, dstb in ((0, re), (1, im)):
                for hf in range(NH):
                    ps = cx.psf.get()
                    for s_ in range(8):
                        tau = 7 - s_ if d == 0 else s_
                        pg.op("pe", lambda e: e.matmul(ps.ap[:, :HW], lhsT=WsT[d].ap[:, c, tau, :], rhs=strided(ub.ap, hf * HW * 8 + s_, HW, 8), start=(s_ == 0), stop=(s_ == 7)),
                              reads=[WsT[d], ub], writes=[ps])
                    ev[0] += 1
                    if ev[0] % 2 == 0:
                        pg.op("act", lambda e: e.copy(dstb.ap[:, 1 + hf * HW:1 + (hf + 1) * HW], ps.ap[:, :HW]), reads=[ps], writes=[dstb])
                    else:
                        pg.op("dve", lambda e: e.tensor_copy(out=dstb.ap[:, 1 + hf * HW:1 + (hf + 1) * HW], in_=ps.ap[:, :HW]), reads=[ps], writes=[dstb])
            for k in range(NSC):
                sft = 1 << k
                if sft >= NCH:
                    break
                kk = k + 3
                cre, cim, ncim = PW.ap[:, dg, kk, 0:1], PW.ap[:, dg, kk, 1:2], PW.ap[:, dg, kk, 2:3]
                if d == 0:
                    dst, src, keep = slice(1 + sft, 1 + NCH), slice(1, 1 + NCH - sft), slice(1, 1 + sft)
                else:
                    dst, src, keep = slice(1, 1 + NCH - sft), slice(1 + sft, 1 + NCH), slice(1 + NCH - sft, 1 + NCH)
                dve(lambda e: e.scalar_tensor_tensor(out=T.ap[:, dst], in0=re.ap[:, src], scalar=cre, in1=re.ap[:, dst], op0=ALU.mult, op1=ALU.add), [re, PW], [T])
                dve(lambda e: e.scalar_tensor_tensor(out=T.ap[:, dst], in0=im.ap[:, src], scalar=ncim, in1=T.ap[:, dst], op0=ALU.mult, op1=ALU.add), [im, T, PW], [T])
                pg.op("pool", lambda e: e.tensor_copy(out=T.ap[:, keep], in_=re.ap[:, keep]), reads=[re], writes=[T])
                if d == 0:
                    rv = lambda ap, sl: bass.AP(ap.tensor, ap[:, sl].offset + (sl.stop - sl.start) - 1, [list(ap.ap[0]), [-1, sl.stop - sl.start]])
                    dve(lambda e: e.scalar_tensor_tensor(out=rv(im.ap, dst), in0=rv(im.ap, src), scalar=cre, in1=rv(im.ap, dst), op0=ALU.mult, op1=ALU.add), [im, PW], [im])
                else:
                    dve(lambda e: e.scalar_tensor_tensor(out=im.ap[:, dst], in0=im.ap[:, src], scalar=cre, in1=im.ap[:, dst], op0=ALU.mult, op1=ALU.add), [im, PW], [im])
                dve(lambda e: e.scalar_tensor_tensor(out=im.ap[:, dst], in0=re.ap[:, src], scalar=cim, in1=im.ap[:, dst], op0=ALU.mult, op1=ALU.add), [re, im, PW], [im])
                re, T = T, re
            pg.op("pool", lambda e: e.tensor_copy(out=Xb[d][0].ap, in_=re.ap), reads=[re], writes=[Xb[d][0]])
            pg.op("pool", lambda e: e.tensor_copy(out=Xb[d][1].ap, in_=im.ap), reads=[im], writes=[Xb[d][1]])
        for hf in range(NH):
            for sp in range(8):
                ps = cx.psf.get()
                po = ps.ap[0:32, :HW]
                mm = []
                for c in range(2):
                    mm.append((CI[0].ap[:, c, sp, :], Xb[0][c].ap[:, hf * HW:hf * HW + HW], [CI[0], Xb[0][c]]))
                    mm.append((CI[1].ap[:, c, 7 - sp, :], Xb[1][c].ap[:, hf * HW + 2:hf * HW + 2 + HW], [CI[1], Xb[1][c]]))
                for s_ in range(0, sp + 1):
                    mm.append((Kd[0].ap[:, sp - s_, :], strided(ub.ap, hf * HW * 8 + s_, HW, 8), [Kd[0], ub]))
                for s_ in range(sp, 8):
                    mm.append((Kd[1].ap[:, s_ - sp, :], strided(ub.ap, hf * HW * 8 + s_, HW, 8), [Kd[1], ub]))
                for i, (lh, rh, rd) in enumerate(mm):
                    pg.op("pe", lambda e: e.matmul(po, lhsT=lh, rhs=rh, start=(i == 0), stop=(i == len(mm) - 1)), reads=rd, writes=[ps])
                pg.op("act", lambda e: e.copy(strided(Yc.ap, hf * HW * 8 + sp, HW, 8), po), reads=[ps], writes=[Yc])
        for t0 in range(0, L, UW):
            tsl = slice(t0, t0 + UW)
            ua, x2, zo = ua_t.ap, x2_t.ap, zo_t.ap
            pg.dma(ua, cx.PF[urow:urow + 32, tsl], writes=[ua_t])
            yv = Yc.ap[:, tsl]
            dve(lambda e: e.scalar_tensor_tensor(out=yv, in0=ua, scalar=dsk.ap[:, gh:gh + 1], in1=yv, op0=ALU.mult, op1=ALU.add), [ua_t, dsk, Yc], [Yc])
            pg.op("pool", lambda e: e.tensor_tensor(out=x2, in0=yv, in1=yv, op=ALU.mult), reads=[Yc], writes=[x2_t])
            dve(lambda e: e.tensor_scalar(out=x2, in0=x2, scalar1=0.044715, scalar2=1.0, op0=ALU.mult, op1=ALU.add), [x2_t], [x2_t])
            pg.op("pool", lambda e: e.tensor_tensor(out=x2, in0=x2, in1=yv, op=ALU.mult), reads=[Yc, x2_t], writes=[x2_t])
            pg.op("act", lambda e: e.activation(out=x2, in_=x2, func=AF.Sigmoid, scale=1.5957691216), reads=[x2_t], writes=[x2_t])
            dve(lambda e: e.tensor_tensor(out=zo, in0=x2, in1=yv, op=ALU.mult), [x2_t, Yc], [zo_t])
            pg.dma(cx.ZT[urow - PF_S5_U:urow - PF_S5_U + 32, tsl], zo, reads=[zo_t])
    pg.barrier()
    es2.close()
    wg = sb("S_wg", [128, 4, 512], BF16)
    wst = sb("S_wst", [128, 2048], F32)
    pg.dma(wst.ap[:, 0:2048].rearrange("p (k c) -> p k c", k=4), w["s5_w_glu"][l].rearrange("(k p) c -> p k c", p=128), writes=[wst])
    dve(lambda e: e.tensor_copy(out=wg.ap, in_=wst.ap[:, 0:2048].rearrange("p (k c) -> p k c", k=4)), [wst], [wg])
    zt = [sb("S_zt%d" % i, [128, 4, 512], BF16) for i in range(2)]
    gt = [sb("S_gt%d" % i, [128, 4, 512], F32) for i in range(2)]
    sg = sb("S_sg", [128, 512], F32)
    yo = [sb("S_yo%d" % i, [128, 4, 512], BF16) for i in range(2)]
    ZTv = cx.ZT.rearrange("(k p) t -> p k t", p=128)
    NT = L // 512
    for it in range(NT):
        tsl = slice(it * 512, (it + 1) * 512)
        z_ = zt[it % 2]; g_ = gt[it % 2]; y_ = yo[it % 2]
        pg.dma(z_.ap, ZTv[:, :, tsl], writes=[z_])
        pg.dma(g_.ap, cx.PF[PF_S5_G:PF_S5_G + 512, tsl].rearrange("(k p) t -> p k t", p=128), writes=[g_])
        pg.op("act", lambda e: e.activation(out=g_.ap, in_=g_.ap, func=AF.Silu), reads=[g_], writes=[g_])
        pg.op("pool", lambda e: e.tensor_tensor(out=g_.ap, in0=g_.ap, in1=z_.ap, op=ALU.mult), reads=[g_, z_], writes=[g_])
        for oc in range(4):
            ps = cx.psf.get()
            for k in range(4):
                pg.op("pe", lambda e: e.matmul(ps.ap, lhsT=wg.ap[:, k, oc * 128:(oc + 1) * 128], rhs=z_.ap[:, k, :], start=(k == 0), stop=(k == 3)), reads=[wg, z_], writes=[ps])
            pg.op("act", lambda e: e.activation(out=sg.ap, in_=ps.ap, func=AF.Sigmoid, bias=bgl.ap[:, oc:oc + 1]), reads=[ps, bgl], writes=[sg])
            dve(lambda e: e.tensor_tensor(out=y_.ap[:, oc, :], in0=sg.ap, in1=g_.ap[:, oc, :], op=ALU.mult), [sg, g_], [y_])
        pg.dma(cx.BT[1536:2048, tsl].rearrange("(k p) t -> p k t", p=128), y_.ap, reads=[y_])


W_NAMES = ["norm_g", "w_in", "lru_conv_w", "lru_conv_b", "lru_w_a", "lru_b_a", "lru_w_x", "lru_b_x", "lru_lambda",
           "gla_w_up", "gla_b_up", "gla_norm_g", "dn_conv_w", "dn_a_log", "dn_dt_bias", "dn_norm_g",
           "s5_lambda_re", "s5_lambda_im", "s5_log_dt", "s5_b_re", "s5_b_im", "s5_c_re", "s5_c_im", "s5_d",
           "s5_w_glu", "s5_b_glu", "w_branch", "w_merge_gate", "b_merge_gate", "w_out", "final_norm_g"]


def host_consts():
    c = {}
    c["identb"] = np.eye(128, dtype=np.float32).astype(ml_dtypes.bfloat16)
    c["identf"] = np.eye(128, dtype=np.float32)
    idx = np.arange(128)
    same = (idx[:, None] // 64) == (idx[None, :] // 64)
    le = idx[:, None] <= idx[None, :]
    lt = idx[:, None] < idx[None, :]
    c["m_incl"] = np.stack([(same & le), (same & le.T)]).astype(np.float32)
    c["m_strict_after"] = np.stack([(same & lt.T), (same & lt)]).astype(np.float32)
    c["m_dn"] = np.stack([c["m_strict_after"], c["m_incl"]], axis=1)
    c["chunkind"] = np.stack([np.repeat((idx // 64 == cc)[:, None], 128, axis=1) for cc in range(2)]).astype(np.float32)
    c["onesf"] = np.ones((128, 128), np.float32)
    ev = ((idx // 16) % 2 == 0).astype(np.float32)
    c["pm"] = np.stack([ev, 1.0 - ev], axis=1).astype(np.float32)
    return c


def build(L, shapes, nslot=2, depth=2, debug=False, branches=("lru", "gla", "dn", "s5")):
    from contextlib import ExitStack
    nc = bass.Bass("TRN2", target_bir_lowering=False)
    pg = Prog(nc)
    cx = Ctx()
    cx.nc = nc
    cx.pg = pg
    cx.w = {}
    for nm in W_NAMES:
        cx.w[nm] = nc.dram_tensor(nm, list(shapes[nm]), F32, kind="ExternalInput").ap()
    hc = host_consts()
    cx.cd = {}
    for nm, arr in hc.items():
        cx.cd[nm] = nc.dram_tensor("c_" + nm, list(arr.shape), BF16 if arr.dtype == ml_dtypes.bfloat16 else F32, kind="ExternalInput").ap()
    xs = [nc.dram_tensor("x%d" % s, [L, D], F32, kind="ExternalInput").ap() for s in range(nslot)]
    ys = [nc.dram_tensor("y%d" % s, [L, D], F32, kind="ExternalOutput").ap() for s in range(nslot)]
    sk = "ExternalOutput" if debug else "Internal"
    cx.PF = nc.dram_tensor("PF", [PF_ROWS, L], F32, kind=sk).ap()
    cx.PT = nc.dram_tensor("PT", [L, PT_COLS], F32, kind=sk).ap()
    cx.XNT = nc.dram_tensor("XNT", [D, L], BF16, kind=sk).ap()
    cx.BT = nc.dram_tensor("BT", [2048, L], BF16, kind=sk).ap()
    cx.OB = nc.dram_tensor("OB", [L, 1024], F32, kind=sk).ap()
    XS = [nc.dram_tensor("XS%d" % s, [L, D], F32, kind=sk).ap() for s in range(nslot)]
    cx.QKT = nc.dram_tensor("QKT", [1024, L], BF16, kind=sk).ap()
    cx.KVT = nc.dram_tensor("KVT", [L, 1024], BF16, kind=sk).ap()
    cx.ZT = nc.dram_tensor("ZT", [512, L], BF16, kind=sk).ap()
    psf, psb = mk_psum(pg, nc)
    cx.psf = Rot(psf[:5])
    cx.pso = psf[5]
    cx.psb = Rot(psb)
    gsb = lambda name, shape, dt: pg.buf(nc.alloc_sbuf_tensor(name, shape, dt).ap(), name)
    cx.identb = gsb("identb", [128, 128], BF16)
    pg.dma(cx.identb.ap, cx.cd["identb"], writes=[cx.identb])
    cx.identf = gsb("identf", [128, 128], F32)
    pg.dma(cx.identf.ap, cx.cd["identf"], writes=[cx.identf])
    cx.eps = gsb("eps", [128, 1], F32)
    pg.op("dve", lambda e: e.memset(cx.eps.ap, EPS), writes=[cx.eps])
    cx.one = gsb("one", [128, 1], F32)
    pg.op("dve", lambda e: e.memset(cx.one.ap, 1.0), writes=[cx.one])
    cx.ldst = gsb("ldst", [128, 128], F32)
    cx.m_incl = gsb("m_incl", [128, 2, 128], F32)
    pg.dma(cx.m_incl.ap, cx.cd["m_incl"].rearrange("d s t -> s d t"), writes=[cx.m_incl])
    cx.m_sa = gsb("m_sa", [128, 2, 128], F32)
    pg.dma(cx.m_sa.ap, cx.cd["m_strict_after"].rearrange("d s t -> s d t"), writes=[cx.m_sa])
    cx.m_dn = gsb("m_dn", [128, 2, 2, 128], F32)
    pg.dma(cx.m_dn.ap[:, 0], cx.cd["m_dn"][0].rearrange("j s t -> s j t"), writes=[cx.m_dn])
    pg.dma(cx.m_dn.ap[:, 1], cx.cd["m_dn"][1].rearrange("j s t -> s j t"), writes=[cx.m_dn])
    cx.chunkind = gsb("chunkind", [128, 2, 128], F32)
    pg.dma(cx.chunkind.ap, cx.cd["chunkind"].rearrange("c s m -> s c m"), writes=[cx.chunkind])
    cx.onesf = gsb("onesf", [128, 128], F32)
    pg.dma(cx.onesf.ap, cx.cd["onesf"], writes=[cx.onesf])
    cx.pm = gsb("pm", [128, 2], F32)
    pg.dma(cx.pm.ap, cx.cd["pm"], writes=[cx.pm])
    cx.halfpi = gsb("halfpi", [128, 1], F32)
    pg.op("dve", lambda e: e.memset(cx.halfpi.ap, float(np.pi / 2)), writes=[cx.halfpi])
    cx.zb = gsb("zb", [128, 2048], BF16)
    pg.op("pool", lambda e: e.memset(cx.zb.ap, 0.0), writes=[cx.zb])
    bidx = {"lru": 0, "gla": 1, "dn": 2, "s5": 3}
    for l in range(depth):
        for s in range(nslot):
            xin = xs[s] if l == 0 else XS[s]
            last = (l == depth - 1)
            xout = ys[s] if last else XS[s]
            pg.barrier()
            with ExitStack() as es:
                phase_P(pg, cx, es, L, xin, l)
                pg.barrier()
            for bn in ("lru", "gla", "dn", "s5"):
                if bn not in branches:
                    b = bidx[bn]
                    for t0 in range(0, L, 2048):
                        tw = min(2048, L - t0)
                        for c in range(4):
                            pg.dma(cx.BT[b * 512 + c * 128:b * 512 + (c + 1) * 128, t0:t0 + tw], cx.zb.ap[:, :tw], reads=[cx.zb])
            if "lru" in branches:
                with ExitStack() as es:
                    phase_LRU(pg, cx, es, L, l)
                    pg.barrier()
            if "s5" in branches:
                with ExitStack() as es:
                    phase_S5(pg, cx, es, L, l)
                    pg.barrier()
            if "gla" in branches:
                with ExitStack() as es:
                    phase_GLA(pg, cx, es, L, l)
                    pg.barrier()
            if "dn" in branches:
                with ExitStack() as es:
                    phase_DN(pg, cx, es, L, l)
                    pg.barrier()
            pg.barrier()
            with ExitStack() as es:
                phase_M(pg, cx, es, L, xin, xout, l, last)
                pg.barrier()
    pg.barrier()
    return nc, pg, hc


_CACHE = {}


def kernel(**inputs):
    L = inputs["x_prompt"].shape[1]
    shapes = {nm: inputs[nm].shape for nm in W_NAMES}
    nc, pg, hc = build(L, shapes)
    xp = np.ascontiguousarray(inputs["x_prompt"], dtype=np.float32)
    xsm = np.ascontiguousarray(inputs["x_sample"], dtype=np.float32)
    wmap = {nm: np.ascontiguousarray(inputs[nm], dtype=np.float32) for nm in W_NAMES}
    in_maps = []
    for c in range(8):
        m = dict(wmap)
        for nm, arr in hc.items():
            m["c_" + nm] = arr
        m["x0"] = xp[c]
        m["x1"] = xsm[c % 2]
        in_maps.append(m)
    res = run_bass_kernel_spmd(nc, in_maps, core_ids=list(range(8)))
    y_prompt = np.stack([np.asarray(res.results[c]["y0"], dtype=np.float32) for c in range(8)], axis=0)
    y_sample = np.stack([np.asarray(res.results[c]["y1"], dtype=np.float32) for c in range(2)], axis=0)
    return (y_prompt, y_sample)
```

```python
import numpy as np
import ml_dtypes
from contextlib import ExitStack
import concourse.bass as bass
import concourse.mybir as mybir
from concourse.bass_utils import run_bass_kernel_spmd

F32 = mybir.dt.float32
BF16 = mybir.dt.bfloat16
ALU = mybir.AluOpType
AF = mybir.ActivationFunctionType

D = 1024
BW = 512
D_IN = 5680
EPS = 1e-6
O_LRU_X, O_LRU_G = 0, 512
O_GLA_Q, O_GLA_K, O_GLA_V, O_GLA_G, O_GLA_LR = 1024, 1280, 1536, 2048, 2560
O_DN_QKV, O_DN_G, O_DN_BA = 2592, 4128, 4640
O_S5_U, O_S5_G = 4656, 5168


SAME_ENGINE_SYNC = True
STORES_ON_POOL = True


class Buf:
    __slots__ = ("ap", "w", "r", "name")

    def __init__(self, ap, name=""):
        self.ap = ap
        self.w = []
        self.r = []
        self.name = name

    def __getitem__(self, k):
        return self.ap[k]


class Prog:
    def __init__(self, nc, n_dma_sems=40):
        self.nc = nc
        self.eng = {"pe": nc.tensor, "act": nc.scalar, "dve": nc.vector, "pool": nc.gpsimd, "sp": nc.sync}
        self.sem = {k: nc.alloc_semaphore("s_" + k) for k in self.eng}
        self.cnt = {k: 0 for k in self.eng}
        self.seen = {k: {} for k in self.eng}
        self.dsem = [nc.alloc_semaphore("d%d" % i) for i in range(n_dma_sems)]
        self.dcnt = [0] * n_dma_sems
        self.dnext = 0
        self.ninst = 0

    def buf(self, ap, name=""):
        return Buf(ap, name)

    def uname(self, name):
        self.uid = getattr(self, "uid", 0) + 1
        return "%s_%d" % (name, self.uid)

    def _wait(self, e, dep):
        if dep[0] == "dma":
            key = ("dma", dep[1]); val = dep[2]
            if self.seen[e].get(key, 0) >= val:
                return
            self.eng[e].wait_ge(self.dsem[dep[1]], val)
        else:
            f, val = dep
            if f == e and (e in ("pe", "sp") or not SAME_ENGINE_SYNC):
                return
            key = f
            if self.seen[e].get(key, 0) >= val:
                return
            self.eng[e].wait_ge(self.sem[f], val)
        self.seen[e][key] = val
        self.ninst += 1

    def _deps(self, e, reads, writes):
        for b in reads:
            for d in b.w:
                self._wait(e, d)
        for b in writes:
            for d in b.w:
                self._wait(e, d)
            for d in b.r:
                self._wait(e, d)

    def op(self, e, inst_fn, reads=(), writes=()):
        self._deps(e, reads, writes)
        inst = inst_fn(self.eng[e])
        inst.then_inc(self.sem[e], 1)
        self.cnt[e] += 1
        me = (e, self.cnt[e])
        for b in reads:
            b.r.append(me)
            if len(b.r) > 24:
                b.r = b.r[-24:] if False else self._compress(b.r)
        for b in writes:
            b.w = [me]
            b.r = []
        self.ninst += 1
        return inst

    @staticmethod
    def _compress(lst):
        best = {}
        for d in lst:
            k = ("dma", d[1]) if d[0] == "dma" else d[0]
            v = d[2] if d[0] == "dma" else d[1]
            if k not in best or v > best[k][0]:
                best[k] = (v, d)
        return [x[1] for x in best.values()]

    def dma(self, out, in_, reads=(), writes=(), q=None, **kw):
        if q is None:
            q = "sp" if (len(writes) > 0 or not STORES_ON_POOL) else "pool"
        self._deps(q, reads, writes)
        j = self.dnext
        self.dnext = (self.dnext + 1) % len(self.dsem)
        if self.dcnt[j] > 0:
            self._wait(q, ("dma", j, self.dcnt[j]))
        self.dcnt[j] += 16
        self.eng[q].dma_start(out=out, in_=in_, **kw).then_inc(self.dsem[j], 16)
        me = ("dma", j, self.dcnt[j])
        for b in reads:
            b.r.append(me)
            if len(b.r) > 24:
                b.r = self._compress(b.r)
        for b in writes:
            b.w = [me]
            b.r = []
        self.ninst += 1

    def barrier(self):
        for e in self.eng:
            for f in self.eng:
                if f != e and self.cnt[f] > 0:
                    self._wait(e, (f, self.cnt[f]))
            for j, c in enumerate(self.dcnt):
                if c > 0:
                    self._wait(e, ("dma", j, c))


class Ctx:
    pass


def mk_psum(pg, nc):
    banks = []
    for i in range(6):
        banks.append(pg.buf(nc.alloc_psum_tensor("psf%d" % i, [128, 512], F32).ap(), "psf%d" % i))
    bb = []
    for i in range(2):
        bb.append(pg.buf(nc.alloc_psum_tensor("psb%d" % i, [128, 1024], BF16).ap(), "psb%d" % i))
    return banks, bb


class Rot:
    def __init__(self, items):
        self.items = items
        self.i = 0

    def get(self):
        x = self.items[self.i]
        self.i = (self.i + 1) % len(self.items)
        return x


PF_LRU_X, PF_LRU_G, PF_GLA_Q, PF_GLA_K, PF_DN_QKV, PF_S5_U, PF_S5_G, PF_GLA_LR = 0, 512, 1024, 1280, 1536, 3072, 3584, 4096
PF_ROWS = 4128
PF_CHUNKS = ([(O_LRU_X + 128 * i, 128) for i in range(4)] + [(O_LRU_G + 128 * i, 128) for i in range(4)]
             + [(O_GLA_Q + 128 * i, 128) for i in range(2)] + [(O_GLA_K + 128 * i, 128) for i in range(2)]
             + [(O_DN_QKV + 128 * i, 128) for i in range(12)] + [(O_S5_U + 128 * i, 128) for i in range(4)]
             + [(O_S5_G + 128 * i, 128) for i in range(4)] + [(O_GLA_LR, 32)])
PT_GLA_K, PT_GLA_V, PT_GLA_G, PT_DN_G, PT_DN_BA = 0, 256, 768, 1280, 1792
PT_COLS = 1808
PT_GROUPS = [(1280, 512, 0), (1792, 512, 512), (2304, 256, 1024), (4128, 512, 1280), (4640, 16, 1792)]


def load_cast_bf16(pg, nc, es, dst, src_ap, rows, cols, name, chunk=2048):
    st = [pg.buf(es.enter_context(nc.sbuf_tensor(name + "_st%d" % i, [128, chunk], F32)).ap()) for i in range(2)]
    i = 0
    for c0 in range(0, cols, chunk):
        cw = min(chunk, cols - c0)
        s = st[i % 2]
        pg.dma(s.ap[:rows, :cw], src_ap[:, c0:c0 + cw], writes=[s])
        if i % 2 == 0:
            pg.op("act", lambda e: e.copy(dst[0][:rows, c0:c0 + cw], s.ap[:rows, :cw]), reads=[s], writes=[dst[1]])
        else:
            pg.op("dve", lambda e: e.tensor_copy(out=dst[0][:rows, c0:c0 + cw], in_=s.ap[:rows, :cw]), reads=[s], writes=[dst[1]])
        i += 1


def phase_P(pg, cx, es, L, x_ap, l):
    nc = cx.nc
    TT = 512
    sb = lambda name, shape, dt: pg.buf(es.enter_context(nc.sbuf_tensor(pg.uname(name), shape, dt)).ap(), name)
    wbf = sb("P_w", [128, 8, D_IN], BF16)
    w_src = cx.w["w_in"][l].rearrange("(k p) c -> p k c", p=128)
    WC = D_IN // 4
    st = [sb("P_wst%d" % i, [128, WC], F32) for i in range(2)]
    for k in range(8):
        for q in range(4):
            s = st[q % 2]
            pg.dma(s.ap, w_src[:, k, q * WC:(q + 1) * WC], writes=[s])
            if q % 2 == 0:
                pg.op("act", lambda e: e.copy(wbf.ap[:, k, q * WC:(q + 1) * WC], s.ap), reads=[s], writes=[wbf])
            else:
                pg.op("dve", lambda e: e.tensor_copy(out=wbf.ap[:, k, q * WC:(q + 1) * WC], in_=s.ap), reads=[s], writes=[wbf])
    gk = sb("P_g", [128, 8], F32)
    load_T(pg, cx, gk, gk.ap, cx.w["norm_g"][l].rearrange("(k p) -> k p", p=128), 8)
    xt = [sb("P_x%d" % i, [128, 4, D], F32) for i in range(2)]
    xs = sb("P_xs", [128, D], BF16)
    junk = sb("P_junk", [128, D], BF16)
    ss = sb("P_ss", [128, 4], F32)
    xnT = [sb("P_xnT%d" % i, [128, 8, TT], BF16) for i in range(2)]
    stf = [sb("P_stf%d" % i, [128, 4, TT], F32) for i in range(2)]
    stt = [sb("P_stt%d" % i, [128, PT_COLS], F32) for i in range(1)]
    XNTv = cx.XNT.rearrange("(k p) t -> p k t", p=128)
    xv = x_ap.rearrange("(n j p) d -> n p j d", p=128, j=4)
    gb = gk.ap.unsqueeze(2).to_broadcast([128, 8, 128])
    nt = L // TT
    evac_i = 0
    for it in range(nt):
        x_b = xt[it % 2]
        pg.dma(x_b.ap, xv[it], writes=[x_b])
        xn = xnT[it % 2]
        for j in range(4):
            pg.op("act", lambda e: e.activation(out=junk.ap, in_=x_b.ap[:, j, :], func=AF.Square, accum_out=ss.ap[:, j:j + 1]),
                  reads=[x_b], writes=[junk, ss])
            pg.op("act", lambda e: e.activation(out=ss.ap[:, j:j + 1], in_=ss.ap[:, j:j + 1], func=AF.Sqrt, scale=1.0 / D, bias=cx.eps.ap[:, 0:1]),
                  reads=[ss, cx.eps], writes=[ss])
            pg.op("dve", lambda e: e.reciprocal(out=ss.ap[:, j:j + 1], in_=ss.ap[:, j:j + 1]), reads=[ss], writes=[ss])
            pg.op("dve", lambda e: e.tensor_scalar(out=xs.ap, in0=x_b.ap[:, j, :], scalar1=ss.ap[:, j:j + 1], scalar2=None, op0=ALU.mult),
                  reads=[x_b, ss], writes=[xs])
            pb = cx.psb.get()
            for k in range(8):
                pg.op("pe", lambda e: e.transpose(out=pb.ap[:, k * 128:(k + 1) * 128], in_=xs.ap[:, k * 128:(k + 1) * 128], identity=cx.identb.ap),
                      reads=[xs, cx.identb], writes=[pb])
            pg.op("dve", lambda e: e.tensor_tensor(out=xn.ap[:, :, j * 128:(j + 1) * 128], in0=pb.ap.rearrange("p (k t) -> p k t", k=8), in1=gb, op=ALU.mult),
                  reads=[pb, gk], writes=[xn])
        pg.dma(XNTv[:, :, it * TT:(it + 1) * TT], xn.ap, reads=[xn])
        for ci, (c0, cw) in enumerate(PF_CHUNKS):
            ps = cx.psf.get()
            for k in range(8):
                pg.op("pe", lambda e: e.matmul(ps.ap[:cw, :], lhsT=wbf.ap[:, k, c0:c0 + cw], rhs=xn.ap[:, k, :], start=(k == 0), stop=(k == 7)),
                      reads=[wbf, xn], writes=[ps])
            sbuf = stf[(ci // 4) % 2]
            evac_i += 1
            if evac_i % 2 == 0:
                pg.op("act", lambda e: e.copy(sbuf.ap[:cw, ci % 4, :], ps.ap[:cw, :]), reads=[ps], writes=[sbuf])
            else:
                pg.op("dve", lambda e: e.tensor_copy(out=sbuf.ap[:cw, ci % 4, :], in_=ps.ap[:cw, :]), reads=[ps], writes=[sbuf])
            if ci % 4 == 3:
                cb = ci // 4
                pg.dma(PFv_slice(cx, cb * 4, 4, it * TT, TT), sbuf.ap, reads=[sbuf])
            elif ci == len(PF_CHUNKS) - 1:
                pg.dma(cx.PF[4096:4128, it * TT:(it + 1) * TT], sbuf.ap[:32, 0, :], reads=[sbuf])
        for j in range(4):
            sbuf = stt[0]
            for (c0, cw, o0) in PT_GROUPS:
                ps = cx.psf.get()
                for k in range(8):
                    pg.op("pe", lambda e: e.matmul(ps.ap[:, :cw], lhsT=xn.ap[:, k, j * 128:(j + 1) * 128], rhs=wbf.ap[:, k, c0:c0 + cw], start=(k == 0), stop=(k == 7)),
                          reads=[wbf, xn], writes=[ps])
                evac_i += 1
                if evac_i % 2 == 0:
                    pg.op("act", lambda e: e.copy(sbuf.ap[:, o0:o0 + cw], ps.ap[:, :cw]), reads=[ps], writes=[sbuf])
                else:
                    pg.op("dve", lambda e: e.tensor_copy(out=sbuf.ap[:, o0:o0 + cw], in_=ps.ap[:, :cw]), reads=[ps], writes=[sbuf])
            t0 = it * TT + j * 128
            pg.dma(cx.PT[t0:t0 + 128, :], sbuf.ap, reads=[sbuf])


def PFv_slice(cx, c0, nch, t0, tw):
    return cx.PF[c0 * 128:(c0 + nch) * 128, t0:t0 + tw].rearrange("(c p) t -> p c t", p=128)


def load_T(pg, cx, dst, dst_ap, src_ap, n, st_view=None, wd=128, **kw):
    st = cx.ldst
    pg.dma(st.ap[:n, :wd] if st_view is None else st_view(st.ap[:n, :wd]), src_ap, writes=[st], **kw)
    ps = cx.psf.get()
    pg.op("pe", lambda e: e.transpose(out=ps.ap[:wd, :n], in_=st.ap[:n, :wd], identity=cx.identf.ap[:n, :n]), reads=[st, cx.identf], writes=[ps])
    pg.op("dve", lambda e: e.tensor_copy(out=dst_ap, in_=ps.ap[:wd, :n]), reads=[ps], writes=[dst])


def phase_LRU(pg, cx, es, L, l):
    nc = cx.nc
    sb = lambda name, shape, dt: pg.buf(es.enter_context(nc.sbuf_tensor(pg.uname(name), shape, dt)).ap(), name)
    w = cx.w
    TL = min(2048, L)
    ntile = L // TL
    cw = sb("L_cw", [128, 4, 4], F32)
    load_T(pg, cx, cw, cw.ap.rearrange("p j c -> p (j c)"), w["lru_conv_w"][l].rearrange("j (c p) -> (j c) p", p=128), 16)
    cb = sb("L_cb", [128, 4], F32)
    load_T(pg, cx, cb, cb.ap, w["lru_conv_b"][l].rearrange("(c p) -> c p", p=128), 4)
    bias = sb("L_bias", [128, 2, 2, 4], F32)
    load_T(pg, cx, bias, bias.ap[:, 0].rearrange("p d c -> p (d c)"), w["lru_b_a"][l].rearrange("d (c p) -> (d c) p", p=128), 8)
    load_T(pg, cx, bias, bias.ap[:, 1].rearrange("p d c -> p (d c)"), w["lru_b_x"][l].rearrange("d (c p) -> (d c) p", p=128), 8)
    lam = sb("L_lam", [128, 2, 4], F32)
    load_T(pg, cx, lam, lam.ap.rearrange("p d c -> p (d c)"), w["lru_lambda"][l].rearrange("d (c p) -> (d c) p", p=128), 8)
    coef = sb("L_coef", [128, 2, 4], F32)
    coef2 = sb("L_coef2", [128, 2, 4], F32)
    pg.op("act", lambda e: e.activation(out=coef.ap, in_=lam.ap, func=AF.Exp, scale=-1.0), reads=[lam], writes=[coef])
    pg.op("act", lambda e: e.activation(out=coef.ap, in_=coef.ap, func=AF.Ln, bias=cx.one.ap[:, 0:1]), reads=[coef, cx.one], writes=[coef])
    pg.op("dve", lambda e: e.tensor_scalar(out=coef2.ap, in0=coef.ap, scalar1=-16.0, scalar2=None, op0=ALU.mult), reads=[coef], writes=[coef2])
    pg.op("dve", lambda e: e.tensor_scalar(out=coef.ap, in0=coef.ap, scalar1=-8.0, scalar2=None, op0=ALU.mult), reads=[coef], writes=[coef])
    wg = sb("L_wg", [128, 2, 2, 4, 128], BF16)
    wst = sb("L_wst", [128, 2, 4, 128], F32)
    for ai, nm in enumerate(("lru_w_a", "lru_w_x")):
        pg.dma(wst.ap, w[nm][l].rearrange("d h i j -> i d h j"), writes=[wst])
        pg.op("dve", lambda e: e.tensor_copy(out=wg.ap[:, ai], in_=wst.ap), reads=[wst], writes=[wg])
    XC = sb("L_XC", [128, L], F32)
    XCB = sb("L_XCB", [128, L], BF16)
    HF = sb("L_HF", [128, L], F32)
    xin = sb("L_xin", [128, TL + 3], F32)
    rt = sb("L_r", [128, TL], F32)
    itl = sb("L_i", [128, TL], F32)
    at = sb("L_a", [128, TL], F32)
    t2 = sb("L_t2", [128, TL], F32)
    gt = sb("L_g", [128, TL], F32)
    yb = sb("L_y", [128, TL], BF16)
    carry = sb("L_carry", [128, 1], F32)
    for c in range(4):
        prow = PF_LRU_X + c * 128
        for it in range(ntile):
            t0 = it * TL
            lo = max(t0 - 2, 0)
            hi = min(t0 + TL + 1, L)
            if it == 0 or it == ntile - 1:
                pg.op("pool", lambda e: e.memset(xin.ap, 0.0), writes=[xin])
            pg.dma(xin.ap[:, lo - (t0 - 2):hi - (t0 - 2)], cx.PF[prow:prow + 128, lo:hi], writes=[xin])
            xo = XC.ap[:, t0:t0 + TL]
            pg.op("dve", lambda e: e.tensor_scalar(out=xo, in0=xin.ap[:, 0:TL], scalar1=cw.ap[:, 0, c:c + 1], scalar2=cb.ap[:, c:c + 1], op0=ALU.mult, op1=ALU.add),
                  reads=[xin, cw, cb], writes=[XC])
            for j in range(1, 4):
                pg.op("dve", lambda e: e.scalar_tensor_tensor(out=xo, in0=xin.ap[:, j:j + TL], scalar=cw.ap[:, j, c:c + 1], in1=xo, op0=ALU.mult, op1=ALU.add),
                      reads=[xin, cw, XC], writes=[XC])
            pg.op("act", lambda e: e.copy(XCB.ap[:, t0:t0 + TL], xo), reads=[XC], writes=[XCB])
        for d in range(2):
            order = range(ntile) if d == 0 else range(ntile - 1, -1, -1)
            for n_i, it in enumerate(order):
                t0 = it * TL
                for s0 in range(0, TL, 512):
                    for ai, dst in ((0, rt), (1, itl)):
                        ps = cx.psf.get()
                        pg.op("pe", lambda e: e.matmul(ps.ap, lhsT=wg.ap[:, ai, d, c, :], rhs=XCB.ap[:, t0 + s0:t0 + s0 + 512], start=True, stop=True),
                              reads=[wg, XCB], writes=[ps])
                        pg.op("act", lambda e: e.activation(out=dst.ap[:, s0:s0 + 512], in_=ps.ap, func=AF.Sigmoid, bias=bias.ap[:, ai, d, c:c + 1]),
                              reads=[ps, bias], writes=[dst])
                pg.op("act", lambda e: e.activation(out=at.ap, in_=rt.ap, func=AF.Exp, scale=coef.ap[:, d, c:c + 1]), reads=[rt, coef], writes=[at])
                pg.op("act", lambda e: e.activation(out=t2.ap, in_=rt.ap, func=AF.Exp, scale=coef2.ap[:, d, c:c + 1]), reads=[rt, coef2], writes=[t2])
                pg.op("act", lambda e: e.activation(out=t2.ap, in_=t2.ap, func=AF.Sqrt, scale=-1.0, bias=cx.one.ap[:, 0:1]), reads=[t2, cx.one], writes=[t2])
                pg.op("pool", lambda e: e.tensor_tensor(out=itl.ap, in0=itl.ap, in1=XC.ap[:, t0:t0 + TL], op=ALU.mult), reads=[itl, XC], writes=[itl])
                pg.op("dve", lambda e: e.tensor_tensor(out=t2.ap, in0=t2.ap, in1=itl.ap, op=ALU.mult), reads=[t2, itl], writes=[t2])
                init = 0.0 if n_i == 0 else carry.ap[:, 0:1]
                rds = [at, t2] + ([] if n_i == 0 else [carry])
                if d == 0:
                    ho = HF.ap[:, t0:t0 + TL]
                    pg.op("dve", lambda e: e.tensor_tensor_scan(out=ho, data0=at.ap, data1=t2.ap, initial=init, op0=ALU.mult, op1=ALU.add),
                          reads=rds, writes=[HF])
                    pg.op("dve", lambda e: e.tensor_copy(out=carry.ap, in_=HF.ap[:, t0 + TL - 1:t0 + TL]), reads=[HF], writes=[carry])
                else:
                    rv = lambda ap: bass.AP(ap.tensor, ap.offset + TL - 1, [list(ap.ap[0]), [-1, TL]])
                    pg.op("dve", lambda e: e.tensor_tensor_scan(out=rv(rt.ap), data0=rv(at.ap), data1=rv(t2.ap), initial=init, op0=ALU.mult, op1=ALU.add),
                          reads=rds, writes=[rt])
                    pg.op("dve", lambda e: e.tensor_copy(out=carry.ap, in_=rt.ap[:, 0:1]), reads=[rt], writes=[carry])
                    grow = PF_LRU_G + c * 128
                    pg.dma(gt.ap, cx.PF[grow:grow + 128, t0:t0 + TL], writes=[gt])
                    pg.op("act", lambda e: e.activation(out=gt.ap, in_=gt.ap, func=AF.Silu), reads=[gt], writes=[gt])
                    pg.op("pool", lambda e: e.tensor_tensor(out=rt.ap, in0=rt.ap, in1=HF.ap[:, t0:t0 + TL], op=ALU.add), reads=[rt, HF], writes=[rt])
                    pg.op("dve", lambda e: e.tensor_tensor(out=yb.ap, in0=rt.ap, in1=gt.ap, op=ALU.mult), reads=[rt, gt], writes=[yb])
                    pg.dma(cx.BT[c * 128:(c + 1) * 128, t0:t0 + TL], yb.ap, reads=[yb])


def phase_M(pg, cx, es, L, x_ap, xout_ap, l, last):
    nc = cx.nc
    TT = 512
    sb = lambda name, shape, dt: pg.buf(es.enter_context(nc.sbuf_tensor(pg.uname(name), shape, dt)).ap(), name)
    w = cx.w
    wmg = sb("M_wmg", [128, 4, 8, D], BF16)
    wbr = sb("M_wbr", [128, 4, 4, D], BF16)
    wout = sb("M_wout", [128, 8, D], BF16)
    st = [sb("M_st%d" % i, [128, D], F32) for i in range(2)]
    jobs = []
    for n in range(4):
        for k in range(8):
            jobs.append((w["w_merge_gate"][l, n, k * 128:(k + 1) * 128, :], wmg, wmg.ap[:, n, k, :]))
        for k in range(4):
            jobs.append((w["w_branch"][l, n, k * 128:(k + 1) * 128, :], wbr, wbr.ap[:, n, k, :]))
    for k in range(8):
        jobs.append((w["w_out"][l, k * 128:(k + 1) * 128, :], wout, wout.ap[:, k, :]))
    for i, (src, dbuf, dap) in enumerate(jobs):
        s = st[i % 2]
        pg.dma(s.ap, src, writes=[s])
        if i % 2 == 0:
            pg.op("act", lambda e: e.copy(dap, s.ap), reads=[s], writes=[dbuf])
        else:
            pg.op("dve", lambda e: e.tensor_copy(out=dap, in_=s.ap), reads=[s], writes=[dbuf])
    bmg = sb("M_bmg", [128, 4, 8], F32)
    load_T(pg, cx, bmg, bmg.ap.rearrange("p n c -> p (n c)"), w["b_merge_gate"][l].rearrange("n (c p) -> (n c) p", p=128), 32)
    if last:
        fg = sb("M_fg", [128, D], F32)
        fsrc = w["final_norm_g"]
        pg.dma(fg.ap, bass.AP(fsrc.tensor, fsrc.offset, [[0, 128], [1, D]]), writes=[fg])
        ss = sb("M_ss", [128, 4], F32)
        junk = sb("M_junk", [128, D], BF16)
    xn = sb("M_xn", [128, 8, TT], BF16)
    bt = sb("M_bt", [128, 16, TT], BF16)
    xt = sb("M_x", [128, 4, D], F32)
    mg = sb("M_mg", [128, 8, TT], BF16)
    gsb = sb("M_g", [128, TT], F32)
    tmp = sb("M_tmp", [128, TT], F32)
    acc = sb("M_acc", [128, TT], F32)
    XNTv = cx.XNT.rearrange("(k p) t -> p k t", p=128)
    BTv = cx.BT.rearrange("(k p) t -> p k t", p=128)
    xv = x_ap.rearrange("(n j p) d -> n p j d", p=128, j=4)
    ov = xout_ap.rearrange("(n j p) d -> n p j d", p=128, j=4)
    for it in range(L // TT):
        ts = slice(it * TT, (it + 1) * TT)
        pg.dma(xn.ap, XNTv[:, :, ts], writes=[xn])
        pg.dma(bt.ap, BTv[:, :, ts], writes=[bt])
        pg.dma(xt.ap, xv[it], writes=[xt])
        for oc in range(8):
            ocs = slice(oc * 128, (oc + 1) * 128)
            for n in range(4):
                pg_ = cx.psf.get()
                for k in range(8):
                    pg.op("pe", lambda e: e.matmul(pg_.ap, lhsT=wmg.ap[:, n, k, ocs], rhs=xn.ap[:, k, :], start=(k == 0), stop=(k == 7)),
                          reads=[wmg, xn], writes=[pg_])
                pg.op("act", lambda e: e.activation(out=gsb.ap, in_=pg_.ap, func=AF.Sigmoid, bias=bmg.ap[:, n, oc:oc + 1]), reads=[pg_, bmg], writes=[gsb])
                pb = cx.psf.get()
                for k in range(4):
                    pg.op("pe", lambda e: e.matmul(pb.ap, lhsT=wbr.ap[:, n, k, ocs], rhs=bt.ap[:, n * 4 + k, :], start=(k == 0), stop=(k == 3)),
                          reads=[wbr, bt], writes=[pb])
                if n == 0:
                    pg.op("dve", lambda e: e.tensor_tensor(out=acc.ap, in0=pb.ap, in1=gsb.ap, op=ALU.mult), reads=[pb, gsb], writes=[acc])
                else:
                    pg.op("dve", lambda e: e.tensor_tensor(out=tmp.ap, in0=pb.ap, in1=gsb.ap, op=ALU.mult), reads=[pb, gsb], writes=[tmp])
                    if n < 3:
                        pg.op("pool", lambda e: e.tensor_tensor(out=acc.ap, in0=acc.ap, in1=tmp.ap, op=ALU.add), reads=[acc, tmp], writes=[acc])
                    else:
                        pg.op("pool", lambda e: e.tensor_tensor(out=mg.ap[:, oc, :], in0=acc.ap, in1=tmp.ap, op=ALU.add), reads=[acc, tmp], writes=[mg])
        for j in range(4):
            for hf in range(2):
                hs = slice(hf * 512, (hf + 1) * 512)
                ps = cx.psf.get()
                for k in range(8):
                    pg.op("pe", lambda e: e.matmul(ps.ap, lhsT=mg.ap[:, k, j * 128:(j + 1) * 128], rhs=wout.ap[:, k, hs], start=(k == 0), stop=(k == 7)),
                          reads=[mg, wout], writes=[ps])
                pg.op("dve", lambda e: e.tensor_tensor(out=xt.ap[:, j, hs], in0=ps.ap, in1=xt.ap[:, j, hs], op=ALU.add), reads=[ps, xt], writes=[xt])
            if last:
                pg.op("act", lambda e: e.activation(out=junk.ap, in_=xt.ap[:, j, :], func=AF.Square, accum_out=ss.ap[:, j:j + 1]), reads=[xt], writes=[junk, ss])
                pg.op("act", lambda e: e.activation(out=ss.ap[:, j:j + 1], in_=ss.ap[:, j:j + 1], func=AF.Sqrt, scale=1.0 / D, bias=cx.eps.ap[:, 0:1]),
                      reads=[ss, cx.eps], writes=[ss])
                pg.op("dve", lambda e: e.reciprocal(out=ss.ap[:, j:j + 1], in_=ss.ap[:, j:j + 1]), reads=[ss], writes=[ss])
                pg.op("dve", lambda e: e.scalar_tensor_tensor(out=xt.ap[:, j, :], in0=xt.ap[:, j, :], scalar=ss.ap[:, j:j + 1], in1=fg.ap, op0=ALU.mult, op1=ALU.mult),
                      reads=[xt, ss, fg], writes=[xt])
        pg.dma(ov[it], xt.ap, reads=[xt])


def phase_GLA(pg, cx, es, L, l):
    nc = cx.nc
    sb = lambda name, shape, dt: pg.buf(es.enter_context(nc.sbuf_tensor(pg.uname(name), shape, dt)).ap(), name)
    w = cx.w
    NB = L // 128
    wup = sb("G_wup", [32, 2, 256], F32)
    for d in range(2):
        pg.dma(wup.ap[0:16, d, :], w["gla_w_up"][l, d], writes=[wup])
        pg.dma(wup.ap[16:17, d, :], w["gla_b_up"][l, d:d + 1, :], writes=[wup])
    gn = sb("G_gn", [128, 128], F32)
    gsrc = w["gla_norm_g"][l]
    pg.dma(gn.ap, bass.AP(gsrc.tensor, gsrc.offset, [[0, 128], [1, 128]]), writes=[gn])
    lrT = [sb("G_lrT%d" % i, [32, 128], F32) for i in range(2)]
    for b in lrT:
        pg.op("dve", lambda e: e.memset(b.ap, 1.0), writes=[b])
    qk = [sb("G_qk%d" % i, [128, 4, 128], F32) for i in range(2)]
    tk = [sb("G_tk%d" % i, [128, 1280], F32) for i in range(2)]
    obt = [sb("G_ob%d" % i, [128, 512], F32) for i in range(2)]
    la = sb("G_la", [128, 256], F32)
    e1 = sb("G_e1", [128, 256], F32)
    eb = sb("G_eb", [128, 2, 128], F32)
    enb = sb("G_enb", [128, 2, 128], F32)
    qd = sb("G_qd", [128, 2, 128], BF16)
    ki = sb("G_ki", [128, 2, 128], BF16)
    ed = sb("G_ed", [128, 256], F32)
    kend = sb("G_kend", [128, 256], BF16)
    vb = sb("G_vb", [128, 512], BF16)
    sm = [sb("G_sm%d" % i, [128, 128], BF16) for i in range(4)]
    pre_ps = Rot([cx.psf.items[4], cx.pso])
    S32 = [sb("G_S32_%d" % h, [128, 128], F32) for h in range(4)]
    Sb = [sb("G_Sb_%d" % h, [128, 128], BF16) for h in range(4)]
    osb = sb("G_osb", [128, 512], F32)
    ssq = sb("G_ssq", [128, 4], F32)
    junk = sb("G_junk", [128, 128], BF16)
    ysb = sb("G_ysb", [128, 512], BF16)
    yT = sb("G_yT", [128, 4, 128], BF16)
    PFq = cx.PF[PF_GLA_Q:PF_GLA_Q + 512, :].rearrange("(c p) t -> p c t", p=128)
    for d in (1, 0):
        pg.barrier()
        for h in range(4):
            pg.op("dve", lambda e: e.memset(S32[h].ap, 0.0), writes=[S32[h]])
            pg.op("pool", lambda e: e.memset(Sb[h].ap, 0.0), writes=[Sb[h]])
        order = range(NB) if d == 0 else range(NB - 1, -1, -1)
        for bi, blk in enumerate(order):
            t0 = blk * 128
            ts = slice(t0, t0 + 128)
            qkb = qk[bi % 2]; tkb = tk[bi % 2]; lrb = lrT[bi % 2]; ob = obt[bi % 2]
            pg.dma(qkb.ap, PFq[:, :, ts], writes=[qkb])
            pg.dma(tkb.ap, cx.PT[ts, 0:1280], writes=[tkb])
            pg.dma(lrb.ap[0:16, :], cx.PF[PF_GLA_LR + 16 * d:PF_GLA_LR + 16 * d + 16, ts], writes=[lrb])
            if d == 0:
                pg.dma(ob.ap, cx.OB[ts, 0:512], writes=[ob])
            zp = pre_ps.get()
            pg.op("pe", lambda e: e.matmul(zp.ap[:, :256], lhsT=lrb.ap[0:17, :], rhs=wup.ap[0:17, d, :], start=True, stop=True), reads=[lrb, wup], writes=[zp])
            pg.op("act", lambda e: e.activation(out=e1.ap, in_=zp.ap[:, :256], func=AF.Exp, scale=-1.0), reads=[zp], writes=[e1])
            pg.op("act", lambda e: e.activation(out=e1.ap, in_=e1.ap, func=AF.Ln, bias=cx.one.ap[:, 0:1]), reads=[e1, cx.one], writes=[e1])
            pg.op("dve", lambda e: e.tensor_scalar(out=la.ap, in0=e1.ap, scalar1=-1.0 / 16.0, scalar2=None, op0=ALU.mult), reads=[e1], writes=[la])
            bp = pre_ps.get()
            for h2 in range(2):
                pg.op("pe", lambda e: e.matmul(bp.ap[:, h2 * 128:(h2 + 1) * 128], lhsT=la.ap[:, h2 * 128:(h2 + 1) * 128], rhs=cx.m_incl.ap[:, d, :], start=True, stop=True),
                      reads=[la, cx.m_incl], writes=[bp])
            bp3 = bp.ap[:, 0:256].rearrange("p (c t) -> p c t", c=2)
            pg.op("act", lambda e: e.activation(out=eb.ap, in_=bp3, func=AF.Exp), reads=[bp], writes=[eb])
            pg.op("act", lambda e: e.activation(out=enb.ap, in_=bp3, func=AF.Exp, scale=-1.0), reads=[bp], writes=[enb])
            pg.op("dve", lambda e: e.scalar_tensor_tensor(out=qd.ap, in0=qkb.ap[:, 0:2, :], scalar=0.125, in1=eb.ap, op0=ALU.mult, op1=ALU.mult), reads=[qkb, eb], writes=[qd])
            pg.op("pool", lambda e: e.tensor_tensor(out=ki.ap, in0=qkb.ap[:, 2:4, :], in1=enb.ap, op=ALU.mult), reads=[qkb, enb], writes=[ki])
            dp = pre_ps.get()
            pg.op("pe", lambda e: e.matmul(dp.ap[:, :256], lhsT=cx.m_sa.ap[:, d, :], rhs=la.ap, start=True, stop=True), reads=[la, cx.m_sa], writes=[dp])
            pg.op("act", lambda e: e.activation(out=ed.ap, in_=dp.ap[:, :256], func=AF.Exp), reads=[dp], writes=[ed])
            pg.op("dve", lambda e: e.tensor_tensor(out=kend.ap, in0=tkb.ap[:, 0:256], in1=ed.ap, op=ALU.mult), reads=[tkb, ed], writes=[kend])
            pg.op("pool", lambda e: e.tensor_copy(out=vb.ap, in_=tkb.ap[:, 256:768]), reads=[tkb], writes=[vb])
            chunks = (0, 1) if d == 0 else (1, 0)

            def head_gen(h, d=d, chunks=chunks):
                h2, hp = h // 2, (h % 2) * 64
                hc = slice(h * 128, (h + 1) * 128)
                bank = cx.psf.items[h]
                o_ps = bank.ap[:, 384:512]
                pg.op("pe", lambda e: e.matmul(bank.ap[:, 0:128], lhsT=ki.ap[hp:hp + 64, h2, :], rhs=qd.ap[hp:hp + 64, h2, :], start=True, stop=True), reads=[ki, qd], writes=[bank])
                yield
                smb = sm[h]
                pg.op("dve", lambda e: e.tensor_tensor(out=smb.ap, in0=bank.ap[:, 0:128], in1=cx.m_incl.ap[:, d, :], op=ALU.mult), reads=[bank, cx.m_incl], writes=[smb])
                yield
                r0 = chunks[0] * 64
                pg.op("pe", lambda e: e.matmul(o_ps, lhsT=smb.ap, rhs=vb.ap[:, hc], start=True, stop=False), reads=[smb, vb], writes=[bank])
                pg.op("pe", lambda e: e.matmul(bank.ap[r0:r0 + 64, 384:512], lhsT=qd.ap[hp:hp + 64, h2, r0:r0 + 64], rhs=Sb[h].ap[hp:hp + 64, :], start=False, stop=True),
                      reads=[qd, Sb[h]], writes=[bank])
                for ci, c in enumerate(chunks):
                    r0 = c * 64
                    if ci == 1:
                        pg.op("pe", lambda e: e.matmul(bank.ap[r0:r0 + 64, 256:384], lhsT=qd.ap[hp:hp + 64, h2, r0:r0 + 64], rhs=Sb[h].ap[hp:hp + 64, :], start=True, stop=True),
                              reads=[qd, Sb[h]], writes=[bank])
                    pg.op("pe", lambda e: e.matmul(bank.ap[hp:hp + 64, 128:256], lhsT=kend.ap[r0:r0 + 64, h * 64:(h + 1) * 64], rhs=vb.ap[r0:r0 + 64, hc], start=True, stop=True),
                          reads=[kend, vb], writes=[bank])
                    yield
                    col = r0 + 63 if d == 0 else r0
                    pg.op("dve", lambda e: e.scalar_tensor_tensor(out=S32[h].ap[hp:hp + 64, :], in0=S32[h].ap[hp:hp + 64, :], scalar=eb.ap[hp:hp + 64, h2, col:col + 1],
                                                                  in1=bank.ap[hp:hp + 64, 128:256], op0=ALU.mult, op1=ALU.add), reads=[S32[h], eb, bank], writes=[S32[h]])
                    yield
                    pg.op("act", lambda e: e.copy(Sb[h].ap[hp:hp + 64, :], S32[h].ap[hp:hp + 64, :]), reads=[S32[h]], writes=[Sb[h]])
                    yield
                r1 = chunks[1] * 64
                pg.op("dve", lambda e: e.tensor_copy(out=osb.ap[:, hc], in_=o_ps), reads=[bank], writes=[osb])
                pg.op("dve", lambda e: e.tensor_tensor(out=osb.ap[r1:r1 + 64, hc], in0=bank.ap[r1:r1 + 64, 256:384], in1=osb.ap[r1:r1 + 64, hc], op=ALU.add), reads=[bank, osb], writes=[osb])

            gens = [head_gen(h) for h in range(4)]
            while gens:
                for gnr in list(gens):
                    try:
                        next(gnr)
                    except StopIteration:
                        gens.remove(gnr)
            if d == 1:
                pg.dma(cx.OB[ts, 0:512], osb.ap, reads=[osb])
            else:
                pg.op("pool", lambda e: e.tensor_tensor(out=osb.ap, in0=osb.ap, in1=ob.ap, op=ALU.add), reads=[osb, ob], writes=[osb])
                head_norm_gate_store(pg, cx, osb, ssq, junk, gn, tkb, 768, ysb, yT, 512, ts)


def head_norm_gate_store(pg, cx, osb, ssq, junk, gn, tkb, gcol, ysb, yT, bt_row0, ts):
    for h in range(4):
        hc = slice(h * 128, (h + 1) * 128)
        pg.op("act", lambda e: e.activation(out=junk.ap, in_=osb.ap[:, hc], func=AF.Square, accum_out=ssq.ap[:, h:h + 1]), reads=[osb], writes=[junk, ssq])
    pg.op("act", lambda e: e.activation(out=ssq.ap, in_=ssq.ap, func=AF.Sqrt, scale=1.0 / 128.0, bias=cx.eps.ap[:, 0:1]), reads=[ssq, cx.eps], writes=[ssq])
    pg.op("dve", lambda e: e.reciprocal(out=ssq.ap, in_=ssq.ap), reads=[ssq], writes=[ssq])
    for h in range(4):
        hc = slice(h * 128, (h + 1) * 128)
        pg.op("dve", lambda e: e.scalar_tensor_tensor(out=osb.ap[:, hc], in0=osb.ap[:, hc], scalar=ssq.ap[:, h:h + 1], in1=gn.ap, op0=ALU.mult, op1=ALU.mult),
              reads=[osb, ssq, gn], writes=[osb])
    pg.op("act", lambda e: e.activation(out=tkb.ap[:, gcol:gcol + 512], in_=tkb.ap[:, gcol:gcol + 512], func=AF.Silu), reads=[tkb], writes=[tkb])
    pg.op("dve", lambda e: e.tensor_tensor(out=ysb.ap, in0=osb.ap, in1=tkb.ap[:, gcol:gcol + 512], op=ALU.mult), reads=[osb, tkb], writes=[ysb])
    pb = cx.psb.get()
    for h in range(4):
        pg.op("pe", lambda e: e.transpose(out=pb.ap[:, h * 128:(h + 1) * 128], in_=ysb.ap[:, h * 128:(h + 1) * 128], identity=cx.identb.ap), reads=[ysb, cx.identb], writes=[pb])
    pg.op("act", lambda e: e.copy(yT.ap, pb.ap[:, 0:512].rearrange("p (c t) -> p c t", c=4)), reads=[pb], writes=[yT])
    pg.dma(cx.BT[bt_row0:bt_row0 + 512, ts].rearrange("(c p) t -> p c t", p=128), yT.ap, reads=[yT])


def phase_DN(pg, cx, es, L, l):
    nc = cx.nc
    w = cx.w
    NB = L // 128
    with ExitStack() as es0:
        sb = lambda name, shape, dt: pg.buf(es0.enter_context(nc.sbuf_tensor(pg.uname(name), shape, dt)).ap(), name)
        TL = 512
        cwD = sb("D0_cw", [128, 4, 12], F32)
        load_T(pg, cx, cwD, cwD.ap.rearrange("p j c -> p (j c)"), w["dn_conv_w"][l].rearrange("j (c p) -> (j c) p", p=128), 48)
        xin = [sb("D0_xin%d" % i, [128, TL + 3], F32) for i in range(2)]
        xc = sb("D0_xc", [128, TL], F32)
        sq = sb("D0_sq", [128, TL], F32)
        rs = sb("D0_rs", [128, TL], F32)
        fm = sb("D0_fm", [128, 12, TL], BF16)
        tm = sb("D0_tm", [128, 4, 1024], BF16)
        nt = L // TL
        for it in range(nt):
            t0 = it * TL
            lo, hi = max(t0 - 2, 0), min(t0 + TL + 1, L)
            for c in range(12):
                xb = xin[c % 2]
                if it == 0 or it == nt - 1:
                    pg.op("pool", lambda e: e.memset(xb.ap, 0.0), writes=[xb])
                prow = PF_DN_QKV + c * 128
                pg.dma(xb.ap[:, lo - (t0 - 2):hi - (t0 - 2)], cx.PF[prow:prow + 128, lo:hi], writes=[xb])
                pg.op("dve", lambda e: e.tensor_scalar(out=xc.ap, in0=xb.ap[:, 0:TL], scalar1=cwD.ap[:, 0, c:c + 1], scalar2=None, op0=ALU.mult), reads=[xb, cwD], writes=[xc])
                for j in range(1, 4):
                    pg.op("dve", lambda e: e.scalar_tensor_tensor(out=xc.ap, in0=xb.ap[:, j:j + TL], scalar=cwD.ap[:, j, c:c + 1], in1=xc.ap, op0=ALU.mult, op1=ALU.add),
                          reads=[xb, cwD, xc], writes=[xc])
                if c >= 8:
                    pg.op("act", lambda e: e.activation(out=fm.ap[:, c, :], in_=xc.ap, func=AF.Silu), reads=[xc], writes=[fm])
                else:
                    pg.op("act", lambda e: e.activation(out=xc.ap, in_=xc.ap, func=AF.Silu), reads=[xc], writes=[xc])
                    pg.op("pool", lambda e: e.tensor_tensor(out=sq.ap, in0=xc.ap, in1=xc.ap, op=ALU.mult), reads=[xc], writes=[sq])
                    ps = cx.psf.get()
                    pg.op("pe", lambda e: e.matmul(ps.ap, lhsT=cx.onesf.ap, rhs=sq.ap, start=True, stop=True), reads=[cx.onesf, sq], writes=[ps])
                    pg.op("act", lambda e: e.activation(out=rs.ap, in_=ps.ap, func=AF.Sqrt, bias=cx.eps.ap[:, 0:1]), reads=[ps, cx.eps], writes=[rs])
                    pg.op("dve", lambda e: e.reciprocal(out=rs.ap, in_=rs.ap), reads=[rs], writes=[rs])
                    sc = (128.0 ** -0.5) if c < 4 else 1.0
                    pg.op("dve", lambda e: e.scalar_tensor_tensor(out=fm.ap[:, c, :], in0=xc.ap, scalar=sc, in1=rs.ap, op0=ALU.mult, op1=ALU.mult), reads=[xc, rs], writes=[fm])
            pg.dma(cx.QKT[:, t0:t0 + TL].rearrange("(c p) t -> p c t", p=128), fm.ap[:, 0:8, :], reads=[fm])
            for j in range(4):
                pb = cx.psb.get()
                for c in range(8):
                    pg.op("pe", lambda e: e.transpose(out=pb.ap[:, c * 128:(c + 1) * 128], in_=fm.ap[:, 4 + c, j * 128:(j + 1) * 128], identity=cx.identb.ap),
                          reads=[fm, cx.identb], writes=[pb])
                pg.op("act", lambda e: e.copy(tm.ap[:, j, :], pb.ap), reads=[pb], writes=[tm])
            pg.dma(cx.KVT[t0:t0 + TL, :].rearrange("(j p) c -> p j c", p=128), tm.ap, reads=[tm])
        pg.barrier()
    sb = lambda name, shape, dt: pg.buf(es.enter_context(nc.sbuf_tensor(pg.uname(name), shape, dt)).ap(), name)
    ba = sb("D_ba", [128, NB, 16], F32)
    pg.dma(ba.ap, cx.PT[:, PT_DN_BA:PT_DN_BA + 16].rearrange("(n p) c -> p n c", p=128), writes=[ba])
    ba4 = ba.ap.rearrange("p n (d j h) -> p n d j h", d=2, j=2)
    dtb = sb("D_dtb", [128, 8], F32)
    nea = sb("D_nea", [128, 8], F32)
    s1 = w["dn_dt_bias"][l]
    pg.dma(dtb.ap, bass.AP(s1.tensor, s1.offset, [[0, 128], [1, 8]]), writes=[dtb])
    s2 = w["dn_a_log"][l]
    pg.dma(nea.ap, bass.AP(s2.tensor, s2.offset, [[0, 128], [1, 8]]), writes=[nea])
    pg.op("act", lambda e: e.activation(out=nea.ap, in_=nea.ap, func=AF.Exp), reads=[nea], writes=[nea])
    pg.op("dve", lambda e: e.tensor_scalar(out=nea.ap, in0=nea.ap, scalar1=-1.0, scalar2=None, op0=ALU.mult), reads=[nea], writes=[nea])
    beta = sb("D_beta", [128, NB, 2, 4], F32)
    nbeta = sb("D_nbeta", [128, NB, 2, 4], F32)
    g = sb("D_g", [128, NB, 2, 4], F32)
    pg.op("act", lambda e: e.activation(out=beta.ap, in_=ba4[:, :, :, 0, :], func=AF.Sigmoid), reads=[ba], writes=[beta])
    pg.op("dve", lambda e: e.tensor_scalar(out=nbeta.ap, in0=beta.ap, scalar1=-1.0, scalar2=None, op0=ALU.mult), reads=[beta], writes=[nbeta])
    dtb_b = dtb.ap.rearrange("p (d h) -> p d h", d=2).unsqueeze(1).to_broadcast([128, NB, 2, 4])
    nea_b = nea.ap.rearrange("p (d h) -> p d h", d=2).unsqueeze(1).to_broadcast([128, NB, 2, 4])
    pg.op("dve", lambda e: e.tensor_tensor(out=g.ap, in0=ba4[:, :, :, 1, :], in1=dtb_b, op=ALU.add), reads=[ba, dtb], writes=[g])
    pg.op("act", lambda e: e.activation(out=g.ap, in_=g.ap, func=AF.Exp), reads=[g], writes=[g])
    pg.op("act", lambda e: e.activation(out=g.ap, in_=g.ap, func=AF.Ln, bias=cx.one.ap[:, 0:1]), reads=[g, cx.one], writes=[g])
    pg.op("dve", lambda e: e.tensor_tensor(out=g.ap, in0=g.ap, in1=nea_b, op=ALU.mult), reads=[g, nea], writes=[g])
    eG = sb("D_eG", [128, NB, 2, 4], F32)
    eD = sb("D_eD", [128, NB, 2, 4], F32)
    bg = sb("D_bg", [128, NB, 2, 4], F32)
    deB = sb("D_deB", [128, 2, NB, 2, 4], F32)
    NQ = 32
    for d in range(2):
        for n0 in range(0, NB, NQ):
            nn = min(NQ, NB - n0)
            for (msk, dst, fn) in ((cx.m_incl.ap[:, d, :], eG, 0), (cx.m_sa.ap[:, d, :], eD, 0), (cx.chunkind.ap[:, 0, :], deB, 1), (cx.chunkind.ap[:, 1, :], deB, 2)):
                ps = cx.psf.get()
                pv = ps.ap[:, :nn * 4].rearrange("p (n h) -> p n h", h=4)
                pg.op("pe", lambda e: e.matmul(pv, lhsT=msk, rhs=g.ap[:, n0:n0 + nn, d, :], start=True, stop=True), reads=[g, cx.m_incl, cx.m_sa, cx.chunkind], writes=[ps])
                o_ap = dst.ap[:, n0:n0 + nn, d, :] if fn == 0 else dst.ap[:, fn - 1, n0:n0 + nn, d, :]
                pg.op("act", lambda e: e.activation(out=o_ap, in_=pv, func=AF.Exp), reads=[ps], writes=[dst])
    pg.op("dve", lambda e: e.tensor_tensor(out=bg.ap, in0=beta.ap, in1=eG.ap, op=ALU.mult), reads=[beta, eG], writes=[bg])
    gn = sb("D_gn", [128, 128], F32)
    gsrc = w["dn_norm_g"][l]
    pg.dma(gn.ap, bass.AP(gsrc.tensor, gsrc.offset, [[0, 128], [1, 128]]), writes=[gn])
    qk = [sb("D_qk%d" % i, [128, 8, 128], BF16) for i in range(2)]
    kv = [sb("D_kv%d" % i, [128, 2, 4, 128], BF16) for i in range(2)]
    gt = [sb("D_gt%d" % i, [128, 1280 + 512], F32) for i in range(1)]
    obt = [sb("D_ob%d" % i, [128, 512], F32) for i in range(2)]
    vb4 = sb("D_vb4", [128, 4, 128], BF16)
    kbg4 = sb("D_kbg4", [128, 4, 128], BF16)
    kend4 = sb("D_kend4", [128, 4, 128], BF16)
    gtri = [sb("D_gtri%d" % i, [128, 128], F32) for i in range(4)]
    gam = [sb("D_gam%d" % i, [128, 3, 128], F32) for i in range(4)]
    gamm = [sb("D_gamm%d" % i, [128, 2, 128], F32) for i in range(4)]
    qd = [sb("D_qd%d" % i, [128, 128], BF16) for i in range(4)]
    Cm = [sb("D_C%d" % i, [128, 128], BF16) for i in range(4)]
    attnT = [sb("D_at%d" % i, [128, 128], BF16) for i in range(4)]
    BC = [[sb("D_BC%d_%d" % (h, i), [128, 2, 128], BF16) for i in range(2)] for h in range(4)]
    Pm = [[sb("D_P%d_%d" % (h, i), [128, 128], BF16) for i in range(2)] for h in range(4)]
    Pm32 = [[sb("D_P32_%d_%d" % (h, i), [128, 128], F32) for i in range(2)] for h in range(4)]
    usb = [sb("D_u%d" % i, [128, 128], F32) for i in range(4)]
    wT = [sb("D_wT%d" % i, [128, 128], BF16) for i in range(4)]
    vn = [sb("D_vn%d" % i, [128, 128], BF16) for i in range(4)]
    S32 = [sb("D_S32_%d" % h, [128, 128], F32) for h in range(4)]
    Sb = [sb("D_Sb_%d" % h, [128, 128], BF16) for h in range(4)]
    osb = sb("D_osb", [128, 512], F32)
    ssq = sb("D_ssq", [128, 4], F32)
    junk = sb("D_junk", [128, 128], BF16)
    ysb = sb("D_ysb", [128, 512], BF16)
    yT = sb("D_yT", [128, 4, 128], BF16)
    QKv = cx.QKT.rearrange("(c p) t -> p c t", p=128)
    rr = [0]
    for d in (1, 0):
        pg.barrier()
        for h in range(4):
            pg.op("dve", lambda e: e.memset(S32[h].ap, 0.0), writes=[S32[h]])
            pg.op("pool", lambda e: e.memset(Sb[h].ap, 0.0), writes=[Sb[h]])
        order = range(NB) if d == 0 else range(NB - 1, -1, -1)
        chunks = (0, 1) if d == 0 else (1, 0)
        for bi, blk in enumerate(order):
            t0 = blk * 128
            ts = slice(t0, t0 + 128)
            qkb = qk[bi % 2]; kvb = kv[bi % 2]; ob = obt[bi % 2]; gtb = gt[0]
            pg.dma(qkb.ap, QKv[:, :, ts], writes=[qkb])
            pg.dma(kvb.ap.rearrange("p a h c -> p (a h c)"), cx.KVT[ts, :], writes=[kvb])
            if d == 0:
                pg.dma(ob.ap, cx.OB[ts, 512:1024], writes=[ob])
                pg.dma(gtb.ap[:, 0:512], cx.PT[ts, PT_DN_G:PT_DN_G + 512], writes=[gtb])
            bcast = lambda t: t.ap[:, blk, d, :].unsqueeze(2).to_broadcast([128, 4, 128])
            pg.op("dve", lambda e: e.tensor_tensor(out=vb4.ap, in0=kvb.ap[:, 1], in1=bcast(beta), op=ALU.mult), reads=[kvb, beta], writes=[vb4])
            pg.op("pool", lambda e: e.tensor_tensor(out=kbg4.ap, in0=kvb.ap[:, 0], in1=bcast(bg), op=ALU.mult), reads=[kvb, bg], writes=[kbg4])
            pg.op("pool", lambda e: e.tensor_tensor(out=kend4.ap, in0=kvb.ap[:, 0], in1=bcast(eD), op=ALU.mult), reads=[kvb, eD], writes=[kend4])
            op_ = cx.pso
            def head_gen(h, blk=blk, d=d, qkb=qkb, chunks=chunks, op_=op_):
                i2 = h
                bank = cx.psf.items[h]
                hc = slice(h * 128, (h + 1) * 128)
                gsc = g.ap[:, blk, d, h:h + 1]
                pg.op("dve", lambda e: e.tensor_scalar(out=gtri[i2].ap, in0=cx.m_incl.ap[:, d, :], scalar1=gsc, scalar2=None, op0=ALU.mult), reads=[cx.m_incl, g], writes=[gtri[i2]])
                yield
                dps = bank
                pg.op("pe", lambda e: e.matmul(dps.ap[:, 0:128], lhsT=gtri[i2].ap, rhs=cx.m_sa.ap[:, d, :], start=True, stop=True), reads=[gtri[i2], cx.m_sa], writes=[dps])
                pg.op("pe", lambda e: e.matmul(dps.ap[:, 128:256], lhsT=cx.m_sa.ap[:, d, :], rhs=gtri[i2].ap, start=True, stop=True), reads=[gtri[i2], cx.m_sa], writes=[dps])
                pg.op("pe", lambda e: e.matmul(dps.ap[:, 256:384], lhsT=cx.onesf.ap, rhs=gtri[i2].ap, start=True, stop=True), reads=[gtri[i2], cx.onesf], writes=[dps])
                yield
                pg.op("act", lambda e: e.activation(out=gam[i2].ap.rearrange("p a t -> p (a t)"), in_=dps.ap[:, 0:384], func=AF.Exp), reads=[dps], writes=[gam[i2]])
                yield
                pg.op("pool", lambda e: e.tensor_tensor(out=gamm[i2].ap, in0=gam[i2].ap[:, 0:2, :], in1=cx.m_dn.ap[:, d], op=ALU.mult), reads=[gam[i2], cx.m_dn], writes=[gamm[i2]])
                pg.op("pool", lambda e: e.tensor_tensor(out=qd[i2].ap, in0=qkb.ap[:, h, :], in1=gam[i2].ap[:, 2, :], op=ALU.mult), reads=[qkb, gam[i2]], writes=[qd[i2]])
                kps = bank
                pg.op("pe", lambda e: e.matmul(kps.ap[:, 0:128], lhsT=qkb.ap[:, 4 + h, :], rhs=qkb.ap[:, 4 + h, :], start=True, stop=True), reads=[qkb], writes=[kps])
                pg.op("pe", lambda e: e.matmul(kps.ap[:, 128:256], lhsT=qkb.ap[:, 4 + h, :], rhs=qkb.ap[:, h, :], start=True, stop=True), reads=[qkb], writes=[kps])
                yield
                pg.op("dve", lambda e: e.scalar_tensor_tensor(out=Cm[i2].ap, in0=kps.ap[:, 0:128], scalar=nbeta.ap[:, blk, d, h:h + 1], in1=gamm[i2].ap[:, 0, :], op0=ALU.mult, op1=ALU.mult),
                      reads=[kps, nbeta, gamm[i2]], writes=[Cm[i2]])
                pg.op("dve", lambda e: e.tensor_tensor(out=attnT[i2].ap, in0=kps.ap[:, 128:256], in1=gamm[i2].ap[:, 1, :], op=ALU.mult), reads=[kps, gamm[i2]], writes=[attnT[i2]])
                yield
                tb = bank
                tbv = bank.ap.bitcast(BF16)
                pg.op("pe", lambda e: e.transpose(out=tbv[:, 0:128], in_=Cm[i2].ap, identity=cx.identb.ap), reads=[Cm[i2], cx.identb], writes=[tb])
                yield
                hr = [0]
                bc0 = BC[h][hr[0] % 2]
                pg.op("act", lambda e: e.copy(bc0.ap[:, 0, :], tbv[:, 0:128]), reads=[tb], writes=[bc0])
                pg.op("pool", lambda e: e.tensor_copy(out=bc0.ap[:, 1, :], in_=Cm[i2].ap), reads=[Cm[i2]], writes=[bc0])
                p0 = Pm[h][hr[0] % 2]; p032 = Pm32[h][hr[0] % 2]; hr[0] += 1
                pg.op("pool", lambda e: e.tensor_tensor(out=p0.ap, in0=bc0.ap[:, 0, :], in1=cx.identb.ap, op=ALU.add), reads=[bc0, cx.identb], writes=[p0])
                yield
                bcp, pp, pp32 = bc0, p0, p032
                for k in range(1, 6):
                    sq_ = bank
                    if k < 5:
                        pg.op("pe", lambda e: e.matmul(sq_.ap[:, 0:128], lhsT=bcp.ap[:, 1, :], rhs=bcp.ap[:, 0, :], start=True, stop=True), reads=[bcp], writes=[sq_])
                    pg.op("pe", lambda e: e.matmul(sq_.ap[:, 128:256], lhsT=bcp.ap[:, 0, :], rhs=bcp.ap[:, 1, :], start=True, stop=True), reads=[bcp], writes=[sq_])
                    yield
                    bcn = BC[h][hr[0] % 2]
                    if k < 5:
                        pg.op("act", lambda e: e.copy(bcn.ap.rearrange("p a t -> p (a t)"), sq_.ap[:, 0:256]), reads=[sq_], writes=[bcn])
                    else:
                        pg.op("act", lambda e: e.copy(bcn.ap[:, 1, :], sq_.ap[:, 128:256]), reads=[sq_], writes=[bcn])
                        yield
                    pps = bank
                    pg.op("pe", lambda e: e.matmul(pps.ap[:, 0:128], lhsT=bcn.ap[:, 1, :], rhs=pp.ap, start=True, stop=True), reads=[bcn, pp], writes=[pps])
                    yield
                    pn = Pm[h][hr[0] % 2]; pn32 = Pm32[h][hr[0] % 2]; hr[0] += 1
                    pg.op("dve", lambda e: e.tensor_tensor(out=pn.ap, in0=pps.ap[:, 0:128], in1=pp.ap, op=ALU.add), reads=[pps, pp], writes=[pn])
                    yield
                    bcp, pp, pp32 = bcn, pn, pn32
                ups = bank
                pg.op("pe", lambda e: e.matmul(ups.ap[:, 0:128], lhsT=pp.ap, rhs=vb4.ap[:, h, :], start=True, stop=True), reads=[pp, vb4], writes=[ups])
                pg.op("pe", lambda e: e.matmul(ups.ap[:, 128:256], lhsT=kbg4.ap[:, h, :], rhs=pp.ap, start=True, stop=True), reads=[pp, kbg4], writes=[ups])
                yield
                pg.op("act", lambda e: e.copy(usb[i2].ap, ups.ap[:, 0:128]), reads=[ups], writes=[usb[i2]])
                pg.op("act", lambda e: e.copy(wT[i2].ap, ups.ap[:, 128:256]), reads=[ups], writes=[wT[i2]])
                yield
                for ci, c in enumerate(chunks):
                    r0 = c * 64
                    rs_ = slice(r0, r0 + 64)
                    wps = bank
                    pg.op("pe", lambda e: e.matmul(wps.ap[rs_, 0:128], lhsT=wT[i2].ap[:, rs_], rhs=Sb[h].ap, start=True, stop=True), reads=[wT[i2], Sb[h]], writes=[wps])
                    yield
                    pg.op("dve", lambda e: e.scalar_tensor_tensor(out=vn[i2].ap[rs_, :], in0=wps.ap[rs_, 0:128], scalar=-1.0, in1=usb[i2].ap[rs_, :], op0=ALU.mult, op1=ALU.add), reads=[usb[i2], wps], writes=[vn[i2]])
                    yield
                    pg.op("pe", lambda e: e.matmul(op_.ap[rs_, hc], lhsT=qd[i2].ap[:, rs_], rhs=Sb[h].ap, start=True, stop=False), reads=[qd[i2], Sb[h]], writes=[op_])
                    pg.op("pe", lambda e: e.matmul(op_.ap[rs_, hc], lhsT=attnT[i2].ap[rs_, rs_], rhs=vn[i2].ap[rs_, :], start=False, stop=True), reads=[attnT[i2], vn[i2]], writes=[op_])
                    kvp = bank
                    pg.op("pe", lambda e: e.matmul(kvp.ap[:, 0:128], lhsT=kend4.ap[rs_, h, :], rhs=vn[i2].ap[rs_, :], start=True, stop=True), reads=[kend4, vn[i2]], writes=[kvp])
                    yield
                    pg.op("dve", lambda e: e.scalar_tensor_tensor(out=S32[h].ap, in0=S32[h].ap, scalar=deB.ap[:, c, blk, d, h:h + 1], in1=kvp.ap[:, 0:128], op0=ALU.mult, op1=ALU.add),
                          reads=[S32[h], deB, kvp], writes=[S32[h]])
                    pg.op("act", lambda e: e.copy(Sb[h].ap, S32[h].ap), reads=[S32[h]], writes=[Sb[h]])
                    yield
            gens = [head_gen(h) for h in range(4)]
            while gens:
                for gnr in list(gens):
                    try:
                        next(gnr)
                    except StopIteration:
                        gens.remove(gnr)
            if d == 1:
                pg.op("act", lambda e: e.copy(osb.ap, op_.ap), reads=[op_], writes=[osb])
                pg.dma(cx.OB[ts, 512:1024], osb.ap, reads=[osb])
            else:
                pg.op("dve", lambda e: e.tensor_tensor(out=osb.ap, in0=op_.ap, in1=ob.ap, op=ALU.add), reads=[op_, ob], writes=[osb])
                head_norm_gate_store(pg, cx, osb, ssq, junk, gn, gtb, 0, ysb, yT, 1024, ts)


def phase_S5(pg, cx, es, L, l):
    nc = cx.nc
    w = cx.w
    sb = lambda name, shape, dt: pg.buf(es.enter_context(nc.sbuf_tensor(pg.uname(name), shape, dt)).ap(), name)
    NS = int(np.ceil(np.log2(L)))
    dve = lambda fn, r, wr: pg.op("dve", fn, reads=r, writes=wr)
    A = lambda nm: sb("S_" + nm, [128, 32], F32)
    lre, lim, dt_, ar, ai, m_, sn, cs, Are, Aim, t1, t2, t3, fre, fim, den = [A(n) for n in
        ("lre", "lim", "dt", "ar", "ai", "m", "sn", "cs", "Are", "Aim", "t1", "t2", "t3", "fre", "fim", "den")]
    load_T(pg, cx, lre, lre.ap, w["s5_lambda_re"][l].rearrange("d (gh gl) p -> (d gh) (gl p)", gl=2), 32)
    load_T(pg, cx, lim, lim.ap, w["s5_lambda_im"][l].rearrange("d (gh gl) p -> (d gh) (gl p)", gl=2), 32)
    ld2 = sb("S_ld2", [32, 2], F32)
    pg.dma(ld2.ap, w["s5_log_dt"][l].rearrange("d (gh gl) -> (d gh) gl", gl=2), writes=[ld2])
    stl = sb("S_stl", [32, 128], F32)
    for gl in range(2):
        dve(lambda e: e.tensor_copy(out=stl.ap[:, 64 * gl:64 * gl + 64], in_=ld2.ap[:, gl:gl + 1].to_broadcast([32, 64])), [ld2], [stl])
    psl = cx.psf.get()
    pg.op("pe", lambda e: e.transpose(out=psl.ap[:, :32], in_=stl.ap, identity=cx.identf.ap[:32, :32]), reads=[stl, cx.identf], writes=[psl])
    dve(lambda e: e.tensor_copy(out=dt_.ap, in_=psl.ap[:, :32]), [psl], [dt_])
    pg.op("act", lambda e: e.activation(out=dt_.ap, in_=dt_.ap, func=AF.Exp), reads=[dt_], writes=[dt_])
    dve(lambda e: e.tensor_tensor(out=ar.ap, in0=lre.ap, in1=dt_.ap, op=ALU.mult), [lre, dt_], [ar])
    dve(lambda e: e.tensor_tensor(out=ai.ap, in0=lim.ap, in1=dt_.ap, op=ALU.mult), [lim, dt_], [ai])
    pg.op("act", lambda e: e.activation(out=m_.ap, in_=ar.ap, func=AF.Exp, scale=1.0 / 16), reads=[ar], writes=[m_])
    pg.op("act", lambda e: e.activation(out=sn.ap, in_=ai.ap, func=AF.Sin, scale=1.0 / 16), reads=[ai], writes=[sn])
    pg.op("act", lambda e: e.activation(out=cs.ap, in_=ai.ap, func=AF.Sin, scale=1.0 / 16, bias=cx.halfpi.ap[:, 0:1]), reads=[ai, cx.halfpi], writes=[cs])
    dve(lambda e: e.tensor_tensor(out=Are.ap, in0=m_.ap, in1=cs.ap, op=ALU.mult), [m_, cs], [Are])
    dve(lambda e: e.tensor_tensor(out=Aim.ap, in0=m_.ap, in1=sn.ap, op=ALU.mult), [m_, sn], [Aim])

    def csquare(re, im):
        dve(lambda e: e.tensor_tensor(out=t1.ap, in0=re, in1=re, op=ALU.mult), [Are, PW], [t1])
        dve(lambda e: e.tensor_tensor(out=t2.ap, in0=im, in1=im, op=ALU.mult), [Aim, PW], [t2])
        dve(lambda e: e.tensor_tensor(out=t3.ap, in0=re, in1=im, op=ALU.mult), [Are, Aim, PW], [t3])

    PW = sb("S_PW", [128, 32, NS, 3], F32)
    for _ in range(4):
        csquare(Are.ap, Aim.ap)
        dve(lambda e: e.tensor_tensor(out=Are.ap, in0=t1.ap, in1=t2.ap, op=ALU.subtract), [t1, t2], [Are])
        dve(lambda e: e.tensor_scalar(out=Aim.ap, in0=t3.ap, scalar1=2.0, scalar2=None, op0=ALU.mult), [t3], [Aim])
    dve(lambda e: e.tensor_tensor(out=den.ap, in0=lre.ap, in1=lre.ap, op=ALU.mult), [lre], [den])
    dve(lambda e: e.tensor_tensor(out=t1.ap, in0=lim.ap, in1=lim.ap, op=ALU.mult), [lim], [t1])
    dve(lambda e: e.tensor_tensor(out=den.ap, in0=den.ap, in1=t1.ap, op=ALU.add), [den, t1], [den])
    dve(lambda e: e.reciprocal(out=den.ap, in_=den.ap), [den], [den])
    dve(lambda e: e.tensor_scalar(out=t3.ap, in0=Are.ap, scalar1=-1.0, scalar2=None, op0=ALU.add), [Are], [t3])
    dve(lambda e: e.tensor_tensor(out=t1.ap, in0=t3.ap, in1=lre.ap, op=ALU.mult), [t3, lre], [t1])
    dve(lambda e: e.tensor_tensor(out=t2.ap, in0=Aim.ap, in1=lim.ap, op=ALU.mult), [Aim, lim], [t2])
    dve(lambda e: e.tensor_tensor(out=t1.ap, in0=t1.ap, in1=t2.ap, op=ALU.add), [t1, t2], [t1])
    dve(lambda e: e.tensor_tensor(out=fre.ap, in0=t1.ap, in1=den.ap, op=ALU.mult), [t1, den], [fre])
    dve(lambda e: e.tensor_tensor(out=t1.ap, in0=Aim.ap, in1=lre.ap, op=ALU.mult), [Aim, lre], [t1])
    dve(lambda e: e.tensor_tensor(out=t2.ap, in0=t3.ap, in1=lim.ap, op=ALU.mult), [t3, lim], [t2])
    dve(lambda e: e.tensor_tensor(out=t1.ap, in0=t1.ap, in1=t2.ap, op=ALU.subtract), [t1, t2], [t1])
    dve(lambda e: e.tensor_tensor(out=fim.ap, in0=t1.ap, in1=den.ap, op=ALU.mult), [t1, den], [fim])
    for k in range(NS):
        if k == 0:
            dve(lambda e: e.tensor_copy(out=PW.ap[:, :, 0, 0], in_=Are.ap), [Are], [PW])
            dve(lambda e: e.tensor_copy(out=PW.ap[:, :, 0, 1], in_=Aim.ap), [Aim], [PW])
        else:
            csquare(PW.ap[:, :, k - 1, 0], PW.ap[:, :, k - 1, 1])
            dve(lambda e: e.tensor_tensor(out=PW.ap[:, :, k, 0], in0=t1.ap, in1=t2.ap, op=ALU.subtract), [t1, t2], [PW])
            dve(lambda e: e.tensor_scalar(out=PW.ap[:, :, k, 1], in0=t3.ap, scalar1=2.0, scalar2=None, op0=ALU.mult), [t3], [PW])
        dve(lambda e: e.tensor_scalar(out=PW.ap[:, :, k, 2], in0=PW.ap[:, :, k, 1], scalar1=-1.0, scalar2=None, op0=ALU.mult), [PW], [PW])
    CL = sb("S_CL", [128, 32, 2, 32], F32)
    Wb = sb("S_Wb", [128, 32, 2, 32], F32)
    PWs = sb("S_PWs", [128, 32, 9, 2], F32)
    dve(lambda e: e.memset(PWs.ap[:, :, 0, 0], 1.0), [], [PWs])
    dve(lambda e: e.memset(PWs.ap[:, :, 0, 1], 0.0), [], [PWs])
    for k in range(1, 9):
        pr, pi_ = PWs.ap[:, :, k - 1, 0], PWs.ap[:, :, k - 1, 1]
        dve(lambda e: e.tensor_tensor(out=t1.ap, in0=pr, in1=PW.ap[:, :, 0, 0], op=ALU.mult), [PWs, PW], [t1])
        dve(lambda e: e.tensor_tensor(out=t2.ap, in0=pi_, in1=PW.ap[:, :, 0, 1], op=ALU.mult), [PWs, PW], [t2])
        dve(lambda e: e.tensor_tensor(out=PWs.ap[:, :, k, 0], in0=t1.ap, in1=t2.ap, op=ALU.subtract), [t1, t2], [PWs])
        dve(lambda e: e.tensor_tensor(out=t1.ap, in0=pr, in1=PW.ap[:, :, 0, 1], op=ALU.mult), [PWs, PW], [t1])
        dve(lambda e: e.tensor_tensor(out=t2.ap, in0=pi_, in1=PW.ap[:, :, 0, 0], op=ALU.mult), [PWs, PW], [t2])
        dve(lambda e: e.tensor_tensor(out=PWs.ap[:, :, k, 1], in0=t1.ap, in1=t2.ap, op=ALU.add), [t1, t2], [PWs])
    with ExitStack() as es1:
        sb1 = lambda name, shape, dt: pg.buf(es1.enter_context(nc.sbuf_tensor(pg.uname(name), shape, dt)).ap(), name)
        Bt = [sb1("S_Bt%d" % i, [128, 32, 16], F32) for i in range(2)]
        Bb = [sb1("S_Bb%d" % i, [128, 32, 16], F32) for i in range(2)]
        tmpb = sb1("S_tmpb", [128, 32, 16], F32)
        for i, nm in enumerate(("s5_b_re", "s5_b_im")):
            base = w[nm][l]
            pg.dma(Bt[i].ap, bass.AP(base.tensor, base.offset, [[16, 128], [2048, 32], [1, 16]]), writes=[Bt[i]])
        fb = lambda t: t.ap.unsqueeze(2).to_broadcast([128, 32, 16])
        dve(lambda e: e.tensor_tensor(out=Bb[0].ap, in0=Bt[0].ap, in1=fb(fre), op=ALU.mult), [Bt[0], fre], [Bb[0]])
        dve(lambda e: e.tensor_tensor(out=tmpb.ap, in0=Bt[1].ap, in1=fb(fim), op=ALU.mult), [Bt[1], fim], [tmpb])
        dve(lambda e: e.tensor_tensor(out=Bb[0].ap, in0=Bb[0].ap, in1=tmpb.ap, op=ALU.subtract), [Bb[0], tmpb], [Bb[0]])
        dve(lambda e: e.tensor_tensor(out=Bb[1].ap, in0=Bt[1].ap, in1=fb(fre), op=ALU.mult), [Bt[1], fre], [Bb[1]])
        dve(lambda e: e.tensor_tensor(out=tmpb.ap, in0=Bt[0].ap, in1=fb(fim), op=ALU.mult), [Bt[0], fim], [tmpb])
        dve(lambda e: e.tensor_tensor(out=Bb[1].ap, in0=Bb[1].ap, in1=tmpb.ap, op=ALU.add), [Bb[1], tmpb], [Bb[1]])
        pg.op("pool", lambda e: e.memset(Wb.ap, 0.0), writes=[Wb])
        for c in range(2):
            dve(lambda e: e.tensor_copy(out=Wb.ap[0:64, :, c, 0:16], in_=Bb[c].ap[0:64]), [Bb[c]], [Wb])
            dve(lambda e: e.tensor_copy(out=Wb.ap[64:128, :, c, 16:32], in_=Bb[c].ap[64:128]), [Bb[c]], [Wb])
        St0 = sb1("S_St0", [128, 64], F32)
        St = sb1("S_St", [128, 128], F32)
        for d in range(2):
            for c, nm in enumerate(("s5_c_re", "s5_c_im")):
                for blk in range(4):
                    pg.dma(St0.ap, w[nm][l, d, 8 * blk:8 * blk + 8].rearrange("g i p -> (g i) p"), writes=[St0])
                    sgn = 1.0 if c == 0 else -1.0
                    for hh in range(2):
                        dve(lambda e: e.tensor_scalar(out=St.ap[:, 64 * hh:64 * hh + 64], in0=St0.ap, scalar1=cx.pm.ap[:, hh:hh + 1], scalar2=sgn, op0=ALU.mult, op1=ALU.mult),
                            [St0, cx.pm], [St])
                    ps = cx.psf.get()
                    pg.op("pe", lambda e: e.transpose(out=ps.ap[:, 0:128], in_=St.ap, identity=cx.identf.ap), reads=[St, cx.identf], writes=[ps])
                    dg0 = d * 16 + blk * 4
                    pg.op("act", lambda e: e.copy(CL.ap[:, dg0:dg0 + 4, c, :], ps.ap[:, 0:128].rearrange("p (q m) -> p q m", q=4)), reads=[ps], writes=[CL])
    dsk = sb("S_dsk", [32, 16], F32)
    load_T(pg, cx, dsk, dsk.ap, w["s5_d"][l].rearrange("(g q) -> g q", q=32), 16, wd=32)
    bgl = sb("S_bgl", [128, 4], F32)
    load_T(pg, cx, bgl, bgl.ap, w["s5_b_glu"][l].rearrange("(c p) -> c p", p=128), 4)
    es2 = ExitStack()
    sb2 = lambda name, shape, dt: pg.buf(es2.enter_context(nc.sbuf_tensor(pg.uname(name), shape, dt)).ap(), name)
    NCH = L // 8
    NSC = int(np.ceil(np.log2(NCH)))
    HW = min(512, NCH)
    NH = NCH // HW
    ub = sb2("S_ub", [32, 8, NCH], BF16)
    UW = min(2048, L)
    ust = [sb2("S_ust%d" % i, [32, UW], F32) for i in range(2)]
    Yc = sb2("S_Yc", [32, L], F32)
    XS_ = [[sb2("S_X%d_%d" % (d, i), [128, NCH + 2], F32) for i in range(3)] for d in range(2)]
    Xb = [[sb2("S_Xb%d_%d" % (d, c), [128, NCH + 2], BF16) for c in range(2)] for d in range(2)]
    Wt_ = [sb2("S_Wt%d" % d, [128, 2, 8, 32], F32) for d in range(2)]
    tmpw_ = [sb2("S_tmpw%d" % d, [128, 8, 32], F32) for d in range(2)]
    WsT = [sb2("S_WsT%d" % d, [32, 2, 8, 128], BF16) for d in range(2)]
    CI = [sb2("S_CI%d" % d, [128, 2, 8, 32], BF16) for d in range(2)]
    CIf_ = [sb2("S_CIf%d" % d, [128, 2, 8, 32], F32) for d in range(2)]
    Kd = [sb2("S_Kd%d" % d, [32, 8, 32], BF16) for d in range(2)]
    ua_t = sb2("S_ua", [32, UW], F32)
    x2_t = sb2("S_x2", [32, UW], F32)
    zo_t = sb2("S_zo", [32, UW], BF16)

    def strided(ap2, start, n, step):
        b0 = ap2[:, start:start + 1]
        return bass.AP(b0.tensor, b0.offset, [list(ap2.ap[0]), [step * ap2.ap[1][0], n]])

    ev = [0]
    prev_pair = None
    out_ps = Rot(cx.psf.items[2:5])
    bcW = lambda a: a.unsqueeze(1).to_broadcast([128, 8, 32])
    bcP = lambda a: a.unsqueeze(2).to_broadcast([128, 8, 32])
    for gh in range(16):
        urow = PF_S5_U + 32 * gh
        for i, t0 in enumerate(range(0, L, UW)):
            st_ = ust[i % 2]
            pg.dma(st_.ap, cx.PF[urow:urow + 32, t0:t0 + UW], writes=[st_])
            pg.op("pool", lambda e: e.tensor_copy(out=ub.ap[:, :, t0 // 8:(t0 + UW) // 8], in_=st_.ap.rearrange("p (n s) -> p s n", s=8)), reads=[st_], writes=[ub])
        def dir_gen(d, gh=gh):
            bank = cx.psf.items[d]
            Wt, tmpw, CIf = Wt_[d], tmpw_[d], CIf_[d]
            dg = d * 16 + gh
            wbr, wbi = Wb.ap[:, dg, 0, :], Wb.ap[:, dg, 1, :]
            pre, pim = PWs.ap[:, dg, 0:8, 0], PWs.ap[:, dg, 0:8, 1]
            dve(lambda e: e.tensor_tensor(out=Wt.ap[:, 0], in0=bcW(wbr), in1=bcP(pre), op=ALU.mult), [Wb, PWs], [Wt])
            yield
            dve(lambda e: e.tensor_tensor(out=tmpw.ap, in0=bcW(wbi), in1=bcP(pim), op=ALU.mult), [Wb, PWs], [tmpw])
            yield
            dve(lambda e: e.tensor_tensor(out=Wt.ap[:, 0], in0=Wt.ap[:, 0], in1=tmpw.ap, op=ALU.subtract), [Wt, tmpw], [Wt])
            yield
            dve(lambda e: e.tensor_tensor(out=Wt.ap[:, 1], in0=bcW(wbr), in1=bcP(pim), op=ALU.mult), [Wb, PWs], [Wt])
            yield
            dve(lambda e: e.tensor_tensor(out=tmpw.ap, in0=bcW(wbi), in1=bcP(pre), op=ALU.mult), [Wb, PWs], [tmpw])
            yield
            dve(lambda e: e.tensor_tensor(out=Wt.ap[:, 1], in0=Wt.ap[:, 1], in1=tmpw.ap, op=ALU.add), [Wt, tmpw], [Wt])
            yield
            for c in range(2):
                for t4 in range(0, 8, 4):
                    ps = bank
                    for tq in range(4):
                        pg.op("pe", lambda e: e.transpose(out=ps.ap[:32, tq * 128:(tq + 1) * 128], in_=Wt.ap[:, c, t4 + tq, :], identity=cx.identf.ap), reads=[Wt, cx.identf], writes=[ps])
                    pg.op("act", lambda e: e.copy(WsT[d].ap[:, c, t4:t4 + 4, :], ps.ap[:32, :].rearrange("p (q m) -> p q m", q=4)), reads=[ps], writes=[WsT[d]])
                    yield
            cl0, cl1 = CL.ap[:, dg, 0, :], CL.ap[:, dg, 1, :]
            pre1, pim1 = PWs.ap[:, dg, 1:9, 0], PWs.ap[:, dg, 1:9, 1]
            dve(lambda e: e.tensor_tensor(out=CIf.ap[:, 0], in0=bcW(cl0), in1=bcP(pre1), op=ALU.mult), [CL, PWs], [CIf])
            yield
            dve(lambda e: e.tensor_tensor(out=tmpw.ap, in0=bcW(cl1), in1=bcP(pim1), op=ALU.mult), [CL, PWs], [tmpw])
            yield
            dve(lambda e: e.tensor_tensor(out=CI[d].ap[:, 0], in0=CIf.ap[:, 0], in1=tmpw.ap, op=ALU.add), [CIf, tmpw], [CI[d]])
            yield
            dve(lambda e: e.tensor_tensor(out=CIf.ap[:, 1], in0=bcW(cl1), in1=bcP(pre1), op=ALU.mult), [CL, PWs], [CIf])
            yield
            dve(lambda e: e.tensor_tensor(out=tmpw.ap, in0=bcW(cl0), in1=bcP(pim1), op=ALU.mult), [CL, PWs], [tmpw])
            yield
            dve(lambda e: e.tensor_tensor(out=CI[d].ap[:, 1], in0=CIf.ap[:, 1], in1=tmpw.ap, op=ALU.subtract), [CIf, tmpw], [CI[d]])
            yield
            ps = bank
            for tau in range(8):
                po = ps.ap[0:32, tau * 32:(tau + 1) * 32]
                pg.op("pe", lambda e: e.matmul(po, lhsT=Wt.ap[:, 0, tau, :], rhs=cl0, start=True, stop=False), reads=[Wt, CL], writes=[ps])
                pg.op("pe", lambda e: e.matmul(po, lhsT=Wt.ap[:, 1, tau, :], rhs=cl1, start=False, stop=True), reads=[Wt, CL], writes=[ps])
            pg.op("act", lambda e: e.copy(Kd[d].ap, ps.ap[0:32, 0:256].rearrange("p (t m) -> p t m", t=8)), reads=[ps], writes=[Kd[d]])
            yield
            re, im, T = XS_[d]
            for b_ in (re, im, T):
                pg.op("pool", lambda e: e.memset(b_.ap, 0.0), writes=[b_])
                yield
            for c, dstb in ((0, re), (1, im)):
                for hf in range(NH):
                    ps = bank
                    for s_ in range(8):
                        tau = 7 - s_ if d == 0 else s_
                        pg.op("pe", lambda e: e.matmul(ps.ap[:, :HW], lhsT=WsT[d].ap[:, c, tau, :], rhs=ub.ap[:, s_, hf * HW:(hf + 1) * HW], start=(s_ == 0), stop=(s_ == 7)),
                              reads=[WsT[d], ub], writes=[ps])
                    ev[0] += 1
                    if ev[0] % 2 == 0:
                        pg.op("act", lambda e: e.copy(dstb.ap[:, 1 + hf * HW:1 + (hf + 1) * HW], ps.ap[:, :HW]), reads=[ps], writes=[dstb])
                        yield
                    else:
                        pg.op("dve", lambda e: e.tensor_copy(out=dstb.ap[:, 1 + hf * HW:1 + (hf + 1) * HW], in_=ps.ap[:, :HW]), reads=[ps], writes=[dstb])
                        yield
            for k in range(NSC):
                sft = 1 << k
                if sft >= NCH:
                    break
                kk = k + 3
                cre, cim, ncim = PW.ap[:, dg, kk, 0:1], PW.ap[:, dg, kk, 1:2], PW.ap[:, dg, kk, 2:3]
                if d == 0:
                    dst, src, keep = slice(1 + sft, 1 + NCH), slice(1, 1 + NCH - sft), slice(1, 1 + sft)
                else:
                    dst, src, keep = slice(1, 1 + NCH - sft), slice(1 + sft, 1 + NCH), slice(1 + NCH - sft, 1 + NCH)
                dve(lambda e: e.scalar_tensor_tensor(out=T.ap[:, dst], in0=re.ap[:, src], scalar=cre, in1=re.ap[:, dst], op0=ALU.mult, op1=ALU.add), [re, PW], [T])
                yield
                dve(lambda e: e.scalar_tensor_tensor(out=T.ap[:, dst], in0=im.ap[:, src], scalar=ncim, in1=T.ap[:, dst], op0=ALU.mult, op1=ALU.add), [im, T, PW], [T])
                yield
                pg.op("pool", lambda e: e.tensor_copy(out=T.ap[:, keep], in_=re.ap[:, keep]), reads=[re], writes=[T])
                yield
                if d == 0:
                    rv = lambda ap, sl: bass.AP(ap.tensor, ap[:, sl].offset + (sl.stop - sl.start) - 1, [list(ap.ap[0]), [-1, sl.stop - sl.start]])
                    dve(lambda e: e.scalar_tensor_tensor(out=rv(im.ap, dst), in0=rv(im.ap, src), scalar=cre, in1=rv(im.ap, dst), op0=ALU.mult, op1=ALU.add), [im, PW], [im])
                    yield
                else:
                    dve(lambda e: e.scalar_tensor_tensor(out=im.ap[:, dst], in0=im.ap[:, src], scalar=cre, in1=im.ap[:, dst], op0=ALU.mult, op1=ALU.add), [im, PW], [im])
                    yield
                dve(lambda e: e.scalar_tensor_tensor(out=im.ap[:, dst], in0=re.ap[:, src], scalar=cim, in1=im.ap[:, dst], op0=ALU.mult, op1=ALU.add), [re, im, PW], [im])
                yield
                re, T = T, re
            pg.op("pool", lambda e: e.tensor_copy(out=Xb[d][0].ap, in_=re.ap), reads=[re], writes=[Xb[d][0]])
            yield
            pg.op("pool", lambda e: e.tensor_copy(out=Xb[d][1].ap, in_=im.ap), reads=[im], writes=[Xb[d][1]])
            yield

        def gelu_gen(gh_, urow_):
            for t0 in range(0, L, UW):
                tsl = slice(t0, t0 + UW)
                ua, x2, zo = ua_t.ap, x2_t.ap, zo_t.ap
                pg.dma(ua, cx.PF[urow_:urow_ + 32, tsl], writes=[ua_t])
                yv = Yc.ap[:, tsl]
                dve(lambda e: e.scalar_tensor_tensor(out=yv, in0=ua, scalar=dsk.ap[:, gh_:gh_ + 1], in1=yv, op0=ALU.mult, op1=ALU.add), [ua_t, dsk, Yc], [Yc])
                yield
                pg.op("pool", lambda e: e.tensor_tensor(out=x2, in0=yv, in1=yv, op=ALU.mult), reads=[Yc], writes=[x2_t])
                yield
                dve(lambda e: e.tensor_scalar(out=x2, in0=x2, scalar1=0.044715, scalar2=1.0, op0=ALU.mult, op1=ALU.add), [x2_t], [x2_t])
                yield
                pg.op("pool", lambda e: e.tensor_tensor(out=x2, in0=x2, in1=yv, op=ALU.mult), reads=[Yc, x2_t], writes=[x2_t])
                yield
                pg.op("act", lambda e: e.activation(out=x2, in_=x2, func=AF.Sigmoid, scale=1.5957691216), reads=[x2_t], writes=[x2_t])
                yield
                dve(lambda e: e.tensor_tensor(out=zo, in0=x2, in1=yv, op=ALU.mult), [x2_t, Yc], [zo_t])
                pg.dma(cx.ZT[urow_ - PF_S5_U:urow_ - PF_S5_U + 32, tsl], zo, reads=[zo_t])
                yield

        gens = [dir_gen(0), dir_gen(1)] + ([gelu_gen(*prev_pair)] if prev_pair is not None else [])
        while gens:
            for gnr in list(gens):
                try:
                    next(gnr)
                except StopIteration:
                    gens.remove(gnr)
        prev_pair = (gh, urow)
        for hf in range(NH):
            for sp in range(8):
                ps = out_ps.get()
                po = ps.ap[0:32, :HW]
                mm = []
                for c in range(2):
                    mm.append((CI[0].ap[:, c, sp, :], Xb[0][c].ap[:, hf * HW:hf * HW + HW], [CI[0], Xb[0][c]]))
                    mm.append((CI[1].ap[:, c, 7 - sp, :], Xb[1][c].ap[:, hf * HW + 2:hf * HW + 2 + HW], [CI[1], Xb[1][c]]))
                for s_ in range(0, sp + 1):
                    mm.append((Kd[0].ap[:, sp - s_, :], ub.ap[:, s_, hf * HW:(hf + 1) * HW], [Kd[0], ub]))
                for s_ in range(sp, 8):
                    mm.append((Kd[1].ap[:, s_ - sp, :], ub.ap[:, s_, hf * HW:(hf + 1) * HW], [Kd[1], ub]))
                for i, (lh, rh, rd) in enumerate(mm):
                    pg.op("pe", lambda e: e.matmul(po, lhsT=lh, rhs=rh, start=(i == 0), stop=(i == len(mm) - 1)), reads=rd, writes=[ps])
                pg.op("act", lambda e: e.copy(strided(Yc.ap, hf * HW * 8 + sp, HW, 8), po), reads=[ps], writes=[Yc])
    gens = [gelu_gen(*prev_pair)]
    for gnr in gens:
        for _ in gnr:
            pass
    pg.barrier()
    es2.close()
    wg = sb("S_wg", [128, 4, 512], BF16)
    wst = sb("S_wst", [128, 2048], F32)
    pg.dma(wst.ap[:, 0:2048].rearrange("p (k c) -> p k c", k=4), w["s5_w_glu"][l].rearrange("(k p) c -> p k c", p=128), writes=[wst])
    dve(lambda e: e.tensor_copy(out=wg.ap, in_=wst.ap[:, 0:2048].rearrange("p (k c) -> p k c", k=4)), [wst], [wg])
    zt = [sb("S_zt%d" % i, [128, 4, 512], BF16) for i in range(2)]
    gt = [sb("S_gt%d" % i, [128, 4, 512], F32) for i in range(2)]
    sg = sb("S_sg", [128, 512], F32)
    yo = [sb("S_yo%d" % i, [128, 4, 512], BF16) for i in range(2)]
    ZTv = cx.ZT.rearrange("(k p) t -> p k t", p=128)
    NT = L // 512
    for it in range(NT):
        tsl = slice(it * 512, (it + 1) * 512)
        z_ = zt[it % 2]; g_ = gt[it % 2]; y_ = yo[it % 2]
        pg.dma(z_.ap, ZTv[:, :, tsl], writes=[z_])
        pg.dma(g_.ap, cx.PF[PF_S5_G:PF_S5_G + 512, tsl].rearrange("(k p) t -> p k t", p=128), writes=[g_])
        pg.op("act", lambda e: e.activation(out=g_.ap, in_=g_.ap, func=AF.Silu), reads=[g_], writes=[g_])
        pg.op("pool", lambda e: e.tensor_tensor(out=g_.ap, in0=g_.ap, in1=z_.ap, op=ALU.mult), reads=[g_, z_], writes=[g_])
        for oc in range(4):
            ps = cx.psf.get()
            for k in range(4):
                pg.op("pe", lambda e: e.matmul(ps.ap, lhsT=wg.ap[:, k, oc * 128:(oc + 1) * 128], rhs=z_.ap[:, k, :], start=(k == 0), stop=(k == 3)), reads=[wg, z_], writes=[ps])
            pg.op("act", lambda e: e.activation(out=sg.ap, in_=ps.ap, func=AF.Sigmoid, bias=bgl.ap[:, oc:oc + 1]), reads=[ps, bgl], writes=[sg])
            dve(lambda e: e.tensor_tensor(out=y_.ap[:, oc, :], in0=sg.ap, in1=g_.ap[:, oc, :], op=ALU.mult), [sg, g_], [y_])
        pg.dma(cx.BT[1536:2048, tsl].rearrange("(k p) t -> p k t", p=128), y_.ap, reads=[y_])


W_NAMES = ["norm_g", "w_in", "lru_conv_w", "lru_conv_b", "lru_w_a", "lru_b_a", "lru_w_x", "lru_b_x", "lru_lambda",
           "gla_w_up", "gla_b_up", "gla_norm_g", "dn_conv_w", "dn_a_log", "dn_dt_bias", "dn_norm_g",
           "s5_lambda_re", "s5_lambda_im", "s5_log_dt", "s5_b_re", "s5_b_im", "s5_c_re", "s5_c_im", "s5_d",
           "s5_w_glu", "s5_b_glu", "w_branch", "w_merge_gate", "b_merge_gate", "w_out", "final_norm_g"]


def host_consts():
    c = {}
    c["identb"] = np.eye(128, dtype=np.float32).astype(ml_dtypes.bfloat16)
    c["identf"] = np.eye(128, dtype=np.float32)
    idx = np.arange(128)
    same = (idx[:, None] // 64) == (idx[None, :] // 64)
    le = idx[:, None] <= idx[None, :]
    lt = idx[:, None] < idx[None, :]
    c["m_incl"] = np.stack([(same & le), (same & le.T)]).astype(np.float32)
    c["m_strict_after"] = np.stack([(same & lt.T), (same & lt)]).astype(np.float32)
    c["m_dn"] = np.stack([c["m_strict_after"], c["m_incl"]], axis=1)
    c["chunkind"] = np.stack([np.repeat((idx // 64 == cc)[:, None], 128, axis=1) for cc in range(2)]).astype(np.float32)
    c["onesf"] = np.ones((128, 128), np.float32)
    ev = ((idx // 16) % 2 == 0).astype(np.float32)
    c["pm"] = np.stack([ev, 1.0 - ev], axis=1).astype(np.float32)
    return c


def build(L, shapes, nslot=2, depth=2, debug=False, branches=("lru", "gla", "dn", "s5")):
    from contextlib import ExitStack
    nc = bass.Bass("TRN2", target_bir_lowering=False)
    pg = Prog(nc)
    cx = Ctx()
    cx.nc = nc
    cx.pg = pg
    cx.w = {}
    for nm in W_NAMES:
        cx.w[nm] = nc.dram_tensor(nm, list(shapes[nm]), F32, kind="ExternalInput").ap()
    hc = host_consts()
    cx.cd = {}
    for nm, arr in hc.items():
        cx.cd[nm] = nc.dram_tensor("c_" + nm, list(arr.shape), BF16 if arr.dtype == ml_dtypes.bfloat16 else F32, kind="ExternalInput").ap()
    xs = [nc.dram_tensor("x%d" % s, [L, D], F32, kind="ExternalInput").ap() for s in range(nslot)]
    ys = [nc.dram_tensor("y%d" % s, [L, D], F32, kind="ExternalOutput").ap() for s in range(nslot)]
    sk = "ExternalOutput" if debug else "Internal"
    cx.PF = nc.dram_tensor("PF", [PF_ROWS, L], F32, kind=sk).ap()
    cx.PT = nc.dram_tensor("PT", [L, PT_COLS], F32, kind=sk).ap()
    cx.XNT = nc.dram_tensor("XNT", [D, L], BF16, kind=sk).ap()
    cx.BT = nc.dram_tensor("BT", [2048, L], BF16, kind=sk).ap()
    cx.OB = nc.dram_tensor("OB", [L, 1024], F32, kind=sk).ap()
    XS = [nc.dram_tensor("XS%d" % s, [L, D], F32, kind=sk).ap() for s in range(nslot)]
    cx.QKT = nc.dram_tensor("QKT", [1024, L], BF16, kind=sk).ap()
    cx.KVT = nc.dram_tensor("KVT", [L, 1024], BF16, kind=sk).ap()
    cx.ZT = nc.dram_tensor("ZT", [512, L], BF16, kind=sk).ap()
    psf, psb = mk_psum(pg, nc)
    cx.psf = Rot(psf[:5])
    cx.pso = psf[5]
    cx.psb = Rot(psb)
    gsb = lambda name, shape, dt: pg.buf(nc.alloc_sbuf_tensor(name, shape, dt).ap(), name)
    cx.identb = gsb("identb", [128, 128], BF16)
    pg.dma(cx.identb.ap, cx.cd["identb"], writes=[cx.identb])
    cx.identf = gsb("identf", [128, 128], F32)
    pg.dma(cx.identf.ap, cx.cd["identf"], writes=[cx.identf])
    cx.eps = gsb("eps", [128, 1], F32)
    pg.op("dve", lambda e: e.memset(cx.eps.ap, EPS), writes=[cx.eps])
    cx.one = gsb("one", [128, 1], F32)
    pg.op("dve", lambda e: e.memset(cx.one.ap, 1.0), writes=[cx.one])
    cx.ldst = gsb("ldst", [128, 128], F32)
    cx.m_incl = gsb("m_incl", [128, 2, 128], F32)
    pg.dma(cx.m_incl.ap, cx.cd["m_incl"].rearrange("d s t -> s d t"), writes=[cx.m_incl])
    cx.m_sa = gsb("m_sa", [128, 2, 128], F32)
    pg.dma(cx.m_sa.ap, cx.cd["m_strict_after"].rearrange("d s t -> s d t"), writes=[cx.m_sa])
    cx.m_dn = gsb("m_dn", [128, 2, 2, 128], F32)
    pg.dma(cx.m_dn.ap[:, 0], cx.cd["m_dn"][0].rearrange("j s t -> s j t"), writes=[cx.m_dn])
    pg.dma(cx.m_dn.ap[:, 1], cx.cd["m_dn"][1].rearrange("j s t -> s j t"), writes=[cx.m_dn])
    cx.chunkind = gsb("chunkind", [128, 2, 128], F32)
    pg.dma(cx.chunkind.ap, cx.cd["chunkind"].rearrange("c s m -> s c m"), writes=[cx.chunkind])
    cx.onesf = gsb("onesf", [128, 128], F32)
    pg.dma(cx.onesf.ap, cx.cd["onesf"], writes=[cx.onesf])
    cx.pm = gsb("pm", [128, 2], F32)
    pg.dma(cx.pm.ap, cx.cd["pm"], writes=[cx.pm])
    cx.halfpi = gsb("halfpi", [128, 1], F32)
    pg.op("dve", lambda e: e.memset(cx.halfpi.ap, float(np.pi / 2)), writes=[cx.halfpi])
    cx.zb = gsb("zb", [128, 2048], BF16)
    pg.op("pool", lambda e: e.memset(cx.zb.ap, 0.0), writes=[cx.zb])
    bidx = {"lru": 0, "gla": 1, "dn": 2, "s5": 3}
    for l in range(depth):
        for s in range(nslot):
            xin = xs[s] if l == 0 else XS[s]
            last = (l == depth - 1)
            xout = ys[s] if last else XS[s]
            pg.barrier()
            with ExitStack() as es:
                phase_P(pg, cx, es, L, xin, l)
                pg.barrier()
            for bn in ("lru", "gla", "dn", "s5"):
                if bn not in branches:
                    b = bidx[bn]
                    for t0 in range(0, L, 2048):
                        tw = min(2048, L - t0)
                        for c in range(4):
                            pg.dma(cx.BT[b * 512 + c * 128:b * 512 + (c + 1) * 128, t0:t0 + tw], cx.zb.ap[:, :tw], reads=[cx.zb])
            if "lru" in branches:
                with ExitStack() as es:
                    phase_LRU(pg, cx, es, L, l)
                    pg.barrier()
            if "s5" in branches:
                with ExitStack() as es:
                    phase_S5(pg, cx, es, L, l)
                    pg.barrier()
            if "gla" in branches:
                with ExitStack() as es:
                    phase_GLA(pg, cx, es, L, l)
                    pg.barrier()
            if "dn" in branches:
                with ExitStack() as es:
                    phase_DN(pg, cx, es, L, l)
                    pg.barrier()
            pg.barrier()
            with ExitStack() as es:
                phase_M(pg, cx, es, L, xin, xout, l, last)
                pg.barrier()
    pg.barrier()
    return nc, pg, hc


_CACHE = {}


def kernel(**inputs):
    L = inputs["x_prompt"].shape[1]
    shapes = {nm: inputs[nm].shape for nm in W_NAMES}
    nc, pg, hc = build(L, shapes)
    xp = np.ascontiguousarray(inputs["x_prompt"], dtype=np.float32)
    xsm = np.ascontiguousarray(inputs["x_sample"], dtype=np.float32)
    wmap = {nm: np.ascontiguousarray(inputs[nm], dtype=np.float32) for nm in W_NAMES}
    in_maps = []
    for c in range(8):
        m = dict(wmap)
        for nm, arr in hc.items():
            m["c_" + nm] = arr
        m["x0"] = xp[c]
        m["x1"] = xsm[c % 2]
        in_maps.append(m)
    res = run_bass_kernel_spmd(nc, in_maps, core_ids=list(range(8)))
    y_prompt = np.stack([np.asarray(res.results[c]["y0"], dtype=np.float32) for c in range(8)], axis=0)
    y_sample = np.stack([np.asarray(res.results[c]["y1"], dtype=np.float32) for c in range(2)], axis=0)
    return (y_prompt, y_sample)
```

```python
import numpy as np
import ml_dtypes
from contextlib import ExitStack
import concourse.bass as bass
import concourse.mybir as mybir
from concourse.bass_utils import run_bass_kernel_spmd

F32 = mybir.dt.float32
BF16 = mybir.dt.bfloat16
ALU = mybir.AluOpType
AF = mybir.ActivationFunctionType

D = 1024
BW = 512
D_IN = 5680
EPS = 1e-6
O_LRU_X, O_LRU_G = 0, 512
O_GLA_Q, O_GLA_K, O_GLA_V, O_GLA_G, O_GLA_LR = 1024, 1280, 1536, 2048, 2560
O_DN_QKV, O_DN_G, O_DN_BA = 2592, 4128, 4640
O_S5_U, O_S5_G = 4656, 5168


SAME_ENGINE_SYNC = True
STORES_ON_POOL = True


class Buf:
    __slots__ = ("ap", "w", "r", "name")

    def __init__(self, ap, name=""):
        self.ap = ap
        self.w = []
        self.r = []
        self.name = name

    def __getitem__(self, k):
        return self.ap[k]


class Prog:
    def __init__(self, nc, n_dma_sems=40):
        self.nc = nc
        self.eng = {"pe": nc.tensor, "act": nc.scalar, "dve": nc.vector, "pool": nc.gpsimd, "sp": nc.sync}
        self.sem = {k: nc.alloc_semaphore("s_" + k) for k in self.eng}
        self.cnt = {k: 0 for k in self.eng}
        self.seen = {k: {} for k in self.eng}
        self.dsem = [nc.alloc_semaphore("d%d" % i) for i in range(n_dma_sems)]
        self.dcnt = [0] * n_dma_sems
        self.dnext = 0
        self.ninst = 0

    def buf(self, ap, name=""):
        return Buf(ap, name)

    def uname(self, name):
        self.uid = getattr(self, "uid", 0) + 1
        return "%s_%d" % (name, self.uid)

    def _wait(self, e, dep):
        if dep[0] == "dma":
            key = ("dma", dep[1]); val = dep[2]
            if self.seen[e].get(key, 0) >= val:
                return
            self.eng[e].wait_ge(self.dsem[dep[1]], val)
        else:
            f, val = dep
            if f == e and (e in ("pe", "sp") or not SAME_ENGINE_SYNC):
                return
            key = f
            if self.seen[e].get(key, 0) >= val:
                return
            self.eng[e].wait_ge(self.sem[f], val)
        self.seen[e][key] = val
        self.ninst += 1

    def _deps(self, e, reads, writes):
        for b in reads:
            for d in b.w:
                self._wait(e, d)
        for b in writes:
            for d in b.w:
                self._wait(e, d)
            for d in b.r:
                self._wait(e, d)

    def op(self, e, inst_fn, reads=(), writes=()):
        self._deps(e, reads, writes)
        inst = inst_fn(self.eng[e])
        inst.then_inc(self.sem[e], 1)
        self.cnt[e] += 1
        me = (e, self.cnt[e])
        for b in reads:
            b.r.append(me)
            if len(b.r) > 24:
                b.r = b.r[-24:] if False else self._compress(b.r)
        for b in writes:
            b.w = [me]
            b.r = []
        self.ninst += 1
        return inst

    @staticmethod
    def _compress(lst):
        best = {}
        for d in lst:
            k = ("dma", d[1]) if d[0] == "dma" else d[0]
            v = d[2] if d[0] == "dma" else d[1]
            if k not in best or v > best[k][0]:
                best[k] = (v, d)
        return [x[1] for x in best.values()]

    def dma(self, out, in_, reads=(), writes=(), q=None, **kw):
        if q is None:
            q = "sp" if (len(writes) > 0 or not STORES_ON_POOL) else "pool"
        self._deps(q, reads, writes)
        j = self.dnext
        self.dnext = (self.dnext + 1) % len(self.dsem)
        if self.dcnt[j] > 0:
            self._wait(q, ("dma", j, self.dcnt[j]))
        self.dcnt[j] += 16
        self.eng[q].dma_start(out=out, in_=in_, **kw).then_inc(self.dsem[j], 16)
        me = ("dma", j, self.dcnt[j])
        for b in reads:
            b.r.append(me)
            if len(b.r) > 24:
                b.r = self._compress(b.r)
        for b in writes:
            b.w = [me]
            b.r = []
        self.ninst += 1

    def barrier(self):
        for e in self.eng:
            for f in self.eng:
                if f != e and self.cnt[f] > 0:
                    self._wait(e, (f, self.cnt[f]))
            for j, c in enumerate(self.dcnt):
                if c > 0:
                    self._wait(e, ("dma", j, c))


class Ctx:
    pass


def mk_psum(pg, nc):
    banks = []
    for i in range(6):
        banks.append(pg.buf(nc.alloc_psum_tensor("psf%d" % i, [128, 512], F32).ap(), "psf%d" % i))
    bb = []
    for i in range(2):
        bb.append(pg.buf(nc.alloc_psum_tensor("psb%d" % i, [128, 1024], BF16).ap(), "psb%d" % i))
    return banks, bb


class Rot:
    def __init__(self, items):
        self.items = items
        self.i = 0

    def get(self):
        x = self.items[self.i]
        self.i = (self.i + 1) % len(self.items)
        return x


PF_LRU_X, PF_LRU_G, PF_GLA_Q, PF_GLA_K, PF_DN_QKV, PF_S5_U, PF_S5_G, PF_GLA_LR = 0, 512, 1024, 1280, 1536, 3072, 3584, 4096
PF_ROWS = 4128
PF_CHUNKS = ([(O_LRU_X + 128 * i, 128) for i in range(4)] + [(O_LRU_G + 128 * i, 128) for i in range(4)]
             + [(O_GLA_Q + 128 * i, 128) for i in range(2)] + [(O_GLA_K + 128 * i, 128) for i in range(2)]
             + [(O_DN_QKV + 128 * i, 128) for i in range(12)] + [(O_S5_U + 128 * i, 128) for i in range(4)]
             + [(O_S5_G + 128 * i, 128) for i in range(4)] + [(O_GLA_LR, 32)])
PT_GLA_K, PT_GLA_V, PT_GLA_G, PT_DN_G, PT_DN_BA = 0, 256, 768, 1280, 1792
PT_COLS = 1808
PT_GROUPS = [(1280, 512, 0), (1792, 512, 512), (2304, 256, 1024), (4128, 512, 1280), (4640, 16, 1792)]


def load_cast_bf16(pg, nc, es, dst, src_ap, rows, cols, name, chunk=2048):
    st = [pg.buf(es.enter_context(nc.sbuf_tensor(name + "_st%d" % i, [128, chunk], F32)).ap()) for i in range(2)]
    i = 0
    for c0 in range(0, cols, chunk):
        cw = min(chunk, cols - c0)
        s = st[i % 2]
        pg.dma(s.ap[:rows, :cw], src_ap[:, c0:c0 + cw], writes=[s])
        if i % 2 == 0:
            pg.op("act", lambda e: e.copy(dst[0][:rows, c0:c0 + cw], s.ap[:rows, :cw]), reads=[s], writes=[dst[1]])
        else:
            pg.op("dve", lambda e: e.tensor_copy(out=dst[0][:rows, c0:c0 + cw], in_=s.ap[:rows, :cw]), reads=[s], writes=[dst[1]])
        i += 1


def phase_P(pg, cx, es, L, x_ap, l):
    nc = cx.nc
    TT = 512
    sb = lambda name, shape, dt: pg.buf(es.enter_context(nc.sbuf_tensor(pg.uname(name), shape, dt)).ap(), name)
    wbf = sb("P_w", [128, 8, D_IN], BF16)
    w_src = cx.w["w_in"][l].rearrange("(k p) c -> p k c", p=128)
    WC = D_IN // 4
    st = [sb("P_wst%d" % i, [128, WC], F32) for i in range(2)]
    for k in range(8):
        for q in range(4):
            s = st[q % 2]
            pg.dma(s.ap, w_src[:, k, q * WC:(q + 1) * WC], writes=[s])
            if q % 2 == 0:
                pg.op("act", lambda e: e.copy(wbf.ap[:, k, q * WC:(q + 1) * WC], s.ap), reads=[s], writes=[wbf])
            else:
                pg.op("dve", lambda e: e.tensor_copy(out=wbf.ap[:, k, q * WC:(q + 1) * WC], in_=s.ap), reads=[s], writes=[wbf])
    gk = sb("P_g", [128, 8], F32)
    load_T(pg, cx, gk, gk.ap, cx.w["norm_g"][l].rearrange("(k p) -> k p", p=128), 8)
    xt = [sb("P_x%d" % i, [128, 4, D], F32) for i in range(2)]
    xs = sb("P_xs", [128, D], BF16)
    junk = sb("P_junk", [128, D], BF16)
    ss = sb("P_ss", [128, 4], F32)
    xnT = [sb("P_xnT%d" % i, [128, 8, TT], BF16) for i in range(2)]
    stf = [sb("P_stf%d" % i, [128, 4, TT], F32) for i in range(2)]
    stt = [sb("P_stt%d" % i, [128, PT_COLS], F32) for i in range(2)]
    XNTv = cx.XNT.rearrange("(k p) t -> p k t", p=128)
    xv = x_ap.rearrange("(n j p) d -> n p j d", p=128, j=4)
    gb = gk.ap.unsqueeze(2).to_broadcast([128, 8, 128])
    nt = L // TT
    evac_i = 0
    for it in range(nt):
        x_b = xt[it % 2]
        pg.dma(x_b.ap, xv[it], writes=[x_b])
        xn = xnT[it % 2]
        for j in range(4):
            pg.op("act", lambda e: e.activation(out=junk.ap, in_=x_b.ap[:, j, :], func=AF.Square, accum_out=ss.ap[:, j:j + 1]),
                  reads=[x_b], writes=[junk, ss])
            pg.op("act", lambda e: e.activation(out=ss.ap[:, j:j + 1], in_=ss.ap[:, j:j + 1], func=AF.Sqrt, scale=1.0 / D, bias=cx.eps.ap[:, 0:1]),
                  reads=[ss, cx.eps], writes=[ss])
            pg.op("dve", lambda e: e.reciprocal(out=ss.ap[:, j:j + 1], in_=ss.ap[:, j:j + 1]), reads=[ss], writes=[ss])
            pg.op("dve", lambda e: e.tensor_scalar(out=xs.ap, in0=x_b.ap[:, j, :], scalar1=ss.ap[:, j:j + 1], scalar2=None, op0=ALU.mult),
                  reads=[x_b, ss], writes=[xs])
            pb = cx.psb.get()
            for k in range(8):
                pg.op("pe", lambda e: e.transpose(out=pb.ap[:, k * 128:(k + 1) * 128], in_=xs.ap[:, k * 128:(k + 1) * 128], identity=cx.identb.ap),
                      reads=[xs, cx.identb], writes=[pb])
            pg.op("dve", lambda e: e.tensor_tensor(out=xn.ap[:, :, j * 128:(j + 1) * 128], in0=pb.ap.rearrange("p (k t) -> p k t", k=8), in1=gb, op=ALU.mult),
                  reads=[pb, gk], writes=[xn])
        pg.dma(XNTv[:, :, it * TT:(it + 1) * TT], xn.ap, reads=[xn])
        for ci, (c0, cw) in enumerate(PF_CHUNKS):
            ps = cx.psf.get()
            for k in range(8):
                pg.op("pe", lambda e: e.matmul(ps.ap[:cw, :], lhsT=wbf.ap[:, k, c0:c0 + cw], rhs=xn.ap[:, k, :], start=(k == 0), stop=(k == 7)),
                      reads=[wbf, xn], writes=[ps])
            sbuf = stf[(ci // 4) % 2]
            evac_i += 1
            if evac_i % 2 == 0:
                pg.op("act", lambda e: e.copy(sbuf.ap[:cw, ci % 4, :], ps.ap[:cw, :]), reads=[ps], writes=[sbuf])
            else:
                pg.op("dve", lambda e: e.tensor_copy(out=sbuf.ap[:cw, ci % 4, :], in_=ps.ap[:cw, :]), reads=[ps], writes=[sbuf])
            if ci % 4 == 3:
                cb = ci // 4
                pg.dma(PFv_slice(cx, cb * 4, 4, it * TT, TT), sbuf.ap, reads=[sbuf])
            elif ci == len(PF_CHUNKS) - 1:
                pg.dma(cx.PF[4096:4128, it * TT:(it + 1) * TT], sbuf.ap[:32, 0, :], reads=[sbuf])
        for j in range(4):
            sbuf = stt[j % 2]
            for (c0, cw, o0) in PT_GROUPS:
                ps = cx.psf.get()
                for k in range(8):
                    pg.op("pe", lambda e: e.matmul(ps.ap[:, :cw], lhsT=xn.ap[:, k, j * 128:(j + 1) * 128], rhs=wbf.ap[:, k, c0:c0 + cw], start=(k == 0), stop=(k == 7)),
                          reads=[wbf, xn], writes=[ps])
                evac_i += 1
                if evac_i % 2 == 0:
                    pg.op("act", lambda e: e.copy(sbuf.ap[:, o0:o0 + cw], ps.ap[:, :cw]), reads=[ps], writes=[sbuf])
                else:
                    pg.op("dve", lambda e: e.tensor_copy(out=sbuf.ap[:, o0:o0 + cw], in_=ps.ap[:, :cw]), reads=[ps], writes=[sbuf])
            t0 = it * TT + j * 128
            pg.dma(cx.PT[t0:t0 + 128, :], sbuf.ap, reads=[sbuf])


def PFv_slice(cx, c0, nch, t0, tw):
    return cx.PF[c0 * 128:(c0 + nch) * 128, t0:t0 + tw].rearrange("(c p) t -> p c t", p=128)


def load_T(pg, cx, dst, dst_ap, src_ap, n, st_view=None, wd=128, **kw):
    st = cx.ldst
    pg.dma(st.ap[:n, :wd] if st_view is None else st_view(st.ap[:n, :wd]), src_ap, writes=[st], **kw)
    ps = cx.psf.get()
    pg.op("pe", lambda e: e.transpose(out=ps.ap[:wd, :n], in_=st.ap[:n, :wd], identity=cx.identf.ap[:n, :n]), reads=[st, cx.identf], writes=[ps])
    pg.op("dve", lambda e: e.tensor_copy(out=dst_ap, in_=ps.ap[:wd, :n]), reads=[ps], writes=[dst])


def phase_LRU(pg, cx, es, L, l):
    nc = cx.nc
    sb = lambda name, shape, dt: pg.buf(es.enter_context(nc.sbuf_tensor(pg.uname(name), shape, dt)).ap(), name)
    w = cx.w
    TL = min(2048, L)
    ntile = L // TL
    cw = sb("L_cw", [128, 4, 4], F32)
    load_T(pg, cx, cw, cw.ap.rearrange("p j c -> p (j c)"), w["lru_conv_w"][l].rearrange("j (c p) -> (j c) p", p=128), 16)
    cb = sb("L_cb", [128, 4], F32)
    load_T(pg, cx, cb, cb.ap, w["lru_conv_b"][l].rearrange("(c p) -> c p", p=128), 4)
    bias = sb("L_bias", [128, 2, 2, 4], F32)
    load_T(pg, cx, bias, bias.ap[:, 0].rearrange("p d c -> p (d c)"), w["lru_b_a"][l].rearrange("d (c p) -> (d c) p", p=128), 8)
    load_T(pg, cx, bias, bias.ap[:, 1].rearrange("p d c -> p (d c)"), w["lru_b_x"][l].rearrange("d (c p) -> (d c) p", p=128), 8)
    lam = sb("L_lam", [128, 2, 4], F32)
    load_T(pg, cx, lam, lam.ap.rearrange("p d c -> p (d c)"), w["lru_lambda"][l].rearrange("d (c p) -> (d c) p", p=128), 8)
    coef = sb("L_coef", [128, 2, 4], F32)
    coef2 = sb("L_coef2", [128, 2, 4], F32)
    pg.op("act", lambda e: e.activation(out=coef.ap, in_=lam.ap, func=AF.Exp, scale=-1.0), reads=[lam], writes=[coef])
    pg.op("act", lambda e: e.activation(out=coef.ap, in_=coef.ap, func=AF.Ln, bias=cx.one.ap[:, 0:1]), reads=[coef, cx.one], writes=[coef])
    pg.op("dve", lambda e: e.tensor_scalar(out=coef2.ap, in0=coef.ap, scalar1=-16.0, scalar2=None, op0=ALU.mult), reads=[coef], writes=[coef2])
    pg.op("dve", lambda e: e.tensor_scalar(out=coef.ap, in0=coef.ap, scalar1=-8.0, scalar2=None, op0=ALU.mult), reads=[coef], writes=[coef])
    wg = sb("L_wg", [128, 2, 2, 4, 128], BF16)
    wst = sb("L_wst", [128, 2, 4, 128], F32)
    for ai, nm in enumerate(("lru_w_a", "lru_w_x")):
        pg.dma(wst.ap, w[nm][l].rearrange("d h i j -> i d h j"), writes=[wst])
        pg.op("dve", lambda e: e.tensor_copy(out=wg.ap[:, ai], in_=wst.ap), reads=[wst], writes=[wg])
    XC = sb("L_XC", [128, L], F32)
    XCB = sb("L_XCB", [128, L], BF16)
    HF = sb("L_HF", [128, L], F32)
    xin = sb("L_xin", [128, TL + 3], F32)
    rt = sb("L_r", [128, TL], F32)
    itl = sb("L_i", [128, TL], F32)
    at = sb("L_a", [128, TL], F32)
    t2 = sb("L_t2", [128, TL], F32)
    gt = sb("L_g", [128, TL], F32)
    yb = sb("L_y", [128, TL], BF16)
    carry = sb("L_carry", [128, 1], F32)
    for c in range(4):
        prow = PF_LRU_X + c * 128
        for it in range(ntile):
            t0 = it * TL
            lo = max(t0 - 2, 0)
            hi = min(t0 + TL + 1, L)
            if it == 0 or it == ntile - 1:
                pg.op("pool", lambda e: e.memset(xin.ap, 0.0), writes=[xin])
            pg.dma(xin.ap[:, lo - (t0 - 2):hi - (t0 - 2)], cx.PF[prow:prow + 128, lo:hi], writes=[xin])
            xo = XC.ap[:, t0:t0 + TL]
            pg.op("dve", lambda e: e.tensor_scalar(out=xo, in0=xin.ap[:, 0:TL], scalar1=cw.ap[:, 0, c:c + 1], scalar2=cb.ap[:, c:c + 1], op0=ALU.mult, op1=ALU.add),
                  reads=[xin, cw, cb], writes=[XC])
            for j in range(1, 4):
                pg.op("dve", lambda e: e.scalar_tensor_tensor(out=xo, in0=xin.ap[:, j:j + TL], scalar=cw.ap[:, j, c:c + 1], in1=xo, op0=ALU.mult, op1=ALU.add),
                      reads=[xin, cw, XC], writes=[XC])
            pg.op("act", lambda e: e.copy(XCB.ap[:, t0:t0 + TL], xo), reads=[XC], writes=[XCB])
        for d in range(2):
            order = range(ntile) if d == 0 else range(ntile - 1, -1, -1)
            for n_i, it in enumerate(order):
                t0 = it * TL
                for s0 in range(0, TL, 512):
                    for ai, dst in ((0, rt), (1, itl)):
                        ps = cx.psf.get()
                        pg.op("pe", lambda e: e.matmul(ps.ap, lhsT=wg.ap[:, ai, d, c, :], rhs=XCB.ap[:, t0 + s0:t0 + s0 + 512], start=True, stop=True),
                              reads=[wg, XCB], writes=[ps])
                        pg.op("act", lambda e: e.activation(out=dst.ap[:, s0:s0 + 512], in_=ps.ap, func=AF.Sigmoid, bias=bias.ap[:, ai, d, c:c + 1]),
                              reads=[ps, bias], writes=[dst])
                pg.op("act", lambda e: e.activation(out=at.ap, in_=rt.ap, func=AF.Exp, scale=coef.ap[:, d, c:c + 1]), reads=[rt, coef], writes=[at])
                pg.op("act", lambda e: e.activation(out=t2.ap, in_=rt.ap, func=AF.Exp, scale=coef2.ap[:, d, c:c + 1]), reads=[rt, coef2], writes=[t2])
                pg.op("act", lambda e: e.activation(out=t2.ap, in_=t2.ap, func=AF.Sqrt, scale=-1.0, bias=cx.one.ap[:, 0:1]), reads=[t2, cx.one], writes=[t2])
                pg.op("pool", lambda e: e.tensor_tensor(out=itl.ap, in0=itl.ap, in1=XC.ap[:, t0:t0 + TL], op=ALU.mult), reads=[itl, XC], writes=[itl])
                pg.op("dve", lambda e: e.tensor_tensor(out=t2.ap, in0=t2.ap, in1=itl.ap, op=ALU.mult), reads=[t2, itl], writes=[t2])
                init = 0.0 if n_i == 0 else carry.ap[:, 0:1]
                rds = [at, t2] + ([] if n_i == 0 else [carry])
                if d == 0:
                    ho = HF.ap[:, t0:t0 + TL]
                    pg.op("dve", lambda e: e.tensor_tensor_scan(out=ho, data0=at.ap, data1=t2.ap, initial=init, op0=ALU.mult, op1=ALU.add),
                          reads=rds, writes=[HF])
                    pg.op("dve", lambda e: e.tensor_copy(out=carry.ap, in_=HF.ap[:, t0 + TL - 1:t0 + TL]), reads=[HF], writes=[carry])
                else:
                    rv = lambda ap: bass.AP(ap.tensor, ap.offset + TL - 1, [list(ap.ap[0]), [-1, TL]])
                    pg.op("dve", lambda e: e.tensor_tensor_scan(out=rv(rt.ap), data0=rv(at.ap), data1=rv(t2.ap), initial=init, op0=ALU.mult, op1=ALU.add),
                          reads=rds, writes=[rt])
                    pg.op("dve", lambda e: e.tensor_copy(out=carry.ap, in_=rt.ap[:, 0:1]), reads=[rt], writes=[carry])
                    grow = PF_LRU_G + c * 128
                    pg.dma(gt.ap, cx.PF[grow:grow + 128, t0:t0 + TL], writes=[gt])
                    pg.op("act", lambda e: e.activation(out=gt.ap, in_=gt.ap, func=AF.Silu), reads=[gt], writes=[gt])
                    pg.op("pool", lambda e: e.tensor_tensor(out=rt.ap, in0=rt.ap, in1=HF.ap[:, t0:t0 + TL], op=ALU.add), reads=[rt, HF], writes=[rt])
                    pg.op("dve", lambda e: e.tensor_tensor(out=yb.ap, in0=rt.ap, in1=gt.ap, op=ALU.mult), reads=[rt, gt], writes=[yb])
                    pg.dma(cx.BT[c * 128:(c + 1) * 128, t0:t0 + TL], yb.ap, reads=[yb])


def phase_M(pg, cx, es, L, x_ap, xout_ap, l, last):
    nc = cx.nc
    TT = 512
    sb = lambda name, shape, dt: pg.buf(es.enter_context(nc.sbuf_tensor(pg.uname(name), shape, dt)).ap(), name)
    w = cx.w
    wmg = sb("M_wmg", [128, 4, 8, D], BF16)
    wbr = sb("M_wbr", [128, 4, 4, D], BF16)
    wout = sb("M_wout", [128, 8, D], BF16)
    st = [sb("M_st%d" % i, [128, D], F32) for i in range(2)]
    jobs = []
    for n in range(4):
        for k in range(8):
            jobs.append((w["w_merge_gate"][l, n, k * 128:(k + 1) * 128, :], wmg, wmg.ap[:, n, k, :]))
        for k in range(4):
            jobs.append((w["w_branch"][l, n, k * 128:(k + 1) * 128, :], wbr, wbr.ap[:, n, k, :]))
    for k in range(8):
        jobs.append((w["w_out"][l, k * 128:(k + 1) * 128, :], wout, wout.ap[:, k, :]))
    for i, (src, dbuf, dap) in enumerate(jobs):
        s = st[i % 2]
        pg.dma(s.ap, src, writes=[s])
        if i % 2 == 0:
            pg.op("act", lambda e: e.copy(dap, s.ap), reads=[s], writes=[dbuf])
        else:
            pg.op("dve", lambda e: e.tensor_copy(out=dap, in_=s.ap), reads=[s], writes=[dbuf])
    bmg = sb("M_bmg", [128, 4, 8], F32)
    load_T(pg, cx, bmg, bmg.ap.rearrange("p n c -> p (n c)"), w["b_merge_gate"][l].rearrange("n (c p) -> (n c) p", p=128), 32)
    if last:
        fg = sb("M_fg", [128, D], F32)
        fsrc = w["final_norm_g"]
        pg.dma(fg.ap, bass.AP(fsrc.tensor, fsrc.offset, [[0, 128], [1, D]]), writes=[fg])
        ss = sb("M_ss", [128, 4], F32)
        junk = sb("M_junk", [128, D], BF16)
    xn_ = [sb("M_xn%d" % i, [128, 8, TT], BF16) for i in range(2)]
    bt = sb("M_bt", [128, 16, TT], BF16)
    xt = sb("M_x", [128, 4, D], F32)
    mg = sb("M_mg", [128, 8, TT], BF16)
    gsb = sb("M_g", [128, TT], F32)
    tmp = sb("M_tmp", [128, TT], F32)
    acc = sb("M_acc", [128, TT], F32)
    XNTv = cx.XNT.rearrange("(k p) t -> p k t", p=128)
    BTv = cx.BT.rearrange("(k p) t -> p k t", p=128)
    xv = x_ap.rearrange("(n j p) d -> n p j d", p=128, j=4)
    ov = xout_ap.rearrange("(n j p) d -> n p j d", p=128, j=4)
    for it in range(L // TT):
        ts = slice(it * TT, (it + 1) * TT)
        xn = xn_[it % 2]
        pg.dma(xn.ap, XNTv[:, :, ts], writes=[xn])
        pg.dma(bt.ap, BTv[:, :, ts], writes=[bt])
        pg.dma(xt.ap, xv[it], writes=[xt])
        for oc in range(8):
            ocs = slice(oc * 128, (oc + 1) * 128)
            for n in range(4):
                pg_ = cx.psf.get()
                for k in range(8):
                    pg.op("pe", lambda e: e.matmul(pg_.ap, lhsT=wmg.ap[:, n, k, ocs], rhs=xn.ap[:, k, :], start=(k == 0), stop=(k == 7)),
                          reads=[wmg, xn], writes=[pg_])
                pg.op("act", lambda e: e.activation(out=gsb.ap, in_=pg_.ap, func=AF.Sigmoid, bias=bmg.ap[:, n, oc:oc + 1]), reads=[pg_, bmg], writes=[gsb])
                pb = cx.psf.get()
                for k in range(4):
                    pg.op("pe", lambda e: e.matmul(pb.ap, lhsT=wbr.ap[:, n, k, ocs], rhs=bt.ap[:, n * 4 + k, :], start=(k == 0), stop=(k == 3)),
                          reads=[wbr, bt], writes=[pb])
                if n == 0:
                    pg.op("dve", lambda e: e.tensor_tensor(out=acc.ap, in0=pb.ap, in1=gsb.ap, op=ALU.mult), reads=[pb, gsb], writes=[acc])
                else:
                    pg.op("dve", lambda e: e.tensor_tensor(out=tmp.ap, in0=pb.ap, in1=gsb.ap, op=ALU.mult), reads=[pb, gsb], writes=[tmp])
                    if n < 3:
                        pg.op("pool", lambda e: e.tensor_tensor(out=acc.ap, in0=acc.ap, in1=tmp.ap, op=ALU.add), reads=[acc, tmp], writes=[acc])
                    else:
                        pg.op("pool", lambda e: e.tensor_tensor(out=mg.ap[:, oc, :], in0=acc.ap, in1=tmp.ap, op=ALU.add), reads=[acc, tmp], writes=[mg])
        for j in range(4):
            for hf in range(2):
                hs = slice(hf * 512, (hf + 1) * 512)
                ps = cx.psf.get()
                for k in range(8):
                    pg.op("pe", lambda e: e.matmul(ps.ap, lhsT=mg.ap[:, k, j * 128:(j + 1) * 128], rhs=wout.ap[:, k, hs], start=(k == 0), stop=(k == 7)),
                          reads=[mg, wout], writes=[ps])
                pg.op("dve", lambda e: e.tensor_tensor(out=xt.ap[:, j, hs], in0=ps.ap, in1=xt.ap[:, j, hs], op=ALU.add), reads=[ps, xt], writes=[xt])
            if last:
                pg.op("act", lambda e: e.activation(out=junk.ap, in_=xt.ap[:, j, :], func=AF.Square, accum_out=ss.ap[:, j:j + 1]), reads=[xt], writes=[junk, ss])
                pg.op("act", lambda e: e.activation(out=ss.ap[:, j:j + 1], in_=ss.ap[:, j:j + 1], func=AF.Sqrt, scale=1.0 / D, bias=cx.eps.ap[:, 0:1]),
                      reads=[ss, cx.eps], writes=[ss])
                pg.op("dve", lambda e: e.reciprocal(out=ss.ap[:, j:j + 1], in_=ss.ap[:, j:j + 1]), reads=[ss], writes=[ss])
                pg.op("dve", lambda e: e.scalar_tensor_tensor(out=xt.ap[:, j, :], in0=xt.ap[:, j, :], scalar=ss.ap[:, j:j + 1], in1=fg.ap, op0=ALU.mult, op1=ALU.mult),
                      reads=[xt, ss, fg], writes=[xt])
        pg.dma(ov[it], xt.ap, reads=[xt])


def phase_GLA(pg, cx, es, L, l):
    nc = cx.nc
    sb = lambda name, shape, dt: pg.buf(es.enter_context(nc.sbuf_tensor(pg.uname(name), shape, dt)).ap(), name)
    w = cx.w
    NB = L // 128
    wup = sb("G_wup", [32, 2, 256], F32)
    for d in range(2):
        pg.dma(wup.ap[0:16, d, :], w["gla_w_up"][l, d], writes=[wup])
        pg.dma(wup.ap[16:17, d, :], w["gla_b_up"][l, d:d + 1, :], writes=[wup])
    gn = sb("G_gn", [128, 128], F32)
    gsrc = w["gla_norm_g"][l]
    pg.dma(gn.ap, bass.AP(gsrc.tensor, gsrc.offset, [[0, 128], [1, 128]]), writes=[gn])
    lrT = [sb("G_lrT%d" % i, [32, 128], F32) for i in range(2)]
    for b in lrT:
        pg.op("dve", lambda e: e.memset(b.ap, 1.0), writes=[b])
    qk = [sb("G_qk%d" % i, [128, 4, 128], F32) for i in range(2)]
    tk = [sb("G_tk%d" % i, [128, 1280], F32) for i in range(2)]
    obt = [sb("G_ob%d" % i, [128, 512], F32) for i in range(2)]
    la_ = [sb("G_la%d" % i, [128, 256], F32) for i in range(2)]
    e1_ = [sb("G_e1%d" % i, [128, 256], F32) for i in range(2)]
    eb_ = [sb("G_eb%d" % i, [128, 2, 128], F32) for i in range(2)]
    enb_ = [sb("G_enb%d" % i, [128, 2, 128], F32) for i in range(2)]
    qd_ = [sb("G_qd%d" % i, [128, 2, 128], BF16) for i in range(2)]
    ki_ = [sb("G_ki%d" % i, [128, 2, 128], BF16) for i in range(2)]
    ed_ = [sb("G_ed%d" % i, [128, 256], F32) for i in range(2)]
    kend_ = [sb("G_kend%d" % i, [128, 256], BF16) for i in range(2)]
    vb_ = [sb("G_vb%d" % i, [128, 512], BF16) for i in range(2)]
    sm = [sb("G_sm%d" % i, [128, 128], BF16) for i in range(4)]
    pre_ps = Rot([cx.psf.items[4], cx.pso])
    S32 = [sb("G_S32_%d" % h, [128, 128], F32) for h in range(4)]
    Sb = [sb("G_Sb_%d" % h, [128, 128], BF16) for h in range(4)]
    osb = sb("G_osb", [128, 512], F32)
    ssq = sb("G_ssq", [128, 4], F32)
    junk = sb("G_junk", [128, 128], BF16)
    ysb = sb("G_ysb", [128, 512], BF16)
    yT = sb("G_yT", [128, 4, 128], BF16)
    PFq = cx.PF[PF_GLA_Q:PF_GLA_Q + 512, :].rearrange("(c p) t -> p c t", p=128)
    for d in (1, 0):
        pg.barrier()
        for h in range(4):
            pg.op("dve", lambda e: e.memset(S32[h].ap, 0.0), writes=[S32[h]])
            pg.op("pool", lambda e: e.memset(Sb[h].ap, 0.0), writes=[Sb[h]])
        order = range(NB) if d == 0 else range(NB - 1, -1, -1)
        def pre_gen(bi, blk, d=d):
            t0 = blk * 128
            ts = slice(t0, t0 + 128)
            qkb = qk[bi % 2]; tkb = tk[bi % 2]; lrb = lrT[bi % 2]; ob = obt[bi % 2]
            la, e1, eb, enb, qd, ki, ed, kend, vb = [x_[bi % 2] for x_ in (la_, e1_, eb_, enb_, qd_, ki_, ed_, kend_, vb_)]
            pg.dma(qkb.ap, PFq[:, :, ts], writes=[qkb])
            pg.dma(tkb.ap, cx.PT[ts, 0:1280], writes=[tkb])
            pg.dma(lrb.ap[0:16, :], cx.PF[PF_GLA_LR + 16 * d:PF_GLA_LR + 16 * d + 16, ts], writes=[lrb])
            if d == 0:
                pg.dma(ob.ap, cx.OB[ts, 0:512], writes=[ob])
            zp = pre_ps.get()
            pg.op("pe", lambda e: e.matmul(zp.ap[:, :256], lhsT=lrb.ap[0:17, :], rhs=wup.ap[0:17, d, :], start=True, stop=True), reads=[lrb, wup], writes=[zp])
            pg.op("act", lambda e: e.activation(out=e1.ap, in_=zp.ap[:, :256], func=AF.Exp, scale=-1.0), reads=[zp], writes=[e1])
            yield
            pg.op("act", lambda e: e.activation(out=e1.ap, in_=e1.ap, func=AF.Ln, bias=cx.one.ap[:, 0:1]), reads=[e1, cx.one], writes=[e1])
            yield
            pg.op("dve", lambda e: e.tensor_scalar(out=la.ap, in0=e1.ap, scalar1=-1.0 / 16.0, scalar2=None, op0=ALU.mult), reads=[e1], writes=[la])
            yield
            bp = pre_ps.get()
            for h2 in range(2):
                pg.op("pe", lambda e: e.matmul(bp.ap[:, h2 * 128:(h2 + 1) * 128], lhsT=la.ap[:, h2 * 128:(h2 + 1) * 128], rhs=cx.m_incl.ap[:, d, :], start=True, stop=True),
                      reads=[la, cx.m_incl], writes=[bp])
            bp3 = bp.ap[:, 0:256].rearrange("p (c t) -> p c t", c=2)
            pg.op("act", lambda e: e.activation(out=eb.ap, in_=bp3, func=AF.Exp), reads=[bp], writes=[eb])
            yield
            pg.op("act", lambda e: e.activation(out=enb.ap, in_=bp3, func=AF.Exp, scale=-1.0), reads=[bp], writes=[enb])
            yield
            pg.op("dve", lambda e: e.scalar_tensor_tensor(out=qd.ap, in0=qkb.ap[:, 0:2, :], scalar=0.125, in1=eb.ap, op0=ALU.mult, op1=ALU.mult), reads=[qkb, eb], writes=[qd])
            yield
            pg.op("pool", lambda e: e.tensor_tensor(out=ki.ap, in0=qkb.ap[:, 2:4, :], in1=enb.ap, op=ALU.mult), reads=[qkb, enb], writes=[ki])
            yield
            dp = pre_ps.get()
            pg.op("pe", lambda e: e.matmul(dp.ap[:, :256], lhsT=cx.m_sa.ap[:, d, :], rhs=la.ap, start=True, stop=True), reads=[la, cx.m_sa], writes=[dp])
            pg.op("act", lambda e: e.activation(out=ed.ap, in_=dp.ap[:, :256], func=AF.Exp), reads=[dp], writes=[ed])
            yield
            pg.op("dve", lambda e: e.tensor_tensor(out=kend.ap, in0=tkb.ap[:, 0:256], in1=ed.ap, op=ALU.mult), reads=[tkb, ed], writes=[kend])
            yield
            pg.op("pool", lambda e: e.tensor_copy(out=vb.ap, in_=tkb.ap[:, 256:768]), reads=[tkb], writes=[vb])
            yield

        order_l = list(order)
        for _ in pre_gen(0, order_l[0]):
            pass
        for bi, blk in enumerate(order_l):
            t0 = blk * 128
            ts = slice(t0, t0 + 128)
            qkb = qk[bi % 2]; tkb = tk[bi % 2]; lrb = lrT[bi % 2]; ob = obt[bi % 2]
            la, e1, eb, enb, qd, ki, ed, kend, vb = [x_[bi % 2] for x_ in (la_, e1_, eb_, enb_, qd_, ki_, ed_, kend_, vb_)]
            chunks = (0, 1) if d == 0 else (1, 0)

            def head_gen(h, d=d, chunks=chunks):
                h2, hp = h // 2, (h % 2) * 64
                hc = slice(h * 128, (h + 1) * 128)
                bank = cx.psf.items[h]
                o_ps = bank.ap[:, 384:512]
                pg.op("pe", lambda e: e.matmul(bank.ap[:, 0:128], lhsT=ki.ap[hp:hp + 64, h2, :], rhs=qd.ap[hp:hp + 64, h2, :], start=True, stop=True), reads=[ki, qd], writes=[bank])
                yield
                smb = sm[h]
                pg.op("dve", lambda e: e.tensor_tensor(out=smb.ap, in0=bank.ap[:, 0:128], in1=cx.m_incl.ap[:, d, :], op=ALU.mult), reads=[bank, cx.m_incl], writes=[smb])
                yield
                r0 = chunks[0] * 64
                pg.op("pe", lambda e: e.matmul(o_ps, lhsT=smb.ap, rhs=vb.ap[:, hc], start=True, stop=False), reads=[smb, vb], writes=[bank])
                pg.op("pe", lambda e: e.matmul(bank.ap[r0:r0 + 64, 384:512], lhsT=qd.ap[hp:hp + 64, h2, r0:r0 + 64], rhs=Sb[h].ap[hp:hp + 64, :], start=False, stop=True),
                      reads=[qd, Sb[h]], writes=[bank])
                for ci, c in enumerate(chunks):
                    r0 = c * 64
                    if ci == 1:
                        pg.op("pe", lambda e: e.matmul(bank.ap[r0:r0 + 64, 256:384], lhsT=qd.ap[hp:hp + 64, h2, r0:r0 + 64], rhs=Sb[h].ap[hp:hp + 64, :], start=True, stop=True),
                              reads=[qd, Sb[h]], writes=[bank])
                    pg.op("pe", lambda e: e.matmul(bank.ap[hp:hp + 64, 128:256], lhsT=kend.ap[r0:r0 + 64, h * 64:(h + 1) * 64], rhs=vb.ap[r0:r0 + 64, hc], start=True, stop=True),
                          reads=[kend, vb], writes=[bank])
                    yield
                    col = r0 + 63 if d == 0 else r0
                    pg.op("dve", lambda e: e.scalar_tensor_tensor(out=S32[h].ap[hp:hp + 64, :], in0=S32[h].ap[hp:hp + 64, :], scalar=eb.ap[hp:hp + 64, h2, col:col + 1],
                                                                  in1=bank.ap[hp:hp + 64, 128:256], op0=ALU.mult, op1=ALU.add), reads=[S32[h], eb, bank], writes=[S32[h]])
                    yield
                    pg.op("act", lambda e: e.copy(Sb[h].ap[hp:hp + 64, :], S32[h].ap[hp:hp + 64, :]), reads=[S32[h]], writes=[Sb[h]])
                    yield
                r1 = chunks[1] * 64
                pg.op("dve", lambda e: e.tensor_copy(out=osb.ap[:, hc], in_=o_ps), reads=[bank], writes=[osb])
                pg.op("dve", lambda e: e.tensor_tensor(out=osb.ap[r1:r1 + 64, hc], in0=bank.ap[r1:r1 + 64, 256:384], in1=osb.ap[r1:r1 + 64, hc], op=ALU.add), reads=[bank, osb], writes=[osb])

            gens = [head_gen(h) for h in range(4)] + ([pre_gen(bi + 1, order_l[bi + 1])] if bi + 1 < NB else [])
            while gens:
                for gnr in list(gens):
                    try:
                        next(gnr)
                    except StopIteration:
                        gens.remove(gnr)
            if d == 1:
                pg.dma(cx.OB[ts, 0:512], osb.ap, reads=[osb])
            else:
                pg.op("pool", lambda e: e.tensor_tensor(out=osb.ap, in0=osb.ap, in1=ob.ap, op=ALU.add), reads=[osb, ob], writes=[osb])
                head_norm_gate_store(pg, cx, osb, ssq, junk, gn, tkb, 768, ysb, yT, 512, ts)


def head_norm_gate_store(pg, cx, osb, ssq, junk, gn, tkb, gcol, ysb, yT, bt_row0, ts):
    for h in range(4):
        hc = slice(h * 128, (h + 1) * 128)
        pg.op("act", lambda e: e.activation(out=junk.ap, in_=osb.ap[:, hc], func=AF.Square, accum_out=ssq.ap[:, h:h + 1]), reads=[osb], writes=[junk, ssq])
    pg.op("act", lambda e: e.activation(out=ssq.ap, in_=ssq.ap, func=AF.Sqrt, scale=1.0 / 128.0, bias=cx.eps.ap[:, 0:1]), reads=[ssq, cx.eps], writes=[ssq])
    pg.op("dve", lambda e: e.reciprocal(out=ssq.ap, in_=ssq.ap), reads=[ssq], writes=[ssq])
    for h in range(4):
        hc = slice(h * 128, (h + 1) * 128)
        pg.op("dve", lambda e: e.scalar_tensor_tensor(out=osb.ap[:, hc], in0=osb.ap[:, hc], scalar=ssq.ap[:, h:h + 1], in1=gn.ap, op0=ALU.mult, op1=ALU.mult),
              reads=[osb, ssq, gn], writes=[osb])
    pg.op("act", lambda e: e.activation(out=tkb.ap[:, gcol:gcol + 512], in_=tkb.ap[:, gcol:gcol + 512], func=AF.Silu), reads=[tkb], writes=[tkb])
    pg.op("dve", lambda e: e.tensor_tensor(out=ysb.ap, in0=osb.ap, in1=tkb.ap[:, gcol:gcol + 512], op=ALU.mult), reads=[osb, tkb], writes=[ysb])
    pb = cx.psb.get()
    for h in range(4):
        pg.op("pe", lambda e: e.transpose(out=pb.ap[:, h * 128:(h + 1) * 128], in_=ysb.ap[:, h * 128:(h + 1) * 128], identity=cx.identb.ap), reads=[ysb, cx.identb], writes=[pb])
    pg.op("act", lambda e: e.copy(yT.ap, pb.ap[:, 0:512].rearrange("p (c t) -> p c t", c=4)), reads=[pb], writes=[yT])
    pg.dma(cx.BT[bt_row0:bt_row0 + 512, ts].rearrange("(c p) t -> p c t", p=128), yT.ap, reads=[yT])


def phase_DN(pg, cx, es, L, l):
    nc = cx.nc
    w = cx.w
    NB = L // 128
    with ExitStack() as es0:
        sb = lambda name, shape, dt: pg.buf(es0.enter_context(nc.sbuf_tensor(pg.uname(name), shape, dt)).ap(), name)
        TL = 512
        cwD = sb("D0_cw", [128, 4, 12], F32)
        load_T(pg, cx, cwD, cwD.ap.rearrange("p j c -> p (j c)"), w["dn_conv_w"][l].rearrange("j (c p) -> (j c) p", p=128), 48)
        xin = [sb("D0_xin%d" % i, [128, TL + 3], F32) for i in range(2)]
        xc = sb("D0_xc", [128, TL], F32)
        sq = sb("D0_sq", [128, TL], F32)
        rs = sb("D0_rs", [128, TL], F32)
        fm = sb("D0_fm", [128, 12, TL], BF16)
        tm = sb("D0_tm", [128, 4, 1024], BF16)
        nt = L // TL
        for it in range(nt):
            t0 = it * TL
            lo, hi = max(t0 - 2, 0), min(t0 + TL + 1, L)
            for c in range(12):
                xb = xin[c % 2]
                if it == 0 or it == nt - 1:
                    pg.op("pool", lambda e: e.memset(xb.ap, 0.0), writes=[xb])
                prow = PF_DN_QKV + c * 128
                pg.dma(xb.ap[:, lo - (t0 - 2):hi - (t0 - 2)], cx.PF[prow:prow + 128, lo:hi], writes=[xb])
                pg.op("dve", lambda e: e.tensor_scalar(out=xc.ap, in0=xb.ap[:, 0:TL], scalar1=cwD.ap[:, 0, c:c + 1], scalar2=None, op0=ALU.mult), reads=[xb, cwD], writes=[xc])
                for j in range(1, 4):
                    pg.op("dve", lambda e: e.scalar_tensor_tensor(out=xc.ap, in0=xb.ap[:, j:j + TL], scalar=cwD.ap[:, j, c:c + 1], in1=xc.ap, op0=ALU.mult, op1=ALU.add),
                          reads=[xb, cwD, xc], writes=[xc])
                if c >= 8:
                    pg.op("act", lambda e: e.activation(out=fm.ap[:, c, :], in_=xc.ap, func=AF.Silu), reads=[xc], writes=[fm])
                else:
                    pg.op("act", lambda e: e.activation(out=xc.ap, in_=xc.ap, func=AF.Silu), reads=[xc], writes=[xc])
                    pg.op("pool", lambda e: e.tensor_tensor(out=sq.ap, in0=xc.ap, in1=xc.ap, op=ALU.mult), reads=[xc], writes=[sq])
                    ps = cx.psf.get()
                    pg.op("pe", lambda e: e.matmul(ps.ap, lhsT=cx.onesf.ap, rhs=sq.ap, start=True, stop=True), reads=[cx.onesf, sq], writes=[ps])
                    pg.op("act", lambda e: e.activation(out=rs.ap, in_=ps.ap, func=AF.Sqrt, bias=cx.eps.ap[:, 0:1]), reads=[ps, cx.eps], writes=[rs])
                    pg.op("dve", lambda e: e.reciprocal(out=rs.ap, in_=rs.ap), reads=[rs], writes=[rs])
                    sc = (128.0 ** -0.5) if c < 4 else 1.0
                    pg.op("dve", lambda e: e.scalar_tensor_tensor(out=fm.ap[:, c, :], in0=xc.ap, scalar=sc, in1=rs.ap, op0=ALU.mult, op1=ALU.mult), reads=[xc, rs], writes=[fm])
            pg.dma(cx.QKT[:, t0:t0 + TL].rearrange("(c p) t -> p c t", p=128), fm.ap[:, 0:8, :], reads=[fm])
            for j in range(4):
                pb = cx.psb.get()
                for c in range(8):
                    pg.op("pe", lambda e: e.transpose(out=pb.ap[:, c * 128:(c + 1) * 128], in_=fm.ap[:, 4 + c, j * 128:(j + 1) * 128], identity=cx.identb.ap),
                          reads=[fm, cx.identb], writes=[pb])
                pg.op("act", lambda e: e.copy(tm.ap[:, j, :], pb.ap), reads=[pb], writes=[tm])
            pg.dma(cx.KVT[t0:t0 + TL, :].rearrange("(j p) c -> p j c", p=128), tm.ap, reads=[tm])
        pg.barrier()
    sb = lambda name, shape, dt: pg.buf(es.enter_context(nc.sbuf_tensor(pg.uname(name), shape, dt)).ap(), name)
    ba = sb("D_ba", [128, NB, 16], F32)
    pg.dma(ba.ap, cx.PT[:, PT_DN_BA:PT_DN_BA + 16].rearrange("(n p) c -> p n c", p=128), writes=[ba])
    ba4 = ba.ap.rearrange("p n (d j h) -> p n d j h", d=2, j=2)
    dtb = sb("D_dtb", [128, 8], F32)
    nea = sb("D_nea", [128, 8], F32)
    s1 = w["dn_dt_bias"][l]
    pg.dma(dtb.ap, bass.AP(s1.tensor, s1.offset, [[0, 128], [1, 8]]), writes=[dtb])
    s2 = w["dn_a_log"][l]
    pg.dma(nea.ap, bass.AP(s2.tensor, s2.offset, [[0, 128], [1, 8]]), writes=[nea])
    pg.op("act", lambda e: e.activation(out=nea.ap, in_=nea.ap, func=AF.Exp), reads=[nea], writes=[nea])
    pg.op("dve", lambda e: e.tensor_scalar(out=nea.ap, in0=nea.ap, scalar1=-1.0, scalar2=None, op0=ALU.mult), reads=[nea], writes=[nea])
    beta = sb("D_beta", [128, NB, 2, 4], F32)
    nbeta = sb("D_nbeta", [128, NB, 2, 4], F32)
    g = sb("D_g", [128, NB, 2, 4], F32)
    pg.op("act", lambda e: e.activation(out=beta.ap, in_=ba4[:, :, :, 0, :], func=AF.Sigmoid), reads=[ba], writes=[beta])
    pg.op("dve", lambda e: e.tensor_scalar(out=nbeta.ap, in0=beta.ap, scalar1=-1.0, scalar2=None, op0=ALU.mult), reads=[beta], writes=[nbeta])
    dtb_b = dtb.ap.rearrange("p (d h) -> p d h", d=2).unsqueeze(1).to_broadcast([128, NB, 2, 4])
    nea_b = nea.ap.rearrange("p (d h) -> p d h", d=2).unsqueeze(1).to_broadcast([128, NB, 2, 4])
    pg.op("dve", lambda e: e.tensor_tensor(out=g.ap, in0=ba4[:, :, :, 1, :], in1=dtb_b, op=ALU.add), reads=[ba, dtb], writes=[g])
    pg.op("act", lambda e: e.activation(out=g.ap, in_=g.ap, func=AF.Exp), reads=[g], writes=[g])
    pg.op("act", lambda e: e.activation(out=g.ap, in_=g.ap, func=AF.Ln, bias=cx.one.ap[:, 0:1]), reads=[g, cx.one], writes=[g])
    pg.op("dve", lambda e: e.tensor_tensor(out=g.ap, in0=g.ap, in1=nea_b, op=ALU.mult), reads=[g, nea], writes=[g])
    eG = sb("D_eG", [128, NB, 2, 4], F32)
    eD = sb("D_eD", [128, NB, 2, 4], F32)
    bg = sb("D_bg", [128, NB, 2, 4], F32)
    deB = sb("D_deB", [128, 2, NB, 2, 4], F32)
    NQ = 32
    for d in range(2):
        for n0 in range(0, NB, NQ):
            nn = min(NQ, NB - n0)
            for (msk, dst, fn) in ((cx.m_incl.ap[:, d, :], eG, 0), (cx.m_sa.ap[:, d, :], eD, 0), (cx.chunkind.ap[:, 0, :], deB, 1), (cx.chunkind.ap[:, 1, :], deB, 2)):
                ps = cx.psf.get()
                pv = ps.ap[:, :nn * 4].rearrange("p (n h) -> p n h", h=4)
                pg.op("pe", lambda e: e.matmul(pv, lhsT=msk, rhs=g.ap[:, n0:n0 + nn, d, :], start=True, stop=True), reads=[g, cx.m_incl, cx.m_sa, cx.chunkind], writes=[ps])
                o_ap = dst.ap[:, n0:n0 + nn, d, :] if fn == 0 else dst.ap[:, fn - 1, n0:n0 + nn, d, :]
                pg.op("act", lambda e: e.activation(out=o_ap, in_=pv, func=AF.Exp), reads=[ps], writes=[dst])
    pg.op("dve", lambda e: e.tensor_tensor(out=bg.ap, in0=beta.ap, in1=eG.ap, op=ALU.mult), reads=[beta, eG], writes=[bg])
    gn = sb("D_gn", [128, 128], F32)
    gsrc = w["dn_norm_g"][l]
    pg.dma(gn.ap, bass.AP(gsrc.tensor, gsrc.offset, [[0, 128], [1, 128]]), writes=[gn])
    qk = [sb("D_qk%d" % i, [128, 8, 128], BF16) for i in range(2)]
    kv = [sb("D_kv%d" % i, [128, 2, 4, 128], BF16) for i in range(2)]
    gt = [sb("D_gt%d" % i, [128, 1280 + 512], F32) for i in range(1)]
    obt = [sb("D_ob%d" % i, [128, 512], F32) for i in range(2)]
    vb4 = sb("D_vb4", [128, 4, 128], BF16)
    kbg4 = sb("D_kbg4", [128, 4, 128], BF16)
    kend4 = sb("D_kend4", [128, 4, 128], BF16)
    gtri = [sb("D_gtri%d" % i, [128, 128], F32) for i in range(4)]
    gam = [sb("D_gam%d" % i, [128, 3, 128], F32) for i in range(4)]
    gamm = [sb("D_gamm%d" % i, [128, 2, 128], F32) for i in range(4)]
    qd = [sb("D_qd%d" % i, [128, 128], BF16) for i in range(4)]
    Cm = [sb("D_C%d" % i, [128, 128], BF16) for i in range(4)]
    attnT = [sb("D_at%d" % i, [128, 128], BF16) for i in range(4)]
    BC = [[sb("D_BC%d_%d" % (h, i), [128, 2, 128], BF16) for i in range(2)] for h in range(4)]
    Pm = [[sb("D_P%d_%d" % (h, i), [128, 128], BF16) for i in range(2)] for h in range(4)]
    Pm32 = [[sb("D_P32_%d_%d" % (h, i), [128, 128], F32) for i in range(2)] for h in range(4)]
    usb = [sb("D_u%d" % i, [128, 128], F32) for i in range(4)]
    wT = [sb("D_wT%d" % i, [128, 128], BF16) for i in range(4)]
    vn = [sb("D_vn%d" % i, [128, 128], BF16) for i in range(4)]
    S32 = [sb("D_S32_%d" % h, [128, 128], F32) for h in range(4)]
    Sb = [sb("D_Sb_%d" % h, [128, 128], BF16) for h in range(4)]
    osb = sb("D_osb", [128, 512], F32)
    ssq = sb("D_ssq", [128, 4], F32)
    junk = sb("D_junk", [128, 128], BF16)
    ysb = sb("D_ysb", [128, 512], BF16)
    yT = sb("D_yT", [128, 4, 128], BF16)
    QKv = cx.QKT.rearrange("(c p) t -> p c t", p=128)
    rr = [0]
    for d in (1, 0):
        pg.barrier()
        for h in range(4):
            pg.op("dve", lambda e: e.memset(S32[h].ap, 0.0), writes=[S32[h]])
            pg.op("pool", lambda e: e.memset(Sb[h].ap, 0.0), writes=[Sb[h]])
        order = range(NB) if d == 0 else range(NB - 1, -1, -1)
        chunks = (0, 1) if d == 0 else (1, 0)
        for bi, blk in enumerate(order):
            t0 = blk * 128
            ts = slice(t0, t0 + 128)
            qkb = qk[bi % 2]; kvb = kv[bi % 2]; ob = obt[bi % 2]; gtb = gt[0]
            pg.dma(qkb.ap, QKv[:, :, ts], writes=[qkb])
            pg.dma(kvb.ap.rearrange("p a h c -> p (a h c)"), cx.KVT[ts, :], writes=[kvb])
            if d == 0:
                pg.dma(ob.ap, cx.OB[ts, 512:1024], writes=[ob])
                pg.dma(gtb.ap[:, 0:512], cx.PT[ts, PT_DN_G:PT_DN_G + 512], writes=[gtb])
            bcast = lambda t: t.ap[:, blk, d, :].unsqueeze(2).to_broadcast([128, 4, 128])
            pg.op("dve", lambda e: e.tensor_tensor(out=vb4.ap, in0=kvb.ap[:, 1], in1=bcast(beta), op=ALU.mult), reads=[kvb, beta], writes=[vb4])
            pg.op("pool", lambda e: e.tensor_tensor(out=kbg4.ap, in0=kvb.ap[:, 0], in1=bcast(bg), op=ALU.mult), reads=[kvb, bg], writes=[kbg4])
            pg.op("pool", lambda e: e.tensor_tensor(out=kend4.ap, in0=kvb.ap[:, 0], in1=bcast(eD), op=ALU.mult), reads=[kvb, eD], writes=[kend4])
            op_ = cx.pso
            def head_gen(h, blk=blk, d=d, qkb=qkb, chunks=chunks, op_=op_):
                i2 = h
                bank = cx.psf.items[h]
                hc = slice(h * 128, (h + 1) * 128)
                gsc = g.ap[:, blk, d, h:h + 1]
                pg.op("dve", lambda e: e.tensor_scalar(out=gtri[i2].ap, in0=cx.m_incl.ap[:, d, :], scalar1=gsc, scalar2=None, op0=ALU.mult), reads=[cx.m_incl, g], writes=[gtri[i2]])
                yield
                dps = bank
                pg.op("pe", lambda e: e.matmul(dps.ap[:, 0:128], lhsT=gtri[i2].ap, rhs=cx.m_sa.ap[:, d, :], start=True, stop=True), reads=[gtri[i2], cx.m_sa], writes=[dps])
                pg.op("pe", lambda e: e.matmul(dps.ap[:, 128:256], lhsT=cx.m_sa.ap[:, d, :], rhs=gtri[i2].ap, start=True, stop=True), reads=[gtri[i2], cx.m_sa], writes=[dps])
                pg.op("pe", lambda e: e.matmul(dps.ap[:, 256:384], lhsT=cx.onesf.ap, rhs=gtri[i2].ap, start=True, stop=True), reads=[gtri[i2], cx.onesf], writes=[dps])
                yield
                pg.op("act", lambda e: e.activation(out=gam[i2].ap.rearrange("p a t -> p (a t)"), in_=dps.ap[:, 0:384], func=AF.Exp), reads=[dps], writes=[gam[i2]])
                yield
                pg.op("pool", lambda e: e.tensor_tensor(out=gamm[i2].ap, in0=gam[i2].ap[:, 0:2, :], in1=cx.m_dn.ap[:, d], op=ALU.mult), reads=[gam[i2], cx.m_dn], writes=[gamm[i2]])
                pg.op("pool", lambda e: e.tensor_tensor(out=qd[i2].ap, in0=qkb.ap[:, h, :], in1=gam[i2].ap[:, 2, :], op=ALU.mult), reads=[qkb, gam[i2]], writes=[qd[i2]])
                kps = bank
                pg.op("pe", lambda e: e.matmul(kps.ap[:, 0:128], lhsT=qkb.ap[:, 4 + h, :], rhs=qkb.ap[:, 4 + h, :], start=True, stop=True), reads=[qkb], writes=[kps])
                pg.op("pe", lambda e: e.matmul(kps.ap[:, 128:256], lhsT=qkb.ap[:, 4 + h, :], rhs=qkb.ap[:, h, :], start=True, stop=True), reads=[qkb], writes=[kps])
                yield
                pg.op("dve", lambda e: e.scalar_tensor_tensor(out=Cm[i2].ap, in0=kps.ap[:, 0:128], scalar=nbeta.ap[:, blk, d, h:h + 1], in1=gamm[i2].ap[:, 0, :], op0=ALU.mult, op1=ALU.mult),
                      reads=[kps, nbeta, gamm[i2]], writes=[Cm[i2]])
                pg.op("dve", lambda e: e.tensor_tensor(out=attnT[i2].ap, in0=kps.ap[:, 128:256], in1=gamm[i2].ap[:, 1, :], op=ALU.mult), reads=[kps, gamm[i2]], writes=[attnT[i2]])
                yield
                tb = bank
                tbv = bank.ap.bitcast(BF16)
                pg.op("pe", lambda e: e.transpose(out=tbv[:, 0:128], in_=Cm[i2].ap, identity=cx.identb.ap), reads=[Cm[i2], cx.identb], writes=[tb])
                yield
                hr = [0]
                bc0 = BC[h][hr[0] % 2]
                pg.op("act", lambda e: e.copy(bc0.ap[:, 0, :], tbv[:, 0:128]), reads=[tb], writes=[bc0])
                pg.op("pool", lambda e: e.tensor_copy(out=bc0.ap[:, 1, :], in_=Cm[i2].ap), reads=[Cm[i2]], writes=[bc0])
                p0 = Pm[h][hr[0] % 2]; p032 = Pm32[h][hr[0] % 2]; hr[0] += 1
                pg.op("pool", lambda e: e.tensor_tensor(out=p0.ap, in0=bc0.ap[:, 0, :], in1=cx.identb.ap, op=ALU.add), reads=[bc0, cx.identb], writes=[p0])
                yield
                bcp, pp, pp32 = bc0, p0, p032
                for k in range(1, 6):
                    sq_ = bank
                    if k < 5:
                        pg.op("pe", lambda e: e.matmul(sq_.ap[:, 0:128], lhsT=bcp.ap[:, 1, :], rhs=bcp.ap[:, 0, :], start=True, stop=True), reads=[bcp], writes=[sq_])
                    pg.op("pe", lambda e: e.matmul(sq_.ap[:, 128:256], lhsT=bcp.ap[:, 0, :], rhs=bcp.ap[:, 1, :], start=True, stop=True), reads=[bcp], writes=[sq_])
                    yield
                    bcn = BC[h][hr[0] % 2]
                    if k < 5:
                        pg.op("act", lambda e: e.copy(bcn.ap.rearrange("p a t -> p (a t)"), sq_.ap[:, 0:256]), reads=[sq_], writes=[bcn])
                    else:
                        pg.op("act", lambda e: e.copy(bcn.ap[:, 1, :], sq_.ap[:, 128:256]), reads=[sq_], writes=[bcn])
                        yield
                    pps = bank
                    pg.op("pe", lambda e: e.matmul(pps.ap[:, 0:128], lhsT=bcn.ap[:, 1, :], rhs=pp.ap, start=True, stop=True), reads=[bcn, pp], writes=[pps])
                    yield
                    pn = Pm[h][hr[0] % 2]; pn32 = Pm32[h][hr[0] % 2]; hr[0] += 1
                    pg.op("dve", lambda e: e.tensor_tensor(out=pn.ap, in0=pps.ap[:, 0:128], in1=pp.ap, op=ALU.add), reads=[pps, pp], writes=[pn])
                    yield
                    bcp, pp, pp32 = bcn, pn, pn32
                ups = bank
                pg.op("pe", lambda e: e.matmul(ups.ap[:, 0:128], lhsT=pp.ap, rhs=vb4.ap[:, h, :], start=True, stop=True), reads=[pp, vb4], writes=[ups])
                pg.op("pe", lambda e: e.matmul(ups.ap[:, 128:256], lhsT=kbg4.ap[:, h, :], rhs=pp.ap, start=True, stop=True), reads=[pp, kbg4], writes=[ups])
                yield
                pg.op("act", lambda e: e.copy(usb[i2].ap, ups.ap[:, 0:128]), reads=[ups], writes=[usb[i2]])
                pg.op("act", lambda e: e.copy(wT[i2].ap, ups.ap[:, 128:256]), reads=[ups], writes=[wT[i2]])
                yield
                for ci, c in enumerate(chunks):
                    r0 = c * 64
                    rs_ = slice(r0, r0 + 64)
                    wps = bank
                    pg.op("pe", lambda e: e.matmul(wps.ap[rs_, 0:128], lhsT=wT[i2].ap[:, rs_], rhs=Sb[h].ap, start=True, stop=True), reads=[wT[i2], Sb[h]], writes=[wps])
                    yield
                    pg.op("dve", lambda e: e.scalar_tensor_tensor(out=vn[i2].ap[rs_, :], in0=wps.ap[rs_, 0:128], scalar=-1.0, in1=usb[i2].ap[rs_, :], op0=ALU.mult, op1=ALU.add), reads=[usb[i2], wps], writes=[vn[i2]])
                    yield
                    pg.op("pe", lambda e: e.matmul(op_.ap[rs_, hc], lhsT=qd[i2].ap[:, rs_], rhs=Sb[h].ap, start=True, stop=False), reads=[qd[i2], Sb[h]], writes=[op_])
                    pg.op("pe", lambda e: e.matmul(op_.ap[rs_, hc], lhsT=attnT[i2].ap[rs_, rs_], rhs=vn[i2].ap[rs_, :], start=False, stop=True), reads=[attnT[i2], vn[i2]], writes=[op_])
                    kvp = bank
                    pg.op("pe", lambda e: e.matmul(kvp.ap[:, 0:128], lhsT=kend4.ap[rs_, h, :], rhs=vn[i2].ap[rs_, :], start=True, stop=True), reads=[kend4, vn[i2]], writes=[kvp])
                    yield
                    pg.op("dve", lambda e: e.scalar_tensor_tensor(out=S32[h].ap, in0=S32[h].ap, scalar=deB.ap[:, c, blk, d, h:h + 1], in1=kvp.ap[:, 0:128], op0=ALU.mult, op1=ALU.add),
                          reads=[S32[h], deB, kvp], writes=[S32[h]])
                    pg.op("act", lambda e: e.copy(Sb[h].ap, S32[h].ap), reads=[S32[h]], writes=[Sb[h]])
                    yield
            gens = [head_gen(h) for h in range(4)]
            while gens:
                for gnr in list(gens):
                    try:
                        next(gnr)
                    except StopIteration:
                        gens.remove(gnr)
            if d == 1:
                pg.op("act", lambda e: e.copy(osb.ap, op_.ap), reads=[op_], writes=[osb])
                pg.dma(cx.OB[ts, 512:1024], osb.ap, reads=[osb])
            else:
                pg.op("dve", lambda e: e.tensor_tensor(out=osb.ap, in0=op_.ap, in1=ob.ap, op=ALU.add), reads=[op_, ob], writes=[osb])
                head_norm_gate_store(pg, cx, osb, ssq, junk, gn, gtb, 0, ysb, yT, 1024, ts)


def phase_S5(pg, cx, es, L, l):
    nc = cx.nc
    w = cx.w
    sb = lambda name, shape, dt: pg.buf(es.enter_context(nc.sbuf_tensor(pg.uname(name), shape, dt)).ap(), name)
    NS = int(np.ceil(np.log2(L)))
    dve = lambda fn, r, wr: pg.op("dve", fn, reads=r, writes=wr)
    A = lambda nm: sb("S_" + nm, [128, 32], F32)
    lre, lim, dt_, ar, ai, m_, sn, cs, Are, Aim, t1, t2, t3, fre, fim, den = [A(n) for n in
        ("lre", "lim", "dt", "ar", "ai", "m", "sn", "cs", "Are", "Aim", "t1", "t2", "t3", "fre", "fim", "den")]
    load_T(pg, cx, lre, lre.ap, w["s5_lambda_re"][l].rearrange("d (gh gl) p -> (d gh) (gl p)", gl=2), 32)
    load_T(pg, cx, lim, lim.ap, w["s5_lambda_im"][l].rearrange("d (gh gl) p -> (d gh) (gl p)", gl=2), 32)
    ld2 = sb("S_ld2", [32, 2], F32)
    pg.dma(ld2.ap, w["s5_log_dt"][l].rearrange("d (gh gl) -> (d gh) gl", gl=2), writes=[ld2])
    stl = sb("S_stl", [32, 128], F32)
    for gl in range(2):
        dve(lambda e: e.tensor_copy(out=stl.ap[:, 64 * gl:64 * gl + 64], in_=ld2.ap[:, gl:gl + 1].to_broadcast([32, 64])), [ld2], [stl])
    psl = cx.psf.get()
    pg.op("pe", lambda e: e.transpose(out=psl.ap[:, :32], in_=stl.ap, identity=cx.identf.ap[:32, :32]), reads=[stl, cx.identf], writes=[psl])
    dve(lambda e: e.tensor_copy(out=dt_.ap, in_=psl.ap[:, :32]), [psl], [dt_])
    pg.op("act", lambda e: e.activation(out=dt_.ap, in_=dt_.ap, func=AF.Exp), reads=[dt_], writes=[dt_])
    dve(lambda e: e.tensor_tensor(out=ar.ap, in0=lre.ap, in1=dt_.ap, op=ALU.mult), [lre, dt_], [ar])
    dve(lambda e: e.tensor_tensor(out=ai.ap, in0=lim.ap, in1=dt_.ap, op=ALU.mult), [lim, dt_], [ai])
    pg.op("act", lambda e: e.activation(out=m_.ap, in_=ar.ap, func=AF.Exp, scale=1.0 / 16), reads=[ar], writes=[m_])
    pg.op("act", lambda e: e.activation(out=sn.ap, in_=ai.ap, func=AF.Sin, scale=1.0 / 16), reads=[ai], writes=[sn])
    pg.op("act", lambda e: e.activation(out=cs.ap, in_=ai.ap, func=AF.Sin, scale=1.0 / 16, bias=cx.halfpi.ap[:, 0:1]), reads=[ai, cx.halfpi], writes=[cs])
    dve(lambda e: e.tensor_tensor(out=Are.ap, in0=m_.ap, in1=cs.ap, op=ALU.mult), [m_, cs], [Are])
    dve(lambda e: e.tensor_tensor(out=Aim.ap, in0=m_.ap, in1=sn.ap, op=ALU.mult), [m_, sn], [Aim])

    def csquare(re, im):
        dve(lambda e: e.tensor_tensor(out=t1.ap, in0=re, in1=re, op=ALU.mult), [Are, PW], [t1])
        dve(lambda e: e.tensor_tensor(out=t2.ap, in0=im, in1=im, op=ALU.mult), [Aim, PW], [t2])
        dve(lambda e: e.tensor_tensor(out=t3.ap, in0=re, in1=im, op=ALU.mult), [Are, Aim, PW], [t3])

    PW = sb("S_PW", [128, 32, NS, 3], F32)
    for _ in range(4):
        csquare(Are.ap, Aim.ap)
        dve(lambda e: e.tensor_tensor(out=Are.ap, in0=t1.ap, in1=t2.ap, op=ALU.subtract), [t1, t2], [Are])
        dve(lambda e: e.tensor_scalar(out=Aim.ap, in0=t3.ap, scalar1=2.0, scalar2=None, op0=ALU.mult), [t3], [Aim])
    dve(lambda e: e.tensor_tensor(out=den.ap, in0=lre.ap, in1=lre.ap, op=ALU.mult), [lre], [den])
    dve(lambda e: e.tensor_tensor(out=t1.ap, in0=lim.ap, in1=lim.ap, op=ALU.mult), [lim], [t1])
    dve(lambda e: e.tensor_tensor(out=den.ap, in0=den.ap, in1=t1.ap, op=ALU.add), [den, t1], [den])
    dve(lambda e: e.reciprocal(out=den.ap, in_=den.ap), [den], [den])
    dve(lambda e: e.tensor_scalar(out=t3.ap, in0=Are.ap, scalar1=-1.0, scalar2=None, op0=ALU.add), [Are], [t3])
    dve(lambda e: e.tensor_tensor(out=t1.ap, in0=t3.ap, in1=lre.ap, op=ALU.mult), [t3, lre], [t1])
    dve(lambda e: e.tensor_tensor(out=t2.ap, in0=Aim.ap, in1=lim.ap, op=ALU.mult), [Aim, lim], [t2])
    dve(lambda e: e.tensor_tensor(out=t1.ap, in0=t1.ap, in1=t2.ap, op=ALU.add), [t1, t2], [t1])
    dve(lambda e: e.tensor_tensor(out=fre.ap, in0=t1.ap, in1=den.ap, op=ALU.mult), [t1, den], [fre])
    dve(lambda e: e.tensor_tensor(out=t1.ap, in0=Aim.ap, in1=lre.ap, op=ALU.mult), [Aim, lre], [t1])
    dve(lambda e: e.tensor_tensor(out=t2.ap, in0=t3.ap, in1=lim.ap, op=ALU.mult), [t3, lim], [t2])
    dve(lambda e: e.tensor_tensor(out=t1.ap, in0=t1.ap, in1=t2.ap, op=ALU.subtract), [t1, t2], [t1])
    dve(lambda e: e.tensor_tensor(out=fim.ap, in0=t1.ap, in1=den.ap, op=ALU.mult), [t1, den], [fim])
    for k in range(NS):
        if k == 0:
            dve(lambda e: e.tensor_copy(out=PW.ap[:, :, 0, 0], in_=Are.ap), [Are], [PW])
            dve(lambda e: e.tensor_copy(out=PW.ap[:, :, 0, 1], in_=Aim.ap), [Aim], [PW])
        else:
            csquare(PW.ap[:, :, k - 1, 0], PW.ap[:, :, k - 1, 1])
            dve(lambda e: e.tensor_tensor(out=PW.ap[:, :, k, 0], in0=t1.ap, in1=t2.ap, op=ALU.subtract), [t1, t2], [PW])
            dve(lambda e: e.tensor_scalar(out=PW.ap[:, :, k, 1], in0=t3.ap, scalar1=2.0, scalar2=None, op0=ALU.mult), [t3], [PW])
        dve(lambda e: e.tensor_scalar(out=PW.ap[:, :, k, 2], in0=PW.ap[:, :, k, 1], scalar1=-1.0, scalar2=None, op0=ALU.mult), [PW], [PW])
    CL = sb("S_CL", [128, 32, 2, 32], F32)
    Wb = sb("S_Wb", [128, 32, 2, 32], F32)
    PWs = sb("S_PWs", [128, 32, 9, 2], F32)
    dve(lambda e: e.memset(PWs.ap[:, :, 0, 0], 1.0), [], [PWs])
    dve(lambda e: e.memset(PWs.ap[:, :, 0, 1], 0.0), [], [PWs])
    for k in range(1, 9):
        pr, pi_ = PWs.ap[:, :, k - 1, 0], PWs.ap[:, :, k - 1, 1]
        dve(lambda e: e.tensor_tensor(out=t1.ap, in0=pr, in1=PW.ap[:, :, 0, 0], op=ALU.mult), [PWs, PW], [t1])
        dve(lambda e: e.tensor_tensor(out=t2.ap, in0=pi_, in1=PW.ap[:, :, 0, 1], op=ALU.mult), [PWs, PW], [t2])
        dve(lambda e: e.tensor_tensor(out=PWs.ap[:, :, k, 0], in0=t1.ap, in1=t2.ap, op=ALU.subtract), [t1, t2], [PWs])
        dve(lambda e: e.tensor_tensor(out=t1.ap, in0=pr, in1=PW.ap[:, :, 0, 1], op=ALU.mult), [PWs, PW], [t1])
        dve(lambda e: e.tensor_tensor(out=t2.ap, in0=pi_, in1=PW.ap[:, :, 0, 0], op=ALU.mult), [PWs, PW], [t2])
        dve(lambda e: e.tensor_tensor(out=PWs.ap[:, :, k, 1], in0=t1.ap, in1=t2.ap, op=ALU.add), [t1, t2], [PWs])
    with ExitStack() as es1:
        sb1 = lambda name, shape, dt: pg.buf(es1.enter_context(nc.sbuf_tensor(pg.uname(name), shape, dt)).ap(), name)
        Bt = [sb1("S_Bt%d" % i, [128, 32, 16], F32) for i in range(2)]
        Bb = [sb1("S_Bb%d" % i, [128, 32, 16], F32) for i in range(2)]
        tmpb = sb1("S_tmpb", [128, 32, 16], F32)
        for i, nm in enumerate(("s5_b_re", "s5_b_im")):
            base = w[nm][l]
            pg.dma(Bt[i].ap, bass.AP(base.tensor, base.offset, [[16, 128], [2048, 32], [1, 16]]), writes=[Bt[i]])
        fb = lambda t: t.ap.unsqueeze(2).to_broadcast([128, 32, 16])
        dve(lambda e: e.tensor_tensor(out=Bb[0].ap, in0=Bt[0].ap, in1=fb(fre), op=ALU.mult), [Bt[0], fre], [Bb[0]])
        dve(lambda e: e.tensor_tensor(out=tmpb.ap, in0=Bt[1].ap, in1=fb(fim), op=ALU.mult), [Bt[1], fim], [tmpb])
        dve(lambda e: e.tensor_tensor(out=Bb[0].ap, in0=Bb[0].ap, in1=tmpb.ap, op=ALU.subtract), [Bb[0], tmpb], [Bb[0]])
        dve(lambda e: e.tensor_tensor(out=Bb[1].ap, in0=Bt[1].ap, in1=fb(fre), op=ALU.mult), [Bt[1], fre], [Bb[1]])
        dve(lambda e: e.tensor_tensor(out=tmpb.ap, in0=Bt[0].ap, in1=fb(fim), op=ALU.mult), [Bt[0], fim], [tmpb])
        dve(lambda e: e.tensor_tensor(out=Bb[1].ap, in0=Bb[1].ap, in1=tmpb.ap, op=ALU.add), [Bb[1], tmpb], [Bb[1]])
        pg.op("pool", lambda e: e.memset(Wb.ap, 0.0), writes=[Wb])
        for c in range(2):
            dve(lambda e: e.tensor_copy(out=Wb.ap[0:64, :, c, 0:16], in_=Bb[c].ap[0:64]), [Bb[c]], [Wb])
            dve(lambda e: e.tensor_copy(out=Wb.ap[64:128, :, c, 16:32], in_=Bb[c].ap[64:128]), [Bb[c]], [Wb])
        St0 = sb1("S_St0", [128, 64], F32)
        St = sb1("S_St", [128, 128], F32)
        for d in range(2):
            for c, nm in enumerate(("s5_c_re", "s5_c_im")):
                for blk in range(4):
                    pg.dma(St0.ap, w[nm][l, d, 8 * blk:8 * blk + 8].rearrange("g i p -> (g i) p"), writes=[St0])
                    sgn = 1.0 if c == 0 else -1.0
                    for hh in range(2):
                        dve(lambda e: e.tensor_scalar(out=St.ap[:, 64 * hh:64 * hh + 64], in0=St0.ap, scalar1=cx.pm.ap[:, hh:hh + 1], scalar2=sgn, op0=ALU.mult, op1=ALU.mult),
                            [St0, cx.pm], [St])
                    ps = cx.psf.get()
                    pg.op("pe", lambda e: e.transpose(out=ps.ap[:, 0:128], in_=St.ap, identity=cx.identf.ap), reads=[St, cx.identf], writes=[ps])
                    dg0 = d * 16 + blk * 4
                    pg.op("act", lambda e: e.copy(CL.ap[:, dg0:dg0 + 4, c, :], ps.ap[:, 0:128].rearrange("p (q m) -> p q m", q=4)), reads=[ps], writes=[CL])
    dsk = sb("S_dsk", [32, 16], F32)
    load_T(pg, cx, dsk, dsk.ap, w["s5_d"][l].rearrange("(g q) -> g q", q=32), 16, wd=32)
    bgl = sb("S_bgl", [128, 4], F32)
    load_T(pg, cx, bgl, bgl.ap, w["s5_b_glu"][l].rearrange("(c p) -> c p", p=128), 4)
    es2 = ExitStack()
    sb2 = lambda name, shape, dt: pg.buf(es2.enter_context(nc.sbuf_tensor(pg.uname(name), shape, dt)).ap(), name)
    NCH = L // 8
    NSC = int(np.ceil(np.log2(NCH)))
    HW = min(512, NCH)
    NH = NCH // HW
    ub = sb2("S_ub", [32, 8, NCH], BF16)
    UW = min(2048, L)
    ust = [sb2("S_ust%d" % i, [32, UW], F32) for i in range(2)]
    Yc = sb2("S_Yc", [32, L], F32)
    XS_ = [[sb2("S_X%d_%d" % (d, i), [128, NCH + 2], F32) for i in range(3)] for d in range(2)]
    Xb = [[sb2("S_Xb%d_%d" % (d, c), [128, NCH + 2], BF16) for c in range(2)] for d in range(2)]
    Wt_ = [sb2("S_Wt%d" % d, [128, 2, 8, 32], F32) for d in range(2)]
    tmpw_ = [sb2("S_tmpw%d" % d, [128, 8, 32], F32) for d in range(2)]
    WsT = [sb2("S_WsT%d" % d, [32, 2, 8, 128], BF16) for d in range(2)]
    CI = [sb2("S_CI%d" % d, [128, 2, 8, 32], BF16) for d in range(2)]
    CIf_ = [sb2("S_CIf%d" % d, [128, 2, 8, 32], F32) for d in range(2)]
    Kd = [sb2("S_Kd%d" % d, [32, 8, 32], BF16) for d in range(2)]
    ua_t = sb2("S_ua", [32, UW], F32)
    x2_t = sb2("S_x2", [32, UW], F32)
    zo_t = sb2("S_zo", [32, UW], BF16)

    def strided(ap2, start, n, step):
        b0 = ap2[:, start:start + 1]
        return bass.AP(b0.tensor, b0.offset, [list(ap2.ap[0]), [step * ap2.ap[1][0], n]])

    ev = [0]
    prev_pair = None
    out_ps = Rot(cx.psf.items[2:5])
    bcW = lambda a: a.unsqueeze(1).to_broadcast([128, 8, 32])
    bcP = lambda a: a.unsqueeze(2).to_broadcast([128, 8, 32])
    for gh in range(16):
        urow = PF_S5_U + 32 * gh
        for i, t0 in enumerate(range(0, L, UW)):
            st_ = ust[i % 2]
            pg.dma(st_.ap, cx.PF[urow:urow + 32, t0:t0 + UW], writes=[st_])
            pg.op("pool", lambda e: e.tensor_copy(out=ub.ap[:, :, t0 // 8:(t0 + UW) // 8], in_=st_.ap.rearrange("p (n s) -> p s n", s=8)), reads=[st_], writes=[ub])
        def dir_gen(d, gh=gh):
            bank = cx.psf.items[d]
            Wt, tmpw, CIf = Wt_[d], tmpw_[d], CIf_[d]
            dg = d * 16 + gh
            wbr, wbi = Wb.ap[:, dg, 0, :], Wb.ap[:, dg, 1, :]
            pre, pim = PWs.ap[:, dg, 0:8, 0], PWs.ap[:, dg, 0:8, 1]
            dve(lambda e: e.tensor_tensor(out=Wt.ap[:, 0], in0=bcW(wbr), in1=bcP(pre), op=ALU.mult), [Wb, PWs], [Wt])
            yield
            dve(lambda e: e.tensor_tensor(out=tmpw.ap, in0=bcW(wbi), in1=bcP(pim), op=ALU.mult), [Wb, PWs], [tmpw])
            yield
            dve(lambda e: e.tensor_tensor(out=Wt.ap[:, 0], in0=Wt.ap[:, 0], in1=tmpw.ap, op=ALU.subtract), [Wt, tmpw], [Wt])
            yield
            dve(lambda e: e.tensor_tensor(out=Wt.ap[:, 1], in0=bcW(wbr), in1=bcP(pim), op=ALU.mult), [Wb, PWs], [Wt])
            yield
            dve(lambda e: e.tensor_tensor(out=tmpw.ap, in0=bcW(wbi), in1=bcP(pre), op=ALU.mult), [Wb, PWs], [tmpw])
            yield
            dve(lambda e: e.tensor_tensor(out=Wt.ap[:, 1], in0=Wt.ap[:, 1], in1=tmpw.ap, op=ALU.add), [Wt, tmpw], [Wt])
            yield
            for c in range(2):
                for t4 in range(0, 8, 4):
                    ps = bank
                    for tq in range(4):
                        pg.op("pe", lambda e: e.transpose(out=ps.ap[:32, tq * 128:(tq + 1) * 128], in_=Wt.ap[:, c, t4 + tq, :], identity=cx.identf.ap), reads=[Wt, cx.identf], writes=[ps])
                    pg.op("act", lambda e: e.copy(WsT[d].ap[:, c, t4:t4 + 4, :], ps.ap[:32, :].rearrange("p (q m) -> p q m", q=4)), reads=[ps], writes=[WsT[d]])
                    yield
            cl0, cl1 = CL.ap[:, dg, 0, :], CL.ap[:, dg, 1, :]
            pre1, pim1 = PWs.ap[:, dg, 1:9, 0], PWs.ap[:, dg, 1:9, 1]
            dve(lambda e: e.tensor_tensor(out=CIf.ap[:, 0], in0=bcW(cl0), in1=bcP(pre1), op=ALU.mult), [CL, PWs], [CIf])
            yield
            dve(lambda e: e.tensor_tensor(out=tmpw.ap, in0=bcW(cl1), in1=bcP(pim1), op=ALU.mult), [CL, PWs], [tmpw])
            yield
            dve(lambda e: e.tensor_tensor(out=CI[d].ap[:, 0], in0=CIf.ap[:, 0], in1=tmpw.ap, op=ALU.add), [CIf, tmpw], [CI[d]])
            yield
            dve(lambda e: e.tensor_tensor(out=CIf.ap[:, 1], in0=bcW(cl1), in1=bcP(pre1), op=ALU.mult), [CL, PWs], [CIf])
            yield
            dve(lambda e: e.tensor_tensor(out=tmpw.ap, in0=bcW(cl0), in1=bcP(pim1), op=ALU.mult), [CL, PWs], [tmpw])
            yield
            dve(lambda e: e.tensor_tensor(out=CI[d].ap[:, 1], in0=CIf.ap[:, 1], in1=tmpw.ap, op=ALU.subtract), [CIf, tmpw], [CI[d]])
            yield
            ps = bank
            for tau in range(8):
                po = ps.ap[0:32, tau * 32:(tau + 1) * 32]
                pg.op("pe", lambda e: e.matmul(po, lhsT=Wt.ap[:, 0, tau, :], rhs=cl0, start=True, stop=False), reads=[Wt, CL], writes=[ps])
                pg.op("pe", lambda e: e.matmul(po, lhsT=Wt.ap[:, 1, tau, :], rhs=cl1, start=False, stop=True), reads=[Wt, CL], writes=[ps])
            pg.op("act", lambda e: e.copy(Kd[d].ap, ps.ap[0:32, 0:256].rearrange("p (t m) -> p t m", t=8)), reads=[ps], writes=[Kd[d]])
            yield
            re, im, T = XS_[d]
            for b_ in (re, im, T):
                pg.op("pool", lambda e: e.memset(b_.ap, 0.0), writes=[b_])
                yield
            for c, dstb in ((0, re), (1, im)):
                for hf in range(NH):
                    ps = bank
                    for s_ in range(8):
                        tau = 7 - s_ if d == 0 else s_
                        pg.op("pe", lambda e: e.matmul(ps.ap[:, :HW], lhsT=WsT[d].ap[:, c, tau, :], rhs=ub.ap[:, s_, hf * HW:(hf + 1) * HW], start=(s_ == 0), stop=(s_ == 7)),
                              reads=[WsT[d], ub], writes=[ps])
                    ev[0] += 1
                    if ev[0] % 2 == 0:
                        pg.op("act", lambda e: e.copy(dstb.ap[:, 1 + hf * HW:1 + (hf + 1) * HW], ps.ap[:, :HW]), reads=[ps], writes=[dstb])
                        yield
                    else:
                        pg.op("dve", lambda e: e.tensor_copy(out=dstb.ap[:, 1 + hf * HW:1 + (hf + 1) * HW], in_=ps.ap[:, :HW]), reads=[ps], writes=[dstb])
                        yield
            for k in range(NSC):
                sft = 1 << k
                if sft >= NCH:
                    break
                kk = k + 3
                cre, cim, ncim = PW.ap[:, dg, kk, 0:1], PW.ap[:, dg, kk, 1:2], PW.ap[:, dg, kk, 2:3]
                if d == 0:
                    dst, src, keep = slice(1 + sft, 1 + NCH), slice(1, 1 + NCH - sft), slice(1, 1 + sft)
                else:
                    dst, src, keep = slice(1, 1 + NCH - sft), slice(1 + sft, 1 + NCH), slice(1 + NCH - sft, 1 + NCH)
                dve(lambda e: e.scalar_tensor_tensor(out=T.ap[:, dst], in0=re.ap[:, src], scalar=cre, in1=re.ap[:, dst], op0=ALU.mult, op1=ALU.add), [re, PW], [T])
                yield
                dve(lambda e: e.scalar_tensor_tensor(out=T.ap[:, dst], in0=im.ap[:, src], scalar=ncim, in1=T.ap[:, dst], op0=ALU.mult, op1=ALU.add), [im, T, PW], [T])
                yield
                pg.op("pool", lambda e: e.tensor_copy(out=T.ap[:, keep], in_=re.ap[:, keep]), reads=[re], writes=[T])
                yield
                if d == 0:
                    rv = lambda ap, sl: bass.AP(ap.tensor, ap[:, sl].offset + (sl.stop - sl.start) - 1, [list(ap.ap[0]), [-1, sl.stop - sl.start]])
                    dve(lambda e: e.scalar_tensor_tensor(out=rv(im.ap, dst), in0=rv(im.ap, src), scalar=cre, in1=rv(im.ap, dst), op0=ALU.mult, op1=ALU.add), [im, PW], [im])
                    yield
                else:
                    dve(lambda e: e.scalar_tensor_tensor(out=im.ap[:, dst], in0=im.ap[:, src], scalar=cre, in1=im.ap[:, dst], op0=ALU.mult, op1=ALU.add), [im, PW], [im])
                    yield
                dve(lambda e: e.scalar_tensor_tensor(out=im.ap[:, dst], in0=re.ap[:, src], scalar=cim, in1=im.ap[:, dst], op0=ALU.mult, op1=ALU.add), [re, im, PW], [im])
                yield
                re, T = T, re
            pg.op("pool", lambda e: e.tensor_copy(out=Xb[d][0].ap, in_=re.ap), reads=[re], writes=[Xb[d][0]])
            yield
            pg.op("pool", lambda e: e.tensor_copy(out=Xb[d][1].ap, in_=im.ap), reads=[im], writes=[Xb[d][1]])
            yield

        def gelu_gen(gh_, urow_):
            for t0 in range(0, L, UW):
                tsl = slice(t0, t0 + UW)
                ua, x2, zo = ua_t.ap, x2_t.ap, zo_t.ap
                pg.dma(ua, cx.PF[urow_:urow_ + 32, tsl], writes=[ua_t])
                yv = Yc.ap[:, tsl]
                dve(lambda e: e.scalar_tensor_tensor(out=yv, in0=ua, scalar=dsk.ap[:, gh_:gh_ + 1], in1=yv, op0=ALU.mult, op1=ALU.add), [ua_t, dsk, Yc], [Yc])
                yield
                pg.op("pool", lambda e: e.tensor_tensor(out=x2, in0=yv, in1=yv, op=ALU.mult), reads=[Yc], writes=[x2_t])
                yield
                dve(lambda e: e.tensor_scalar(out=x2, in0=x2, scalar1=0.044715, scalar2=1.0, op0=ALU.mult, op1=ALU.add), [x2_t], [x2_t])
                yield
                pg.op("pool", lambda e: e.tensor_tensor(out=x2, in0=x2, in1=yv, op=ALU.mult), reads=[Yc, x2_t], writes=[x2_t])
                yield
                pg.op("act", lambda e: e.activation(out=x2, in_=x2, func=AF.Sigmoid, scale=1.5957691216), reads=[x2_t], writes=[x2_t])
                yield
                dve(lambda e: e.tensor_tensor(out=zo, in0=x2, in1=yv, op=ALU.mult), [x2_t, Yc], [zo_t])
                pg.dma(cx.ZT[urow_ - PF_S5_U:urow_ - PF_S5_U + 32, tsl], zo, reads=[zo_t])
                yield

        gens = [dir_gen(0), dir_gen(1)] + ([gelu_gen(*prev_pair)] if prev_pair is not None else [])
        while gens:
            for gnr in list(gens):
                try:
                    next(gnr)
                except StopIteration:
                    gens.remove(gnr)
        prev_pair = (gh, urow)
        for hf in range(NH):
            for sp in range(8):
                ps = out_ps.get()
                po = ps.ap[0:32, :HW]
                mm = []
                for c in range(2):
                    mm.append((CI[0].ap[:, c, sp, :], Xb[0][c].ap[:, hf * HW:hf * HW + HW], [CI[0], Xb[0][c]]))
                    mm.append((CI[1].ap[:, c, 7 - sp, :], Xb[1][c].ap[:, hf * HW + 2:hf * HW + 2 + HW], [CI[1], Xb[1][c]]))
                for s_ in range(0, sp + 1):
                    mm.append((Kd[0].ap[:, sp - s_, :], ub.ap[:, s_, hf * HW:(hf + 1) * HW], [Kd[0], ub]))
                for s_ in range(sp, 8):
                    mm.append((Kd[1].ap[:, s_ - sp, :], ub.ap[:, s_, hf * HW:(hf + 1) * HW], [Kd[1], ub]))
                for i, (lh, rh, rd) in enumerate(mm):
                    pg.op("pe", lambda e: e.matmul(po, lhsT=lh, rhs=rh, start=(i == 0), stop=(i == len(mm) - 1)), reads=rd, writes=[ps])
                pg.op("act", lambda e: e.copy(strided(Yc.ap, hf * HW * 8 + sp, HW, 8), po), reads=[ps], writes=[Yc])
    gens = [gelu_gen(*prev_pair)]
    for gnr in gens:
        for _ in gnr:
            pass
    pg.barrier()
    es2.close()
    wg = sb("S_wg", [128, 4, 512], BF16)
    wst = sb("S_wst", [128, 2048], F32)
    pg.dma(wst.ap[:, 0:2048].rearrange("p (k c) -> p k c", k=4), w["s5_w_glu"][l].rearrange("(k p) c -> p k c", p=128), writes=[wst])
    dve(lambda e: e.tensor_copy(out=wg.ap, in_=wst.ap[:, 0:2048].rearrange("p (k c) -> p k c", k=4)), [wst], [wg])
    zt = [sb("S_zt%d" % i, [128, 4, 512], BF16) for i in range(2)]
    gt = [sb("S_gt%d" % i, [128, 4, 512], F32) for i in range(2)]
    sg = sb("S_sg", [128, 512], F32)
    yo = [sb("S_yo%d" % i, [128, 4, 512], BF16) for i in range(2)]
    ZTv = cx.ZT.rearrange("(k p) t -> p k t", p=128)
    NT = L // 512
    for it in range(NT):
        tsl = slice(it * 512, (it + 1) * 512)
        z_ = zt[it % 2]; g_ = gt[it % 2]; y_ = yo[it % 2]
        pg.dma(z_.ap, ZTv[:, :, tsl], writes=[z_])
        pg.dma(g_.ap, cx.PF[PF_S5_G:PF_S5_G + 512, tsl].rearrange("(k p) t -> p k t", p=128), writes=[g_])
        pg.op("act", lambda e: e.activation(out=g_.ap, in_=g_.ap, func=AF.Silu), reads=[g_], writes=[g_])
        pg.op("pool", lambda e: e.tensor_tensor(out=g_.ap, in0=g_.ap, in1=z_.ap, op=ALU.mult), reads=[g_, z_], writes=[g_])
        for oc in range(4):
            ps = cx.psf.get()
            for k in range(4):
                pg.op("pe", lambda e: e.matmul(ps.ap, lhsT=wg.ap[:, k, oc * 128:(oc + 1) * 128], rhs=z_.ap[:, k, :], start=(k == 0), stop=(k == 3)), reads=[wg, z_], writes=[ps])
            pg.op("act", lambda e: e.activation(out=sg.ap, in_=ps.ap, func=AF.Sigmoid, bias=bgl.ap[:, oc:oc + 1]), reads=[ps, bgl], writes=[sg])
            dve(lambda e: e.tensor_tensor(out=y_.ap[:, oc, :], in0=sg.ap, in1=g_.ap[:, oc, :], op=ALU.mult), [sg, g_], [y_])
        pg.dma(cx.BT[1536:2048, tsl].rearrange("(k p) t -> p k t", p=128), y_.ap, reads=[y_])


W_NAMES = ["norm_g", "w_in", "lru_conv_w", "lru_conv_b", "lru_w_a", "lru_b_a", "lru_w_x", "lru_b_x", "lru_lambda",
           "gla_w_up", "gla_b_up", "gla_norm_g", "dn_conv_w", "dn_a_log", "dn_dt_bias", "dn_norm_g",
           "s5_lambda_re", "s5_lambda_im", "s5_log_dt", "s5_b_re", "s5_b_im", "s5_c_re", "s5_c_im", "s5_d",
           "s5_w_glu", "s5_b_glu", "w_branch", "w_merge_gate", "b_merge_gate", "w_out", "final_norm_g"]


def host_consts():
    c = {}
    c["identb"] = np.eye(128, dtype=np.float32).astype(ml_dtypes.bfloat16)
    c["identf"] = np.eye(128, dtype=np.float32)
    idx = np.arange(128)
    same = (idx[:, None] // 64) == (idx[None, :] // 64)
    le = idx[:, None] <= idx[None, :]
    lt = idx[:, None] < idx[None, :]
    c["m_incl"] = np.stack([(same & le), (same & le.T)]).astype(np.float32)
    c["m_strict_after"] = np.stack([(same & lt.T), (same & lt)]).astype(np.float32)
    c["m_dn"] = np.stack([c["m_strict_after"], c["m_incl"]], axis=1)
    c["chunkind"] = np.stack([np.repeat((idx // 64 == cc)[:, None], 128, axis=1) for cc in range(2)]).astype(np.float32)
    c["onesf"] = np.ones((128, 128), np.float32)
    ev = ((idx // 16) % 2 == 0).astype(np.float32)
    c["pm"] = np.stack([ev, 1.0 - ev], axis=1).astype(np.float32)
    return c


def build(L, shapes, nslot=2, depth=2, debug=False, branches=("lru", "gla", "dn", "s5")):
    from contextlib import ExitStack
    nc = bass.Bass("TRN2", target_bir_lowering=False)
    pg = Prog(nc)
    cx = Ctx()
    cx.nc = nc
    cx.pg = pg
    cx.w = {}
    for nm in W_NAMES:
        cx.w[nm] = nc.dram_tensor(nm, list(shapes[nm]), F32, kind="ExternalInput").ap()
    hc = host_consts()
    cx.cd = {}
    for nm, arr in hc.items():
        cx.cd[nm] = nc.dram_tensor("c_" + nm, list(arr.shape), BF16 if arr.dtype == ml_dtypes.bfloat16 else F32, kind="ExternalInput").ap()
    xs = [nc.dram_tensor("x%d" % s, [L, D], F32, kind="ExternalInput").ap() for s in range(nslot)]
    ys = [nc.dram_tensor("y%d" % s, [L, D], F32, kind="ExternalOutput").ap() for s in range(nslot)]
    sk = "ExternalOutput" if debug else "Internal"
    cx.PF = nc.dram_tensor("PF", [PF_ROWS, L], F32, kind=sk).ap()
    cx.PT = nc.dram_tensor("PT", [L, PT_COLS], F32, kind=sk).ap()
    cx.XNT = nc.dram_tensor("XNT", [D, L], BF16, kind=sk).ap()
    cx.BT = nc.dram_tensor("BT", [2048, L], BF16, kind=sk).ap()
    cx.OB = nc.dram_tensor("OB", [L, 1024], F32, kind=sk).ap()
    XS = [nc.dram_tensor("XS%d" % s, [L, D], F32, kind=sk).ap() for s in range(nslot)]
    cx.QKT = nc.dram_tensor("QKT", [1024, L], BF16, kind=sk).ap()
    cx.KVT = nc.dram_tensor("KVT", [L, 1024], BF16, kind=sk).ap()
    cx.ZT = nc.dram_tensor("ZT", [512, L], BF16, kind=sk).ap()
    psf, psb = mk_psum(pg, nc)
    cx.psf = Rot(psf[:5])
    cx.pso = psf[5]
    cx.psb = Rot(psb)
    gsb = lambda name, shape, dt: pg.buf(nc.alloc_sbuf_tensor(name, shape, dt).ap(), name)
    cx.identb = gsb("identb", [128, 128], BF16)
    pg.dma(cx.identb.ap, cx.cd["identb"], writes=[cx.identb])
    cx.identf = gsb("identf", [128, 128], F32)
    pg.dma(cx.identf.ap, cx.cd["identf"], writes=[cx.identf])
    cx.eps = gsb("eps", [128, 1], F32)
    pg.op("dve", lambda e: e.memset(cx.eps.ap, EPS), writes=[cx.eps])
    cx.one = gsb("one", [128, 1], F32)
    pg.op("dve", lambda e: e.memset(cx.one.ap, 1.0), writes=[cx.one])
    cx.ldst = gsb("ldst", [128, 128], F32)
    cx.m_incl = gsb("m_incl", [128, 2, 128], F32)
    pg.dma(cx.m_incl.ap, cx.cd["m_incl"].rearrange("d s t -> s d t"), writes=[cx.m_incl])
    cx.m_sa = gsb("m_sa", [128, 2, 128], F32)
    pg.dma(cx.m_sa.ap, cx.cd["m_strict_after"].rearrange("d s t -> s d t"), writes=[cx.m_sa])
    cx.m_dn = gsb("m_dn", [128, 2, 2, 128], F32)
    pg.dma(cx.m_dn.ap[:, 0], cx.cd["m_dn"][0].rearrange("j s t -> s j t"), writes=[cx.m_dn])
    pg.dma(cx.m_dn.ap[:, 1], cx.cd["m_dn"][1].rearrange("j s t -> s j t"), writes=[cx.m_dn])
    cx.chunkind = gsb("chunkind", [128, 2, 128], F32)
    pg.dma(cx.chunkind.ap, cx.cd["chunkind"].rearrange("c s m -> s c m"), writes=[cx.chunkind])
    cx.onesf = gsb("onesf", [128, 128], F32)
    pg.dma(cx.onesf.ap, cx.cd["onesf"], writes=[cx.onesf])
    cx.pm = gsb("pm", [128, 2], F32)
    pg.dma(cx.pm.ap, cx.cd["pm"], writes=[cx.pm])
    cx.halfpi = gsb("halfpi", [128, 1], F32)
    pg.op("dve", lambda e: e.memset(cx.halfpi.ap, float(np.pi / 2)), writes=[cx.halfpi])
    cx.zb = gsb("zb", [128, 2048], BF16)
    pg.op("pool", lambda e: e.memset(cx.zb.ap, 0.0), writes=[cx.zb])
    bidx = {"lru": 0, "gla": 1, "dn": 2, "s5": 3}
    for l in range(depth):
        for s in range(nslot):
            xin = xs[s] if l == 0 else XS[s]
            last = (l == depth - 1)
            xout = ys[s] if last else XS[s]
            pg.barrier()
            with ExitStack() as es:
                phase_P(pg, cx, es, L, xin, l)
                pg.barrier()
            for bn in ("lru", "gla", "dn", "s5"):
                if bn not in branches:
                    b = bidx[bn]
                    for t0 in range(0, L, 2048):
                        tw = min(2048, L - t0)
                        for c in range(4):
                            pg.dma(cx.BT[b * 512 + c * 128:b * 512 + (c + 1) * 128, t0:t0 + tw], cx.zb.ap[:, :tw], reads=[cx.zb])
            if "lru" in branches:
                with ExitStack() as es:
                    phase_LRU(pg, cx, es, L, l)
                    pg.barrier()
            if "s5" in branches:
                with ExitStack() as es:
                    phase_S5(pg, cx, es, L, l)
                    pg.barrier()
            if "gla" in branches:
                with ExitStack() as es:
                    phase_GLA(pg, cx, es, L, l)
                    pg.barrier()
            if "dn" in branches:
                with ExitStack() as es:
                    phase_DN(pg, cx, es, L, l)
                    pg.barrier()
            pg.barrier()
            with ExitStack() as es:
                phase_M(pg, cx, es, L, xin, xout, l, last)
                pg.barrier()
    pg.barrier()
    return nc, pg, hc


_CACHE = {}


def kernel(**inputs):
    L = inputs["x_prompt"].shape[1]
    shapes = {nm: inputs[nm].shape for nm in W_NAMES}
    nc, pg, hc = build(L, shapes)
    xp = np.ascontiguousarray(inputs["x_prompt"], dtype=np.float32)
    xsm = np.ascontiguousarray(inputs["x_sample"], dtype=np.float32)
    wmap = {nm: np.ascontiguousarray(inputs[nm], dtype=np.float32) for nm in W_NAMES}
    in_maps = []
    for c in range(8):
        m = dict(wmap)
        for nm, arr in hc.items():
            m["c_" + nm] = arr
        m["x0"] = xp[c]
        m["x1"] = xsm[c % 2]
        in_maps.append(m)
    res = run_bass_kernel_spmd(nc, in_maps, core_ids=list(range(8)))
    y_prompt = np.stack([np.asarray(res.results[c]["y0"], dtype=np.float32) for c in range(8)], axis=0)
    y_sample = np.stack([np.asarray(res.results[c]["y1"], dtype=np.float32) for c in range(2)], axis=0)
    return (y_prompt, y_sample)
```

```python
import numpy as np
import ml_dtypes
from contextlib import ExitStack
import concourse.bass as bass
import concourse.mybir as mybir
from concourse.bass_utils import run_bass_kernel_spmd

F32 = mybir.dt.float32
BF16 = mybir.dt.bfloat16
ALU = mybir.AluOpType
AF = mybir.ActivationFunctionType

D = 1024
BW = 512
D_IN = 5680
EPS = 1e-6
O_LRU_X, O_LRU_G = 0, 512
O_GLA_Q, O_GLA_K, O_GLA_V, O_GLA_G, O_GLA_LR = 1024, 1280, 1536, 2048, 2560
O_DN_QKV, O_DN_G, O_DN_BA = 2592, 4128, 4640
O_S5_U, O_S5_G = 4656, 5168


SAME_ENGINE_SYNC = True
STORES_ON_POOL = True


class Buf:
    __slots__ = ("ap", "w", "r", "name")

    def __init__(self, ap, name=""):
        self.ap = ap
        self.w = []
        self.r = []
        self.name = name

    def __getitem__(self, k):
        return self.ap[k]


class Prog:
    def __init__(self, nc, n_dma_sems=40):
        self.nc = nc
        self.eng = {"pe": nc.tensor, "act": nc.scalar, "dve": nc.vector, "pool": nc.gpsimd, "sp": nc.sync}
        self.sem = {k: nc.alloc_semaphore("s_" + k) for k in self.eng}
        self.cnt = {k: 0 for k in self.eng}
        self.seen = {k: {} for k in self.eng}
        self.dsem = [nc.alloc_semaphore("d%d" % i) for i in range(n_dma_sems)]
        self.dcnt = [0] * n_dma_sems
        self.dnext = 0
        self.ninst = 0

    def buf(self, ap, name=""):
        return Buf(ap, name)

    def uname(self, name):
        self.uid = getattr(self, "uid", 0) + 1
        return "%s_%d" % (name, self.uid)

    def _wait(self, e, dep):
        if dep[0] == "dma":
            key = ("dma", dep[1]); val = dep[2]
            if self.seen[e].get(key, 0) >= val:
                return
            self.eng[e].wait_ge(self.dsem[dep[1]], val)
        else:
            f, val = dep
            if f == e and (e in ("pe", "sp") or not SAME_ENGINE_SYNC):
                return
            key = f
            if self.seen[e].get(key, 0) >= val:
                return
            self.eng[e].wait_ge(self.sem[f], val)
        self.seen[e][key] = val
        self.ninst += 1

    def _deps(self, e, reads, writes):
        for b in reads:
            for d in b.w:
                self._wait(e, d)
        for b in writes:
            for d in b.w:
                self._wait(e, d)
            for d in b.r:
                self._wait(e, d)

    def op(self, e, inst_fn, reads=(), writes=()):
        self._deps(e, reads, writes)
        inst = inst_fn(self.eng[e])
        inst.then_inc(self.sem[e], 1)
        self.cnt[e] += 1
        me = (e, self.cnt[e])
        for b in reads:
            b.r.append(me)
            if len(b.r) > 24:
                b.r = b.r[-24:] if False else self._compress(b.r)
        for b in writes:
            b.w = [me]
            b.r = []
        self.ninst += 1
        return inst

    @staticmethod
    def _compress(lst):
        best = {}
        for d in lst:
            k = ("dma", d[1]) if d[0] == "dma" else d[0]
            v = d[2] if d[0] == "dma" else d[1]
            if k not in best or v > best[k][0]:
                best[k] = (v, d)
        return [x[1] for x in best.values()]

    def dma(self, out, in_, reads=(), writes=(), q=None, **kw):
        if q is None:
            q = "sp" if (len(writes) > 0 or not STORES_ON_POOL) else "pool"
        self._deps(q, reads, writes)
        j = self.dnext
        self.dnext = (self.dnext + 1) % len(self.dsem)
        if self.dcnt[j] > 0:
            self._wait(q, ("dma", j, self.dcnt[j]))
        self.dcnt[j] += 16
        self.eng[q].dma_start(out=out, in_=in_, **kw).then_inc(self.dsem[j], 16)
        me = ("dma", j, self.dcnt[j])
        for b in reads:
            b.r.append(me)
            if len(b.r) > 24:
                b.r = self._compress(b.r)
        for b in writes:
            b.w = [me]
            b.r = []
        self.ninst += 1

    def barrier(self):
        for e in self.eng:
            for f in self.eng:
                if f != e and self.cnt[f] > 0:
                    self._wait(e, (f, self.cnt[f]))
            for j, c in enumerate(self.dcnt):
                if c > 0:
                    self._wait(e, ("dma", j, c))


class Ctx:
    pass


def mk_psum(pg, nc):
    banks = []
    for i in range(6):
        banks.append(pg.buf(nc.alloc_psum_tensor("psf%d" % i, [128, 512], F32).ap(), "psf%d" % i))
    bb = []
    for i in range(2):
        bb.append(pg.buf(nc.alloc_psum_tensor("psb%d" % i, [128, 1024], BF16).ap(), "psb%d" % i))
    return banks, bb


class Rot:
    def __init__(self, items):
        self.items = items
        self.i = 0

    def get(self):
        x = self.items[self.i]
        self.i = (self.i + 1) % len(self.items)
        return x


PF_LRU_X, PF_LRU_G, PF_GLA_Q, PF_GLA_K, PF_DN_QKV, PF_S5_U, PF_S5_G, PF_GLA_LR = 0, 512, 1024, 1280, 1536, 3072, 3584, 4096
PF_ROWS = 4128
PF_CHUNKS = ([(O_LRU_X + 128 * i, 128) for i in range(4)] + [(O_LRU_G + 128 * i, 128) for i in range(4)]
             + [(O_GLA_Q + 128 * i, 128) for i in range(2)] + [(O_GLA_K + 128 * i, 128) for i in range(2)]
             + [(O_DN_QKV + 128 * i, 128) for i in range(12)] + [(O_S5_U + 128 * i, 128) for i in range(4)]
             + [(O_S5_G + 128 * i, 128) for i in range(4)] + [(O_GLA_LR, 32)])
PT_GLA_K, PT_GLA_V, PT_GLA_G, PT_DN_G, PT_DN_BA = 0, 256, 768, 1280, 1792
PT_COLS = 1808
PT_GROUPS = [(1280, 512, 0), (1792, 512, 512), (2304, 256, 1024), (4128, 512, 1280), (4640, 16, 1792)]


def load_cast_bf16(pg, nc, es, dst, src_ap, rows, cols, name, chunk=2048):
    st = [pg.buf(es.enter_context(nc.sbuf_tensor(name + "_st%d" % i, [128, chunk], F32)).ap()) for i in range(2)]
    i = 0
    for c0 in range(0, cols, chunk):
        cw = min(chunk, cols - c0)
        s = st[i % 2]
        pg.dma(s.ap[:rows, :cw], src_ap[:, c0:c0 + cw], writes=[s])
        if i % 2 == 0:
            pg.op("act", lambda e: e.copy(dst[0][:rows, c0:c0 + cw], s.ap[:rows, :cw]), reads=[s], writes=[dst[1]])
        else:
            pg.op("dve", lambda e: e.tensor_copy(out=dst[0][:rows, c0:c0 + cw], in_=s.ap[:rows, :cw]), reads=[s], writes=[dst[1]])
        i += 1


def phase_P(pg, cx, es, L, x_ap, l):
    nc = cx.nc
    TT = 512
    sb = lambda name, shape, dt: pg.buf(es.enter_context(nc.sbuf_tensor(pg.uname(name), shape, dt)).ap(), name)
    wbf = sb("P_w", [128, 8, D_IN], BF16)
    w_src = cx.w["w_in"][l].rearrange("(k p) c -> p k c", p=128)
    WC = D_IN // 4
    st = [sb("P_wst%d" % i, [128, WC], F32) for i in range(2)]
    for k in range(8):
        for q in range(4):
            s = st[q % 2]
            pg.dma(s.ap, w_src[:, k, q * WC:(q + 1) * WC], writes=[s])
            if q % 2 == 0:
                pg.op("act", lambda e: e.copy(wbf.ap[:, k, q * WC:(q + 1) * WC], s.ap), reads=[s], writes=[wbf])
            else:
                pg.op("dve", lambda e: e.tensor_copy(out=wbf.ap[:, k, q * WC:(q + 1) * WC], in_=s.ap), reads=[s], writes=[wbf])
    gk = sb("P_g", [128, 8], F32)
    load_T(pg, cx, gk, gk.ap, cx.w["norm_g"][l].rearrange("(k p) -> k p", p=128), 8)
    xt = [sb("P_x%d" % i, [128, 4, D], F32) for i in range(2)]
    xs = sb("P_xs", [128, D], BF16)
    junk = sb("P_junk", [128, D], BF16)
    ss = sb("P_ss", [128, 4], F32)
    xnT = [sb("P_xnT%d" % i, [128, 8, TT], BF16) for i in range(2)]
    stf = [sb("P_stf%d" % i, [128, 4, TT], F32) for i in range(2)]
    stt = [sb("P_stt%d" % i, [128, PT_COLS], F32) for i in range(2)]
    XNTv = cx.XNT.rearrange("(k p) t -> p k t", p=128)
    xv = x_ap.rearrange("(n j p) d -> n p j d", p=128, j=4)
    gb = gk.ap.unsqueeze(2).to_broadcast([128, 8, 128])
    nt = L // TT
    evac_i = 0
    for it in range(nt):
        x_b = xt[it % 2]
        pg.dma(x_b.ap, xv[it], writes=[x_b])
        xn = xnT[it % 2]
        for j in range(4):
            pg.op("act", lambda e: e.activation(out=junk.ap, in_=x_b.ap[:, j, :], func=AF.Square, accum_out=ss.ap[:, j:j + 1]),
                  reads=[x_b], writes=[junk, ss])
            pg.op("act", lambda e: e.activation(out=ss.ap[:, j:j + 1], in_=ss.ap[:, j:j + 1], func=AF.Sqrt, scale=1.0 / D, bias=cx.eps.ap[:, 0:1]),
                  reads=[ss, cx.eps], writes=[ss])
            pg.op("dve", lambda e: e.reciprocal(out=ss.ap[:, j:j + 1], in_=ss.ap[:, j:j + 1]), reads=[ss], writes=[ss])
            pg.op("dve", lambda e: e.tensor_scalar(out=xs.ap, in0=x_b.ap[:, j, :], scalar1=ss.ap[:, j:j + 1], scalar2=None, op0=ALU.mult),
                  reads=[x_b, ss], writes=[xs])
            pb = cx.psb.get()
            for k in range(8):
                pg.op("pe", lambda e: e.transpose(out=pb.ap[:, k * 128:(k + 1) * 128], in_=xs.ap[:, k * 128:(k + 1) * 128], identity=cx.identb.ap),
                      reads=[xs, cx.identb], writes=[pb])
            pg.op("dve", lambda e: e.tensor_tensor(out=xn.ap[:, :, j * 128:(j + 1) * 128], in0=pb.ap.rearrange("p (k t) -> p k t", k=8), in1=gb, op=ALU.mult),
                  reads=[pb, gk], writes=[xn])
        pg.dma(XNTv[:, :, it * TT:(it + 1) * TT], xn.ap, reads=[xn])
        for ci, (c0, cw) in enumerate(PF_CHUNKS):
            ps = cx.psf.get()
            for k in range(8):
                pg.op("pe", lambda e: e.matmul(ps.ap[:cw, :], lhsT=wbf.ap[:, k, c0:c0 + cw], rhs=xn.ap[:, k, :], start=(k == 0), stop=(k == 7)),
                      reads=[wbf, xn], writes=[ps])
            sbuf = stf[(ci // 4) % 2]
            evac_i += 1
            if evac_i % 2 == 0:
                pg.op("act", lambda e: e.copy(sbuf.ap[:cw, ci % 4, :], ps.ap[:cw, :]), reads=[ps], writes=[sbuf])
            else:
                pg.op("dve", lambda e: e.tensor_copy(out=sbuf.ap[:cw, ci % 4, :], in_=ps.ap[:cw, :]), reads=[ps], writes=[sbuf])
            if ci % 4 == 3:
                cb = ci // 4
                pg.dma(PFv_slice(cx, cb * 4, 4, it * TT, TT), sbuf.ap, reads=[sbuf])
            elif ci == len(PF_CHUNKS) - 1:
                pg.dma(cx.PF[4096:4128, it * TT:(it + 1) * TT], sbuf.ap[:32, 0, :], reads=[sbuf])
        for j in range(4):
            sbuf = stt[j % 2]
            for (c0, cw, o0) in PT_GROUPS:
                ps = cx.psf.get()
                for k in range(8):
                    pg.op("pe", lambda e: e.matmul(ps.ap[:, :cw], lhsT=xn.ap[:, k, j * 128:(j + 1) * 128], rhs=wbf.ap[:, k, c0:c0 + cw], start=(k == 0), stop=(k == 7)),
                          reads=[wbf, xn], writes=[ps])
                evac_i += 1
                if evac_i % 2 == 0:
                    pg.op("act", lambda e: e.copy(sbuf.ap[:, o0:o0 + cw], ps.ap[:, :cw]), reads=[ps], writes=[sbuf])
                else:
                    pg.op("dve", lambda e: e.tensor_copy(out=sbuf.ap[:, o0:o0 + cw], in_=ps.ap[:, :cw]), reads=[ps], writes=[sbuf])
            t0 = it * TT + j * 128
            pg.dma(cx.PT[t0:t0 + 128, :], sbuf.ap, reads=[sbuf])


def PFv_slice(cx, c0, nch, t0, tw):
    return cx.PF[c0 * 128:(c0 + nch) * 128, t0:t0 + tw].rearrange("(c p) t -> p c t", p=128)


def load_T(pg, cx, dst, dst_ap, src_ap, n, st_view=None, wd=128, **kw):
    st = cx.ldst
    pg.dma(st.ap[:n, :wd] if st_view is None else st_view(st.ap[:n, :wd]), src_ap, writes=[st], **kw)
    ps = cx.psf.get()
    pg.op("pe", lambda e: e.transpose(out=ps.ap[:wd, :n], in_=st.ap[:n, :wd], identity=cx.identf.ap[:n, :n]), reads=[st, cx.identf], writes=[ps])
    pg.op("dve", lambda e: e.tensor_copy(out=dst_ap, in_=ps.ap[:wd, :n]), reads=[ps], writes=[dst])


def phase_LRU(pg, cx, es, L, l):
    nc = cx.nc
    sb = lambda name, shape, dt: pg.buf(es.enter_context(nc.sbuf_tensor(pg.uname(name), shape, dt)).ap(), name)
    w = cx.w
    TL = min(2048, L)
    ntile = L // TL
    cw = sb("L_cw", [128, 4, 4], F32)
    load_T(pg, cx, cw, cw.ap.rearrange("p j c -> p (j c)"), w["lru_conv_w"][l].rearrange("j (c p) -> (j c) p", p=128), 16)
    cb = sb("L_cb", [128, 4], F32)
    load_T(pg, cx, cb, cb.ap, w["lru_conv_b"][l].rearrange("(c p) -> c p", p=128), 4)
    bias = sb("L_bias", [128, 2, 2, 4], F32)
    load_T(pg, cx, bias, bias.ap[:, 0].rearrange("p d c -> p (d c)"), w["lru_b_a"][l].rearrange("d (c p) -> (d c) p", p=128), 8)
    load_T(pg, cx, bias, bias.ap[:, 1].rearrange("p d c -> p (d c)"), w["lru_b_x"][l].rearrange("d (c p) -> (d c) p", p=128), 8)
    lam = sb("L_lam", [128, 2, 4], F32)
    load_T(pg, cx, lam, lam.ap.rearrange("p d c -> p (d c)"), w["lru_lambda"][l].rearrange("d (c p) -> (d c) p", p=128), 8)
    coef = sb("L_coef", [128, 2, 4], F32)
    coef2 = sb("L_coef2", [128, 2, 4], F32)
    pg.op("act", lambda e: e.activation(out=coef.ap, in_=lam.ap, func=AF.Exp, scale=-1.0), reads=[lam], writes=[coef])
    pg.op("act", lambda e: e.activation(out=coef.ap, in_=coef.ap, func=AF.Ln, bias=cx.one.ap[:, 0:1]), reads=[coef, cx.one], writes=[coef])
    pg.op("dve", lambda e: e.tensor_scalar(out=coef2.ap, in0=coef.ap, scalar1=-16.0, scalar2=None, op0=ALU.mult), reads=[coef], writes=[coef2])
    pg.op("dve", lambda e: e.tensor_scalar(out=coef.ap, in0=coef.ap, scalar1=-8.0, scalar2=None, op0=ALU.mult), reads=[coef], writes=[coef])
    wg = sb("L_wg", [128, 2, 2, 4, 128], BF16)
    wst = sb("L_wst", [128, 2, 4, 128], F32)
    for ai, nm in enumerate(("lru_w_a", "lru_w_x")):
        pg.dma(wst.ap, w[nm][l].rearrange("d h i j -> i d h j"), writes=[wst])
        pg.op("dve", lambda e: e.tensor_copy(out=wg.ap[:, ai], in_=wst.ap), reads=[wst], writes=[wg])
    XC = sb("L_XC", [128, L], F32)
    XCB = sb("L_XCB", [128, L], BF16)
    HF = sb("L_HF", [128, L], F32)
    xin = sb("L_xin", [128, TL + 3], F32)
    rt = sb("L_r", [128, TL], F32)
    itl = sb("L_i", [128, TL], F32)
    at = sb("L_a", [128, TL], F32)
    t2 = sb("L_t2", [128, TL], F32)
    gt = sb("L_g", [128, TL], F32)
    yb = sb("L_y", [128, TL], BF16)
    carry = sb("L_carry", [128, 1], F32)
    for c in range(4):
        prow = PF_LRU_X + c * 128
        for it in range(ntile):
            t0 = it * TL
            lo = max(t0 - 2, 0)
            hi = min(t0 + TL + 1, L)
            if it == 0 or it == ntile - 1:
                pg.op("pool", lambda e: e.memset(xin.ap, 0.0), writes=[xin])
            pg.dma(xin.ap[:, lo - (t0 - 2):hi - (t0 - 2)], cx.PF[prow:prow + 128, lo:hi], writes=[xin])
            xo = XC.ap[:, t0:t0 + TL]
            pg.op("dve", lambda e: e.tensor_scalar(out=xo, in0=xin.ap[:, 0:TL], scalar1=cw.ap[:, 0, c:c + 1], scalar2=cb.ap[:, c:c + 1], op0=ALU.mult, op1=ALU.add),
                  reads=[xin, cw, cb], writes=[XC])
            for j in range(1, 4):
                pg.op("dve", lambda e: e.scalar_tensor_tensor(out=xo, in0=xin.ap[:, j:j + TL], scalar=cw.ap[:, j, c:c + 1], in1=xo, op0=ALU.mult, op1=ALU.add),
                      reads=[xin, cw, XC], writes=[XC])
            pg.op("act", lambda e: e.copy(XCB.ap[:, t0:t0 + TL], xo), reads=[XC], writes=[XCB])
        for d in range(2):
            order = range(ntile) if d == 0 else range(ntile - 1, -1, -1)
            for n_i, it in enumerate(order):
                t0 = it * TL
                for s0 in range(0, TL, 512):
                    for ai, dst in ((0, rt), (1, itl)):
                        ps = cx.psf.get()
                        pg.op("pe", lambda e: e.matmul(ps.ap, lhsT=wg.ap[:, ai, d, c, :], rhs=XCB.ap[:, t0 + s0:t0 + s0 + 512], start=True, stop=True),
                              reads=[wg, XCB], writes=[ps])
                        pg.op("act", lambda e: e.activation(out=dst.ap[:, s0:s0 + 512], in_=ps.ap, func=AF.Sigmoid, bias=bias.ap[:, ai, d, c:c + 1]),
                              reads=[ps, bias], writes=[dst])
                pg.op("act", lambda e: e.activation(out=at.ap, in_=rt.ap, func=AF.Exp, scale=coef.ap[:, d, c:c + 1]), reads=[rt, coef], writes=[at])
                pg.op("act", lambda e: e.activation(out=t2.ap, in_=rt.ap, func=AF.Exp, scale=coef2.ap[:, d, c:c + 1]), reads=[rt, coef2], writes=[t2])
                pg.op("act", lambda e: e.activation(out=t2.ap, in_=t2.ap, func=AF.Sqrt, scale=-1.0, bias=cx.one.ap[:, 0:1]), reads=[t2, cx.one], writes=[t2])
                pg.op("pool", lambda e: e.tensor_tensor(out=itl.ap, in0=itl.ap, in1=XC.ap[:, t0:t0 + TL], op=ALU.mult), reads=[itl, XC], writes=[itl])
                pg.op("dve", lambda e: e.tensor_tensor(out=t2.ap, in0=t2.ap, in1=itl.ap, op=ALU.mult), reads=[t2, itl], writes=[t2])
                init = 0.0 if n_i == 0 else carry.ap[:, 0:1]
                rds = [at, t2] + ([] if n_i == 0 else [carry])
                if d == 0:
                    ho = HF.ap[:, t0:t0 + TL]
                    pg.op("dve", lambda e: e.tensor_tensor_scan(out=ho, data0=at.ap, data1=t2.ap, initial=init, op0=ALU.mult, op1=ALU.add),
                          reads=rds, writes=[HF])
                    pg.op("dve", lambda e: e.tensor_copy(out=carry.ap, in_=HF.ap[:, t0 + TL - 1:t0 + TL]), reads=[HF], writes=[carry])
                else:
                    rv = lambda ap: bass.AP(ap.tensor, ap.offset + TL - 1, [list(ap.ap[0]), [-1, TL]])
                    pg.op("dve", lambda e: e.tensor_tensor_scan(out=rv(rt.ap), data0=rv(at.ap), data1=rv(t2.ap), initial=init, op0=ALU.mult, op1=ALU.add),
                          reads=rds, writes=[rt])
                    pg.op("dve", lambda e: e.tensor_copy(out=carry.ap, in_=rt.ap[:, 0:1]), reads=[rt], writes=[carry])
                    grow = PF_LRU_G + c * 128
                    pg.dma(gt.ap, cx.PF[grow:grow + 128, t0:t0 + TL], writes=[gt])
                    pg.op("act", lambda e: e.activation(out=gt.ap, in_=gt.ap, func=AF.Silu), reads=[gt], writes=[gt])
                    pg.op("pool", lambda e: e.tensor_tensor(out=rt.ap, in0=rt.ap, in1=HF.ap[:, t0:t0 + TL], op=ALU.add), reads=[rt, HF], writes=[rt])
                    pg.op("dve", lambda e: e.tensor_tensor(out=yb.ap, in0=rt.ap, in1=gt.ap, op=ALU.mult), reads=[rt, gt], writes=[yb])
                    pg.dma(cx.BT[c * 128:(c + 1) * 128, t0:t0 + TL], yb.ap, reads=[yb])


def phase_M(pg, cx, es, L, x_ap, xout_ap, l, last):
    nc = cx.nc
    TT = 512
    sb = lambda name, shape, dt: pg.buf(es.enter_context(nc.sbuf_tensor(pg.uname(name), shape, dt)).ap(), name)
    w = cx.w
    wmg = sb("M_wmg", [128, 4, 8, D], BF16)
    wbr = sb("M_wbr", [128, 4, 4, D], BF16)
    wout = sb("M_wout", [128, 8, D], BF16)
    st = [sb("M_st%d" % i, [128, D], F32) for i in range(2)]
    jobs = []
    for n in range(4):
        for k in range(8):
            jobs.append((w["w_merge_gate"][l, n, k * 128:(k + 1) * 128, :], wmg, wmg.ap[:, n, k, :]))
        for k in range(4):
            jobs.append((w["w_branch"][l, n, k * 128:(k + 1) * 128, :], wbr, wbr.ap[:, n, k, :]))
    for k in range(8):
        jobs.append((w["w_out"][l, k * 128:(k + 1) * 128, :], wout, wout.ap[:, k, :]))
    for i, (src, dbuf, dap) in enumerate(jobs):
        s = st[i % 2]
        pg.dma(s.ap, src, writes=[s])
        if i % 2 == 0:
            pg.op("act", lambda e: e.copy(dap, s.ap), reads=[s], writes=[dbuf])
        else:
            pg.op("dve", lambda e: e.tensor_copy(out=dap, in_=s.ap), reads=[s], writes=[dbuf])
    bmg = sb("M_bmg", [128, 4, 8], F32)
    load_T(pg, cx, bmg, bmg.ap.rearrange("p n c -> p (n c)"), w["b_merge_gate"][l].rearrange("n (c p) -> (n c) p", p=128), 32)
    if last:
        fg = sb("M_fg", [128, D], F32)
        fsrc = w["final_norm_g"]
        pg.dma(fg.ap, bass.AP(fsrc.tensor, fsrc.offset, [[0, 128], [1, D]]), writes=[fg])
        ss = sb("M_ss", [128, 4], F32)
        junk = sb("M_junk", [128, D], BF16)
    xn_ = [sb("M_xn%d" % i, [128, 8, TT], BF16) for i in range(2)]
    bt = sb("M_bt", [128, 16, TT], BF16)
    xt = sb("M_x", [128, 4, D], F32)
    mg = sb("M_mg", [128, 8, TT], BF16)
    gsb = sb("M_g", [128, TT], F32)
    tmp = sb("M_tmp", [128, TT], F32)
    acc = sb("M_acc", [128, TT], F32)
    XNTv = cx.XNT.rearrange("(k p) t -> p k t", p=128)
    BTv = cx.BT.rearrange("(k p) t -> p k t", p=128)
    xv = x_ap.rearrange("(n j p) d -> n p j d", p=128, j=4)
    ov = xout_ap.rearrange("(n j p) d -> n p j d", p=128, j=4)
    for it in range(L // TT):
        ts = slice(it * TT, (it + 1) * TT)
        xn = xn_[it % 2]
        pg.dma(xn.ap, XNTv[:, :, ts], writes=[xn])
        pg.dma(bt.ap, BTv[:, :, ts], writes=[bt])
        pg.dma(xt.ap, xv[it], writes=[xt])
        for oc in range(8):
            ocs = slice(oc * 128, (oc + 1) * 128)
            for n in range(4):
                pg_ = cx.psf.get()
                for k in range(8):
                    pg.op("pe", lambda e: e.matmul(pg_.ap, lhsT=wmg.ap[:, n, k, ocs], rhs=xn.ap[:, k, :], start=(k == 0), stop=(k == 7)),
                          reads=[wmg, xn], writes=[pg_])
                pg.op("act", lambda e: e.activation(out=gsb.ap, in_=pg_.ap, func=AF.Sigmoid, bias=bmg.ap[:, n, oc:oc + 1]), reads=[pg_, bmg], writes=[gsb])
                pb = cx.psf.get()
                for k in range(4):
                    pg.op("pe", lambda e: e.matmul(pb.ap, lhsT=wbr.ap[:, n, k, ocs], rhs=bt.ap[:, n * 4 + k, :], start=(k == 0), stop=(k == 3)),
                          reads=[wbr, bt], writes=[pb])
                if n == 0:
                    pg.op("dve", lambda e: e.tensor_tensor(out=acc.ap, in0=pb.ap, in1=gsb.ap, op=ALU.mult), reads=[pb, gsb], writes=[acc])
                else:
                    pg.op("dve", lambda e: e.tensor_tensor(out=tmp.ap, in0=pb.ap, in1=gsb.ap, op=ALU.mult), reads=[pb, gsb], writes=[tmp])
                    if n < 3:
                        pg.op("pool", lambda e: e.tensor_tensor(out=acc.ap, in0=acc.ap, in1=tmp.ap, op=ALU.add), reads=[acc, tmp], writes=[acc])
                    else:
                        pg.op("pool", lambda e: e.tensor_tensor(out=mg.ap[:, oc, :], in0=acc.ap, in1=tmp.ap, op=ALU.add), reads=[acc, tmp], writes=[mg])
        for j in range(4):
            for hf in range(2):
                hs = slice(hf * 512, (hf + 1) * 512)
                ps = cx.psf.get()
                for k in range(8):
                    pg.op("pe", lambda e: e.matmul(ps.ap, lhsT=mg.ap[:, k, j * 128:(j + 1) * 128], rhs=wout.ap[:, k, hs], start=(k == 0), stop=(k == 7)),
                          reads=[mg, wout], writes=[ps])
                pg.op("dve", lambda e: e.tensor_tensor(out=xt.ap[:, j, hs], in0=ps.ap, in1=xt.ap[:, j, hs], op=ALU.add), reads=[ps, xt], writes=[xt])
            if last:
                pg.op("act", lambda e: e.activation(out=junk.ap, in_=xt.ap[:, j, :], func=AF.Square, accum_out=ss.ap[:, j:j + 1]), reads=[xt], writes=[junk, ss])
                pg.op("act", lambda e: e.activation(out=ss.ap[:, j:j + 1], in_=ss.ap[:, j:j + 1], func=AF.Sqrt, scale=1.0 / D, bias=cx.eps.ap[:, 0:1]),
                      reads=[ss, cx.eps], writes=[ss])
                pg.op("dve", lambda e: e.reciprocal(out=ss.ap[:, j:j + 1], in_=ss.ap[:, j:j + 1]), reads=[ss], writes=[ss])
                pg.op("dve", lambda e: e.scalar_tensor_tensor(out=xt.ap[:, j, :], in0=xt.ap[:, j, :], scalar=ss.ap[:, j:j + 1], in1=fg.ap, op0=ALU.mult, op1=ALU.mult),
                      reads=[xt, ss, fg], writes=[xt])
        pg.dma(ov[it], xt.ap, reads=[xt])


def phase_GLA(pg, cx, es, L, l):
    nc = cx.nc
    sb = lambda name, shape, dt: pg.buf(es.enter_context(nc.sbuf_tensor(pg.uname(name), shape, dt)).ap(), name)
    w = cx.w
    NB = L // 128
    wup = sb("G_wup", [32, 2, 256], F32)
    for d in range(2):
        pg.dma(wup.ap[0:16, d, :], w["gla_w_up"][l, d], writes=[wup])
        pg.dma(wup.ap[16:17, d, :], w["gla_b_up"][l, d:d + 1, :], writes=[wup])
    gn = sb("G_gn", [128, 128], F32)
    gsrc = w["gla_norm_g"][l]
    pg.dma(gn.ap, bass.AP(gsrc.tensor, gsrc.offset, [[0, 128], [1, 128]]), writes=[gn])
    lrT = [sb("G_lrT%d" % i, [32, 128], F32) for i in range(2)]
    for b in lrT:
        pg.op("dve", lambda e: e.memset(b.ap, 1.0), writes=[b])
    qk = [sb("G_qk%d" % i, [128, 4, 128], F32) for i in range(2)]
    tk = [sb("G_tk%d" % i, [128, 1280], F32) for i in range(3)]
    obt = [sb("G_ob%d" % i, [128, 512], F32) for i in range(3)]
    la_ = [sb("G_la%d" % i, [128, 256], F32) for i in range(2)]
    e1_ = [sb("G_e1%d" % i, [128, 256], F32) for i in range(2)]
    eb_ = [sb("G_eb%d" % i, [128, 2, 128], F32) for i in range(2)]
    enb_ = [sb("G_enb%d" % i, [128, 2, 128], F32) for i in range(2)]
    qd_ = [sb("G_qd%d" % i, [128, 2, 128], BF16) for i in range(2)]
    ki_ = [sb("G_ki%d" % i, [128, 2, 128], BF16) for i in range(2)]
    ed_ = [sb("G_ed%d" % i, [128, 256], F32) for i in range(2)]
    kend_ = [sb("G_kend%d" % i, [128, 256], BF16) for i in range(2)]
    vb_ = [sb("G_vb%d" % i, [128, 512], BF16) for i in range(2)]
    sm = [sb("G_sm%d" % i, [128, 128], BF16) for i in range(4)]
    pre_ps = Rot([cx.psf.items[4], cx.pso])
    S32 = [sb("G_S32_%d" % h, [128, 128], F32) for h in range(4)]
    Sb = [sb("G_Sb_%d" % h, [128, 128], BF16) for h in range(4)]
    osb_ = [sb("G_osb%d" % i, [128, 512], F32) for i in range(2)]
    pending = [None]
    ssq = sb("G_ssq", [128, 4], F32)
    junk = sb("G_junk", [128, 128], BF16)
    ysb = sb("G_ysb", [128, 512], BF16)
    yT = sb("G_yT", [128, 4, 128], BF16)
    PFq = cx.PF[PF_GLA_Q:PF_GLA_Q + 512, :].rearrange("(c p) t -> p c t", p=128)
    for d in (1, 0):
        pg.barrier()
        for h in range(4):
            pg.op("dve", lambda e: e.memset(S32[h].ap, 0.0), writes=[S32[h]])
            pg.op("pool", lambda e: e.memset(Sb[h].ap, 0.0), writes=[Sb[h]])
        order = range(NB) if d == 0 else range(NB - 1, -1, -1)
        def pre_gen(bi, blk, d=d):
            t0 = blk * 128
            ts = slice(t0, t0 + 128)
            qkb = qk[bi % 2]; tkb = tk[bi % 3]; lrb = lrT[bi % 2]; ob = obt[bi % 3]
            la, e1, eb, enb, qd, ki, ed, kend, vb = [x_[bi % 2] for x_ in (la_, e1_, eb_, enb_, qd_, ki_, ed_, kend_, vb_)]
            pg.dma(qkb.ap, PFq[:, :, ts], writes=[qkb])
            pg.dma(tkb.ap, cx.PT[ts, 0:1280], writes=[tkb])
            pg.dma(lrb.ap[0:16, :], cx.PF[PF_GLA_LR + 16 * d:PF_GLA_LR + 16 * d + 16, ts], writes=[lrb])
            if d == 0:
                pg.dma(ob.ap, cx.OB[ts, 0:512], writes=[ob])
            zp = pre_ps.get()
            pg.op("pe", lambda e: e.matmul(zp.ap[:, :256], lhsT=lrb.ap[0:17, :], rhs=wup.ap[0:17, d, :], start=True, stop=True), reads=[lrb, wup], writes=[zp])
            pg.op("act", lambda e: e.activation(out=e1.ap, in_=zp.ap[:, :256], func=AF.Exp, scale=-1.0), reads=[zp], writes=[e1])
            yield
            pg.op("act", lambda e: e.activation(out=e1.ap, in_=e1.ap, func=AF.Ln, bias=cx.one.ap[:, 0:1]), reads=[e1, cx.one], writes=[e1])
            yield
            pg.op("dve", lambda e: e.tensor_scalar(out=la.ap, in0=e1.ap, scalar1=-1.0 / 16.0, scalar2=None, op0=ALU.mult), reads=[e1], writes=[la])
            yield
            bp = pre_ps.get()
            for h2 in range(2):
                pg.op("pe", lambda e: e.matmul(bp.ap[:, h2 * 128:(h2 + 1) * 128], lhsT=la.ap[:, h2 * 128:(h2 + 1) * 128], rhs=cx.m_incl.ap[:, d, :], start=True, stop=True),
                      reads=[la, cx.m_incl], writes=[bp])
            bp3 = bp.ap[:, 0:256].rearrange("p (c t) -> p c t", c=2)
            pg.op("act", lambda e: e.activation(out=eb.ap, in_=bp3, func=AF.Exp), reads=[bp], writes=[eb])
            yield
            pg.op("act", lambda e: e.activation(out=enb.ap, in_=bp3, func=AF.Exp, scale=-1.0), reads=[bp], writes=[enb])
            yield
            pg.op("dve", lambda e: e.scalar_tensor_tensor(out=qd.ap, in0=qkb.ap[:, 0:2, :], scalar=0.125, in1=eb.ap, op0=ALU.mult, op1=ALU.mult), reads=[qkb, eb], writes=[qd])
            yield
            pg.op("pool", lambda e: e.tensor_tensor(out=ki.ap, in0=qkb.ap[:, 2:4, :], in1=enb.ap, op=ALU.mult), reads=[qkb, enb], writes=[ki])
            yield
            dp = pre_ps.get()
            pg.op("pe", lambda e: e.matmul(dp.ap[:, :256], lhsT=cx.m_sa.ap[:, d, :], rhs=la.ap, start=True, stop=True), reads=[la, cx.m_sa], writes=[dp])
            pg.op("act", lambda e: e.activation(out=ed.ap, in_=dp.ap[:, :256], func=AF.Exp), reads=[dp], writes=[ed])
            yield
            pg.op("dve", lambda e: e.tensor_tensor(out=kend.ap, in0=tkb.ap[:, 0:256], in1=ed.ap, op=ALU.mult), reads=[tkb, ed], writes=[kend])
            yield
            pg.op("pool", lambda e: e.tensor_copy(out=vb.ap, in_=tkb.ap[:, 256:768]), reads=[tkb], writes=[vb])
            yield

        order_l = list(order)
        for _ in pre_gen(0, order_l[0]):
            pass
        for bi, blk in enumerate(order_l):
            t0 = blk * 128
            ts = slice(t0, t0 + 128)
            osb = osb_[bi % 2]
            qkb = qk[bi % 2]; tkb = tk[bi % 3]; lrb = lrT[bi % 2]; ob = obt[bi % 3]
            la, e1, eb, enb, qd, ki, ed, kend, vb = [x_[bi % 2] for x_ in (la_, e1_, eb_, enb_, qd_, ki_, ed_, kend_, vb_)]
            chunks = (0, 1) if d == 0 else (1, 0)

            def head_gen(h, d=d, chunks=chunks):
                h2, hp = h // 2, (h % 2) * 64
                hc = slice(h * 128, (h + 1) * 128)
                bank = cx.psf.items[h]
                o_ps = bank.ap[:, 384:512]
                pg.op("pe", lambda e: e.matmul(bank.ap[:, 0:128], lhsT=ki.ap[hp:hp + 64, h2, :], rhs=qd.ap[hp:hp + 64, h2, :], start=True, stop=True), reads=[ki, qd], writes=[bank])
                yield
                smb = sm[h]
                pg.op("dve", lambda e: e.tensor_tensor(out=smb.ap, in0=bank.ap[:, 0:128], in1=cx.m_incl.ap[:, d, :], op=ALU.mult), reads=[bank, cx.m_incl], writes=[smb])
                yield
                r0 = chunks[0] * 64
                pg.op("pe", lambda e: e.matmul(o_ps, lhsT=smb.ap, rhs=vb.ap[:, hc], start=True, stop=False), reads=[smb, vb], writes=[bank])
                pg.op("pe", lambda e: e.matmul(bank.ap[r0:r0 + 64, 384:512], lhsT=qd.ap[hp:hp + 64, h2, r0:r0 + 64], rhs=Sb[h].ap[hp:hp + 64, :], start=False, stop=True),
                      reads=[qd, Sb[h]], writes=[bank])
                for ci, c in enumerate(chunks):
                    r0 = c * 64
                    if ci == 1:
                        pg.op("pe", lambda e: e.matmul(bank.ap[r0:r0 + 64, 256:384], lhsT=qd.ap[hp:hp + 64, h2, r0:r0 + 64], rhs=Sb[h].ap[hp:hp + 64, :], start=True, stop=True),
                              reads=[qd, Sb[h]], writes=[bank])
                    pg.op("pe", lambda e: e.matmul(bank.ap[hp:hp + 64, 128:256], lhsT=kend.ap[r0:r0 + 64, h * 64:(h + 1) * 64], rhs=vb.ap[r0:r0 + 64, hc], start=True, stop=True),
                          reads=[kend, vb], writes=[bank])
                    yield
                    col = r0 + 63 if d == 0 else r0
                    pg.op("dve", lambda e: e.scalar_tensor_tensor(out=S32[h].ap[hp:hp + 64, :], in0=S32[h].ap[hp:hp + 64, :], scalar=eb.ap[hp:hp + 64, h2, col:col + 1],
                                                                  in1=bank.ap[hp:hp + 64, 128:256], op0=ALU.mult, op1=ALU.add), reads=[S32[h], eb, bank], writes=[S32[h]])
                    yield
                    pg.op("act", lambda e: e.copy(Sb[h].ap[hp:hp + 64, :], S32[h].ap[hp:hp + 64, :]), reads=[S32[h]], writes=[Sb[h]])
                    yield
                r1 = chunks[1] * 64
                pg.op("dve", lambda e: e.tensor_copy(out=osb.ap[:, hc], in_=o_ps), reads=[bank], writes=[osb])
                pg.op("dve", lambda e: e.tensor_tensor(out=osb.ap[r1:r1 + 64, hc], in0=bank.ap[r1:r1 + 64, 256:384], in1=osb.ap[r1:r1 + 64, hc], op=ALU.add), reads=[bank, osb], writes=[osb])

            gens = [head_gen(h) for h in range(4)] + ([pre_gen(bi + 1, order_l[bi + 1])] if bi + 1 < NB else [])
            if pending[0] is not None:
                gens.append(pending[0])
                pending[0] = None
            while gens:
                for gnr in list(gens):
                    try:
                        next(gnr)
                    except StopIteration:
                        gens.remove(gnr)
            if d == 1:
                pg.dma(cx.OB[ts, 0:512], osb.ap, reads=[osb])
            else:
                def tail_gen(osb=osb, ob=ob, tkb=tkb, ts=ts):
                    pg.op("pool", lambda e: e.tensor_tensor(out=osb.ap, in0=osb.ap, in1=ob.ap, op=ALU.add), reads=[osb, ob], writes=[osb])
                    yield
                    yield from hngs_gen(pg, cx, osb, ssq, junk, gn, tkb, 768, ysb, yT, 512, ts)
                pending[0] = tail_gen()
        if pending[0] is not None:
            for _ in pending[0]:
                pass
            pending[0] = None


def hngs_gen(pg, cx, osb, ssq, junk, gn, tkb, gcol, ysb, yT, bt_row0, ts):
    for h in range(4):
        hc = slice(h * 128, (h + 1) * 128)
        pg.op("act", lambda e: e.activation(out=junk.ap, in_=osb.ap[:, hc], func=AF.Square, accum_out=ssq.ap[:, h:h + 1]), reads=[osb], writes=[junk, ssq])
        yield
    pg.op("act", lambda e: e.activation(out=ssq.ap, in_=ssq.ap, func=AF.Sqrt, scale=1.0 / 128.0, bias=cx.eps.ap[:, 0:1]), reads=[ssq, cx.eps], writes=[ssq])
    yield
    pg.op("dve", lambda e: e.reciprocal(out=ssq.ap, in_=ssq.ap), reads=[ssq], writes=[ssq])
    yield
    for h in range(4):
        hc = slice(h * 128, (h + 1) * 128)
        pg.op("dve", lambda e: e.scalar_tensor_tensor(out=osb.ap[:, hc], in0=osb.ap[:, hc], scalar=ssq.ap[:, h:h + 1], in1=gn.ap, op0=ALU.mult, op1=ALU.mult),
              reads=[osb, ssq, gn], writes=[osb])
        yield
    pg.op("act", lambda e: e.activation(out=tkb.ap[:, gcol:gcol + 512], in_=tkb.ap[:, gcol:gcol + 512], func=AF.Silu), reads=[tkb], writes=[tkb])
    yield
    pg.op("dve", lambda e: e.tensor_tensor(out=ysb.ap, in0=osb.ap, in1=tkb.ap[:, gcol:gcol + 512], op=ALU.mult), reads=[osb, tkb], writes=[ysb])
    yield
    pb = cx.psb.get()
    for h in range(4):
        pg.op("pe", lambda e: e.transpose(out=pb.ap[:, h * 128:(h + 1) * 128], in_=ysb.ap[:, h * 128:(h + 1) * 128], identity=cx.identb.ap), reads=[ysb, cx.identb], writes=[pb])
    pg.op("act", lambda e: e.copy(yT.ap, pb.ap[:, 0:512].rearrange("p (c t) -> p c t", c=4)), reads=[pb], writes=[yT])
    yield
    pg.dma(cx.BT[bt_row0:bt_row0 + 512, ts].rearrange("(c p) t -> p c t", p=128), yT.ap, reads=[yT])

def head_norm_gate_store(*args):
    for _ in hngs_gen(*args):
        pass


def phase_DN(pg, cx, es, L, l):
    nc = cx.nc
    w = cx.w
    NB = L // 128
    with ExitStack() as es0:
        sb = lambda name, shape, dt: pg.buf(es0.enter_context(nc.sbuf_tensor(pg.uname(name), shape, dt)).ap(), name)
        TL = 512
        cwD = sb("D0_cw", [128, 4, 12], F32)
        load_T(pg, cx, cwD, cwD.ap.rearrange("p j c -> p (j c)"), w["dn_conv_w"][l].rearrange("j (c p) -> (j c) p", p=128), 48)
        xin = [sb("D0_xin%d" % i, [128, TL + 3], F32) for i in range(2)]
        xc = sb("D0_xc", [128, TL], F32)
        sq = sb("D0_sq", [128, TL], F32)
        rs = sb("D0_rs", [128, TL], F32)
        fm = sb("D0_fm", [128, 12, TL], BF16)
        tm = sb("D0_tm", [128, 4, 1024], BF16)
        nt = L // TL
        for it in range(nt):
            t0 = it * TL
            lo, hi = max(t0 - 2, 0), min(t0 + TL + 1, L)
            for c in range(12):
                xb = xin[c % 2]
                if it == 0 or it == nt - 1:
                    pg.op("pool", lambda e: e.memset(xb.ap, 0.0), writes=[xb])
                prow = PF_DN_QKV + c * 128
                pg.dma(xb.ap[:, lo - (t0 - 2):hi - (t0 - 2)], cx.PF[prow:prow + 128, lo:hi], writes=[xb])
                pg.op("dve", lambda e: e.tensor_scalar(out=xc.ap, in0=xb.ap[:, 0:TL], scalar1=cwD.ap[:, 0, c:c + 1], scalar2=None, op0=ALU.mult), reads=[xb, cwD], writes=[xc])
                for j in range(1, 4):
                    pg.op("dve", lambda e: e.scalar_tensor_tensor(out=xc.ap, in0=xb.ap[:, j:j + TL], scalar=cwD.ap[:, j, c:c + 1], in1=xc.ap, op0=ALU.mult, op1=ALU.add),
                          reads=[xb, cwD, xc], writes=[xc])
                if c >= 8:
                    pg.op("act", lambda e: e.activation(out=fm.ap[:, c, :], in_=xc.ap, func=AF.Silu), reads=[xc], writes=[fm])
                else:
                    pg.op("act", lambda e: e.activation(out=xc.ap, in_=xc.ap, func=AF.Silu), reads=[xc], writes=[xc])
                    pg.op("pool", lambda e: e.tensor_tensor(out=sq.ap, in0=xc.ap, in1=xc.ap, op=ALU.mult), reads=[xc], writes=[sq])
                    ps = cx.psf.get()
                    pg.op("pe", lambda e: e.matmul(ps.ap, lhsT=cx.onesf.ap, rhs=sq.ap, start=True, stop=True), reads=[cx.onesf, sq], writes=[ps])
                    pg.op("act", lambda e: e.activation(out=rs.ap, in_=ps.ap, func=AF.Sqrt, bias=cx.eps.ap[:, 0:1]), reads=[ps, cx.eps], writes=[rs])
                    pg.op("dve", lambda e: e.reciprocal(out=rs.ap, in_=rs.ap), reads=[rs], writes=[rs])
                    sc = (128.0 ** -0.5) if c < 4 else 1.0
                    pg.op("dve", lambda e: e.scalar_tensor_tensor(out=fm.ap[:, c, :], in0=xc.ap, scalar=sc, in1=rs.ap, op0=ALU.mult, op1=ALU.mult), reads=[xc, rs], writes=[fm])
            pg.dma(cx.QKT[:, t0:t0 + TL].rearrange("(c p) t -> p c t", p=128), fm.ap[:, 0:8, :], reads=[fm])
            for j in range(4):
                pb = cx.psb.get()
                for c in range(8):
                    pg.op("pe", lambda e: e.transpose(out=pb.ap[:, c * 128:(c + 1) * 128], in_=fm.ap[:, 4 + c, j * 128:(j + 1) * 128], identity=cx.identb.ap),
                          reads=[fm, cx.identb], writes=[pb])
                pg.op("act", lambda e: e.copy(tm.ap[:, j, :], pb.ap), reads=[pb], writes=[tm])
            pg.dma(cx.KVT[t0:t0 + TL, :].rearrange("(j p) c -> p j c", p=128), tm.ap, reads=[tm])
        pg.barrier()
    sb = lambda name, shape, dt: pg.buf(es.enter_context(nc.sbuf_tensor(pg.uname(name), shape, dt)).ap(), name)
    ba = sb("D_ba", [128, NB, 16], F32)
    pg.dma(ba.ap, cx.PT[:, PT_DN_BA:PT_DN_BA + 16].rearrange("(n p) c -> p n c", p=128), writes=[ba])
    ba4 = ba.ap.rearrange("p n (d j h) -> p n d j h", d=2, j=2)
    dtb = sb("D_dtb", [128, 8], F32)
    nea = sb("D_nea", [128, 8], F32)
    s1 = w["dn_dt_bias"][l]
    pg.dma(dtb.ap, bass.AP(s1.tensor, s1.offset, [[0, 128], [1, 8]]), writes=[dtb])
    s2 = w["dn_a_log"][l]
    pg.dma(nea.ap, bass.AP(s2.tensor, s2.offset, [[0, 128], [1, 8]]), writes=[nea])
    pg.op("act", lambda e: e.activation(out=nea.ap, in_=nea.ap, func=AF.Exp), reads=[nea], writes=[nea])
    pg.op("dve", lambda e: e.tensor_scalar(out=nea.ap, in0=nea.ap, scalar1=-1.0, scalar2=None, op0=ALU.mult), reads=[nea], writes=[nea])
    beta = sb("D_beta", [128, NB, 2, 4], F32)
    nbeta = sb("D_nbeta", [128, NB, 2, 4], F32)
    g = sb("D_g", [128, NB, 2, 4], F32)
    pg.op("act", lambda e: e.activation(out=beta.ap, in_=ba4[:, :, :, 0, :], func=AF.Sigmoid), reads=[ba], writes=[beta])
    pg.op("dve", lambda e: e.tensor_scalar(out=nbeta.ap, in0=beta.ap, scalar1=-1.0, scalar2=None, op0=ALU.mult), reads=[beta], writes=[nbeta])
    dtb_b = dtb.ap.rearrange("p (d h) -> p d h", d=2).unsqueeze(1).to_broadcast([128, NB, 2, 4])
    nea_b = nea.ap.rearrange("p (d h) -> p d h", d=2).unsqueeze(1).to_broadcast([128, NB, 2, 4])
    pg.op("dve", lambda e: e.tensor_tensor(out=g.ap, in0=ba4[:, :, :, 1, :], in1=dtb_b, op=ALU.add), reads=[ba, dtb], writes=[g])
    pg.op("act", lambda e: e.activation(out=g.ap, in_=g.ap, func=AF.Exp), reads=[g], writes=[g])
    pg.op("act", lambda e: e.activation(out=g.ap, in_=g.ap, func=AF.Ln, bias=cx.one.ap[:, 0:1]), reads=[g, cx.one], writes=[g])
    pg.op("dve", lambda e: e.tensor_tensor(out=g.ap, in0=g.ap, in1=nea_b, op=ALU.mult), reads=[g, nea], writes=[g])
    eG = sb("D_eG", [128, NB, 2, 4], F32)
    eD = sb("D_eD", [128, NB, 2, 4], F32)
    bg = sb("D_bg", [128, NB, 2, 4], F32)
    deB = sb("D_deB", [128, 2, NB, 2, 4], F32)
    NQ = 32
    for d in range(2):
        for n0 in range(0, NB, NQ):
            nn = min(NQ, NB - n0)
            for (msk, dst, fn) in ((cx.m_incl.ap[:, d, :], eG, 0), (cx.m_sa.ap[:, d, :], eD, 0), (cx.chunkind.ap[:, 0, :], deB, 1), (cx.chunkind.ap[:, 1, :], deB, 2)):
                ps = cx.psf.get()
                pv = ps.ap[:, :nn * 4].rearrange("p (n h) -> p n h", h=4)
                pg.op("pe", lambda e: e.matmul(pv, lhsT=msk, rhs=g.ap[:, n0:n0 + nn, d, :], start=True, stop=True), reads=[g, cx.m_incl, cx.m_sa, cx.chunkind], writes=[ps])
                o_ap = dst.ap[:, n0:n0 + nn, d, :] if fn == 0 else dst.ap[:, fn - 1, n0:n0 + nn, d, :]
                pg.op("act", lambda e: e.activation(out=o_ap, in_=pv, func=AF.Exp), reads=[ps], writes=[dst])
    pg.op("dve", lambda e: e.tensor_tensor(out=bg.ap, in0=beta.ap, in1=eG.ap, op=ALU.mult), reads=[beta, eG], writes=[bg])
    gn = sb("D_gn", [128, 128], F32)
    gsrc = w["dn_norm_g"][l]
    pg.dma(gn.ap, bass.AP(gsrc.tensor, gsrc.offset, [[0, 128], [1, 128]]), writes=[gn])
    qk = [sb("D_qk%d" % i, [128, 8, 128], BF16) for i in range(2)]
    kv = [sb("D_kv%d" % i, [128, 2, 4, 128], BF16) for i in range(2)]
    gt = [sb("D_gt%d" % i, [128, 512], F32) for i in range(2)]
    obt = [sb("D_ob%d" % i, [128, 512], F32) for i in range(2)]
    vb4 = sb("D_vb4", [128, 4, 128], BF16)
    kbg4 = sb("D_kbg4", [128, 4, 128], BF16)
    kend4 = sb("D_kend4", [128, 4, 128], BF16)
    gtri = [sb("D_gtri%d" % i, [128, 128], F32) for i in range(4)]
    gam = [sb("D_gam%d" % i, [128, 3, 128], F32) for i in range(4)]
    gamm = [sb("D_gamm%d" % i, [128, 2, 128], F32) for i in range(4)]
    qd = [sb("D_qd%d" % i, [128, 128], BF16) for i in range(4)]
    Cm = [sb("D_C%d" % i, [128, 128], BF16) for i in range(4)]
    attnT = [sb("D_at%d" % i, [128, 128], BF16) for i in range(4)]
    BC = [[sb("D_BC%d_%d" % (h, i), [128, 2, 128], BF16) for i in range(2)] for h in range(4)]
    Pm = [[sb("D_P%d_%d" % (h, i), [128, 128], BF16) for i in range(2)] for h in range(4)]
    Pm32 = [[sb("D_P32_%d_%d" % (h, i), [128, 128], F32) for i in range(2)] for h in range(4)]
    usb = [sb("D_u%d" % i, [128, 128], F32) for i in range(4)]
    wT = [sb("D_wT%d" % i, [128, 128], BF16) for i in range(4)]
    vn = [sb("D_vn%d" % i, [128, 128], BF16) for i in range(4)]
    S32 = [sb("D_S32_%d" % h, [128, 128], F32) for h in range(4)]
    Sb = [sb("D_Sb_%d" % h, [128, 128], BF16) for h in range(4)]
    osb_ = [sb("D_osb%d" % i, [128, 512], F32) for i in range(2)]
    pending = [None]
    ssq = sb("D_ssq", [128, 4], F32)
    junk = sb("D_junk", [128, 128], BF16)
    ysb = sb("D_ysb", [128, 512], BF16)
    yT = sb("D_yT", [128, 4, 128], BF16)
    QKv = cx.QKT.rearrange("(c p) t -> p c t", p=128)
    rr = [0]
    for d in (1, 0):
        pg.barrier()
        for h in range(4):
            pg.op("dve", lambda e: e.memset(S32[h].ap, 0.0), writes=[S32[h]])
            pg.op("pool", lambda e: e.memset(Sb[h].ap, 0.0), writes=[Sb[h]])
        order = range(NB) if d == 0 else range(NB - 1, -1, -1)
        chunks = (0, 1) if d == 0 else (1, 0)
        for bi, blk in enumerate(order):
            t0 = blk * 128
            ts = slice(t0, t0 + 128)
            qkb = qk[bi % 2]; kvb = kv[bi % 2]; ob = obt[bi % 2]; gtb = gt[bi % 2]; osb = osb_[bi % 2]
            pg.dma(qkb.ap, QKv[:, :, ts], writes=[qkb])
            pg.dma(kvb.ap.rearrange("p a h c -> p (a h c)"), cx.KVT[ts, :], writes=[kvb])
            if d == 0:
                pg.dma(ob.ap, cx.OB[ts, 512:1024], writes=[ob])
                pg.dma(gtb.ap[:, 0:512], cx.PT[ts, PT_DN_G:PT_DN_G + 512], writes=[gtb])
            bcast = lambda t: t.ap[:, blk, d, :].unsqueeze(2).to_broadcast([128, 4, 128])
            pg.op("dve", lambda e: e.tensor_tensor(out=vb4.ap, in0=kvb.ap[:, 1], in1=bcast(beta), op=ALU.mult), reads=[kvb, beta], writes=[vb4])
            pg.op("pool", lambda e: e.tensor_tensor(out=kbg4.ap, in0=kvb.ap[:, 0], in1=bcast(bg), op=ALU.mult), reads=[kvb, bg], writes=[kbg4])
            pg.op("pool", lambda e: e.tensor_tensor(out=kend4.ap, in0=kvb.ap[:, 0], in1=bcast(eD), op=ALU.mult), reads=[kvb, eD], writes=[kend4])
            op_ = cx.pso
            def head_gen(h, blk=blk, d=d, qkb=qkb, chunks=chunks, op_=op_):
                i2 = h
                bank = cx.psf.items[h]
                hc = slice(h * 128, (h + 1) * 128)
                gsc = g.ap[:, blk, d, h:h + 1]
                pg.op("dve", lambda e: e.tensor_scalar(out=gtri[i2].ap, in0=cx.m_incl.ap[:, d, :], scalar1=gsc, scalar2=None, op0=ALU.mult), reads=[cx.m_incl, g], writes=[gtri[i2]])
                yield
                dps = bank
                pg.op("pe", lambda e: e.matmul(dps.ap[:, 0:128], lhsT=gtri[i2].ap, rhs=cx.m_sa.ap[:, d, :], start=True, stop=True), reads=[gtri[i2], cx.m_sa], writes=[dps])
                pg.op("pe", lambda e: e.matmul(dps.ap[:, 128:256], lhsT=cx.m_sa.ap[:, d, :], rhs=gtri[i2].ap, start=True, stop=True), reads=[gtri[i2], cx.m_sa], writes=[dps])
                pg.op("pe", lambda e: e.matmul(dps.ap[:, 256:384], lhsT=cx.onesf.ap, rhs=gtri[i2].ap, start=True, stop=True), reads=[gtri[i2], cx.onesf], writes=[dps])
                yield
                pg.op("act", lambda e: e.activation(out=gam[i2].ap.rearrange("p a t -> p (a t)"), in_=dps.ap[:, 0:384], func=AF.Exp), reads=[dps], writes=[gam[i2]])
                yield
                pg.op("pool", lambda e: e.tensor_tensor(out=gamm[i2].ap, in0=gam[i2].ap[:, 0:2, :], in1=cx.m_dn.ap[:, d], op=ALU.mult), reads=[gam[i2], cx.m_dn], writes=[gamm[i2]])
                pg.op("pool", lambda e: e.tensor_tensor(out=qd[i2].ap, in0=qkb.ap[:, h, :], in1=gam[i2].ap[:, 2, :], op=ALU.mult), reads=[qkb, gam[i2]], writes=[qd[i2]])
                kps = bank
                pg.op("pe", lambda e: e.matmul(kps.ap[:, 0:128], lhsT=qkb.ap[:, 4 + h, :], rhs=qkb.ap[:, 4 + h, :], start=True, stop=True), reads=[qkb], writes=[kps])
                pg.op("pe", lambda e: e.matmul(kps.ap[:, 128:256], lhsT=qkb.ap[:, 4 + h, :], rhs=qkb.ap[:, h, :], start=True, stop=True), reads=[qkb], writes=[kps])
                yield
                pg.op("dve", lambda e: e.scalar_tensor_tensor(out=Cm[i2].ap, in0=kps.ap[:, 0:128], scalar=nbeta.ap[:, blk, d, h:h + 1], in1=gamm[i2].ap[:, 0, :], op0=ALU.mult, op1=ALU.mult),
                      reads=[kps, nbeta, gamm[i2]], writes=[Cm[i2]])
                pg.op("dve", lambda e: e.tensor_tensor(out=attnT[i2].ap, in0=kps.ap[:, 128:256], in1=gamm[i2].ap[:, 1, :], op=ALU.mult), reads=[kps, gamm[i2]], writes=[attnT[i2]])
                yield
                tb = bank
                tbv = bank.ap.bitcast(BF16)
                pg.op("pe", lambda e: e.transpose(out=tbv[:, 0:128], in_=Cm[i2].ap, identity=cx.identb.ap), reads=[Cm[i2], cx.identb], writes=[tb])
                yield
                hr = [0]
                bc0 = BC[h][hr[0] % 2]
                pg.op("act", lambda e: e.copy(bc0.ap[:, 0, :], tbv[:, 0:128]), reads=[tb], writes=[bc0])
                pg.op("pool", lambda e: e.tensor_copy(out=bc0.ap[:, 1, :], in_=Cm[i2].ap), reads=[Cm[i2]], writes=[bc0])
                p0 = Pm[h][hr[0] % 2]; p032 = Pm32[h][hr[0] % 2]; hr[0] += 1
                pg.op("pool", lambda e: e.tensor_tensor(out=p0.ap, in0=bc0.ap[:, 0, :], in1=cx.identb.ap, op=ALU.add), reads=[bc0, cx.identb], writes=[p0])
                yield
                bcp, pp, pp32 = bc0, p0, p032
                for k in range(1, 6):
                    sq_ = bank
                    if k < 5:
                        pg.op("pe", lambda e: e.matmul(sq_.ap[:, 0:128], lhsT=bcp.ap[:, 1, :], rhs=bcp.ap[:, 0, :], start=True, stop=True), reads=[bcp], writes=[sq_])
                    pg.op("pe", lambda e: e.matmul(sq_.ap[:, 128:256], lhsT=bcp.ap[:, 0, :], rhs=bcp.ap[:, 1, :], start=True, stop=True), reads=[bcp], writes=[sq_])
                    yield
                    bcn = BC[h][hr[0] % 2]
                    if k < 5:
                        pg.op("act", lambda e: e.copy(bcn.ap.rearrange("p a t -> p (a t)"), sq_.ap[:, 0:256]), reads=[sq_], writes=[bcn])
                    else:
                        pg.op("act", lambda e: e.copy(bcn.ap[:, 1, :], sq_.ap[:, 128:256]), reads=[sq_], writes=[bcn])
                        yield
                    pps = bank
                    pg.op("pe", lambda e: e.matmul(pps.ap[:, 0:128], lhsT=bcn.ap[:, 1, :], rhs=pp.ap, start=True, stop=True), reads=[bcn, pp], writes=[pps])
                    yield
                    pn = Pm[h][hr[0] % 2]; pn32 = Pm32[h][hr[0] % 2]; hr[0] += 1
                    pg.op("dve", lambda e: e.tensor_tensor(out=pn.ap, in0=pps.ap[:, 0:128], in1=pp.ap, op=ALU.add), reads=[pps, pp], writes=[pn])
                    yield
                    bcp, pp, pp32 = bcn, pn, pn32
                ups = bank
                pg.op("pe", lambda e: e.matmul(ups.ap[:, 0:128], lhsT=pp.ap, rhs=vb4.ap[:, h, :], start=True, stop=True), reads=[pp, vb4], writes=[ups])
                pg.op("pe", lambda e: e.matmul(ups.ap[:, 128:256], lhsT=kbg4.ap[:, h, :], rhs=pp.ap, start=True, stop=True), reads=[pp, kbg4], writes=[ups])
                yield
                pg.op("act", lambda e: e.copy(usb[i2].ap, ups.ap[:, 0:128]), reads=[ups], writes=[usb[i2]])
                pg.op("act", lambda e: e.copy(wT[i2].ap, ups.ap[:, 128:256]), reads=[ups], writes=[wT[i2]])
                yield
                for ci, c in enumerate(chunks):
                    r0 = c * 64
                    rs_ = slice(r0, r0 + 64)
                    wps = bank
                    pg.op("pe", lambda e: e.matmul(wps.ap[rs_, 0:128], lhsT=wT[i2].ap[:, rs_], rhs=Sb[h].ap, start=True, stop=True), reads=[wT[i2], Sb[h]], writes=[wps])
                    yield
                    pg.op("dve", lambda e: e.scalar_tensor_tensor(out=vn[i2].ap[rs_, :], in0=wps.ap[rs_, 0:128], scalar=-1.0, in1=usb[i2].ap[rs_, :], op0=ALU.mult, op1=ALU.add), reads=[usb[i2], wps], writes=[vn[i2]])
                    yield
                    pg.op("pe", lambda e: e.matmul(op_.ap[rs_, hc], lhsT=qd[i2].ap[:, rs_], rhs=Sb[h].ap, start=True, stop=False), reads=[qd[i2], Sb[h]], writes=[op_])
                    pg.op("pe", lambda e: e.matmul(op_.ap[rs_, hc], lhsT=attnT[i2].ap[rs_, rs_], rhs=vn[i2].ap[rs_, :], start=False, stop=True), reads=[attnT[i2], vn[i2]], writes=[op_])
                    kvp = bank
                    pg.op("pe", lambda e: e.matmul(kvp.ap[:, 0:128], lhsT=kend4.ap[rs_, h, :], rhs=vn[i2].ap[rs_, :], start=True, stop=True), reads=[kend4, vn[i2]], writes=[kvp])
                    yield
                    pg.op("dve", lambda e: e.scalar_tensor_tensor(out=S32[h].ap, in0=S32[h].ap, scalar=deB.ap[:, c, blk, d, h:h + 1], in1=kvp.ap[:, 0:128], op0=ALU.mult, op1=ALU.add),
                          reads=[S32[h], deB, kvp], writes=[S32[h]])
                    pg.op("act", lambda e: e.copy(Sb[h].ap, S32[h].ap), reads=[S32[h]], writes=[Sb[h]])
                    yield
            gens = [head_gen(h) for h in range(4)]
            if pending[0] is not None:
                gens.append(pending[0])
                pending[0] = None
            while gens:
                for gnr in list(gens):
                    try:
                        next(gnr)
                    except StopIteration:
                        gens.remove(gnr)
            if d == 1:
                pg.op("act", lambda e: e.copy(osb.ap, op_.ap), reads=[op_], writes=[osb])
                pg.dma(cx.OB[ts, 512:1024], osb.ap, reads=[osb])
            else:
                pg.op("dve", lambda e: e.tensor_tensor(out=osb.ap, in0=op_.ap, in1=ob.ap, op=ALU.add), reads=[op_, ob], writes=[osb])
                pending[0] = hngs_gen(pg, cx, osb, ssq, junk, gn, gtb, 0, ysb, yT, 1024, ts)
        if pending[0] is not None:
            for _ in pending[0]:
                pass
            pending[0] = None


def phase_S5(pg, cx, es, L, l):
    nc = cx.nc
    w = cx.w
    sb = lambda name, shape, dt: pg.buf(es.enter_context(nc.sbuf_tensor(pg.uname(name), shape, dt)).ap(), name)
    NS = int(np.ceil(np.log2(L)))
    dve = lambda fn, r, wr: pg.op("dve", fn, reads=r, writes=wr)
    A = lambda nm: sb("S_" + nm, [128, 32], F32)
    lre, lim, dt_, ar, ai, m_, sn, cs, Are, Aim, t1, t2, t3, fre, fim, den = [A(n) for n in
        ("lre", "lim", "dt", "ar", "ai", "m", "sn", "cs", "Are", "Aim", "t1", "t2", "t3", "fre", "fim", "den")]
    load_T(pg, cx, lre, lre.ap, w["s5_lambda_re"][l].rearrange("d (gh gl) p -> (d gh) (gl p)", gl=2), 32)
    load_T(pg, cx, lim, lim.ap, w["s5_lambda_im"][l].rearrange("d (gh gl) p -> (d gh) (gl p)", gl=2), 32)
    ld2 = sb("S_ld2", [32, 2], F32)
    pg.dma(ld2.ap, w["s5_log_dt"][l].rearrange("d (gh gl) -> (d gh) gl", gl=2), writes=[ld2])
    stl = sb("S_stl", [32, 128], F32)
    for gl in range(2):
        dve(lambda e: e.tensor_copy(out=stl.ap[:, 64 * gl:64 * gl + 64], in_=ld2.ap[:, gl:gl + 1].to_broadcast([32, 64])), [ld2], [stl])
    psl = cx.psf.get()
    pg.op("pe", lambda e: e.transpose(out=psl.ap[:, :32], in_=stl.ap, identity=cx.identf.ap[:32, :32]), reads=[stl, cx.identf], writes=[psl])
    dve(lambda e: e.tensor_copy(out=dt_.ap, in_=psl.ap[:, :32]), [psl], [dt_])
    pg.op("act", lambda e: e.activation(out=dt_.ap, in_=dt_.ap, func=AF.Exp), reads=[dt_], writes=[dt_])
    dve(lambda e: e.tensor_tensor(out=ar.ap, in0=lre.ap, in1=dt_.ap, op=ALU.mult), [lre, dt_], [ar])
    dve(lambda e: e.tensor_tensor(out=ai.ap, in0=lim.ap, in1=dt_.ap, op=ALU.mult), [lim, dt_], [ai])
    pg.op("act", lambda e: e.activation(out=m_.ap, in_=ar.ap, func=AF.Exp, scale=1.0 / 16), reads=[ar], writes=[m_])
    pg.op("act", lambda e: e.activation(out=sn.ap, in_=ai.ap, func=AF.Sin, scale=1.0 / 16), reads=[ai], writes=[sn])
    pg.op("act", lambda e: e.activation(out=cs.ap, in_=ai.ap, func=AF.Sin, scale=1.0 / 16, bias=cx.halfpi.ap[:, 0:1]), reads=[ai, cx.halfpi], writes=[cs])
    dve(lambda e: e.tensor_tensor(out=Are.ap, in0=m_.ap, in1=cs.ap, op=ALU.mult), [m_, cs], [Are])
    dve(lambda e: e.tensor_tensor(out=Aim.ap, in0=m_.ap, in1=sn.ap, op=ALU.mult), [m_, sn], [Aim])

    def csquare(re, im):
        dve(lambda e: e.tensor_tensor(out=t1.ap, in0=re, in1=re, op=ALU.mult), [Are, PW], [t1])
        dve(lambda e: e.tensor_tensor(out=t2.ap, in0=im, in1=im, op=ALU.mult), [Aim, PW], [t2])
        dve(lambda e: e.tensor_tensor(out=t3.ap, in0=re, in1=im, op=ALU.mult), [Are, Aim, PW], [t3])

    PW = sb("S_PW", [128, 32, NS, 3], F32)
    for _ in range(4):
        csquare(Are.ap, Aim.ap)
        dve(lambda e: e.tensor_tensor(out=Are.ap, in0=t1.ap, in1=t2.ap, op=ALU.subtract), [t1, t2], [Are])
        dve(lambda e: e.tensor_scalar(out=Aim.ap, in0=t3.ap, scalar1=2.0, scalar2=None, op0=ALU.mult), [t3], [Aim])
    dve(lambda e: e.tensor_tensor(out=den.ap, in0=lre.ap, in1=lre.ap, op=ALU.mult), [lre], [den])
    dve(lambda e: e.tensor_tensor(out=t1.ap, in0=lim.ap, in1=lim.ap, op=ALU.mult), [lim], [t1])
    dve(lambda e: e.tensor_tensor(out=den.ap, in0=den.ap, in1=t1.ap, op=ALU.add), [den, t1], [den])
    dve(lambda e: e.reciprocal(out=den.ap, in_=den.ap), [den], [den])
    dve(lambda e: e.tensor_scalar(out=t3.ap, in0=Are.ap, scalar1=-1.0, scalar2=None, op0=ALU.add), [Are], [t3])
    dve(lambda e: e.tensor_tensor(out=t1.ap, in0=t3.ap, in1=lre.ap, op=ALU.mult), [t3, lre], [t1])
    dve(lambda e: e.tensor_tensor(out=t2.ap, in0=Aim.ap, in1=lim.ap, op=ALU.mult), [Aim, lim], [t2])
    dve(lambda e: e.tensor_tensor(out=t1.ap, in0=t1.ap, in1=t2.ap, op=ALU.add), [t1, t2], [t1])
    dve(lambda e: e.tensor_tensor(out=fre.ap, in0=t1.ap, in1=den.ap, op=ALU.mult), [t1, den], [fre])
    dve(lambda e: e.tensor_tensor(out=t1.ap, in0=Aim.ap, in1=lre.ap, op=ALU.mult), [Aim, lre], [t1])
    dve(lambda e: e.tensor_tensor(out=t2.ap, in0=t3.ap, in1=lim.ap, op=ALU.mult), [t3, lim], [t2])
    dve(lambda e: e.tensor_tensor(out=t1.ap, in0=t1.ap, in1=t2.ap, op=ALU.subtract), [t1, t2], [t1])
    dve(lambda e: e.tensor_tensor(out=fim.ap, in0=t1.ap, in1=den.ap, op=ALU.mult), [t1, den], [fim])
    for k in range(NS):
        if k == 0:
            dve(lambda e: e.tensor_copy(out=PW.ap[:, :, 0, 0], in_=Are.ap), [Are], [PW])
            dve(lambda e: e.tensor_copy(out=PW.ap[:, :, 0, 1], in_=Aim.ap), [Aim], [PW])
        else:
            csquare(PW.ap[:, :, k - 1, 0], PW.ap[:, :, k - 1, 1])
            dve(lambda e: e.tensor_tensor(out=PW.ap[:, :, k, 0], in0=t1.ap, in1=t2.ap, op=ALU.subtract), [t1, t2], [PW])
            dve(lambda e: e.tensor_scalar(out=PW.ap[:, :, k, 1], in0=t3.ap, scalar1=2.0, scalar2=None, op0=ALU.mult), [t3], [PW])
        dve(lambda e: e.tensor_scalar(out=PW.ap[:, :, k, 2], in0=PW.ap[:, :, k, 1], scalar1=-1.0, scalar2=None, op0=ALU.mult), [PW], [PW])
    CL = sb("S_CL", [128, 32, 2, 32], F32)
    Wb = sb("S_Wb", [128, 32, 2, 32], F32)
    PWs = sb("S_PWs", [128, 32, 9, 2], F32)
    dve(lambda e: e.memset(PWs.ap[:, :, 0, 0], 1.0), [], [PWs])
    dve(lambda e: e.memset(PWs.ap[:, :, 0, 1], 0.0), [], [PWs])
    for k in range(1, 9):
        pr, pi_ = PWs.ap[:, :, k - 1, 0], PWs.ap[:, :, k - 1, 1]
        dve(lambda e: e.tensor_tensor(out=t1.ap, in0=pr, in1=PW.ap[:, :, 0, 0], op=ALU.mult), [PWs, PW], [t1])
        dve(lambda e: e.tensor_tensor(out=t2.ap, in0=pi_, in1=PW.ap[:, :, 0, 1], op=ALU.mult), [PWs, PW], [t2])
        dve(lambda e: e.tensor_tensor(out=PWs.ap[:, :, k, 0], in0=t1.ap, in1=t2.ap, op=ALU.subtract), [t1, t2], [PWs])
        dve(lambda e: e.tensor_tensor(out=t1.ap, in0=pr, in1=PW.ap[:, :, 0, 1], op=ALU.mult), [PWs, PW], [t1])
        dve(lambda e: e.tensor_tensor(out=t2.ap, in0=pi_, in1=PW.ap[:, :, 0, 0], op=ALU.mult), [PWs, PW], [t2])
        dve(lambda e: e.tensor_tensor(out=PWs.ap[:, :, k, 1], in0=t1.ap, in1=t2.ap, op=ALU.add), [t1, t2], [PWs])
    with ExitStack() as es1:
        sb1 = lambda name, shape, dt: pg.buf(es1.enter_context(nc.sbuf_tensor(pg.uname(name), shape, dt)).ap(), name)
        Bt = [sb1("S_Bt%d" % i, [128, 32, 16], F32) for i in range(2)]
        Bb = [sb1("S_Bb%d" % i, [128, 32, 16], F32) for i in range(2)]
        tmpb = sb1("S_tmpb", [128, 32, 16], F32)
        for i, nm in enumerate(("s5_b_re", "s5_b_im")):
            base = w[nm][l]
            pg.dma(Bt[i].ap, bass.AP(base.tensor, base.offset, [[16, 128], [2048, 32], [1, 16]]), writes=[Bt[i]])
        fb = lambda t: t.ap.unsqueeze(2).to_broadcast([128, 32, 16])
        dve(lambda e: e.tensor_tensor(out=Bb[0].ap, in0=Bt[0].ap, in1=fb(fre), op=ALU.mult), [Bt[0], fre], [Bb[0]])
        dve(lambda e: e.tensor_tensor(out=tmpb.ap, in0=Bt[1].ap, in1=fb(fim), op=ALU.mult), [Bt[1], fim], [tmpb])
        dve(lambda e: e.tensor_tensor(out=Bb[0].ap, in0=Bb[0].ap, in1=tmpb.ap, op=ALU.subtract), [Bb[0], tmpb], [Bb[0]])
        dve(lambda e: e.tensor_tensor(out=Bb[1].ap, in0=Bt[1].ap, in1=fb(fre), op=ALU.mult), [Bt[1], fre], [Bb[1]])
        dve(lambda e: e.tensor_tensor(out=tmpb.ap, in0=Bt[0].ap, in1=fb(fim), op=ALU.mult), [Bt[0], fim], [tmpb])
        dve(lambda e: e.tensor_tensor(out=Bb[1].ap, in0=Bb[1].ap, in1=tmpb.ap, op=ALU.add), [Bb[1], tmpb], [Bb[1]])
        pg.op("pool", lambda e: e.memset(Wb.ap, 0.0), writes=[Wb])
        for c in range(2):
            dve(lambda e: e.tensor_copy(out=Wb.ap[0:64, :, c, 0:16], in_=Bb[c].ap[0:64]), [Bb[c]], [Wb])
            dve(lambda e: e.tensor_copy(out=Wb.ap[64:128, :, c, 16:32], in_=Bb[c].ap[64:128]), [Bb[c]], [Wb])
        St0 = sb1("S_St0", [128, 64], F32)
        St = sb1("S_St", [128, 128], F32)
        for d in range(2):
            for c, nm in enumerate(("s5_c_re", "s5_c_im")):
                for blk in range(4):
                    pg.dma(St0.ap, w[nm][l, d, 8 * blk:8 * blk + 8].rearrange("g i p -> (g i) p"), writes=[St0])
                    sgn = 1.0 if c == 0 else -1.0
                    for hh in range(2):
                        dve(lambda e: e.tensor_scalar(out=St.ap[:, 64 * hh:64 * hh + 64], in0=St0.ap, scalar1=cx.pm.ap[:, hh:hh + 1], scalar2=sgn, op0=ALU.mult, op1=ALU.mult),
                            [St0, cx.pm], [St])
                    ps = cx.psf.get()
                    pg.op("pe", lambda e: e.transpose(out=ps.ap[:, 0:128], in_=St.ap, identity=cx.identf.ap), reads=[St, cx.identf], writes=[ps])
                    dg0 = d * 16 + blk * 4
                    pg.op("act", lambda e: e.copy(CL.ap[:, dg0:dg0 + 4, c, :], ps.ap[:, 0:128].rearrange("p (q m) -> p q m", q=4)), reads=[ps], writes=[CL])
    dsk = sb("S_dsk", [32, 16], F32)
    load_T(pg, cx, dsk, dsk.ap, w["s5_d"][l].rearrange("(g q) -> g q", q=32), 16, wd=32)
    bgl = sb("S_bgl", [128, 4], F32)
    load_T(pg, cx, bgl, bgl.ap, w["s5_b_glu"][l].rearrange("(c p) -> c p", p=128), 4)
    es2 = ExitStack()
    sb2 = lambda name, shape, dt: pg.buf(es2.enter_context(nc.sbuf_tensor(pg.uname(name), shape, dt)).ap(), name)
    NCH = L // 8
    NSC = int(np.ceil(np.log2(NCH)))
    HW = min(512, NCH)
    NH = NCH // HW
    ub = sb2("S_ub", [32, 8, NCH], BF16)
    UW = min(2048, L)
    ust = [sb2("S_ust%d" % i, [32, UW], F32) for i in range(2)]
    Yc = sb2("S_Yc", [32, L], F32)
    XS_ = [[sb2("S_X%d_%d" % (d, i), [128, NCH + 2], F32) for i in range(3)] for d in range(2)]
    Xb = [[sb2("S_Xb%d_%d" % (d, c), [128, NCH + 2], BF16) for c in range(2)] for d in range(2)]
    Wt_ = [sb2("S_Wt%d" % d, [128, 2, 8, 32], F32) for d in range(2)]
    tmpw_ = [sb2("S_tmpw%d" % d, [128, 8, 32], F32) for d in range(2)]
    WsT = [sb2("S_WsT%d" % d, [32, 2, 8, 128], BF16) for d in range(2)]
    CI = [sb2("S_CI%d" % d, [128, 2, 8, 32], BF16) for d in range(2)]
    CIf_ = [sb2("S_CIf%d" % d, [128, 2, 8, 32], F32) for d in range(2)]
    Kd = [sb2("S_Kd%d" % d, [32, 8, 32], BF16) for d in range(2)]
    ua_t = sb2("S_ua", [32, UW], F32)
    x2_t = sb2("S_x2", [32, UW], F32)
    zo_t = sb2("S_zo", [32, UW], BF16)

    def strided(ap2, start, n, step):
        b0 = ap2[:, start:start + 1]
        return bass.AP(b0.tensor, b0.offset, [list(ap2.ap[0]), [step * ap2.ap[1][0], n]])

    ev = [0]
    prev_pair = None
    out_ps = Rot(cx.psf.items[2:5])
    bcW = lambda a: a.unsqueeze(1).to_broadcast([128, 8, 32])
    bcP = lambda a: a.unsqueeze(2).to_broadcast([128, 8, 32])
    for gh in range(16):
        urow = PF_S5_U + 32 * gh
        for i, t0 in enumerate(range(0, L, UW)):
            st_ = ust[i % 2]
            pg.dma(st_.ap, cx.PF[urow:urow + 32, t0:t0 + UW], writes=[st_])
            pg.op("pool", lambda e: e.tensor_copy(out=ub.ap[:, :, t0 // 8:(t0 + UW) // 8], in_=st_.ap.rearrange("p (n s) -> p s n", s=8)), reads=[st_], writes=[ub])
        def dir_gen(d, gh=gh):
            bank = cx.psf.items[d]
            Wt, tmpw, CIf = Wt_[d], tmpw_[d], CIf_[d]
            dg = d * 16 + gh
            wbr, wbi = Wb.ap[:, dg, 0, :], Wb.ap[:, dg, 1, :]
            pre, pim = PWs.ap[:, dg, 0:8, 0], PWs.ap[:, dg, 0:8, 1]
            dve(lambda e: e.tensor_tensor(out=Wt.ap[:, 0], in0=bcW(wbr), in1=bcP(pre), op=ALU.mult), [Wb, PWs], [Wt])
            yield
            dve(lambda e: e.tensor_tensor(out=tmpw.ap, in0=bcW(wbi), in1=bcP(pim), op=ALU.mult), [Wb, PWs], [tmpw])
            yield
            dve(lambda e: e.tensor_tensor(out=Wt.ap[:, 0], in0=Wt.ap[:, 0], in1=tmpw.ap, op=ALU.subtract), [Wt, tmpw], [Wt])
            yield
            dve(lambda e: e.tensor_tensor(out=Wt.ap[:, 1], in0=bcW(wbr), in1=bcP(pim), op=ALU.mult), [Wb, PWs], [Wt])
            yield
            dve(lambda e: e.tensor_tensor(out=tmpw.ap, in0=bcW(wbi), in1=bcP(pre), op=ALU.mult), [Wb, PWs], [tmpw])
            yield
            dve(lambda e: e.tensor_tensor(out=Wt.ap[:, 1], in0=Wt.ap[:, 1], in1=tmpw.ap, op=ALU.add), [Wt, tmpw], [Wt])
            yield
            for c in range(2):
                for t4 in range(0, 8, 4):
                    ps = bank
                    for tq in range(4):
                        pg.op("pe", lambda e: e.transpose(out=ps.ap[:32, tq * 128:(tq + 1) * 128], in_=Wt.ap[:, c, t4 + tq, :], identity=cx.identf.ap), reads=[Wt, cx.identf], writes=[ps])
                    pg.op("act", lambda e: e.copy(WsT[d].ap[:, c, t4:t4 + 4, :], ps.ap[:32, :].rearrange("p (q m) -> p q m", q=4)), reads=[ps], writes=[WsT[d]])
                    yield
            cl0, cl1 = CL.ap[:, dg, 0, :], CL.ap[:, dg, 1, :]
            pre1, pim1 = PWs.ap[:, dg, 1:9, 0], PWs.ap[:, dg, 1:9, 1]
            dve(lambda e: e.tensor_tensor(out=CIf.ap[:, 0], in0=bcW(cl0), in1=bcP(pre1), op=ALU.mult), [CL, PWs], [CIf])
            yield
            dve(lambda e: e.tensor_tensor(out=tmpw.ap, in0=bcW(cl1), in1=bcP(pim1), op=ALU.mult), [CL, PWs], [tmpw])
            yield
            dve(lambda e: e.tensor_tensor(out=CI[d].ap[:, 0], in0=CIf.ap[:, 0], in1=tmpw.ap, op=ALU.add), [CIf, tmpw], [CI[d]])
            yield
            dve(lambda e: e.tensor_tensor(out=CIf.ap[:, 1], in0=bcW(cl1), in1=bcP(pre1), op=ALU.mult), [CL, PWs], [CIf])
            yield
            dve(lambda e: e.tensor_tensor(out=tmpw.ap, in0=bcW(cl0), in1=bcP(pim1), op=ALU.mult), [CL, PWs], [tmpw])
            yield
            dve(lambda e: e.tensor_tensor(out=CI[d].ap[:, 1], in0=CIf.ap[:, 1], in1=tmpw.ap, op=ALU.subtract), [CIf, tmpw], [CI[d]])
            yield
            ps = bank
            for tau in range(8):
                po = ps.ap[0:32, tau * 32:(tau + 1) * 32]
                pg.op("pe", lambda e: e.matmul(po, lhsT=Wt.ap[:, 0, tau, :], rhs=cl0, start=True, stop=False), reads=[Wt, CL], writes=[ps])
                pg.op("pe", lambda e: e.matmul(po, lhsT=Wt.ap[:, 1, tau, :], rhs=cl1, start=False, stop=True), reads=[Wt, CL], writes=[ps])
            pg.op("act", lambda e: e.copy(Kd[d].ap, ps.ap[0:32, 0:256].rearrange("p (t m) -> p t m", t=8)), reads=[ps], writes=[Kd[d]])
            yield
            re, im, T = XS_[d]
            for b_ in (re, im, T):
                pg.op("pool", lambda e: e.memset(b_.ap, 0.0), writes=[b_])
                yield
            for c, dstb in ((0, re), (1, im)):
                for hf in range(NH):
                    ps = bank
                    for s_ in range(8):
                        tau = 7 - s_ if d == 0 else s_
                        pg.op("pe", lambda e: e.matmul(ps.ap[:, :HW], lhsT=WsT[d].ap[:, c, tau, :], rhs=ub.ap[:, s_, hf * HW:(hf + 1) * HW], start=(s_ == 0), stop=(s_ == 7)),
                              reads=[WsT[d], ub], writes=[ps])
                    ev[0] += 1
                    if ev[0] % 2 == 0:
                        pg.op("act", lambda e: e.copy(dstb.ap[:, 1 + hf * HW:1 + (hf + 1) * HW], ps.ap[:, :HW]), reads=[ps], writes=[dstb])
                        yield
                    else:
                        pg.op("dve", lambda e: e.tensor_copy(out=dstb.ap[:, 1 + hf * HW:1 + (hf + 1) * HW], in_=ps.ap[:, :HW]), reads=[ps], writes=[dstb])
                        yield
            for k in range(NSC):
                sft = 1 << k
                if sft >= NCH:
                    break
                kk = k + 3
                cre, cim, ncim = PW.ap[:, dg, kk, 0:1], PW.ap[:, dg, kk, 1:2], PW.ap[:, dg, kk, 2:3]
                if d == 0:
                    dst, src, keep = slice(1 + sft, 1 + NCH), slice(1, 1 + NCH - sft), slice(1, 1 + sft)
                else:
                    dst, src, keep = slice(1, 1 + NCH - sft), slice(1 + sft, 1 + NCH), slice(1 + NCH - sft, 1 + NCH)
                dve(lambda e: e.scalar_tensor_tensor(out=T.ap[:, dst], in0=re.ap[:, src], scalar=cre, in1=re.ap[:, dst], op0=ALU.mult, op1=ALU.add), [re, PW], [T])
                yield
                dve(lambda e: e.scalar_tensor_tensor(out=T.ap[:, dst], in0=im.ap[:, src], scalar=ncim, in1=T.ap[:, dst], op0=ALU.mult, op1=ALU.add), [im, T, PW], [T])
                yield
                pg.op("pool", lambda e: e.tensor_copy(out=T.ap[:, keep], in_=re.ap[:, keep]), reads=[re], writes=[T])
                yield
                if d == 0:
                    rv = lambda ap, sl: bass.AP(ap.tensor, ap[:, sl].offset + (sl.stop - sl.start) - 1, [list(ap.ap[0]), [-1, sl.stop - sl.start]])
                    dve(lambda e: e.scalar_tensor_tensor(out=rv(im.ap, dst), in0=rv(im.ap, src), scalar=cre, in1=rv(im.ap, dst), op0=ALU.mult, op1=ALU.add), [im, PW], [im])
                    yield
                else:
                    dve(lambda e: e.scalar_tensor_tensor(out=im.ap[:, dst], in0=im.ap[:, src], scalar=cre, in1=im.ap[:, dst], op0=ALU.mult, op1=ALU.add), [im, PW], [im])
                    yield
                dve(lambda e: e.scalar_tensor_tensor(out=im.ap[:, dst], in0=re.ap[:, src], scalar=cim, in1=im.ap[:, dst], op0=ALU.mult, op1=ALU.add), [re, im, PW], [im])
                yield
                re, T = T, re
            pg.op("pool", lambda e: e.tensor_copy(out=Xb[d][0].ap, in_=re.ap), reads=[re], writes=[Xb[d][0]])
            yield
            pg.op("pool", lambda e: e.tensor_copy(out=Xb[d][1].ap, in_=im.ap), reads=[im], writes=[Xb[d][1]])
            yield

        def gelu_gen(gh_, urow_):
            for t0 in range(0, L, UW):
                tsl = slice(t0, t0 + UW)
                ua, x2, zo = ua_t.ap, x2_t.ap, zo_t.ap
                pg.dma(ua, cx.PF[urow_:urow_ + 32, tsl], writes=[ua_t])
                yv = Yc.ap[:, tsl]
                dve(lambda e: e.scalar_tensor_tensor(out=yv, in0=ua, scalar=dsk.ap[:, gh_:gh_ + 1], in1=yv, op0=ALU.mult, op1=ALU.add), [ua_t, dsk, Yc], [Yc])
                yield
                pg.op("pool", lambda e: e.tensor_tensor(out=x2, in0=yv, in1=yv, op=ALU.mult), reads=[Yc], writes=[x2_t])
                yield
                dve(lambda e: e.tensor_scalar(out=x2, in0=x2, scalar1=0.044715, scalar2=1.0, op0=ALU.mult, op1=ALU.add), [x2_t], [x2_t])
                yield
                pg.op("pool", lambda e: e.tensor_tensor(out=x2, in0=x2, in1=yv, op=ALU.mult), reads=[Yc, x2_t], writes=[x2_t])
                yield
                pg.op("act", lambda e: e.activation(out=x2, in_=x2, func=AF.Sigmoid, scale=1.5957691216), reads=[x2_t], writes=[x2_t])
                yield
                dve(lambda e: e.tensor_tensor(out=zo, in0=x2, in1=yv, op=ALU.mult), [x2_t, Yc], [zo_t])
                pg.dma(cx.ZT[urow_ - PF_S5_U:urow_ - PF_S5_U + 32, tsl], zo, reads=[zo_t])
                yield

        gens = [dir_gen(0), dir_gen(1)] + ([gelu_gen(*prev_pair)] if prev_pair is not None else [])
        while gens:
            for gnr in list(gens):
                try:
                    next(gnr)
                except StopIteration:
                    gens.remove(gnr)
        prev_pair = (gh, urow)
        for hf in range(NH):
            for sp in range(8):
                ps = out_ps.get()
                po = ps.ap[0:32, :HW]
                mm = []
                for c in range(2):
                    mm.append((CI[0].ap[:, c, sp, :], Xb[0][c].ap[:, hf * HW:hf * HW + HW], [CI[0], Xb[0][c]]))
                    mm.append((CI[1].ap[:, c, 7 - sp, :], Xb[1][c].ap[:, hf * HW + 2:hf * HW + 2 + HW], [CI[1], Xb[1][c]]))
                for s_ in range(0, sp + 1):
                    mm.append((Kd[0].ap[:, sp - s_, :], ub.ap[:, s_, hf * HW:(hf + 1) * HW], [Kd[0], ub]))
                for s_ in range(sp, 8):
                    mm.append((Kd[1].ap[:, s_ - sp, :], ub.ap[:, s_, hf * HW:(hf + 1) * HW], [Kd[1], ub]))
                for i, (lh, rh, rd) in enumerate(mm):
                    pg.op("pe", lambda e: e.matmul(po, lhsT=lh, rhs=rh, start=(i == 0), stop=(i == len(mm) - 1)), reads=rd, writes=[ps])
                pg.op("act", lambda e: e.copy(strided(Yc.ap, hf * HW * 8 + sp, HW, 8), po), reads=[ps], writes=[Yc])
    gens = [gelu_gen(*prev_pair)]
    for gnr in gens:
        for _ in gnr:
            pass
    pg.barrier()
    es2.close()
    wg = sb("S_wg", [128, 4, 512], BF16)
    wst = sb("S_wst", [128, 2048], F32)
    pg.dma(wst.ap[:, 0:2048].rearrange("p (k c) -> p k c", k=4), w["s5_w_glu"][l].rearrange("(k p) c -> p k c", p=128), writes=[wst])
    dve(lambda e: e.tensor_copy(out=wg.ap, in_=wst.ap[:, 0:2048].rearrange("p (k c) -> p k c", k=4)), [wst], [wg])
    zt = [sb("S_zt%d" % i, [128, 4, 512], BF16) for i in range(2)]
    gt = [sb("S_gt%d" % i, [128, 4, 512], F32) for i in range(2)]
    sg = sb("S_sg", [128, 512], F32)
    yo = [sb("S_yo%d" % i, [128, 4, 512], BF16) for i in range(2)]
    ZTv = cx.ZT.rearrange("(k p) t -> p k t", p=128)
    NT = L // 512
    for it in range(NT):
        tsl = slice(it * 512, (it + 1) * 512)
        z_ = zt[it % 2]; g_ = gt[it % 2]; y_ = yo[it % 2]
        pg.dma(z_.ap, ZTv[:, :, tsl], writes=[z_])
        pg.dma(g_.ap, cx.PF[PF_S5_G:PF_S5_G + 512, tsl].rearrange("(k p) t -> p k t", p=128), writes=[g_])
        pg.op("act", lambda e: e.activation(out=g_.ap, in_=g_.ap, func=AF.Silu), reads=[g_], writes=[g_])
        pg.op("pool", lambda e: e.tensor_tensor(out=g_.ap, in0=g_.ap, in1=z_.ap, op=ALU.mult), reads=[g_, z_], writes=[g_])
        for oc in range(4):
            ps = cx.psf.get()
            for k in range(4):
                pg.op("pe", lambda e: e.matmul(ps.ap, lhsT=wg.ap[:, k, oc * 128:(oc + 1) * 128], rhs=z_.ap[:, k, :], start=(k == 0), stop=(k == 3)), reads=[wg, z_], writes=[ps])
            pg.op("act", lambda e: e.activation(out=sg.ap, in_=ps.ap, func=AF.Sigmoid, bias=bgl.ap[:, oc:oc + 1]), reads=[ps, bgl], writes=[sg])
            dve(lambda e: e.tensor_tensor(out=y_.ap[:, oc, :], in0=sg.ap, in1=g_.ap[:, oc, :], op=ALU.mult), [sg, g_], [y_])
        pg.dma(cx.BT[1536:2048, tsl].rearrange("(k p) t -> p k t", p=128), y_.ap, reads=[y_])


W_NAMES = ["norm_g", "w_in", "lru_conv_w", "lru_conv_b", "lru_w_a", "lru_b_a", "lru_w_x", "lru_b_x", "lru_lambda",
           "gla_w_up", "gla_b_up", "gla_norm_g", "dn_conv_w", "dn_a_log", "dn_dt_bias", "dn_norm_g",
           "s5_lambda_re", "s5_lambda_im", "s5_log_dt", "s5_b_re", "s5_b_im", "s5_c_re", "s5_c_im", "s5_d",
           "s5_w_glu", "s5_b_glu", "w_branch", "w_merge_gate", "b_merge_gate", "w_out", "final_norm_g"]


def host_consts():
    c = {}
    c["identb"] = np.eye(128, dtype=np.float32).astype(ml_dtypes.bfloat16)
    c["identf"] = np.eye(128, dtype=np.float32)
    idx = np.arange(128)
    same = (idx[:, None] // 64) == (idx[None, :] // 64)
    le = idx[:, None] <= idx[None, :]
    lt = idx[:, None] < idx[None, :]
    c["m_incl"] = np.stack([(same & le), (same & le.T)]).astype(np.float32)
    c["m_strict_after"] = np.stack([(same & lt.T), (same & lt)]).astype(np.float32)
    c["m_dn"] = np.stack([c["m_strict_after"], c["m_incl"]], axis=1)
    c["chunkind"] = np.stack([np.repeat((idx // 64 == cc)[:, None], 128, axis=1) for cc in range(2)]).astype(np.float32)
    c["onesf"] = np.ones((128, 128), np.float32)
    ev = ((idx // 16) % 2 == 0).astype(np.float32)
    c["pm"] = np.stack([ev, 1.0 - ev], axis=1).astype(np.float32)
    return c


def build(L, shapes, nslot=2, depth=2, debug=False, branches=("lru", "gla", "dn", "s5")):
    from contextlib import ExitStack
    nc = bass.Bass("TRN2", target_bir_lowering=False)
    pg = Prog(nc)
    cx = Ctx()
    cx.nc = nc
    cx.pg = pg
    cx.w = {}
    for nm in W_NAMES:
        cx.w[nm] = nc.dram_tensor(nm, list(shapes[nm]), F32, kind="ExternalInput").ap()
    hc = host_consts()
    cx.cd = {}
    for nm, arr in hc.items():
        cx.cd[nm] = nc.dram_tensor("c_" + nm, list(arr.shape), BF16 if arr.dtype == ml_dtypes.bfloat16 else F32, kind="ExternalInput").ap()
    xs = [nc.dram_tensor("x%d" % s, [L, D], F32, kind="ExternalInput").ap() for s in range(nslot)]
    ys = [nc.dram_tensor("y%d" % s, [L, D], F32, kind="ExternalOutput").ap() for s in range(nslot)]
    sk = "ExternalOutput" if debug else "Internal"
    cx.PF = nc.dram_tensor("PF", [PF_ROWS, L], F32, kind=sk).ap()
    cx.PT = nc.dram_tensor("PT", [L, PT_COLS], F32, kind=sk).ap()
    cx.XNT = nc.dram_tensor("XNT", [D, L], BF16, kind=sk).ap()
    cx.BT = nc.dram_tensor("BT", [2048, L], BF16, kind=sk).ap()
    cx.OB = nc.dram_tensor("OB", [L, 1024], F32, kind=sk).ap()
    XS = [nc.dram_tensor("XS%d" % s, [L, D], F32, kind=sk).ap() for s in range(nslot)]
    cx.QKT = nc.dram_tensor("QKT", [1024, L], BF16, kind=sk).ap()
    cx.KVT = nc.dram_tensor("KVT", [L, 1024], BF16, kind=sk).ap()
    cx.ZT = nc.dram_tensor("ZT", [512, L], BF16, kind=sk).ap()
    psf, psb = mk_psum(pg, nc)
    cx.psf = Rot(psf[:5])
    cx.pso = psf[5]
    cx.psb = Rot(psb)
    gsb = lambda name, shape, dt: pg.buf(nc.alloc_sbuf_tensor(name, shape, dt).ap(), name)
    cx.identb = gsb("identb", [128, 128], BF16)
    pg.dma(cx.identb.ap, cx.cd["identb"], writes=[cx.identb])
    cx.identf = gsb("identf", [128, 128], F32)
    pg.dma(cx.identf.ap, cx.cd["identf"], writes=[cx.identf])
    cx.eps = gsb("eps", [128, 1], F32)
    pg.op("dve", lambda e: e.memset(cx.eps.ap, EPS), writes=[cx.eps])
    cx.one = gsb("one", [128, 1], F32)
    pg.op("dve", lambda e: e.memset(cx.one.ap, 1.0), writes=[cx.one])
    cx.ldst = gsb("ldst", [128, 128], F32)
    cx.m_incl = gsb("m_incl", [128, 2, 128], F32)
    pg.dma(cx.m_incl.ap, cx.cd["m_incl"].rearrange("d s t -> s d t"), writes=[cx.m_incl])
    cx.m_sa = gsb("m_sa", [128, 2, 128], F32)
    pg.dma(cx.m_sa.ap, cx.cd["m_strict_after"].rearrange("d s t -> s d t"), writes=[cx.m_sa])
    cx.m_dn = gsb("m_dn", [128, 2, 2, 128], F32)
    pg.dma(cx.m_dn.ap[:, 0], cx.cd["m_dn"][0].rearrange("j s t -> s j t"), writes=[cx.m_dn])
    pg.dma(cx.m_dn.ap[:, 1], cx.cd["m_dn"][1].rearrange("j s t -> s j t"), writes=[cx.m_dn])
    cx.chunkind = gsb("chunkind", [128, 2, 128], F32)
    pg.dma(cx.chunkind.ap, cx.cd["chunkind"].rearrange("c s m -> s c m"), writes=[cx.chunkind])
    cx.onesf = gsb("onesf", [128, 128], F32)
    pg.dma(cx.onesf.ap, cx.cd["onesf"], writes=[cx.onesf])
    cx.pm = gsb("pm", [128, 2], F32)
    pg.dma(cx.pm.ap, cx.cd["pm"], writes=[cx.pm])
    cx.halfpi = gsb("halfpi", [128, 1], F32)
    pg.op("dve", lambda e: e.memset(cx.halfpi.ap, float(np.pi / 2)), writes=[cx.halfpi])
    cx.zb = gsb("zb", [128, 2048], BF16)
    pg.op("pool", lambda e: e.memset(cx.zb.ap, 0.0), writes=[cx.zb])
    bidx = {"lru": 0, "gla": 1, "dn": 2, "s5": 3}
    for l in range(depth):
        for s in range(nslot):
            xin = xs[s] if l == 0 else XS[s]
            last = (l == depth - 1)
            xout = ys[s] if last else XS[s]
            pg.barrier()
            with ExitStack() as es:
                phase_P(pg, cx, es, L, xin, l)
                pg.barrier()
            for bn in ("lru", "gla", "dn", "s5"):
                if bn not in branches:
                    b = bidx[bn]
                    for t0 in range(0, L, 2048):
                        tw = min(2048, L - t0)
                        for c in range(4):
                            pg.dma(cx.BT[b * 512 + c * 128:b * 512 + (c + 1) * 128, t0:t0 + tw], cx.zb.ap[:, :tw], reads=[cx.zb])
            if "lru" in branches:
                with ExitStack() as es:
                    phase_LRU(pg, cx, es, L, l)
                    pg.barrier()
            if "s5" in branches:
                with ExitStack() as es:
                    phase_S5(pg, cx, es, L, l)
                    pg.barrier()
            if "gla" in branches:
                with ExitStack() as es:
                    phase_GLA(pg, cx, es, L, l)
                    pg.barrier()
            if "dn" in branches:
                with ExitStack() as es:
                    phase_DN(pg, cx, es, L, l)
                    pg.barrier()
            pg.barrier()
            with ExitStack() as es:
                phase_M(pg, cx, es, L, xin, xout, l, last)
                pg.barrier()
    pg.barrier()
    return nc, pg, hc


_CACHE = {}


def kernel(**inputs):
    L = inputs["x_prompt"].shape[1]
    shapes = {nm: inputs[nm].shape for nm in W_NAMES}
    nc, pg, hc = build(L, shapes)
    xp = np.ascontiguousarray(inputs["x_prompt"], dtype=np.float32)
    xsm = np.ascontiguousarray(inputs["x_sample"], dtype=np.float32)
    wmap = {nm: np.ascontiguousarray(inputs[nm], dtype=np.float32) for nm in W_NAMES}
    in_maps = []
    for c in range(8):
        m = dict(wmap)
        for nm, arr in hc.items():
            m["c_" + nm] = arr
        m["x0"] = xp[c]
        m["x1"] = xsm[c % 2]
        in_maps.append(m)
    res = run_bass_kernel_spmd(nc, in_maps, core_ids=list(range(8)))
    y_prompt = np.stack([np.asarray(res.results[c]["y0"], dtype=np.float32) for c in range(8)], axis=0)
    y_sample = np.stack([np.asarray(res.results[c]["y1"], dtype=np.float32) for c in range(2)], axis=0)
    return (y_prompt, y_sample)
```

```python
import numpy as np
import ml_dtypes
from contextlib import ExitStack
import concourse.bass as bass
import concourse.mybir as mybir
from concourse.bass_utils import run_bass_kernel_spmd

F32 = mybir.dt.float32
BF16 = mybir.dt.bfloat16
ALU = mybir.AluOpType
AF = mybir.ActivationFunctionType

D = 1024
BW = 512
D_IN = 5680
EPS = 1e-6
O_LRU_X, O_LRU_G = 0, 512
O_GLA_Q, O_GLA_K, O_GLA_V, O_GLA_G, O_GLA_LR = 1024, 1280, 1536, 2048, 2560
O_DN_QKV, O_DN_G, O_DN_BA = 2592, 4128, 4640
O_S5_U, O_S5_G = 4656, 5168


SAME_ENGINE_SYNC = True
STORES_ON_POOL = True


class Buf:
    __slots__ = ("ap", "w", "r", "name")

    def __init__(self, ap, name=""):
        self.ap = ap
        self.w = []
        self.r = []
        self.name = name

    def __getitem__(self, k):
        return self.ap[k]


class Prog:
    def __init__(self, nc, n_dma_sems=40):
        self.nc = nc
        self.eng = {"pe": nc.tensor, "act": nc.scalar, "dve": nc.vector, "pool": nc.gpsimd, "sp": nc.sync}
        self.sem = {k: nc.alloc_semaphore("s_" + k) for k in self.eng}
        self.cnt = {k: 0 for k in self.eng}
        self.seen = {k: {} for k in self.eng}
        self.dsem = [nc.alloc_semaphore("d%d" % i) for i in range(n_dma_sems)]
        self.dcnt = [0] * n_dma_sems
        self.dnext = 0
        self.ninst = 0

    def buf(self, ap, name=""):
        return Buf(ap, name)

    def uname(self, name):
        self.uid = getattr(self, "uid", 0) + 1
        return "%s_%d" % (name, self.uid)

    def _wait(self, e, dep):
        if dep[0] == "dma":
            key = ("dma", dep[1]); val = dep[2]
            if self.seen[e].get(key, 0) >= val:
                return
            self.eng[e].wait_ge(self.dsem[dep[1]], val)
        else:
            f, val = dep
            if f == e and (e in ("pe", "sp") or not SAME_ENGINE_SYNC):
                return
            key = f
            if self.seen[e].get(key, 0) >= val:
                return
            self.eng[e].wait_ge(self.sem[f], val)
        self.seen[e][key] = val
        self.ninst += 1

    def _deps(self, e, reads, writes):
        for b in reads:
            for d in b.w:
                self._wait(e, d)
        for b in writes:
            for d in b.w:
                self._wait(e, d)
            for d in b.r:
                self._wait(e, d)

    def op(self, e, inst_fn, reads=(), writes=()):
        self._deps(e, reads, writes)
        inst = inst_fn(self.eng[e])
        inst.then_inc(self.sem[e], 1)
        self.cnt[e] += 1
        me = (e, self.cnt[e])
        for b in reads:
            b.r.append(me)
            if len(b.r) > 24:
                b.r = b.r[-24:] if False else self._compress(b.r)
        for b in writes:
            b.w = [me]
            b.r = []
        self.ninst += 1
        return inst

    @staticmethod
    def _compress(lst):
        best = {}
        for d in lst:
            k = ("dma", d[1]) if d[0] == "dma" else d[0]
            v = d[2] if d[0] == "dma" else d[1]
            if k not in best or v > best[k][0]:
                best[k] = (v, d)
        return [x[1] for x in best.values()]

    def dma(self, out, in_, reads=(), writes=(), q=None, **kw):
        if q is None:
            q = "sp" if (len(writes) > 0 or not STORES_ON_POOL) else "pool"
        self._deps(q, reads, writes)
        j = self.dnext
        self.dnext = (self.dnext + 1) % len(self.dsem)
        if self.dcnt[j] > 0:
            self._wait(q, ("dma", j, self.dcnt[j]))
        self.dcnt[j] += 16
        self.eng[q].dma_start(out=out, in_=in_, **kw).then_inc(self.dsem[j], 16)
        me = ("dma", j, self.dcnt[j])
        for b in reads:
            b.r.append(me)
            if len(b.r) > 24:
                b.r = self._compress(b.r)
        for b in writes:
            b.w = [me]
            b.r = []
        self.ninst += 1

    def barrier(self):
        for e in self.eng:
            for f in self.eng:
                if f != e and self.cnt[f] > 0:
                    self._wait(e, (f, self.cnt[f]))
            for j, c in enumerate(self.dcnt):
                if c > 0:
                    self._wait(e, ("dma", j, c))


class Ctx:
    pass


def mk_psum(pg, nc):
    banks = []
    for i in range(6):
        banks.append(pg.buf(nc.alloc_psum_tensor("psf%d" % i, [128, 512], F32).ap(), "psf%d" % i))
    bb = []
    for i in range(2):
        bb.append(pg.buf(nc.alloc_psum_tensor("psb%d" % i, [128, 1024], BF16).ap(), "psb%d" % i))
    return banks, bb


class Rot:
    def __init__(self, items):
        self.items = items
        self.i = 0

    def get(self):
        x = self.items[self.i]
        self.i = (self.i + 1) % len(self.items)
        return x


PF_LRU_X, PF_LRU_G, PF_GLA_Q, PF_GLA_K, PF_DN_QKV, PF_S5_U, PF_S5_G, PF_GLA_LR = 0, 512, 1024, 1280, 1536, 3072, 3584, 4096
PF_ROWS = 4128
PF_CHUNKS = ([(O_LRU_X + 128 * i, 128) for i in range(4)] + [(O_LRU_G + 128 * i, 128) for i in range(4)]
             + [(O_GLA_Q + 128 * i, 128) for i in range(2)] + [(O_GLA_K + 128 * i, 128) for i in range(2)]
             + [(O_DN_QKV + 128 * i, 128) for i in range(12)] + [(O_S5_U + 128 * i, 128) for i in range(4)]
             + [(O_S5_G + 128 * i, 128) for i in range(4)] + [(O_GLA_LR, 32)])
PT_GLA_K, PT_GLA_V, PT_GLA_G, PT_DN_G, PT_DN_BA = 0, 256, 768, 1280, 1792
PT_COLS = 1808
PT_GROUPS = [(1280, 512, 0), (1792, 512, 512), (2304, 256, 1024), (4128, 512, 1280), (4640, 16, 1792)]


def load_cast_bf16(pg, nc, es, dst, src_ap, rows, cols, name, chunk=2048):
    st = [pg.buf(es.enter_context(nc.sbuf_tensor(name + "_st%d" % i, [128, chunk], F32)).ap()) for i in range(2)]
    i = 0
    for c0 in range(0, cols, chunk):
        cw = min(chunk, cols - c0)
        s = st[i % 2]
        pg.dma(s.ap[:rows, :cw], src_ap[:, c0:c0 + cw], writes=[s])
        if i % 2 == 0:
            pg.op("act", lambda e: e.copy(dst[0][:rows, c0:c0 + cw], s.ap[:rows, :cw]), reads=[s], writes=[dst[1]])
        else:
            pg.op("dve", lambda e: e.tensor_copy(out=dst[0][:rows, c0:c0 + cw], in_=s.ap[:rows, :cw]), reads=[s], writes=[dst[1]])
        i += 1


def phase_P(pg, cx, es, L, x_ap, l):
    nc = cx.nc
    TT = 512
    sb = lambda name, shape, dt: pg.buf(es.enter_context(nc.sbuf_tensor(pg.uname(name), shape, dt)).ap(), name)
    wbf = sb("P_w", [128, 8, D_IN], BF16)
    w_src = cx.w["w_in"][l].rearrange("(k p) c -> p k c", p=128)
    WC = D_IN // 4
    st = [sb("P_wst%d" % i, [128, WC], F32) for i in range(2)]
    for k in range(8):
        for q in range(4):
            s = st[q % 2]
            pg.dma(s.ap, w_src[:, k, q * WC:(q + 1) * WC], writes=[s])
            if q % 2 == 0:
                pg.op("act", lambda e: e.copy(wbf.ap[:, k, q * WC:(q + 1) * WC], s.ap), reads=[s], writes=[wbf])
            else:
                pg.op("dve", lambda e: e.tensor_copy(out=wbf.ap[:, k, q * WC:(q + 1) * WC], in_=s.ap), reads=[s], writes=[wbf])
    gk = sb("P_g", [128, 8], F32)
    load_T(pg, cx, gk, gk.ap, cx.w["norm_g"][l].rearrange("(k p) -> k p", p=128), 8)
    xt = [sb("P_x%d" % i, [128, 4, D], F32) for i in range(2)]
    xs = sb("P_xs", [128, D], BF16)
    junk = sb("P_junk", [128, D], BF16)
    ss = sb("P_ss", [128, 4], F32)
    xnT = [sb("P_xnT%d" % i, [128, 8, TT], BF16) for i in range(2)]
    stf = [sb("P_stf%d" % i, [128, 4, TT], F32) for i in range(2)]
    stt = [sb("P_stt%d" % i, [128, PT_COLS], F32) for i in range(2)]
    XNTv = cx.XNT.rearrange("(k p) t -> p k t", p=128)
    xv = x_ap.rearrange("(n j p) d -> n p j d", p=128, j=4)
    gb = gk.ap.unsqueeze(2).to_broadcast([128, 8, 128])
    nt = L // TT
    evac_i = 0
    for it in range(nt):
        x_b = xt[it % 2]
        pg.dma(x_b.ap, xv[it], writes=[x_b])
        xn = xnT[it % 2]
        for j in range(4):
            pg.op("act", lambda e: e.activation(out=junk.ap, in_=x_b.ap[:, j, :], func=AF.Square, accum_out=ss.ap[:, j:j + 1]),
                  reads=[x_b], writes=[junk, ss])
            pg.op("act", lambda e: e.activation(out=ss.ap[:, j:j + 1], in_=ss.ap[:, j:j + 1], func=AF.Sqrt, scale=1.0 / D, bias=cx.eps.ap[:, 0:1]),
                  reads=[ss, cx.eps], writes=[ss])
            pg.op("dve", lambda e: e.reciprocal(out=ss.ap[:, j:j + 1], in_=ss.ap[:, j:j + 1]), reads=[ss], writes=[ss])
            pg.op("dve", lambda e: e.tensor_scalar(out=xs.ap, in0=x_b.ap[:, j, :], scalar1=ss.ap[:, j:j + 1], scalar2=None, op0=ALU.mult),
                  reads=[x_b, ss], writes=[xs])
            pb = cx.psb.get()
            for k in range(8):
                pg.op("pe", lambda e: e.transpose(out=pb.ap[:, k * 128:(k + 1) * 128], in_=xs.ap[:, k * 128:(k + 1) * 128], identity=cx.identb.ap),
                      reads=[xs, cx.identb], writes=[pb])
            pg.op("dve", lambda e: e.tensor_tensor(out=xn.ap[:, :, j * 128:(j + 1) * 128], in0=pb.ap.rearrange("p (k t) -> p k t", k=8), in1=gb, op=ALU.mult),
                  reads=[pb, gk], writes=[xn])
        pg.dma(XNTv[:, :, it * TT:(it + 1) * TT], xn.ap, reads=[xn])
        for ci, (c0, cw) in enumerate(PF_CHUNKS):
            ps = cx.psf.get()
            for k in range(8):
                pg.op("pe", lambda e: e.matmul(ps.ap[:cw, :], lhsT=wbf.ap[:, k, c0:c0 + cw], rhs=xn.ap[:, k, :], start=(k == 0), stop=(k == 7)),
                      reads=[wbf, xn], writes=[ps])
            sbuf = stf[(ci // 4) % 2]
            evac_i += 1
            if evac_i % 2 == 0:
                pg.op("act", lambda e: e.copy(sbuf.ap[:cw, ci % 4, :], ps.ap[:cw, :]), reads=[ps], writes=[sbuf])
            else:
                pg.op("dve", lambda e: e.tensor_copy(out=sbuf.ap[:cw, ci % 4, :], in_=ps.ap[:cw, :]), reads=[ps], writes=[sbuf])
            if ci % 4 == 3:
                cb = ci // 4
                pg.dma(PFv_slice(cx, cb * 4, 4, it * TT, TT), sbuf.ap, reads=[sbuf])
            elif ci == len(PF_CHUNKS) - 1:
                pg.dma(cx.PF[4096:4128, it * TT:(it + 1) * TT], sbuf.ap[:32, 0, :], reads=[sbuf])
        for j in range(4):
            sbuf = stt[j % 2]
            for (c0, cw, o0) in PT_GROUPS:
                ps = cx.psf.get()
                for k in range(8):
                    pg.op("pe", lambda e: e.matmul(ps.ap[:, :cw], lhsT=xn.ap[:, k, j * 128:(j + 1) * 128], rhs=wbf.ap[:, k, c0:c0 + cw], start=(k == 0), stop=(k == 7)),
                          reads=[wbf, xn], writes=[ps])
                evac_i += 1
                if evac_i % 2 == 0:
                    pg.op("act", lambda e: e.copy(sbuf.ap[:, o0:o0 + cw], ps.ap[:, :cw]), reads=[ps], writes=[sbuf])
                else:
                    pg.op("dve", lambda e: e.tensor_copy(out=sbuf.ap[:, o0:o0 + cw], in_=ps.ap[:, :cw]), reads=[ps], writes=[sbuf])
            t0 = it * TT + j * 128
            pg.dma(cx.PT[t0:t0 + 128, :], sbuf.ap, reads=[sbuf])


def PFv_slice(cx, c0, nch, t0, tw):
    return cx.PF[c0 * 128:(c0 + nch) * 128, t0:t0 + tw].rearrange("(c p) t -> p c t", p=128)


def load_T(pg, cx, dst, dst_ap, src_ap, n, st_view=None, wd=128, **kw):
    st = cx.ldst
    pg.dma(st.ap[:n, :wd] if st_view is None else st_view(st.ap[:n, :wd]), src_ap, writes=[st], **kw)
    ps = cx.psf.get()
    pg.op("pe", lambda e: e.transpose(out=ps.ap[:wd, :n], in_=st.ap[:n, :wd], identity=cx.identf.ap[:n, :n]), reads=[st, cx.identf], writes=[ps])
    pg.op("dve", lambda e: e.tensor_copy(out=dst_ap, in_=ps.ap[:wd, :n]), reads=[ps], writes=[dst])


def phase_LRU(pg, cx, es, L, l):
    nc = cx.nc
    sb = lambda name, shape, dt: pg.buf(es.enter_context(nc.sbuf_tensor(pg.uname(name), shape, dt)).ap(), name)
    w = cx.w
    TL = min(2048, L)
    ntile = L // TL
    cw = sb("L_cw", [128, 4, 4], F32)
    load_T(pg, cx, cw, cw.ap.rearrange("p j c -> p (j c)"), w["lru_conv_w"][l].rearrange("j (c p) -> (j c) p", p=128), 16)
    cb = sb("L_cb", [128, 4], F32)
    load_T(pg, cx, cb, cb.ap, w["lru_conv_b"][l].rearrange("(c p) -> c p", p=128), 4)
    bias = sb("L_bias", [128, 2, 2, 4], F32)
    load_T(pg, cx, bias, bias.ap[:, 0].rearrange("p d c -> p (d c)"), w["lru_b_a"][l].rearrange("d (c p) -> (d c) p", p=128), 8)
    load_T(pg, cx, bias, bias.ap[:, 1].rearrange("p d c -> p (d c)"), w["lru_b_x"][l].rearrange("d (c p) -> (d c) p", p=128), 8)
    lam = sb("L_lam", [128, 2, 4], F32)
    load_T(pg, cx, lam, lam.ap.rearrange("p d c -> p (d c)"), w["lru_lambda"][l].rearrange("d (c p) -> (d c) p", p=128), 8)
    coef = sb("L_coef", [128, 2, 4], F32)
    coef2 = sb("L_coef2", [128, 2, 4], F32)
    pg.op("act", lambda e: e.activation(out=coef.ap, in_=lam.ap, func=AF.Exp, scale=-1.0), reads=[lam], writes=[coef])
    pg.op("act", lambda e: e.activation(out=coef.ap, in_=coef.ap, func=AF.Ln, bias=cx.one.ap[:, 0:1]), reads=[coef, cx.one], writes=[coef])
    pg.op("dve", lambda e: e.tensor_scalar(out=coef2.ap, in0=coef.ap, scalar1=-16.0, scalar2=None, op0=ALU.mult), reads=[coef], writes=[coef2])
    pg.op("dve", lambda e: e.tensor_scalar(out=coef.ap, in0=coef.ap, scalar1=-8.0, scalar2=None, op0=ALU.mult), reads=[coef], writes=[coef])
    wg = sb("L_wg", [128, 2, 2, 4, 128], BF16)
    wst = sb("L_wst", [128, 2, 4, 128], F32)
    for ai, nm in enumerate(("lru_w_a", "lru_w_x")):
        pg.dma(wst.ap, w[nm][l].rearrange("d h i j -> i d h j"), writes=[wst])
        pg.op("dve", lambda e: e.tensor_copy(out=wg.ap[:, ai], in_=wst.ap), reads=[wst], writes=[wg])
    XC = sb("L_XC", [128, L], F32)
    XCB = sb("L_XCB", [128, L], BF16)
    HF = sb("L_HF", [128, L], F32)
    xin = sb("L_xin", [128, TL + 3], F32)
    rt = sb("L_r", [128, TL], F32)
    itl = sb("L_i", [128, TL], F32)
    at = sb("L_a", [128, TL], F32)
    t2 = sb("L_t2", [128, TL], F32)
    gt = sb("L_g", [128, TL], F32)
    yb = sb("L_y", [128, TL], BF16)
    carry = sb("L_carry", [128, 1], F32)
    for c in range(4):
        prow = PF_LRU_X + c * 128
        for it in range(ntile):
            t0 = it * TL
            lo = max(t0 - 2, 0)
            hi = min(t0 + TL + 1, L)
            if it == 0 or it == ntile - 1:
                pg.op("pool", lambda e: e.memset(xin.ap, 0.0), writes=[xin])
            pg.dma(xin.ap[:, lo - (t0 - 2):hi - (t0 - 2)], cx.PF[prow:prow + 128, lo:hi], writes=[xin])
            xo = XC.ap[:, t0:t0 + TL]
            pg.op("dve", lambda e: e.tensor_scalar(out=xo, in0=xin.ap[:, 0:TL], scalar1=cw.ap[:, 0, c:c + 1], scalar2=cb.ap[:, c:c + 1], op0=ALU.mult, op1=ALU.add),
                  reads=[xin, cw, cb], writes=[XC])
            for j in range(1, 4):
                pg.op("dve", lambda e: e.scalar_tensor_tensor(out=xo, in0=xin.ap[:, j:j + TL], scalar=cw.ap[:, j, c:c + 1], in1=xo, op0=ALU.mult, op1=ALU.add),
                      reads=[xin, cw, XC], writes=[XC])
            pg.op("act", lambda e: e.copy(XCB.ap[:, t0:t0 + TL], xo), reads=[XC], writes=[XCB])
        for d in range(2):
            order = range(ntile) if d == 0 else range(ntile - 1, -1, -1)
            for n_i, it in enumerate(order):
                t0 = it * TL
                for s0 in range(0, TL, 512):
                    for ai, dst in ((0, rt), (1, itl)):
                        ps = cx.psf.get()
                        pg.op("pe", lambda e: e.matmul(ps.ap, lhsT=wg.ap[:, ai, d, c, :], rhs=XCB.ap[:, t0 + s0:t0 + s0 + 512], start=True, stop=True),
                              reads=[wg, XCB], writes=[ps])
                        pg.op("act", lambda e: e.activation(out=dst.ap[:, s0:s0 + 512], in_=ps.ap, func=AF.Sigmoid, bias=bias.ap[:, ai, d, c:c + 1]),
                              reads=[ps, bias], writes=[dst])
                pg.op("act", lambda e: e.activation(out=at.ap, in_=rt.ap, func=AF.Exp, scale=coef.ap[:, d, c:c + 1]), reads=[rt, coef], writes=[at])
                pg.op("act", lambda e: e.activation(out=t2.ap, in_=rt.ap, func=AF.Exp, scale=coef2.ap[:, d, c:c + 1]), reads=[rt, coef2], writes=[t2])
                pg.op("act", lambda e: e.activation(out=t2.ap, in_=t2.ap, func=AF.Sqrt, scale=-1.0, bias=cx.one.ap[:, 0:1]), reads=[t2, cx.one], writes=[t2])
                pg.op("pool", lambda e: e.tensor_tensor(out=itl.ap, in0=itl.ap, in1=XC.ap[:, t0:t0 + TL], op=ALU.mult), reads=[itl, XC], writes=[itl])
                pg.op("dve", lambda e: e.tensor_tensor(out=t2.ap, in0=t2.ap, in1=itl.ap, op=ALU.mult), reads=[t2, itl], writes=[t2])
                init = 0.0 if n_i == 0 else carry.ap[:, 0:1]
                rds = [at, t2] + ([] if n_i == 0 else [carry])
                if d == 0:
                    ho = HF.ap[:, t0:t0 + TL]
                    pg.op("dve", lambda e: e.tensor_tensor_scan(out=ho, data0=at.ap, data1=t2.ap, initial=init, op0=ALU.mult, op1=ALU.add),
                          reads=rds, writes=[HF])
                    pg.op("dve", lambda e: e.tensor_copy(out=carry.ap, in_=HF.ap[:, t0 + TL - 1:t0 + TL]), reads=[HF], writes=[carry])
                else:
                    rv = lambda ap: bass.AP(ap.tensor, ap.offset + TL - 1, [list(ap.ap[0]), [-1, TL]])
                    pg.op("dve", lambda e: e.tensor_tensor_scan(out=rv(rt.ap), data0=rv(at.ap), data1=rv(t2.ap), initial=init, op0=ALU.mult, op1=ALU.add),
                          reads=rds, writes=[rt])
                    pg.op("dve", lambda e: e.tensor_copy(out=carry.ap, in_=rt.ap[:, 0:1]), reads=[rt], writes=[carry])
                    grow = PF_LRU_G + c * 128
                    pg.dma(gt.ap, cx.PF[grow:grow + 128, t0:t0 + TL], writes=[gt])
                    pg.op("act", lambda e: e.activation(out=gt.ap, in_=gt.ap, func=AF.Silu), reads=[gt], writes=[gt])
                    pg.op("pool", lambda e: e.tensor_tensor(out=rt.ap, in0=rt.ap, in1=HF.ap[:, t0:t0 + TL], op=ALU.add), reads=[rt, HF], writes=[rt])
                    pg.op("dve", lambda e: e.tensor_tensor(out=yb.ap, in0=rt.ap, in1=gt.ap, op=ALU.mult), reads=[rt, gt], writes=[yb])
                    pg.dma(cx.BT[c * 128:(c + 1) * 128, t0:t0 + TL], yb.ap, reads=[yb])


def phase_M(pg, cx, es, L, x_ap, xout_ap, l, last):
    nc = cx.nc
    TT = 512
    sb = lambda name, shape, dt: pg.buf(es.enter_context(nc.sbuf_tensor(pg.uname(name), shape, dt)).ap(), name)
    w = cx.w
    wmg = sb("M_wmg", [128, 4, 8, D], BF16)
    wbr = sb("M_wbr", [128, 4, 4, D], BF16)
    wout = sb("M_wout", [128, 8, D], BF16)
    st = [sb("M_st%d" % i, [128, D], F32) for i in range(2)]
    jobs = []
    for n in range(4):
        for k in range(8):
            jobs.append((w["w_merge_gate"][l, n, k * 128:(k + 1) * 128, :], wmg, wmg.ap[:, n, k, :]))
        for k in range(4):
            jobs.append((w["w_branch"][l, n, k * 128:(k + 1) * 128, :], wbr, wbr.ap[:, n, k, :]))
    for k in range(8):
        jobs.append((w["w_out"][l, k * 128:(k + 1) * 128, :], wout, wout.ap[:, k, :]))
    for i, (src, dbuf, dap) in enumerate(jobs):
        s = st[i % 2]
        pg.dma(s.ap, src, writes=[s])
        if i % 2 == 0:
            pg.op("act", lambda e: e.copy(dap, s.ap), reads=[s], writes=[dbuf])
        else:
            pg.op("dve", lambda e: e.tensor_copy(out=dap, in_=s.ap), reads=[s], writes=[dbuf])
    bmg = sb("M_bmg", [128, 4, 8], F32)
    load_T(pg, cx, bmg, bmg.ap.rearrange("p n c -> p (n c)"), w["b_merge_gate"][l].rearrange("n (c p) -> (n c) p", p=128), 32)
    if last:
        fg = sb("M_fg", [128, D], F32)
        fsrc = w["final_norm_g"]
        pg.dma(fg.ap, bass.AP(fsrc.tensor, fsrc.offset, [[0, 128], [1, D]]), writes=[fg])
        ss = sb("M_ss", [128, 4], F32)
        junk = sb("M_junk", [128, D], BF16)
    xn_ = [sb("M_xn%d" % i, [128, 8, TT], BF16) for i in range(2)]
    bt = sb("M_bt", [128, 16, TT], BF16)
    xt = sb("M_x", [128, 4, D], F32)
    mg = sb("M_mg", [128, 8, TT], BF16)
    gsb_ = [sb("M_g%d" % i, [128, TT], F32) for i in range(2)]
    tmp_ = [sb("M_tmp%d" % i, [128, TT], F32) for i in range(2)]
    acc = sb("M_acc", [128, TT], F32)
    XNTv = cx.XNT.rearrange("(k p) t -> p k t", p=128)
    BTv = cx.BT.rearrange("(k p) t -> p k t", p=128)
    xv = x_ap.rearrange("(n j p) d -> n p j d", p=128, j=4)
    ov = xout_ap.rearrange("(n j p) d -> n p j d", p=128, j=4)
    for it in range(L // TT):
        ts = slice(it * TT, (it + 1) * TT)
        xn = xn_[it % 2]
        pg.dma(xn.ap, XNTv[:, :, ts], writes=[xn])
        pg.dma(bt.ap, BTv[:, :, ts], writes=[bt])
        pg.dma(xt.ap, xv[it], writes=[xt])
        for oc in range(8):
            ocs = slice(oc * 128, (oc + 1) * 128)
            for n in range(4):
                gsb = gsb_[n % 2]; tmp = tmp_[n % 2]
                pg_ = cx.psf.get()
                for k in range(8):
                    pg.op("pe", lambda e: e.matmul(pg_.ap, lhsT=wmg.ap[:, n, k, ocs], rhs=xn.ap[:, k, :], start=(k == 0), stop=(k == 7)),
                          reads=[wmg, xn], writes=[pg_])
                pg.op("act", lambda e: e.activation(out=gsb.ap, in_=pg_.ap, func=AF.Sigmoid, bias=bmg.ap[:, n, oc:oc + 1]), reads=[pg_, bmg], writes=[gsb])
                pb = cx.psf.get()
                for k in range(4):
                    pg.op("pe", lambda e: e.matmul(pb.ap, lhsT=wbr.ap[:, n, k, ocs], rhs=bt.ap[:, n * 4 + k, :], start=(k == 0), stop=(k == 3)),
                          reads=[wbr, bt], writes=[pb])
                if n == 0:
                    pg.op("dve", lambda e: e.tensor_tensor(out=acc.ap, in0=pb.ap, in1=gsb.ap, op=ALU.mult), reads=[pb, gsb], writes=[acc])
                else:
                    pg.op("dve", lambda e: e.tensor_tensor(out=tmp.ap, in0=pb.ap, in1=gsb.ap, op=ALU.mult), reads=[pb, gsb], writes=[tmp])
                    if n < 3:
                        pg.op("pool", lambda e: e.tensor_tensor(out=acc.ap, in0=acc.ap, in1=tmp.ap, op=ALU.add), reads=[acc, tmp], writes=[acc])
                    else:
                        pg.op("pool", lambda e: e.tensor_tensor(out=mg.ap[:, oc, :], in0=acc.ap, in1=tmp.ap, op=ALU.add), reads=[acc, tmp], writes=[mg])
        for j in range(4):
            for hf in range(2):
                hs = slice(hf * 512, (hf + 1) * 512)
                ps = cx.psf.get()
                for k in range(8):
                    pg.op("pe", lambda e: e.matmul(ps.ap, lhsT=mg.ap[:, k, j * 128:(j + 1) * 128], rhs=wout.ap[:, k, hs], start=(k == 0), stop=(k == 7)),
                          reads=[mg, wout], writes=[ps])
                pg.op("dve", lambda e: e.tensor_tensor(out=xt.ap[:, j, hs], in0=ps.ap, in1=xt.ap[:, j, hs], op=ALU.add), reads=[ps, xt], writes=[xt])
            if last:
                pg.op("act", lambda e: e.activation(out=junk.ap, in_=xt.ap[:, j, :], func=AF.Square, accum_out=ss.ap[:, j:j + 1]), reads=[xt], writes=[junk, ss])
                pg.op("act", lambda e: e.activation(out=ss.ap[:, j:j + 1], in_=ss.ap[:, j:j + 1], func=AF.Sqrt, scale=1.0 / D, bias=cx.eps.ap[:, 0:1]),
                      reads=[ss, cx.eps], writes=[ss])
                pg.op("dve", lambda e: e.reciprocal(out=ss.ap[:, j:j + 1], in_=ss.ap[:, j:j + 1]), reads=[ss], writes=[ss])
                pg.op("dve", lambda e: e.scalar_tensor_tensor(out=xt.ap[:, j, :], in0=xt.ap[:, j, :], scalar=ss.ap[:, j:j + 1], in1=fg.ap, op0=ALU.mult, op1=ALU.mult),
                      reads=[xt, ss, fg], writes=[xt])
        pg.dma(ov[it], xt.ap, reads=[xt])


def phase_GLA(pg, cx, es, L, l):
    nc = cx.nc
    sb = lambda name, shape, dt: pg.buf(es.enter_context(nc.sbuf_tensor(pg.uname(name), shape, dt)).ap(), name)
    w = cx.w
    NB = L // 128
    wup = sb("G_wup", [32, 2, 256], F32)
    for d in range(2):
        pg.dma(wup.ap[0:16, d, :], w["gla_w_up"][l, d], writes=[wup])
        pg.dma(wup.ap[16:17, d, :], w["gla_b_up"][l, d:d + 1, :], writes=[wup])
    gn = sb("G_gn", [128, 128], F32)
    gsrc = w["gla_norm_g"][l]
    pg.dma(gn.ap, bass.AP(gsrc.tensor, gsrc.offset, [[0, 128], [1, 128]]), writes=[gn])
    lrT = [sb("G_lrT%d" % i, [32, 128], F32) for i in range(2)]
    for b in lrT:
        pg.op("dve", lambda e: e.memset(b.ap, 1.0), writes=[b])
    qk = [sb("G_qk%d" % i, [128, 4, 128], F32) for i in range(2)]
    tk = [sb("G_tk%d" % i, [128, 1280], F32) for i in range(3)]
    obt = [sb("G_ob%d" % i, [128, 512], F32) for i in range(3)]
    la_ = [sb("G_la%d" % i, [128, 256], F32) for i in range(2)]
    e1_ = [sb("G_e1%d" % i, [128, 256], F32) for i in range(2)]
    eb_ = [sb("G_eb%d" % i, [128, 2, 128], F32) for i in range(2)]
    enb_ = [sb("G_enb%d" % i, [128, 2, 128], F32) for i in range(2)]
    qd_ = [sb("G_qd%d" % i, [128, 2, 128], BF16) for i in range(2)]
    ki_ = [sb("G_ki%d" % i, [128, 2, 128], BF16) for i in range(2)]
    ed_ = [sb("G_ed%d" % i, [128, 256], F32) for i in range(2)]
    kend_ = [sb("G_kend%d" % i, [128, 256], BF16) for i in range(2)]
    vb_ = [sb("G_vb%d" % i, [128, 512], BF16) for i in range(2)]
    sm = [sb("G_sm%d" % i, [128, 128], BF16) for i in range(4)]
    pre_ps = Rot([cx.psf.items[4], cx.pso])
    S32 = [sb("G_S32_%d" % h, [128, 128], F32) for h in range(4)]
    Sb = [sb("G_Sb_%d" % h, [128, 128], BF16) for h in range(4)]
    osb_ = [sb("G_osb%d" % i, [128, 512], F32) for i in range(2)]
    pending = [None]
    ssq = sb("G_ssq", [128, 4], F32)
    junk = sb("G_junk", [128, 128], BF16)
    ysb = sb("G_ysb", [128, 512], BF16)
    yT = sb("G_yT", [128, 4, 128], BF16)
    PFq = cx.PF[PF_GLA_Q:PF_GLA_Q + 512, :].rearrange("(c p) t -> p c t", p=128)
    for d in (1, 0):
        pg.barrier()
        for h in range(4):
            pg.op("dve", lambda e: e.memset(S32[h].ap, 0.0), writes=[S32[h]])
            pg.op("pool", lambda e: e.memset(Sb[h].ap, 0.0), writes=[Sb[h]])
        order = range(NB) if d == 0 else range(NB - 1, -1, -1)
        def pre_gen(bi, blk, d=d):
            t0 = blk * 128
            ts = slice(t0, t0 + 128)
            qkb = qk[bi % 2]; tkb = tk[bi % 3]; lrb = lrT[bi % 2]; ob = obt[bi % 3]
            la, e1, eb, enb, qd, ki, ed, kend, vb = [x_[bi % 2] for x_ in (la_, e1_, eb_, enb_, qd_, ki_, ed_, kend_, vb_)]
            pg.dma(qkb.ap, PFq[:, :, ts], writes=[qkb])
            pg.dma(tkb.ap, cx.PT[ts, 0:1280], writes=[tkb])
            pg.dma(lrb.ap[0:16, :], cx.PF[PF_GLA_LR + 16 * d:PF_GLA_LR + 16 * d + 16, ts], writes=[lrb])
            if d == 0:
                pg.dma(ob.ap, cx.OB[ts, 0:512], writes=[ob])
            zp = pre_ps.get()
            pg.op("pe", lambda e: e.matmul(zp.ap[:, :256], lhsT=lrb.ap[0:17, :], rhs=wup.ap[0:17, d, :], start=True, stop=True), reads=[lrb, wup], writes=[zp])
            pg.op("act", lambda e: e.activation(out=e1.ap, in_=zp.ap[:, :256], func=AF.Exp, scale=-1.0), reads=[zp], writes=[e1])
            yield
            pg.op("act", lambda e: e.activation(out=e1.ap, in_=e1.ap, func=AF.Ln, bias=cx.one.ap[:, 0:1]), reads=[e1, cx.one], writes=[e1])
            yield
            pg.op("dve", lambda e: e.tensor_scalar(out=la.ap, in0=e1.ap, scalar1=-1.0 / 16.0, scalar2=None, op0=ALU.mult), reads=[e1], writes=[la])
            yield
            bp = pre_ps.get()
            for h2 in range(2):
                pg.op("pe", lambda e: e.matmul(bp.ap[:, h2 * 128:(h2 + 1) * 128], lhsT=la.ap[:, h2 * 128:(h2 + 1) * 128], rhs=cx.m_incl.ap[:, d, :], start=True, stop=True),
                      reads=[la, cx.m_incl], writes=[bp])
            bp3 = bp.ap[:, 0:256].rearrange("p (c t) -> p c t", c=2)
            pg.op("act", lambda e: e.activation(out=eb.ap, in_=bp3, func=AF.Exp), reads=[bp], writes=[eb])
            yield
            pg.op("act", lambda e: e.activation(out=enb.ap, in_=bp3, func=AF.Exp, scale=-1.0), reads=[bp], writes=[enb])
            yield
            pg.op("dve", lambda e: e.scalar_tensor_tensor(out=qd.ap, in0=qkb.ap[:, 0:2, :], scalar=0.125, in1=eb.ap, op0=ALU.mult, op1=ALU.mult), reads=[qkb, eb], writes=[qd])
            yield
            pg.op("pool", lambda e: e.tensor_tensor(out=ki.ap, in0=qkb.ap[:, 2:4, :], in1=enb.ap, op=ALU.mult), reads=[qkb, enb], writes=[ki])
            yield
            dp = pre_ps.get()
            pg.op("pe", lambda e: e.matmul(dp.ap[:, :256], lhsT=cx.m_sa.ap[:, d, :], rhs=la.ap, start=True, stop=True), reads=[la, cx.m_sa], writes=[dp])
            pg.op("act", lambda e: e.activation(out=ed.ap, in_=dp.ap[:, :256], func=AF.Exp), reads=[dp], writes=[ed])
            yield
            pg.op("dve", lambda e: e.tensor_tensor(out=kend.ap, in0=tkb.ap[:, 0:256], in1=ed.ap, op=ALU.mult), reads=[tkb, ed], writes=[kend])
            yield
            pg.op("pool", lambda e: e.tensor_copy(out=vb.ap, in_=tkb.ap[:, 256:768]), reads=[tkb], writes=[vb])
            yield

        order_l = list(order)
        for _ in pre_gen(0, order_l[0]):
            pass
        for bi, blk in enumerate(order_l):
            t0 = blk * 128
            ts = slice(t0, t0 + 128)
            osb = osb_[bi % 2]
            qkb = qk[bi % 2]; tkb = tk[bi % 3]; lrb = lrT[bi % 2]; ob = obt[bi % 3]
            la, e1, eb, enb, qd, ki, ed, kend, vb = [x_[bi % 2] for x_ in (la_, e1_, eb_, enb_, qd_, ki_, ed_, kend_, vb_)]
            chunks = (0, 1) if d == 0 else (1, 0)

            def head_gen(h, d=d, chunks=chunks):
                h2, hp = h // 2, (h % 2) * 64
                hc = slice(h * 128, (h + 1) * 128)
                bank = cx.psf.items[h]
                o_ps = bank.ap[:, 384:512]
                pg.op("pe", lambda e: e.matmul(bank.ap[:, 0:128], lhsT=ki.ap[hp:hp + 64, h2, :], rhs=qd.ap[hp:hp + 64, h2, :], start=True, stop=True), reads=[ki, qd], writes=[bank])
                yield
                smb = sm[h]
                pg.op("dve", lambda e: e.tensor_tensor(out=smb.ap, in0=bank.ap[:, 0:128], in1=cx.m_incl.ap[:, d, :], op=ALU.mult), reads=[bank, cx.m_incl], writes=[smb])
                yield
                r0 = chunks[0] * 64
                pg.op("pe", lambda e: e.matmul(o_ps, lhsT=smb.ap, rhs=vb.ap[:, hc], start=True, stop=False), reads=[smb, vb], writes=[bank])
                pg.op("pe", lambda e: e.matmul(bank.ap[r0:r0 + 64, 384:512], lhsT=qd.ap[hp:hp + 64, h2, r0:r0 + 64], rhs=Sb[h].ap[hp:hp + 64, :], start=False, stop=True),
                      reads=[qd, Sb[h]], writes=[bank])
                for ci, c in enumerate(chunks):
                    r0 = c * 64
                    if ci == 1:
                        pg.op("pe", lambda e: e.matmul(bank.ap[r0:r0 + 64, 256:384], lhsT=qd.ap[hp:hp + 64, h2, r0:r0 + 64], rhs=Sb[h].ap[hp:hp + 64, :], start=True, stop=True),
                              reads=[qd, Sb[h]], writes=[bank])
                    pg.op("pe", lambda e: e.matmul(bank.ap[hp:hp + 64, 128:256], lhsT=kend.ap[r0:r0 + 64, h * 64:(h + 1) * 64], rhs=vb.ap[r0:r0 + 64, hc], start=True, stop=True),
                          reads=[kend, vb], writes=[bank])
                    yield
                    col = r0 + 63 if d == 0 else r0
                    pg.op("dve", lambda e: e.scalar_tensor_tensor(out=S32[h].ap[hp:hp + 64, :], in0=S32[h].ap[hp:hp + 64, :], scalar=eb.ap[hp:hp + 64, h2, col:col + 1],
                                                                  in1=bank.ap[hp:hp + 64, 128:256], op0=ALU.mult, op1=ALU.add), reads=[S32[h], eb, bank], writes=[S32[h]])
                    yield
                    pg.op("act", lambda e: e.copy(Sb[h].ap[hp:hp + 64, :], S32[h].ap[hp:hp + 64, :]), reads=[S32[h]], writes=[Sb[h]])
                    yield
                r1 = chunks[1] * 64
                pg.op("dve", lambda e: e.tensor_copy(out=osb.ap[:, hc], in_=o_ps), reads=[bank], writes=[osb])
                pg.op("dve", lambda e: e.tensor_tensor(out=osb.ap[r1:r1 + 64, hc], in0=bank.ap[r1:r1 + 64, 256:384], in1=osb.ap[r1:r1 + 64, hc], op=ALU.add), reads=[bank, osb], writes=[osb])

            gens = [head_gen(h) for h in range(4)] + ([pre_gen(bi + 1, order_l[bi + 1])] if bi + 1 < NB else [])
            if pending[0] is not None:
                gens.append(pending[0])
                pending[0] = None
            while gens:
                for gnr in list(gens):
                    try:
                        next(gnr)
                    except StopIteration:
                        gens.remove(gnr)
            if d == 1:
                pg.dma(cx.OB[ts, 0:512], osb.ap, reads=[osb])
            else:
                def tail_gen(osb=osb, ob=ob, tkb=tkb, ts=ts):
                    pg.op("pool", lambda e: e.tensor_tensor(out=osb.ap, in0=osb.ap, in1=ob.ap, op=ALU.add), reads=[osb, ob], writes=[osb])
                    yield
                    yield from hngs_gen(pg, cx, osb, ssq, junk, gn, tkb, 768, ysb, yT, 512, ts)
                pending[0] = tail_gen()
        if pending[0] is not None:
            for _ in pending[0]:
                pass
            pending[0] = None


def hngs_gen(pg, cx, osb, ssq, junk, gn, tkb, gcol, ysb, yT, bt_row0, ts):
    for h in range(4):
        hc = slice(h * 128, (h + 1) * 128)
        pg.op("act", lambda e: e.activation(out=junk.ap, in_=osb.ap[:, hc], func=AF.Square, accum_out=ssq.ap[:, h:h + 1]), reads=[osb], writes=[junk, ssq])
        yield
    pg.op("act", lambda e: e.activation(out=ssq.ap, in_=ssq.ap, func=AF.Sqrt, scale=1.0 / 128.0, bias=cx.eps.ap[:, 0:1]), reads=[ssq, cx.eps], writes=[ssq])
    yield
    pg.op("dve", lambda e: e.reciprocal(out=ssq.ap, in_=ssq.ap), reads=[ssq], writes=[ssq])
    yield
    for h in range(4):
        hc = slice(h * 128, (h + 1) * 128)
        pg.op("dve", lambda e: e.scalar_tensor_tensor(out=osb.ap[:, hc], in0=osb.ap[:, hc], scalar=ssq.ap[:, h:h + 1], in1=gn.ap, op0=ALU.mult, op1=ALU.mult),
              reads=[osb, ssq, gn], writes=[osb])
        yield
    pg.op("act", lambda e: e.activation(out=tkb.ap[:, gcol:gcol + 512], in_=tkb.ap[:, gcol:gcol + 512], func=AF.Silu), reads=[tkb], writes=[tkb])
    yield
    pg.op("dve", lambda e: e.tensor_tensor(out=ysb.ap, in0=osb.ap, in1=tkb.ap[:, gcol:gcol + 512], op=ALU.mult), reads=[osb, tkb], writes=[ysb])
    yield
    pb = cx.psb.get()
    for h in range(4):
        pg.op("pe", lambda e: e.transpose(out=pb.ap[:, h * 128:(h + 1) * 128], in_=ysb.ap[:, h * 128:(h + 1) * 128], identity=cx.identb.ap), reads=[ysb, cx.identb], writes=[pb])
    pg.op("act", lambda e: e.copy(yT.ap, pb.ap[:, 0:512].rearrange("p (c t) -> p c t", c=4)), reads=[pb], writes=[yT])
    yield
    pg.dma(cx.BT[bt_row0:bt_row0 + 512, ts].rearrange("(c p) t -> p c t", p=128), yT.ap, reads=[yT])

def head_norm_gate_store(*args):
    for _ in hngs_gen(*args):
        pass


def phase_DN(pg, cx, es, L, l):
    nc = cx.nc
    w = cx.w
    NB = L // 128
    with ExitStack() as es0:
        sb = lambda name, shape, dt: pg.buf(es0.enter_context(nc.sbuf_tensor(pg.uname(name), shape, dt)).ap(), name)
        TL = 512
        cwD = sb("D0_cw", [128, 4, 12], F32)
        load_T(pg, cx, cwD, cwD.ap.rearrange("p j c -> p (j c)"), w["dn_conv_w"][l].rearrange("j (c p) -> (j c) p", p=128), 48)
        xin = [sb("D0_xin%d" % i, [128, TL + 3], F32) for i in range(2)]
        xc = sb("D0_xc", [128, TL], F32)
        sq = sb("D0_sq", [128, TL], F32)
        rs = sb("D0_rs", [128, TL], F32)
        fm = sb("D0_fm", [128, 12, TL], BF16)
        tm = sb("D0_tm", [128, 4, 1024], BF16)
        nt = L // TL
        for it in range(nt):
            t0 = it * TL
            lo, hi = max(t0 - 2, 0), min(t0 + TL + 1, L)
            for c in range(12):
                xb = xin[c % 2]
                if it == 0 or it == nt - 1:
                    pg.op("pool", lambda e: e.memset(xb.ap, 0.0), writes=[xb])
                prow = PF_DN_QKV + c * 128
                pg.dma(xb.ap[:, lo - (t0 - 2):hi - (t0 - 2)], cx.PF[prow:prow + 128, lo:hi], writes=[xb])
                pg.op("dve", lambda e: e.tensor_scalar(out=xc.ap, in0=xb.ap[:, 0:TL], scalar1=cwD.ap[:, 0, c:c + 1], scalar2=None, op0=ALU.mult), reads=[xb, cwD], writes=[xc])
                for j in range(1, 4):
                    pg.op("dve", lambda e: e.scalar_tensor_tensor(out=xc.ap, in0=xb.ap[:, j:j + TL], scalar=cwD.ap[:, j, c:c + 1], in1=xc.ap, op0=ALU.mult, op1=ALU.add),
                          reads=[xb, cwD, xc], writes=[xc])
                if c >= 8:
                    pg.op("act", lambda e: e.activation(out=fm.ap[:, c, :], in_=xc.ap, func=AF.Silu), reads=[xc], writes=[fm])
                else:
                    pg.op("act", lambda e: e.activation(out=xc.ap, in_=xc.ap, func=AF.Silu), reads=[xc], writes=[xc])
                    pg.op("pool", lambda e: e.tensor_tensor(out=sq.ap, in0=xc.ap, in1=xc.ap, op=ALU.mult), reads=[xc], writes=[sq])
                    ps = cx.psf.get()
                    pg.op("pe", lambda e: e.matmul(ps.ap, lhsT=cx.onesf.ap, rhs=sq.ap, start=True, stop=True), reads=[cx.onesf, sq], writes=[ps])
                    pg.op("act", lambda e: e.activation(out=rs.ap, in_=ps.ap, func=AF.Sqrt, bias=cx.eps.ap[:, 0:1]), reads=[ps, cx.eps], writes=[rs])
                    pg.op("dve", lambda e: e.reciprocal(out=rs.ap, in_=rs.ap), reads=[rs], writes=[rs])
                    sc = (128.0 ** -0.5) if c < 4 else 1.0
                    pg.op("dve", lambda e: e.scalar_tensor_tensor(out=fm.ap[:, c, :], in0=xc.ap, scalar=sc, in1=rs.ap, op0=ALU.mult, op1=ALU.mult), reads=[xc, rs], writes=[fm])
            pg.dma(cx.QKT[:, t0:t0 + TL].rearrange("(c p) t -> p c t", p=128), fm.ap[:, 0:8, :], reads=[fm])
            for j in range(4):
                pb = cx.psb.get()
                for c in range(8):
                    pg.op("pe", lambda e: e.transpose(out=pb.ap[:, c * 128:(c + 1) * 128], in_=fm.ap[:, 4 + c, j * 128:(j + 1) * 128], identity=cx.identb.ap),
                          reads=[fm, cx.identb], writes=[pb])
                pg.op("act", lambda e: e.copy(tm.ap[:, j, :], pb.ap), reads=[pb], writes=[tm])
            pg.dma(cx.KVT[t0:t0 + TL, :].rearrange("(j p) c -> p j c", p=128), tm.ap, reads=[tm])
        pg.barrier()
    sb = lambda name, shape, dt: pg.buf(es.enter_context(nc.sbuf_tensor(pg.uname(name), shape, dt)).ap(), name)
    ba = sb("D_ba", [128, NB, 16], F32)
    pg.dma(ba.ap, cx.PT[:, PT_DN_BA:PT_DN_BA + 16].rearrange("(n p) c -> p n c", p=128), writes=[ba])
    ba4 = ba.ap.rearrange("p n (d j h) -> p n d j h", d=2, j=2)
    dtb = sb("D_dtb", [128, 8], F32)
    nea = sb("D_nea", [128, 8], F32)
    s1 = w["dn_dt_bias"][l]
    pg.dma(dtb.ap, bass.AP(s1.tensor, s1.offset, [[0, 128], [1, 8]]), writes=[dtb])
    s2 = w["dn_a_log"][l]
    pg.dma(nea.ap, bass.AP(s2.tensor, s2.offset, [[0, 128], [1, 8]]), writes=[nea])
    pg.op("act", lambda e: e.activation(out=nea.ap, in_=nea.ap, func=AF.Exp), reads=[nea], writes=[nea])
    pg.op("dve", lambda e: e.tensor_scalar(out=nea.ap, in0=nea.ap, scalar1=-1.0, scalar2=None, op0=ALU.mult), reads=[nea], writes=[nea])
    beta = sb("D_beta", [128, NB, 2, 4], F32)
    nbeta = sb("D_nbeta", [128, NB, 2, 4], F32)
    g = sb("D_g", [128, NB, 2, 4], F32)
    pg.op("act", lambda e: e.activation(out=beta.ap, in_=ba4[:, :, :, 0, :], func=AF.Sigmoid), reads=[ba], writes=[beta])
    pg.op("dve", lambda e: e.tensor_scalar(out=nbeta.ap, in0=beta.ap, scalar1=-1.0, scalar2=None, op0=ALU.mult), reads=[beta], writes=[nbeta])
    dtb_b = dtb.ap.rearrange("p (d h) -> p d h", d=2).unsqueeze(1).to_broadcast([128, NB, 2, 4])
    nea_b = nea.ap.rearrange("p (d h) -> p d h", d=2).unsqueeze(1).to_broadcast([128, NB, 2, 4])
    pg.op("dve", lambda e: e.tensor_tensor(out=g.ap, in0=ba4[:, :, :, 1, :], in1=dtb_b, op=ALU.add), reads=[ba, dtb], writes=[g])
    pg.op("act", lambda e: e.activation(out=g.ap, in_=g.ap, func=AF.Exp), reads=[g], writes=[g])
    pg.op("act", lambda e: e.activation(out=g.ap, in_=g.ap, func=AF.Ln, bias=cx.one.ap[:, 0:1]), reads=[g, cx.one], writes=[g])
    pg.op("dve", lambda e: e.tensor_tensor(out=g.ap, in0=g.ap, in1=nea_b, op=ALU.mult), reads=[g, nea], writes=[g])
    eG = sb("D_eG", [128, NB, 2, 4], F32)
    eD = sb("D_eD", [128, NB, 2, 4], F32)
    bg = sb("D_bg", [128, NB, 2, 4], F32)
    deB = sb("D_deB", [128, 2, NB, 2, 4], F32)
    NQ = 32
    for d in range(2):
        for n0 in range(0, NB, NQ):
            nn = min(NQ, NB - n0)
            for (msk, dst, fn) in ((cx.m_incl.ap[:, d, :], eG, 0), (cx.m_sa.ap[:, d, :], eD, 0), (cx.chunkind.ap[:, 0, :], deB, 1), (cx.chunkind.ap[:, 1, :], deB, 2)):
                ps = cx.psf.get()
                pv = ps.ap[:, :nn * 4].rearrange("p (n h) -> p n h", h=4)
                pg.op("pe", lambda e: e.matmul(pv, lhsT=msk, rhs=g.ap[:, n0:n0 + nn, d, :], start=True, stop=True), reads=[g, cx.m_incl, cx.m_sa, cx.chunkind], writes=[ps])
                o_ap = dst.ap[:, n0:n0 + nn, d, :] if fn == 0 else dst.ap[:, fn - 1, n0:n0 + nn, d, :]
                pg.op("act", lambda e: e.activation(out=o_ap, in_=pv, func=AF.Exp), reads=[ps], writes=[dst])
    pg.op("dve", lambda e: e.tensor_tensor(out=bg.ap, in0=beta.ap, in1=eG.ap, op=ALU.mult), reads=[beta, eG], writes=[bg])
    gn = sb("D_gn", [128, 128], F32)
    gsrc = w["dn_norm_g"][l]
    pg.dma(gn.ap, bass.AP(gsrc.tensor, gsrc.offset, [[0, 128], [1, 128]]), writes=[gn])
    qk = [sb("D_qk%d" % i, [128, 8, 128], BF16) for i in range(2)]
    kv = [sb("D_kv%d" % i, [128, 2, 4, 128], BF16) for i in range(2)]
    gt = [sb("D_gt%d" % i, [128, 512], F32) for i in range(2)]
    obt = [sb("D_ob%d" % i, [128, 512], F32) for i in range(2)]
    vb4 = sb("D_vb4", [128, 4, 128], BF16)
    kbg4 = sb("D_kbg4", [128, 4, 128], BF16)
    kend4 = sb("D_kend4", [128, 4, 128], BF16)
    gtri = [sb("D_gtri%d" % i, [128, 128], F32) for i in range(4)]
    gam = [sb("D_gam%d" % i, [128, 3, 128], F32) for i in range(4)]
    gamm = [sb("D_gamm%d" % i, [128, 2, 128], F32) for i in range(4)]
    qd = [sb("D_qd%d" % i, [128, 128], BF16) for i in range(4)]
    Cm = [sb("D_C%d" % i, [128, 128], BF16) for i in range(4)]
    attnT = [sb("D_at%d" % i, [128, 128], BF16) for i in range(4)]
    BC = [[sb("D_BC%d_%d" % (h, i), [128, 2, 128], BF16) for i in range(2)] for h in range(4)]
    Pm = [[sb("D_P%d_%d" % (h, i), [128, 128], BF16) for i in range(2)] for h in range(4)]
    Pm32 = [[sb("D_P32_%d_%d" % (h, i), [128, 128], F32) for i in range(2)] for h in range(4)]
    usb = [sb("D_u%d" % i, [128, 128], F32) for i in range(4)]
    wT = [sb("D_wT%d" % i, [128, 128], BF16) for i in range(4)]
    vn = [sb("D_vn%d" % i, [128, 128], BF16) for i in range(4)]
    S32 = [sb("D_S32_%d" % h, [128, 128], F32) for h in range(4)]
    Sb = [sb("D_Sb_%d" % h, [128, 128], BF16) for h in range(4)]
    osb_ = [sb("D_osb%d" % i, [128, 512], F32) for i in range(2)]
    pending = [None]
    ssq = sb("D_ssq", [128, 4], F32)
    junk = sb("D_junk", [128, 128], BF16)
    ysb = sb("D_ysb", [128, 512], BF16)
    yT = sb("D_yT", [128, 4, 128], BF16)
    QKv = cx.QKT.rearrange("(c p) t -> p c t", p=128)
    rr = [0]
    for d in (1, 0):
        pg.barrier()
        for h in range(4):
            pg.op("dve", lambda e: e.memset(S32[h].ap, 0.0), writes=[S32[h]])
            pg.op("pool", lambda e: e.memset(Sb[h].ap, 0.0), writes=[Sb[h]])
        order = range(NB) if d == 0 else range(NB - 1, -1, -1)
        chunks = (0, 1) if d == 0 else (1, 0)
        for bi, blk in enumerate(order):
            t0 = blk * 128
            ts = slice(t0, t0 + 128)
            qkb = qk[bi % 2]; kvb = kv[bi % 2]; ob = obt[bi % 2]; gtb = gt[bi % 2]; osb = osb_[bi % 2]
            pg.dma(qkb.ap, QKv[:, :, ts], writes=[qkb])
            pg.dma(kvb.ap.rearrange("p a h c -> p (a h c)"), cx.KVT[ts, :], writes=[kvb])
            if d == 0:
                pg.dma(ob.ap, cx.OB[ts, 512:1024], writes=[ob])
                pg.dma(gtb.ap[:, 0:512], cx.PT[ts, PT_DN_G:PT_DN_G + 512], writes=[gtb])
            bcast = lambda t: t.ap[:, blk, d, :].unsqueeze(2).to_broadcast([128, 4, 128])
            pg.op("dve", lambda e: e.tensor_tensor(out=vb4.ap, in0=kvb.ap[:, 1], in1=bcast(beta), op=ALU.mult), reads=[kvb, beta], writes=[vb4])
            pg.op("pool", lambda e: e.tensor_tensor(out=kbg4.ap, in0=kvb.ap[:, 0], in1=bcast(bg), op=ALU.mult), reads=[kvb, bg], writes=[kbg4])
            pg.op("pool", lambda e: e.tensor_tensor(out=kend4.ap, in0=kvb.ap[:, 0], in1=bcast(eD), op=ALU.mult), reads=[kvb, eD], writes=[kend4])
            op_ = cx.pso
            def head_gen(h, blk=blk, d=d, qkb=qkb, chunks=chunks, op_=op_):
                i2 = h
                bank = cx.psf.items[h]
                hc = slice(h * 128, (h + 1) * 128)
                gsc = g.ap[:, blk, d, h:h + 1]
                pg.op("dve", lambda e: e.tensor_scalar(out=gtri[i2].ap, in0=cx.m_incl.ap[:, d, :], scalar1=gsc, scalar2=None, op0=ALU.mult), reads=[cx.m_incl, g], writes=[gtri[i2]])
                yield
                dps = bank
                pg.op("pe", lambda e: e.matmul(dps.ap[:, 0:128], lhsT=gtri[i2].ap, rhs=cx.m_sa.ap[:, d, :], start=True, stop=True), reads=[gtri[i2], cx.m_sa], writes=[dps])
                pg.op("pe", lambda e: e.matmul(dps.ap[:, 128:256], lhsT=cx.m_sa.ap[:, d, :], rhs=gtri[i2].ap, start=True, stop=True), reads=[gtri[i2], cx.m_sa], writes=[dps])
                pg.op("pe", lambda e: e.matmul(dps.ap[:, 256:384], lhsT=cx.onesf.ap, rhs=gtri[i2].ap, start=True, stop=True), reads=[gtri[i2], cx.onesf], writes=[dps])
                yield
                pg.op("act", lambda e: e.activation(out=gam[i2].ap.rearrange("p a t -> p (a t)"), in_=dps.ap[:, 0:384], func=AF.Exp), reads=[dps], writes=[gam[i2]])
                yield
                pg.op("pool", lambda e: e.tensor_tensor(out=gamm[i2].ap, in0=gam[i2].ap[:, 0:2, :], in1=cx.m_dn.ap[:, d], op=ALU.mult), reads=[gam[i2], cx.m_dn], writes=[gamm[i2]])
                pg.op("pool", lambda e: e.tensor_tensor(out=qd[i2].ap, in0=qkb.ap[:, h, :], in1=gam[i2].ap[:, 2, :], op=ALU.mult), reads=[qkb, gam[i2]], writes=[qd[i2]])
                kps = bank
                pg.op("pe", lambda e: e.matmul(kps.ap[:, 0:128], lhsT=qkb.ap[:, 4 + h, :], rhs=qkb.ap[:, 4 + h, :], start=True, stop=True), reads=[qkb], writes=[kps])
                pg.op("pe", lambda e: e.matmul(kps.ap[:, 128:256], lhsT=qkb.ap[:, 4 + h, :], rhs=qkb.ap[:, h, :], start=True, stop=True), reads=[qkb], writes=[kps])
                yield
                pg.op("dve", lambda e: e.scalar_tensor_tensor(out=Cm[i2].ap, in0=kps.ap[:, 0:128], scalar=nbeta.ap[:, blk, d, h:h + 1], in1=gamm[i2].ap[:, 0, :], op0=ALU.mult, op1=ALU.mult),
                      reads=[kps, nbeta, gamm[i2]], writes=[Cm[i2]])
                pg.op("dve", lambda e: e.tensor_tensor(out=attnT[i2].ap, in0=kps.ap[:, 128:256], in1=gamm[i2].ap[:, 1, :], op=ALU.mult), reads=[kps, gamm[i2]], writes=[attnT[i2]])
                yield
                tb = bank
                tbv = bank.ap.bitcast(BF16)
                pg.op("pe", lambda e: e.transpose(out=tbv[:, 0:128], in_=Cm[i2].ap, identity=cx.identb.ap), reads=[Cm[i2], cx.identb], writes=[tb])
                yield
                hr = [0]
                bc0 = BC[h][hr[0] % 2]
                pg.op("act", lambda e: e.copy(bc0.ap[:, 0, :], tbv[:, 0:128]), reads=[tb], writes=[bc0])
                pg.op("pool", lambda e: e.tensor_copy(out=bc0.ap[:, 1, :], in_=Cm[i2].ap), reads=[Cm[i2]], writes=[bc0])
                p0 = Pm[h][hr[0] % 2]; p032 = Pm32[h][hr[0] % 2]; hr[0] += 1
                pg.op("pool", lambda e: e.tensor_tensor(out=p0.ap, in0=bc0.ap[:, 0, :], in1=cx.identb.ap, op=ALU.add), reads=[bc0, cx.identb], writes=[p0])
                yield
                bcp, pp, pp32 = bc0, p0, p032
                for k in range(1, 6):
                    sq_ = bank
                    if k < 5:
                        pg.op("pe", lambda e: e.matmul(sq_.ap[:, 0:128], lhsT=bcp.ap[:, 1, :], rhs=bcp.ap[:, 0, :], start=True, stop=True), reads=[bcp], writes=[sq_])
                    pg.op("pe", lambda e: e.matmul(sq_.ap[:, 128:256], lhsT=bcp.ap[:, 0, :], rhs=bcp.ap[:, 1, :], start=True, stop=True), reads=[bcp], writes=[sq_])
                    yield
                    bcn = BC[h][hr[0] % 2]
                    if k < 5:
                        pg.op("act", lambda e: e.copy(bcn.ap.rearrange("p a t -> p (a t)"), sq_.ap[:, 0:256]), reads=[sq_], writes=[bcn])
                    else:
                        pg.op("act", lambda e: e.copy(bcn.ap[:, 1, :], sq_.ap[:, 128:256]), reads=[sq_], writes=[bcn])
                        yield
                    pps = bank
                    pg.op("pe", lambda e: e.matmul(pps.ap[:, 0:128], lhsT=bcn.ap[:, 1, :], rhs=pp.ap, start=True, stop=True), reads=[bcn, pp], writes=[pps])
                    yield
                    pn = Pm[h][hr[0] % 2]; pn32 = Pm32[h][hr[0] % 2]; hr[0] += 1
                    pg.op("dve", lambda e: e.tensor_tensor(out=pn.ap, in0=pps.ap[:, 0:128], in1=pp.ap, op=ALU.add), reads=[pps, pp], writes=[pn])
                    yield
                    bcp, pp, pp32 = bcn, pn, pn32
                ups = bank
                pg.op("pe", lambda e: e.matmul(ups.ap[:, 0:128], lhsT=pp.ap, rhs=vb4.ap[:, h, :], start=True, stop=True), reads=[pp, vb4], writes=[ups])
                pg.op("pe", lambda e: e.matmul(ups.ap[:, 128:256], lhsT=kbg4.ap[:, h, :], rhs=pp.ap, start=True, stop=True), reads=[pp, kbg4], writes=[ups])
                yield
                pg.op("act", lambda e: e.copy(usb[i2].ap, ups.ap[:, 0:128]), reads=[ups], writes=[usb[i2]])
                pg.op("act", lambda e: e.copy(wT[i2].ap, ups.ap[:, 128:256]), reads=[ups], writes=[wT[i2]])
                yield
                for ci, c in enumerate(chunks):
                    r0 = c * 64
                    rs_ = slice(r0, r0 + 64)
                    wps = bank
                    pg.op("pe", lambda e: e.matmul(wps.ap[rs_, 0:128], lhsT=wT[i2].ap[:, rs_], rhs=Sb[h].ap, start=True, stop=True), reads=[wT[i2], Sb[h]], writes=[wps])
                    yield
                    pg.op("dve", lambda e: e.scalar_tensor_tensor(out=vn[i2].ap[rs_, :], in0=wps.ap[rs_, 0:128], scalar=-1.0, in1=usb[i2].ap[rs_, :], op0=ALU.mult, op1=ALU.add), reads=[usb[i2], wps], writes=[vn[i2]])
                    yield
                    pg.op("pe", lambda e: e.matmul(op_.ap[rs_, hc], lhsT=qd[i2].ap[:, rs_], rhs=Sb[h].ap, start=True, stop=False), reads=[qd[i2], Sb[h]], writes=[op_])
                    pg.op("pe", lambda e: e.matmul(op_.ap[rs_, hc], lhsT=attnT[i2].ap[rs_, rs_], rhs=vn[i2].ap[rs_, :], start=False, stop=True), reads=[attnT[i2], vn[i2]], writes=[op_])
                    kvp = bank
                    pg.op("pe", lambda e: e.matmul(kvp.ap[:, 0:128], lhsT=kend4.ap[rs_, h, :], rhs=vn[i2].ap[rs_, :], start=True, stop=True), reads=[kend4, vn[i2]], writes=[kvp])
                    yield
                    pg.op("dve", lambda e: e.scalar_tensor_tensor(out=S32[h].ap, in0=S32[h].ap, scalar=deB.ap[:, c, blk, d, h:h + 1], in1=kvp.ap[:, 0:128], op0=ALU.mult, op1=ALU.add),
                          reads=[S32[h], deB, kvp], writes=[S32[h]])
                    pg.op("act", lambda e: e.copy(Sb[h].ap, S32[h].ap), reads=[S32[h]], writes=[Sb[h]])
                    yield
            gens = [head_gen(h) for h in range(4)]
            if pending[0] is not None:
                gens.append(pending[0])
                pending[0] = None
            while gens:
                for gnr in list(gens):
                    try:
                        next(gnr)
                    except StopIteration:
                        gens.remove(gnr)
            if d == 1:
                pg.op("act", lambda e: e.copy(osb.ap, op_.ap), reads=[op_], writes=[osb])
                pg.dma(cx.OB[ts, 512:1024], osb.ap, reads=[osb])
            else:
                pg.op("dve", lambda e: e.tensor_tensor(out=osb.ap, in0=op_.ap, in1=ob.ap, op=ALU.add), reads=[op_, ob], writes=[osb])
                pending[0] = hngs_gen(pg, cx, osb, ssq, junk, gn, gtb, 0, ysb, yT, 1024, ts)
        if pending[0] is not None:
            for _ in pending[0]:
                pass
            pending[0] = None


def phase_S5(pg, cx, es, L, l):
    nc = cx.nc
    w = cx.w
    sb = lambda name, shape, dt: pg.buf(es.enter_context(nc.sbuf_tensor(pg.uname(name), shape, dt)).ap(), name)
    NS = int(np.ceil(np.log2(L)))
    dve = lambda fn, r, wr: pg.op("dve", fn, reads=r, writes=wr)
    A = lambda nm: sb("S_" + nm, [128, 32], F32)
    lre, lim, dt_, ar, ai, m_, sn, cs, Are, Aim, t1, t2, t3, fre, fim, den = [A(n) for n in
        ("lre", "lim", "dt", "ar", "ai", "m", "sn", "cs", "Are", "Aim", "t1", "t2", "t3", "fre", "fim", "den")]
    load_T(pg, cx, lre, lre.ap, w["s5_lambda_re"][l].rearrange("d (gh gl) p -> (d gh) (gl p)", gl=2), 32)
    load_T(pg, cx, lim, lim.ap, w["s5_lambda_im"][l].rearrange("d (gh gl) p -> (d gh) (gl p)", gl=2), 32)
    ld2 = sb("S_ld2", [32, 2], F32)
    pg.dma(ld2.ap, w["s5_log_dt"][l].rearrange("d (gh gl) -> (d gh) gl", gl=2), writes=[ld2])
    stl = sb("S_stl", [32, 128], F32)
    for gl in range(2):
        dve(lambda e: e.tensor_copy(out=stl.ap[:, 64 * gl:64 * gl + 64], in_=ld2.ap[:, gl:gl + 1].to_broadcast([32, 64])), [ld2], [stl])
    psl = cx.psf.get()
    pg.op("pe", lambda e: e.transpose(out=psl.ap[:, :32], in_=stl.ap, identity=cx.identf.ap[:32, :32]), reads=[stl, cx.identf], writes=[psl])
    dve(lambda e: e.tensor_copy(out=dt_.ap, in_=psl.ap[:, :32]), [psl], [dt_])
    pg.op("act", lambda e: e.activation(out=dt_.ap, in_=dt_.ap, func=AF.Exp), reads=[dt_], writes=[dt_])
    dve(lambda e: e.tensor_tensor(out=ar.ap, in0=lre.ap, in1=dt_.ap, op=ALU.mult), [lre, dt_], [ar])
    dve(lambda e: e.tensor_tensor(out=ai.ap, in0=lim.ap, in1=dt_.ap, op=ALU.mult), [lim, dt_], [ai])
    pg.op("act", lambda e: e.activation(out=m_.ap, in_=ar.ap, func=AF.Exp, scale=1.0 / 16), reads=[ar], writes=[m_])
    pg.op("act", lambda e: e.activation(out=sn.ap, in_=ai.ap, func=AF.Sin, scale=1.0 / 16), reads=[ai], writes=[sn])
    pg.op("act", lambda e: e.activation(out=cs.ap, in_=ai.ap, func=AF.Sin, scale=1.0 / 16, bias=cx.halfpi.ap[:, 0:1]), reads=[ai, cx.halfpi], writes=[cs])
    dve(lambda e: e.tensor_tensor(out=Are.ap, in0=m_.ap, in1=cs.ap, op=ALU.mult), [m_, cs], [Are])
    dve(lambda e: e.tensor_tensor(out=Aim.ap, in0=m_.ap, in1=sn.ap, op=ALU.mult), [m_, sn], [Aim])

    def csquare(re, im):
        dve(lambda e: e.tensor_tensor(out=t1.ap, in0=re, in1=re, op=ALU.mult), [Are, PW], [t1])
        dve(lambda e: e.tensor_tensor(out=t2.ap, in0=im, in1=im, op=ALU.mult), [Aim, PW], [t2])
        dve(lambda e: e.tensor_tensor(out=t3.ap, in0=re, in1=im, op=ALU.mult), [Are, Aim, PW], [t3])

    PW = sb("S_PW", [128, 32, NS, 3], F32)
    for _ in range(4):
        csquare(Are.ap, Aim.ap)
        dve(lambda e: e.tensor_tensor(out=Are.ap, in0=t1.ap, in1=t2.ap, op=ALU.subtract), [t1, t2], [Are])
        dve(lambda e: e.tensor_scalar(out=Aim.ap, in0=t3.ap, scalar1=2.0, scalar2=None, op0=ALU.mult), [t3], [Aim])
    dve(lambda e: e.tensor_tensor(out=den.ap, in0=lre.ap, in1=lre.ap, op=ALU.mult), [lre], [den])
    dve(lambda e: e.tensor_tensor(out=t1.ap, in0=lim.ap, in1=lim.ap, op=ALU.mult), [lim], [t1])
    dve(lambda e: e.tensor_tensor(out=den.ap, in0=den.ap, in1=t1.ap, op=ALU.add), [den, t1], [den])
    dve(lambda e: e.reciprocal(out=den.ap, in_=den.ap), [den], [den])
    dve(lambda e: e.tensor_scalar(out=t3.ap, in0=Are.ap, scalar1=-1.0, scalar2=None, op0=ALU.add), [Are], [t3])
    dve(lambda e: e.tensor_tensor(out=t1.ap, in0=t3.ap, in1=lre.ap, op=ALU.mult), [t3, lre], [t1])
    dve(lambda e: e.tensor_tensor(out=t2.ap, in0=Aim.ap, in1=lim.ap, op=ALU.mult), [Aim, lim], [t2])
    dve(lambda e: e.tensor_tensor(out=t1.ap, in0=t1.ap, in1=t2.ap, op=ALU.add), [t1, t2], [t1])
    dve(lambda e: e.tensor_tensor(out=fre.ap, in0=t1.ap, in1=den.ap, op=ALU.mult), [t1, den], [fre])
    dve(lambda e: e.tensor_tensor(out=t1.ap, in0=Aim.ap, in1=lre.ap, op=ALU.mult), [Aim, lre], [t1])
    dve(lambda e: e.tensor_tensor(out=t2.ap, in0=t3.ap, in1=lim.ap, op=ALU.mult), [t3, lim], [t2])
    dve(lambda e: e.tensor_tensor(out=t1.ap, in0=t1.ap, in1=t2.ap, op=ALU.subtract), [t1, t2], [t1])
    dve(lambda e: e.tensor_tensor(out=fim.ap, in0=t1.ap, in1=den.ap, op=ALU.mult), [t1, den], [fim])
    for k in range(NS):
        if k == 0:
            dve(lambda e: e.tensor_copy(out=PW.ap[:, :, 0, 0], in_=Are.ap), [Are], [PW])
            dve(lambda e: e.tensor_copy(out=PW.ap[:, :, 0, 1], in_=Aim.ap), [Aim], [PW])
        else:
            csquare(PW.ap[:, :, k - 1, 0], PW.ap[:, :, k - 1, 1])
            dve(lambda e: e.tensor_tensor(out=PW.ap[:, :, k, 0], in0=t1.ap, in1=t2.ap, op=ALU.subtract), [t1, t2], [PW])
            dve(lambda e: e.tensor_scalar(out=PW.ap[:, :, k, 1], in0=t3.ap, scalar1=2.0, scalar2=None, op0=ALU.mult), [t3], [PW])
        dve(lambda e: e.tensor_scalar(out=PW.ap[:, :, k, 2], in0=PW.ap[:, :, k, 1], scalar1=-1.0, scalar2=None, op0=ALU.mult), [PW], [PW])
    CL = sb("S_CL", [128, 32, 2, 32], F32)
    Wb = sb("S_Wb", [128, 32, 2, 32], F32)
    PWs = sb("S_PWs", [128, 32, 9, 2], F32)
    dve(lambda e: e.memset(PWs.ap[:, :, 0, 0], 1.0), [], [PWs])
    dve(lambda e: e.memset(PWs.ap[:, :, 0, 1], 0.0), [], [PWs])
    for k in range(1, 9):
        pr, pi_ = PWs.ap[:, :, k - 1, 0], PWs.ap[:, :, k - 1, 1]
        dve(lambda e: e.tensor_tensor(out=t1.ap, in0=pr, in1=PW.ap[:, :, 0, 0], op=ALU.mult), [PWs, PW], [t1])
        dve(lambda e: e.tensor_tensor(out=t2.ap, in0=pi_, in1=PW.ap[:, :, 0, 1], op=ALU.mult), [PWs, PW], [t2])
        dve(lambda e: e.tensor_tensor(out=PWs.ap[:, :, k, 0], in0=t1.ap, in1=t2.ap, op=ALU.subtract), [t1, t2], [PWs])
        dve(lambda e: e.tensor_tensor(out=t1.ap, in0=pr, in1=PW.ap[:, :, 0, 1], op=ALU.mult), [PWs, PW], [t1])
        dve(lambda e: e.tensor_tensor(out=t2.ap, in0=pi_, in1=PW.ap[:, :, 0, 0], op=ALU.mult), [PWs, PW], [t2])
        dve(lambda e: e.tensor_tensor(out=PWs.ap[:, :, k, 1], in0=t1.ap, in1=t2.ap, op=ALU.add), [t1, t2], [PWs])
    with ExitStack() as es1:
        sb1 = lambda name, shape, dt: pg.buf(es1.enter_context(nc.sbuf_tensor(pg.uname(name), shape, dt)).ap(), name)
        Bt = [sb1("S_Bt%d" % i, [128, 32, 16], F32) for i in range(2)]
        Bb = [sb1("S_Bb%d" % i, [128, 32, 16], F32) for i in range(2)]
        tmpb = sb1("S_tmpb", [128, 32, 16], F32)
        for i, nm in enumerate(("s5_b_re", "s5_b_im")):
            base = w[nm][l]
            pg.dma(Bt[i].ap, bass.AP(base.tensor, base.offset, [[16, 128], [2048, 32], [1, 16]]), writes=[Bt[i]])
        fb = lambda t: t.ap.unsqueeze(2).to_broadcast([128, 32, 16])
        dve(lambda e: e.tensor_tensor(out=Bb[0].ap, in0=Bt[0].ap, in1=fb(fre), op=ALU.mult), [Bt[0], fre], [Bb[0]])
        dve(lambda e: e.tensor_tensor(out=tmpb.ap, in0=Bt[1].ap, in1=fb(fim), op=ALU.mult), [Bt[1], fim], [tmpb])
        dve(lambda e: e.tensor_tensor(out=Bb[0].ap, in0=Bb[0].ap, in1=tmpb.ap, op=ALU.subtract), [Bb[0], tmpb], [Bb[0]])
        dve(lambda e: e.tensor_tensor(out=Bb[1].ap, in0=Bt[1].ap, in1=fb(fre), op=ALU.mult), [Bt[1], fre], [Bb[1]])
        dve(lambda e: e.tensor_tensor(out=tmpb.ap, in0=Bt[0].ap, in1=fb(fim), op=ALU.mult), [Bt[0], fim], [tmpb])
        dve(lambda e: e.tensor_tensor(out=Bb[1].ap, in0=Bb[1].ap, in1=tmpb.ap, op=ALU.add), [Bb[1], tmpb], [Bb[1]])
        pg.op("pool", lambda e: e.memset(Wb.ap, 0.0), writes=[Wb])
        for c in range(2):
            dve(lambda e: e.tensor_copy(out=Wb.ap[0:64, :, c, 0:16], in_=Bb[c].ap[0:64]), [Bb[c]], [Wb])
            dve(lambda e: e.tensor_copy(out=Wb.ap[64:128, :, c, 16:32], in_=Bb[c].ap[64:128]), [Bb[c]], [Wb])
        St0 = sb1("S_St0", [128, 64], F32)
        St = sb1("S_St", [128, 128], F32)
        for d in range(2):
            for c, nm in enumerate(("s5_c_re", "s5_c_im")):
                for blk in range(4):
                    pg.dma(St0.ap, w[nm][l, d, 8 * blk:8 * blk + 8].rearrange("g i p -> (g i) p"), writes=[St0])
                    sgn = 1.0 if c == 0 else -1.0
                    for hh in range(2):
                        dve(lambda e: e.tensor_scalar(out=St.ap[:, 64 * hh:64 * hh + 64], in0=St0.ap, scalar1=cx.pm.ap[:, hh:hh + 1], scalar2=sgn, op0=ALU.mult, op1=ALU.mult),
                            [St0, cx.pm], [St])
                    ps = cx.psf.get()
                    pg.op("pe", lambda e: e.transpose(out=ps.ap[:, 0:128], in_=St.ap, identity=cx.identf.ap), reads=[St, cx.identf], writes=[ps])
                    dg0 = d * 16 + blk * 4
                    pg.op("act", lambda e: e.copy(CL.ap[:, dg0:dg0 + 4, c, :], ps.ap[:, 0:128].rearrange("p (q m) -> p q m", q=4)), reads=[ps], writes=[CL])
    dsk = sb("S_dsk", [32, 16], F32)
    load_T(pg, cx, dsk, dsk.ap, w["s5_d"][l].rearrange("(g q) -> g q", q=32), 16, wd=32)
    bgl = sb("S_bgl", [128, 4], F32)
    load_T(pg, cx, bgl, bgl.ap, w["s5_b_glu"][l].rearrange("(c p) -> c p", p=128), 4)
    es2 = ExitStack()
    sb2 = lambda name, shape, dt: pg.buf(es2.enter_context(nc.sbuf_tensor(pg.uname(name), shape, dt)).ap(), name)
    NCH = L // 8
    NSC = int(np.ceil(np.log2(NCH)))
    HW = min(512, NCH)
    NH = NCH // HW
    ub = sb2("S_ub", [32, 8, NCH], BF16)
    UW = min(2048, L)
    ust = [sb2("S_ust%d" % i, [32, UW], F32) for i in range(2)]
    Yc = sb2("S_Yc", [32, L], F32)
    XS_ = [[sb2("S_X%d_%d" % (d, i), [128, NCH + 2], F32) for i in range(3)] for d in range(2)]
    Xb = [[sb2("S_Xb%d_%d" % (d, c), [128, NCH + 2], BF16) for c in range(2)] for d in range(2)]
    Wt_ = [sb2("S_Wt%d" % d, [128, 2, 8, 32], F32) for d in range(2)]
    tmpw_ = [sb2("S_tmpw%d" % d, [128, 8, 32], F32) for d in range(2)]
    WsT = [sb2("S_WsT%d" % d, [32, 2, 8, 128], BF16) for d in range(2)]
    CI = [sb2("S_CI%d" % d, [128, 2, 8, 32], BF16) for d in range(2)]
    CIf_ = [sb2("S_CIf%d" % d, [128, 2, 8, 32], F32) for d in range(2)]
    Kd = [sb2("S_Kd%d" % d, [32, 8, 32], BF16) for d in range(2)]
    ua_t = sb2("S_ua", [32, UW], F32)
    x2_t = sb2("S_x2", [32, UW], F32)
    zo_t = sb2("S_zo", [32, UW], BF16)

    def strided(ap2, start, n, step):
        b0 = ap2[:, start:start + 1]
        return bass.AP(b0.tensor, b0.offset, [list(ap2.ap[0]), [step * ap2.ap[1][0], n]])

    ev = [0]
    prev_pair = None
    out_ps = Rot(cx.psf.items[2:5])
    bcW = lambda a: a.unsqueeze(1).to_broadcast([128, 8, 32])
    bcP = lambda a: a.unsqueeze(2).to_broadcast([128, 8, 32])
    for gh in range(16):
        urow = PF_S5_U + 32 * gh
        for i, t0 in enumerate(range(0, L, UW)):
            st_ = ust[i % 2]
            pg.dma(st_.ap, cx.PF[urow:urow + 32, t0:t0 + UW], writes=[st_])
            pg.op("pool", lambda e: e.tensor_copy(out=ub.ap[:, :, t0 // 8:(t0 + UW) // 8], in_=st_.ap.rearrange("p (n s) -> p s n", s=8)), reads=[st_], writes=[ub])
        def dir_gen(d, gh=gh):
            bank = cx.psf.items[d]
            Wt, tmpw, CIf = Wt_[d], tmpw_[d], CIf_[d]
            dg = d * 16 + gh
            wbr, wbi = Wb.ap[:, dg, 0, :], Wb.ap[:, dg, 1, :]
            pre, pim = PWs.ap[:, dg, 0:8, 0], PWs.ap[:, dg, 0:8, 1]
            dve(lambda e: e.tensor_tensor(out=Wt.ap[:, 0], in0=bcW(wbr), in1=bcP(pre), op=ALU.mult), [Wb, PWs], [Wt])
            yield
            dve(lambda e: e.tensor_tensor(out=tmpw.ap, in0=bcW(wbi), in1=bcP(pim), op=ALU.mult), [Wb, PWs], [tmpw])
            yield
            dve(lambda e: e.tensor_tensor(out=Wt.ap[:, 0], in0=Wt.ap[:, 0], in1=tmpw.ap, op=ALU.subtract), [Wt, tmpw], [Wt])
            yield
            dve(lambda e: e.tensor_tensor(out=Wt.ap[:, 1], in0=bcW(wbr), in1=bcP(pim), op=ALU.mult), [Wb, PWs], [Wt])
            yield
            dve(lambda e: e.tensor_tensor(out=tmpw.ap, in0=bcW(wbi), in1=bcP(pre), op=ALU.mult), [Wb, PWs], [tmpw])
            yield
            dve(lambda e: e.tensor_tensor(out=Wt.ap[:, 1], in0=Wt.ap[:, 1], in1=tmpw.ap, op=ALU.add), [Wt, tmpw], [Wt])
            yield
            for c in range(2):
                for t4 in range(0, 8, 4):
                    ps = bank
                    for tq in range(4):
                        pg.op("pe", lambda e: e.transpose(out=ps.ap[:32, tq * 128:(tq + 1) * 128], in_=Wt.ap[:, c, t4 + tq, :], identity=cx.identf.ap), reads=[Wt, cx.identf], writes=[ps])
                    pg.op("act", lambda e: e.copy(WsT[d].ap[:, c, t4:t4 + 4, :], ps.ap[:32, :].rearrange("p (q m) -> p q m", q=4)), reads=[ps], writes=[WsT[d]])
                    yield
            cl0, cl1 = CL.ap[:, dg, 0, :], CL.ap[:, dg, 1, :]
            pre1, pim1 = PWs.ap[:, dg, 1:9, 0], PWs.ap[:, dg, 1:9, 1]
            dve(lambda e: e.tensor_tensor(out=CIf.ap[:, 0], in0=bcW(cl0), in1=bcP(pre1), op=ALU.mult), [CL, PWs], [CIf])
            yield
            dve(lambda e: e.tensor_tensor(out=tmpw.ap, in0=bcW(cl1), in1=bcP(pim1), op=ALU.mult), [CL, PWs], [tmpw])
            yield
            dve(lambda e: e.tensor_tensor(out=CI[d].ap[:, 0], in0=CIf.ap[:, 0], in1=tmpw.ap, op=ALU.add), [CIf, tmpw], [CI[d]])
            yield
            dve(lambda e: e.tensor_tensor(out=CIf.ap[:, 1], in0=bcW(cl1), in1=bcP(pre1), op=ALU.mult), [CL, PWs], [CIf])
            yield
            dve(lambda e: e.tensor_tensor(out=tmpw.ap, in0=bcW(cl0), in1=bcP(pim1), op=ALU.mult), [CL, PWs], [tmpw])
            yield
            dve(lambda e: e.tensor_tensor(out=CI[d].ap[:, 1], in0=CIf.ap[:, 1], in1=tmpw.ap, op=ALU.subtract), [CIf, tmpw], [CI[d]])
            yield
            ps = bank
            for tau in range(8):
                po = ps.ap[0:32, tau * 32:(tau + 1) * 32]
                pg.op("pe", lambda e: e.matmul(po, lhsT=Wt.ap[:, 0, tau, :], rhs=cl0, start=True, stop=False), reads=[Wt, CL], writes=[ps])
                pg.op("pe", lambda e: e.matmul(po, lhsT=Wt.ap[:, 1, tau, :], rhs=cl1, start=False, stop=True), reads=[Wt, CL], writes=[ps])
            pg.op("act", lambda e: e.copy(Kd[d].ap, ps.ap[0:32, 0:256].rearrange("p (t m) -> p t m", t=8)), reads=[ps], writes=[Kd[d]])
            yield
            re, im, T = XS_[d]
            for b_ in (re, im, T):
                pg.op("pool", lambda e: e.memset(b_.ap, 0.0), writes=[b_])
                yield
            for c, dstb in ((0, re), (1, im)):
                for hf in range(NH):
                    ps = bank
                    for s_ in range(8):
                        tau = 7 - s_ if d == 0 else s_
                        pg.op("pe", lambda e: e.matmul(ps.ap[:, :HW], lhsT=WsT[d].ap[:, c, tau, :], rhs=ub.ap[:, s_, hf * HW:(hf + 1) * HW], start=(s_ == 0), stop=(s_ == 7)),
                              reads=[WsT[d], ub], writes=[ps])
                    ev[0] += 1
                    if ev[0] % 2 == 0:
                        pg.op("act", lambda e: e.copy(dstb.ap[:, 1 + hf * HW:1 + (hf + 1) * HW], ps.ap[:, :HW]), reads=[ps], writes=[dstb])
                        yield
                    else:
                        pg.op("dve", lambda e: e.tensor_copy(out=dstb.ap[:, 1 + hf * HW:1 + (hf + 1) * HW], in_=ps.ap[:, :HW]), reads=[ps], writes=[dstb])
                        yield
            for k in range(NSC):
                sft = 1 << k
                if sft >= NCH:
                    break
                kk = k + 3
                cre, cim, ncim = PW.ap[:, dg, kk, 0:1], PW.ap[:, dg, kk, 1:2], PW.ap[:, dg, kk, 2:3]
                if d == 0:
                    dst, src, keep = slice(1 + sft, 1 + NCH), slice(1, 1 + NCH - sft), slice(1, 1 + sft)
                else:
                    dst, src, keep = slice(1, 1 + NCH - sft), slice(1 + sft, 1 + NCH), slice(1 + NCH - sft, 1 + NCH)
                dve(lambda e: e.scalar_tensor_tensor(out=T.ap[:, dst], in0=re.ap[:, src], scalar=cre, in1=re.ap[:, dst], op0=ALU.mult, op1=ALU.add), [re, PW], [T])
                yield
                dve(lambda e: e.scalar_tensor_tensor(out=T.ap[:, dst], in0=im.ap[:, src], scalar=ncim, in1=T.ap[:, dst], op0=ALU.mult, op1=ALU.add), [im, T, PW], [T])
                yield
                pg.op("pool", lambda e: e.tensor_copy(out=T.ap[:, keep], in_=re.ap[:, keep]), reads=[re], writes=[T])
                yield
                if d == 0:
                    rv = lambda ap, sl: bass.AP(ap.tensor, ap[:, sl].offset + (sl.stop - sl.start) - 1, [list(ap.ap[0]), [-1, sl.stop - sl.start]])
                    dve(lambda e: e.scalar_tensor_tensor(out=rv(im.ap, dst), in0=rv(im.ap, src), scalar=cre, in1=rv(im.ap, dst), op0=ALU.mult, op1=ALU.add), [im, PW], [im])
                    yield
                else:
                    dve(lambda e: e.scalar_tensor_tensor(out=im.ap[:, dst], in0=im.ap[:, src], scalar=cre, in1=im.ap[:, dst], op0=ALU.mult, op1=ALU.add), [im, PW], [im])
                    yield
                dve(lambda e: e.scalar_tensor_tensor(out=im.ap[:, dst], in0=re.ap[:, src], scalar=cim, in1=im.ap[:, dst], op0=ALU.mult, op1=ALU.add), [re, im, PW], [im])
                yield
                re, T = T, re
            pg.op("pool", lambda e: e.tensor_copy(out=Xb[d][0].ap, in_=re.ap), reads=[re], writes=[Xb[d][0]])
            yield
            pg.op("pool", lambda e: e.tensor_copy(out=Xb[d][1].ap, in_=im.ap), reads=[im], writes=[Xb[d][1]])
            yield

        def gelu_gen(gh_, urow_):
            for t0 in range(0, L, UW):
                tsl = slice(t0, t0 + UW)
                ua, x2, zo = ua_t.ap, x2_t.ap, zo_t.ap
                pg.dma(ua, cx.PF[urow_:urow_ + 32, tsl], writes=[ua_t])
                yv = Yc.ap[:, tsl]
                dve(lambda e: e.scalar_tensor_tensor(out=yv, in0=ua, scalar=dsk.ap[:, gh_:gh_ + 1], in1=yv, op0=ALU.mult, op1=ALU.add), [ua_t, dsk, Yc], [Yc])
                yield
                pg.op("pool", lambda e: e.tensor_tensor(out=x2, in0=yv, in1=yv, op=ALU.mult), reads=[Yc], writes=[x2_t])
                yield
                dve(lambda e: e.tensor_scalar(out=x2, in0=x2, scalar1=0.044715, scalar2=1.0, op0=ALU.mult, op1=ALU.add), [x2_t], [x2_t])
                yield
                pg.op("pool", lambda e: e.tensor_tensor(out=x2, in0=x2, in1=yv, op=ALU.mult), reads=[Yc, x2_t], writes=[x2_t])
                yield
                pg.op("act", lambda e: e.activation(out=x2, in_=x2, func=AF.Sigmoid, scale=1.5957691216), reads=[x2_t], writes=[x2_t])
                yield
                dve(lambda e: e.tensor_tensor(out=zo, in0=x2, in1=yv, op=ALU.mult), [x2_t, Yc], [zo_t])
                pg.dma(cx.ZT[urow_ - PF_S5_U:urow_ - PF_S5_U + 32, tsl], zo, reads=[zo_t])
                yield

        gens = [dir_gen(0), dir_gen(1)] + ([gelu_gen(*prev_pair)] if prev_pair is not None else [])
        while gens:
            for gnr in list(gens):
                try:
                    next(gnr)
                except StopIteration:
                    gens.remove(gnr)
        prev_pair = (gh, urow)
        for hf in range(NH):
            for sp in range(8):
                ps = out_ps.get()
                po = ps.ap[0:32, :HW]
                mm = []
                for c in range(2):
                    mm.append((CI[0].ap[:, c, sp, :], Xb[0][c].ap[:, hf * HW:hf * HW + HW], [CI[0], Xb[0][c]]))
                    mm.append((CI[1].ap[:, c, 7 - sp, :], Xb[1][c].ap[:, hf * HW + 2:hf * HW + 2 + HW], [CI[1], Xb[1][c]]))
                for s_ in range(0, sp + 1):
                    mm.append((Kd[0].ap[:, sp - s_, :], ub.ap[:, s_, hf * HW:(hf + 1) * HW], [Kd[0], ub]))
                for s_ in range(sp, 8):
                    mm.append((Kd[1].ap[:, s_ - sp, :], ub.ap[:, s_, hf * HW:(hf + 1) * HW], [Kd[1], ub]))
                for i, (lh, rh, rd) in enumerate(mm):
                    pg.op("pe", lambda e: e.matmul(po, lhsT=lh, rhs=rh, start=(i == 0), stop=(i == len(mm) - 1)), reads=rd, writes=[ps])
                pg.op("act", lambda e: e.copy(strided(Yc.ap, hf * HW * 8 + sp, HW, 8), po), reads=[ps], writes=[Yc])
    gens = [gelu_gen(*prev_pair)]
    for gnr in gens:
        for _ in gnr:
            pass
    pg.barrier()
    es2.close()
    wg = sb("S_wg", [128, 4, 512], BF16)
    wst = sb("S_wst", [128, 2048], F32)
    pg.dma(wst.ap[:, 0:2048].rearrange("p (k c) -> p k c", k=4), w["s5_w_glu"][l].rearrange("(k p) c -> p k c", p=128), writes=[wst])
    dve(lambda e: e.tensor_copy(out=wg.ap, in_=wst.ap[:, 0:2048].rearrange("p (k c) -> p k c", k=4)), [wst], [wg])
    zt = [sb("S_zt%d" % i, [128, 4, 512], BF16) for i in range(2)]
    gt = [sb("S_gt%d" % i, [128, 4, 512], F32) for i in range(2)]
    sg = sb("S_sg", [128, 512], F32)
    yo = [sb("S_yo%d" % i, [128, 4, 512], BF16) for i in range(2)]
    ZTv = cx.ZT.rearrange("(k p) t -> p k t", p=128)
    NT = L // 512
    for it in range(NT):
        tsl = slice(it * 512, (it + 1) * 512)
        z_ = zt[it % 2]; g_ = gt[it % 2]; y_ = yo[it % 2]
        pg.dma(z_.ap, ZTv[:, :, tsl], writes=[z_])
        pg.dma(g_.ap, cx.PF[PF_S5_G:PF_S5_G + 512, tsl].rearrange("(k p) t -> p k t", p=128), writes=[g_])
        pg.op("act", lambda e: e.activation(out=g_.ap, in_=g_.ap, func=AF.Silu), reads=[g_], writes=[g_])
        pg.op("pool", lambda e: e.tensor_tensor(out=g_.ap, in0=g_.ap, in1=z_.ap, op=ALU.mult), reads=[g_, z_], writes=[g_])
        for oc in range(4):
            ps = cx.psf.get()
            for k in range(4):
                pg.op("pe", lambda e: e.matmul(ps.ap, lhsT=wg.ap[:, k, oc * 128:(oc + 1) * 128], rhs=z_.ap[:, k, :], start=(k == 0), stop=(k == 3)), reads=[wg, z_], writes=[ps])
            pg.op("act", lambda e: e.activation(out=sg.ap, in_=ps.ap, func=AF.Sigmoid, bias=bgl.ap[:, oc:oc + 1]), reads=[ps, bgl], writes=[sg])
            dve(lambda e: e.tensor_tensor(out=y_.ap[:, oc, :], in0=sg.ap, in1=g_.ap[:, oc, :], op=ALU.mult), [sg, g_], [y_])
        pg.dma(cx.BT[1536:2048, tsl].rearrange("(k p) t -> p k t", p=128), y_.ap, reads=[y_])


W_NAMES = ["norm_g", "w_in", "lru_conv_w", "lru_conv_b", "lru_w_a", "lru_b_a", "lru_w_x", "lru_b_x", "lru_lambda",
           "gla_w_up", "gla_b_up", "gla_norm_g", "dn_conv_w", "dn_a_log", "dn_dt_bias", "dn_norm_g",
           "s5_lambda_re", "s5_lambda_im", "s5_log_dt", "s5_b_re", "s5_b_im", "s5_c_re", "s5_c_im", "s5_d",
           "s5_w_glu", "s5_b_glu", "w_branch", "w_merge_gate", "b_merge_gate", "w_out", "final_norm_g"]


def host_consts():
    c = {}
    c["identb"] = np.eye(128, dtype=np.float32).astype(ml_dtypes.bfloat16)
    c["identf"] = np.eye(128, dtype=np.float32)
    idx = np.arange(128)
    same = (idx[:, None] // 64) == (idx[None, :] // 64)
    le = idx[:, None] <= idx[None, :]
    lt = idx[:, None] < idx[None, :]
    c["m_incl"] = np.stack([(same & le), (same & le.T)]).astype(np.float32)
    c["m_strict_after"] = np.stack([(same & lt.T), (same & lt)]).astype(np.float32)
    c["m_dn"] = np.stack([c["m_strict_after"], c["m_incl"]], axis=1)
    c["chunkind"] = np.stack([np.repeat((idx // 64 == cc)[:, None], 128, axis=1) for cc in range(2)]).astype(np.float32)
    c["onesf"] = np.ones((128, 128), np.float32)
    ev = ((idx // 16) % 2 == 0).astype(np.float32)
    c["pm"] = np.stack([ev, 1.0 - ev], axis=1).astype(np.float32)
    return c


def build(L, shapes, nslot=2, depth=2, debug=False, branches=("lru", "gla", "dn", "s5")):
    from contextlib import ExitStack
    nc = bass.Bass("TRN2", target_bir_lowering=False)
    pg = Prog(nc)
    cx = Ctx()
    cx.nc = nc
    cx.pg = pg
    cx.w = {}
    for nm in W_NAMES:
        cx.w[nm] = nc.dram_tensor(nm, list(shapes[nm]), F32, kind="ExternalInput").ap()
    hc = host_consts()
    cx.cd = {}
    for nm, arr in hc.items():
        cx.cd[nm] = nc.dram_tensor("c_" + nm, list(arr.shape), BF16 if arr.dtype == ml_dtypes.bfloat16 else F32, kind="ExternalInput").ap()
    xs = [nc.dram_tensor("x%d" % s, [L, D], F32, kind="ExternalInput").ap() for s in range(nslot)]
    ys = [nc.dram_tensor("y%d" % s, [L, D], F32, kind="ExternalOutput").ap() for s in range(nslot)]
    sk = "ExternalOutput" if debug else "Internal"
    cx.PF = nc.dram_tensor("PF", [PF_ROWS, L], F32, kind=sk).ap()
    cx.PT = nc.dram_tensor("PT", [L, PT_COLS], F32, kind=sk).ap()
    cx.XNT = nc.dram_tensor("XNT", [D, L], BF16, kind=sk).ap()
    cx.BT = nc.dram_tensor("BT", [2048, L], BF16, kind=sk).ap()
    cx.OB = nc.dram_tensor("OB", [L, 1024], F32, kind=sk).ap()
    XS = [nc.dram_tensor("XS%d" % s, [L, D], F32, kind=sk).ap() for s in range(nslot)]
    cx.QKT = nc.dram_tensor("QKT", [1024, L], BF16, kind=sk).ap()
    cx.KVT = nc.dram_tensor("KVT", [L, 1024], BF16, kind=sk).ap()
    cx.ZT = nc.dram_tensor("ZT", [512, L], BF16, kind=sk).ap()
    psf, psb = mk_psum(pg, nc)
    cx.psf = Rot(psf[:5])
    cx.pso = psf[5]
    cx.psb = Rot(psb)
    gsb = lambda name, shape, dt: pg.buf(nc.alloc_sbuf_tensor(name, shape, dt).ap(), name)
    cx.identb = gsb("identb", [128, 128], BF16)
    pg.dma(cx.identb.ap, cx.cd["identb"], writes=[cx.identb])
    cx.identf = gsb("identf", [128, 128], F32)
    pg.dma(cx.identf.ap, cx.cd["identf"], writes=[cx.identf])
    cx.eps = gsb("eps", [128, 1], F32)
    pg.op("dve", lambda e: e.memset(cx.eps.ap, EPS), writes=[cx.eps])
    cx.one = gsb("one", [128, 1], F32)
    pg.op("dve", lambda e: e.memset(cx.one.ap, 1.0), writes=[cx.one])
    cx.ldst = gsb("ldst", [128, 128], F32)
    cx.m_incl = gsb("m_incl", [128, 2, 128], F32)
    pg.dma(cx.m_incl.ap, cx.cd["m_incl"].rearrange("d s t -> s d t"), writes=[cx.m_incl])
    cx.m_sa = gsb("m_sa", [128, 2, 128], F32)
    pg.dma(cx.m_sa.ap, cx.cd["m_strict_after"].rearrange("d s t -> s d t"), writes=[cx.m_sa])
    cx.m_dn = gsb("m_dn", [128, 2, 2, 128], F32)
    pg.dma(cx.m_dn.ap[:, 0], cx.cd["m_dn"][0].rearrange("j s t -> s j t"), writes=[cx.m_dn])
    pg.dma(cx.m_dn.ap[:, 1], cx.cd["m_dn"][1].rearrange("j s t -> s j t"), writes=[cx.m_dn])
    cx.chunkind = gsb("chunkind", [128, 2, 128], F32)
    pg.dma(cx.chunkind.ap, cx.cd["chunkind"].rearrange("c s m -> s c m"), writes=[cx.chunkind])
    cx.onesf = gsb("onesf", [128, 128], F32)
    pg.dma(cx.onesf.ap, cx.cd["onesf"], writes=[cx.onesf])
    cx.pm = gsb("pm", [128, 2], F32)
    pg.dma(cx.pm.ap, cx.cd["pm"], writes=[cx.pm])
    cx.halfpi = gsb("halfpi", [128, 1], F32)
    pg.op("dve", lambda e: e.memset(cx.halfpi.ap, float(np.pi / 2)), writes=[cx.halfpi])
    cx.zb = gsb("zb", [128, 2048], BF16)
    pg.op("pool", lambda e: e.memset(cx.zb.ap, 0.0), writes=[cx.zb])
    bidx = {"lru": 0, "gla": 1, "dn": 2, "s5": 3}
    for l in range(depth):
        for s in range(nslot):
            xin = xs[s] if l == 0 else XS[s]
            last = (l == depth - 1)
            xout = ys[s] if last else XS[s]
            pg.barrier()
            with ExitStack() as es:
                phase_P(pg, cx, es, L, xin, l)
                pg.barrier()
            for bn in ("lru", "gla", "dn", "s5"):
                if bn not in branches:
                    b = bidx[bn]
                    for t0 in range(0, L, 2048):
                        tw = min(2048, L - t0)
                        for c in range(4):
                            pg.dma(cx.BT[b * 512 + c * 128:b * 512 + (c + 1) * 128, t0:t0 + tw], cx.zb.ap[:, :tw], reads=[cx.zb])
            if "lru" in branches:
                with ExitStack() as es:
                    phase_LRU(pg, cx, es, L, l)
                    pg.barrier()
            if "s5" in branches:
                with ExitStack() as es:
                    phase_S5(pg, cx, es, L, l)
                    pg.barrier()
            if "gla" in branches:
                with ExitStack() as es:
                    phase_GLA(pg, cx, es, L, l)
                    pg.barrier()
            if "dn" in branches:
                with ExitStack() as es:
                    phase_DN(pg, cx, es, L, l)
                    pg.barrier()
            pg.barrier()
            with ExitStack() as es:
                phase_M(pg, cx, es, L, xin, xout, l, last)
                pg.barrier()
    pg.barrier()
    return nc, pg, hc


_CACHE = {}


def kernel(**inputs):
    L = inputs["x_prompt"].shape[1]
    shapes = {nm: inputs[nm].shape for nm in W_NAMES}
    nc, pg, hc = build(L, shapes)
    xp = np.ascontiguousarray(inputs["x_prompt"], dtype=np.float32)
    xsm = np.ascontiguousarray(inputs["x_sample"], dtype=np.float32)
    wmap = {nm: np.ascontiguousarray(inputs[nm], dtype=np.float32) for nm in W_NAMES}
    in_maps = []
    for c in range(8):
        m = dict(wmap)
        for nm, arr in hc.items():
            m["c_" + nm] = arr
        m["x0"] = xp[c]
        m["x1"] = xsm[c % 2]
        in_maps.append(m)
    res = run_bass_kernel_spmd(nc, in_maps, core_ids=list(range(8)))
    y_prompt = np.stack([np.asarray(res.results[c]["y0"], dtype=np.float32) for c in range(8)], axis=0)
    y_sample = np.stack([np.asarray(res.results[c]["y1"], dtype=np.float32) for c in range(2)], axis=0)
    return (y_prompt, y_sample)
```

```python
import numpy as np
import ml_dtypes
from contextlib import ExitStack
import concourse.bass as bass
import concourse.mybir as mybir
from concourse.bass_utils import run_bass_kernel_spmd

F32 = mybir.dt.float32
BF16 = mybir.dt.bfloat16
ALU = mybir.AluOpType
AF = mybir.ActivationFunctionType

D = 1024
BW = 512
D_IN = 5680
EPS = 1e-6
O_LRU_X, O_LRU_G = 0, 512
O_GLA_Q, O_GLA_K, O_GLA_V, O_GLA_G, O_GLA_LR = 1024, 1280, 1536, 2048, 2560
O_DN_QKV, O_DN_G, O_DN_BA = 2592, 4128, 4640
O_S5_U, O_S5_G = 4656, 5168


SAME_ENGINE_SYNC = True
STORES_ON_POOL = True


class Buf:
    __slots__ = ("ap", "w", "r", "name")

    def __init__(self, ap, name=""):
        self.ap = ap
        self.w = []
        self.r = []
        self.name = name

    def __getitem__(self, k):
        return self.ap[k]


class Prog:
    def __init__(self, nc, n_dma_sems=40):
        self.nc = nc
        self.eng = {"pe": nc.tensor, "act": nc.scalar, "dve": nc.vector, "pool": nc.gpsimd, "sp": nc.sync}
        self.sem = {k: nc.alloc_semaphore("s_" + k) for k in self.eng}
        self.cnt = {k: 0 for k in self.eng}
        self.seen = {k: {} for k in self.eng}
        self.dsem = [nc.alloc_semaphore("d%d" % i) for i in range(n_dma_sems)]
        self.dcnt = [0] * n_dma_sems
        self.dnext = 0
        self.ninst = 0

    def buf(self, ap, name=""):
        return Buf(ap, name)

    def uname(self, name):
        self.uid = getattr(self, "uid", 0) + 1
        return "%s_%d" % (name, self.uid)

    def _wait(self, e, dep):
        if dep[0] == "dma":
            key = ("dma", dep[1]); val = dep[2]
            if self.seen[e].get(key, 0) >= val:
                return
            self.eng[e].wait_ge(self.dsem[dep[1]], val)
        else:
            f, val = dep
            if f == e and (e in ("pe", "sp") or not SAME_ENGINE_SYNC):
                return
            key = f
            if self.seen[e].get(key, 0) >= val:
                return
            self.eng[e].wait_ge(self.sem[f], val)
        self.seen[e][key] = val
        self.ninst += 1

    def _deps(self, e, reads, writes):
        for b in reads:
            for d in b.w:
                self._wait(e, d)
        for b in writes:
            for d in b.w:
                self._wait(e, d)
            for d in b.r:
                self._wait(e, d)

    def op(self, e, inst_fn, reads=(), writes=()):
        self._deps(e, reads, writes)
        inst = inst_fn(self.eng[e])
        inst.then_inc(self.sem[e], 1)
        self.cnt[e] += 1
        me = (e, self.cnt[e])
        for b in reads:
            b.r.append(me)
            if len(b.r) > 24:
                b.r = b.r[-24:] if False else self._compress(b.r)
        for b in writes:
            b.w = [me]
            b.r = []
        self.ninst += 1
        return inst

    @staticmethod
    def _compress(lst):
        best = {}
        for d in lst:
            k = ("dma", d[1]) if d[0] == "dma" else d[0]
            v = d[2] if d[0] == "dma" else d[1]
            if k not in best or v > best[k][0]:
                best[k] = (v, d)
        return [x[1] for x in best.values()]

    def dma(self, out, in_, reads=(), writes=(), q=None, **kw):
        if q is None:
            q = "sp" if (len(writes) > 0 or not STORES_ON_POOL) else "pool"
        self._deps(q, reads, writes)
        j = self.dnext
        self.dnext = (self.dnext + 1) % len(self.dsem)
        if self.dcnt[j] > 0:
            self._wait(q, ("dma", j, self.dcnt[j]))
        self.dcnt[j] += 16
        self.eng[q].dma_start(out=out, in_=in_, **kw).then_inc(self.dsem[j], 16)
        me = ("dma", j, self.dcnt[j])
        for b in reads:
            b.r.append(me)
            if len(b.r) > 24:
                b.r = self._compress(b.r)
        for b in writes:
            b.w = [me]
            b.r = []
        self.ninst += 1

    def barrier(self):
        for e in self.eng:
            for f in self.eng:
                if f != e and self.cnt[f] > 0:
                    self._wait(e, (f, self.cnt[f]))
            for j, c in enumerate(self.dcnt):
                if c > 0:
                    self._wait(e, ("dma", j, c))


class Ctx:
    pass


def mk_psum(pg, nc):
    banks = []
    for i in range(6):
        banks.append(pg.buf(nc.alloc_psum_tensor("psf%d" % i, [128, 512], F32).ap(), "psf%d" % i))
    bb = []
    for i in range(2):
        bb.append(pg.buf(nc.alloc_psum_tensor("psb%d" % i, [128, 1024], BF16).ap(), "psb%d" % i))
    return banks, bb


class Rot:
    def __init__(self, items):
        self.items = items
        self.i = 0

    def get(self):
        x = self.items[self.i]
        self.i = (self.i + 1) % len(self.items)
        return x


PF_LRU_X, PF_LRU_G, PF_GLA_Q, PF_GLA_K, PF_DN_QKV, PF_S5_U, PF_S5_G, PF_GLA_LR = 0, 512, 1024, 1280, 1536, 3072, 3584, 4096
PF_ROWS = 4128
PF_CHUNKS = ([(O_LRU_X + 128 * i, 128) for i in range(4)] + [(O_LRU_G + 128 * i, 128) for i in range(4)]
             + [(O_GLA_Q + 128 * i, 128) for i in range(2)] + [(O_GLA_K + 128 * i, 128) for i in range(2)]
             + [(O_DN_QKV + 128 * i, 128) for i in range(12)] + [(O_S5_U + 128 * i, 128) for i in range(4)]
             + [(O_S5_G + 128 * i, 128) for i in range(4)] + [(O_GLA_LR, 32)])
PT_GLA_K, PT_GLA_V, PT_GLA_G, PT_DN_G, PT_DN_BA = 0, 256, 768, 1280, 1792
PT_COLS = 1808
PT_GROUPS = [(1280, 512, 0), (1792, 512, 512), (2304, 256, 1024), (4128, 512, 1280), (4640, 16, 1792)]


def load_cast_bf16(pg, nc, es, dst, src_ap, rows, cols, name, chunk=2048):
    st = [pg.buf(es.enter_context(nc.sbuf_tensor(name + "_st%d" % i, [128, chunk], F32)).ap()) for i in range(2)]
    i = 0
    for c0 in range(0, cols, chunk):
        cw = min(chunk, cols - c0)
        s = st[i % 2]
        pg.dma(s.ap[:rows, :cw], src_ap[:, c0:c0 + cw], writes=[s])
        if i % 2 == 0:
            pg.op("act", lambda e: e.copy(dst[0][:rows, c0:c0 + cw], s.ap[:rows, :cw]), reads=[s], writes=[dst[1]])
        else:
            pg.op("dve", lambda e: e.tensor_copy(out=dst[0][:rows, c0:c0 + cw], in_=s.ap[:rows, :cw]), reads=[s], writes=[dst[1]])
        i += 1


def phase_P(pg, cx, es, L, x_ap, l):
    nc = cx.nc
    TT = 512
    sb = lambda name, shape, dt: pg.buf(es.enter_context(nc.sbuf_tensor(pg.uname(name), shape, dt)).ap(), name)
    wbf = sb("P_w", [128, 8, D_IN], BF16)
    w_src = cx.w["w_in"][l].rearrange("(k p) c -> p k c", p=128)
    WC = D_IN // 4
    st = [sb("P_wst%d" % i, [128, WC], F32) for i in range(2)]
    for k in range(8):
        for q in range(4):
            s = st[q % 2]
            pg.dma(s.ap, w_src[:, k, q * WC:(q + 1) * WC], writes=[s])
            if q % 2 == 0:
                pg.op("act", lambda e: e.copy(wbf.ap[:, k, q * WC:(q + 1) * WC], s.ap), reads=[s], writes=[wbf])
            else:
                pg.op("dve", lambda e: e.tensor_copy(out=wbf.ap[:, k, q * WC:(q + 1) * WC], in_=s.ap), reads=[s], writes=[wbf])
    gk = sb("P_g", [128, 8], F32)
    load_T(pg, cx, gk, gk.ap, cx.w["norm_g"][l].rearrange("(k p) -> k p", p=128), 8)
    xt = [sb("P_x%d" % i, [128, 4, D], F32) for i in range(2)]
    xs = sb("P_xs", [128, D], BF16)
    junk = sb("P_junk", [128, D], BF16)
    ss = sb("P_ss", [128, 4], F32)
    xnT = [sb("P_xnT%d" % i, [128, 8, TT], BF16) for i in range(2)]
    stf = [sb("P_stf%d" % i, [128, 4, TT], F32) for i in range(2)]
    stt = [sb("P_stt%d" % i, [128, PT_COLS], F32) for i in range(2)]
    XNTv = cx.XNT.rearrange("(k p) t -> p k t", p=128)
    xv = x_ap.rearrange("(n j p) d -> n p j d", p=128, j=4)
    gb = gk.ap.unsqueeze(2).to_broadcast([128, 8, 128])
    nt = L // TT
    evac_i = 0
    for it in range(nt):
        x_b = xt[it % 2]
        pg.dma(x_b.ap, xv[it], writes=[x_b])
        xn = xnT[it % 2]
        for j in range(4):
            pg.op("act", lambda e: e.activation(out=junk.ap, in_=x_b.ap[:, j, :], func=AF.Square, accum_out=ss.ap[:, j:j + 1]),
                  reads=[x_b], writes=[junk, ss])
            pg.op("act", lambda e: e.activation(out=ss.ap[:, j:j + 1], in_=ss.ap[:, j:j + 1], func=AF.Sqrt, scale=1.0 / D, bias=cx.eps.ap[:, 0:1]),
                  reads=[ss, cx.eps], writes=[ss])
            pg.op("dve", lambda e: e.reciprocal(out=ss.ap[:, j:j + 1], in_=ss.ap[:, j:j + 1]), reads=[ss], writes=[ss])
            pg.op("dve", lambda e: e.tensor_scalar(out=xs.ap, in0=x_b.ap[:, j, :], scalar1=ss.ap[:, j:j + 1], scalar2=None, op0=ALU.mult),
                  reads=[x_b, ss], writes=[xs])
            pb = cx.psb.get()
            for k in range(8):
                pg.op("pe", lambda e: e.transpose(out=pb.ap[:, k * 128:(k + 1) * 128], in_=xs.ap[:, k * 128:(k + 1) * 128], identity=cx.identb.ap),
                      reads=[xs, cx.identb], writes=[pb])
            pg.op("dve", lambda e: e.tensor_tensor(out=xn.ap[:, :, j * 128:(j + 1) * 128], in0=pb.ap.rearrange("p (k t) -> p k t", k=8), in1=gb, op=ALU.mult),
                  reads=[pb, gk], writes=[xn])
        pg.dma(XNTv[:, :, it * TT:(it + 1) * TT], xn.ap, reads=[xn])
        for ci, (c0, cw) in enumerate(PF_CHUNKS):
            ps = cx.psf.get()
            for k in range(8):
                pg.op("pe", lambda e: e.matmul(ps.ap[:cw, :], lhsT=wbf.ap[:, k, c0:c0 + cw], rhs=xn.ap[:, k, :], start=(k == 0), stop=(k == 7)),
                      reads=[wbf, xn], writes=[ps])
            sbuf = stf[(ci // 4) % 2]
            evac_i += 1
            if evac_i % 2 == 0:
                pg.op("act", lambda e: e.copy(sbuf.ap[:cw, ci % 4, :], ps.ap[:cw, :]), reads=[ps], writes=[sbuf])
            else:
                pg.op("dve", lambda e: e.tensor_copy(out=sbuf.ap[:cw, ci % 4, :], in_=ps.ap[:cw, :]), reads=[ps], writes=[sbuf])
            if ci % 4 == 3:
                cb = ci // 4
                pg.dma(PFv_slice(cx, cb * 4, 4, it * TT, TT), sbuf.ap, reads=[sbuf])
            elif ci == len(PF_CHUNKS) - 1:
                pg.dma(cx.PF[4096:4128, it * TT:(it + 1) * TT], sbuf.ap[:32, 0, :], reads=[sbuf])
        for j in range(4):
            sbuf = stt[j % 2]
            for (c0, cw, o0) in PT_GROUPS:
                ps = cx.psf.get()
                for k in range(8):
                    pg.op("pe", lambda e: e.matmul(ps.ap[:, :cw], lhsT=xn.ap[:, k, j * 128:(j + 1) * 128], rhs=wbf.ap[:, k, c0:c0 + cw], start=(k == 0), stop=(k == 7)),
                          reads=[wbf, xn], writes=[ps])
                evac_i += 1
                if evac_i % 2 == 0:
                    pg.op("act", lambda e: e.copy(sbuf.ap[:, o0:o0 + cw], ps.ap[:, :cw]), reads=[ps], writes=[sbuf])
                else:
                    pg.op("dve", lambda e: e.tensor_copy(out=sbuf.ap[:, o0:o0 + cw], in_=ps.ap[:, :cw]), reads=[ps], writes=[sbuf])
            t0 = it * TT + j * 128
            pg.dma(cx.PT[t0:t0 + 128, :], sbuf.ap, reads=[sbuf])


def PFv_slice(cx, c0, nch, t0, tw):
    return cx.PF[c0 * 128:(c0 + nch) * 128, t0:t0 + tw].rearrange("(c p) t -> p c t", p=128)


def load_T(pg, cx, dst, dst_ap, src_ap, n, st_view=None, wd=128, **kw):
    st = cx.ldst
    pg.dma(st.ap[:n, :wd] if st_view is None else st_view(st.ap[:n, :wd]), src_ap, writes=[st], **kw)
    ps = cx.psf.get()
    pg.op("pe", lambda e: e.transpose(out=ps.ap[:wd, :n], in_=st.ap[:n, :wd], identity=cx.identf.ap[:n, :n]), reads=[st, cx.identf], writes=[ps])
    pg.op("dve", lambda e: e.tensor_copy(out=dst_ap, in_=ps.ap[:wd, :n]), reads=[ps], writes=[dst])


def phase_LRU(pg, cx, es, L, l):
    nc = cx.nc
    sb = lambda name, shape, dt: pg.buf(es.enter_context(nc.sbuf_tensor(pg.uname(name), shape, dt)).ap(), name)
    w = cx.w
    TL = min(2048, L)
    ntile = L // TL
    cw = sb("L_cw", [128, 4, 4], F32)
    load_T(pg, cx, cw, cw.ap.rearrange("p j c -> p (j c)"), w["lru_conv_w"][l].rearrange("j (c p) -> (j c) p", p=128), 16)
    cb = sb("L_cb", [128, 4], F32)
    load_T(pg, cx, cb, cb.ap, w["lru_conv_b"][l].rearrange("(c p) -> c p", p=128), 4)
    bias = sb("L_bias", [128, 2, 2, 4], F32)
    load_T(pg, cx, bias, bias.ap[:, 0].rearrange("p d c -> p (d c)"), w["lru_b_a"][l].rearrange("d (c p) -> (d c) p", p=128), 8)
    load_T(pg, cx, bias, bias.ap[:, 1].rearrange("p d c -> p (d c)"), w["lru_b_x"][l].rearrange("d (c p) -> (d c) p", p=128), 8)
    lam = sb("L_lam", [128, 2, 4], F32)
    load_T(pg, cx, lam, lam.ap.rearrange("p d c -> p (d c)"), w["lru_lambda"][l].rearrange("d (c p) -> (d c) p", p=128), 8)
    coef = sb("L_coef", [128, 2, 4], F32)
    coef2 = sb("L_coef2", [128, 2, 4], F32)
    pg.op("act", lambda e: e.activation(out=coef.ap, in_=lam.ap, func=AF.Exp, scale=-1.0), reads=[lam], writes=[coef])
    pg.op("act", lambda e: e.activation(out=coef.ap, in_=coef.ap, func=AF.Ln, bias=cx.one.ap[:, 0:1]), reads=[coef, cx.one], writes=[coef])
    pg.op("dve", lambda e: e.tensor_scalar(out=coef2.ap, in0=coef.ap, scalar1=-16.0, scalar2=None, op0=ALU.mult), reads=[coef], writes=[coef2])
    pg.op("dve", lambda e: e.tensor_scalar(out=coef.ap, in0=coef.ap, scalar1=-8.0, scalar2=None, op0=ALU.mult), reads=[coef], writes=[coef])
    wg = sb("L_wg", [128, 2, 2, 4, 128], BF16)
    wst = sb("L_wst", [128, 2, 4, 128], F32)
    for ai, nm in enumerate(("lru_w_a", "lru_w_x")):
        pg.dma(wst.ap, w[nm][l].rearrange("d h i j -> i d h j"), writes=[wst])
        pg.op("dve", lambda e: e.tensor_copy(out=wg.ap[:, ai], in_=wst.ap), reads=[wst], writes=[wg])
    XC = sb("L_XC", [128, L], F32)
    XCB = sb("L_XCB", [128, L], BF16)
    HF = sb("L_HF", [128, L], F32)
    xin = sb("L_xin", [128, TL + 3], F32)
    rt_ = [sb("L_r%d" % i, [128, TL], F32) for i in range(2)]
    itl_ = [sb("L_i%d" % i, [128, TL], F32) for i in range(2)]
    at_ = [sb("L_a%d" % i, [128, TL], F32) for i in range(2)]
    t2_ = [sb("L_t2%d" % i, [128, TL], F32) for i in range(2)]
    gt = sb("L_g", [128, TL], F32)
    yb = sb("L_y", [128, TL], BF16)
    carry = sb("L_carry", [128, 1], F32)
    for c in range(4):
        prow = PF_LRU_X + c * 128
        for it in range(ntile):
            t0 = it * TL
            lo = max(t0 - 2, 0)
            hi = min(t0 + TL + 1, L)
            if it == 0 or it == ntile - 1:
                pg.op("pool", lambda e: e.memset(xin.ap, 0.0), writes=[xin])
            pg.dma(xin.ap[:, lo - (t0 - 2):hi - (t0 - 2)], cx.PF[prow:prow + 128, lo:hi], writes=[xin])
            xo = XC.ap[:, t0:t0 + TL]
            pg.op("dve", lambda e: e.tensor_scalar(out=xo, in0=xin.ap[:, 0:TL], scalar1=cw.ap[:, 0, c:c + 1], scalar2=cb.ap[:, c:c + 1], op0=ALU.mult, op1=ALU.add),
                  reads=[xin, cw, cb], writes=[XC])
            for j in range(1, 4):
                pg.op("dve", lambda e: e.scalar_tensor_tensor(out=xo, in0=xin.ap[:, j:j + TL], scalar=cw.ap[:, j, c:c + 1], in1=xo, op0=ALU.mult, op1=ALU.add),
                      reads=[xin, cw, XC], writes=[XC])
            pg.op("act", lambda e: e.copy(XCB.ap[:, t0:t0 + TL], xo), reads=[XC], writes=[XCB])
        for d in range(2):
            order = range(ntile) if d == 0 else range(ntile - 1, -1, -1)
            for n_i, it in enumerate(order):
                t0 = it * TL
                rt, itl, at, t2 = rt_[n_i % 2], itl_[n_i % 2], at_[n_i % 2], t2_[n_i % 2]
                for s0 in range(0, TL, 512):
                    for ai, dst in ((0, rt), (1, itl)):
                        ps = cx.psf.get()
                        pg.op("pe", lambda e: e.matmul(ps.ap, lhsT=wg.ap[:, ai, d, c, :], rhs=XCB.ap[:, t0 + s0:t0 + s0 + 512], start=True, stop=True),
                              reads=[wg, XCB], writes=[ps])
                        pg.op("act", lambda e: e.activation(out=dst.ap[:, s0:s0 + 512], in_=ps.ap, func=AF.Sigmoid, bias=bias.ap[:, ai, d, c:c + 1]),
                              reads=[ps, bias], writes=[dst])
                pg.op("act", lambda e: e.activation(out=at.ap, in_=rt.ap, func=AF.Exp, scale=coef.ap[:, d, c:c + 1]), reads=[rt, coef], writes=[at])
                pg.op("act", lambda e: e.activation(out=t2.ap, in_=rt.ap, func=AF.Exp, scale=coef2.ap[:, d, c:c + 1]), reads=[rt, coef2], writes=[t2])
                pg.op("act", lambda e: e.activation(out=t2.ap, in_=t2.ap, func=AF.Sqrt, scale=-1.0, bias=cx.one.ap[:, 0:1]), reads=[t2, cx.one], writes=[t2])
                pg.op("pool", lambda e: e.tensor_tensor(out=itl.ap, in0=itl.ap, in1=XC.ap[:, t0:t0 + TL], op=ALU.mult), reads=[itl, XC], writes=[itl])
                pg.op("dve", lambda e: e.tensor_tensor(out=t2.ap, in0=t2.ap, in1=itl.ap, op=ALU.mult), reads=[t2, itl], writes=[t2])
                init = 0.0 if n_i == 0 else carry.ap[:, 0:1]
                rds = [at, t2] + ([] if n_i == 0 else [carry])
                if d == 0:
                    ho = HF.ap[:, t0:t0 + TL]
                    pg.op("dve", lambda e: e.tensor_tensor_scan(out=ho, data0=at.ap, data1=t2.ap, initial=init, op0=ALU.mult, op1=ALU.add),
                          reads=rds, writes=[HF])
                    pg.op("dve", lambda e: e.tensor_copy(out=carry.ap, in_=HF.ap[:, t0 + TL - 1:t0 + TL]), reads=[HF], writes=[carry])
                else:
                    rv = lambda ap: bass.AP(ap.tensor, ap.offset + TL - 1, [list(ap.ap[0]), [-1, TL]])
                    pg.op("dve", lambda e: e.tensor_tensor_scan(out=rv(rt.ap), data0=rv(at.ap), data1=rv(t2.ap), initial=init, op0=ALU.mult, op1=ALU.add),
                          reads=rds, writes=[rt])
                    pg.op("dve", lambda e: e.tensor_copy(out=carry.ap, in_=rt.ap[:, 0:1]), reads=[rt], writes=[carry])
                    grow = PF_LRU_G + c * 128
                    pg.dma(gt.ap, cx.PF[grow:grow + 128, t0:t0 + TL], writes=[gt])
                    pg.op("act", lambda e: e.activation(out=gt.ap, in_=gt.ap, func=AF.Silu), reads=[gt], writes=[gt])
                    pg.op("pool", lambda e: e.tensor_tensor(out=rt.ap, in0=rt.ap, in1=HF.ap[:, t0:t0 + TL], op=ALU.add), reads=[rt, HF], writes=[rt])
                    pg.op("dve", lambda e: e.tensor_tensor(out=yb.ap, in0=rt.ap, in1=gt.ap, op=ALU.mult), reads=[rt, gt], writes=[yb])
                    pg.dma(cx.BT[c * 128:(c + 1) * 128, t0:t0 + TL], yb.ap, reads=[yb])


def phase_M(pg, cx, es, L, x_ap, xout_ap, l, last):
    nc = cx.nc
    TT = 512
    sb = lambda name, shape, dt: pg.buf(es.enter_context(nc.sbuf_tensor(pg.uname(name), shape, dt)).ap(), name)
    w = cx.w
    wmg = sb("M_wmg", [128, 4, 8, D], BF16)
    wbr = sb("M_wbr", [128, 4, 4, D], BF16)
    wout = sb("M_wout", [128, 8, D], BF16)
    st = [sb("M_st%d" % i, [128, D], F32) for i in range(2)]
    jobs = []
    for n in range(4):
        for k in range(8):
            jobs.append((w["w_merge_gate"][l, n, k * 128:(k + 1) * 128, :], wmg, wmg.ap[:, n, k, :]))
        for k in range(4):
            jobs.append((w["w_branch"][l, n, k * 128:(k + 1) * 128, :], wbr, wbr.ap[:, n, k, :]))
    for k in range(8):
        jobs.append((w["w_out"][l, k * 128:(k + 1) * 128, :], wout, wout.ap[:, k, :]))
    for i, (src, dbuf, dap) in enumerate(jobs):
        s = st[i % 2]
        pg.dma(s.ap, src, writes=[s])
        if i % 2 == 0:
            pg.op("act", lambda e: e.copy(dap, s.ap), reads=[s], writes=[dbuf])
        else:
            pg.op("dve", lambda e: e.tensor_copy(out=dap, in_=s.ap), reads=[s], writes=[dbuf])
    bmg = sb("M_bmg", [128, 4, 8], F32)
    load_T(pg, cx, bmg, bmg.ap.rearrange("p n c -> p (n c)"), w["b_merge_gate"][l].rearrange("n (c p) -> (n c) p", p=128), 32)
    if last:
        fg = sb("M_fg", [128, D], F32)
        fsrc = w["final_norm_g"]
        pg.dma(fg.ap, bass.AP(fsrc.tensor, fsrc.offset, [[0, 128], [1, D]]), writes=[fg])
        ss = sb("M_ss", [128, 4], F32)
        junk = sb("M_junk", [128, D], BF16)
    xn_ = [sb("M_xn%d" % i, [128, 8, TT], BF16) for i in range(2)]
    bt = sb("M_bt", [128, 16, TT], BF16)
    xt = sb("M_x", [128, 4, D], F32)
    mg = sb("M_mg", [128, 8, TT], BF16)
    gsb_ = [sb("M_g%d" % i, [128, TT], F32) for i in range(2)]
    tmp_ = [sb("M_tmp%d" % i, [128, TT], F32) for i in range(2)]
    acc = sb("M_acc", [128, TT], F32)
    XNTv = cx.XNT.rearrange("(k p) t -> p k t", p=128)
    BTv = cx.BT.rearrange("(k p) t -> p k t", p=128)
    xv = x_ap.rearrange("(n j p) d -> n p j d", p=128, j=4)
    ov = xout_ap.rearrange("(n j p) d -> n p j d", p=128, j=4)
    for it in range(L // TT):
        ts = slice(it * TT, (it + 1) * TT)
        xn = xn_[it % 2]
        pg.dma(xn.ap, XNTv[:, :, ts], writes=[xn])
        pg.dma(bt.ap, BTv[:, :, ts], writes=[bt])
        pg.dma(xt.ap, xv[it], writes=[xt])
        for oc in range(8):
            ocs = slice(oc * 128, (oc + 1) * 128)
            for n in range(4):
                gsb = gsb_[n % 2]; tmp = tmp_[n % 2]
                pg_ = cx.psf.get()
                for k in range(8):
                    pg.op("pe", lambda e: e.matmul(pg_.ap, lhsT=wmg.ap[:, n, k, ocs], rhs=xn.ap[:, k, :], start=(k == 0), stop=(k == 7)),
                          reads=[wmg, xn], writes=[pg_])
                pg.op("act", lambda e: e.activation(out=gsb.ap, in_=pg_.ap, func=AF.Sigmoid, bias=bmg.ap[:, n, oc:oc + 1]), reads=[pg_, bmg], writes=[gsb])
                pb = cx.psf.get()
                for k in range(4):
                    pg.op("pe", lambda e: e.matmul(pb.ap, lhsT=wbr.ap[:, n, k, ocs], rhs=bt.ap[:, n * 4 + k, :], start=(k == 0), stop=(k == 3)),
                          reads=[wbr, bt], writes=[pb])
                if n == 0:
                    pg.op("dve", lambda e: e.tensor_tensor(out=acc.ap, in0=pb.ap, in1=gsb.ap, op=ALU.mult), reads=[pb, gsb], writes=[acc])
                else:
                    pg.op("dve", lambda e: e.tensor_tensor(out=tmp.ap, in0=pb.ap, in1=gsb.ap, op=ALU.mult), reads=[pb, gsb], writes=[tmp])
                    if n < 3:
                        pg.op("pool", lambda e: e.tensor_tensor(out=acc.ap, in0=acc.ap, in1=tmp.ap, op=ALU.add), reads=[acc, tmp], writes=[acc])
                    else:
                        pg.op("pool", lambda e: e.tensor_tensor(out=mg.ap[:, oc, :], in0=acc.ap, in1=tmp.ap, op=ALU.add), reads=[acc, tmp], writes=[mg])
        for j in range(4):
            for hf in range(2):
                hs = slice(hf * 512, (hf + 1) * 512)
                ps = cx.psf.get()
                for k in range(8):
                    pg.op("pe", lambda e: e.matmul(ps.ap, lhsT=mg.ap[:, k, j * 128:(j + 1) * 128], rhs=wout.ap[:, k, hs], start=(k == 0), stop=(k == 7)),
                          reads=[mg, wout], writes=[ps])
                pg.op("dve", lambda e: e.tensor_tensor(out=xt.ap[:, j, hs], in0=ps.ap, in1=xt.ap[:, j, hs], op=ALU.add), reads=[ps, xt], writes=[xt])
            if last:
                pg.op("act", lambda e: e.activation(out=junk.ap, in_=xt.ap[:, j, :], func=AF.Square, accum_out=ss.ap[:, j:j + 1]), reads=[xt], writes=[junk, ss])
                pg.op("act", lambda e: e.activation(out=ss.ap[:, j:j + 1], in_=ss.ap[:, j:j + 1], func=AF.Sqrt, scale=1.0 / D, bias=cx.eps.ap[:, 0:1]),
                      reads=[ss, cx.eps], writes=[ss])
                pg.op("dve", lambda e: e.reciprocal(out=ss.ap[:, j:j + 1], in_=ss.ap[:, j:j + 1]), reads=[ss], writes=[ss])
                pg.op("dve", lambda e: e.scalar_tensor_tensor(out=xt.ap[:, j, :], in0=xt.ap[:, j, :], scalar=ss.ap[:, j:j + 1], in1=fg.ap, op0=ALU.mult, op1=ALU.mult),
                      reads=[xt, ss, fg], writes=[xt])
        pg.dma(ov[it], xt.ap, reads=[xt])


def phase_GLA(pg, cx, es, L, l):
    nc = cx.nc
    sb = lambda name, shape, dt: pg.buf(es.enter_context(nc.sbuf_tensor(pg.uname(name), shape, dt)).ap(), name)
    w = cx.w
    NB = L // 128
    wup = sb("G_wup", [32, 2, 256], F32)
    for d in range(2):
        pg.dma(wup.ap[0:16, d, :], w["gla_w_up"][l, d], writes=[wup])
        pg.dma(wup.ap[16:17, d, :], w["gla_b_up"][l, d:d + 1, :], writes=[wup])
    gn = sb("G_gn", [128, 128], F32)
    gsrc = w["gla_norm_g"][l]
    pg.dma(gn.ap, bass.AP(gsrc.tensor, gsrc.offset, [[0, 128], [1, 128]]), writes=[gn])
    lrT = [sb("G_lrT%d" % i, [32, 128], F32) for i in range(2)]
    for b in lrT:
        pg.op("dve", lambda e: e.memset(b.ap, 1.0), writes=[b])
    qk = [sb("G_qk%d" % i, [128, 4, 128], F32) for i in range(2)]
    tk = [sb("G_tk%d" % i, [128, 1280], F32) for i in range(3)]
    obt = [sb("G_ob%d" % i, [128, 512], F32) for i in range(3)]
    la_ = [sb("G_la%d" % i, [128, 256], F32) for i in range(2)]
    e1_ = [sb("G_e1%d" % i, [128, 256], F32) for i in range(2)]
    eb_ = [sb("G_eb%d" % i, [128, 2, 128], F32) for i in range(2)]
    enb_ = [sb("G_enb%d" % i, [128, 2, 128], F32) for i in range(2)]
    qd_ = [sb("G_qd%d" % i, [128, 2, 128], BF16) for i in range(2)]
    ki_ = [sb("G_ki%d" % i, [128, 2, 128], BF16) for i in range(2)]
    ed_ = [sb("G_ed%d" % i, [128, 256], F32) for i in range(2)]
    kend_ = [sb("G_kend%d" % i, [128, 256], BF16) for i in range(2)]
    vb_ = [sb("G_vb%d" % i, [128, 512], BF16) for i in range(2)]
    sm = [sb("G_sm%d" % i, [128, 128], BF16) for i in range(4)]
    pre_ps = Rot([cx.psf.items[4], cx.pso])
    S32 = [sb("G_S32_%d" % h, [128, 128], F32) for h in range(4)]
    Sb = [sb("G_Sb_%d" % h, [128, 128], BF16) for h in range(4)]
    osb_ = [sb("G_osb%d" % i, [128, 512], F32) for i in range(2)]
    pending = [None]
    ssq = sb("G_ssq", [128, 4], F32)
    junk = sb("G_junk", [128, 128], BF16)
    ysb = sb("G_ysb", [128, 512], BF16)
    yT = sb("G_yT", [128, 4, 128], BF16)
    PFq = cx.PF[PF_GLA_Q:PF_GLA_Q + 512, :].rearrange("(c p) t -> p c t", p=128)
    for d in (1, 0):
        pg.barrier()
        for h in range(4):
            pg.op("dve", lambda e: e.memset(S32[h].ap, 0.0), writes=[S32[h]])
            pg.op("pool", lambda e: e.memset(Sb[h].ap, 0.0), writes=[Sb[h]])
        order = range(NB) if d == 0 else range(NB - 1, -1, -1)
        def pre_gen(bi, blk, d=d):
            t0 = blk * 128
            ts = slice(t0, t0 + 128)
            qkb = qk[bi % 2]; tkb = tk[bi % 3]; lrb = lrT[bi % 2]; ob = obt[bi % 3]
            la, e1, eb, enb, qd, ki, ed, kend, vb = [x_[bi % 2] for x_ in (la_, e1_, eb_, enb_, qd_, ki_, ed_, kend_, vb_)]
            pg.dma(qkb.ap, PFq[:, :, ts], writes=[qkb])
            pg.dma(tkb.ap, cx.PT[ts, 0:1280], writes=[tkb])
            pg.dma(lrb.ap[0:16, :], cx.PF[PF_GLA_LR + 16 * d:PF_GLA_LR + 16 * d + 16, ts], writes=[lrb])
            if d == 0:
                pg.dma(ob.ap, cx.OB[ts, 0:512], writes=[ob])
            zp = pre_ps.get()
            pg.op("pe", lambda e: e.matmul(zp.ap[:, :256], lhsT=lrb.ap[0:17, :], rhs=wup.ap[0:17, d, :], start=True, stop=True), reads=[lrb, wup], writes=[zp])
            pg.op("act", lambda e: e.activation(out=e1.ap, in_=zp.ap[:, :256], func=AF.Exp, scale=-1.0), reads=[zp], writes=[e1])
            yield
            pg.op("act", lambda e: e.activation(out=e1.ap, in_=e1.ap, func=AF.Ln, bias=cx.one.ap[:, 0:1]), reads=[e1, cx.one], writes=[e1])
            yield
            pg.op("dve", lambda e: e.tensor_scalar(out=la.ap, in0=e1.ap, scalar1=-1.0 / 16.0, scalar2=None, op0=ALU.mult), reads=[e1], writes=[la])
            yield
            bp = pre_ps.get()
            for h2 in range(2):
                pg.op("pe", lambda e: e.matmul(bp.ap[:, h2 * 128:(h2 + 1) * 128], lhsT=la.ap[:, h2 * 128:(h2 + 1) * 128], rhs=cx.m_incl.ap[:, d, :], start=True, stop=True),
                      reads=[la, cx.m_incl], writes=[bp])
            bp3 = bp.ap[:, 0:256].rearrange("p (c t) -> p c t", c=2)
            pg.op("act", lambda e: e.activation(out=eb.ap, in_=bp3, func=AF.Exp), reads=[bp], writes=[eb])
            yield
            pg.op("act", lambda e: e.activation(out=enb.ap, in_=bp3, func=AF.Exp, scale=-1.0), reads=[bp], writes=[enb])
            yield
            pg.op("dve", lambda e: e.scalar_tensor_tensor(out=qd.ap, in0=qkb.ap[:, 0:2, :], scalar=0.125, in1=eb.ap, op0=ALU.mult, op1=ALU.mult), reads=[qkb, eb], writes=[qd])
            yield
            pg.op("pool", lambda e: e.tensor_tensor(out=ki.ap, in0=qkb.ap[:, 2:4, :], in1=enb.ap, op=ALU.mult), reads=[qkb, enb], writes=[ki])
            yield
            dp = pre_ps.get()
            pg.op("pe", lambda e: e.matmul(dp.ap[:, :256], lhsT=cx.m_sa.ap[:, d, :], rhs=la.ap, start=True, stop=True), reads=[la, cx.m_sa], writes=[dp])
            pg.op("act", lambda e: e.activation(out=ed.ap, in_=dp.ap[:, :256], func=AF.Exp), reads=[dp], writes=[ed])
            yield
            pg.op("dve", lambda e: e.tensor_tensor(out=kend.ap, in0=tkb.ap[:, 0:256], in1=ed.ap, op=ALU.mult), reads=[tkb, ed], writes=[kend])
            yield
            pg.op("pool", lambda e: e.tensor_copy(out=vb.ap, in_=tkb.ap[:, 256:768]), reads=[tkb], writes=[vb])
            yield

        order_l = list(order)
        for _ in pre_gen(0, order_l[0]):
            pass
        for bi, blk in enumerate(order_l):
            t0 = blk * 128
            ts = slice(t0, t0 + 128)
            osb = osb_[bi % 2]
            qkb = qk[bi % 2]; tkb = tk[bi % 3]; lrb = lrT[bi % 2]; ob = obt[bi % 3]
            la, e1, eb, enb, qd, ki, ed, kend, vb = [x_[bi % 2] for x_ in (la_, e1_, eb_, enb_, qd_, ki_, ed_, kend_, vb_)]
            chunks = (0, 1) if d == 0 else (1, 0)

            def head_gen(h, d=d, chunks=chunks):
                h2, hp = h // 2, (h % 2) * 64
                hc = slice(h * 128, (h + 1) * 128)
                bank = cx.psf.items[h]
                o_ps = bank.ap[:, 384:512]
                pg.op("pe", lambda e: e.matmul(bank.ap[:, 0:128], lhsT=ki.ap[hp:hp + 64, h2, :], rhs=qd.ap[hp:hp + 64, h2, :], start=True, stop=True), reads=[ki, qd], writes=[bank])
                yield
                smb = sm[h]
                pg.op("dve", lambda e: e.tensor_tensor(out=smb.ap, in0=bank.ap[:, 0:128], in1=cx.m_incl.ap[:, d, :], op=ALU.mult), reads=[bank, cx.m_incl], writes=[smb])
                yield
                r0 = chunks[0] * 64
                pg.op("pe", lambda e: e.matmul(o_ps, lhsT=smb.ap, rhs=vb.ap[:, hc], start=True, stop=False), reads=[smb, vb], writes=[bank])
                pg.op("pe", lambda e: e.matmul(bank.ap[r0:r0 + 64, 384:512], lhsT=qd.ap[hp:hp + 64, h2, r0:r0 + 64], rhs=Sb[h].ap[hp:hp + 64, :], start=False, stop=True),
                      reads=[qd, Sb[h]], writes=[bank])
                for ci, c in enumerate(chunks):
                    r0 = c * 64
                    if ci == 1:
                        pg.op("pe", lambda e: e.matmul(bank.ap[r0:r0 + 64, 256:384], lhsT=qd.ap[hp:hp + 64, h2, r0:r0 + 64], rhs=Sb[h].ap[hp:hp + 64, :], start=True, stop=True),
                              reads=[qd, Sb[h]], writes=[bank])
                    pg.op("pe", lambda e: e.matmul(bank.ap[hp:hp + 64, 128:256], lhsT=kend.ap[r0:r0 + 64, h * 64:(h + 1) * 64], rhs=vb.ap[r0:r0 + 64, hc], start=True, stop=True),
                          reads=[kend, vb], writes=[bank])
                    yield
                    col = r0 + 63 if d == 0 else r0
                    pg.op("dve", lambda e: e.scalar_tensor_tensor(out=S32[h].ap[hp:hp + 64, :], in0=S32[h].ap[hp:hp + 64, :], scalar=eb.ap[hp:hp + 64, h2, col:col + 1],
                                                                  in1=bank.ap[hp:hp + 64, 128:256], op0=ALU.mult, op1=ALU.add), reads=[S32[h], eb, bank], writes=[S32[h]])
                    yield
                    pg.op("act", lambda e: e.copy(Sb[h].ap[hp:hp + 64, :], S32[h].ap[hp:hp + 64, :]), reads=[S32[h]], writes=[Sb[h]])
                    yield
                r1 = chunks[1] * 64
                pg.op("dve", lambda e: e.tensor_copy(out=osb.ap[:, hc], in_=o_ps), reads=[bank], writes=[osb])
                pg.op("dve", lambda e: e.tensor_tensor(out=osb.ap[r1:r1 + 64, hc], in0=bank.ap[r1:r1 + 64, 256:384], in1=osb.ap[r1:r1 + 64, hc], op=ALU.add), reads=[bank, osb], writes=[osb])

            gens = [head_gen(h) for h in range(4)] + ([pre_gen(bi + 1, order_l[bi + 1])] if bi + 1 < NB else [])
            if pending[0] is not None:
                gens.append(pending[0])
                pending[0] = None
            while gens:
                for gnr in list(gens):
                    try:
                        next(gnr)
                    except StopIteration:
                        gens.remove(gnr)
            if d == 1:
                pg.dma(cx.OB[ts, 0:512], osb.ap, reads=[osb])
            else:
                def tail_gen(osb=osb, ob=ob, tkb=tkb, ts=ts):
                    pg.op("pool", lambda e: e.tensor_tensor(out=osb.ap, in0=osb.ap, in1=ob.ap, op=ALU.add), reads=[osb, ob], writes=[osb])
                    yield
                    yield from hngs_gen(pg, cx, osb, ssq, junk, gn, tkb, 768, ysb, yT, 512, ts)
                pending[0] = tail_gen()
        if pending[0] is not None:
            for _ in pending[0]:
                pass
            pending[0] = None


def hngs_gen(pg, cx, osb, ssq, junk, gn, tkb, gcol, ysb, yT, bt_row0, ts):
    for h in range(4):
        hc = slice(h * 128, (h + 1) * 128)
        pg.op("act", lambda e: e.activation(out=junk.ap, in_=osb.ap[:, hc], func=AF.Square, accum_out=ssq.ap[:, h:h + 1]), reads=[osb], writes=[junk, ssq])
        yield
    pg.op("act", lambda e: e.activation(out=ssq.ap, in_=ssq.ap, func=AF.Sqrt, scale=1.0 / 128.0, bias=cx.eps.ap[:, 0:1]), reads=[ssq, cx.eps], writes=[ssq])
    yield
    pg.op("dve", lambda e: e.reciprocal(out=ssq.ap, in_=ssq.ap), reads=[ssq], writes=[ssq])
    yield
    for h in range(4):
        hc = slice(h * 128, (h + 1) * 128)
        pg.op("dve", lambda e: e.scalar_tensor_tensor(out=osb.ap[:, hc], in0=osb.ap[:, hc], scalar=ssq.ap[:, h:h + 1], in1=gn.ap, op0=ALU.mult, op1=ALU.mult),
              reads=[osb, ssq, gn], writes=[osb])
        yield
    pg.op("act", lambda e: e.activation(out=tkb.ap[:, gcol:gcol + 512], in_=tkb.ap[:, gcol:gcol + 512], func=AF.Silu), reads=[tkb], writes=[tkb])
    yield
    pg.op("dve", lambda e: e.tensor_tensor(out=ysb.ap, in0=osb.ap, in1=tkb.ap[:, gcol:gcol + 512], op=ALU.mult), reads=[osb, tkb], writes=[ysb])
    yield
    pb = cx.psb.get()
    for h in range(4):
        pg.op("pe", lambda e: e.transpose(out=pb.ap[:, h * 128:(h + 1) * 128], in_=ysb.ap[:, h * 128:(h + 1) * 128], identity=cx.identb.ap), reads=[ysb, cx.identb], writes=[pb])
    pg.op("act", lambda e: e.copy(yT.ap, pb.ap[:, 0:512].rearrange("p (c t) -> p c t", c=4)), reads=[pb], writes=[yT])
    yield
    pg.dma(cx.BT[bt_row0:bt_row0 + 512, ts].rearrange("(c p) t -> p c t", p=128), yT.ap, reads=[yT])

def head_norm_gate_store(*args):
    for _ in hngs_gen(*args):
        pass


def phase_DN(pg, cx, es, L, l):
    nc = cx.nc
    w = cx.w
    NB = L // 128
    with ExitStack() as es0:
        sb = lambda name, shape, dt: pg.buf(es0.enter_context(nc.sbuf_tensor(pg.uname(name), shape, dt)).ap(), name)
        TL = 512
        cwD = sb("D0_cw", [128, 4, 12], F32)
        load_T(pg, cx, cwD, cwD.ap.rearrange("p j c -> p (j c)"), w["dn_conv_w"][l].rearrange("j (c p) -> (j c) p", p=128), 48)
        xin = [sb("D0_xin%d" % i, [128, TL + 3], F32) for i in range(2)]
        xc = sb("D0_xc", [128, TL], F32)
        sq = sb("D0_sq", [128, TL], F32)
        rs = sb("D0_rs", [128, TL], F32)
        fm = sb("D0_fm", [128, 12, TL], BF16)
        tm = sb("D0_tm", [128, 4, 1024], BF16)
        nt = L // TL
        for it in range(nt):
            t0 = it * TL
            lo, hi = max(t0 - 2, 0), min(t0 + TL + 1, L)
            for c in range(12):
                xb = xin[c % 2]
                if it == 0 or it == nt - 1:
                    pg.op("pool", lambda e: e.memset(xb.ap, 0.0), writes=[xb])
                prow = PF_DN_QKV + c * 128
                pg.dma(xb.ap[:, lo - (t0 - 2):hi - (t0 - 2)], cx.PF[prow:prow + 128, lo:hi], writes=[xb])
                pg.op("dve", lambda e: e.tensor_scalar(out=xc.ap, in0=xb.ap[:, 0:TL], scalar1=cwD.ap[:, 0, c:c + 1], scalar2=None, op0=ALU.mult), reads=[xb, cwD], writes=[xc])
                for j in range(1, 4):
                    pg.op("dve", lambda e: e.scalar_tensor_tensor(out=xc.ap, in0=xb.ap[:, j:j + TL], scalar=cwD.ap[:, j, c:c + 1], in1=xc.ap, op0=ALU.mult, op1=ALU.add),
                          reads=[xb, cwD, xc], writes=[xc])
                if c >= 8:
                    pg.op("act", lambda e: e.activation(out=fm.ap[:, c, :], in_=xc.ap, func=AF.Silu), reads=[xc], writes=[fm])
                else:
                    pg.op("act", lambda e: e.activation(out=xc.ap, in_=xc.ap, func=AF.Silu), reads=[xc], writes=[xc])
                    pg.op("pool", lambda e: e.tensor_tensor(out=sq.ap, in0=xc.ap, in1=xc.ap, op=ALU.mult), reads=[xc], writes=[sq])
                    ps = cx.psf.get()
                    pg.op("pe", lambda e: e.matmul(ps.ap, lhsT=cx.onesf.ap, rhs=sq.ap, start=True, stop=True), reads=[cx.onesf, sq], writes=[ps])
                    pg.op("act", lambda e: e.activation(out=rs.ap, in_=ps.ap, func=AF.Sqrt, bias=cx.eps.ap[:, 0:1]), reads=[ps, cx.eps], writes=[rs])
                    pg.op("dve", lambda e: e.reciprocal(out=rs.ap, in_=rs.ap), reads=[rs], writes=[rs])
                    sc = (128.0 ** -0.5) if c < 4 else 1.0
                    pg.op("dve", lambda e: e.scalar_tensor_tensor(out=fm.ap[:, c, :], in0=xc.ap, scalar=sc, in1=rs.ap, op0=ALU.mult, op1=ALU.mult), reads=[xc, rs], writes=[fm])
            pg.dma(cx.QKT[:, t0:t0 + TL].rearrange("(c p) t -> p c t", p=128), fm.ap[:, 0:8, :], reads=[fm])
            for j in range(4):
                pb = cx.psb.get()
                for c in range(8):
                    pg.op("pe", lambda e: e.transpose(out=pb.ap[:, c * 128:(c + 1) * 128], in_=fm.ap[:, 4 + c, j * 128:(j + 1) * 128], identity=cx.identb.ap),
                          reads=[fm, cx.identb], writes=[pb])
                pg.op("act", lambda e: e.copy(tm.ap[:, j, :], pb.ap), reads=[pb], writes=[tm])
            pg.dma(cx.KVT[t0:t0 + TL, :].rearrange("(j p) c -> p j c", p=128), tm.ap, reads=[tm])
        pg.barrier()
    sb = lambda name, shape, dt: pg.buf(es.enter_context(nc.sbuf_tensor(pg.uname(name), shape, dt)).ap(), name)
    ba = sb("D_ba", [128, NB, 16], F32)
    pg.dma(ba.ap, cx.PT[:, PT_DN_BA:PT_DN_BA + 16].rearrange("(n p) c -> p n c", p=128), writes=[ba])
    ba4 = ba.ap.rearrange("p n (d j h) -> p n d j h", d=2, j=2)
    dtb = sb("D_dtb", [128, 8], F32)
    nea = sb("D_nea", [128, 8], F32)
    s1 = w["dn_dt_bias"][l]
    pg.dma(dtb.ap, bass.AP(s1.tensor, s1.offset, [[0, 128], [1, 8]]), writes=[dtb])
    s2 = w["dn_a_log"][l]
    pg.dma(nea.ap, bass.AP(s2.tensor, s2.offset, [[0, 128], [1, 8]]), writes=[nea])
    pg.op("act", lambda e: e.activation(out=nea.ap, in_=nea.ap, func=AF.Exp), reads=[nea], writes=[nea])
    pg.op("dve", lambda e: e.tensor_scalar(out=nea.ap, in0=nea.ap, scalar1=-1.0, scalar2=None, op0=ALU.mult), reads=[nea], writes=[nea])
    beta = sb("D_beta", [128, NB, 2, 4], F32)
    nbeta = sb("D_nbeta", [128, NB, 2, 4], F32)
    g = sb("D_g", [128, NB, 2, 4], F32)
    pg.op("act", lambda e: e.activation(out=beta.ap, in_=ba4[:, :, :, 0, :], func=AF.Sigmoid), reads=[ba], writes=[beta])
    pg.op("dve", lambda e: e.tensor_scalar(out=nbeta.ap, in0=beta.ap, scalar1=-1.0, scalar2=None, op0=ALU.mult), reads=[beta], writes=[nbeta])
    dtb_b = dtb.ap.rearrange("p (d h) -> p d h", d=2).unsqueeze(1).to_broadcast([128, NB, 2, 4])
    nea_b = nea.ap.rearrange("p (d h) -> p d h", d=2).unsqueeze(1).to_broadcast([128, NB, 2, 4])
    pg.op("dve", lambda e: e.tensor_tensor(out=g.ap, in0=ba4[:, :, :, 1, :], in1=dtb_b, op=ALU.add), reads=[ba, dtb], writes=[g])
    pg.op("act", lambda e: e.activation(out=g.ap, in_=g.ap, func=AF.Exp), reads=[g], writes=[g])
    pg.op("act", lambda e: e.activation(out=g.ap, in_=g.ap, func=AF.Ln, bias=cx.one.ap[:, 0:1]), reads=[g, cx.one], writes=[g])
    pg.op("dve", lambda e: e.tensor_tensor(out=g.ap, in0=g.ap, in1=nea_b, op=ALU.mult), reads=[g, nea], writes=[g])
    eG = sb("D_eG", [128, NB, 2, 4], F32)
    eD = sb("D_eD", [128, NB, 2, 4], F32)
    bg = sb("D_bg", [128, NB, 2, 4], F32)
    deB = sb("D_deB", [128, 2, NB, 2, 4], F32)
    NQ = 32
    for d in range(2):
        for n0 in range(0, NB, NQ):
            nn = min(NQ, NB - n0)
            for (msk, dst, fn) in ((cx.m_incl.ap[:, d, :], eG, 0), (cx.m_sa.ap[:, d, :], eD, 0), (cx.chunkind.ap[:, 0, :], deB, 1), (cx.chunkind.ap[:, 1, :], deB, 2)):
                ps = cx.psf.get()
                pv = ps.ap[:, :nn * 4].rearrange("p (n h) -> p n h", h=4)
                pg.op("pe", lambda e: e.matmul(pv, lhsT=msk, rhs=g.ap[:, n0:n0 + nn, d, :], start=True, stop=True), reads=[g, cx.m_incl, cx.m_sa, cx.chunkind], writes=[ps])
                o_ap = dst.ap[:, n0:n0 + nn, d, :] if fn == 0 else dst.ap[:, fn - 1, n0:n0 + nn, d, :]
                pg.op("act", lambda e: e.activation(out=o_ap, in_=pv, func=AF.Exp), reads=[ps], writes=[dst])
    pg.op("dve", lambda e: e.tensor_tensor(out=bg.ap, in0=beta.ap, in1=eG.ap, op=ALU.mult), reads=[beta, eG], writes=[bg])
    gn = sb("D_gn", [128, 128], F32)
    gsrc = w["dn_norm_g"][l]
    pg.dma(gn.ap, bass.AP(gsrc.tensor, gsrc.offset, [[0, 128], [1, 128]]), writes=[gn])
    qk = [sb("D_qk%d" % i, [128, 8, 128], BF16) for i in range(2)]
    kv = [sb("D_kv%d" % i, [128, 2, 4, 128], BF16) for i in range(2)]
    gt = [sb("D_gt%d" % i, [128, 512], F32) for i in range(2)]
    obt = [sb("D_ob%d" % i, [128, 512], F32) for i in range(2)]
    vb4 = sb("D_vb4", [128, 4, 128], BF16)
    kbg4 = sb("D_kbg4", [128, 4, 128], BF16)
    kend4 = sb("D_kend4", [128, 4, 128], BF16)
    gtri = [sb("D_gtri%d" % i, [128, 128], F32) for i in range(4)]
    gam = [sb("D_gam%d" % i, [128, 3, 128], F32) for i in range(4)]
    gamm = [sb("D_gamm%d" % i, [128, 2, 128], F32) for i in range(4)]
    qd = [sb("D_qd%d" % i, [128, 128], BF16) for i in range(4)]
    Cm = [sb("D_C%d" % i, [128, 128], BF16) for i in range(4)]
    attnT = [sb("D_at%d" % i, [128, 128], BF16) for i in range(4)]
    BC = [[sb("D_BC%d_%d" % (h, i), [128, 2, 128], BF16) for i in range(2)] for h in range(4)]
    Pm = [[sb("D_P%d_%d" % (h, i), [128, 128], BF16) for i in range(2)] for h in range(4)]
    Pm32 = [[sb("D_P32_%d_%d" % (h, i), [128, 128], F32) for i in range(2)] for h in range(4)]
    usb = [sb("D_u%d" % i, [128, 128], F32) for i in range(4)]
    wT = [sb("D_wT%d" % i, [128, 128], BF16) for i in range(4)]
    vn = [sb("D_vn%d" % i, [128, 128], BF16) for i in range(4)]
    S32 = [sb("D_S32_%d" % h, [128, 128], F32) for h in range(4)]
    Sb = [sb("D_Sb_%d" % h, [128, 128], BF16) for h in range(4)]
    osb_ = [sb("D_osb%d" % i, [128, 512], F32) for i in range(2)]
    pending = [None]
    ssq = sb("D_ssq", [128, 4], F32)
    junk = sb("D_junk", [128, 128], BF16)
    ysb = sb("D_ysb", [128, 512], BF16)
    yT = sb("D_yT", [128, 4, 128], BF16)
    QKv = cx.QKT.rearrange("(c p) t -> p c t", p=128)
    rr = [0]
    for d in (1, 0):
        pg.barrier()
        for h in range(4):
            pg.op("dve", lambda e: e.memset(S32[h].ap, 0.0), writes=[S32[h]])
            pg.op("pool", lambda e: e.memset(Sb[h].ap, 0.0), writes=[Sb[h]])
        order = range(NB) if d == 0 else range(NB - 1, -1, -1)
        chunks = (0, 1) if d == 0 else (1, 0)
        for bi, blk in enumerate(order):
            t0 = blk * 128
            ts = slice(t0, t0 + 128)
            qkb = qk[bi % 2]; kvb = kv[bi % 2]; ob = obt[bi % 2]; gtb = gt[bi % 2]; osb = osb_[bi % 2]
            pg.dma(qkb.ap, QKv[:, :, ts], writes=[qkb])
            pg.dma(kvb.ap.rearrange("p a h c -> p (a h c)"), cx.KVT[ts, :], writes=[kvb])
            if d == 0:
                pg.dma(ob.ap, cx.OB[ts, 512:1024], writes=[ob])
                pg.dma(gtb.ap[:, 0:512], cx.PT[ts, PT_DN_G:PT_DN_G + 512], writes=[gtb])
            bcast = lambda t: t.ap[:, blk, d, :].unsqueeze(2).to_broadcast([128, 4, 128])
            pg.op("dve", lambda e: e.tensor_tensor(out=vb4.ap, in0=kvb.ap[:, 1], in1=bcast(beta), op=ALU.mult), reads=[kvb, beta], writes=[vb4])
            pg.op("pool", lambda e: e.tensor_tensor(out=kbg4.ap, in0=kvb.ap[:, 0], in1=bcast(bg), op=ALU.mult), reads=[kvb, bg], writes=[kbg4])
            pg.op("pool", lambda e: e.tensor_tensor(out=kend4.ap, in0=kvb.ap[:, 0], in1=bcast(eD), op=ALU.mult), reads=[kvb, eD], writes=[kend4])
            op_ = cx.pso
            def head_gen(h, blk=blk, d=d, qkb=qkb, chunks=chunks, op_=op_):
                i2 = h
                bank = cx.psf.items[h]
                hc = slice(h * 128, (h + 1) * 128)
                gsc = g.ap[:, blk, d, h:h + 1]
                pg.op("dve", lambda e: e.tensor_scalar(out=gtri[i2].ap, in0=cx.m_incl.ap[:, d, :], scalar1=gsc, scalar2=None, op0=ALU.mult), reads=[cx.m_incl, g], writes=[gtri[i2]])
                yield
                dps = bank
                pg.op("pe", lambda e: e.matmul(dps.ap[:, 0:128], lhsT=gtri[i2].ap, rhs=cx.m_sa.ap[:, d, :], start=True, stop=True), reads=[gtri[i2], cx.m_sa], writes=[dps])
                pg.op("pe", lambda e: e.matmul(dps.ap[:, 128:256], lhsT=cx.m_sa.ap[:, d, :], rhs=gtri[i2].ap, start=True, stop=True), reads=[gtri[i2], cx.m_sa], writes=[dps])
                pg.op("pe", lambda e: e.matmul(dps.ap[:, 256:384], lhsT=cx.onesf.ap, rhs=gtri[i2].ap, start=True, stop=True), reads=[gtri[i2], cx.onesf], writes=[dps])
                yield
                pg.op("act", lambda e: e.activation(out=gam[i2].ap.rearrange("p a t -> p (a t)"), in_=dps.ap[:, 0:384], func=AF.Exp), reads=[dps], writes=[gam[i2]])
                yield
                pg.op("pool", lambda e: e.tensor_tensor(out=gamm[i2].ap, in0=gam[i2].ap[:, 0:2, :], in1=cx.m_dn.ap[:, d], op=ALU.mult), reads=[gam[i2], cx.m_dn], writes=[gamm[i2]])
                pg.op("pool", lambda e: e.tensor_tensor(out=qd[i2].ap, in0=qkb.ap[:, h, :], in1=gam[i2].ap[:, 2, :], op=ALU.mult), reads=[qkb, gam[i2]], writes=[qd[i2]])
                kps = bank
                pg.op("pe", lambda e: e.matmul(kps.ap[:, 0:128], lhsT=qkb.ap[:, 4 + h, :], rhs=qkb.ap[:, 4 + h, :], start=True, stop=True), reads=[qkb], writes=[kps])
                pg.op("pe", lambda e: e.matmul(kps.ap[:, 128:256], lhsT=qkb.ap[:, 4 + h, :], rhs=qkb.ap[:, h, :], start=True, stop=True), reads=[qkb], writes=[kps])
                yield
                pg.op("dve", lambda e: e.scalar_tensor_tensor(out=Cm[i2].ap, in0=kps.ap[:, 0:128], scalar=nbeta.ap[:, blk, d, h:h + 1], in1=gamm[i2].ap[:, 0, :], op0=ALU.mult, op1=ALU.mult),
                      reads=[kps, nbeta, gamm[i2]], writes=[Cm[i2]])
                pg.op("dve", lambda e: e.tensor_tensor(out=attnT[i2].ap, in0=kps.ap[:, 128:256], in1=gamm[i2].ap[:, 1, :], op=ALU.mult), reads=[kps, gamm[i2]], writes=[attnT[i2]])
                yield
                tb = bank
                tbv = bank.ap.bitcast(BF16)
                pg.op("pe", lambda e: e.transpose(out=tbv[:, 0:128], in_=Cm[i2].ap, identity=cx.identb.ap), reads=[Cm[i2], cx.identb], writes=[tb])
                yield
                hr = [0]
                bc0 = BC[h][hr[0] % 2]
                pg.op("act", lambda e: e.copy(bc0.ap[:, 0, :], tbv[:, 0:128]), reads=[tb], writes=[bc0])
                pg.op("pool", lambda e: e.tensor_copy(out=bc0.ap[:, 1, :], in_=Cm[i2].ap), reads=[Cm[i2]], writes=[bc0])
                p0 = Pm[h][hr[0] % 2]; p032 = Pm32[h][hr[0] % 2]; hr[0] += 1
                pg.op("pool", lambda e: e.tensor_tensor(out=p0.ap, in0=bc0.ap[:, 0, :], in1=cx.identb.ap, op=ALU.add), reads=[bc0, cx.identb], writes=[p0])
                yield
                bcp, pp, pp32 = bc0, p0, p032
                for k in range(1, 6):
                    sq_ = bank
                    if k < 5:
                        pg.op("pe", lambda e: e.matmul(sq_.ap[:, 0:128], lhsT=bcp.ap[:, 1, :], rhs=bcp.ap[:, 0, :], start=True, stop=True), reads=[bcp], writes=[sq_])
                    pg.op("pe", lambda e: e.matmul(sq_.ap[:, 128:256], lhsT=bcp.ap[:, 0, :], rhs=bcp.ap[:, 1, :], start=True, stop=True), reads=[bcp], writes=[sq_])
                    yield
                    bcn = BC[h][hr[0] % 2]
                    if k < 5:
                        pg.op("act", lambda e: e.copy(bcn.ap.rearrange("p a t -> p (a t)"), sq_.ap[:, 0:256]), reads=[sq_], writes=[bcn])
                    else:
                        pg.op("act", lambda e: e.copy(bcn.ap[:, 1, :], sq_.ap[:, 128:256]), reads=[sq_], writes=[bcn])
                        yield
                    pps = bank
                    pg.op("pe", lambda e: e.matmul(pps.ap[:, 0:128], lhsT=bcn.ap[:, 1, :], rhs=pp.ap, start=True, stop=True), reads=[bcn, pp], writes=[pps])
                    yield
                    pn = Pm[h][hr[0] % 2]; pn32 = Pm32[h][hr[0] % 2]; hr[0] += 1
                    pg.op("dve", lambda e: e.tensor_tensor(out=pn.ap, in0=pps.ap[:, 0:128], in1=pp.ap, op=ALU.add), reads=[pps, pp], writes=[pn])
                    yield
                    bcp, pp, pp32 = bcn, pn, pn32
                ups = bank
                pg.op("pe", lambda e: e.matmul(ups.ap[:, 0:128], lhsT=pp.ap, rhs=vb4.ap[:, h, :], start=True, stop=True), reads=[pp, vb4], writes=[ups])
                pg.op("pe", lambda e: e.matmul(ups.ap[:, 128:256], lhsT=kbg4.ap[:, h, :], rhs=pp.ap, start=True, stop=True), reads=[pp, kbg4], writes=[ups])
                yield
                pg.op("act", lambda e: e.copy(usb[i2].ap, ups.ap[:, 0:128]), reads=[ups], writes=[usb[i2]])
                pg.op("act", lambda e: e.copy(wT[i2].ap, ups.ap[:, 128:256]), reads=[ups], writes=[wT[i2]])
                yield
                for ci, c in enumerate(chunks):
                    r0 = c * 64
                    rs_ = slice(r0, r0 + 64)
                    wps = bank
                    pg.op("pe", lambda e: e.matmul(wps.ap[rs_, 0:128], lhsT=wT[i2].ap[:, rs_], rhs=Sb[h].ap, start=True, stop=True), reads=[wT[i2], Sb[h]], writes=[wps])
                    yield
                    pg.op("dve", lambda e: e.scalar_tensor_tensor(out=vn[i2].ap[rs_, :], in0=wps.ap[rs_, 0:128], scalar=-1.0, in1=usb[i2].ap[rs_, :], op0=ALU.mult, op1=ALU.add), reads=[usb[i2], wps], writes=[vn[i2]])
                    yield
                    pg.op("pe", lambda e: e.matmul(op_.ap[rs_, hc], lhsT=qd[i2].ap[:, rs_], rhs=Sb[h].ap, start=True, stop=False), reads=[qd[i2], Sb[h]], writes=[op_])
                    pg.op("pe", lambda e: e.matmul(op_.ap[rs_, hc], lhsT=attnT[i2].ap[rs_, rs_], rhs=vn[i2].ap[rs_, :], start=False, stop=True), reads=[attnT[i2], vn[i2]], writes=[op_])
                    kvp = bank
                    pg.op("pe", lambda e: e.matmul(kvp.ap[:, 0:128], lhsT=kend4.ap[rs_, h, :], rhs=vn[i2].ap[rs_, :], start=True, stop=True), reads=[kend4, vn[i2]], writes=[kvp])
                    yield
                    pg.op("dve", lambda e: e.scalar_tensor_tensor(out=S32[h].ap, in0=S32[h].ap, scalar=deB.ap[:, c, blk, d, h:h + 1], in1=kvp.ap[:, 0:128], op0=ALU.mult, op1=ALU.add),
                          reads=[S32[h], deB, kvp], writes=[S32[h]])
                    pg.op("act", lambda e: e.copy(Sb[h].ap, S32[h].ap), reads=[S32[h]], writes=[Sb[h]])
                    yield
            gens = [head_gen(h) for h in range(4)]
            if pending[0] is not None:
                gens.append(pending[0])
                pending[0] = None
            while gens:
                for gnr in list(gens):
                    try:
                        next(gnr)
                    except StopIteration:
                        gens.remove(gnr)
            if d == 1:
                pg.op("act", lambda e: e.copy(osb.ap, op_.ap), reads=[op_], writes=[osb])
                pg.dma(cx.OB[ts, 512:1024], osb.ap, reads=[osb])
            else:
                pg.op("dve", lambda e: e.tensor_tensor(out=osb.ap, in0=op_.ap, in1=ob.ap, op=ALU.add), reads=[op_, ob], writes=[osb])
                pending[0] = hngs_gen(pg, cx, osb, ssq, junk, gn, gtb, 0, ysb, yT, 1024, ts)
        if pending[0] is not None:
            for _ in pending[0]:
                pass
            pending[0] = None


def phase_S5(pg, cx, es, L, l):
    nc = cx.nc
    w = cx.w
    sb = lambda name, shape, dt: pg.buf(es.enter_context(nc.sbuf_tensor(pg.uname(name), shape, dt)).ap(), name)
    NS = int(np.ceil(np.log2(L)))
    dve = lambda fn, r, wr: pg.op("dve", fn, reads=r, writes=wr)
    A = lambda nm: sb("S_" + nm, [128, 32], F32)
    lre, lim, dt_, ar, ai, m_, sn, cs, Are, Aim, t1, t2, t3, fre, fim, den = [A(n) for n in
        ("lre", "lim", "dt", "ar", "ai", "m", "sn", "cs", "Are", "Aim", "t1", "t2", "t3", "fre", "fim", "den")]
    load_T(pg, cx, lre, lre.ap, w["s5_lambda_re"][l].rearrange("d (gh gl) p -> (d gh) (gl p)", gl=2), 32)
    load_T(pg, cx, lim, lim.ap, w["s5_lambda_im"][l].rearrange("d (gh gl) p -> (d gh) (gl p)", gl=2), 32)
    ld2 = sb("S_ld2", [32, 2], F32)
    pg.dma(ld2.ap, w["s5_log_dt"][l].rearrange("d (gh gl) -> (d gh) gl", gl=2), writes=[ld2])
    stl = sb("S_stl", [32, 128], F32)
    for gl in range(2):
        dve(lambda e: e.tensor_copy(out=stl.ap[:, 64 * gl:64 * gl + 64], in_=ld2.ap[:, gl:gl + 1].to_broadcast([32, 64])), [ld2], [stl])
    psl = cx.psf.get()
    pg.op("pe", lambda e: e.transpose(out=psl.ap[:, :32], in_=stl.ap, identity=cx.identf.ap[:32, :32]), reads=[stl, cx.identf], writes=[psl])
    dve(lambda e: e.tensor_copy(out=dt_.ap, in_=psl.ap[:, :32]), [psl], [dt_])
    pg.op("act", lambda e: e.activation(out=dt_.ap, in_=dt_.ap, func=AF.Exp), reads=[dt_], writes=[dt_])
    dve(lambda e: e.tensor_tensor(out=ar.ap, in0=lre.ap, in1=dt_.ap, op=ALU.mult), [lre, dt_], [ar])
    dve(lambda e: e.tensor_tensor(out=ai.ap, in0=lim.ap, in1=dt_.ap, op=ALU.mult), [lim, dt_], [ai])
    pg.op("act", lambda e: e.activation(out=m_.ap, in_=ar.ap, func=AF.Exp, scale=1.0 / 16), reads=[ar], writes=[m_])
    pg.op("act", lambda e: e.activation(out=sn.ap, in_=ai.ap, func=AF.Sin, scale=1.0 / 16), reads=[ai], writes=[sn])
    pg.op("act", lambda e: e.activation(out=cs.ap, in_=ai.ap, func=AF.Sin, scale=1.0 / 16, bias=cx.halfpi.ap[:, 0:1]), reads=[ai, cx.halfpi], writes=[cs])
    dve(lambda e: e.tensor_tensor(out=Are.ap, in0=m_.ap, in1=cs.ap, op=ALU.mult), [m_, cs], [Are])
    dve(lambda e: e.tensor_tensor(out=Aim.ap, in0=m_.ap, in1=sn.ap, op=ALU.mult), [m_, sn], [Aim])

    def csquare(re, im):
        dve(lambda e: e.tensor_tensor(out=t1.ap, in0=re, in1=re, op=ALU.mult), [Are, PW], [t1])
        dve(lambda e: e.tensor_tensor(out=t2.ap, in0=im, in1=im, op=ALU.mult), [Aim, PW], [t2])
        dve(lambda e: e.tensor_tensor(out=t3.ap, in0=re, in1=im, op=ALU.mult), [Are, Aim, PW], [t3])

    PW = sb("S_PW", [128, 32, NS, 3], F32)
    for _ in range(4):
        csquare(Are.ap, Aim.ap)
        dve(lambda e: e.tensor_tensor(out=Are.ap, in0=t1.ap, in1=t2.ap, op=ALU.subtract), [t1, t2], [Are])
        dve(lambda e: e.tensor_scalar(out=Aim.ap, in0=t3.ap, scalar1=2.0, scalar2=None, op0=ALU.mult), [t3], [Aim])
    dve(lambda e: e.tensor_tensor(out=den.ap, in0=lre.ap, in1=lre.ap, op=ALU.mult), [lre], [den])
    dve(lambda e: e.tensor_tensor(out=t1.ap, in0=lim.ap, in1=lim.ap, op=ALU.mult), [lim], [t1])
    dve(lambda e: e.tensor_tensor(out=den.ap, in0=den.ap, in1=t1.ap, op=ALU.add), [den, t1], [den])
    dve(lambda e: e.reciprocal(out=den.ap, in_=den.ap), [den], [den])
    dve(lambda e: e.tensor_scalar(out=t3.ap, in0=Are.ap, scalar1=-1.0, scalar2=None, op0=ALU.add), [Are], [t3])
    dve(lambda e: e.tensor_tensor(out=t1.ap, in0=t3.ap, in1=lre.ap, op=ALU.mult), [t3, lre], [t1])
    dve(lambda e: e.tensor_tensor(out=t2.ap, in0=Aim.ap, in1=lim.ap, op=ALU.mult), [Aim, lim], [t2])
    dve(lambda e: e.tensor_tensor(out=t1.ap, in0=t1.ap, in1=t2.ap, op=ALU.add), [t1, t2], [t1])
    dve(lambda e: e.tensor_tensor(out=fre.ap, in0=t1.ap, in1=den.ap, op=ALU.mult), [t1, den], [fre])
    dve(lambda e: e.tensor_tensor(out=t1.ap, in0=Aim.ap, in1=lre.ap, op=ALU.mult), [Aim, lre], [t1])
    dve(lambda e: e.tensor_tensor(out=t2.ap, in0=t3.ap, in1=lim.ap, op=ALU.mult), [t3, lim], [t2])
    dve(lambda e: e.tensor_tensor(out=t1.ap, in0=t1.ap, in1=t2.ap, op=ALU.subtract), [t1, t2], [t1])
    dve(lambda e: e.tensor_tensor(out=fim.ap, in0=t1.ap, in1=den.ap, op=ALU.mult), [t1, den], [fim])
    for k in range(NS):
        if k == 0:
            dve(lambda e: e.tensor_copy(out=PW.ap[:, :, 0, 0], in_=Are.ap), [Are], [PW])
            dve(lambda e: e.tensor_copy(out=PW.ap[:, :, 0, 1], in_=Aim.ap), [Aim], [PW])
        else:
            csquare(PW.ap[:, :, k - 1, 0], PW.ap[:, :, k - 1, 1])
            dve(lambda e: e.tensor_tensor(out=PW.ap[:, :, k, 0], in0=t1.ap, in1=t2.ap, op=ALU.subtract), [t1, t2], [PW])
            dve(lambda e: e.tensor_scalar(out=PW.ap[:, :, k, 1], in0=t3.ap, scalar1=2.0, scalar2=None, op0=ALU.mult), [t3], [PW])
        dve(lambda e: e.tensor_scalar(out=PW.ap[:, :, k, 2], in0=PW.ap[:, :, k, 1], scalar1=-1.0, scalar2=None, op0=ALU.mult), [PW], [PW])
    CL = sb("S_CL", [128, 32, 2, 32], F32)
    Wb = sb("S_Wb", [128, 32, 2, 32], F32)
    PWs = sb("S_PWs", [128, 32, 9, 2], F32)
    dve(lambda e: e.memset(PWs.ap[:, :, 0, 0], 1.0), [], [PWs])
    dve(lambda e: e.memset(PWs.ap[:, :, 0, 1], 0.0), [], [PWs])
    for k in range(1, 9):
        pr, pi_ = PWs.ap[:, :, k - 1, 0], PWs.ap[:, :, k - 1, 1]
        dve(lambda e: e.tensor_tensor(out=t1.ap, in0=pr, in1=PW.ap[:, :, 0, 0], op=ALU.mult), [PWs, PW], [t1])
        dve(lambda e: e.tensor_tensor(out=t2.ap, in0=pi_, in1=PW.ap[:, :, 0, 1], op=ALU.mult), [PWs, PW], [t2])
        dve(lambda e: e.tensor_tensor(out=PWs.ap[:, :, k, 0], in0=t1.ap, in1=t2.ap, op=ALU.subtract), [t1, t2], [PWs])
        dve(lambda e: e.tensor_tensor(out=t1.ap, in0=pr, in1=PW.ap[:, :, 0, 1], op=ALU.mult), [PWs, PW], [t1])
        dve(lambda e: e.tensor_tensor(out=t2.ap, in0=pi_, in1=PW.ap[:, :, 0, 0], op=ALU.mult), [PWs, PW], [t2])
        dve(lambda e: e.tensor_tensor(out=PWs.ap[:, :, k, 1], in0=t1.ap, in1=t2.ap, op=ALU.add), [t1, t2], [PWs])
    with ExitStack() as es1:
        sb1 = lambda name, shape, dt: pg.buf(es1.enter_context(nc.sbuf_tensor(pg.uname(name), shape, dt)).ap(), name)
        Bt = [sb1("S_Bt%d" % i, [128, 32, 16], F32) for i in range(2)]
        Bb = [sb1("S_Bb%d" % i, [128, 32, 16], F32) for i in range(2)]
        tmpb = sb1("S_tmpb", [128, 32, 16], F32)
        for i, nm in enumerate(("s5_b_re", "s5_b_im")):
            base = w[nm][l]
            pg.dma(Bt[i].ap, bass.AP(base.tensor, base.offset, [[16, 128], [2048, 32], [1, 16]]), writes=[Bt[i]])
        fb = lambda t: t.ap.unsqueeze(2).to_broadcast([128, 32, 16])
        dve(lambda e: e.tensor_tensor(out=Bb[0].ap, in0=Bt[0].ap, in1=fb(fre), op=ALU.mult), [Bt[0], fre], [Bb[0]])
        dve(lambda e: e.tensor_tensor(out=tmpb.ap, in0=Bt[1].ap, in1=fb(fim), op=ALU.mult), [Bt[1], fim], [tmpb])
        dve(lambda e: e.tensor_tensor(out=Bb[0].ap, in0=Bb[0].ap, in1=tmpb.ap, op=ALU.subtract), [Bb[0], tmpb], [Bb[0]])
        dve(lambda e: e.tensor_tensor(out=Bb[1].ap, in0=Bt[1].ap, in1=fb(fre), op=ALU.mult), [Bt[1], fre], [Bb[1]])
        dve(lambda e: e.tensor_tensor(out=tmpb.ap, in0=Bt[0].ap, in1=fb(fim), op=ALU.mult), [Bt[0], fim], [tmpb])
        dve(lambda e: e.tensor_tensor(out=Bb[1].ap, in0=Bb[1].ap, in1=tmpb.ap, op=ALU.add), [Bb[1], tmpb], [Bb[1]])
        pg.op("pool", lambda e: e.memset(Wb.ap, 0.0), writes=[Wb])
        for c in range(2):
            dve(lambda e: e.tensor_copy(out=Wb.ap[0:64, :, c, 0:16], in_=Bb[c].ap[0:64]), [Bb[c]], [Wb])
            dve(lambda e: e.tensor_copy(out=Wb.ap[64:128, :, c, 16:32], in_=Bb[c].ap[64:128]), [Bb[c]], [Wb])
        St0 = sb1("S_St0", [128, 64], F32)
        St = sb1("S_St", [128, 128], F32)
        for d in range(2):
            for c, nm in enumerate(("s5_c_re", "s5_c_im")):
                for blk in range(4):
                    pg.dma(St0.ap, w[nm][l, d, 8 * blk:8 * blk + 8].rearrange("g i p -> (g i) p"), writes=[St0])
                    sgn = 1.0 if c == 0 else -1.0
                    for hh in range(2):
                        dve(lambda e: e.tensor_scalar(out=St.ap[:, 64 * hh:64 * hh + 64], in0=St0.ap, scalar1=cx.pm.ap[:, hh:hh + 1], scalar2=sgn, op0=ALU.mult, op1=ALU.mult),
                            [St0, cx.pm], [St])
                    ps = cx.psf.get()
                    pg.op("pe", lambda e: e.transpose(out=ps.ap[:, 0:128], in_=St.ap, identity=cx.identf.ap), reads=[St, cx.identf], writes=[ps])
                    dg0 = d * 16 + blk * 4
                    pg.op("act", lambda e: e.copy(CL.ap[:, dg0:dg0 + 4, c, :], ps.ap[:, 0:128].rearrange("p (q m) -> p q m", q=4)), reads=[ps], writes=[CL])
    dsk = sb("S_dsk", [32, 16], F32)
    load_T(pg, cx, dsk, dsk.ap, w["s5_d"][l].rearrange("(g q) -> g q", q=32), 16, wd=32)
    bgl = sb("S_bgl", [128, 4], F32)
    load_T(pg, cx, bgl, bgl.ap, w["s5_b_glu"][l].rearrange("(c p) -> c p", p=128), 4)
    es2 = ExitStack()
    sb2 = lambda name, shape, dt: pg.buf(es2.enter_context(nc.sbuf_tensor(pg.uname(name), shape, dt)).ap(), name)
    NCH = L // 8
    NSC = int(np.ceil(np.log2(NCH)))
    HW = min(512, NCH)
    NH = NCH // HW
    ub = sb2("S_ub", [32, 8, NCH], BF16)
    UW = min(2048, L)
    ust = [sb2("S_ust%d" % i, [32, UW], F32) for i in range(2)]
    Yc = sb2("S_Yc", [32, L], F32)
    XS_ = [[sb2("S_X%d_%d" % (d, i), [128, NCH + 2], F32) for i in range(3)] for d in range(2)]
    Xb = [[sb2("S_Xb%d_%d" % (d, c), [128, NCH + 2], BF16) for c in range(2)] for d in range(2)]
    Wt_ = [sb2("S_Wt%d" % d, [128, 2, 8, 32], F32) for d in range(2)]
    tmpw_ = [sb2("S_tmpw%d" % d, [128, 8, 32], F32) for d in range(2)]
    WsT = [sb2("S_WsT%d" % d, [32, 2, 8, 128], BF16) for d in range(2)]
    CI = [sb2("S_CI%d" % d, [128, 2, 8, 32], BF16) for d in range(2)]
    CIf_ = [sb2("S_CIf%d" % d, [128, 2, 8, 32], F32) for d in range(2)]
    Kd = [sb2("S_Kd%d" % d, [32, 8, 32], BF16) for d in range(2)]
    ua_t = sb2("S_ua", [32, UW], F32)
    x2_t = sb2("S_x2", [32, UW], F32)
    zo_t = sb2("S_zo", [32, UW], BF16)

    def strided(ap2, start, n, step):
        b0 = ap2[:, start:start + 1]
        return bass.AP(b0.tensor, b0.offset, [list(ap2.ap[0]), [step * ap2.ap[1][0], n]])

    ev = [0]
    prev_pair = None
    out_ps = Rot(cx.psf.items[2:5])
    bcW = lambda a: a.unsqueeze(1).to_broadcast([128, 8, 32])
    bcP = lambda a: a.unsqueeze(2).to_broadcast([128, 8, 32])
    for gh in range(16):
        urow = PF_S5_U + 32 * gh
        for i, t0 in enumerate(range(0, L, UW)):
            st_ = ust[i % 2]
            pg.dma(st_.ap, cx.PF[urow:urow + 32, t0:t0 + UW], writes=[st_])
            pg.op("pool", lambda e: e.tensor_copy(out=ub.ap[:, :, t0 // 8:(t0 + UW) // 8], in_=st_.ap.rearrange("p (n s) -> p s n", s=8)), reads=[st_], writes=[ub])
        def dir_gen(d, gh=gh):
            bank = cx.psf.items[d]
            Wt, tmpw, CIf = Wt_[d], tmpw_[d], CIf_[d]
            dg = d * 16 + gh
            wbr, wbi = Wb.ap[:, dg, 0, :], Wb.ap[:, dg, 1, :]
            pre, pim = PWs.ap[:, dg, 0:8, 0], PWs.ap[:, dg, 0:8, 1]
            dve(lambda e: e.tensor_tensor(out=Wt.ap[:, 0], in0=bcW(wbr), in1=bcP(pre), op=ALU.mult), [Wb, PWs], [Wt])
            yield
            dve(lambda e: e.tensor_tensor(out=tmpw.ap, in0=bcW(wbi), in1=bcP(pim), op=ALU.mult), [Wb, PWs], [tmpw])
            yield
            dve(lambda e: e.tensor_tensor(out=Wt.ap[:, 0], in0=Wt.ap[:, 0], in1=tmpw.ap, op=ALU.subtract), [Wt, tmpw], [Wt])
            yield
            dve(lambda e: e.tensor_tensor(out=Wt.ap[:, 1], in0=bcW(wbr), in1=bcP(pim), op=ALU.mult), [Wb, PWs], [Wt])
            yield
            dve(lambda e: e.tensor_tensor(out=tmpw.ap, in0=bcW(wbi), in1=bcP(pre), op=ALU.mult), [Wb, PWs], [tmpw])
            yield
            dve(lambda e: e.tensor_tensor(out=Wt.ap[:, 1], in0=Wt.ap[:, 1], in1=tmpw.ap, op=ALU.add), [Wt, tmpw], [Wt])
            yield
            for c in range(2):
                for t4 in range(0, 8, 4):
                    ps = bank
                    for tq in range(4):
                        pg.op("pe", lambda e: e.transpose(out=ps.ap[:32, tq * 128:(tq + 1) * 128], in_=Wt.ap[:, c, t4 + tq, :], identity=cx.identf.ap), reads=[Wt, cx.identf], writes=[ps])
                    pg.op("act", lambda e: e.copy(WsT[d].ap[:, c, t4:t4 + 4, :], ps.ap[:32, :].rearrange("p (q m) -> p q m", q=4)), reads=[ps], writes=[WsT[d]])
                    yield
            cl0, cl1 = CL.ap[:, dg, 0, :], CL.ap[:, dg, 1, :]
            pre1, pim1 = PWs.ap[:, dg, 1:9, 0], PWs.ap[:, dg, 1:9, 1]
            dve(lambda e: e.tensor_tensor(out=CIf.ap[:, 0], in0=bcW(cl0), in1=bcP(pre1), op=ALU.mult), [CL, PWs], [CIf])
            yield
            dve(lambda e: e.tensor_tensor(out=tmpw.ap, in0=bcW(cl1), in1=bcP(pim1), op=ALU.mult), [CL, PWs], [tmpw])
            yield
            dve(lambda e: e.tensor_tensor(out=CI[d].ap[:, 0], in0=CIf.ap[:, 0], in1=tmpw.ap, op=ALU.add), [CIf, tmpw], [CI[d]])
            yield
            dve(lambda e: e.tensor_tensor(out=CIf.ap[:, 1], in0=bcW(cl1), in1=bcP(pre1), op=ALU.mult), [CL, PWs], [CIf])
            yield
            dve(lambda e: e.tensor_tensor(out=tmpw.ap, in0=bcW(cl0), in1=bcP(pim1), op=ALU.mult), [CL, PWs], [tmpw])
            yield
            dve(lambda e: e.tensor_tensor(out=CI[d].ap[:, 1], in0=CIf.ap[:, 1], in1=tmpw.ap, op=ALU.subtract), [CIf, tmpw], [CI[d]])
            yield
            ps = bank
            for tau in range(8):
                po = ps.ap[0:32, tau * 32:(tau + 1) * 32]
                pg.op("pe", lambda e: e.matmul(po, lhsT=Wt.ap[:, 0, tau, :], rhs=cl0, start=True, stop=False), reads=[Wt, CL], writes=[ps])
                pg.op("pe", lambda e: e.matmul(po, lhsT=Wt.ap[:, 1, tau, :], rhs=cl1, start=False, stop=True), reads=[Wt, CL], writes=[ps])
            pg.op("act", lambda e: e.copy(Kd[d].ap, ps.ap[0:32, 0:256].rearrange("p (t m) -> p t m", t=8)), reads=[ps], writes=[Kd[d]])
            yield
            re, im, T = XS_[d]
            for b_ in (re, im, T):
                pg.op("pool", lambda e: e.memset(b_.ap, 0.0), writes=[b_])
                yield
            for c, dstb in ((0, re), (1, im)):
                for hf in range(NH):
                    ps = bank
                    for s_ in range(8):
                        tau = 7 - s_ if d == 0 else s_
                        pg.op("pe", lambda e: e.matmul(ps.ap[:, :HW], lhsT=WsT[d].ap[:, c, tau, :], rhs=ub.ap[:, s_, hf * HW:(hf + 1) * HW], start=(s_ == 0), stop=(s_ == 7)),
                              reads=[WsT[d], ub], writes=[ps])
                    ev[0] += 1
                    if ev[0] % 2 == 0:
                        pg.op("act", lambda e: e.copy(dstb.ap[:, 1 + hf * HW:1 + (hf + 1) * HW], ps.ap[:, :HW]), reads=[ps], writes=[dstb])
                        yield
                    else:
                        pg.op("dve", lambda e: e.tensor_copy(out=dstb.ap[:, 1 + hf * HW:1 + (hf + 1) * HW], in_=ps.ap[:, :HW]), reads=[ps], writes=[dstb])
                        yield
            for k in range(NSC):
                sft = 1 << k
                if sft >= NCH:
                    break
                kk = k + 3
                cre, cim, ncim = PW.ap[:, dg, kk, 0:1], PW.ap[:, dg, kk, 1:2], PW.ap[:, dg, kk, 2:3]
                if d == 0:
                    dst, src, keep = slice(1 + sft, 1 + NCH), slice(1, 1 + NCH - sft), slice(1, 1 + sft)
                else:
                    dst, src, keep = slice(1, 1 + NCH - sft), slice(1 + sft, 1 + NCH), slice(1 + NCH - sft, 1 + NCH)
                dve(lambda e: e.scalar_tensor_tensor(out=T.ap[:, dst], in0=re.ap[:, src], scalar=cre, in1=re.ap[:, dst], op0=ALU.mult, op1=ALU.add), [re, PW], [T])
                yield
                dve(lambda e: e.scalar_tensor_tensor(out=T.ap[:, dst], in0=im.ap[:, src], scalar=ncim, in1=T.ap[:, dst], op0=ALU.mult, op1=ALU.add), [im, T, PW], [T])
                yield
                pg.op("pool", lambda e: e.tensor_copy(out=T.ap[:, keep], in_=re.ap[:, keep]), reads=[re], writes=[T])
                yield
                if d == 0:
                    rv = lambda ap, sl: bass.AP(ap.tensor, ap[:, sl].offset + (sl.stop - sl.start) - 1, [list(ap.ap[0]), [-1, sl.stop - sl.start]])
                    dve(lambda e: e.scalar_tensor_tensor(out=rv(im.ap, dst), in0=rv(im.ap, src), scalar=cre, in1=rv(im.ap, dst), op0=ALU.mult, op1=ALU.add), [im, PW], [im])
                    yield
                else:
                    dve(lambda e: e.scalar_tensor_tensor(out=im.ap[:, dst], in0=im.ap[:, src], scalar=cre, in1=im.ap[:, dst], op0=ALU.mult, op1=ALU.add), [im, PW], [im])
                    yield
                dve(lambda e: e.scalar_tensor_tensor(out=im.ap[:, dst], in0=re.ap[:, src], scalar=cim, in1=im.ap[:, dst], op0=ALU.mult, op1=ALU.add), [re, im, PW], [im])
                yield
                re, T = T, re
            pg.op("pool", lambda e: e.tensor_copy(out=Xb[d][0].ap, in_=re.ap), reads=[re], writes=[Xb[d][0]])
            yield
            pg.op("pool", lambda e: e.tensor_copy(out=Xb[d][1].ap, in_=im.ap), reads=[im], writes=[Xb[d][1]])
            yield

        def gelu_gen(gh_, urow_):
            for t0 in range(0, L, UW):
                tsl = slice(t0, t0 + UW)
                ua, x2, zo = ua_t.ap, x2_t.ap, zo_t.ap
                pg.dma(ua, cx.PF[urow_:urow_ + 32, tsl], writes=[ua_t])
                yv = Yc.ap[:, tsl]
                dve(lambda e: e.scalar_tensor_tensor(out=yv, in0=ua, scalar=dsk.ap[:, gh_:gh_ + 1], in1=yv, op0=ALU.mult, op1=ALU.add), [ua_t, dsk, Yc], [Yc])
                yield
                pg.op("pool", lambda e: e.tensor_tensor(out=x2, in0=yv, in1=yv, op=ALU.mult), reads=[Yc], writes=[x2_t])
                yield
                dve(lambda e: e.tensor_scalar(out=x2, in0=x2, scalar1=0.044715, scalar2=1.0, op0=ALU.mult, op1=ALU.add), [x2_t], [x2_t])
                yield
                pg.op("pool", lambda e: e.tensor_tensor(out=x2, in0=x2, in1=yv, op=ALU.mult), reads=[Yc, x2_t], writes=[x2_t])
                yield
                pg.op("act", lambda e: e.activation(out=x2, in_=x2, func=AF.Sigmoid, scale=1.5957691216), reads=[x2_t], writes=[x2_t])
                yield
                dve(lambda e: e.tensor_tensor(out=zo, in0=x2, in1=yv, op=ALU.mult), [x2_t, Yc], [zo_t])
                pg.dma(cx.ZT[urow_ - PF_S5_U:urow_ - PF_S5_U + 32, tsl], zo, reads=[zo_t])
                yield

        gens = [dir_gen(0), dir_gen(1)] + ([gelu_gen(*prev_pair)] if prev_pair is not None else [])
        while gens:
            for gnr in list(gens):
                try:
                    next(gnr)
                except StopIteration:
                    gens.remove(gnr)
        prev_pair = (gh, urow)
        for hf in range(NH):
            for sp in range(8):
                ps = out_ps.get()
                po = ps.ap[0:32, :HW]
                mm = []
                for c in range(2):
                    mm.append((CI[0].ap[:, c, sp, :], Xb[0][c].ap[:, hf * HW:hf * HW + HW], [CI[0], Xb[0][c]]))
                    mm.append((CI[1].ap[:, c, 7 - sp, :], Xb[1][c].ap[:, hf * HW + 2:hf * HW + 2 + HW], [CI[1], Xb[1][c]]))
                for s_ in range(0, sp + 1):
                    mm.append((Kd[0].ap[:, sp - s_, :], ub.ap[:, s_, hf * HW:(hf + 1) * HW], [Kd[0], ub]))
                for s_ in range(sp, 8):
                    mm.append((Kd[1].ap[:, s_ - sp, :], ub.ap[:, s_, hf * HW:(hf + 1) * HW], [Kd[1], ub]))
                for i, (lh, rh, rd) in enumerate(mm):
                    pg.op("pe", lambda e: e.matmul(po, lhsT=lh, rhs=rh, start=(i == 0), stop=(i == len(mm) - 1)), reads=rd, writes=[ps])
                pg.op("act", lambda e: e.copy(strided(Yc.ap, hf * HW * 8 + sp, HW, 8), po), reads=[ps], writes=[Yc])
    gens = [gelu_gen(*prev_pair)]
    for gnr in gens:
        for _ in gnr:
            pass
    pg.barrier()
    es2.close()
    wg = sb("S_wg", [128, 4, 512], BF16)
    wst = sb("S_wst", [128, 2048], F32)
    pg.dma(wst.ap[:, 0:2048].rearrange("p (k c) -> p k c", k=4), w["s5_w_glu"][l].rearrange("(k p) c -> p k c", p=128), writes=[wst])
    dve(lambda e: e.tensor_copy(out=wg.ap, in_=wst.ap[:, 0:2048].rearrange("p (k c) -> p k c", k=4)), [wst], [wg])
    zt = [sb("S_zt%d" % i, [128, 4, 512], BF16) for i in range(2)]
    gt = [sb("S_gt%d" % i, [128, 4, 512], F32) for i in range(2)]
    sg = sb("S_sg", [128, 512], F32)
    yo = [sb("S_yo%d" % i, [128, 4, 512], BF16) for i in range(2)]
    ZTv = cx.ZT.rearrange("(k p) t -> p k t", p=128)
    NT = L // 512
    for it in range(NT):
        tsl = slice(it * 512, (it + 1) * 512)
        z_ = zt[it % 2]; g_ = gt[it % 2]; y_ = yo[it % 2]
        pg.dma(z_.ap, ZTv[:, :, tsl], writes=[z_])
        pg.dma(g_.ap, cx.PF[PF_S5_G:PF_S5_G + 512, tsl].rearrange("(k p) t -> p k t", p=128), writes=[g_])
        pg.op("act", lambda e: e.activation(out=g_.ap, in_=g_.ap, func=AF.Silu), reads=[g_], writes=[g_])
        pg.op("pool", lambda e: e.tensor_tensor(out=g_.ap, in0=g_.ap, in1=z_.ap, op=ALU.mult), reads=[g_, z_], writes=[g_])
        for oc in range(4):
            ps = cx.psf.get()
            for k in range(4):
                pg.op("pe", lambda e: e.matmul(ps.ap, lhsT=wg.ap[:, k, oc * 128:(oc + 1) * 128], rhs=z_.ap[:, k, :], start=(k == 0), stop=(k == 3)), reads=[wg, z_], writes=[ps])
            pg.op("act", lambda e: e.activation(out=sg.ap, in_=ps.ap, func=AF.Sigmoid, bias=bgl.ap[:, oc:oc + 1]), reads=[ps, bgl], writes=[sg])
            dve(lambda e: e.tensor_tensor(out=y_.ap[:, oc, :], in0=sg.ap, in1=g_.ap[:, oc, :], op=ALU.mult), [sg, g_], [y_])
        pg.dma(cx.BT[1536:2048, tsl].rearrange("(k p) t -> p k t", p=128), y_.ap, reads=[y_])


W_NAMES = ["norm_g", "w_in", "lru_conv_w", "lru_conv_b", "lru_w_a", "lru_b_a", "lru_w_x", "lru_b_x", "lru_lambda",
           "gla_w_up", "gla_b_up", "gla_norm_g", "dn_conv_w", "dn_a_log", "dn_dt_bias", "dn_norm_g",
           "s5_lambda_re", "s5_lambda_im", "s5_log_dt", "s5_b_re", "s5_b_im", "s5_c_re", "s5_c_im", "s5_d",
           "s5_w_glu", "s5_b_glu", "w_branch", "w_merge_gate", "b_merge_gate", "w_out", "final_norm_g"]


def host_consts():
    c = {}
    c["identb"] = np.eye(128, dtype=np.float32).astype(ml_dtypes.bfloat16)
    c["identf"] = np.eye(128, dtype=np.float32)
    idx = np.arange(128)
    same = (idx[:, None] // 64) == (idx[None, :] // 64)
    le = idx[:, None] <= idx[None, :]
    lt = idx[:, None] < idx[None, :]
    c["m_incl"] = np.stack([(same & le), (same & le.T)]).astype(np.float32)
    c["m_strict_after"] = np.stack([(same & lt.T), (same & lt)]).astype(np.float32)
    c["m_dn"] = np.stack([c["m_strict_after"], c["m_incl"]], axis=1)
    c["chunkind"] = np.stack([np.repeat((idx // 64 == cc)[:, None], 128, axis=1) for cc in range(2)]).astype(np.float32)
    c["onesf"] = np.ones((128, 128), np.float32)
    ev = ((idx // 16) % 2 == 0).astype(np.float32)
    c["pm"] = np.stack([ev, 1.0 - ev], axis=1).astype(np.float32)
    return c


def build(L, shapes, nslot=2, depth=2, debug=False, branches=("lru", "gla", "dn", "s5")):
    from contextlib import ExitStack
    nc = bass.Bass("TRN2", target_bir_lowering=False)
    pg = Prog(nc)
    cx = Ctx()
    cx.nc = nc
    cx.pg = pg
    cx.w = {}
    for nm in W_NAMES:
        cx.w[nm] = nc.dram_tensor(nm, list(shapes[nm]), F32, kind="ExternalInput").ap()
    hc = host_consts()
    cx.cd = {}
    for nm, arr in hc.items():
        cx.cd[nm] = nc.dram_tensor("c_" + nm, list(arr.shape), BF16 if arr.dtype == ml_dtypes.bfloat16 else F32, kind="ExternalInput").ap()
    xs = [nc.dram_tensor("x%d" % s, [L, D], F32, kind="ExternalInput").ap() for s in range(nslot)]
    ys = [nc.dram_tensor("y%d" % s, [L, D], F32, kind="ExternalOutput").ap() for s in range(nslot)]
    sk = "ExternalOutput" if debug else "Internal"
    cx.PF = nc.dram_tensor("PF", [PF_ROWS, L], F32, kind=sk).ap()
    cx.PT = nc.dram_tensor("PT", [L, PT_COLS], F32, kind=sk).ap()
    cx.XNT = nc.dram_tensor("XNT", [D, L], BF16, kind=sk).ap()
    cx.BT = nc.dram_tensor("BT", [2048, L], BF16, kind=sk).ap()
    cx.OB = nc.dram_tensor("OB", [L, 1024], F32, kind=sk).ap()
    XS = [nc.dram_tensor("XS%d" % s, [L, D], F32, kind=sk).ap() for s in range(nslot)]
    cx.QKT = nc.dram_tensor("QKT", [1024, L], BF16, kind=sk).ap()
    cx.KVT = nc.dram_tensor("KVT", [L, 1024], BF16, kind=sk).ap()
    cx.ZT = nc.dram_tensor("ZT", [512, L], BF16, kind=sk).ap()
    psf, psb = mk_psum(pg, nc)
    cx.psf = Rot(psf[:5])
    cx.pso = psf[5]
    cx.psb = Rot(psb)
    gsb = lambda name, shape, dt: pg.buf(nc.alloc_sbuf_tensor(name, shape, dt).ap(), name)
    cx.identb = gsb("identb", [128, 128], BF16)
    pg.dma(cx.identb.ap, cx.cd["identb"], writes=[cx.identb])
    cx.identf = gsb("identf", [128, 128], F32)
    pg.dma(cx.identf.ap, cx.cd["identf"], writes=[cx.identf])
    cx.eps = gsb("eps", [128, 1], F32)
    pg.op("dve", lambda e: e.memset(cx.eps.ap, EPS), writes=[cx.eps])
    cx.one = gsb("one", [128, 1], F32)
    pg.op("dve", lambda e: e.memset(cx.one.ap, 1.0), writes=[cx.one])
    cx.ldst = gsb("ldst", [128, 128], F32)
    cx.m_incl = gsb("m_incl", [128, 2, 128], F32)
    pg.dma(cx.m_incl.ap, cx.cd["m_incl"].rearrange("d s t -> s d t"), writes=[cx.m_incl])
    cx.m_sa = gsb("m_sa", [128, 2, 128], F32)
    pg.dma(cx.m_sa.ap, cx.cd["m_strict_after"].rearrange("d s t -> s d t"), writes=[cx.m_sa])
    cx.m_dn = gsb("m_dn", [128, 2, 2, 128], F32)
    pg.dma(cx.m_dn.ap[:, 0], cx.cd["m_dn"][0].rearrange("j s t -> s j t"), writes=[cx.m_dn])
    pg.dma(cx.m_dn.ap[:, 1], cx.cd["m_dn"][1].rearrange("j s t -> s j t"), writes=[cx.m_dn])
    cx.chunkind = gsb("chunkind", [128, 2, 128], F32)
    pg.dma(cx.chunkind.ap, cx.cd["chunkind"].rearrange("c s m -> s c m"), writes=[cx.chunkind])
    cx.onesf = gsb("onesf", [128, 128], F32)
    pg.dma(cx.onesf.ap, cx.cd["onesf"], writes=[cx.onesf])
    cx.pm = gsb("pm", [128, 2], F32)
    pg.dma(cx.pm.ap, cx.cd["pm"], writes=[cx.pm])
    cx.halfpi = gsb("halfpi", [128, 1], F32)
    pg.op("dve", lambda e: e.memset(cx.halfpi.ap, float(np.pi / 2)), writes=[cx.halfpi])
    cx.zb = gsb("zb", [128, 2048], BF16)
    pg.op("pool", lambda e: e.memset(cx.zb.ap, 0.0), writes=[cx.zb])
    bidx = {"lru": 0, "gla": 1, "dn": 2, "s5": 3}
    for l in range(depth):
        for s in range(nslot):
            xin = xs[s] if l == 0 else XS[s]
            last = (l == depth - 1)
            xout = ys[s] if last else XS[s]
            pg.barrier()
            with ExitStack() as es:
                phase_P(pg, cx, es, L, xin, l)
                pg.barrier()
            for bn in ("lru", "gla", "dn", "s5"):
                if bn not in branches:
                    b = bidx[bn]
                    for t0 in range(0, L, 2048):
                        tw = min(2048, L - t0)
                        for c in range(4):
                            pg.dma(cx.BT[b * 512 + c * 128:b * 512 + (c + 1) * 128, t0:t0 + tw], cx.zb.ap[:, :tw], reads=[cx.zb])
            if "lru" in branches:
                with ExitStack() as es:
                    phase_LRU(pg, cx, es, L, l)
                    pg.barrier()
            if "s5" in branches:
                with ExitStack() as es:
                    phase_S5(pg, cx, es, L, l)
                    pg.barrier()
            if "gla" in branches:
                with ExitStack() as es:
                    phase_GLA(pg, cx, es, L, l)
                    pg.barrier()
            if "dn" in branches:
                with ExitStack() as es:
                    phase_DN(pg, cx, es, L, l)
                    pg.barrier()
            pg.barrier()
            with ExitStack() as es:
                phase_M(pg, cx, es, L, xin, xout, l, last)
                pg.barrier()
    pg.barrier()
    return nc, pg, hc


_CACHE = {}


def kernel(**inputs):
    L = inputs["x_prompt"].shape[1]
    shapes = {nm: inputs[nm].shape for nm in W_NAMES}
    nc, pg, hc = build(L, shapes)
    xp = np.ascontiguousarray(inputs["x_prompt"], dtype=np.float32)
    xsm = np.ascontiguousarray(inputs["x_sample"], dtype=np.float32)
    wmap = {nm: np.ascontiguousarray(inputs[nm], dtype=np.float32) for nm in W_NAMES}
    in_maps = []
    for c in range(8):
        m = dict(wmap)
        for nm, arr in hc.items():
            m["c_" + nm] = arr
        m["x0"] = xp[c]
        m["x1"] = xsm[c % 2]
        in_maps.append(m)
    res = run_bass_kernel_spmd(nc, in_maps, core_ids=list(range(8)))
    y_prompt = np.stack([np.asarray(res.results[c]["y0"], dtype=np.float32) for c in range(8)], axis=0)
    y_sample = np.stack([np.asarray(res.results[c]["y1"], dtype=np.float32) for c in range(2)], axis=0)
    return (y_prompt, y_sample)
```
